# Optimizing a Trainium2 kernel written in Bass

```python
import jax, jax.numpy as jnp
from jax import lax
import numpy as np

D_MODEL = 1024
BATCH = 8
SEQ = 2048
DEPTH = 2
DEC_BATCH = 32
DEC_SEQ = 1
PAST_LEN = 8192
PAGE_SIZE = 128

N_EVEN = (DEPTH + 1) // 2
N_ODD = DEPTH // 2
D_A = D_MODEL // 2
CONV_W = 31
H_B = 8
HD_B = 64
D_B = H_B * HD_B
KV_H = 2
GRP = H_B // KV_H
L_CMP = 32
L_SEL = 64
K_SEL = 16
WINDOW = 512
Q_BLK_SEL = 64
Q_BLK_WIN = 128
FORCE = 1e4
HD_C = 64
H_C = D_MODEL // HD_C
D_C = H_C * HD_C
LORA_W = 64
LORA_A = 64
GN_EPS = 64e-5
ALPHA = (2 * DEPTH) ** 0.25
BETA = (8 * DEPTH) ** -0.25
LN_EPS = 1e-5
NEG = -1e30
D_IN_EVEN = 3 * D_A + D_B + 6 * KV_H * HD_B + 3 * H_B + D_B

kernel_name = 'hybrid_conformer_nsa_rwkv7_deepnorm_step'


def _layer_norm(x, g, b, eps=LN_EPS):
    xf = x.astype(jnp.float32)
    mu = xf.mean(-1, keepdims=True)
    var = jnp.square(xf - mu).mean(-1, keepdims=True)
    return ((xf - mu) * lax.rsqrt(var + eps) * g + b).astype(x.dtype)


def _masked_softmax(s, mask):
    p = jax.nn.softmax(jnp.where(mask, s, NEG), axis=-1)
    return jnp.where(mask, p, 0.0)


def _alibi_slopes():
    return jnp.exp2(-8.0 * (jnp.arange(H_B, dtype=jnp.float32) + 1.0) / H_B).reshape(KV_H, GRP)


def nsa_cmp_sel(qf, q_pos, kv_cmp, kv_sel, wk_cmp, wv_cmp, slopes):
    f32 = jnp.float32
    B, Tq = qf.shape[:2]
    Lk = kv_cmp.shape[1]
    n_cmp = Lk // L_CMP
    blk = kv_cmp[:, :n_cmp * L_CMP].reshape(B, n_cmp, L_CMP, 2, KV_H, HD_B).astype(f32)
    kc = jnp.einsum('bnlgd,lg->bngd', blk[:, :, :, 0], wk_cmp.astype(f32))
    vc = jnp.einsum('bnlgd,lg->bngd', blk[:, :, :, 1], wv_cmp.astype(f32))
    c_end = jnp.arange(n_cmp) * L_CMP + (L_CMP - 1)
    dist_c = q_pos[:, None] - c_end[None, :]
    s_c = (jnp.einsum('btgrd,bngd->btgrn', qf, kc)
           - slopes[None, None, :, :, None] * dist_c[None, :, None, None, :].astype(f32))
    p_c = _masked_softmax(s_c, (dist_c >= 0)[None, :, None, None, :])
    o_c = jnp.einsum('btgrn,bngd->btgrd', p_c, vc)
    n_sel = -(-Lk // L_SEL)
    ratio = L_SEL // L_CMP
    imp = jnp.pad(p_c.sum(axis=3), ((0, 0), (0, 0), (0, 0), (0, n_sel * ratio - n_cmp)))
    imp = imp.reshape(B, Tq, KV_H, n_sel, ratio).sum(-1)
    sb = jnp.arange(n_sel)[None, :]
    tb = (q_pos // L_SEL)[:, None]
    forced = (sb == 0) | (sb == tb) | (sb == tb - 1)
    imp = jnp.where(forced[None, :, None, :], FORCE, imp)
    imp = jnp.where((sb <= tb)[None, :, None, :], imp, -jnp.inf)
    k_top = min(K_SEL, n_sel)
    _, idx = lax.top_k(imp, k_top)
    pad = n_sel * L_SEL - Lk
    ks = jnp.pad(kv_sel, ((0, 0), (0, pad), (0, 0), (0, 0), (0, 0)))
    ks = ks.reshape(B, n_sel, L_SEL, 2, KV_H, HD_B).transpose(0, 4, 1, 3, 2, 5)
    bi = jnp.arange(B)[:, None, None, None]
    gi = jnp.arange(KV_H)[None, :, None, None]
    lpos = jnp.arange(L_SEL)

    def sel_block(args):
        qb, pb, ib = args
        it = ib.transpose(0, 2, 1, 3)
        kvg = ks[bi, gi, it].astype(f32)
        kpos = it[..., None] * L_SEL + lpos
        dist = pb[None, None, :, None, None] - kpos
        s = (jnp.einsum('bqgrd,bgqkld->bgqrkl', qb, kvg[:, :, :, :, 0])
             - slopes[None, :, None, :, None, None] * dist[:, :, :, None].astype(f32))
        sh = s.shape
        p = _masked_softmax(s.reshape(sh[:4] + (-1,)), (dist >= 0)[:, :, :, None].reshape(sh[:3] + (1, -1)))
        return jnp.einsum('bgqrkl,bgqkld->bqgrd', p.reshape(sh), kvg[:, :, :, :, 1])

    qbs = Q_BLK_SEL if Tq % Q_BLK_SEL == 0 else Tq
    nb = Tq // qbs
    o_s = lax.map(sel_block, (qf.reshape(B, nb, qbs, KV_H, GRP, HD_B).swapaxes(0, 1),
                              q_pos.reshape(nb, qbs),
                              idx.reshape(B, nb, qbs, KV_H, k_top).swapaxes(0, 1)))
    o_s = o_s.swapaxes(0, 1).reshape(B, Tq, KV_H, GRP, HD_B)
    return o_c, o_s


def window_banded(qf, kv_w, slopes):
    f32 = jnp.float32
    B, T = qf.shape[:2]
    qbs = min(Q_BLK_WIN, T)
    nb = T // qbs
    kvp = jnp.pad(kv_w, ((0, 0), (WINDOW, 0), (0, 0), (0, 0), (0, 0)))
    kidx = jnp.arange(nb)[:, None] * qbs + jnp.arange(qbs + WINDOW)[None, :]
    kvb = kvp[:, kidx].astype(f32)
    kpos = kidx - WINDOW
    qpos = jnp.arange(T).reshape(nb, qbs)
    dist = qpos[:, :, None] - kpos[:, None, :]
    mask = (dist >= 0) & (dist < WINDOW) & (kpos[:, None, :] >= 0)
    qb = qf.reshape(B, nb, qbs, KV_H, GRP, HD_B)
    s = (jnp.einsum('bnqgrd,bnkgd->bngrqk', qb, kvb[:, :, :, 0])
         - slopes[None, None, :, :, None, None] * dist[None, :, None, None].astype(f32))
    p = _masked_softmax(s, mask[None, :, None, None])
    o = jnp.einsum('bngrqk,bnkgd->bnqgrd', p, kvb[:, :, :, 1])
    return o.reshape(B, T, KV_H, GRP, HD_B)


def window_dense(qf, q_pos, kv_ctx, k_pos0, slopes):
    f32 = jnp.float32
    kvf = kv_ctx.astype(f32)
    kpos = k_pos0 + jnp.arange(kv_ctx.shape[1])
    dist = q_pos[:, None] - kpos[None, :]
    mask = (dist >= 0) & (dist < WINDOW)
    s = (jnp.einsum('btgrd,bkgd->bgrtk', qf, kvf[:, :, 0])
         - slopes[None, :, :, None, None] * dist[None, None, None].astype(f32))
    p = _masked_softmax(s, mask[None, None, None])
    return jnp.einsum('bgrtk,bkgd->btgrd', p, kvf[:, :, 1])


def even_mixer(x, q_pos, conv_prev, past_cmp, past_sel, win_prev, w_in, conv_w, conv_b, cln_g, cln_b,
               wk_cmp, wv_cmp, w_out, slopes, banded):
    f32 = jnp.float32
    B, T, _ = x.shape
    h = x @ w_in
    cuts = [int(c) for c in np.cumsum([D_A, D_A, D_A, D_B, 6 * KV_H * HD_B, 3 * H_B])]
    a_val, a_glu, z_a, q, kv6, g3, z_b = jnp.split(h, cuts, axis=-1)
    u = a_val * jax.nn.sigmoid(a_glu)
    ext = jnp.concatenate([conv_prev.astype(u.dtype), u], axis=1)
    c = lax.conv_general_dilated(ext, conv_w.astype(u.dtype), (1,), 'VALID',
                                 dimension_numbers=('NWC', 'WIO', 'NWC'), feature_group_count=D_A) + conv_b
    ya = jax.nn.silu(_layer_norm(c, cln_g, cln_b)) * jax.nn.silu(z_a)
    conv_new = ext[:, ext.shape[1] - (CONV_W - 1):]
    qf = q.reshape(B, T, KV_H, GRP, HD_B).astype(f32) * (HD_B ** -0.5)
    kv6 = kv6.reshape(B, T, 3, 2, KV_H, HD_B)
    cmp_new, sel_new, win_new = kv6[:, :, 0], kv6[:, :, 1], kv6[:, :, 2]
    kv_cmp = jnp.concatenate([past_cmp.astype(x.dtype), cmp_new], axis=1)
    kv_sel = jnp.concatenate([past_sel.astype(x.dtype), sel_new], axis=1)
    o_c, o_s = nsa_cmp_sel(qf, q_pos, kv_cmp, kv_sel, wk_cmp, wv_cmp, slopes)
    if banded:
        o_w = window_banded(qf, win_new, slopes)
        win_state = win_new[:, T - min(WINDOW, T):]
    else:
        ctx = jnp.concatenate([win_prev.astype(x.dtype), win_new], axis=1)
        o_w = window_dense(qf, q_pos, ctx, q_pos[0] - win_prev.shape[1], slopes)
        win_state = ctx[:, ctx.shape[1] - min(WINDOW, ctx.shape[1]):]
    g = jax.nn.sigmoid(g3.astype(f32)).reshape(B, T, KV_H, GRP, 3)
    o = g[..., 0:1] * o_c + g[..., 1:2] * o_s + g[..., 2:3] * o_w
    yb = o.reshape(B, T, D_B).astype(x.dtype) * jax.nn.silu(z_b)
    y = jnp.concatenate([ya, yb], axis=-1) @ w_out
    return y, cmp_new, sel_new, win_state, conv_new


def rwkv_mixer(x, shift_prev, S0, mu, w_rkvz, w0, w1, w2, a0, a1, a2, k_k, k_a, r_k, gn_g, gn_b, w_out):
    f32 = jnp.float32
    B, T, _ = x.shape
    x_prev = jnp.concatenate([shift_prev[:, None].astype(x.dtype), x[:, :-1]], axis=1)
    xsh = x[None] + (x_prev - x)[None] * mu[:, None, None, :]
    r, k, v, z = jnp.einsum('nbtd,nde->nbte', xsh[:4], w_rkvz)
    r, k, v = r.astype(f32), k.astype(f32), v.astype(f32)
    w_raw = -jax.nn.softplus(-(w0 + jnp.tanh(xsh[4] @ w1) @ w2).astype(f32)) - 0.5
    decay = jnp.exp(-jnp.exp(w_raw))
    a = jax.nn.sigmoid((a0 + (xsh[5] @ a1) @ a2).astype(f32))
    kk = (k * k_k).reshape(B, T, H_C, HD_C)
    kk = kk / jnp.maximum(jnp.sqrt(jnp.sum(kk * kk, axis=-1, keepdims=True)), 1e-12)
    k = k * (1.0 + (a - 1.0) * k_a)
    hs = lambda t: t.reshape(B, T, H_C, HD_C)
    r, decay, k, v, a = hs(r), hs(decay), hs(k), hs(v), hs(a)

    def step(S, inp):
        r_t, w_t, k_t, v_t, kk_t, a_t = inp
        sa = jnp.einsum('bhvk,bhk->bhv', S, kk_t)
        S = S * w_t[:, :, None, :] - sa[..., None] * (kk_t * a_t)[:, :, None, :] + v_t[..., None] * k_t[:, :, None, :]
        return S, jnp.einsum('bhvk,bhk->bhv', S, r_t)

    tm = lambda t: jnp.swapaxes(t, 0, 1)
    S_fin, y = lax.scan(step, S0.astype(f32), (tm(r), tm(decay), tm(k), tm(v), tm(kk), tm(a)))
    y = jnp.swapaxes(y, 0, 1)
    mu_y = y.mean(-1, keepdims=True)
    var_y = jnp.square(y - mu_y).mean(-1, keepdims=True)
    y = (y - mu_y) * lax.rsqrt(var_y + GN_EPS) * gn_g.reshape(H_C, HD_C) + gn_b.reshape(H_C, HD_C)
    y = y + jnp.sum(r * k * r_k, axis=-1, keepdims=True) * v
    out = (y.reshape(B, T, D_C).astype(x.dtype) * jax.nn.silu(z)) @ w_out
    return out, S_fin, x[:, -1]


def setup_inputs(seed: int = 0) -> dict:
    key = jax.random.key(seed)
    ks = iter(jax.random.split(key, 48))
    nrm = lambda shape, scale: scale * jax.random.normal(next(ks), shape, jnp.float32)
    n_pages = PAST_LEN // PAGE_SIZE
    used = DEC_BATCH * n_pages
    n_pool = used + max(1, used // 4)
    w_buf = min(WINDOW, PAST_LEN)
    return {
        'x_prompt': nrm((BATCH, SEQ, D_MODEL), 1.0),
        'x_sample': nrm((DEC_BATCH, DEC_SEQ, D_MODEL), 1.0),
        'cache_cmp_kv': nrm((N_EVEN, n_pool, PAGE_SIZE, 2, KV_H, HD_B), 1.0),
        'cache_sel_kv': nrm((N_EVEN, n_pool, PAGE_SIZE, 2, KV_H, HD_B), 1.0),
        'cache_win_kv': nrm((N_EVEN, DEC_BATCH, w_buf, 2, KV_H, HD_B), 1.0),
        'state_conv': nrm((N_EVEN, DEC_BATCH, CONV_W - 1, D_A), 0.5),
        'state_wkv': nrm((N_ODD, DEC_BATCH, H_C, HD_C, HD_C), 0.5),
        'state_shift': nrm((N_ODD, DEC_BATCH, D_MODEL), 1.0),
        'page_table': jax.random.permutation(next(ks), n_pool)[:used].reshape(DEC_BATCH, n_pages).astype(jnp.int32),
        'w_in_even': nrm((N_EVEN, D_MODEL, D_IN_EVEN), D_MODEL ** -0.5),
        'conv_w': nrm((N_EVEN, CONV_W, 1, D_A), CONV_W ** -0.5),
        'conv_b': nrm((N_EVEN, D_A), 0.02),
        'conv_ln_g': 1.0 + nrm((N_EVEN, D_A), 0.02),
        'conv_ln_b': nrm((N_EVEN, D_A), 0.02),
        'wk_cmp': (1.0 + nrm((N_EVEN, L_CMP, KV_H), 0.1)) / L_CMP,
        'wv_cmp': (1.0 + nrm((N_EVEN, L_CMP, KV_H), 0.1)) / L_CMP,
        'w_out_even': nrm((N_EVEN, D_A + D_B, D_MODEL), BETA * (D_A + D_B) ** -0.5),
        'mu_c': jax.random.uniform(next(ks), (N_ODD, 6, D_MODEL), jnp.float32),
        'w_rkvz': nrm((N_ODD, 4, D_MODEL, D_C), D_MODEL ** -0.5),
        'w0': -0.6 + nrm((N_ODD, D_C), 0.5),
        'w1': nrm((N_ODD, D_MODEL, LORA_W), D_MODEL ** -0.5),
        'w2': nrm((N_ODD, LORA_W, D_C), 0.5 * LORA_W ** -0.5),
        'a0': nrm((N_ODD, D_C), 0.3),
        'a1': nrm((N_ODD, D_MODEL, LORA_A), D_MODEL ** -0.5),
        'a2': nrm((N_ODD, LORA_A, D_C), 0.5 * LORA_A ** -0.5),
        'k_k': 0.85 + nrm((N_ODD, D_C), 0.05),
        'k_a': 1.0 + nrm((N_ODD, D_C), 0.05),
        'r_k': nrm((N_ODD, H_C, HD_C), 0.1),
        'gn_g': 1.0 + nrm((N_ODD, D_C), 0.02),
        'gn_b': nrm((N_ODD, D_C), 0.02),
        'w_out_odd': nrm((N_ODD, D_C, D_MODEL), BETA * D_C ** -0.5),
        'ln_g': 1.0 + nrm((DEPTH, D_MODEL), 0.02),
        'ln_b': nrm((DEPTH, D_MODEL), 0.02),
    }


def reference(x_prompt, x_sample, cache_cmp_kv, cache_sel_kv, cache_win_kv, state_conv, state_wkv, state_shift,
              page_table, w_in_even, conv_w, conv_b, conv_ln_g, conv_ln_b, wk_cmp, wv_cmp, w_out_even,
              mu_c, w_rkvz, w0, w1, w2, a0, a1, a2, k_k, k_a, r_k, gn_g, gn_b, w_out_odd, ln_g, ln_b):
    slopes = _alibi_slopes()
    Bp, Tp, _ = x_prompt.shape
    Bs, Ts, _ = x_sample.shape
    past_len = page_table.shape[1] * PAGE_SIZE
    pos_p = jnp.arange(Tp, dtype=jnp.int32)
    pos_s = past_len + jnp.arange(Ts, dtype=jnp.int32)
    xp, xq = x_prompt, x_sample
    cmp_p, cmp_s, sel_p, sel_s, win_p, win_s, conv_p, conv_s = [], [], [], [], [], [], [], []
    wkv_p, wkv_s, sh_p, sh_s = [], [], [], []
    for l in range(DEPTH):
        if l % 2 == 0:
            e = l // 2
            pe = (w_in_even[e], conv_w[e], conv_b[e], conv_ln_g[e], conv_ln_b[e], wk_cmp[e], wv_cmp[e], w_out_even[e], slopes)
            empty = jnp.zeros((Bp, 0, 2, KV_H, HD_B), xp.dtype)
            yp, c1, s1, w1_, v1 = even_mixer(xp, pos_p, jnp.zeros((Bp, CONV_W - 1, D_A), xp.dtype),
                                             empty, empty, empty, *pe, banded=True)
            past_c = cache_cmp_kv[e][page_table].reshape(Bs, past_len, 2, KV_H, HD_B)
            past_s = cache_sel_kv[e][page_table].reshape(Bs, past_len, 2, KV_H, HD_B)
            ys, c2, s2, w2_, v2 = even_mixer(xq, pos_s, state_conv[e], past_c, past_s, cache_win_kv[e],
                                             *pe, banded=False)
            cmp_p.append(c1); cmp_s.append(c2); sel_p.append(s1); sel_s.append(s2)
            win_p.append(w1_); win_s.append(w2_); conv_p.append(v1); conv_s.append(v2)
        else:
            o = l // 2
            po = (mu_c[o], w_rkvz[o], w0[o], w1[o], w2[o], a0[o], a1[o], a2[o], k_k[o], k_a[o], r_k[o],
                  gn_g[o], gn_b[o], w_out_odd[o])
            yp, S1, h1 = rwkv_mixer(xp, jnp.zeros((Bp, D_MODEL), xp.dtype),
                                    jnp.zeros((Bp, H_C, HD_C, HD_C), jnp.float32), *po)
            ys, S2, h2 = rwkv_mixer(xq, state_shift[o], state_wkv[o], *po)
            wkv_p.append(S1); wkv_s.append(S2); sh_p.append(h1); sh_s.append(h2)
        xp = _layer_norm(ALPHA * xp + yp, ln_g[l], ln_b[l])
        xq = _layer_norm(ALPHA * xq + ys, ln_g[l], ln_b[l])
    return (xp, xq,
            jnp.stack(cmp_p), jnp.stack(cmp_s), jnp.stack(sel_p), jnp.stack(sel_s),
            jnp.stack(win_p), jnp.stack(win_s), jnp.stack(conv_p), jnp.stack(conv_s),
            jnp.stack(wkv_p), jnp.stack(wkv_s), jnp.stack(sh_p), jnp.stack(sh_s))
```

```python
import numpy as np
import concourse.bass as bass
import concourse.mybir as mybir
from concourse.bass_utils import run_bass_kernel_spmd

F32 = mybir.dt.float32
BF16 = mybir.dt.bfloat16
I32 = mybir.dt.int32
AF = mybir.ActivationFunctionType
ALU = mybir.AluOpType
AX = mybir.AxisListType

NCORES = 8
T = 2048
NS = 4
TT = T + NS
DM = 1024
import os
NPOOL = int(os.environ.get('KDEV_NPOOL', '2560'))
STAGE = int(os.environ.get('KDEV_STAGE', '99'))


class _Stop(Exception):
    pass


def stage_end(k):
    if STAGE == k:
        raise _Stop()
ENGS = ('pe', 'act', 'dve', 'pool', 'sp')
SAME_ENG_DIST = 3
NDS = {'sp': 8, 'act': 4, 'pool': 6}


class Op:
    __slots__ = ('eng', 'fn', 'r', 'w', 'dma', 'deps', 'signal', 'count', 'dsem', 'dcount',
                 'dprev', 'waits', 'idx', 'eidx', 'bar', 'need')


class Sched:
    def __init__(self):
        self.ops = []
        self.nbar = 0

    def add(self, eng, fn, r=(), w=(), dma=False):
        op = Op()
        op.eng, op.fn, op.r, op.w, op.dma = eng, fn, tuple(r), tuple(w), dma
        op.signal = False
        op.bar = None
        op.count = 0
        self.ops.append(op)
        return op

    def pe(self, fn, r=(), w=()):
        return self.add('pe', fn, r, w)

    def act(self, fn, r=(), w=()):
        return self.add('act', fn, r, w)

    def dve(self, fn, r=(), w=()):
        return self.add('dve', fn, r, w)

    def pool(self, fn, r=(), w=()):
        return self.add('pool', fn, r, w)

    def dma(self, q, fn, r=(), w=()):
        return self.add(q, fn, r, w, dma=True)

    def barrier(self):
        self.nbar += 1
        for e in ENGS:
            op = self.add(e, None)
            op.bar = self.nbar

    def analyze(self):
        ops = self.ops
        last_w = {}
        rd = {}
        eng_ops = {e: [] for e in ENGS}
        last_on_eng = {}
        all_dmas = []
        cur_bar = None
        snap = None
        for i, op in enumerate(ops):
            op.idx = i
            deps = set()
            if op.bar is not None:
                if cur_bar != op.bar:
                    cur_bar = op.bar
                    snap = (dict(last_on_eng), list(all_dmas))
                    all_dmas = []
                for f, j in snap[0].items():
                    if f != op.eng and ops[j].bar is None:
                        deps.add(j)
                deps.update(snap[1])
            else:
                for k in op.r:
                    j = last_w.get(k)
                    if j is not None:
                        deps.add(j)
                for k in op.w:
                    j = last_w.get(k)
                    if j is not None:
                        deps.add(j)
                    rr = rd.get(k)
                    if rr:
                        deps.update(rr[0].values())
                        deps.update(rr[1])
                for k in op.r:
                    rr = rd.setdefault(k, ({}, []))
                    if op.dma:
                        rr[1].append(i)
                    else:
                        rr[0][op.eng] = i
                for k in op.w:
                    last_w[k] = i
                    rd[k] = ({}, [])
            deps.discard(i)
            op.deps = deps
            op.eidx = len(eng_ops[op.eng])
            eng_ops[op.eng].append(op)
            last_on_eng[op.eng] = i
            if op.dma:
                all_dmas.append(i)
        for op in ops:
            need = []
            for j in op.deps:
                d = ops[j]
                if d.dma:
                    need.append(j)
                elif d.eng == op.eng:
                    if op.eng == 'pe' and not op.dma:
                        continue
                    if op.eidx - d.eidx > SAME_ENG_DIST:
                        continue
                    need.append(j)
                else:
                    need.append(j)
            op.need = need
            for j in need:
                if not ops[j].dma:
                    ops[j].signal = True
        self.eng_ops = eng_ops

    def emit(self, nc, sems, dsems):
        ops = self.ops
        eng_ops = self.eng_ops
        for e in ENGS:
            cnt = 0
            for op in eng_ops[e]:
                if op.signal:
                    cnt += 1
                    op.count = cnt
        print('SEMCOUNTS', {e: max([op.count for op in eng_ops[e]] + [0]) for e in ENGS}, {e: len(eng_ops[e]) for e in ENGS})
        finals = {}
        for q, pool in dsems.items():
            uses = [0] * len(pool)
            k = 0
            for op in eng_ops[q]:
                if op.dma:
                    s = k % len(pool)
                    k += 1
                    op.dsem = pool[s]
                    op.dprev = 16 * uses[s]
                    uses[s] += 1
                    op.dcount = 16 * uses[s]
                    finals[id(pool[s])] = (pool[s], op.dcount)
        for e in ENGS:
            waited = {}
            for op in eng_ops[e]:
                ws = []
                for j in sorted(op.need):
                    d = ops[j]
                    if d.dma:
                        sem, val = d.dsem, d.dcount
                    else:
                        sem, val = sems[d.eng], d.count
                    if waited.get(id(sem), 0) >= val:
                        continue
                    waited[id(sem)] = val
                    ws.append((sem, val))
                if op.dma and op.dprev > 0:
                    if waited.get(id(op.dsem), 0) < op.dprev:
                        waited[id(op.dsem)] = op.dprev
                        ws.append((op.dsem, op.dprev))
                op.waits = ws

        def make(e):
            def body(eo):
                for op in eng_ops[e]:
                    for (sem, v) in op.waits:
                        eo.wait_ge(sem, v)
                    if op.fn is not None:
                        ins = op.fn(eo)
                        if op.dma:
                            ins.then_inc(op.dsem, 16)
                        elif op.signal:
                            ins.then_inc(sems[e], 1)
                if e == 'sp':
                    for (sem, v) in finals.values():
                        eo.wait_ge(sem, v)
            return body

        with nc.Block() as block:
            block.tensor(make('pe'))
            block.scalar(make('act'))
            block.vector(make('dve'))
            block.gpsimd(make('pool'))
            block.sync(make('sp'))


class Arena:
    def __init__(self, t32, nbytes):
        self.t32 = t32
        self.tbf = t32.bitcast(BF16)
        self.ti32 = t32.bitcast(I32)
        self.nbytes = nbytes
        self.top = 0
        self.marks = []
        self.peak = 0

    def alloc(self, shape, dt):
        es = 2 if dt == BF16 else 4
        n = 1
        for s in shape[1:]:
            n *= s
        nb = (n * es + 63) // 64 * 64
        off = self.top
        self.top += nb
        self.peak = max(self.peak, self.top)
        assert self.top <= self.nbytes, ("SBUF arena overflow", self.top, self.nbytes)
        base = {BF16: self.tbf, F32: self.t32, I32: self.ti32}[dt]
        v = base[0:shape[0], off // es: off // es + n]
        if len(shape) > 2:
            names = "abcdefg"[:len(shape) - 1]
            pat = "p (%s) -> p %s" % (" ".join(names), " ".join(names))
            v = v.rearrange(pat, **{names[i]: shape[1 + i] for i in range(len(shape) - 2)})
        return v

    def mark(self):
        self.marks.append(self.top)

    def release(self):
        self.top = self.marks.pop()


OUT_SPECS = [
    ("y_prompt", [T, DM]),
    ("y_sample", [NS, DM]),
    ("cmp_p", [T, 256]),
    ("cmp_s", [NS, 256]),
    ("sel_p", [T, 256]),
    ("sel_s", [NS, 256]),
    ("win_p", [512, 256]),
    ("win_s", [NS * 512, 256]),
    ("conv_p", [30, 512]),
    ("conv_s", [NS * 30, 512]),
    ("wkv_p", [16 * 64, 64]),
    ("wkv_s", [NS * 16 * 64, 64]),
    ("shift_p", [1, DM]),
    ("shift_s", [NS, DM]),
]

IN_SPECS = [
    ("xp", [T, DM], F32),
    ("xs", [NS, DM], F32),
    ("cache_cmp_a", [NPOOL * 64, 256], F32),
    ("cache_cmp_b", [NPOOL * 64, 256], F32),
    ("cache_sel_a", [NPOOL * 64, 256], F32),
    ("cache_sel_b", [NPOOL * 64, 256], F32),
    ("cache_win", [NS * 512, 256], F32),
    ("state_conv", [NS * 30, 512], F32),
    ("state_wkv", [NS * 16 * 64, 64], F32),
    ("state_shift", [NS, DM], F32),
    ("page_table", [NS, 64], I32),
    ("w_in", [DM, 3352], F32),
    ("conv_w", [31, 512], F32),
    ("conv_b", [1, 512], F32),
    ("conv_ln_g", [1, 512], F32),
    ("conv_ln_b", [1, 512], F32),
    ("wk_cmp", [32, 2], F32),
    ("wv_cmp", [32, 2], F32),
    ("w_out_even", [DM, DM], F32),
    ("mu_c", [6, DM], F32),
    ("w_rkvz", [4 * DM, DM], F32),
    ("w0", [1, DM], F32),
    ("w1", [DM, 64], F32),
    ("w2", [64, DM], F32),
    ("a0", [1, DM], F32),
    ("a1", [DM, 64], F32),
    ("a2", [64, DM], F32),
    ("k_k", [1, DM], F32),
    ("k_a", [1, DM], F32),
    ("r_k", [1, DM], F32),
    ("gn_g", [1, DM], F32),
    ("gn_b", [1, DM], F32),
    ("w_out_odd", [DM, DM], F32),
    ("ln_g", [2, DM], F32),
    ("ln_b", [2, DM], F32),
]

C_AVAL, C_AGLU, C_ZA, C_Q, C_KV, C_G3, C_ZB = 0, 512, 1024, 1536, 2048, 2816, 2840


def build(stage=99):
    nc = bass.Bass("TRN2", target_bir_lowering=False)
    I = {}
    for name, shape, dt in IN_SPECS:
        I[name] = nc.dram_tensor(name, shape, dt, kind="ExternalInput").ap()
    O = {}
    for name, shape in OUT_SPECS:
        O[name] = nc.dram_tensor(name, shape, F32, kind="ExternalOutput").ap()

    x1_scr = nc.dram_tensor("x1_scr", [TT, DM], F32, kind="Internal").ap()
    r_scr = nc.dram_tensor("r_scr", [128, 8, TT], F32, kind="Internal").ap()
    k_scr = nc.dram_tensor("k_scr", [128, 8, TT], F32, kind="Internal").ap()
    sw_scr = nc.dram_tensor("sw_scr", [128, 8, TT], F32, kind="Internal").ap()
    a_scr = nc.dram_tensor("a_scr", [128, 8, TT], F32, kind="Internal").ap()
    v_scr = nc.dram_tensor("v_scr", [TT, DM], F32, kind="Internal").ap()
    z_scr = nc.dram_tensor("z_scr", [TT, DM], F32, kind="Internal").ap()
    S = Sched()
    ARENA_BYTES = 176 * 1024
    from contextlib import ExitStack
    with ExitStack() as st:
        arena_t = st.enter_context(nc.sbuf_tensor("arena", [128, ARENA_BYTES // 4], F32))
        ps_t = st.enter_context(nc.psum_tensor("psum", [128, 4096], F32))
        sems = {e: st.enter_context(nc.semaphore("sem_" + e)) for e in ENGS}
        dsems = {q: [st.enter_context(nc.semaphore("dsem_%s%d" % (q, i))) for i in range(n)]
                 for q, n in NDS.items()}
        A = Arena(arena_t, ARENA_BYTES)
        ps_bf = ps_t.bitcast(BF16)
        try:

            def PS(b, p0=0, p1=128, c0=0, c1=512):
                return ps_t[p0:p1, b * 512 + c0: b * 512 + c1]

            def PSB(b, p0=0, p1=128, c0=0, c1=1024):
                return ps_bf[p0:p1, b * 1024 + c0: b * 1024 + c1]

            def pk(b):
                return ('ps', b)

            FR = {}

            def freg(e, v):
                if v not in FR:
                    FR[v] = e.to_reg(float(v))
                return FR[v]

            ident_f = A.alloc([128, 128], F32)
            ident_b = A.alloc([128, 128], BF16)
            ones_f = A.alloc([128, 128], F32)

            def mk_ident(e):
                e.memset(ones_f, 1.0)
                return e.affine_select(out=ident_f, in_=ones_f, pattern=[[-1, 128]], compare_op=ALU.is_equal,
                                       fill=freg(e, 0.0), base=0, channel_multiplier=1)
            S.pool(mk_ident, w=['ident_f', 'ones_f'])
            S.dve(lambda e: e.tensor_copy(out=ident_b, in_=ident_f), r=['ident_f'], w=['ident_b'])

            xT_off = A.top
            xT = A.alloc([128, 8, TT], BF16)
            xT_raw32 = arena_t[:, xT_off // 4: xT_off // 4 + 8208]
            L1_BASE = A.top
            yaT = A.alloc([128, 4, TT], BF16)
            Vsel_off = A.top
            Vsel = A.alloc([128, 16, 128], BF16)
            Vwin_off = A.top
            Vwin = A.alloc([128, 16, 128], BF16)
            Vsel_raw32 = arena_t[:, Vsel_off // 4: Vsel_off // 4 + 1024]
            Vwin_raw32 = arena_t[:, Vwin_off // 4: Vwin_off // 4 + 1024]
            kvsv = A.alloc([NS, 2, 128], F32)
            qs_f = A.alloc([68, 2, 4, NS], F32)
            ksn = A.alloc([68, 2, 2, NS], F32)
            gs_s = A.alloc([24, NS], BF16)
            w_in_t = I['w_in'].rearrange("(c p) f -> p c f", p=128)
            A.mark()
            xst = [A.alloc([128, DM], BF16) for _ in range(2)]
            xp_t = I['xp'].rearrange("(n p) d -> n p d", p=128)
            for n in range(16):
                b = n % 2
                S.dma('pool', lambda e, n=n, b=b: e.dma_start(out=xst[b], in_=xp_t[n]), w=[('xst', b)])
                bank = n % 2

                def tr(e, b=b, bank=bank):
                    ins = None
                    for c in range(8):
                        ins = e.transpose(out=PSB(bank, 0, 128, c * 128, (c + 1) * 128),
                                          in_=xst[b][:, c * 128:(c + 1) * 128], identity=ident_b)
                    return ins
                S.pe(tr, r=[('xst', b), 'ident_b'], w=[pk(bank)])
                S.dve(lambda e, n=n, bank=bank: e.tensor_copy(
                    out=xT[:, :, n * 128:(n + 1) * 128],
                    in_=PSB(bank).rearrange("p (c t) -> p c t", c=8)),
                    r=[pk(bank)], w=[('xT', n)])
            S.dma('pool', lambda e: e.dma_start(out=xst[0][0:NS, :], in_=I['xs']), w=[('xst', 0)])

            def trs(e):
                ins = None
                for c in range(8):
                    ins = e.transpose(out=PSB(0, 0, 128, c * NS, (c + 1) * NS),
                                      in_=xst[0][0:NS, c * 128:(c + 1) * 128], identity=ident_b[0:NS, 0:NS])
                return ins
            S.pe(trs, r=[('xst', 0), 'ident_b'], w=[pk(0)])
            S.dve(lambda e: e.tensor_copy(out=xT[:, :, T:TT],
                                          in_=PSB(0, 0, 128, 0, 8 * NS).rearrange("p (c t) -> p c t", c=8)),
                  r=[pk(0)], w=[('xT', 16)])
            XT_ALL = [('xT', n) for n in range(17)]
            S.barrier()
            A.release()
            stage_end(1)

            A.mark()
            wkv = A.alloc([128, 8, 768], BF16)
            kvst = [A.alloc([128, 768], F32) for _ in range(2)]
            winb = [A.alloc([128, 4, 256], F32) for _ in range(2)]
            for c in range(8):
                S.dma('pool', lambda e, c=c: e.dma_start(out=wkv[:, c, :], in_=w_in_t[:, c, C_KV:C_KV + 768]),
                      w=[('wkv', c)])
            WKV = [('wkv', c) for c in range(8)]
            for n in range(16):
                bA, bB = 4 + (n % 2) * 2, 5 + (n % 2) * 2

                def mmkv(e, n=n, bA=bA, bB=bB):
                    ins = None
                    for c in range(8):
                        ins = e.matmul(PS(bA), lhsT=xT[:, c, n * 128:(n + 1) * 128], rhs=wkv[:, c, 0:512],
                                       start=(c == 0), stop=(c == 7))
                    for c in range(8):
                        ins = e.matmul(PS(bB, 0, 128, 0, 256), lhsT=xT[:, c, n * 128:(n + 1) * 128],
                                       rhs=wkv[:, c, 512:768], start=(c == 0), stop=(c == 7))
                    return ins
                S.pe(mmkv, r=[('xT', n)] + WKV, w=[pk(bA), pk(bB)])
                sb = n % 2
                S.act(lambda e, sb=sb, bA=bA: e.copy(out=kvst[sb][:, 0:512], in_=PS(bA)),
                      r=[pk(bA)], w=[('kvst', sb, 0)])
                S.act(lambda e, sb=sb, bB=bB: e.copy(out=kvst[sb][:, 512:768], in_=PS(bB, 0, 128, 0, 256)),
                      r=[pk(bB)], w=[('kvst', sb, 1)])
                S.dma('sp', lambda e, n=n, sb=sb: e.dma_start(out=O['cmp_p'][n * 128:(n + 1) * 128, :],
                                                               in_=kvst[sb][:, 0:256]), r=[('kvst', sb, 0)])
                S.dma('sp', lambda e, n=n, sb=sb: e.dma_start(out=O['sel_p'][n * 128:(n + 1) * 128, :],
                                                               in_=kvst[sb][:, 256:512]), r=[('kvst', sb, 0)])
                if n >= 12:
                    S.dma('sp', lambda e, n=n, sb=sb: e.dma_start(
                        out=O['win_p'][(n - 12) * 128:(n - 11) * 128, :], in_=kvst[sb][:, 512:768]),
                        r=[('kvst', sb, 1)])
                S.pool(lambda e, n=n, sb=sb: e.tensor_copy(out=Vsel[:, n, :], in_=kvst[sb][:, 384:512]),
                       r=[('kvst', sb, 0)], w=[('Vsel', n)])
                S.pool(lambda e, n=n, sb=sb: e.tensor_copy(out=Vwin[:, n, :], in_=kvst[sb][:, 640:768]),
                       r=[('kvst', sb, 1)], w=[('Vwin', n)])
            stage_end(21)
            def mmkvs(e):
                ins = None
                for c in range(8):
                    ins = e.matmul(PS(4, 0, NS, 0, 512), lhsT=xT[:, c, T:TT], rhs=wkv[:, c, 0:512],
                                   start=(c == 0), stop=(c == 7))
                for c in range(8):
                    ins = e.matmul(PS(5, 0, NS, 0, 256), lhsT=xT[:, c, T:TT], rhs=wkv[:, c, 512:768],
                                   start=(c == 0), stop=(c == 7))
                return ins
            S.pe(mmkvs, r=[('xT', 16)] + WKV, w=[pk(4), pk(5)])
            S.act(lambda e: e.copy(out=kvst[0][0:NS, 0:512], in_=PS(4, 0, NS, 0, 512)), r=[pk(4)], w=[('kvst', 0, 0)])
            S.act(lambda e: e.copy(out=kvst[0][0:NS, 512:768], in_=PS(5, 0, NS, 0, 256)), r=[pk(5)], w=[('kvst', 0, 1)])
            S.dma('sp', lambda e: e.dma_start(out=O['cmp_s'], in_=kvst[0][0:NS, 0:256]), r=[('kvst', 0, 0)])
            S.dma('sp', lambda e: e.dma_start(out=O['sel_s'], in_=kvst[0][0:NS, 256:512]), r=[('kvst', 0, 0)])
            win_s_v = O['win_s'].rearrange("(s r) c -> s r c", r=512)
            S.dma('sp', lambda e: e.dma_start(out=win_s_v[:, 511, :], in_=kvst[0][0:NS, 512:768]), r=[('kvst', 0, 1)])
            S.pool(lambda e: e.tensor_copy(out=kvsv[0:NS, 0, :], in_=kvst[0][0:NS, 384:512]),
                   r=[('kvst', 0, 0)], w=['kvsv'])
            S.pool(lambda e: e.tensor_copy(out=kvsv[0:NS, 1, :], in_=kvst[0][0:NS, 640:768]),
                   r=[('kvst', 0, 1)], w=['kvsv'])
            stage_end(22)
            for s in range(NS):
                wb_ = winb[s % 2]
                src = I['cache_win'][s * 512:(s + 1) * 512, :].rearrange("(p j) c -> p j c", j=4)
                dst = O['win_s'][s * 512:(s + 1) * 512, :].rearrange("(p j) c -> p j c", j=4)
                S.dma('sp', lambda e, wb_=wb_, src=src: e.dma_start(out=wb_, in_=src), w=[('winb', s % 2)])
                S.dma('sp', lambda e, wb_=wb_, dst=dst: e.dma_start(out=dst[:, 0:3, :], in_=wb_[:, 1:4, :]),
                      r=[('winb', s % 2)])
                S.dma('sp', lambda e, wb_=wb_, dst=dst: e.dma_start(out=dst[0:127, 3, :], in_=wb_[1:128, 0, :]),
                      r=[('winb', s % 2)])
            stage_end(23)
            S.barrier()
            A.release()
            stage_end(2)
            BLKS = [(tb * 512, 512) for tb in range(4)] + [(T, NS)]

            def xT_keys(c0, n):
                if c0 >= T:
                    return [('xT', 16)]
                return [('xT', c0 // 128 + i) for i in range(max(1, n // 128))]

            pbank = [0]

            def proj_multi(units, blks=BLKS):
                for (c0, n) in blks:
                    for (wt, wkey, M, evac, aug) in units:
                        bank = pbank[0] % 4
                        pbank[0] += 1

                        def mm(e, c0=c0, n=n, bank=bank, wt=wt, M=M, aug=aug):
                            ins = None
                            for c in range(8):
                                ins = e.matmul(PS(bank, 0, M, 0, n), lhsT=wt[:, c, :], rhs=xT[:, c, c0:c0 + n],
                                               start=(c == 0), stop=(c == 7 and aug is None))
                            if aug is not None:
                                ins = e.matmul(PS(bank, 0, M, 0, n), lhsT=aug, rhs=bas[0:3, c0:c0 + n],
                                               start=False, stop=True)
                            return ins
                        S.pe(mm, r=[wkey, 'bas', 'bas0', 'coef'] + xT_keys(c0, n), w=[pk(bank)])
                        evac(bank, c0, n)

            def proj_fm(wt, wkey, M, evac, aug=None, blks=BLKS):
                proj_multi([(wt, wkey, M, evac, aug)], blks)


            A.mark()
            NWB = 4
            wbuf = [A.alloc([128, 8, 128], BF16) for _ in range(NWB)]
            wctr = [0]

            def load_w(col0, M):
                i = wctr[0] % NWB
                wctr[0] += 1
                S.dma('pool', lambda e: e.dma_start(out=wbuf[i][:, :, 0:M], in_=w_in_t[:, :, col0:col0 + M]),
                      w=[('wbuf', i)])
                return wbuf[i][:, :, 0:M], ('wbuf', i)
            u_ext = A.alloc([128, 4, 30 + T], BF16)
            sza = A.alloc([128, 4, TT], BF16)
            c32 = A.alloc([128, 4, TT], F32)
            u32t = A.alloc([128, 4, 32], F32)
            us32 = A.alloc([128, 4, NS], F32)
            cw_sb = A.alloc([31, 512], F32)
            cwT = A.alloc([128, 4, 31], F32)
            cvec = A.alloc([128, 3, 4], F32)
            A.mark()
            sig = [A.alloc([128, 512], F32) for _ in range(2)]
            sigc = [0]
            S.pool(lambda e: e.memset(u_ext[:, :, 0:30], 0.0), w=[('uext', 'h')])
            for j in range(4):
                wtg, wkg = load_w(C_AGLU + j * 128, 128)
                wta, wka = load_w(C_AVAL + j * 128, 128)
                wtz, wkz = load_w(C_ZA + j * 128, 128)

                def ev_sig(bank, c0, n, j=j):
                    sb = sigc[0] % 2
                    S.act(lambda e: e.activation(out=sig[sb][:, 0:n], in_=PS(bank, 0, 128, 0, n), func=AF.Sigmoid),
                          r=[pk(bank)], w=[('sig', sb)])

                def ev_u(bank, c0, n, j=j):
                    sb = sigc[0] % 2
                    sigc[0] += 1
                    if c0 < T:
                        S.dve(lambda e: e.tensor_tensor(out=u_ext[:, j, 30 + c0:30 + c0 + n], in0=PS(bank, 0, 128, 0, n),
                                                        in1=sig[sb][:, 0:n], op=ALU.mult),
                              r=[pk(bank), ('sig', sb)], w=[('uext', j, c0)])
                        if c0 == 1536:
                            S.dve(lambda e: e.tensor_tensor(out=u32t[:, j, :], in0=PS(bank, 0, 128, 480, 512),
                                                            in1=sig[sb][:, 480:512], op=ALU.mult),
                                  r=[pk(bank), ('sig', sb)], w=[('u32t', j)])
                    else:
                        S.dve(lambda e: e.tensor_tensor(out=us32[:, j, :], in0=PS(bank, 0, 128, 0, n),
                                                        in1=sig[sb][:, 0:n], op=ALU.mult),
                              r=[pk(bank), ('sig', sb)], w=[('us32', j)])

                def ev_za(bank, c0, n, j=j):
                    S.act(lambda e: e.activation(out=sza[:, j, c0:c0 + n], in_=PS(bank, 0, 128, 0, n), func=AF.Silu),
                          r=[pk(bank)], w=[('sza', j, c0)])
                proj_multi([(wtg, wkg, 128, ev_sig, None), (wta, wka, 128, ev_u, None), (wtz, wkz, 128, ev_za, None)])

            stage_end(3)
            S.dma('sp', lambda e: e.dma_start(out=cw_sb, in_=I['conv_w']), w=['cw_sb'])

            def trcw(e):
                ins = None
                for j in range(4):
                    ins = e.transpose(out=PS(7, 0, 128, j * 32, j * 32 + 31), in_=cw_sb[:, j * 128:(j + 1) * 128],
                                      identity=ident_f[0:31, 0:31])
                return ins
            S.pe(trcw, r=['cw_sb', 'ident_f'], w=[pk(7)])
            S.dve(lambda e: e.tensor_copy(out=cwT, in_=PS(7, 0, 128, 0, 128).rearrange("p (j t) -> p j t", j=4)[:, :, 0:31]),
                  r=[pk(7)], w=['cwT'])
            for i, nm in enumerate(['conv_b', 'conv_ln_g', 'conv_ln_b']):
                S.dma('sp', lambda e, i=i, nm=nm: e.dma_start(out=cvec[:, i, :],
                                                              in_=I[nm].rearrange("o (j p) -> p (o j)", p=128),
                                                              allow_slow_non_contiguous=True),
                      w=[('cvec', i)])

            diag = [A.alloc([128, 31, 128], BF16) for _ in range(2)]
            for j in range(4):
                db = j % 2
                for tap in range(31):
                    S.pool(lambda e, j=j, tap=tap, db=db: e.tensor_scalar(
                        out=diag[db][:, tap, :], in0=ident_b, scalar1=cwT[:, j, tap:tap + 1], scalar2=None, op0=ALU.mult),
                        r=['cwT', 'ident_b'], w=[('diag', db, tap)])
                for tb in range(4):
                    bank = 4 + (tb % 2)

                    def cm(e, j=j, tb=tb, db=db, bank=bank):
                        ins = None
                        for tap in range(31):
                            ins = e.matmul(PS(bank), lhsT=diag[db][:, tap, :],
                                           rhs=u_ext[:, j, tb * 512 + tap: tb * 512 + tap + 512],
                                           start=(tap == 0), stop=(tap == 30))
                        return ins
                    rk = [('diag', db, tap) for tap in range(31)] + [('uext', j, tb * 512)]
                    rk += [('uext', j, (tb - 1) * 512)] if tb > 0 else [('uext', 'h')]
                    S.pe(cm, r=rk, w=[pk(bank)])
                    S.act(lambda e, j=j, tb=tb, bank=bank: e.activation(
                        out=c32[:, j, tb * 512:(tb + 1) * 512], in_=PS(bank), func=AF.Identity, bias=cvec[:, 0, j:j + 1]),
                        r=[pk(bank), ('cvec', 0)], w=[('c32', j, tb * 512)])

            stage_end(4)
            exts = A.alloc([128, 4, NS, 31], F32)
            scv = A.alloc([30, NS, 512], F32)
            S.dma('sp', lambda e: e.dma_start(out=scv, in_=I['state_conv'].rearrange("(s r) c -> r s c", r=30)), w=['scv'])
            for s in range(NS):
                def trs_(e, s=s):
                    ins = None
                    for j in range(4):
                        ins = e.transpose(out=PS(6, 0, 128, j * 32, j * 32 + 30), in_=scv[:, s, j * 128:(j + 1) * 128],
                                          identity=ident_f[0:30, 0:30])
                    return ins
                S.pe(trs_, r=['scv', 'ident_f'], w=[pk(6)])
                S.dve(lambda e, s=s: e.tensor_copy(
                    out=exts[:, :, s, 0:30], in_=PS(6, 0, 128, 0, 128).rearrange("p (j t) -> p j t", j=4)[:, :, 0:30]),
                    r=[pk(6)], w=[('exts', s)])
            S.dve(lambda e: e.tensor_copy(out=exts[:, :, :, 30], in_=us32),
                  r=[('us32', j) for j in range(4)], w=[('exts', 'u')])
            prod = A.alloc([128, 4, NS, 31], F32)
            S.dve(lambda e: e.tensor_tensor(out=prod, in0=exts, in1=cwT.unsqueeze(2).broadcast_to([128, 4, NS, 31]),
                                            op=ALU.mult),
                  r=[('exts', s) for s in range(NS)] + [('exts', 'u'), 'cwT'], w=['prod'])
            cs_ = A.alloc([128, 4, NS], F32)
            S.dve(lambda e: e.tensor_reduce(out=cs_, in_=prod, axis=AX.X, op=ALU.add), r=['prod'], w=['cs'])
            S.dve(lambda e: e.tensor_tensor(out=c32[:, :, T:TT], in0=cs_,
                                            in1=cvec[:, 0, :].unsqueeze(2).broadcast_to([128, 4, NS]), op=ALU.add),
                  r=['cs', ('cvec', 0)], w=[('c32', j, T) for j in range(4)])

            cvo = A.alloc([32, 512], F32)

            def tru(e):
                ins = None
                for j in range(4):
                    ins = e.transpose(out=PS(6, 0, 32, j * 128, (j + 1) * 128), in_=u32t[:, j, :], identity=ident_f)
                return ins
            S.pe(tru, r=[('u32t', j) for j in range(4)] + ['ident_f'], w=[pk(6)])
            S.act(lambda e: e.copy(out=cvo, in_=PS(6, 0, 32, 0, 512)), r=[pk(6)], w=['cvo'])
            S.dma('sp', lambda e: e.dma_start(out=O['conv_p'], in_=cvo[2:32, :]), r=['cvo'])
            cso = A.alloc([NS, 512], F32)

            def trus(e):
                ins = None
                for j in range(4):
                    ins = e.transpose(out=PS(6, 0, NS, j * 128, (j + 1) * 128), in_=us32[:, j, :], identity=ident_f)
                return ins
            S.pe(trus, r=[('us32', j) for j in range(4)] + ['ident_f'], w=[pk(6)])
            S.act(lambda e: e.copy(out=cso, in_=PS(6, 0, NS, 0, 512)), r=[pk(6)], w=['cso'])
            conv_s_v = O['conv_s'].rearrange("(s r) c -> s r c", r=30)
            S.dma('sp', lambda e: e.dma_start(out=conv_s_v[:, 29, :], in_=cso), r=['cso'])
            for s in range(NS):
                S.dma('sp', lambda e, s=s: e.dma_start(out=conv_s_v[s, 0:29, :], in_=scv[1:30, s, :]), r=['scv'])

            stage_end(5)
            S.barrier()
            A.release()
            onesm = A.alloc([128, 128], F32)
            S.pool(lambda e: e.memset(onesm, 1.0 / 512.0), w=['onesm'])
            sq4 = A.alloc([128, 4, 512], F32)
            mean_sb = A.alloc([128, 512], F32)
            rstd_sb = A.alloc([128, 512], F32)
            tmpv = A.alloc([128, 512], F32)
            tj = [A.alloc([128, 512], F32) for _ in range(2)]
            tjc = 0
            for (c0, n) in BLKS:
                ck = [('c32', j, c0) for j in range(4)]
                S.act(lambda e, c0=c0, n=n: e.activation(out=sq4[:, :, 0:n], in_=c32[:, :, c0:c0 + n], func=AF.Square),
                      r=ck, w=['sq4'])

                def stm(e, c0=c0, n=n):
                    ins = None
                    for j in range(4):
                        ins = e.matmul(PS(0, 0, 128, 0, n), lhsT=onesm, rhs=c32[:, j, c0:c0 + n], start=(j == 0), stop=(j == 3))
                    for j in range(4):
                        ins = e.matmul(PS(1, 0, 128, 0, n), lhsT=onesm, rhs=sq4[:, j, 0:n], start=(j == 0), stop=(j == 3))
                    return ins
                S.pe(stm, r=ck + ['sq4', 'onesm'], w=[pk(0), pk(1)])
                S.act(lambda e, n=n: e.copy(out=mean_sb[:, 0:n], in_=PS(0, 0, 128, 0, n)), r=[pk(0)], w=['mean_sb'])
                S.dve(lambda e, n=n: e.tensor_tensor(out=tmpv[:, 0:n], in0=mean_sb[:, 0:n], in1=mean_sb[:, 0:n], op=ALU.mult),
                      r=['mean_sb'], w=['tmpv'])
                S.dve(lambda e, n=n: e.tensor_tensor(out=tmpv[:, 0:n], in0=PS(1, 0, 128, 0, n), in1=tmpv[:, 0:n],
                                                     op=ALU.subtract), r=[pk(1), 'tmpv'], w=['tmpv'])
                S.dve(lambda e, n=n: e.tensor_scalar(out=tmpv[:, 0:n], in0=tmpv[:, 0:n], scalar1=1e-5, scalar2=None,
                                                     op0=ALU.add), r=['tmpv'], w=['tmpv'])
                S.act(lambda e, n=n: e.activation(out=tmpv[:, 0:n], in_=tmpv[:, 0:n], func=AF.Ln), r=['tmpv'], w=['tmpv'])
                S.act(lambda e, n=n: e.activation(out=rstd_sb[:, 0:n], in_=tmpv[:, 0:n], func=AF.Exp, scale=-0.5),
                      r=['tmpv'], w=['rstd_sb'])
                for j in range(4):
                    tb_ = tj[tjc % 2]
                    tk = ('tj', tjc % 2)
                    tjc += 1
                    S.dve(lambda e, j=j, c0=c0, n=n, tb_=tb_: e.tensor_tensor(
                        out=tb_[:, 0:n], in0=c32[:, j, c0:c0 + n], in1=mean_sb[:, 0:n], op=ALU.subtract),
                        r=[('c32', j, c0), 'mean_sb'], w=[tk])
                    S.dve(lambda e, n=n, tb_=tb_: e.tensor_tensor(out=tb_[:, 0:n], in0=tb_[:, 0:n], in1=rstd_sb[:, 0:n],
                                                                  op=ALU.mult), r=[tk, 'rstd_sb'], w=[tk])
                    S.dve(lambda e, j=j, n=n, tb_=tb_: e.tensor_scalar(
                        out=tb_[:, 0:n], in0=tb_[:, 0:n], scalar1=cvec[:, 1, j:j + 1], scalar2=cvec[:, 2, j:j + 1],
                        op0=ALU.mult, op1=ALU.add), r=[tk, ('cvec', 1), ('cvec', 2)], w=[tk])
                    S.act(lambda e, n=n, tb_=tb_: e.activation(out=tb_[:, 0:n], in_=tb_[:, 0:n], func=AF.Silu),
                          r=[tk], w=[tk])
                    S.pool(lambda e, j=j, c0=c0, n=n, tb_=tb_: e.tensor_tensor(
                        out=yaT[:, j, c0:c0 + n], in0=tb_[:, 0:n], in1=sza[:, j, c0:c0 + n], op=ALU.mult),
                        r=[tk, ('sza', j, c0)], w=[('yaT', j, c0)])
            S.barrier()
            A.release()
            stage_end(6)
            FORCE = 1.0e4
            BIG = 30000.0
            ybT = A.alloc([64, 8, TT], BF16)
            A.mark()
            Qaug = [A.alloc([128, 4, TT], BF16) for g in range(2)]
            Ksel = [A.alloc([128, TT], BF16) for g in range(2)]
            Kwin = [A.alloc([68, TT], BF16) for g in range(2)]
            gsig = A.alloc([24, TT], BF16)
            KCa = [A.alloc([68, 64], BF16) for g in range(2)]
            VC = [A.alloc([64, 64], BF16) for g in range(2)]
            kcT = A.alloc([64, 2, 2, 64], F32)
            Wkv = A.alloc([64, 2, 2, 32], F32)
            oh = A.alloc([24, 24, 64], BF16)
            ones64 = A.alloc([128, 64], BF16)
            Mpad = A.alloc([128, 128], BF16)
            A.mark()
            NWB = 4
            wpad = [A.alloc([128, 8, 68], BF16) for _ in range(NWB)]
            for i in range(NWB):
                S.pool(lambda e, i=i: e.memset(wpad[i], 0.0), w=[('wpad', i)])
            wpc = [0]

            def load_wpad(col0, M=64):
                i = wpc[0] % NWB
                wpc[0] += 1
                S.dma('pool', lambda e: e.dma_start(out=wpad[i][:, :, 0:M], in_=w_in_t[:, :, col0:col0 + M]),
                      w=[('wpad', i)])
                return wpad[i], ('wpad', i)
            bas = A.alloc([3, TT], BF16)
            basc = A.alloc([3, 64], BF16)
            brow = A.alloc([1, 2, TT], BF16)
            browc = A.alloc([1, 2, 64], BF16)
            coefQ = A.alloc([3, 8, 68], BF16)
            coefK = A.alloc([3, 68], BF16)
            cq0 = A.alloc([1, 3, 8, 68], BF16)
            ck0 = A.alloc([1, 3, 68], BF16)

            def mkbas(e):
                e.iota(brow[:, 0, 0:T], pattern=[[1, 32], [0, 64]], base=0, channel_multiplier=0,
                       allow_small_or_imprecise_dtypes=True)
                e.iota(brow[:, 1, 0:T], pattern=[[0, 32], [1, 64]], base=0, channel_multiplier=0,
                       allow_small_or_imprecise_dtypes=True)
                e.memset(brow[:, 0, T:TT], 128.0)
                e.memset(brow[:, 1, T:TT], 0.0)
                e.iota(browc[:, 0, :], pattern=[[1, 32], [0, 2]], base=0, channel_multiplier=0,
                       allow_small_or_imprecise_dtypes=True)
                e.iota(browc[:, 1, :], pattern=[[0, 32], [32, 2]], base=31, channel_multiplier=0,
                       allow_small_or_imprecise_dtypes=True)
                e.memset(bas[0:1, :], 1.0)
                e.memset(basc[0:1, :], 1.0)
                e.memset(cq0, 0.0)
                e.memset(ck0, 0.0)
                for h in range(8):
                    sl = 2.0 ** (-(h + 1))
                    e.memset(cq0[:, 0, h, 64:65], 8.0 * sl * 64.0)
                    e.memset(cq0[:, 0, h, 65:66], 8.0 * sl)
                    e.memset(cq0[:, 1, h, 66:67], -8.0 * sl * 64.0)
                    e.memset(cq0[:, 2, h, 67:68], -8.0 * sl)
                e.memset(ck0[:, 0, 66:68], 1.0)
                e.memset(ck0[:, 1, 64:65], 1.0)
                e.memset(ck0[:, 2, 65:66], 1.0)
                e.memset(ones64, 1.0)
                e.memset(Mpad, 0.0)
                return e.tensor_copy(out=oh, in_=ident_b[0:24, 0:24].unsqueeze(2).broadcast_to([24, 24, 64]))
            S.pool(mkbas, r=['ident_b'], w=['brow', 'bas0', 'ones64', 'Mpad', 'oh'])
            for i in range(2):
                S.dma('sp', lambda e, i=i: e.dma_start(out=bas[1 + i:2 + i, :], in_=brow[0:1, i, :]), r=['brow'], w=['bas'])
                S.dma('sp', lambda e, i=i: e.dma_start(out=basc[1 + i:2 + i, :], in_=browc[0:1, i, :]), r=['brow'], w=['bas'])
            for i in range(3):
                S.dma('sp', lambda e, i=i: e.dma_start(out=coefQ[i:i + 1, :, :], in_=cq0[0:1, i, :, :]), r=['brow'], w=['coef'])
                S.dma('sp', lambda e, i=i: e.dma_start(out=coefK[i:i + 1, :], in_=ck0[0:1, i, :]), r=['brow'], w=['coef'])

            QK = [[('Q', g, c0) for (c0, n) in BLKS] + [('Qm', g, qt) for qt in range(17)] for g in range(2)]
            KK = [[('Ks', g, c0) for (c0, n) in BLKS] for g in range(2)]
            for g in range(2):
                S.pool(lambda e, g=g: e.memset(Qaug[g][64:128, :, :], 0.0), w=QK[g])

                def kinit(e, g=g):
                    e.memset(Ksel[g][64:128, :], 0.0)
                    e.memset(Ksel[g][96:128, 0:T], 1.0)
                    e.affine_select(out=Ksel[g][96:128, 0:T], in_=Ksel[g][96:128, 0:T], pattern=[[1, T]],
                                    compare_op=ALU.is_ge, fill=freg(e, 0.0), base=0, channel_multiplier=-64)
                    return e.affine_select(out=Ksel[g][96:128, 0:T], in_=Ksel[g][96:128, 0:T], pattern=[[-1, T]],
                                           compare_op=ALU.is_ge, fill=freg(e, 0.0), base=63, channel_multiplier=64)
                S.pool(kinit, w=KK[g])

            stage_end(7)
            wt, wk_ = load_wpad(C_G3, 24)

            def ev_g(bank, c0, n):
                S.act(lambda e: e.activation(out=gsig[0:24, c0:c0 + n], in_=PS(bank, 0, 24, 0, n), func=AF.Sigmoid),
                      r=[pk(bank)], w=[('gsig', c0)])
            proj_fm(wt[:, :, 0:24], wk_, 24, ev_g)
            for h in range(8):
                wt, wk_ = load_wpad(C_ZB + h * 64, 64)

                def ev_zb(bank, c0, n, h=h):
                    S.act(lambda e: e.activation(out=ybT[0:64, h, c0:c0 + n], in_=PS(bank, 0, 64, 0, n), func=AF.Silu),
                          r=[pk(bank)], w=[('ybT', h, c0)])
                proj_fm(wt[:, :, 0:64], wk_, 64, ev_zb)
            for h in range(8):
                g, r_ = h // 4, h % 4
                wt, wk_ = load_wpad(C_Q + h * 64)

                def ev_q(bank, c0, n, g=g, r_=r_):
                    S.dve(lambda e: e.tensor_scalar(out=Qaug[g][0:68, r_, c0:c0 + n], in0=PS(bank, 0, 68, 0, n),
                                                    scalar1=0.125, scalar2=None, op0=ALU.mult),
                          r=[pk(bank)], w=[('Q', g, c0)])
                proj_fm(wt, wk_, 68, ev_q, aug=coefQ[0:3, h, :])
            augK = coefK[0:3, :]
            for g in range(2):
                wt, wk_ = load_wpad(C_KV + 256 + g * 64)

                def ev_ks(bank, c0, n, g=g):
                    S.act(lambda e: e.copy(out=Ksel[g][0:68, c0:c0 + n], in_=PS(bank, 0, 68, 0, n)),
                          r=[pk(bank)], w=[('Ks', g, c0)])
                proj_fm(wt, wk_, 68, ev_ks, aug=augK)
                wt, wk_ = load_wpad(C_KV + 512 + g * 64)

                def ev_kw(bank, c0, n, g=g):
                    S.act(lambda e: e.copy(out=Kwin[g][0:68, c0:c0 + n], in_=PS(bank, 0, 68, 0, n)),
                          r=[pk(bank)], w=[('Kw', g, c0)])
                proj_fm(wt, wk_, 68, ev_kw, aug=augK)
            for kv_, nm in enumerate(['wk_cmp', 'wv_cmp']):
                for g in range(2):
                    S.dma('sp', lambda e, kv_=kv_, nm=nm, g=g: e.dma_start(
                        out=Wkv[:, kv_, g:g + 1, :], in_=I[nm][:, g:g + 1].rearrange("l o -> o l").partition_broadcast(64),
                        allow_slow_non_contiguous=True), w=[('Wkv', kv_, g)])
            ptmp = [A.alloc([64, 16, 32], F32) for _ in range(1)]
            pcnt = [0]
            for kv_ in range(2):
                for g in range(2):
                    wt, wk_ = load_wpad(C_KV + kv_ * 128 + g * 64, 64)

                    def ev_pool(bank, c0, n, kv_=kv_, g=g):
                        pb = 0
                        pcnt[0] += 1
                        S.dve(lambda e: e.tensor_tensor(
                            out=ptmp[pb], in0=PS(bank, 0, 64, 0, 512).rearrange("p (a l) -> p a l", l=32),
                            in1=Wkv[:, kv_, g:g + 1, :].broadcast_to([64, 16, 32]), op=ALU.mult),
                            r=[pk(bank), ('Wkv', kv_, g)], w=[('ptmp', pb)])
                        S.dve(lambda e: e.tensor_reduce(out=kcT[:, kv_, g, c0 // 32:c0 // 32 + 16], in_=ptmp[pb],
                                                        axis=AX.X, op=ALU.add),
                              r=[('ptmp', pb)], w=[('kcT', kv_, g, c0)])
                    proj_fm(wt[:, :, 0:64], wk_, 64, ev_pool, blks=BLKS[0:4])
            for g in range(2):
                kck = [('kcT', 0, g, c0) for (c0, n) in BLKS[0:4]]
                vck = [('kcT', 1, g, c0) for (c0, n) in BLKS[0:4]]
                S.dve(lambda e, g=g: e.tensor_copy(out=KCa[g][0:64, :], in_=kcT[:, 0, g, :]), r=kck, w=[('KCa', g)])

                def mmaug(e, g=g):
                    return e.matmul(PS(7, 0, 68, 0, 64), lhsT=coefK[0:3, :], rhs=basc[0:3, :], start=True, stop=True)
                S.pe(mmaug, r=['coef', 'bas', 'bas0'], w=[pk(7)])
                S.act(lambda e, g=g: e.copy(out=KCa[g][64:68, :], in_=PS(7, 64, 68, 0, 64)), r=[pk(7)], w=[('KCa', g)])
                S.pe(lambda e, g=g: e.transpose(out=PS(6, 0, 64, 0, 64), in_=kcT[:, 1, g, :], identity=ident_f[0:64, 0:64]),
                     r=vck + ['ident_f'], w=[pk(6)])
                S.act(lambda e, g=g: e.copy(out=VC[g], in_=PS(6, 0, 64, 0, 64)), r=[pk(6)], w=[('VC', g)])

            stage_end(8)
            S.barrier()
            A.release()
            Ec = A.alloc([128, 4, 64], F32)
            sums_c = A.alloc([128, 4], F32)
            imp64 = A.alloc([128, 64], F32)
            imp = A.alloc([128, 32], F32)
            imp2 = A.alloc([128, 32], F32)
            m8 = A.alloc([128, 16], F32)
            PT = [A.alloc([128, 512], BF16) for _ in range(3)]
            acc = [A.alloc([64, 512], F32) for _ in range(2)]
            rs_ = [A.alloc([64, 512], F32) for _ in range(2)]
            tt_ = [A.alloc([64, 512], F32) for _ in range(2)]
            ptc = [0]
            sbc = [0]
            rsc = [0]

            def qkeys(g, qt):
                return [('Q', g, (qt // 4) * 512), ('Qm', g, qt)]

            def attn_pair(g, qt, lhsT, lkeys, KKrows, vt, vkeys, masks, ob, sb_, first, last):
                q0 = qt * 128
                sbank = sbc[0] % 2
                sbc[0] += 1
                pt = PT[ptc[0] % 3]
                ptk = ('PT', ptc[0] % 3)
                ptc[0] += 1
                M = lhsT.shape[1]
                S.pe(lambda e: e.matmul(PS(sbank, 0, M, 0, 512), lhsT=lhsT, rhs=Qaug[g][0:KKrows, :, q0:q0 + 128],
                                        start=True, stop=True),
                     r=lkeys + qkeys(g, qt), w=[pk(sbank)])
                S.act(lambda e: e.activation(out=pt[0:M, :], in_=PS(sbank, 0, M, 0, 512), func=AF.Exp),
                      r=[pk(sbank)], w=[ptk])
                for (base, cm, pat) in masks:
                    S.pool(lambda e, base=base, cm=cm, pat=pat: e.affine_select(
                        out=pt[0:M, :], in_=pt[0:M, :], pattern=pat, compare_op=ALU.is_ge, fill=freg(e, 0.0), base=base,
                        channel_multiplier=cm), r=[ptk], w=[ptk])

                def pv(e):
                    e.matmul(PS(ob, 0, 64, 0, 512), lhsT=vt, rhs=pt[0:M, :], start=first, stop=last)
                    return e.matmul(PS(sb_, 0, 64, 0, 512), lhsT=ones64[0:M, :], rhs=pt[0:M, :], start=first, stop=last)
                S.pe(pv, r=[ptk, 'ones64'] + vkeys, w=[pk(ob), pk(sb_)])

            def combine(g, qt, j, ob, sb_, ab):
                q0 = qt * 128
                rb = rsc[0] % 2
                rsc[0] += 1

                def gmm(e):
                    ins = None
                    for r_ in range(4):
                        ins = e.matmul(PS(6, 0, 64, r_ * 128, (r_ + 1) * 128), lhsT=oh[:, 3 * (4 * g + r_) + j, :],
                                       rhs=gsig[0:24, q0:q0 + 128], start=True, stop=True)
                    return ins
                S.pe(gmm, r=['oh', ('gsig', (qt // 4) * 512)], w=[pk(6)])
                S.dve(lambda e: e.tensor_scalar(out=rs_[rb], in0=PS(sb_, 0, 64, 0, 512), scalar1=1e-30, scalar2=None,
                                                op0=ALU.max), r=[pk(sb_)], w=[('rs', rb)])
                S.dve(lambda e: e.reciprocal(out=rs_[rb], in_=rs_[rb]), r=[('rs', rb)], w=[('rs', rb)])
                S.dve(lambda e: e.tensor_tensor(out=rs_[rb], in0=rs_[rb], in1=PS(6, 0, 64, 0, 512), op=ALU.mult),
                      r=[('rs', rb), pk(6)], w=[('rs', rb)])
                if j == 0:
                    S.dve(lambda e: e.tensor_tensor(out=acc[ab], in0=PS(ob, 0, 64, 0, 512), in1=rs_[rb], op=ALU.mult),
                          r=[pk(ob), ('rs', rb)], w=[('acc', ab)])
                else:
                    S.dve(lambda e: e.tensor_tensor(out=tt_[rb], in0=PS(ob, 0, 64, 0, 512), in1=rs_[rb], op=ALU.mult),
                          r=[pk(ob), ('rs', rb)], w=[('tt', rb)])
                    S.pool(lambda e: e.tensor_tensor(out=acc[ab], in0=acc[ab], in1=tt_[rb], op=ALU.add),
                           r=[('tt', rb), ('acc', ab)], w=[('acc', ab)])

            pat_q = [[0, 4], [1, 128]]
            pat_qn = [[0, 4], [-1, 128]]
            for qt in range(16):
                q0 = qt * 128
                for g in range(2):
                    ab = (qt * 2 + g) % 2
                    def cmm(e, g=g, q0=q0):
                        ins = None
                        for r_ in range(4):
                            ins = e.matmul(PS(7, 0, 128, r_ * 64, (r_ + 1) * 64), lhsT=Qaug[g][0:68, r_, q0:q0 + 128],
                                           rhs=KCa[g][0:68, :], start=True, stop=True)
                        return ins
                    S.pe(cmm, r=qkeys(g, qt) + [('KCa', g)], w=[pk(7)])
                    S.act(lambda e: e.activation(out=Ec, in_=PS(7, 0, 128, 0, 256).rearrange("p (r n) -> p r n", r=4),
                                                 func=AF.Exp), r=[pk(7)], w=['Ec'])
                    S.pool(lambda e, q0=q0: e.affine_select(out=Ec, in_=Ec, pattern=[[0, 4], [-32, 64]],
                                                            compare_op=ALU.is_ge, fill=freg(e, 0.0), base=q0 - 31,
                                                            channel_multiplier=1), r=['Ec'], w=['Ec'])
                    S.dve(lambda e: e.tensor_reduce(out=sums_c, in_=Ec, axis=AX.X, op=ALU.add), r=['Ec'], w=['sums_c'])
                    S.dve(lambda e: e.tensor_scalar(out=sums_c, in0=sums_c, scalar1=1e-30, scalar2=None, op0=ALU.max),
                          r=['sums_c'], w=['sums_c'])
                    S.dve(lambda e: e.reciprocal(out=sums_c, in_=sums_c), r=['sums_c'], w=['sums_c'])
                    S.dve(lambda e: e.tensor_tensor(out=Ec, in0=Ec, in1=sums_c.unsqueeze(2).broadcast_to([128, 4, 64]),
                                                    op=ALU.mult), r=['Ec', 'sums_c'], w=['Ec'])
                    S.dve(lambda e: e.tensor_reduce(out=imp64, in_=Ec.rearrange("p r n -> p n r"), axis=AX.X, op=ALU.add),
                          r=['Ec'], w=['imp64'])
                    S.dve(lambda e: e.tensor_reduce(out=imp, in_=imp64.rearrange("p (a b) -> p a b", b=2), axis=AX.X,
                                                    op=ALU.add), r=['imp64'], w=['imp'])
                    S.pool(lambda e, q0=q0: e.affine_select(out=imp, in_=imp, pattern=[[-64, 32]], compare_op=ALU.is_ge,
                                                            fill=freg(e, FORCE), base=q0 - 128, channel_multiplier=1),
                           r=['imp'], w=['imp'])
                    S.pool(lambda e: e.memset(imp[:, 0:1], FORCE), r=['imp'], w=['imp'])
                    S.pool(lambda e, q0=q0: e.affine_select(out=imp, in_=imp, pattern=[[-64, 32]], compare_op=ALU.is_ge,
                                                            fill=freg(e, -1.0e30), base=q0, channel_multiplier=1),
                           r=['imp'], w=['imp'])
                    S.dve(lambda e: e.max(out=m8[:, 0:8], in_=imp), r=['imp'], w=['m8a'])
                    S.dve(lambda e: e.match_replace(out=imp2, in_to_replace=m8[:, 0:8], in_values=imp, imm_value=-3.0e38),
                          r=['imp', 'm8a'], w=['imp2'])
                    S.dve(lambda e: e.max(out=m8[:, 8:16], in_=imp2), r=['imp2'], w=['m8b'])
                    S.dve(lambda e: e.tensor_scalar(out=Mpad[:, 96:128], in0=imp, scalar1=m8[:, 15:16], scalar2=None,
                                                    op0=ALU.is_ge), r=['imp', 'm8b'], w=['Mpad'])
                    S.pe(lambda e: e.matmul(PS(6, 0, 128, 0, 128), lhsT=Mpad, rhs=ident_b, start=True, stop=True),
                         r=['Mpad', 'ident_b'], w=[pk(6)])
                    S.dve(lambda e, g=g, q0=q0: e.tensor_scalar(
                        out=Qaug[g][96:128, :, q0:q0 + 128],
                        in0=PS(6, 96, 128, 0, 128).unsqueeze(1).broadcast_to([32, 4, 128]),
                        scalar1=1.0, scalar2=BIG, op0=ALU.subtract, op1=ALU.mult), r=[pk(6)], w=[('Qm', g, qt)])
                    attn_pair(g, qt, KCa[g][0:68, :], [('KCa', g)], 68, VC[g], [('VC', g)],
                              [(q0 - 31, -32, pat_q)], 2, 3, True, True)
                    combine(g, qt, 0, 2, 3, ab)
                    for kt in range(qt + 1):
                        masks = [(0, -1, pat_q)] if kt == qt else []
                        attn_pair(g, qt, Ksel[g][:, kt * 128:(kt + 1) * 128], [('Ks', g, (kt // 4) * 512)] + KK[g][0:0], 128,
                                  Vsel[:, kt, g * 64:(g + 1) * 64], [('Vsel', kt)], masks, 4, 5, kt == 0, kt == qt)
                    combine(g, qt, 1, 4, 5, ab)
                    k_lo = max(0, qt - 4)
                    for kt in range(k_lo, qt + 1):
                        masks = []
                        if kt == qt:
                            masks.append((0, -1, pat_q))
                        if kt == qt - 4:
                            masks.append((-1, 1, pat_qn))
                        attn_pair(g, qt, Kwin[g][0:68, kt * 128:(kt + 1) * 128], [('Kw', g, (kt // 4) * 512)], 68,
                                  Vwin[:, kt, g * 64:(g + 1) * 64], [('Vwin', kt)], masks, 2, 3, kt == k_lo, kt == qt)
                    combine(g, qt, 2, 2, 3, ab)
                    S.dve(lambda e, g=g, q0=q0, ab=ab: e.tensor_tensor(
                        out=ybT[0:64, 4 * g:4 * g + 4, q0:q0 + 128],
                        in0=acc[ab].rearrange("p (r q) -> p r q", r=4),
                        in1=ybT[0:64, 4 * g:4 * g + 4, q0:q0 + 128], op=ALU.mult),
                        r=[('acc', ab)] + [('ybT', 4 * g + r_, (qt // 4) * 512) for r_ in range(4)],
                        w=[('ybT', 4 * g + r_, (qt // 4) * 512) for r_ in range(4)])
            for g in range(2):
                S.dve(lambda e, g=g: e.tensor_copy(out=qs_f[:, g, :, :], in_=Qaug[g][0:68, :, T:TT]), r=[('Q', g, T)], w=['qs_f'])
                S.dve(lambda e, g=g: e.tensor_copy(out=ksn[:, 0, g, :], in_=Ksel[g][0:68, T:TT]), r=[('Ks', g, T)], w=['ksn'])
                S.dve(lambda e, g=g: e.tensor_copy(out=ksn[:, 1, g, :], in_=Kwin[g][0:68, T:TT]), r=[('Kw', g, T)], w=['ksn'])
            S.dve(lambda e: e.tensor_copy(out=gs_s, in_=gsig[0:24, T:TT]), r=[('gsig', T)], w=['gs_s'])
            stage_end(9)
            S.barrier()
            A.release()
            A.mark()
            Xb = [A.alloc([128, 32, 256], F32) for _ in range(2)]
            tmp4 = xT_raw32[:, 0:8192].rearrange("p (l r d) -> p l r d", l=32, r=4)
            pt_i = A.alloc([32, NS, 2], I32)
            pt_f = A.alloc([32, NS * 2], F32)
            E32 = A.alloc([32, 128], F32)
            qcol_i = A.alloc([128, 1], I32)
            qcol_f = A.alloc([128, 1], F32)
            idx_f = A.alloc([128, NS * 2], F32)
            idx_i = A.alloc([128, NS * 2], I32)
            Wl4 = A.alloc([128, 32, 4], F32)
            slopes = A.alloc([128, 8], F32)
            posc = A.alloc([128, 2], F32)
            ABc = A.alloc([128, 2, 8], F32)
            poss = A.alloc([128, 2, 32], F32)
            ABs = A.alloc([128, 2, 8, 32], F32)
            posw = A.alloc([128, 4], F32)
            ABw = A.alloc([128, 8, 4], F32)
            Ex = A.alloc([128, 2, 128], BF16)
            qrep = A.alloc([64, 8, 128], BF16)
            qb = A.alloc([128, 8, 64], F32)
            kvc = Vwin_raw32[:, 0:512].rearrange("p (t c) -> p t c", t=2)
            tmpc = A.alloc([128, 2, 4, 64], F32)
            sc = A.alloc([128, 2, 8], F32)
            Esum = A.alloc([128, 8], F32)
            pcg = A.alloc([128, 2, 2], F32)
            impn = A.alloc([2, 256], F32)
            impd = A.alloc([2, 129], F32)
            impd2 = A.alloc([2, 129], F32)
            m8d = A.alloc([2, 16], F32)
            Md = A.alloc([2, 129], F32)
            MT = A.alloc([128, 2], BF16)
            mskd = A.alloc([128, 2, 2], F32)
            ss = Vwin_raw32[:, 512:1024].rearrange("p (t h l) -> p t h l", t=2, h=8)
            Pl = A.alloc([128, 2, 8], F32)
            Wn = Vsel_raw32[:, 0:1024].rearrange("p (l c) -> p l c", l=4)
            sw = A.alloc([128, 8, 4], F32)
            Plw = A.alloc([128, 8], F32)
            prodn = A.alloc([68, 2, 2, 4, NS], F32)
            pnew = A.alloc([1, 2, 2, 4, NS], F32)
            vrow = A.alloc([1, NS, 2, 128], F32)
            accd = A.alloc([4, 2, 64], F32)
            gts = A.alloc([4, 6, NS], F32)
            rsd = A.alloc([4, 4], F32)

            cache_v = {(c_, hf): I['cache_%s_%s' % (c_, hf)].rearrange("(b q) c -> b (q c)", q=32)
                       for c_ in ('cmp', 'sel') for hf in ('a', 'b')}
            HALF = NPOOL * 2
            idx_b = A.alloc([128, NS * 2], I32)
            idx_m = A.alloc([128, NS * 2], F32)

            def chain(addfn, fns, key, r=()):
                for fn in fns:
                    addfn(fn, r=[key] + list(r), w=[key])

            fns1 = [
                lambda e: e.iota(qcol_i, pattern=[[0, 1]], base=0, channel_multiplier=1),
                lambda e: e.memset(E32, 1.0),
                lambda e: e.memset(accd, 0.0),
                lambda e: e.affine_select(out=E32, in_=E32, pattern=[[1, 128]], compare_op=ALU.is_ge, fill=freg(e, 0.0), base=0,
                                          channel_multiplier=-4),
                lambda e: e.iota(posc, pattern=[[4096, 2]], base=31 - 8192, channel_multiplier=32,
                                 allow_small_or_imprecise_dtypes=True),
                lambda e: e.affine_select(out=E32, in_=E32, pattern=[[-1, 128]], compare_op=ALU.is_ge, fill=freg(e, 0.0), base=3,
                                          channel_multiplier=4),
                lambda e: e.iota(poss, pattern=[[4096, 2], [1, 32]], base=-8192, channel_multiplier=32,
                                 allow_small_or_imprecise_dtypes=True),
                lambda e: e.iota(posw, pattern=[[1, 4]], base=7680 - 8192, channel_multiplier=4,
                                 allow_small_or_imprecise_dtypes=True),
            ]
            for h in range(8):
                fns1.append(lambda e, h=h: e.memset(slopes[:, h:h + 1], 2.0 ** (-(h + 1))))
            for t in range(2):
                fns1.append(lambda e, t=t: e.memset(Ex[:, t, :], 1.0))
            for t in range(2):
                fns1.append(lambda e, t=t: e.affine_select(out=Ex[:, t, :], in_=Ex[:, t, :], pattern=[[1, 128]], compare_op=ALU.is_ge,
                                                          fill=freg(e, 0.0), base=128 * t, channel_multiplier=-2))
            for t in range(2):
                fns1.append(lambda e, t=t: e.affine_select(out=Ex[:, t, :], in_=Ex[:, t, :], pattern=[[-1, 128]], compare_op=ALU.is_ge,
                                                          fill=freg(e, 0.0), base=1 - 128 * t, channel_multiplier=2))
            chain(S.pool, fns1, 'dsetup')
            S.dma('sp', lambda e: e.dma_start(out=pt_i, in_=I['page_table'].rearrange("s (t j) -> j s t", j=32),
                                              allow_slow_non_contiguous=True), w=['pt_i'])
            S.dma('sp', lambda e: e.dma_start(out=Wl4[:, :, 0:2], in_=I['wk_cmp'].partition_broadcast(128)), w=['Wl4a'])
            S.dma('sp', lambda e: e.dma_start(out=Wl4[:, :, 2:4], in_=I['wv_cmp'].partition_broadcast(128)), w=['Wl4b'])
            for s in range(NS):
                S.dma('sp', lambda e, s=s: e.dma_start(out=vrow[0:1, s, 0, :], in_=kvsv[s:s + 1, 0, :]), r=['kvsv'],
                      w=[('vrow', s)])
                S.dma('sp', lambda e, s=s: e.dma_start(out=vrow[0:1, s, 1, :], in_=kvsv[s:s + 1, 1, :]), r=['kvsv'],
                      w=[('vrow', s)])

            fns2 = [
                lambda e: e.tensor_copy(out=pt_f, in_=pt_i.rearrange("p s t -> p (s t)")),
                lambda e: e.tensor_single_scalar(out=qcol_i, in_=qcol_i, scalar=3, op=ALU.bitwise_and),
                lambda e: e.tensor_tensor(out=ABc, in0=posc.unsqueeze(2).broadcast_to([128, 2, 8]),
                                          in1=slopes.unsqueeze(1).broadcast_to([128, 2, 8]), op=ALU.mult),
                lambda e: e.tensor_copy(out=qcol_f, in_=qcol_i),
                lambda e: e.tensor_tensor(out=ABw, in0=posw.unsqueeze(1).broadcast_to([128, 8, 4]),
                                          in1=slopes.unsqueeze(2).broadcast_to([128, 8, 4]), op=ALU.mult),
            ]
            for t in range(2):
                fns2.append(lambda e, t=t: e.tensor_tensor(out=ABs[:, t, :, :], in0=poss[:, t, :].unsqueeze(1).broadcast_to([128, 8, 32]),
                                                          in1=slopes.unsqueeze(2).broadcast_to([128, 8, 32]), op=ALU.mult))
            chain(S.dve, fns2, 'dsetup2', r=['dsetup', 'pt_i'])
            S.pe(lambda e: e.matmul(PS(1, 0, 128, 0, NS * 2), lhsT=E32, rhs=pt_f, start=True, stop=True),
                 r=['dsetup', 'dsetup2'], w=[pk(1)])
            S.dve(lambda e: e.tensor_scalar(out=idx_f, in0=PS(1, 0, 128, 0, NS * 2), scalar1=4.0, scalar2=qcol_f[:, 0:1],
                                            op0=ALU.mult, op1=ALU.add), r=[pk(1), 'dsetup2'], w=['idx_f'])
            idx_mb = A.alloc([128, NS * 2], F32)

            idx_c = A.alloc([128, 2, NS * 2], F32)
            chain(S.dve, [
                lambda e: e.tensor_single_scalar(out=idx_mb, in_=idx_f, scalar=float(HALF), op=ALU.is_ge),
                lambda e: e.tensor_scalar(out=idx_m, in0=idx_mb, scalar1=-1.0, scalar2=1.0, op0=ALU.mult, op1=ALU.add),
                lambda e: e.tensor_scalar(out=idx_c[:, 0, :], in0=idx_f, scalar1=float(HALF - 1), scalar2=None, op0=ALU.min),
                lambda e: e.tensor_scalar(out=idx_c[:, 1, :], in0=idx_f, scalar1=float(HALF), scalar2=0.0, op0=ALU.subtract,
                                          op1=ALU.max),
                lambda e: e.tensor_copy(out=idx_i, in_=idx_c[:, 0, :]),
                lambda e: e.tensor_copy(out=idx_b, in_=idx_c[:, 1, :]),
            ], 'idx_i', r=['idx_f'])
            for br in range(2):
                S.dve(lambda e, br=br: e.tensor_tensor(
                    out=prodn[:, br, :, :, :], in0=qs_f, in1=ksn[:, br, :, :].unsqueeze(2).broadcast_to([68, 2, 4, NS]),
                    op=ALU.mult), r=['qs_f', 'ksn'], w=[('prodn', br)])
            S.pe(lambda e: e.matmul(PS(7, 0, 1, 0, 64), lhsT=ones_f[0:68, 0:1],
                                    rhs=prodn.rearrange("p a g r s -> p (a g r s)"), start=True, stop=True),
                 r=[('prodn', 0), ('prodn', 1), 'ones_f'], w=[pk(7)])
            S.act(lambda e: e.activation(out=pnew.rearrange("p a g r s -> p (a g r s)"), in_=PS(7, 0, 1, 0, 64), func=AF.Exp),
                  r=[pk(7)], w=['pnew'])
            def gmm_d(e):
                ins = None
                for g in range(2):
                    for j in range(3):
                        c0_ = 12 * g + j
                        ins = e.matmul(PS(7, 0, 4, 64 + (g * 3 + j) * NS, 64 + (g * 3 + j + 1) * NS),
                                       lhsT=ident_b[0:24, c0_:c0_ + 10:3], rhs=gs_s, start=True, stop=True)
                return ins
            S.pe(gmm_d, r=['gs_s', 'ident_b'], w=[pk(7)])
            S.act(lambda e: e.copy(out=gts.rearrange("p a s -> p (a s)"), in_=PS(7, 0, 4, 64, 64 + 6 * NS)), r=[pk(7)],
                  w=['gts'])

            def gather(cname, s, t, xb):
                col = s * 2 + t
                S.dma('pool', lambda e: e.indirect_dma_start(
                    out=Xb[xb].rearrange("p l c -> p (l c)"), out_offset=None, in_=cache_v[(cname, 'a')],
                    in_offset=bass.IndirectOffsetOnAxis(ap=idx_i[:, col:col + 1], axis=0)), r=['idx_i'], w=[('Xb', xb)])
                S.dma('pool', lambda e: e.indirect_dma_start(
                    out=xT_raw32[:, 0:8192], out_offset=None, in_=cache_v[(cname, 'b')],
                    in_offset=bass.IndirectOffsetOnAxis(ap=idx_b[:, col:col + 1], axis=0)), r=['idx_i'], w=['tmp4'])
                S.dve(lambda e: e.tensor_scalar(out=Xb[xb].rearrange("p l c -> p (l c)"), in0=Xb[xb].rearrange("p l c -> p (l c)"),
                                                scalar1=idx_m[:, col:col + 1], scalar2=None, op0=ALU.mult),
                      r=[('Xb', xb), 'idx_i'], w=[('Xb', xb)])
                S.dve(lambda e: e.scalar_tensor_tensor(out=Xb[xb].rearrange("p l c -> p (l c)"), in0=xT_raw32[:, 0:8192],
                                                       scalar=idx_mb[:, col:col + 1], in1=Xb[xb].rearrange("p l c -> p (l c)"),
                                                       op0=ALU.mult, op1=ALU.add), r=[('Xb', xb), 'tmp4', 'idx_i'], w=[('Xb', xb)])

            stage_end(105)
            for s in range(NS):
                for t in range(2):
                    gather('cmp', s, t, t)
                    S.dve(lambda e, t=t: e.tensor_tensor(
                        out=Xb[t].rearrange("p l (a d) -> p l a d", a=4), in0=Xb[t].rearrange("p l (a d) -> p l a d", a=4),
                        in1=Wl4.unsqueeze(3).broadcast_to([128, 32, 4, 64]), op=ALU.mult),
                        r=[('Xb', t), 'Wl4a', 'Wl4b'], w=[('Xb', t)])
                    S.dve(lambda e, t=t: e.tensor_reduce(out=kvc[:, t, :], in_=Xb[t].rearrange("p l c -> p c l"), axis=AX.X,
                                                         op=ALU.add), r=[('Xb', t)], w=[('kvc', t)])
                stage_end(106)
                S.dve(lambda e, s=s: e.tensor_copy(
                    out=qrep, in_=qs_f[0:64, :, :, s].rearrange("p g r -> p (g r)").unsqueeze(2).broadcast_to([64, 8, 128])),
                    r=['qs_f'], w=['qrep'])

                def qbm(e):
                    ins = None
                    for h in range(8):
                        ins = e.matmul(PS(0, 0, 128, h * 64, (h + 1) * 64), lhsT=qrep[:, h, :], rhs=ident_b[0:64, 0:64],
                                       start=True, stop=True)
                    return ins
                S.pe(qbm, r=['qrep', 'ident_b'], w=[pk(0)])
                S.act(lambda e: e.copy(out=qb.rearrange("p h d -> p (h d)"), in_=PS(0)), r=[pk(0)], w=['qb'])
                for t in range(2):
                    S.dve(lambda e, t=t: e.tensor_tensor(
                        out=tmpc, in0=kvc[:, t, 0:128].rearrange("p (g d) -> p g d", g=2).unsqueeze(2).broadcast_to([128, 2, 4, 64]),
                        in1=qb.rearrange("p (g r) d -> p g r d", g=2), op=ALU.mult),
                        r=[('kvc', t), 'qb'], w=['tmpc'])
                    S.dve(lambda e, t=t: e.tensor_reduce(out=sc[:, t, :], in_=tmpc.rearrange("p g r d -> p (g r) d"),
                                                         axis=AX.X, op=ALU.add), r=['tmpc'], w=[('sc', t)])
                S.dve(lambda e: e.tensor_tensor(out=sc, in0=sc, in1=ABc, op=ALU.add), r=[('sc', 0), ('sc', 1), 'dsetup2'],
                      w=['sc'])
                S.act(lambda e: e.activation(out=sc, in_=sc, func=AF.Exp), r=['sc'], w=['sc'])
                S.dve(lambda e: e.tensor_tensor(out=Esum, in0=sc[:, 0, :], in1=sc[:, 1, :], op=ALU.add), r=['sc'], w=['Esum'])
                S.pe(lambda e: e.matmul(PS(1, 0, 128, 0, 8), lhsT=ones_f, rhs=Esum, start=True, stop=True),
                     r=['Esum', 'ones_f'], w=[pk(1)])
                S.dve(lambda e: e.reciprocal(out=Esum, in_=PS(1, 0, 128, 0, 8)), r=[pk(1)], w=['Esum'])
                S.dve(lambda e: e.tensor_tensor(out=sc, in0=sc, in1=Esum.unsqueeze(1).broadcast_to([128, 2, 8]), op=ALU.mult),
                      r=['sc', 'Esum'], w=['sc'])

                def ocm(e):
                    ins = None
                    for g in range(2):
                        for t in range(2):
                            ins = e.matmul(PS(2, 0, 4, g * 64, (g + 1) * 64), lhsT=sc[:, t, 4 * g:4 * g + 4],
                                           rhs=kvc[:, t, 128 + g * 64:128 + (g + 1) * 64], start=(t == 0), stop=(t == 1))
                    return ins
                S.pe(ocm, r=['sc', ('kvc', 0), ('kvc', 1)], w=[pk(2)])
                S.dve(lambda e: e.tensor_reduce(out=pcg, in_=sc.rearrange("p t (g r) -> p t g r", g=2), axis=AX.X, op=ALU.add),
                      r=['sc'], w=['pcg'])

                def trp(e):
                    ins = None
                    for t in range(2):
                        ins = e.transpose(out=PS(3, 0, 2, t * 128, (t + 1) * 128), in_=pcg[:, t, :], identity=ident_f)
                    return ins
                S.pe(trp, r=['pcg', 'ident_f'], w=[pk(3)])
                S.act(lambda e: e.copy(out=impn, in_=PS(3, 0, 2, 0, 256)), r=[pk(3)], w=['impn'])
                S.dve(lambda e: e.tensor_reduce(out=impd[:, 0:128], in_=impn.rearrange("p (a b) -> p a b", b=2), axis=AX.X,
                                                op=ALU.add), r=['impn'], w=['impd'])

                S.pool(lambda e: e.memset(impd[:, 0:1], FORCE), r=['impd'], w=['impd'])
                S.pool(lambda e: e.memset(impd[:, 127:129], FORCE), r=['impd'], w=['impd'])
                S.dve(lambda e: e.max(out=m8d[:, 0:8], in_=impd), r=['impd'], w=['m8da'])
                S.dve(lambda e: e.match_replace(out=impd2, in_to_replace=m8d[:, 0:8], in_values=impd, imm_value=-3.0e38),
                      r=['impd', 'm8da'], w=['impd2'])
                S.dve(lambda e: e.max(out=m8d[:, 8:16], in_=impd2), r=['impd2'], w=['m8db'])
                S.dve(lambda e: e.tensor_scalar(out=Md, in0=impd, scalar1=m8d[:, 15:16], scalar2=None, op0=ALU.is_ge),
                      r=['impd', 'm8db'], w=['Md'])
                S.pe(lambda e: e.transpose(out=PS(3, 0, 128, 256, 258), in_=Md[:, 0:128], identity=ident_f[0:2, 0:2]),
                     r=['Md', 'ident_f'], w=[pk(3)])
                S.act(lambda e: e.copy(out=MT, in_=PS(3, 0, 128, 256, 258)), r=[pk(3)], w=['MT'])

                def mskm(e):
                    ins = None
                    for t in range(2):
                        ins = e.matmul(PS(3, 0, 128, 260 + 2 * t, 262 + 2 * t), lhsT=Ex[:, t, :], rhs=MT, start=True, stop=True)
                    return ins
                S.pe(mskm, r=['MT', 'dsetup'], w=[pk(3)])
                S.act(lambda e: e.copy(out=mskd.rearrange("p t g -> p (t g)"), in_=PS(3, 0, 128, 260, 264)), r=[pk(3)],
                      w=['mskd'])
                for t in range(2):
                    gather('sel', s, t, t)
                    for g in range(2):
                        S.dve(lambda e, t=t, g=g: e.tensor_tensor(
                            out=tmp4, in0=Xb[t][:, :, g * 64:(g + 1) * 64].unsqueeze(2).broadcast_to([128, 32, 4, 64]),
                            in1=qb[:, 4 * g:4 * g + 4, :].unsqueeze(1).broadcast_to([128, 32, 4, 64]), op=ALU.mult),
                            r=[('Xb', t), 'qb'], w=['tmp4'])
                        S.dve(lambda e, t=t, g=g: e.tensor_reduce(
                            out=ss[:, t, 4 * g:4 * g + 4, :].rearrange("p r l -> p l r"), in_=tmp4, axis=AX.X, op=ALU.add),
                            r=['tmp4'], w=[('ss', t, g)])
                SSK = [('ss', t, g) for t in range(2) for g in range(2)]
                S.dve(lambda e: e.tensor_tensor(out=ss, in0=ss, in1=ABs, op=ALU.add), r=SSK + ['dsetup2'], w=['ss'])
                S.act(lambda e: e.activation(out=ss, in_=ss, func=AF.Exp), r=['ss'], w=['ss'])
                for t in range(2):
                    S.dve(lambda e, t=t: e.tensor_tensor(
                        out=ss[:, t, :, :].rearrange("p (g r) l -> p g (r l)", g=2),
                        in0=ss[:, t, :, :].rearrange("p (g r) l -> p g (r l)", g=2),
                        in1=mskd[:, t, :].unsqueeze(2).broadcast_to([128, 2, 128]), op=ALU.mult),
                        r=['ss', 'mskd'], w=['ss'])
                S.dve(lambda e: e.tensor_reduce(out=Pl, in_=ss, axis=AX.X, op=ALU.add), r=['ss'], w=['Pl'])

                def pvs(e, s=s):
                    ins = None
                    for g in range(2):
                        k = 0
                        for t in range(2):
                            for l in range(32):
                                ins = e.matmul(PS(4, 0, 4, g * 64, (g + 1) * 64), lhsT=ss[:, t, 4 * g:4 * g + 4, l],
                                               rhs=Xb[t][:, l, 128 + g * 64:128 + (g + 1) * 64], start=(k == 0), stop=False)
                                k += 1
                        ins = e.matmul(PS(4, 0, 4, g * 64, (g + 1) * 64), lhsT=pnew[0:1, 0, g, :, s],
                                       rhs=vrow[0:1, s, 0, g * 64:(g + 1) * 64], start=False, stop=True)
                        for t in range(2):
                            ins = e.matmul(PS(5, 0, 4, g, g + 1), lhsT=Pl[:, t, 4 * g:4 * g + 4], rhs=ones_f[:, 0:1],
                                           start=(t == 0), stop=False)
                        ins = e.matmul(PS(5, 0, 4, g, g + 1), lhsT=pnew[0:1, 0, g, :, s], rhs=ones_f[0:1, 0:1],
                                       start=False, stop=True)
                    return ins
                S.pe(pvs, r=['ss', 'Pl', ('Xb', 0), ('Xb', 1), 'pnew', ('vrow', s), 'ones_f'], w=[pk(4), pk(5)])
                S.dma('sp', lambda e, s=s: e.dma_start(out=Wn, in_=I['cache_win'][s * 512:(s + 1) * 512, :].rearrange(
                    "(p j) c -> p j c", j=4)), w=['Wn'])
                for g in range(2):
                    S.dve(lambda e, g=g: e.tensor_tensor(
                        out=tmp4[:, 0:4, :, :], in0=Wn[:, :, g * 64:(g + 1) * 64].unsqueeze(2).broadcast_to([128, 4, 4, 64]),
                        in1=qb[:, 4 * g:4 * g + 4, :].unsqueeze(1).broadcast_to([128, 4, 4, 64]), op=ALU.mult),
                        r=['Wn', 'qb'], w=['tmp4'])
                    S.dve(lambda e, g=g: e.tensor_reduce(out=sw[:, 4 * g:4 * g + 4, :].rearrange("p r l -> p l r"),
                                                         in_=tmp4[:, 0:4, :, :], axis=AX.X, op=ALU.add),
                          r=['tmp4'], w=[('sw', g)])
                S.dve(lambda e: e.tensor_tensor(out=sw, in0=sw, in1=ABw, op=ALU.add), r=[('sw', 0), ('sw', 1), 'dsetup2'],
                      w=['sw'])
                S.act(lambda e: e.activation(out=sw, in_=sw, func=AF.Exp), r=['sw'], w=['sw'])
                S.pool(lambda e: e.memset(sw[0:1, :, 0:1], 0.0), r=['sw'], w=['sw'])
                S.dve(lambda e: e.tensor_reduce(out=Plw, in_=sw, axis=AX.X, op=ALU.add), r=['sw'], w=['Plw'])

                def pvw(e, s=s):
                    ins = None
                    for g in range(2):
                        for l in range(4):
                            ins = e.matmul(PS(6, 0, 4, g * 64, (g + 1) * 64), lhsT=sw[:, 4 * g:4 * g + 4, l],
                                           rhs=Wn[:, l, 128 + g * 64:128 + (g + 1) * 64], start=(l == 0), stop=False)
                        ins = e.matmul(PS(6, 0, 4, g * 64, (g + 1) * 64), lhsT=pnew[0:1, 1, g, :, s],
                                       rhs=vrow[0:1, s, 1, g * 64:(g + 1) * 64], start=False, stop=True)
                        ins = e.matmul(PS(5, 0, 4, 2 + g, 3 + g), lhsT=Plw[:, 4 * g:4 * g + 4], rhs=ones_f[:, 0:1],
                                       start=True, stop=False)
                        ins = e.matmul(PS(5, 0, 4, 2 + g, 3 + g), lhsT=pnew[0:1, 1, g, :, s], rhs=ones_f[0:1, 0:1],
                                       start=False, stop=True)
                    return ins
                S.pe(pvw, r=['sw', 'Plw', 'Wn', 'pnew', ('vrow', s), 'ones_f'], w=[pk(6), pk(5)])
                S.dve(lambda e: e.reciprocal(out=rsd, in_=PS(5, 0, 4, 0, 4)), r=[pk(5)], w=['rsd'])
                for g in range(2):
                    S.dve(lambda e, g=g, s=s: e.tensor_scalar(out=accd[:, g, :], in0=PS(2, 0, 4, g * 64, (g + 1) * 64),
                                                              scalar1=gts[:, g * 3 + 0, s:s + 1], scalar2=None, op0=ALU.mult),
                          r=[pk(2), 'gts'], w=[('accd', g)])
                    for bi, (bank, col) in enumerate([(4, g), (6, 2 + g)]):
                        S.dve(lambda e, g=g, s=s, bi=bi, col=col: e.tensor_tensor(
                            out=rsd[:, col:col + 1], in0=rsd[:, col:col + 1], in1=gts[:, g * 3 + 1 + bi, s:s + 1], op=ALU.mult),
                            r=['rsd', 'gts'], w=['rsd'])
                        S.dve(lambda e, g=g, bank=bank, col=col: e.scalar_tensor_tensor(
                            out=accd[:, g, :], in0=PS(bank, 0, 4, g * 64, (g + 1) * 64), scalar=rsd[:, col:col + 1],
                            in1=accd[:, g, :], op0=ALU.mult, op1=ALU.add), r=[pk(bank), 'rsd', ('accd', g)], w=[('accd', g)])
                    S.pe(lambda e, g=g: e.transpose(out=PS(7, 0, 64, 128 + 4 * g, 132 + 4 * g), in_=accd[:, g, :],
                                                    identity=ident_f[0:4, 0:4]), r=[('accd', g), 'ident_f'], w=[pk(7)])
                    S.dve(lambda e, g=g, s=s: e.tensor_tensor(
                        out=ybT[0:64, 4 * g:4 * g + 4, T + s], in0=PS(7, 0, 64, 128 + 4 * g, 132 + 4 * g),
                        in1=ybT[0:64, 4 * g:4 * g + 4, T + s], op=ALU.mult),
                        r=[pk(7)] + [('ybT', 4 * g + r_, T) for r_ in range(4)],
                        w=[('ybT', 4 * g + r_, T) for r_ in range(4)] + ['dec_yb'])
            if os.environ.get('KDEV_DUMP'):
                yp = O['y_prompt']
                S.dma('sp', lambda e: e.dma_start(out=yp[0:128, :], in_=Xb[0][:, 0:4, :].rearrange("p l c -> p (l c)")), r=[('Xb', 0)])
                S.dma('sp', lambda e: e.dma_start(out=yp[128:256, 0:512], in_=kvc.rearrange("p t c -> p (t c)")), r=[('kvc', 0), ('kvc', 1)])
                S.dma('sp', lambda e: e.dma_start(out=yp[256:384, 0:16], in_=sc.rearrange("p t h -> p (t h)")), r=['sc'])
                S.dma('sp', lambda e: e.dma_start(out=yp[384:512, 0:512], in_=ss.rearrange("p t h l -> p (t h l)")), r=['ss'])
                S.dma('sp', lambda e: e.dma_start(out=yp[512:516, 0:4], in_=rsd), r=['rsd'])
                S.dma('sp', lambda e: e.dma_start(out=yp[516:520, 0:128], in_=accd.rearrange("p g d -> p (g d)")), r=[('accd', 0), ('accd', 1)])
                S.dma('sp', lambda e: e.dma_start(out=yp[640:768, 0:8], in_=idx_f), r=['idx_i'])
                S.dma('sp', lambda e: e.dma_start(out=yp[768:896, 0:4], in_=mskd.rearrange("p t g -> p (t g)")), r=['mskd'])
                S.dma('sp', lambda e: e.dma_start(out=yp[896:898, 0:129], in_=impd), r=['impd'])
                S.dma('sp', lambda e: e.dma_start(out=yp[900:1028, 0:512], in_=qb.rearrange("p h d -> p (h d)")), r=['qb'])
                S.dma('sp', lambda e: e.dma_start(out=yp[1028:1029, 0:64], in_=pnew.rearrange("p a g r s -> p (a g r s)")), r=['pnew'])
                S.dma('sp', lambda e: e.dma_start(out=yp[1030:1034, 0:24], in_=gts.rearrange("p a s -> p (a s)")), r=['gts'])
            S.barrier()
            A.release()
            stage_end(11)
            ALPHA = float((2 * 2) ** 0.25)
            x1T = xT
            A.mark()
            wo_a = A.alloc([128, 4, DM], BF16)
            wo_b = A.alloc([64, 8, DM], BF16)
            lnp = A.alloc([128, 2, DM], F32)
            xres = [A.alloc([128, DM], F32) for _ in range(2)]
            x1b = [A.alloc([128, DM], BF16) for _ in range(2)]
            bst = A.alloc([128, 2, 6], F32)
            mv = A.alloc([128, 4], F32)
            for j in range(4):
                S.dma('pool', lambda e, j=j: e.dma_start(out=wo_a[:, j, :], in_=I['w_out_even'][j * 128:(j + 1) * 128, :]),
                      w=[('wo_a', j)])
            for h in range(8):
                S.dma('pool', lambda e, h=h: e.dma_start(out=wo_b[:, h, :],
                                                         in_=I['w_out_even'][512 + h * 64:512 + (h + 1) * 64, :]),
                      w=[('wo_b', h)])
            S.dma('sp', lambda e: e.dma_start(out=lnp[:, 0, :], in_=I['ln_g'][0:1, :].partition_broadcast(128)), w=[('lnp', 0)])
            S.dma('sp', lambda e: e.dma_start(out=lnp[:, 1, :], in_=I['ln_b'][0:1, :].partition_broadcast(128)), w=[('lnp', 1)])
            WO = [('wo_a', j) for j in range(4)] + [('wo_b', h) for h in range(8)]

            def resid_ln(n, rows, c0, xsrc, yk, lidx, ya_, yb_, outs):
                b = n % 2
                S.dma('sp', lambda e: e.dma_start(out=xres[b][0:rows, :], in_=xsrc), w=[('xres', b)])

                def mm(e):
                    ins = None
                    for cb in range(2):
                        k = 0
                        tot = len(ya_) + len(yb_)
                        for (lt, rt) in ya_ + yb_:
                            ins = e.matmul(PS(cb, 0, rows, 0, 512), lhsT=lt[:, c0:c0 + rows], rhs=rt[:, cb * 512:(cb + 1) * 512],
                                           start=(k == 0), stop=(k == tot - 1))
                            k += 1
                    return ins
                S.pe(mm, r=yk + WO, w=[pk(0), pk(1)])
                for cb in range(2):
                    S.dve(lambda e, cb=cb: e.scalar_tensor_tensor(
                        out=xres[b][0:rows, cb * 512:(cb + 1) * 512], in0=xres[b][0:rows, cb * 512:(cb + 1) * 512],
                        scalar=ALPHA, in1=PS(cb, 0, rows, 0, 512), op0=ALU.mult, op1=ALU.add),
                        r=[('xres', b), pk(cb)], w=[('xres', b)])
                    S.dve(lambda e, cb=cb: e.bn_stats(out=bst[0:rows, cb, :], in_=xres[b][0:rows, cb * 512:(cb + 1) * 512]),
                          r=[('xres', b)], w=[('bst', cb)])
                S.dve(lambda e: e.bn_aggr(out=mv[0:rows, 0:2], in_=bst[0:rows, :, :].rearrange("p a b -> p (a b)")),
                      r=[('bst', 0), ('bst', 1)], w=['mv'])
                S.dve(lambda e: e.tensor_scalar(out=mv[0:rows, 2:3], in0=mv[0:rows, 1:2], scalar1=1e-5, scalar2=None,
                                                op0=ALU.add), r=['mv'], w=['mv2'])
                S.act(lambda e: e.activation(out=mv[0:rows, 2:3], in_=mv[0:rows, 2:3], func=AF.Ln), r=['mv2'], w=['mv2'])
                S.act(lambda e: e.activation(out=mv[0:rows, 2:3], in_=mv[0:rows, 2:3], func=AF.Exp, scale=-0.5),
                      r=['mv2'], w=['mv2'])
                S.dve(lambda e: e.tensor_scalar(out=xres[b][0:rows, :], in0=xres[b][0:rows, :], scalar1=mv[0:rows, 0:1],
                                                scalar2=mv[0:rows, 2:3], op0=ALU.subtract, op1=ALU.mult),
                      r=[('xres', b), 'mv', 'mv2'], w=[('xres', b)])
                S.dve(lambda e: e.tensor_tensor(out=xres[b][0:rows, :], in0=xres[b][0:rows, :], in1=lnp[0:rows, 2 * lidx, :],
                                                op=ALU.mult), r=[('xres', b), ('lnp', 2 * lidx)], w=[('xres', b)])
                S.pool(lambda e: e.tensor_tensor(out=xres[b][0:rows, :], in0=xres[b][0:rows, :],
                                                 in1=lnp[0:rows, 2 * lidx + 1, :], op=ALU.add),
                       r=[('xres', b), ('lnp', 2 * lidx + 1)], w=[('xres', b)])
                outs(b)

            def l0_outs(n, rows, c0):
                def f(b):
                    S.dma('sp', lambda e: e.dma_start(out=x1_scr[c0:c0 + rows, :], in_=xres[b][0:rows, :]),
                          r=[('xres', b)], w=[('x1scr', n)])
                    if n == 15:
                        S.dma('sp', lambda e: e.dma_start(out=O['shift_p'], in_=xres[b][127:128, :]), r=[('xres', b)])
                    if n == 16:
                        S.dma('sp', lambda e: e.dma_start(out=O['shift_s'], in_=xres[b][0:rows, :]), r=[('xres', b)])
                    S.act(lambda e: e.copy(out=x1b[b][0:rows, :], in_=xres[b][0:rows, :]), r=[('xres', b)], w=[('x1b', b)])
                    bank = 2 + (n % 2)

                    def tr(e):
                        ins = None
                        for c in range(8):
                            ins = e.transpose(out=PSB(bank, 0, 128, c * rows, (c + 1) * rows),
                                              in_=x1b[b][0:rows, c * 128:(c + 1) * 128], identity=ident_b[0:rows, 0:rows])
                        return ins
                    S.pe(tr, r=[('x1b', b), 'ident_b'], w=[pk(bank)])
                    S.dve(lambda e: e.tensor_copy(out=x1T[:, :, c0:c0 + rows],
                                                  in_=PSB(bank, 0, 128, 0, 8 * rows).rearrange("p (c t) -> p c t", c=8)),
                          r=[pk(bank)], w=[('x1T', n)])
                return f

            for n in range(17):
                rows = 128 if n < 16 else NS
                c0 = n * 128 if n < 16 else T
                xsrc = I['xp'][c0:c0 + 128, :] if n < 16 else I['xs']
                blk = (c0 // 512) * 512 if n < 16 else T
                yk = [('yaT', j, blk) for j in range(4)] + [('ybT', h, blk) for h in range(8)]
                if n == 16:
                    yk += ['dec_yb']
                ya_ = [(yaT[:, j, :], wo_a[:, j, :]) for j in range(4)]
                yb_ = [(ybT[0:64, h, :], wo_b[0:64, h, :]) for h in range(8)]
                resid_ln(n, rows, c0, xsrc, yk, 0, ya_, yb_, l0_outs(n, rows, c0))
            S.barrier()
            A.release()
            stage_end(10)
            S.barrier()
            A.top = L1_BASE
            A.marks = []
            A.mark()
            dT = A.alloc([128, 8, TT], BF16)
            xsh = A.alloc([128, 8, TT], BF16)
            muT = A.alloc([128, 6, 8], F32)
            vecs = A.alloc([128, 5, 8], F32)
            shb = A.alloc([NS, DM], BF16)
            stg = [A.alloc([128, 1024], F32) for _ in range(3)]
            wl1 = [A.alloc([128, 8, 128], BF16) for _ in range(3)]
            wfull = A.alloc([128, 8, 1024], BF16)
            lo1 = A.alloc([128, 8, 64], BF16)
            lo2 = A.alloc([64, 1024], BF16)
            t1T = A.alloc([64, TT], BF16)
            S.dma('sp', lambda e: e.dma_start(out=muT, in_=I['mu_c'].rearrange("i (c p) -> p i c", p=128),
                                              allow_slow_non_contiguous=True), w=['muT'])
            for i, nm in enumerate(['w0', 'a0', 'k_k', 'k_a', 'r_k']):
                S.dma('sp', lambda e, i=i, nm=nm: e.dma_start(out=vecs[:, i, :], in_=I[nm].rearrange("o (c p) -> p (o c)", p=128),
                                                              allow_slow_non_contiguous=True), w=[('vecs', i)])
            S.dma('pool', lambda e: e.dma_start(out=shb, in_=I['state_shift']), w=['shb'])

            def trsh(e):
                ins = None
                for c in range(8):
                    ins = e.transpose(out=PSB(0, 0, 128, c * NS, (c + 1) * NS), in_=shb[0:NS, c * 128:(c + 1) * 128],
                                      identity=ident_b[0:NS, 0:NS])
                return ins
            S.pe(trsh, r=['shb', 'ident_b'], w=[pk(0)])
            X1K = [('x1T', n) for n in range(17)]
            S.dve(lambda e: e.tensor_tensor(out=dT[:, :, T:TT], in0=PSB(0, 0, 128, 0, 8 * NS).rearrange("p (c t) -> p c t", c=8),
                                            in1=x1T[:, :, T:TT], op=ALU.subtract), r=[pk(0)] + X1K, w=['dT'])
            S.dve(lambda e: e.tensor_tensor(out=dT[:, :, 1:T], in0=x1T[:, :, 0:T - 1], in1=x1T[:, :, 1:T], op=ALU.subtract),
                  r=X1K, w=['dT'])
            S.dve(lambda e: e.tensor_scalar(out=dT[:, :, 0:1], in0=x1T[:, :, 0:1], scalar1=-1.0, scalar2=None, op0=ALU.mult),
                  r=X1K, w=['dT'])

            def mk_xsh(i):
                for c in range(8):
                    eng = S.dve if c % 2 == 0 else S.pool
                    if c % 2 == 0:
                        S.dve(lambda e, c=c: e.scalar_tensor_tensor(out=xsh[:, c, :], in0=dT[:, c, :], scalar=muT[:, i, c:c + 1],
                                                                    in1=x1T[:, c, :], op0=ALU.mult, op1=ALU.add),
                              r=['dT', 'muT'] + X1K, w=[('xsh', c)])
                    else:
                        S.pool(lambda e, c=c: e.tensor_scalar(out=xsh[:, c, :], in0=dT[:, c, :], scalar1=muT[:, i, c:c + 1],
                                                              scalar2=None, op0=ALU.mult), r=['dT', 'muT'], w=[('xsh', c)])
                        S.pool(lambda e, c=c: e.tensor_tensor(out=xsh[:, c, :], in0=xsh[:, c, :], in1=x1T[:, c, :], op=ALU.add),
                               r=[('xsh', c)] + X1K, w=[('xsh', c)])
            XSH = [('xsh', c) for c in range(8)]
            wrk_t = I['w_rkvz'].rearrange("(i c p) f -> i p c f", i=4, p=128)
            sgc = [0]
            wlc = [0]

            def proj_fm1(widx, dst, func, bias_i):
                for pr in range(8):
                    wi = wlc[0] % 3
                    wlc[0] += 1
                    S.dma('pool', lambda e, wi=wi, pr=pr: e.dma_start(out=wl1[wi], in_=wrk_t[widx][:, :, pr * 128:(pr + 1) * 128]),
                          w=[('wl1', wi)])
                    si = sgc[0] % 3
                    sgc[0] += 1
                    for bi_, (c0, n) in enumerate(BLKS):
                        bank = pbank[0] % 4
                        pbank[0] += 1

                        def mm(e, wi=wi, c0=c0, n=n, bank=bank):
                            ins = None
                            for c in range(8):
                                ins = e.matmul(PS(bank, 0, 128, 0, n), lhsT=wl1[wi][:, c, :], rhs=xsh[:, c, c0:c0 + n],
                                               start=(c == 0), stop=(c == 7))
                            return ins
                        S.pe(mm, r=[('wl1', wi)] + XSH, w=[pk(bank)])
                        cc0 = c0 if c0 < T else 0
                        sdst = stg[si][:, cc0 % 1024:cc0 % 1024 + n] if c0 < T else stg[si][:, 0:n]
                        S.act(lambda e, bank=bank, n=n, sdst=sdst: e.copy(out=sdst, in_=PS(bank, 0, 128, 0, n)),
                              r=[pk(bank)], w=[('stg', si)])
                        if bi_ in (1, 3, 4):
                            lo = {1: 0, 3: 1024, 4: T}[bi_]
                            wdt = 1024 if bi_ != 4 else NS
                            S.dma('sp', lambda e, si=si, pr=pr, lo=lo, wdt=wdt: e.dma_start(out=dst[:, pr, lo:lo + wdt],
                                                                                           in_=stg[si][:, 0:wdt]),
                                  r=[('stg', si)], w=[('scr', widx, pr)])
                            if bi_ != 4:
                                si = sgc[0] % 3
                                sgc[0] += 1

            def proj_tm1(widx, dst, silu):
                for c in range(8):
                    S.dma('pool', lambda e, c=c: e.dma_start(out=wfull[:, c, :], in_=wrk_t[widx][:, c, :]), w=[('wfull', c)])
                WF = [('wfull', c) for c in range(8)]
                for n in range(17):
                    rows = 128 if n < 16 else NS
                    c0 = n * 128 if n < 16 else T
                    si = sgc[0] % 3
                    sgc[0] += 1

                    def mm(e, rows=rows, c0=c0):
                        ins = None
                        for cb in range(2):
                            for c in range(8):
                                ins = e.matmul(PS(4 + cb, 0, rows, 0, 512), lhsT=xsh[:, c, c0:c0 + rows],
                                               rhs=wfull[:, c, cb * 512:(cb + 1) * 512], start=(c == 0), stop=(c == 7))
                        return ins
                    S.pe(mm, r=WF + XSH, w=[pk(4), pk(5)])
                    for cb in range(2):
                        if silu:
                            S.act(lambda e, cb=cb, rows=rows, si=si: e.activation(
                                out=stg[si][0:rows, cb * 512:(cb + 1) * 512], in_=PS(4 + cb, 0, rows, 0, 512), func=AF.Silu),
                                r=[pk(4 + cb)], w=[('stg', si)])
                        else:
                            S.act(lambda e, cb=cb, rows=rows, si=si: e.copy(out=stg[si][0:rows, cb * 512:(cb + 1) * 512],
                                                                            in_=PS(4 + cb, 0, rows, 0, 512)),
                                  r=[pk(4 + cb)], w=[('stg', si)])
                    S.dma('sp', lambda e, rows=rows, c0=c0, si=si: e.dma_start(out=dst[c0:c0 + rows, :], in_=stg[si][0:rows, :]),
                          r=[('stg', si)], w=[('scrt', widx, n)])

            def proj_lora(mi, w1n, w2n, vi, dst, use_tanh):
                S.dma('pool', lambda e: e.dma_start(out=lo1, in_=I[w1n].rearrange("(c p) f -> p c f", p=128)), w=['lo1'])
                S.dma('pool', lambda e: e.dma_start(out=lo2, in_=I[w2n]), w=['lo2'])
                for (c0, n) in BLKS:
                    bank = pbank[0] % 4
                    pbank[0] += 1

                    def mm(e, c0=c0, n=n, bank=bank):
                        ins = None
                        for c in range(8):
                            ins = e.matmul(PS(bank, 0, 64, 0, n), lhsT=lo1[:, c, :], rhs=xsh[:, c, c0:c0 + n],
                                           start=(c == 0), stop=(c == 7))
                        return ins
                    S.pe(mm, r=['lo1'] + XSH, w=[pk(bank)])
                    if use_tanh:
                        S.act(lambda e, c0=c0, n=n, bank=bank: e.activation(out=t1T[:, c0:c0 + n], in_=PS(bank, 0, 64, 0, n),
                                                                            func=AF.Tanh), r=[pk(bank)], w=[('t1T', c0)])
                    else:
                        S.act(lambda e, c0=c0, n=n, bank=bank: e.copy(out=t1T[:, c0:c0 + n], in_=PS(bank, 0, 64, 0, n)),
                              r=[pk(bank)], w=[('t1T', c0)])
                T1K = [('t1T', c0) for (c0, n) in BLKS]
                for pr in range(8):
                    si = sgc[0] % 3
                    sgc[0] += 1
                    for bi_, (c0, n) in enumerate(BLKS):
                        bank = pbank[0] % 4
                        pbank[0] += 1
                        S.pe(lambda e, pr=pr, c0=c0, n=n, bank=bank: e.matmul(
                            PS(bank, 0, 128, 0, n), lhsT=lo2[:, pr * 128:(pr + 1) * 128], rhs=t1T[:, c0:c0 + n],
                            start=True, stop=True), r=['lo2'] + T1K, w=[pk(bank)])
                        sdst = stg[si][:, c0 % 1024:c0 % 1024 + n] if c0 < T else stg[si][:, 0:n]
                        S.act(lambda e, bank=bank, n=n, sdst=sdst, pr=pr: e.activation(
                            out=sdst, in_=PS(bank, 0, 128, 0, n), func=AF.Sigmoid, bias=vecs[:, vi, pr:pr + 1]),
                            r=[pk(bank), ('vecs', vi)], w=[('stg', si)])
                        if bi_ in (1, 3, 4):
                            lo = {1: 0, 3: 1024, 4: T}[bi_]
                            wdt = 1024 if bi_ != 4 else NS
                            S.dma('sp', lambda e, si=si, pr=pr, lo=lo, wdt=wdt: e.dma_start(out=dst[:, pr, lo:lo + wdt],
                                                                                           in_=stg[si][:, 0:wdt]),
                                  r=[('stg', si)], w=[('scr', mi, pr)])
                            if bi_ != 4:
                                si = sgc[0] % 3
                                sgc[0] += 1

            mk_xsh(0)
            proj_fm1(0, r_scr, None, None)
            mk_xsh(1)
            proj_fm1(1, k_scr, None, None)
            mk_xsh(2)
            proj_tm1(2, v_scr, False)
            mk_xsh(3)
            proj_tm1(3, z_scr, True)
            mk_xsh(4)
            proj_lora(4, 'w1', 'w2', 0, sw_scr, True)
            mk_xsh(5)
            proj_lora(5, 'a1', 'a2', 1, a_scr, False)
            S.barrier()
            A.release()
            stage_end(12)
            A.top = xT_off
            A.marks = []
            A.mark()
            C = 64
            NEG_E05 = -float(np.exp(-0.5))
            wo_o = A.alloc([128, 8, DM], BF16)
            for c in range(8):
                S.dma('pool', lambda e, c=c: e.dma_start(out=wo_o[:, c, :], in_=I['w_out_odd'][c * 128:(c + 1) * 128, :]),
                      w=[('wo_o', c)])
            WOO = [('wo_o', c) for c in range(8)]
            gnp = A.alloc([64, 4, DM], F32)
            S.dma('sp', lambda e: e.dma_start(out=gnp[:, 0, :], in_=I['gn_g'].partition_broadcast(64)), w=[('gnp', 0)])
            S.dma('sp', lambda e: e.dma_start(out=gnp[:, 1, :], in_=I['gn_b'].partition_broadcast(64)), w=[('gnp', 1)])
            S.dma('sp', lambda e: e.dma_start(out=gnp[:, 2, :], in_=I['ln_g'][1:2, :].partition_broadcast(64)), w=[('gnp', 2)])
            S.dma('sp', lambda e: e.dma_start(out=gnp[:, 3, :], in_=I['ln_b'][1:2, :].partition_broadcast(64)), w=[('gnp', 3)])
            vec1 = A.alloc([64, 5, 16], F32)
            for i, nm in enumerate(['w0', 'a0', 'k_k', 'k_a', 'r_k']):
                S.dma('sp', lambda e, i=i, nm=nm: e.dma_start(out=vec1[:, i, :], in_=I[nm].rearrange("o (ph c) -> c (o ph)", c=64),
                                                              allow_slow_non_contiguous=True), w=['vec1'])
            m_lt = A.alloc([64, 64], F32)
            m_le = A.alloc([64, 64], F32)
            m_gt = A.alloc([64, 64], F32)
            blk1 = A.alloc([64, 64], F32)
            sel2 = A.alloc([64, 1], BF16)
            ones_t = A.alloc([64, 64], F32)

            def l1const(e):
                e.memset(ones_t, 1.0)
                e.memset(m_lt, 1.0)
                e.memset(m_le, 1.0)
                e.memset(m_gt, 1.0)
                e.affine_select(out=m_lt, in_=m_lt, pattern=[[1, 64]], compare_op=ALU.is_ge, fill=freg(e, 0.0), base=-1,
                                channel_multiplier=-1)
                e.affine_select(out=m_le, in_=m_le, pattern=[[1, 64]], compare_op=ALU.is_ge, fill=freg(e, 0.0), base=0,
                                channel_multiplier=-1)
                e.affine_select(out=m_gt, in_=m_gt, pattern=[[-1, 64]], compare_op=ALU.is_ge, fill=freg(e, 0.0), base=-1,
                                channel_multiplier=1)
                e.memset(blk1, 1.0)
                return e.memset(sel2, 1.0)
            S.pool(l1const, w=['l1c'])

            ST = A.alloc([64, 16, 64], F32)
            STb = A.alloc([64, 16, 64], BF16)
            fr = A.alloc([64, 16, C], F32)
            fk = A.alloc([64, 16, C], F32)
            fw = A.alloc([64, 16, C], F32)
            fa = A.alloc([64, 16, C], F32)
            cl = A.alloc([64, 16, C], F32)
            L1 = A.alloc([64, 16, C], F32)
            L2 = A.alloc([64, 16, C], F32)
            L3 = A.alloc([64, 16, C], F32)
            Lend = A.alloc([64, 16], F32)
            t_a = A.alloc([64, 16, C], F32)
            t_b = A.alloc([64, 16, C], F32)
            kap = A.alloc([64, 16, C], F32)
            kmd = A.alloc([64, 16, C], F32)
            kt_b = A.alloc([64, 16, C], BF16)
            bt_b = A.alloc([64, 16, C], BF16)
            ktl_b = A.alloc([64, 16, C], BF16)
            rt_b = A.alloc([64, 16, C], BF16)
            bh_f = A.alloc([64, 16, C], BF16)
            kh_f = A.alloc([64, 16, C], BF16)
            pr_b = A.alloc([64, 16, C], BF16)
            Bh = A.alloc([64, DM], BF16)
            Kh = A.alloc([64, DM], BF16)
            Vf = A.alloc([64, DM], F32)
            Vb = A.alloc([64, DM], BF16)
            Zf = A.alloc([64, DM], F32)
            gN = A.alloc([64, 16, 64], BF16)
            gNT = A.alloc([64, 16, 64], BF16)
            gAk = A.alloc([64, 16, 64], BF16)
            gBb = A.alloc([64, 16, 64], BF16)
            gBk = A.alloc([64, 16, 64], BF16)
            gX = [A.alloc([64, 16, 64], BF16) for _ in range(2)]
            gP = [A.alloc([64, 16, 64], BF16) for _ in range(2)]
            gPT = [A.alloc([64, 16, 64], BF16) for _ in range(2)]
            Rm = A.alloc([64, DM], BF16)
            Ub = A.alloc([64, DM], BF16)
            Yf = A.alloc([64, DM], F32)
            Yc = A.alloc([64, DM], F32)
            st1 = A.alloc([64, 16], F32)
            st2 = A.alloc([64, 16], F32)
            rkb = A.alloc([64, 16], F32)
            gb = A.alloc([64, DM], BF16)
            gT = A.alloc([128, 8, 64], BF16)
            xr = A.alloc([64, DM], F32)
            bst1 = A.alloc([64, 2, 6], F32)
            mv1 = A.alloc([64, 4], F32)
            wko = A.alloc([64, 16, 64], F32)
            stin2 = A.alloc([128, 8, 64], F32)

            def hp_rows(ap3, h):
                return ap3[:, h, :]

            def chunk(cols, ncol, first, xsrc_rows, yout, yrows, ck):
                pad = ncol < C
                if pad:
                    cols = cols - (C - 1)
                for (tile_, scr, nm) in ((fr, r_scr, 'fr'), (fk, k_scr, 'fk'), (fw, sw_scr, 'fw'), (fa, a_scr, 'fa')):
                    S.dma('sp', lambda e, tile_=tile_, scr=scr: e.dma_start(
                        out=tile_.rearrange("c (pr hp) t -> c pr hp t", hp=2),
                        in_=scr.rearrange("(hp c) pr t -> c pr hp t", hp=2)[:, :, :, cols:cols + C]), r=[('scr_all',)], w=[nm])
                    if pad:
                        S.pool(lambda e, tile_=tile_: e.memset(tile_[:, :, 0:C - 1], 0.0), r=[nm], w=[nm])
                S.dma('sp', lambda e: e.dma_start(out=Vf, in_=v_scr[cols:cols + C, :]), r=[('scr_all',)], w=['Vf'])
                S.dma('sp', lambda e: e.dma_start(out=Zf, in_=z_scr[cols:cols + C, :]), r=[('scr_all',)], w=['Zf'])
                S.dma('sp', lambda e: e.dma_start(out=xr, in_=x1_scr[cols:cols + C, :]), r=[('scr_all',)], w=['xr'])
                if pad:
                    S.pool(lambda e: e.memset(Vf[0:C - 1, :], 0.0), r=['Vf'], w=['Vf'])
                    S.pool(lambda e: e.memset(Zf[0:C - 1, :], 0.0), r=['Zf'], w=['Zf'])
                S.act(lambda e: e.copy(out=Vb, in_=Vf), r=['Vf'], w=['Vb'])
                S.dve(lambda e: e.tensor_scalar(out=fw, in0=fw, scalar1=NEG_E05, scalar2=None, op0=ALU.mult), r=['fw'], w=['fw'])

                def scans(e):
                    ins = None
                    for pr_ in range(16):
                        ins = e.tensor_tensor_scan(out=cl[:, pr_, :], data0=ones_t, data1=fw[:, pr_, :], initial=0.0,
                                                   op0=ALU.mult, op1=ALU.add)
                    return ins
                S.dve(scans, r=['fw', 'l1c'], w=['cl'])
                S.act(lambda e: e.activation(out=L1, in_=cl, func=AF.Exp), r=['cl'], w=['L1'])
                S.act(lambda e: e.activation(out=L3, in_=cl, func=AF.Exp, scale=-1.0), r=['cl'], w=['L3'])
                S.dve(lambda e: e.tensor_tensor(out=t_a, in0=cl, in1=fw, op=ALU.subtract), r=['cl', 'fw'], w=['t_a'])
                S.act(lambda e: e.activation(out=L2, in_=t_a, func=AF.Exp), r=['t_a'], w=['L2'])
                S.dve(lambda e: e.tensor_copy(out=Lend, in_=L1[:, :, C - 1]), r=['L1'], w=['Lend'])
                if ck == 0:
                    stage_end(131)
                S.dve(lambda e: e.tensor_tensor(out=kap, in0=fk, in1=vec1[:, 2, :].unsqueeze(2).broadcast_to([64, 16, C]),
                                                op=ALU.mult), r=['fk', 'vec1'], w=['kap'])
                S.dve(lambda e: e.tensor_tensor(out=t_b, in0=kap, in1=kap, op=ALU.mult), r=['kap'], w=['t_b'])
                def ssqm(e):
                    e.matmul(PS(0, 0, 64, 0, 512), lhsT=blk1, rhs=t_b[:, 0:8, :].rearrange("p a t -> p (a t)"), start=True, stop=True)
                    return e.matmul(PS(1, 0, 64, 0, 512), lhsT=blk1, rhs=t_b[:, 8:16, :].rearrange("p a t -> p (a t)"),
                                    start=True, stop=True)
                S.pe(ssqm, r=['t_b', 'l1c'], w=[pk(0), pk(1)])
                for hb in range(2):
                    S.dve(lambda e, hb=hb: e.tensor_scalar(out=t_b[:, hb * 8:(hb + 1) * 8, :].rearrange("p a t -> p (a t)"),
                                                           in0=PS(hb, 0, 64, 0, 512), scalar1=1e-24, scalar2=None, op0=ALU.max),
                          r=[pk(hb)], w=['t_b'])
                S.act(lambda e: e.activation(out=t_b, in_=t_b, func=AF.Ln), r=['t_b'], w=['t_b'])
                S.act(lambda e: e.activation(out=t_b, in_=t_b, func=AF.Exp, scale=-0.5), r=['t_b'], w=['t_b'])
                S.dve(lambda e: e.tensor_tensor(out=kap, in0=kap, in1=t_b, op=ALU.mult), r=['kap', 't_b'], w=['kap'])
                S.dve(lambda e: e.tensor_scalar(out=t_a, in0=fa, scalar1=-1.0, scalar2=None, op0=ALU.add), r=['fa', 'L2'], w=['t_a'])
                S.dve(lambda e: e.tensor_tensor(out=t_a, in0=t_a, in1=vec1[:, 3, :].unsqueeze(2).broadcast_to([64, 16, C]),
                                                op=ALU.mult), r=['t_a', 'vec1'], w=['t_a'])
                S.dve(lambda e: e.scalar_tensor_tensor(out=kmd, in0=t_a, scalar=1.0, in1=fk, op0=ALU.add, op1=ALU.mult),
                      r=['t_a', 'fk'], w=['kmd'])
                S.dve(lambda e: e.tensor_tensor(out=kt_b, in0=kap, in1=L2, op=ALU.mult), r=['kap', 'L2'], w=['kt_b'])
                S.dve(lambda e: e.tensor_tensor(out=t_a, in0=kap, in1=fa, op=ALU.mult), r=['kap', 'fa', 'kmd'], w=['t_a'])
                S.dve(lambda e: e.tensor_tensor(out=t_a, in0=t_a, in1=L3, op=ALU.mult), r=['t_a', 'L3'], w=['t_a'])
                S.act(lambda e: e.copy(out=bt_b, in_=t_a), r=['t_a'], w=['bt_b'])
                S.dve(lambda e: e.tensor_tensor(out=bh_f, in0=t_a, in1=Lend.unsqueeze(2).broadcast_to([64, 16, C]), op=ALU.mult),
                      r=['t_a', 'Lend'], w=['bh_f'])
                S.dve(lambda e: e.tensor_tensor(out=t_b, in0=kmd, in1=L3, op=ALU.mult), r=['kmd', 'L3', 'kap'], w=['t_b'])
                S.act(lambda e: e.copy(out=ktl_b, in_=t_b), r=['t_b'], w=['ktl_b'])
                S.dve(lambda e: e.tensor_tensor(out=kh_f, in0=t_b, in1=Lend.unsqueeze(2).broadcast_to([64, 16, C]), op=ALU.mult),
                      r=['t_b', 'Lend'], w=['kh_f'])
                S.dve(lambda e: e.tensor_tensor(out=rt_b, in0=fr, in1=L1, op=ALU.mult), r=['fr', 'L1'], w=['rt_b'])
                S.dve(lambda e: e.tensor_tensor(out=t_b, in0=fr, in1=kmd, op=ALU.mult), r=['fr', 'kmd', 'ktl_b', 'kh_f'], w=['t_b'])
                S.dve(lambda e: e.tensor_tensor(out=pr_b, in0=t_b, in1=vec1[:, 4, :].unsqueeze(2).broadcast_to([64, 16, C]),
                                                op=ALU.mult), r=['t_b', 'vec1'], w=['pr_b'])
                for (src, dst_, nm, bank) in ((bh_f, Bh, 'Bh', 2), (kh_f, Kh, 'Kh', 3)):
                    def trb(e, src=src, bank=bank):
                        ins = None
                        for h in range(16):
                            ins = e.transpose(out=PSB(bank, 0, 64, h * 64, (h + 1) * 64), in_=src[:, h, :],
                                              identity=ident_b[0:64, 0:64])
                        return ins
                    S.pe(trb, r=[nm.lower() + '_f' if False else ('bh_f' if nm == 'Bh' else 'kh_f'), 'ident_b'], w=[pk(bank)])
                    S.act(lambda e, dst_=dst_, bank=bank: e.copy(out=dst_, in_=PSB(bank, 0, 64, 0, 1024)), r=[pk(bank)], w=[nm])

                def rkm(e):
                    ins = None
                    for h in range(16):
                        ins = e.matmul(PS(1, 0, 64, 2 * h, 2 * h + 1), lhsT=pr_b[:, h, :], rhs=sel2, start=True, stop=True)
                    return ins
                S.pe(rkm, r=['pr_b', 'l1c'], w=[pk(1)])
                S.act(lambda e: e.copy(out=rkb, in_=PS(1, 0, 64, 0, 32).rearrange("p (h two) -> p h two", two=2)[:, :, 0]),
                      r=[pk(1)], w=['rkb'])
                if ck == 0:
                    stage_end(132)
                def gram(lt, rt, dst_, mask, banks, nm, deps):
                    def gm(e):
                        ins = None
                        for h in range(16):
                            ins = e.matmul(PS(banks[h // 8], 0, 64, (h % 8) * 64, (h % 8 + 1) * 64), lhsT=hp_rows(lt, h),
                                           rhs=hp_rows(rt, h), start=True, stop=True)
                        return ins
                    S.pe(gm, r=deps, w=[pk(banks[0]), pk(banks[1])])
                    for hb in range(2):
                        S.dve(lambda e, hb=hb: e.tensor_tensor(
                            out=dst_[:, hb * 8:(hb + 1) * 8, :], in0=PS(banks[hb], 0, 64, 0, 512).rearrange("p (h t) -> p h t", h=8),
                            in1=mask.unsqueeze(1).broadcast_to([64, 8, 64]), op=ALU.mult),
                            r=[pk(banks[hb]), 'l1c'], w=[(nm, hb)])
                gram(bt_b, kt_b, gN, m_lt, (4, 5), 'gN', ['bt_b', 'kt_b'])
                gram(kt_b, bt_b, gNT, m_gt, (6, 7), 'gNT', ['bt_b', 'kt_b'])
                gram(ktl_b, kt_b, gAk, m_lt, (4, 5), 'gAk', ['ktl_b', 'kt_b'])
                gram(bt_b, rt_b, gBb, m_le, (6, 7), 'gBb', ['bt_b', 'rt_b'])
                gram(ktl_b, rt_b, gBk, m_le, (4, 5), 'gBk', ['ktl_b', 'rt_b'])
                if ck == 0:
                    stage_end(133)
                for hb in range(2):
                    S.dve(lambda e, hb=hb: e.scalar_tensor_tensor(
                        out=gX[0][:, hb * 8:(hb + 1) * 8, :], in0=gN[:, hb * 8:(hb + 1) * 8, :], scalar=-1.0,
                        in1=ident_b[0:64, 0:64].unsqueeze(1).broadcast_to([64, 8, 64]), op0=ALU.mult, op1=ALU.add),
                        r=[('gN', hb), 'ident_b'], w=[('gX', 0, hb)])
                Pc, PTc, Pk, PTk = gN, gNT, 'gN', 'gNT'
                xi = 0
                for lev in range(5):
                    Pn, PTn = gP[lev % 2], gPT[lev % 2]
                    Pnk, PTnk = ('gP', lev % 2), ('gPT', lev % 2)

                    def sq(e, Pc=Pc, PTc=PTc):
                        ins = None
                        for h in range(16):
                            ins = e.matmul(PS(h // 8, 0, 64, (h % 8) * 64, (h % 8 + 1) * 64), lhsT=PTc[:, h, :], rhs=Pc[:, h, :],
                                           start=True, stop=True)
                        for h in range(16):
                            ins = e.matmul(PS(2 + h // 8, 0, 64, (h % 8) * 64, (h % 8 + 1) * 64), lhsT=Pc[:, h, :],
                                           rhs=PTc[:, h, :], start=True, stop=True)
                        return ins
                    S.pe(sq, r=[(Pk, 0), (Pk, 1), (PTk, 0), (PTk, 1)] if isinstance(Pk, str) else
                         [Pk + (0,), Pk + (1,), PTk + (0,), PTk + (1,)], w=[pk(0), pk(1), pk(2), pk(3)])
                    for hb in range(2):
                        S.act(lambda e, hb=hb, Pn=Pn: e.copy(out=Pn[:, hb * 8:(hb + 1) * 8, :].rearrange("p h t -> p (h t)"),
                                                             in_=PS(hb, 0, 64, 0, 512)), r=[pk(hb)], w=[Pnk + (hb,)])
                        S.dve(lambda e, hb=hb, PTn=PTn: e.tensor_copy(out=PTn[:, hb * 8:(hb + 1) * 8, :].rearrange("p h t -> p (h t)"),
                                                                      in_=PS(2 + hb, 0, 64, 0, 512)), r=[pk(2 + hb)], w=[PTnk + (hb,)])
                    Xo, Xn = gX[xi % 2], gX[(xi + 1) % 2]

                    def xm(e, PTn=PTn, Xo=Xo):
                        ins = None
                        for h in range(16):
                            ins = e.matmul(PS(4 + h // 8, 0, 64, (h % 8) * 64, (h % 8 + 1) * 64), lhsT=PTn[:, h, :], rhs=Xo[:, h, :],
                                           start=True, stop=True)
                        return ins
                    S.pe(xm, r=[PTnk + (0,), PTnk + (1,), ('gX', xi % 2, 0), ('gX', xi % 2, 1)], w=[pk(4), pk(5)])
                    for hb in range(2):
                        S.dve(lambda e, hb=hb, Xo=Xo, Xn=Xn: e.tensor_tensor(
                            out=Xn[:, hb * 8:(hb + 1) * 8, :].rearrange("p h t -> p (h t)"), in0=PS(4 + hb, 0, 64, 0, 512),
                            in1=Xo[:, hb * 8:(hb + 1) * 8, :].rearrange("p h t -> p (h t)"), op=ALU.add),
                            r=[pk(4 + hb), ('gX', xi % 2, hb)], w=[('gX', (xi + 1) % 2, hb)])
                    xi += 1
                    Pc, PTc, Pk, PTk = Pn, PTn, Pnk, PTnk
                Xf = gX[xi % 2]
                XFK = [('gX', xi % 2, 0), ('gX', xi % 2, 1)]
                if ck == 0:
                    stage_end(134)
                if ck == 100:
                    stage_end(140)
                if first is not None:
                    first()
                if ck == 100:
                    stage_end(141)

                def m1(e):
                    ins = None
                    for h in range(16):
                        o_ = PS(h // 8, 0, 64, (h % 8) * 64, (h % 8 + 1) * 64)
                        e.matmul(o_, lhsT=hp_rows(kt_b, h), rhs=hp_rows(STb, h), start=True, stop=False)
                        ins = e.matmul(o_, lhsT=gAk[:, h, :], rhs=Vb[:, h * 64:(h + 1) * 64], start=False, stop=True)
                    return ins
                S.pe(m1, r=['kt_b', 'STb', ('gAk', 0), ('gAk', 1), 'Vb'], w=[pk(0), pk(1)])
                for hb in range(2):
                    S.act(lambda e, hb=hb: e.activation(out=Rm[:, hb * 512:(hb + 1) * 512], in_=PS(hb, 0, 64, 0, 512),
                                                        func=AF.Copy, scale=-1.0), r=[pk(hb)], w=[('Rm', hb)])

                def m3(e):
                    ins = None
                    for h in range(16):
                        ins = e.matmul(PS(2 + h // 8, 0, 64, (h % 8) * 64, (h % 8 + 1) * 64), lhsT=Xf[:, h, :],
                                       rhs=Rm[:, h * 64:(h + 1) * 64], start=True, stop=True)
                    return ins
                S.pe(m3, r=XFK + [('Rm', 0), ('Rm', 1)], w=[pk(2), pk(3)])
                for hb in range(2):
                    S.act(lambda e, hb=hb: e.copy(out=Ub[:, hb * 512:(hb + 1) * 512], in_=PS(2 + hb, 0, 64, 0, 512)),
                          r=[pk(2 + hb)], w=[('Ub', hb)])

                def m4(e):
                    ins = None
                    for h in range(16):
                        o_ = PS(4 + h // 8, 0, 64, (h % 8) * 64, (h % 8 + 1) * 64)
                        e.matmul(o_, lhsT=hp_rows(rt_b, h), rhs=hp_rows(STb, h), start=True, stop=False)
                        e.matmul(o_, lhsT=gBb[:, h, :], rhs=Ub[:, h * 64:(h + 1) * 64], start=False, stop=False)
                        ins = e.matmul(o_, lhsT=gBk[:, h, :], rhs=Vb[:, h * 64:(h + 1) * 64], start=False, stop=True)
                    return ins
                S.pe(m4, r=['rt_b', 'STb', ('gBb', 0), ('gBb', 1), ('gBk', 0), ('gBk', 1), ('Ub', 0), ('Ub', 1), 'Vb'],
                     w=[pk(4), pk(5)])
                for hb in range(2):
                    S.act(lambda e, hb=hb: e.copy(out=Yf[:, hb * 512:(hb + 1) * 512], in_=PS(4 + hb, 0, 64, 0, 512)),
                          r=[pk(4 + hb)], w=[('Yf', hb)])

                def m5(e):
                    ins = None
                    for h in range(16):
                        o_ = PS(6 + h // 8, 0, 64, (h % 8) * 64, (h % 8 + 1) * 64)
                        e.matmul(o_, lhsT=Bh[:, h * 64:(h + 1) * 64], rhs=Ub[:, h * 64:(h + 1) * 64], start=True, stop=False)
                        ins = e.matmul(o_, lhsT=Kh[:, h * 64:(h + 1) * 64], rhs=Vb[:, h * 64:(h + 1) * 64], start=False, stop=True)
                    return ins
                S.pe(m5, r=['Bh', 'Kh', ('Ub', 0), ('Ub', 1), 'Vb'], w=[pk(6), pk(7)])
                S.dve(lambda e: e.tensor_tensor(out=ST, in0=ST, in1=Lend.unsqueeze(2).broadcast_to([64, 16, 64]), op=ALU.mult),
                      r=['ST', 'Lend', 'STb'], w=['ST'])
                for hb in range(2):
                    S.dve(lambda e, hb=hb: e.tensor_tensor(out=ST[:, hb * 8:(hb + 1) * 8, :], in0=ST[:, hb * 8:(hb + 1) * 8, :],
                                                           in1=PS(6 + hb, 0, 64, 0, 512).rearrange("p (a v) -> p a v", a=8), op=ALU.add),
                          r=['ST', pk(6 + hb)], w=['ST'])
                S.act(lambda e: e.copy(out=STb, in_=ST), r=['ST'], w=['STb'])
                if ck == 0:
                    stage_end(135)
                Y3 = Yf.rearrange("p (h v) -> p h v", h=16)
                Yc3 = Yc.rearrange("p (h v) -> p h v", h=16)
                YK = [('Yf', 0), ('Yf', 1)]
                S.dve(lambda e: e.tensor_reduce(out=st1, in_=Y3, axis=AX.X, op=ALU.add), r=YK, w=['st1'])
                S.dve(lambda e: e.tensor_scalar(out=st1, in0=st1, scalar1=1.0 / 64.0, scalar2=None, op0=ALU.mult), r=['st1'], w=['st1'])
                S.dve(lambda e: e.tensor_tensor(out=Yc3, in0=Y3, in1=st1.unsqueeze(2).broadcast_to([64, 16, 64]), op=ALU.subtract),
                      r=YK + ['st1'], w=['Yc'])
                S.pool(lambda e: e.tensor_tensor(out=Yf, in0=Yc, in1=Yc, op=ALU.mult), r=['Yc'] + YK, w=YK)
                S.dve(lambda e: e.tensor_reduce(out=st2, in_=Y3, axis=AX.X, op=ALU.add), r=YK, w=['st2'])
                S.dve(lambda e: e.tensor_scalar(out=st2, in0=st2, scalar1=1.0 / 64.0, scalar2=64e-5, op0=ALU.mult, op1=ALU.add),
                      r=['st2'], w=['st2'])
                S.act(lambda e: e.activation(out=st2, in_=st2, func=AF.Ln), r=['st2'], w=['st2'])
                S.act(lambda e: e.activation(out=st2, in_=st2, func=AF.Exp, scale=-0.5), r=['st2'], w=['st2'])
                S.dve(lambda e: e.tensor_tensor(out=Yc3, in0=Yc3, in1=st2.unsqueeze(2).broadcast_to([64, 16, 64]), op=ALU.mult),
                      r=['Yc', 'st2'], w=['Yc'])
                S.dve(lambda e: e.tensor_tensor(out=Yc, in0=Yc, in1=gnp[:, 0, :], op=ALU.mult), r=['Yc', ('gnp', 0)], w=['Yc'])
                S.pool(lambda e: e.tensor_tensor(out=Yc, in0=Yc, in1=gnp[:, 1, :], op=ALU.add), r=['Yc', ('gnp', 1)], w=['Yc'])
                S.dve(lambda e: e.tensor_tensor(out=Y3, in0=Vf.rearrange("p (h v) -> p h v", h=16),
                                                in1=rkb.unsqueeze(2).broadcast_to([64, 16, 64]), op=ALU.mult),
                      r=['Vf', 'rkb'] + YK, w=YK)
                S.pool(lambda e: e.tensor_tensor(out=Yc, in0=Yc, in1=Yf, op=ALU.add), r=['Yc'] + YK, w=['Yc'])
                S.dve(lambda e: e.tensor_tensor(out=gb, in0=Yc, in1=Zf, op=ALU.mult), r=['Yc', 'Zf'], w=['gb'])
                if ck == 0:
                    stage_end(136)
                def trg(e):
                    ins = None
                    for c in range(8):
                        ins = e.transpose(out=PSB(7, 0, 128, c * 64, (c + 1) * 64), in_=gb[:, c * 128:(c + 1) * 128],
                                          identity=ident_b[0:64, 0:64])
                    return ins
                S.pe(trg, r=['gb', 'ident_b'], w=[pk(7)])
                S.act(lambda e: e.copy(out=gT.rearrange("p c t -> p (c t)"), in_=PSB(7, 0, 128, 0, 512)), r=[pk(7)], w=['gT'])

                def mo(e):
                    ins = None
                    for cb in range(2):
                        for c in range(8):
                            ins = e.matmul(PS(cb, 0, 64, 0, 512), lhsT=gT[:, c, :], rhs=wo_o[:, c, cb * 512:(cb + 1) * 512],
                                           start=(c == 0), stop=(c == 7))
                    return ins
                S.pe(mo, r=['gT'] + WOO, w=[pk(0), pk(1)])
                for cb in range(2):
                    S.dve(lambda e, cb=cb: e.scalar_tensor_tensor(
                        out=xr[:, cb * 512:(cb + 1) * 512], in0=xr[:, cb * 512:(cb + 1) * 512], scalar=ALPHA,
                        in1=PS(cb, 0, 64, 0, 512), op0=ALU.mult, op1=ALU.add), r=['xr', pk(cb)], w=['xr'])
                    S.dve(lambda e, cb=cb: e.bn_stats(out=bst1[:, cb, :], in_=xr[:, cb * 512:(cb + 1) * 512]), r=['xr'], w=[('bst1', cb)])
                S.dve(lambda e: e.bn_aggr(out=mv1[:, 0:2], in_=bst1.rearrange("p a b -> p (a b)")), r=[('bst1', 0), ('bst1', 1)],
                      w=['mv1'])
                S.dve(lambda e: e.tensor_scalar(out=mv1[:, 2:3], in0=mv1[:, 1:2], scalar1=1e-5, scalar2=None, op0=ALU.add),
                      r=['mv1'], w=['mv1b'])
                S.act(lambda e: e.activation(out=mv1[:, 2:3], in_=mv1[:, 2:3], func=AF.Ln), r=['mv1b'], w=['mv1b'])
                S.act(lambda e: e.activation(out=mv1[:, 2:3], in_=mv1[:, 2:3], func=AF.Exp, scale=-0.5), r=['mv1b'], w=['mv1b'])
                S.dve(lambda e: e.tensor_scalar(out=xr, in0=xr, scalar1=mv1[:, 0:1], scalar2=mv1[:, 2:3], op0=ALU.subtract,
                                                op1=ALU.mult), r=['xr', 'mv1', 'mv1b'], w=['xr'])
                S.dve(lambda e: e.tensor_tensor(out=xr, in0=xr, in1=gnp[:, 2, :], op=ALU.mult), r=['xr', ('gnp', 2)], w=['xr'])
                S.pool(lambda e: e.tensor_tensor(out=xr, in0=xr, in1=gnp[:, 3, :], op=ALU.add), r=['xr', ('gnp', 3)], w=['xr'])
                S.dma('sp', lambda e: e.dma_start(out=yout, in_=(xr[C - 1:C, :] if pad else xr)), r=['xr'])
                if ck == 0:
                    stage_end(137)
                if ck == 1:
                    stage_end(139)
                if ck == 100:
                    stage_end(142)

            def write_state(dst):
                def trs(e):
                    ins = None
                    for h in range(16):
                        ins = e.transpose(out=PS(6 + h // 8, 0, 64, (h % 8) * 64, (h % 8 + 1) * 64), in_=ST[:, h, :],
                                          identity=ident_f[0:64, 0:64])
                    return ins
                S.pe(trs, r=['ST', 'ident_f'], w=[pk(6), pk(7)])
                for hb in range(2):
                    S.act(lambda e, hb=hb: e.copy(out=wko[:, hb * 8:(hb + 1) * 8, :].rearrange("p a b -> p (a b)"),
                                                  in_=PS(6 + hb, 0, 64, 0, 512)), r=[pk(6 + hb)], w=['wko'])
                S.dma('sp', lambda e: e.dma_start(out=dst.rearrange("(h v) k -> v h k", h=16), in_=wko), r=['wko'])

            def zero_state():
                S.pool(lambda e: e.memset(ST, 0.0), w=['ST'])
                S.pool(lambda e: e.memset(STb, 0.0), w=['STb'])
            for ci in range(T // C):
                chunk(ci * C, C, zero_state if ci == 0 else None, x1_scr[ci * C:(ci + 1) * C, :],
                      O['y_prompt'][ci * C:(ci + 1) * C, :], C, ci)
            write_state(O['wkv_p'])
            stage_end(138)
            for s in range(NS):
                def load_state(s=s):
                    S.dma('sp', lambda e: e.dma_start(
                        out=stin2, in_=I['state_wkv'][s * 1024:(s + 1) * 1024, :].rearrange("(pr p) k -> p pr k", pr=8)), w=['stin'])

                    def tri(e):
                        ins = None
                        for pr_ in range(8):
                            ins = e.transpose(out=PS(6 + pr_ // 4, 0, 64, (pr_ % 4) * 128, (pr_ % 4 + 1) * 128), in_=stin2[:, pr_, :],
                                              identity=ident_f)
                        return ins
                    S.pe(tri, r=['stin', 'ident_f'], w=[pk(6), pk(7)])
                    for hb in range(2):
                        S.dve(lambda e, hb=hb: e.tensor_copy(out=ST[:, hb * 8:(hb + 1) * 8, :].rearrange("p a v -> p (a v)"),
                                                             in_=PS(6 + hb, 0, 64, 0, 512)), r=[pk(6 + hb)], w=['ST'])
                    S.act(lambda e: e.copy(out=STb, in_=ST), r=['ST'], w=['STb'])

                def zs_dma(s=s):
                    zero_state()

                    def tri2(e):
                        ins = None
                        for pr_ in range(8):
                            ins = e.transpose(out=PS(6 + pr_ // 4, 0, 64, (pr_ % 4) * 128, (pr_ % 4 + 1) * 128), in_=stin2[:, pr_, :],
                                              identity=ident_f)
                        return ins
                    if os.environ.get("KDEV_ZS3"):
                        S.pe(tri2, r=['stin', 'ident_f'], w=[pk(6), pk(7)])
                        z3 = os.environ.get("KDEV_ZS3")
                        for hb in range(2):
                            if z3 in ("2", "4"):
                                S.dve(lambda e, hb=hb: e.tensor_copy(out=ST[:, hb * 8:(hb + 1) * 8, :].rearrange("p a v -> p (a v)"),
                                                                     in_=PS(6 + hb, 0, 64, 0, 512)), r=[pk(6 + hb)], w=['ST'])
                            if z3 in ("3", "4"):
                                S.act(lambda e, hb=hb: e.copy(out=STb[:, hb * 8:(hb + 1) * 8, :].rearrange("p a v -> p (a v)"),
                                                              in_=PS(6 + hb, 0, 64, 0, 512)), r=[pk(6 + hb)], w=['STb'])
                    S.dma('sp', lambda e: e.dma_start(
                        out=stin2, in_=I['state_wkv'][s * 1024:(s + 1) * 1024, :].rearrange("(pr p) k -> p pr k", pr=8)), w=['stin'])
                chunk(T + s, 1, (zero_state if os.environ.get("KDEV_ZS") == "1" else zs_dma if os.environ.get("KDEV_ZS") == "2" else load_state), x1_scr[T + s:T + s + 1, :], O['y_sample'][s:s + 1, :], 1, 100 + s)
                write_state(O['wkv_s'][s * 1024:(s + 1) * 1024, :])
            S.barrier()
            A.release()
            stage_end(13)
        except _Stop:
            pass
        S.analyze()
        S.emit(nc, sems, dsems)
        print("arena peak bytes", A.peak, "ops", len(S.ops))
    return nc


_NC_CACHE = {}


def kernel(**inputs):
    inp = {k: np.asarray(v) for k, v in inputs.items()}
    if 'nc' not in _NC_CACHE:
        _NC_CACHE['nc'] = build()
    nc = _NC_CACHE['nc']
    f = np.ascontiguousarray
    shared = {
        "cache_cmp_a": f(inp['cache_cmp_kv'][0, :NPOOL // 2].reshape(NPOOL * 64, 256)),
        "cache_cmp_b": f(inp['cache_cmp_kv'][0, NPOOL // 2:].reshape(NPOOL * 64, 256)),
        "cache_sel_a": f(inp['cache_sel_kv'][0, :NPOOL // 2].reshape(NPOOL * 64, 256)),
        "cache_sel_b": f(inp['cache_sel_kv'][0, NPOOL // 2:].reshape(NPOOL * 64, 256)),
        "w_in": f(inp['w_in_even'][0]),
        "conv_w": f(inp['conv_w'][0].reshape(31, 512)),
        "conv_b": f(inp['conv_b'].reshape(1, 512)),
        "conv_ln_g": f(inp['conv_ln_g'].reshape(1, 512)),
        "conv_ln_b": f(inp['conv_ln_b'].reshape(1, 512)),
        "wk_cmp": f(inp['wk_cmp'][0]),
        "wv_cmp": f(inp['wv_cmp'][0]),
        "w_out_even": f(inp['w_out_even'][0]),
        "mu_c": f(inp['mu_c'][0]),
        "w_rkvz": f(inp['w_rkvz'][0].reshape(4 * DM, DM)),
        "w0": f(inp['w0'].reshape(1, DM)),
        "w1": f(inp['w1'][0]),
        "w2": f(inp['w2'][0]),
        "a0": f(inp['a0'].reshape(1, DM)),
        "a1": f(inp['a1'][0]),
        "a2": f(inp['a2'][0]),
        "k_k": f(inp['k_k'].reshape(1, DM)),
        "k_a": f(inp['k_a'].reshape(1, DM)),
        "r_k": f(inp['r_k'].reshape(1, DM)),
        "gn_g": f(inp['gn_g'].reshape(1, DM)),
        "gn_b": f(inp['gn_b'].reshape(1, DM)),
        "w_out_odd": f(inp['w_out_odd'][0]),
        "ln_g": f(inp['ln_g']),
        "ln_b": f(inp['ln_b']),
    }
    in_maps = []
    for c in range(NCORES):
        s0, s1 = c * NS, (c + 1) * NS
        m = dict(shared)
        m["xp"] = f(inp['x_prompt'][c])
        m["xs"] = f(inp['x_sample'][s0:s1, 0, :])
        m["cache_win"] = f(inp['cache_win_kv'][0, s0:s1].reshape(NS * 512, 256))
        m["state_conv"] = f(inp['state_conv'][0, s0:s1].reshape(NS * 30, 512))
        m["state_wkv"] = f(inp['state_wkv'][0, s0:s1].reshape(NS * 16 * 64, 64))
        m["state_shift"] = f(inp['state_shift'][0, s0:s1])
        m["page_table"] = f(inp['page_table'][s0:s1].astype(np.int32))
        in_maps.append(m)
    res = run_bass_kernel_spmd(nc, in_maps, core_ids=list(range(NCORES)))
    R = res.results

    def cat(name):
        return np.stack([np.asarray(R[c][name]) for c in range(NCORES)], axis=0)
    y_prompt = cat("y_prompt").reshape(8, T, DM)
    y_sample = cat("y_sample").reshape(32, 1, DM)
    cmp_p = cat("cmp_p").reshape(1, 8, T, 2, 2, 64)
    cmp_s = cat("cmp_s").reshape(1, 32, 1, 2, 2, 64)
    sel_p = cat("sel_p").reshape(1, 8, T, 2, 2, 64)
    sel_s = cat("sel_s").reshape(1, 32, 1, 2, 2, 64)
    win_p = cat("win_p").reshape(1, 8, 512, 2, 2, 64)
    win_s = cat("win_s").reshape(1, 32, 512, 2, 2, 64)
    conv_p = cat("conv_p").reshape(1, 8, 30, 512)
    conv_s = cat("conv_s").reshape(1, 32, 30, 512)
    wkv_p = cat("wkv_p").reshape(1, 8, 16, 64, 64)
    wkv_s = cat("wkv_s").reshape(1, 32, 16, 64, 64)
    shift_p = cat("shift_p").reshape(1, 8, DM)
    shift_s = cat("shift_s").reshape(1, 32, DM)
    return (y_prompt, y_sample, cmp_p, cmp_s, sel_p, sel_s, win_p, win_s, conv_p, conv_s,
            wkv_p, wkv_s, shift_p, shift_s)
```

```python
import numpy as np
import concourse.bass as bass
import concourse.mybir as mybir
from concourse.bass_utils import run_bass_kernel_spmd

F32 = mybir.dt.float32
BF16 = mybir.dt.bfloat16
I32 = mybir.dt.int32
AF = mybir.ActivationFunctionType
ALU = mybir.AluOpType
AX = mybir.AxisListType

NCORES = 8
T = 2048
NS = 4
TT = T + NS
DM = 1024
import os
NPOOL = int(os.environ.get('KDEV_NPOOL', '2560'))
STAGE = int(os.environ.get('KDEV_STAGE', '99'))


class _Stop(Exception):
    pass


def stage_end(k):
    if STAGE == k:
        raise _Stop()
ENGS = ('pe', 'act', 'dve', 'pool', 'sp')
SAME_ENG_DIST = 3
NDS = {'sp': 8, 'act': 4, 'pool': 6}


class Op:
    __slots__ = ('eng', 'fn', 'r', 'w', 'dma', 'deps', 'signal', 'count', 'dsem', 'dcount',
                 'dprev', 'waits', 'idx', 'eidx', 'bar', 'need')


class Sched:
    def __init__(self):
        self.ops = []
        self.nbar = 0

    def add(self, eng, fn, r=(), w=(), dma=False):
        op = Op()
        op.eng, op.fn, op.r, op.w, op.dma = eng, fn, tuple(r), tuple(w), dma
        op.signal = False
        op.bar = None
        op.count = 0
        self.ops.append(op)
        return op

    def pe(self, fn, r=(), w=()):
        return self.add('pe', fn, r, w)

    def act(self, fn, r=(), w=()):
        return self.add('act', fn, r, w)

    def dve(self, fn, r=(), w=()):
        return self.add('dve', fn, r, w)

    def pool(self, fn, r=(), w=()):
        return self.add('pool', fn, r, w)

    def dma(self, q, fn, r=(), w=()):
        return self.add(q, fn, r, w, dma=True)

    def barrier(self):
        self.nbar += 1
        for e in ENGS:
            op = self.add(e, None)
            op.bar = self.nbar

    def analyze(self):
        ops = self.ops
        last_w = {}
        rd = {}
        eng_ops = {e: [] for e in ENGS}
        last_on_eng = {}
        all_dmas = []
        cur_bar = None
        snap = None
        for i, op in enumerate(ops):
            op.idx = i
            deps = set()
            if op.bar is not None:
                if cur_bar != op.bar:
                    cur_bar = op.bar
                    snap = (dict(last_on_eng), list(all_dmas))
                    all_dmas = []
                for f, j in snap[0].items():
                    if f != op.eng and ops[j].bar is None:
                        deps.add(j)
                deps.update(snap[1])
            else:
                for k in op.r:
                    j = last_w.get(k)
                    if j is not None:
                        deps.add(j)
                for k in op.w:
                    j = last_w.get(k)
                    if j is not None:
                        deps.add(j)
                    rr = rd.get(k)
                    if rr:
                        deps.update(rr[0].values())
                        deps.update(rr[1])
                for k in op.r:
                    rr = rd.setdefault(k, ({}, []))
                    if op.dma:
                        rr[1].append(i)
                    else:
                        rr[0][op.eng] = i
                for k in op.w:
                    last_w[k] = i
                    rd[k] = ({}, [])
            deps.discard(i)
            op.deps = deps
            op.eidx = len(eng_ops[op.eng])
            eng_ops[op.eng].append(op)
            last_on_eng[op.eng] = i
            if op.dma:
                all_dmas.append(i)
        for op in ops:
            need = []
            for j in op.deps:
                d = ops[j]
                if d.dma:
                    need.append(j)
                elif d.eng == op.eng:
                    if op.eng == 'pe' and not op.dma:
                        continue
                    if op.eidx - d.eidx > SAME_ENG_DIST:
                        continue
                    need.append(j)
                else:
                    need.append(j)
            op.need = need
            for j in need:
                if not ops[j].dma:
                    ops[j].signal = True
        self.eng_ops = eng_ops

    def emit(self, nc, sems, dsems):
        ops = self.ops
        eng_ops = self.eng_ops
        for e in ENGS:
            cnt = 0
            for op in eng_ops[e]:
                if op.signal:
                    cnt += 1
                    op.count = cnt
        print('SEMCOUNTS', {e: max([op.count for op in eng_ops[e]] + [0]) for e in ENGS}, {e: len(eng_ops[e]) for e in ENGS})
        finals = {}
        for q, pool in dsems.items():
            uses = [0] * len(pool)
            k = 0
            for op in eng_ops[q]:
                if op.dma:
                    s = k % len(pool)
                    k += 1
                    op.dsem = pool[s]
                    op.dprev = 16 * uses[s]
                    uses[s] += 1
                    op.dcount = 16 * uses[s]
                    finals[id(pool[s])] = (pool[s], op.dcount)
        for e in ENGS:
            waited = {}
            for op in eng_ops[e]:
                ws = []
                for j in sorted(op.need):
                    d = ops[j]
                    if d.dma:
                        sem, val = d.dsem, d.dcount
                    else:
                        sem, val = sems[d.eng], d.count
                    if waited.get(id(sem), 0) >= val:
                        continue
                    waited[id(sem)] = val
                    ws.append((sem, val))
                if op.dma and op.dprev > 0:
                    if waited.get(id(op.dsem), 0) < op.dprev:
                        waited[id(op.dsem)] = op.dprev
                        ws.append((op.dsem, op.dprev))
                op.waits = ws

        def make(e):
            def body(eo):
                for op in eng_ops[e]:
                    for (sem, v) in op.waits:
                        eo.wait_ge(sem, v)
                    if op.fn is not None:
                        ins = op.fn(eo)
                        if op.dma:
                            ins.then_inc(op.dsem, 16)
                        elif op.signal:
                            ins.then_inc(sems[e], 1)
                if e == 'sp':
                    for (sem, v) in finals.values():
                        eo.wait_ge(sem, v)
            return body

        with nc.Block() as block:
            block.tensor(make('pe'))
            block.scalar(make('act'))
            block.vector(make('dve'))
            block.gpsimd(make('pool'))
            block.sync(make('sp'))


class Arena:
    def __init__(self, t32, nbytes):
        self.t32 = t32
        self.tbf = t32.bitcast(BF16)
        self.ti32 = t32.bitcast(I32)
        self.nbytes = nbytes
        self.top = 0
        self.marks = []
        self.peak = 0

    def alloc(self, shape, dt):
        es = 2 if dt == BF16 else 4
        n = 1
        for s in shape[1:]:
            n *= s
        nb = (n * es + 63) // 64 * 64
        off = self.top
        self.top += nb
        self.peak = max(self.peak, self.top)
        assert self.top <= self.nbytes, ("SBUF arena overflow", self.top, self.nbytes)
        base = {BF16: self.tbf, F32: self.t32, I32: self.ti32}[dt]
        v = base[0:shape[0], off // es: off // es + n]
        if len(shape) > 2:
            names = "abcdefg"[:len(shape) - 1]
            pat = "p (%s) -> p %s" % (" ".join(names), " ".join(names))
            v = v.rearrange(pat, **{names[i]: shape[1 + i] for i in range(len(shape) - 2)})
        return v

    def mark(self):
        self.marks.append(self.top)

    def release(self):
        self.top = self.marks.pop()


OUT_SPECS = [
    ("y_prompt", [T, DM]),
    ("y_sample", [NS, DM]),
    ("cmp_p", [T, 256]),
    ("cmp_s", [NS, 256]),
    ("sel_p", [T, 256]),
    ("sel_s", [NS, 256]),
    ("win_p", [512, 256]),
    ("win_s", [NS * 512, 256]),
    ("conv_p", [30, 512]),
    ("conv_s", [NS * 30, 512]),
    ("wkv_p", [16 * 64, 64]),
    ("wkv_s", [NS * 16 * 64, 64]),
    ("shift_p", [1, DM]),
    ("shift_s", [NS, DM]),
]

IN_SPECS = [
    ("xp", [T, DM], F32),
    ("xs", [NS, DM], F32),
    ("cache_cmp_a", [NPOOL * 64, 256], F32),
    ("cache_cmp_b", [NPOOL * 64, 256], F32),
    ("cache_sel_a", [NPOOL * 64, 256], F32),
    ("cache_sel_b", [NPOOL * 64, 256], F32),
    ("cache_win", [NS * 512, 256], F32),
    ("state_conv", [NS * 30, 512], F32),
    ("state_wkv", [NS * 16 * 64, 64], F32),
    ("state_shift", [NS, DM], F32),
    ("page_table", [NS, 64], I32),
    ("w_in", [DM, 3352], F32),
    ("conv_w", [31, 512], F32),
    ("conv_b", [1, 512], F32),
    ("conv_ln_g", [1, 512], F32),
    ("conv_ln_b", [1, 512], F32),
    ("wk_cmp", [32, 2], F32),
    ("wv_cmp", [32, 2], F32),
    ("w_out_even", [DM, DM], F32),
    ("mu_c", [6, DM], F32),
    ("w_rkvz", [4 * DM, DM], F32),
    ("w0", [1, DM], F32),
    ("w1", [DM, 64], F32),
    ("w2", [64, DM], F32),
    ("a0", [1, DM], F32),
    ("a1", [DM, 64], F32),
    ("a2", [64, DM], F32),
    ("k_k", [1, DM], F32),
    ("k_a", [1, DM], F32),
    ("r_k", [1, DM], F32),
    ("gn_g", [1, DM], F32),
    ("gn_b", [1, DM], F32),
    ("w_out_odd", [DM, DM], F32),
    ("ln_g", [2, DM], F32),
    ("ln_b", [2, DM], F32),
]

C_AVAL, C_AGLU, C_ZA, C_Q, C_KV, C_G3, C_ZB = 0, 512, 1024, 1536, 2048, 2816, 2840


def build(stage=99):
    nc = bass.Bass("TRN2", target_bir_lowering=False)
    I = {}
    for name, shape, dt in IN_SPECS:
        I[name] = nc.dram_tensor(name, shape, dt, kind="ExternalInput").ap()
    O = {}
    for name, shape in OUT_SPECS:
        O[name] = nc.dram_tensor(name, shape, F32, kind="ExternalOutput").ap()

    x1_scr = nc.dram_tensor("x1_scr", [TT, DM], F32, kind="Internal").ap()
    r_scr = nc.dram_tensor("r_scr", [128, 8, TT], F32, kind="Internal").ap()
    k_scr = nc.dram_tensor("k_scr", [128, 8, TT], F32, kind="Internal").ap()
    sw_scr = nc.dram_tensor("sw_scr", [128, 8, TT], F32, kind="Internal").ap()
    a_scr = nc.dram_tensor("a_scr", [128, 8, TT], F32, kind="Internal").ap()
    v_scr = nc.dram_tensor("v_scr", [TT, DM], F32, kind="Internal").ap()
    z_scr = nc.dram_tensor("z_scr", [TT, DM], F32, kind="Internal").ap()
    S = Sched()
    ARENA_BYTES = 176 * 1024
    from contextlib import ExitStack
    with ExitStack() as st:
        arena_t = st.enter_context(nc.sbuf_tensor("arena", [128, ARENA_BYTES // 4], F32))
        ps_t = st.enter_context(nc.psum_tensor("psum", [128, 4096], F32))
        sems = {e: st.enter_context(nc.semaphore("sem_" + e)) for e in ENGS}
        dsems = {q: [st.enter_context(nc.semaphore("dsem_%s%d" % (q, i))) for i in range(n)]
                 for q, n in NDS.items()}
        A = Arena(arena_t, ARENA_BYTES)
        ps_bf = ps_t.bitcast(BF16)
        try:

            def PS(b, p0=0, p1=128, c0=0, c1=512):
                return ps_t[p0:p1, b * 512 + c0: b * 512 + c1]

            def PSB(b, p0=0, p1=128, c0=0, c1=1024):
                return ps_bf[p0:p1, b * 1024 + c0: b * 1024 + c1]

            def pk(b):
                return ('ps', b)

            FR = {}

            def freg(e, v):
                if v not in FR:
                    FR[v] = e.to_reg(float(v))
                return FR[v]

            ident_f = A.alloc([128, 128], F32)
            ident_b = A.alloc([128, 128], BF16)
            ones_f = A.alloc([128, 128], F32)

            def mk_ident(e):
                e.memset(ones_f, 1.0)
                return e.affine_select(out=ident_f, in_=ones_f, pattern=[[-1, 128]], compare_op=ALU.is_equal,
                                       fill=freg(e, 0.0), base=0, channel_multiplier=1)
            S.pool(mk_ident, w=['ident_f', 'ones_f'])
            S.dve(lambda e: e.tensor_copy(out=ident_b, in_=ident_f), r=['ident_f'], w=['ident_b'])

            xT_off = A.top
            xT = A.alloc([128, 8, TT], BF16)
            xT_raw32 = arena_t[:, xT_off // 4: xT_off // 4 + 8208]
            L1_BASE = A.top
            yaT = A.alloc([128, 4, TT], BF16)
            Vsel_off = A.top
            Vsel = A.alloc([128, 16, 128], BF16)
            Vwin_off = A.top
            Vwin = A.alloc([128, 16, 128], BF16)
            Vsel_raw32 = arena_t[:, Vsel_off // 4: Vsel_off // 4 + 1024]
            Vwin_raw32 = arena_t[:, Vwin_off // 4: Vwin_off // 4 + 1024]
            kvsv = A.alloc([NS, 2, 128], F32)
            qs_f = A.alloc([68, 2, 4, NS], F32)
            ksn = A.alloc([68, 2, 2, NS], F32)
            gs_s = A.alloc([24, NS], BF16)
            w_in_t = I['w_in'].rearrange("(c p) f -> p c f", p=128)
            A.mark()
            xst = [A.alloc([128, DM], BF16) for _ in range(2)]
            xp_t = I['xp'].rearrange("(n p) d -> n p d", p=128)
            for n in range(16):
                b = n % 2
                S.dma('pool', lambda e, n=n, b=b: e.dma_start(out=xst[b], in_=xp_t[n]), w=[('xst', b)])
                bank = n % 2

                def tr(e, b=b, bank=bank):
                    ins = None
                    for c in range(8):
                        ins = e.transpose(out=PSB(bank, 0, 128, c * 128, (c + 1) * 128),
                                          in_=xst[b][:, c * 128:(c + 1) * 128], identity=ident_b)
                    return ins
                S.pe(tr, r=[('xst', b), 'ident_b'], w=[pk(bank)])
                S.dve(lambda e, n=n, bank=bank: e.tensor_copy(
                    out=xT[:, :, n * 128:(n + 1) * 128],
                    in_=PSB(bank).rearrange("p (c t) -> p c t", c=8)),
                    r=[pk(bank)], w=[('xT', n)])
            S.dma('pool', lambda e: e.dma_start(out=xst[0][0:NS, :], in_=I['xs']), w=[('xst', 0)])

            def trs(e):
                ins = None
                for c in range(8):
                    ins = e.transpose(out=PSB(0, 0, 128, c * NS, (c + 1) * NS),
                                      in_=xst[0][0:NS, c * 128:(c + 1) * 128], identity=ident_b[0:NS, 0:NS])
                return ins
            S.pe(trs, r=[('xst', 0), 'ident_b'], w=[pk(0)])
            S.dve(lambda e: e.tensor_copy(out=xT[:, :, T:TT],
                                          in_=PSB(0, 0, 128, 0, 8 * NS).rearrange("p (c t) -> p c t", c=8)),
                  r=[pk(0)], w=[('xT', 16)])
            XT_ALL = [('xT', n) for n in range(17)]
            S.barrier()
            A.release()
            stage_end(1)

            A.mark()
            wkv = A.alloc([128, 8, 768], BF16)
            kvst = [A.alloc([128, 768], F32) for _ in range(2)]
            winb = [A.alloc([128, 4, 256], F32) for _ in range(2)]
            for c in range(8):
                S.dma('pool', lambda e, c=c: e.dma_start(out=wkv[:, c, :], in_=w_in_t[:, c, C_KV:C_KV + 768]),
                      w=[('wkv', c)])
            WKV = [('wkv', c) for c in range(8)]
            for n in range(16):
                bA, bB = 4 + (n % 2) * 2, 5 + (n % 2) * 2

                def mmkv(e, n=n, bA=bA, bB=bB):
                    ins = None
                    for c in range(8):
                        ins = e.matmul(PS(bA), lhsT=xT[:, c, n * 128:(n + 1) * 128], rhs=wkv[:, c, 0:512],
                                       start=(c == 0), stop=(c == 7))
                    for c in range(8):
                        ins = e.matmul(PS(bB, 0, 128, 0, 256), lhsT=xT[:, c, n * 128:(n + 1) * 128],
                                       rhs=wkv[:, c, 512:768], start=(c == 0), stop=(c == 7))
                    return ins
                S.pe(mmkv, r=[('xT', n)] + WKV, w=[pk(bA), pk(bB)])
                sb = n % 2
                S.act(lambda e, sb=sb, bA=bA: e.copy(out=kvst[sb][:, 0:512], in_=PS(bA)),
                      r=[pk(bA)], w=[('kvst', sb, 0)])
                S.act(lambda e, sb=sb, bB=bB: e.copy(out=kvst[sb][:, 512:768], in_=PS(bB, 0, 128, 0, 256)),
                      r=[pk(bB)], w=[('kvst', sb, 1)])
                S.dma('sp', lambda e, n=n, sb=sb: e.dma_start(out=O['cmp_p'][n * 128:(n + 1) * 128, :],
                                                               in_=kvst[sb][:, 0:256]), r=[('kvst', sb, 0)])
                S.dma('sp', lambda e, n=n, sb=sb: e.dma_start(out=O['sel_p'][n * 128:(n + 1) * 128, :],
                                                               in_=kvst[sb][:, 256:512]), r=[('kvst', sb, 0)])
                if n >= 12:
                    S.dma('sp', lambda e, n=n, sb=sb: e.dma_start(
                        out=O['win_p'][(n - 12) * 128:(n - 11) * 128, :], in_=kvst[sb][:, 512:768]),
                        r=[('kvst', sb, 1)])
                S.pool(lambda e, n=n, sb=sb: e.tensor_copy(out=Vsel[:, n, :], in_=kvst[sb][:, 384:512]),
                       r=[('kvst', sb, 0)], w=[('Vsel', n)])
                S.pool(lambda e, n=n, sb=sb: e.tensor_copy(out=Vwin[:, n, :], in_=kvst[sb][:, 640:768]),
                       r=[('kvst', sb, 1)], w=[('Vwin', n)])
            stage_end(21)
            def mmkvs(e):
                ins = None
                for c in range(8):
                    ins = e.matmul(PS(4, 0, NS, 0, 512), lhsT=xT[:, c, T:TT], rhs=wkv[:, c, 0:512],
                                   start=(c == 0), stop=(c == 7))
                for c in range(8):
                    ins = e.matmul(PS(5, 0, NS, 0, 256), lhsT=xT[:, c, T:TT], rhs=wkv[:, c, 512:768],
                                   start=(c == 0), stop=(c == 7))
                return ins
            S.pe(mmkvs, r=[('xT', 16)] + WKV, w=[pk(4), pk(5)])
            S.act(lambda e: e.copy(out=kvst[0][0:NS, 0:512], in_=PS(4, 0, NS, 0, 512)), r=[pk(4)], w=[('kvst', 0, 0)])
            S.act(lambda e: e.copy(out=kvst[0][0:NS, 512:768], in_=PS(5, 0, NS, 0, 256)), r=[pk(5)], w=[('kvst', 0, 1)])
            S.dma('sp', lambda e: e.dma_start(out=O['cmp_s'], in_=kvst[0][0:NS, 0:256]), r=[('kvst', 0, 0)])
            S.dma('sp', lambda e: e.dma_start(out=O['sel_s'], in_=kvst[0][0:NS, 256:512]), r=[('kvst', 0, 0)])
            win_s_v = O['win_s'].rearrange("(s r) c -> s r c", r=512)
            S.dma('sp', lambda e: e.dma_start(out=win_s_v[:, 511, :], in_=kvst[0][0:NS, 512:768]), r=[('kvst', 0, 1)])
            S.pool(lambda e: e.tensor_copy(out=kvsv[0:NS, 0, :], in_=kvst[0][0:NS, 384:512]),
                   r=[('kvst', 0, 0)], w=['kvsv'])
            S.pool(lambda e: e.tensor_copy(out=kvsv[0:NS, 1, :], in_=kvst[0][0:NS, 640:768]),
                   r=[('kvst', 0, 1)], w=['kvsv'])
            stage_end(22)
            for s in range(NS):
                wb_ = winb[s % 2]
                src = I['cache_win'][s * 512:(s + 1) * 512, :].rearrange("(p j) c -> p j c", j=4)
                dst = O['win_s'][s * 512:(s + 1) * 512, :].rearrange("(p j) c -> p j c", j=4)
                S.dma('sp', lambda e, wb_=wb_, src=src: e.dma_start(out=wb_, in_=src), w=[('winb', s % 2)])
                S.dma('sp', lambda e, wb_=wb_, dst=dst: e.dma_start(out=dst[:, 0:3, :], in_=wb_[:, 1:4, :]),
                      r=[('winb', s % 2)])
                S.dma('sp', lambda e, wb_=wb_, dst=dst: e.dma_start(out=dst[0:127, 3, :], in_=wb_[1:128, 0, :]),
                      r=[('winb', s % 2)])
            stage_end(23)
            S.barrier()
            A.release()
            stage_end(2)
            BLKS = [(tb * 512, 512) for tb in range(4)] + [(T, NS)]

            def xT_keys(c0, n):
                if c0 >= T:
                    return [('xT', 16)]
                return [('xT', c0 // 128 + i) for i in range(max(1, n // 128))]

            pbank = [0]

            def proj_multi(units, blks=BLKS):
                for (c0, n) in blks:
                    for (wt, wkey, M, evac, aug) in units:
                        bank = pbank[0] % 4
                        pbank[0] += 1

                        def mm(e, c0=c0, n=n, bank=bank, wt=wt, M=M, aug=aug):
                            ins = None
                            for c in range(8):
                                ins = e.matmul(PS(bank, 0, M, 0, n), lhsT=wt[:, c, :], rhs=xT[:, c, c0:c0 + n],
                                               start=(c == 0), stop=(c == 7 and aug is None))
                            if aug is not None:
                                ins = e.matmul(PS(bank, 0, M, 0, n), lhsT=aug, rhs=bas[0:3, c0:c0 + n],
                                               start=False, stop=True)
                            return ins
                        S.pe(mm, r=[wkey, 'bas', 'bas0', 'coef'] + xT_keys(c0, n), w=[pk(bank)])
                        evac(bank, c0, n)

            def proj_fm(wt, wkey, M, evac, aug=None, blks=BLKS):
                proj_multi([(wt, wkey, M, evac, aug)], blks)


            A.mark()
            NWB = 4
            wbuf = [A.alloc([128, 8, 128], BF16) for _ in range(NWB)]
            wctr = [0]

            def load_w(col0, M):
                i = wctr[0] % NWB
                wctr[0] += 1
                S.dma('pool', lambda e: e.dma_start(out=wbuf[i][:, :, 0:M], in_=w_in_t[:, :, col0:col0 + M]),
                      w=[('wbuf', i)])
                return wbuf[i][:, :, 0:M], ('wbuf', i)
            u_ext = A.alloc([128, 4, 30 + T], BF16)
            sza = A.alloc([128, 4, TT], BF16)
            c32 = A.alloc([128, 4, TT], F32)
            u32t = A.alloc([128, 4, 32], F32)
            us32 = A.alloc([128, 4, NS], F32)
            cw_sb = A.alloc([31, 512], F32)
            cwT = A.alloc([128, 4, 31], F32)
            cvec = A.alloc([128, 3, 4], F32)
            A.mark()
            sig = [A.alloc([128, 512], F32) for _ in range(2)]
            sigc = [0]
            S.pool(lambda e: e.memset(u_ext[:, :, 0:30], 0.0), w=[('uext', 'h')])
            for j in range(4):
                wtg, wkg = load_w(C_AGLU + j * 128, 128)
                wta, wka = load_w(C_AVAL + j * 128, 128)
                wtz, wkz = load_w(C_ZA + j * 128, 128)

                def ev_sig(bank, c0, n, j=j):
                    sb = sigc[0] % 2
                    S.act(lambda e: e.activation(out=sig[sb][:, 0:n], in_=PS(bank, 0, 128, 0, n), func=AF.Sigmoid),
                          r=[pk(bank)], w=[('sig', sb)])

                def ev_u(bank, c0, n, j=j):
                    sb = sigc[0] % 2
                    sigc[0] += 1
                    if c0 < T:
                        S.dve(lambda e: e.tensor_tensor(out=u_ext[:, j, 30 + c0:30 + c0 + n], in0=PS(bank, 0, 128, 0, n),
                                                        in1=sig[sb][:, 0:n], op=ALU.mult),
                              r=[pk(bank), ('sig', sb)], w=[('uext', j, c0)])
                        if c0 == 1536:
                            S.dve(lambda e: e.tensor_tensor(out=u32t[:, j, :], in0=PS(bank, 0, 128, 480, 512),
                                                            in1=sig[sb][:, 480:512], op=ALU.mult),
                                  r=[pk(bank), ('sig', sb)], w=[('u32t', j)])
                    else:
                        S.dve(lambda e: e.tensor_tensor(out=us32[:, j, :], in0=PS(bank, 0, 128, 0, n),
                                                        in1=sig[sb][:, 0:n], op=ALU.mult),
                              r=[pk(bank), ('sig', sb)], w=[('us32', j)])

                def ev_za(bank, c0, n, j=j):
                    S.act(lambda e: e.activation(out=sza[:, j, c0:c0 + n], in_=PS(bank, 0, 128, 0, n), func=AF.Silu),
                          r=[pk(bank)], w=[('sza', j, c0)])
                proj_multi([(wtg, wkg, 128, ev_sig, None), (wta, wka, 128, ev_u, None), (wtz, wkz, 128, ev_za, None)])

            stage_end(3)
            S.dma('sp', lambda e: e.dma_start(out=cw_sb, in_=I['conv_w']), w=['cw_sb'])

            def trcw(e):
                ins = None
                for j in range(4):
                    ins = e.transpose(out=PS(7, 0, 128, j * 32, j * 32 + 31), in_=cw_sb[:, j * 128:(j + 1) * 128],
                                      identity=ident_f[0:31, 0:31])
                return ins
            S.pe(trcw, r=['cw_sb', 'ident_f'], w=[pk(7)])
            S.dve(lambda e: e.tensor_copy(out=cwT, in_=PS(7, 0, 128, 0, 128).rearrange("p (j t) -> p j t", j=4)[:, :, 0:31]),
                  r=[pk(7)], w=['cwT'])
            for i, nm in enumerate(['conv_b', 'conv_ln_g', 'conv_ln_b']):
                S.dma('sp', lambda e, i=i, nm=nm: e.dma_start(out=cvec[:, i, :],
                                                              in_=I[nm].rearrange("o (j p) -> p (o j)", p=128),
                                                              allow_slow_non_contiguous=True),
                      w=[('cvec', i)])

            diag = [A.alloc([128, 31, 128], BF16) for _ in range(2)]
            for j in range(4):
                db = j % 2
                for tap in range(31):
                    S.pool(lambda e, j=j, tap=tap, db=db: e.tensor_scalar(
                        out=diag[db][:, tap, :], in0=ident_b, scalar1=cwT[:, j, tap:tap + 1], scalar2=None, op0=ALU.mult),
                        r=['cwT', 'ident_b'], w=[('diag', db, tap)])
                for tb in range(4):
                    bank = 4 + (tb % 2)

                    def cm(e, j=j, tb=tb, db=db, bank=bank):
                        ins = None
                        for tap in range(31):
                            ins = e.matmul(PS(bank), lhsT=diag[db][:, tap, :],
                                           rhs=u_ext[:, j, tb * 512 + tap: tb * 512 + tap + 512],
                                           start=(tap == 0), stop=(tap == 30))
                        return ins
                    rk = [('diag', db, tap) for tap in range(31)] + [('uext', j, tb * 512)]
                    rk += [('uext', j, (tb - 1) * 512)] if tb > 0 else [('uext', 'h')]
                    S.pe(cm, r=rk, w=[pk(bank)])
                    S.act(lambda e, j=j, tb=tb, bank=bank: e.activation(
                        out=c32[:, j, tb * 512:(tb + 1) * 512], in_=PS(bank), func=AF.Identity, bias=cvec[:, 0, j:j + 1]),
                        r=[pk(bank), ('cvec', 0)], w=[('c32', j, tb * 512)])

            stage_end(4)
            exts = A.alloc([128, 4, NS, 31], F32)
            scv = A.alloc([30, NS, 512], F32)
            S.dma('sp', lambda e: e.dma_start(out=scv, in_=I['state_conv'].rearrange("(s r) c -> r s c", r=30)), w=['scv'])
            for s in range(NS):
                def trs_(e, s=s):
                    ins = None
                    for j in range(4):
                        ins = e.transpose(out=PS(6, 0, 128, j * 32, j * 32 + 30), in_=scv[:, s, j * 128:(j + 1) * 128],
                                          identity=ident_f[0:30, 0:30])
                    return ins
                S.pe(trs_, r=['scv', 'ident_f'], w=[pk(6)])
                S.dve(lambda e, s=s: e.tensor_copy(
                    out=exts[:, :, s, 0:30], in_=PS(6, 0, 128, 0, 128).rearrange("p (j t) -> p j t", j=4)[:, :, 0:30]),
                    r=[pk(6)], w=[('exts', s)])
            S.dve(lambda e: e.tensor_copy(out=exts[:, :, :, 30], in_=us32),
                  r=[('us32', j) for j in range(4)], w=[('exts', 'u')])
            prod = A.alloc([128, 4, NS, 31], F32)
            S.dve(lambda e: e.tensor_tensor(out=prod, in0=exts, in1=cwT.unsqueeze(2).broadcast_to([128, 4, NS, 31]),
                                            op=ALU.mult),
                  r=[('exts', s) for s in range(NS)] + [('exts', 'u'), 'cwT'], w=['prod'])
            cs_ = A.alloc([128, 4, NS], F32)
            S.dve(lambda e: e.tensor_reduce(out=cs_, in_=prod, axis=AX.X, op=ALU.add), r=['prod'], w=['cs'])
            S.dve(lambda e: e.tensor_tensor(out=c32[:, :, T:TT], in0=cs_,
                                            in1=cvec[:, 0, :].unsqueeze(2).broadcast_to([128, 4, NS]), op=ALU.add),
                  r=['cs', ('cvec', 0)], w=[('c32', j, T) for j in range(4)])

            cvo = A.alloc([32, 512], F32)

            def tru(e):
                ins = None
                for j in range(4):
                    ins = e.transpose(out=PS(6, 0, 32, j * 128, (j + 1) * 128), in_=u32t[:, j, :], identity=ident_f)
                return ins
            S.pe(tru, r=[('u32t', j) for j in range(4)] + ['ident_f'], w=[pk(6)])
            S.act(lambda e: e.copy(out=cvo, in_=PS(6, 0, 32, 0, 512)), r=[pk(6)], w=['cvo'])
            S.dma('sp', lambda e: e.dma_start(out=O['conv_p'], in_=cvo[2:32, :]), r=['cvo'])
            cso = A.alloc([NS, 512], F32)

            def trus(e):
                ins = None
                for j in range(4):
                    ins = e.transpose(out=PS(6, 0, NS, j * 128, (j + 1) * 128), in_=us32[:, j, :], identity=ident_f)
                return ins
            S.pe(trus, r=[('us32', j) for j in range(4)] + ['ident_f'], w=[pk(6)])
            S.act(lambda e: e.copy(out=cso, in_=PS(6, 0, NS, 0, 512)), r=[pk(6)], w=['cso'])
            conv_s_v = O['conv_s'].rearrange("(s r) c -> s r c", r=30)
            S.dma('sp', lambda e: e.dma_start(out=conv_s_v[:, 29, :], in_=cso), r=['cso'])
            for s in range(NS):
                S.dma('sp', lambda e, s=s: e.dma_start(out=conv_s_v[s, 0:29, :], in_=scv[1:30, s, :]), r=['scv'])

            stage_end(5)
            S.barrier()
            A.release()
            onesm = A.alloc([128, 128], F32)
            S.pool(lambda e: e.memset(onesm, 1.0 / 512.0), w=['onesm'])
            sq4 = A.alloc([128, 4, 512], F32)
            mean_sb = A.alloc([128, 512], F32)
            rstd_sb = A.alloc([128, 512], F32)
            tmpv = A.alloc([128, 512], F32)
            tj = [A.alloc([128, 512], F32) for _ in range(2)]
            tjc = 0
            for (c0, n) in BLKS:
                ck = [('c32', j, c0) for j in range(4)]
                S.act(lambda e, c0=c0, n=n: e.activation(out=sq4[:, :, 0:n], in_=c32[:, :, c0:c0 + n], func=AF.Square),
                      r=ck, w=['sq4'])

                def stm(e, c0=c0, n=n):
                    ins = None
                    for j in range(4):
                        ins = e.matmul(PS(0, 0, 128, 0, n), lhsT=onesm, rhs=c32[:, j, c0:c0 + n], start=(j == 0), stop=(j == 3))
                    for j in range(4):
                        ins = e.matmul(PS(1, 0, 128, 0, n), lhsT=onesm, rhs=sq4[:, j, 0:n], start=(j == 0), stop=(j == 3))
                    return ins
                S.pe(stm, r=ck + ['sq4', 'onesm'], w=[pk(0), pk(1)])
                S.act(lambda e, n=n: e.copy(out=mean_sb[:, 0:n], in_=PS(0, 0, 128, 0, n)), r=[pk(0)], w=['mean_sb'])
                S.dve(lambda e, n=n: e.tensor_tensor(out=tmpv[:, 0:n], in0=mean_sb[:, 0:n], in1=mean_sb[:, 0:n], op=ALU.mult),
                      r=['mean_sb'], w=['tmpv'])
                S.dve(lambda e, n=n: e.tensor_tensor(out=tmpv[:, 0:n], in0=PS(1, 0, 128, 0, n), in1=tmpv[:, 0:n],
                                                     op=ALU.subtract), r=[pk(1), 'tmpv'], w=['tmpv'])
                S.dve(lambda e, n=n: e.tensor_scalar(out=tmpv[:, 0:n], in0=tmpv[:, 0:n], scalar1=1e-5, scalar2=None,
                                                     op0=ALU.add), r=['tmpv'], w=['tmpv'])
                S.act(lambda e, n=n: e.activation(out=tmpv[:, 0:n], in_=tmpv[:, 0:n], func=AF.Ln), r=['tmpv'], w=['tmpv'])
                S.act(lambda e, n=n: e.activation(out=rstd_sb[:, 0:n], in_=tmpv[:, 0:n], func=AF.Exp, scale=-0.5),
                      r=['tmpv'], w=['rstd_sb'])
                for j in range(4):
                    tb_ = tj[tjc % 2]
                    tk = ('tj', tjc % 2)
                    tjc += 1
                    S.dve(lambda e, j=j, c0=c0, n=n, tb_=tb_: e.tensor_tensor(
                        out=tb_[:, 0:n], in0=c32[:, j, c0:c0 + n], in1=mean_sb[:, 0:n], op=ALU.subtract),
                        r=[('c32', j, c0), 'mean_sb'], w=[tk])
                    S.dve(lambda e, n=n, tb_=tb_: e.tensor_tensor(out=tb_[:, 0:n], in0=tb_[:, 0:n], in1=rstd_sb[:, 0:n],
                                                                  op=ALU.mult), r=[tk, 'rstd_sb'], w=[tk])
                    S.dve(lambda e, j=j, n=n, tb_=tb_: e.tensor_scalar(
                        out=tb_[:, 0:n], in0=tb_[:, 0:n], scalar1=cvec[:, 1, j:j + 1], scalar2=cvec[:, 2, j:j + 1],
                        op0=ALU.mult, op1=ALU.add), r=[tk, ('cvec', 1), ('cvec', 2)], w=[tk])
                    S.act(lambda e, n=n, tb_=tb_: e.activation(out=tb_[:, 0:n], in_=tb_[:, 0:n], func=AF.Silu),
                          r=[tk], w=[tk])
                    S.pool(lambda e, j=j, c0=c0, n=n, tb_=tb_: e.tensor_tensor(
                        out=yaT[:, j, c0:c0 + n], in0=tb_[:, 0:n], in1=sza[:, j, c0:c0 + n], op=ALU.mult),
                        r=[tk, ('sza', j, c0)], w=[('yaT', j, c0)])
            S.barrier()
            A.release()
            stage_end(6)
            FORCE = 1.0e4
            BIG = 30000.0
            ybT = A.alloc([64, 8, TT], BF16)
            A.mark()
            Qaug = [A.alloc([128, 4, TT], BF16) for g in range(2)]
            Ksel = [A.alloc([128, TT], BF16) for g in range(2)]
            Kwin = [A.alloc([68, TT], BF16) for g in range(2)]
            gsig = A.alloc([24, TT], BF16)
            KCa = [A.alloc([68, 64], BF16) for g in range(2)]
            VC = [A.alloc([64, 64], BF16) for g in range(2)]
            kcT = A.alloc([64, 2, 2, 64], F32)
            Wkv = A.alloc([64, 2, 2, 32], F32)
            oh = A.alloc([24, 24, 64], BF16)
            ones64 = A.alloc([128, 64], BF16)
            Mpad = A.alloc([128, 128], BF16)
            A.mark()
            NWB = 4
            wpad = [A.alloc([128, 8, 68], BF16) for _ in range(NWB)]
            for i in range(NWB):
                S.pool(lambda e, i=i: e.memset(wpad[i], 0.0), w=[('wpad', i)])
            wpc = [0]

            def load_wpad(col0, M=64):
                i = wpc[0] % NWB
                wpc[0] += 1
                S.dma('pool', lambda e: e.dma_start(out=wpad[i][:, :, 0:M], in_=w_in_t[:, :, col0:col0 + M]),
                      w=[('wpad', i)])
                return wpad[i], ('wpad', i)
            bas = A.alloc([3, TT], BF16)
            basc = A.alloc([3, 64], BF16)
            brow = A.alloc([1, 2, TT], BF16)
            browc = A.alloc([1, 2, 64], BF16)
            coefQ = A.alloc([3, 8, 68], BF16)
            coefK = A.alloc([3, 68], BF16)
            cq0 = A.alloc([1, 3, 8, 68], BF16)
            ck0 = A.alloc([1, 3, 68], BF16)

            def mkbas(e):
                e.iota(brow[:, 0, 0:T], pattern=[[1, 32], [0, 64]], base=0, channel_multiplier=0,
                       allow_small_or_imprecise_dtypes=True)
                e.iota(brow[:, 1, 0:T], pattern=[[0, 32], [1, 64]], base=0, channel_multiplier=0,
                       allow_small_or_imprecise_dtypes=True)
                e.memset(brow[:, 0, T:TT], 128.0)
                e.memset(brow[:, 1, T:TT], 0.0)
                e.iota(browc[:, 0, :], pattern=[[1, 32], [0, 2]], base=0, channel_multiplier=0,
                       allow_small_or_imprecise_dtypes=True)
                e.iota(browc[:, 1, :], pattern=[[0, 32], [32, 2]], base=31, channel_multiplier=0,
                       allow_small_or_imprecise_dtypes=True)
                e.memset(bas[0:1, :], 1.0)
                e.memset(basc[0:1, :], 1.0)
                e.memset(cq0, 0.0)
                e.memset(ck0, 0.0)
                for h in range(8):
                    sl = 2.0 ** (-(h + 1))
                    e.memset(cq0[:, 0, h, 64:65], 8.0 * sl * 64.0)
                    e.memset(cq0[:, 0, h, 65:66], 8.0 * sl)
                    e.memset(cq0[:, 1, h, 66:67], -8.0 * sl * 64.0)
                    e.memset(cq0[:, 2, h, 67:68], -8.0 * sl)
                e.memset(ck0[:, 0, 66:68], 1.0)
                e.memset(ck0[:, 1, 64:65], 1.0)
                e.memset(ck0[:, 2, 65:66], 1.0)
                e.memset(ones64, 1.0)
                e.memset(Mpad, 0.0)
                return e.tensor_copy(out=oh, in_=ident_b[0:24, 0:24].unsqueeze(2).broadcast_to([24, 24, 64]))
            S.pool(mkbas, r=['ident_b'], w=['brow', 'bas0', 'ones64', 'Mpad', 'oh'])
            for i in range(2):
                S.dma('sp', lambda e, i=i: e.dma_start(out=bas[1 + i:2 + i, :], in_=brow[0:1, i, :]), r=['brow'], w=['bas'])
                S.dma('sp', lambda e, i=i: e.dma_start(out=basc[1 + i:2 + i, :], in_=browc[0:1, i, :]), r=['brow'], w=['bas'])
            for i in range(3):
                S.dma('sp', lambda e, i=i: e.dma_start(out=coefQ[i:i + 1, :, :], in_=cq0[0:1, i, :, :]), r=['brow'], w=['coef'])
                S.dma('sp', lambda e, i=i: e.dma_start(out=coefK[i:i + 1, :], in_=ck0[0:1, i, :]), r=['brow'], w=['coef'])

            QK = [[('Q', g, c0) for (c0, n) in BLKS] + [('Qm', g, qt) for qt in range(17)] for g in range(2)]
            KK = [[('Ks', g, c0) for (c0, n) in BLKS] for g in range(2)]
            for g in range(2):
                S.pool(lambda e, g=g: e.memset(Qaug[g][64:128, :, :], 0.0), w=QK[g])

                def kinit(e, g=g):
                    e.memset(Ksel[g][64:128, :], 0.0)
                    e.memset(Ksel[g][96:128, 0:T], 1.0)
                    e.affine_select(out=Ksel[g][96:128, 0:T], in_=Ksel[g][96:128, 0:T], pattern=[[1, T]],
                                    compare_op=ALU.is_ge, fill=freg(e, 0.0), base=0, channel_multiplier=-64)
                    return e.affine_select(out=Ksel[g][96:128, 0:T], in_=Ksel[g][96:128, 0:T], pattern=[[-1, T]],
                                           compare_op=ALU.is_ge, fill=freg(e, 0.0), base=63, channel_multiplier=64)
                S.pool(kinit, w=KK[g])

            stage_end(7)
            wt, wk_ = load_wpad(C_G3, 24)

            def ev_g(bank, c0, n):
                S.act(lambda e: e.activation(out=gsig[0:24, c0:c0 + n], in_=PS(bank, 0, 24, 0, n), func=AF.Sigmoid),
                      r=[pk(bank)], w=[('gsig', c0)])
            proj_fm(wt[:, :, 0:24], wk_, 24, ev_g)
            for h in range(8):
                wt, wk_ = load_wpad(C_ZB + h * 64, 64)

                def ev_zb(bank, c0, n, h=h):
                    S.act(lambda e: e.activation(out=ybT[0:64, h, c0:c0 + n], in_=PS(bank, 0, 64, 0, n), func=AF.Silu),
                          r=[pk(bank)], w=[('ybT', h, c0)])
                proj_fm(wt[:, :, 0:64], wk_, 64, ev_zb)
            for h in range(8):
                g, r_ = h // 4, h % 4
                wt, wk_ = load_wpad(C_Q + h * 64)

                def ev_q(bank, c0, n, g=g, r_=r_):
                    S.dve(lambda e: e.tensor_scalar(out=Qaug[g][0:68, r_, c0:c0 + n], in0=PS(bank, 0, 68, 0, n),
                                                    scalar1=0.125, scalar2=None, op0=ALU.mult),
                          r=[pk(bank)], w=[('Q', g, c0)])
                proj_fm(wt, wk_, 68, ev_q, aug=coefQ[0:3, h, :])
            augK = coefK[0:3, :]
            for g in range(2):
                wt, wk_ = load_wpad(C_KV + 256 + g * 64)

                def ev_ks(bank, c0, n, g=g):
                    S.act(lambda e: e.copy(out=Ksel[g][0:68, c0:c0 + n], in_=PS(bank, 0, 68, 0, n)),
                          r=[pk(bank)], w=[('Ks', g, c0)])
                proj_fm(wt, wk_, 68, ev_ks, aug=augK)
                wt, wk_ = load_wpad(C_KV + 512 + g * 64)

                def ev_kw(bank, c0, n, g=g):
                    S.act(lambda e: e.copy(out=Kwin[g][0:68, c0:c0 + n], in_=PS(bank, 0, 68, 0, n)),
                          r=[pk(bank)], w=[('Kw', g, c0)])
                proj_fm(wt, wk_, 68, ev_kw, aug=augK)
            for kv_, nm in enumerate(['wk_cmp', 'wv_cmp']):
                for g in range(2):
                    S.dma('sp', lambda e, kv_=kv_, nm=nm, g=g: e.dma_start(
                        out=Wkv[:, kv_, g:g + 1, :], in_=I[nm][:, g:g + 1].rearrange("l o -> o l").partition_broadcast(64),
                        allow_slow_non_contiguous=True), w=[('Wkv', kv_, g)])
            ptmp = [A.alloc([64, 16, 32], F32) for _ in range(1)]
            pcnt = [0]
            for kv_ in range(2):
                for g in range(2):
                    wt, wk_ = load_wpad(C_KV + kv_ * 128 + g * 64, 64)

                    def ev_pool(bank, c0, n, kv_=kv_, g=g):
                        pb = 0
                        pcnt[0] += 1
                        S.dve(lambda e: e.tensor_tensor(
                            out=ptmp[pb], in0=PS(bank, 0, 64, 0, 512).rearrange("p (a l) -> p a l", l=32),
                            in1=Wkv[:, kv_, g:g + 1, :].broadcast_to([64, 16, 32]), op=ALU.mult),
                            r=[pk(bank), ('Wkv', kv_, g)], w=[('ptmp', pb)])
                        S.dve(lambda e: e.tensor_reduce(out=kcT[:, kv_, g, c0 // 32:c0 // 32 + 16], in_=ptmp[pb],
                                                        axis=AX.X, op=ALU.add),
                              r=[('ptmp', pb)], w=[('kcT', kv_, g, c0)])
                    proj_fm(wt[:, :, 0:64], wk_, 64, ev_pool, blks=BLKS[0:4])
            for g in range(2):
                kck = [('kcT', 0, g, c0) for (c0, n) in BLKS[0:4]]
                vck = [('kcT', 1, g, c0) for (c0, n) in BLKS[0:4]]
                S.dve(lambda e, g=g: e.tensor_copy(out=KCa[g][0:64, :], in_=kcT[:, 0, g, :]), r=kck, w=[('KCa', g)])

                def mmaug(e, g=g):
                    return e.matmul(PS(7, 0, 68, 0, 64), lhsT=coefK[0:3, :], rhs=basc[0:3, :], start=True, stop=True)
                S.pe(mmaug, r=['coef', 'bas', 'bas0'], w=[pk(7)])
                S.act(lambda e, g=g: e.copy(out=KCa[g][64:68, :], in_=PS(7, 64, 68, 0, 64)), r=[pk(7)], w=[('KCa', g)])
                S.pe(lambda e, g=g: e.transpose(out=PS(6, 0, 64, 0, 64), in_=kcT[:, 1, g, :], identity=ident_f[0:64, 0:64]),
                     r=vck + ['ident_f'], w=[pk(6)])
                S.act(lambda e, g=g: e.copy(out=VC[g], in_=PS(6, 0, 64, 0, 64)), r=[pk(6)], w=[('VC', g)])

            stage_end(8)
            S.barrier()
            A.release()
            Ec = A.alloc([128, 4, 64], F32)
            sums_c = A.alloc([128, 4], F32)
            imp64 = A.alloc([128, 64], F32)
            imp = A.alloc([128, 32], F32)
            imp2 = A.alloc([128, 32], F32)
            m8 = A.alloc([128, 16], F32)
            PT = [A.alloc([128, 512], BF16) for _ in range(3)]
            acc = [A.alloc([64, 512], F32) for _ in range(2)]
            rs_ = [A.alloc([64, 512], F32) for _ in range(2)]
            tt_ = [A.alloc([64, 512], F32) for _ in range(2)]
            ptc = [0]
            sbc = [0]
            rsc = [0]

            def qkeys(g, qt):
                return [('Q', g, (qt // 4) * 512), ('Qm', g, qt)]

            pend = [None]

            def attn_pair(g, qt, lhsT, lkeys, KKrows, vt, vkeys, masks, ob, sb_, first, last):
                q0 = qt * 128
                sbank = sbc[0] % 2
                sbc[0] += 1
                pt = PT[ptc[0] % 3]
                ptk = ('PT', ptc[0] % 3)
                ptc[0] += 1
                M = lhsT.shape[1]
                S.pe(lambda e: e.matmul(PS(sbank, 0, M, 0, 512), lhsT=lhsT, rhs=Qaug[g][0:KKrows, :, q0:q0 + 128],
                                        start=True, stop=True),
                     r=lkeys + qkeys(g, qt), w=[pk(sbank)])
                S.act(lambda e: e.activation(out=pt[0:M, :], in_=PS(sbank, 0, M, 0, 512), func=AF.Exp),
                      r=[pk(sbank)], w=[ptk])
                for (base, cm, pat) in masks:
                    S.pool(lambda e, base=base, cm=cm, pat=pat: e.affine_select(
                        out=pt[0:M, :], in_=pt[0:M, :], pattern=pat, compare_op=ALU.is_ge, fill=freg(e, 0.0), base=base,
                        channel_multiplier=cm), r=[ptk], w=[ptk])

                def pv(e):
                    e.matmul(PS(ob, 0, 64, 0, 512), lhsT=vt, rhs=pt[0:M, :], start=first, stop=last)
                    return e.matmul(PS(sb_, 0, 64, 0, 512), lhsT=ones64[0:M, :], rhs=pt[0:M, :], start=first, stop=last)
                prev = pend[0]
                pend[0] = lambda: S.pe(pv, r=[ptk, 'ones64'] + vkeys, w=[pk(ob), pk(sb_)])
                if prev is not None:
                    prev()

            def flush_pv():
                if pend[0] is not None:
                    pend[0]()
                    pend[0] = None

            def combine(g, qt, j, ob, sb_, ab):
                flush_pv()
                q0 = qt * 128
                rb = rsc[0] % 2
                rsc[0] += 1

                def gmm(e):
                    ins = None
                    for r_ in range(4):
                        ins = e.matmul(PS(6, 0, 64, r_ * 128, (r_ + 1) * 128), lhsT=oh[:, 3 * (4 * g + r_) + j, :],
                                       rhs=gsig[0:24, q0:q0 + 128], start=True, stop=True)
                    return ins
                S.pe(gmm, r=['oh', ('gsig', (qt // 4) * 512)], w=[pk(6)])
                S.dve(lambda e: e.tensor_scalar(out=rs_[rb], in0=PS(sb_, 0, 64, 0, 512), scalar1=1e-30, scalar2=None,
                                                op0=ALU.max), r=[pk(sb_)], w=[('rs', rb)])
                S.dve(lambda e: e.reciprocal(out=rs_[rb], in_=rs_[rb]), r=[('rs', rb)], w=[('rs', rb)])
                S.dve(lambda e: e.tensor_tensor(out=rs_[rb], in0=rs_[rb], in1=PS(6, 0, 64, 0, 512), op=ALU.mult),
                      r=[('rs', rb), pk(6)], w=[('rs', rb)])
                if j == 0:
                    S.dve(lambda e: e.tensor_tensor(out=acc[ab], in0=PS(ob, 0, 64, 0, 512), in1=rs_[rb], op=ALU.mult),
                          r=[pk(ob), ('rs', rb)], w=[('acc', ab)])
                else:
                    S.dve(lambda e: e.tensor_tensor(out=tt_[rb], in0=PS(ob, 0, 64, 0, 512), in1=rs_[rb], op=ALU.mult),
                          r=[pk(ob), ('rs', rb)], w=[('tt', rb)])
                    S.pool(lambda e: e.tensor_tensor(out=acc[ab], in0=acc[ab], in1=tt_[rb], op=ALU.add),
                           r=[('tt', rb), ('acc', ab)], w=[('acc', ab)])

            pat_q = [[0, 4], [1, 128]]
            pat_qn = [[0, 4], [-1, 128]]
            def partA(qt, g):
                q0 = qt * 128
                def cmm(e, g=g, q0=q0):
                    ins = None
                    for r_ in range(4):
                        ins = e.matmul(PS(7, 0, 128, r_ * 64, (r_ + 1) * 64), lhsT=Qaug[g][0:68, r_, q0:q0 + 128],
                                       rhs=KCa[g][0:68, :], start=True, stop=True)
                    return ins
                S.pe(cmm, r=qkeys(g, qt) + [('KCa', g)], w=[pk(7)])
                S.act(lambda e: e.activation(out=Ec, in_=PS(7, 0, 128, 0, 256).rearrange("p (r n) -> p r n", r=4),
                                             func=AF.Exp), r=[pk(7)], w=['Ec'])
                S.pool(lambda e, q0=q0: e.affine_select(out=Ec, in_=Ec, pattern=[[0, 4], [-32, 64]],
                                                        compare_op=ALU.is_ge, fill=freg(e, 0.0), base=q0 - 31,
                                                        channel_multiplier=1), r=['Ec'], w=['Ec'])
                S.dve(lambda e: e.tensor_reduce(out=sums_c, in_=Ec, axis=AX.X, op=ALU.add), r=['Ec'], w=['sums_c'])
                S.dve(lambda e: e.tensor_scalar(out=sums_c, in0=sums_c, scalar1=1e-30, scalar2=None, op0=ALU.max),
                      r=['sums_c'], w=['sums_c'])
                S.dve(lambda e: e.reciprocal(out=sums_c, in_=sums_c), r=['sums_c'], w=['sums_c'])
                S.dve(lambda e: e.tensor_tensor(out=Ec, in0=Ec, in1=sums_c.unsqueeze(2).broadcast_to([128, 4, 64]),
                                                op=ALU.mult), r=['Ec', 'sums_c'], w=['Ec'])
                S.dve(lambda e: e.tensor_reduce(out=imp64, in_=Ec.rearrange("p r n -> p n r"), axis=AX.X, op=ALU.add),
                      r=['Ec'], w=['imp64'])
                S.dve(lambda e: e.tensor_reduce(out=imp, in_=imp64.rearrange("p (a b) -> p a b", b=2), axis=AX.X,
                                                op=ALU.add), r=['imp64'], w=['imp'])
                S.pool(lambda e, q0=q0: e.affine_select(out=imp, in_=imp, pattern=[[-64, 32]], compare_op=ALU.is_ge,
                                                        fill=freg(e, FORCE), base=q0 - 128, channel_multiplier=1),
                       r=['imp'], w=['imp'])
                S.pool(lambda e: e.memset(imp[:, 0:1], FORCE), r=['imp'], w=['imp'])
                S.pool(lambda e, q0=q0: e.affine_select(out=imp, in_=imp, pattern=[[-64, 32]], compare_op=ALU.is_ge,
                                                        fill=freg(e, -1.0e30), base=q0, channel_multiplier=1),
                       r=['imp'], w=['imp'])
                S.dve(lambda e: e.max(out=m8[:, 0:8], in_=imp), r=['imp'], w=['m8a'])
                S.dve(lambda e: e.match_replace(out=imp2, in_to_replace=m8[:, 0:8], in_values=imp, imm_value=-3.0e38),
                      r=['imp', 'm8a'], w=['imp2'])
                S.dve(lambda e: e.max(out=m8[:, 8:16], in_=imp2), r=['imp2'], w=['m8b'])
                S.dve(lambda e: e.tensor_scalar(out=Mpad[:, 96:128], in0=imp, scalar1=m8[:, 15:16], scalar2=None,
                                                op0=ALU.is_ge), r=['imp', 'm8b'], w=['Mpad'])

            def partB(qt, g):
                q0 = qt * 128
                S.pe(lambda e: e.matmul(PS(6, 0, 128, 0, 128), lhsT=Mpad, rhs=ident_b, start=True, stop=True),
                     r=['Mpad', 'ident_b'], w=[pk(6)])
                S.dve(lambda e, g=g, q0=q0: e.tensor_scalar(
                    out=Qaug[g][96:128, :, q0:q0 + 128],
                    in0=PS(6, 96, 128, 0, 128).unsqueeze(1).broadcast_to([32, 4, 128]),
                    scalar1=1.0, scalar2=BIG, op0=ALU.subtract, op1=ALU.mult), r=[pk(6)], w=[('Qm', g, qt)])

            def partC(qt, g):
                q0 = qt * 128
                ab = (qt * 2 + g) % 2
                attn_pair(g, qt, KCa[g][0:68, :], [('KCa', g)], 68, VC[g], [('VC', g)],
                          [(q0 - 31, -32, pat_q)], 2, 3, True, True)
                combine(g, qt, 0, 2, 3, ab)
                for kt in range(qt + 1):
                    masks = [(0, -1, pat_q)] if kt == qt else []
                    attn_pair(g, qt, Ksel[g][:, kt * 128:(kt + 1) * 128], [('Ks', g, (kt // 4) * 512)] + KK[g][0:0], 128,
                              Vsel[:, kt, g * 64:(g + 1) * 64], [('Vsel', kt)], masks, 4, 5, kt == 0, kt == qt)
                combine(g, qt, 1, 4, 5, ab)
                k_lo = max(0, qt - 4)
                for kt in range(k_lo, qt + 1):
                    masks = []
                    if kt == qt:
                        masks.append((0, -1, pat_q))
                    if kt == qt - 4:
                        masks.append((-1, 1, pat_qn))
                    attn_pair(g, qt, Kwin[g][0:68, kt * 128:(kt + 1) * 128], [('Kw', g, (kt // 4) * 512)], 68,
                              Vwin[:, kt, g * 64:(g + 1) * 64], [('Vwin', kt)], masks, 2, 3, kt == k_lo, kt == qt)
                combine(g, qt, 2, 2, 3, ab)
                S.dve(lambda e, g=g, q0=q0, ab=ab: e.tensor_tensor(
                    out=ybT[0:64, 4 * g:4 * g + 4, q0:q0 + 128],
                    in0=acc[ab].rearrange("p (r q) -> p r q", r=4),
                    in1=ybT[0:64, 4 * g:4 * g + 4, q0:q0 + 128], op=ALU.mult),
                    r=[('acc', ab)] + [('ybT', 4 * g + r_, (qt // 4) * 512) for r_ in range(4)],
                    w=[('ybT', 4 * g + r_, (qt // 4) * 512) for r_ in range(4)])

            its = [(qt, g) for qt in range(16) for g in range(2)]
            partA(*its[0])
            partB(*its[0])
            for i_, (qt, g) in enumerate(its):
                if i_ + 1 < len(its):
                    partA(*its[i_ + 1])
                partC(qt, g)
                if i_ + 1 < len(its):
                    partB(*its[i_ + 1])
            for g in range(2):
                S.dve(lambda e, g=g: e.tensor_copy(out=qs_f[:, g, :, :], in_=Qaug[g][0:68, :, T:TT]), r=[('Q', g, T)], w=['qs_f'])
                S.dve(lambda e, g=g: e.tensor_copy(out=ksn[:, 0, g, :], in_=Ksel[g][0:68, T:TT]), r=[('Ks', g, T)], w=['ksn'])
                S.dve(lambda e, g=g: e.tensor_copy(out=ksn[:, 1, g, :], in_=Kwin[g][0:68, T:TT]), r=[('Kw', g, T)], w=['ksn'])
            S.dve(lambda e: e.tensor_copy(out=gs_s, in_=gsig[0:24, T:TT]), r=[('gsig', T)], w=['gs_s'])
            stage_end(9)
            S.barrier()
            A.release()
            A.mark()
            Xb = [A.alloc([128, 32, 256], F32) for _ in range(2)]
            tmp4 = xT_raw32[:, 0:8192].rearrange("p (l r d) -> p l r d", l=32, r=4)
            pt_i = A.alloc([32, NS, 2], I32)
            pt_f = A.alloc([32, NS * 2], F32)
            E32 = A.alloc([32, 128], F32)
            qcol_i = A.alloc([128, 1], I32)
            qcol_f = A.alloc([128, 1], F32)
            idx_f = A.alloc([128, NS * 2], F32)
            idx_i = A.alloc([128, NS * 2], I32)
            Wl4 = A.alloc([128, 32, 4], F32)
            slopes = A.alloc([128, 8], F32)
            posc = A.alloc([128, 2], F32)
            ABc = A.alloc([128, 2, 8], F32)
            poss = A.alloc([128, 2, 32], F32)
            ABs = A.alloc([128, 2, 8, 32], F32)
            posw = A.alloc([128, 4], F32)
            ABw = A.alloc([128, 8, 4], F32)
            Ex = A.alloc([128, 2, 128], BF16)
            qrep = A.alloc([64, 8, 128], BF16)
            qb = A.alloc([128, 8, 64], F32)
            kvc = Vwin_raw32[:, 0:512].rearrange("p (t c) -> p t c", t=2)
            tmpc = A.alloc([128, 2, 4, 64], F32)
            sc = A.alloc([128, 2, 8], F32)
            Esum = A.alloc([128, 8], F32)
            pcg = A.alloc([128, 2, 2], F32)
            impn = A.alloc([2, 256], F32)
            impd = A.alloc([2, 129], F32)
            impd2 = A.alloc([2, 129], F32)
            m8d = A.alloc([2, 16], F32)
            Md = A.alloc([2, 129], F32)
            MT = A.alloc([128, 2], BF16)
            mskd = A.alloc([128, 2, 2], F32)
            ss = Vwin_raw32[:, 512:1024].rearrange("p (t h l) -> p t h l", t=2, h=8)
            Pl = A.alloc([128, 2, 8], F32)
            Wn = Vsel_raw32[:, 0:1024].rearrange("p (l c) -> p l c", l=4)
            sw = A.alloc([128, 8, 4], F32)
            Plw = A.alloc([128, 8], F32)
            prodn = A.alloc([68, 2, 2, 4, NS], F32)
            pnew = A.alloc([1, 2, 2, 4, NS], F32)
            vrow = A.alloc([1, NS, 2, 128], F32)
            accd = A.alloc([4, 2, 64], F32)
            gts = A.alloc([4, 6, NS], F32)
            rsd = A.alloc([4, 4], F32)

            cache_v = {(c_, hf): I['cache_%s_%s' % (c_, hf)].rearrange("(b q) c -> b (q c)", q=32)
                       for c_ in ('cmp', 'sel') for hf in ('a', 'b')}
            HALF = NPOOL * 2
            idx_b = A.alloc([128, NS * 2], I32)
            idx_m = A.alloc([128, NS * 2], F32)

            def chain(addfn, fns, key, r=()):
                for fn in fns:
                    addfn(fn, r=[key] + list(r), w=[key])

            fns1 = [
                lambda e: e.iota(qcol_i, pattern=[[0, 1]], base=0, channel_multiplier=1),
                lambda e: e.memset(E32, 1.0),
                lambda e: e.memset(accd, 0.0),
                lambda e: e.affine_select(out=E32, in_=E32, pattern=[[1, 128]], compare_op=ALU.is_ge, fill=freg(e, 0.0), base=0,
                                          channel_multiplier=-4),
                lambda e: e.iota(posc, pattern=[[4096, 2]], base=31 - 8192, channel_multiplier=32,
                                 allow_small_or_imprecise_dtypes=True),
                lambda e: e.affine_select(out=E32, in_=E32, pattern=[[-1, 128]], compare_op=ALU.is_ge, fill=freg(e, 0.0), base=3,
                                          channel_multiplier=4),
                lambda e: e.iota(poss, pattern=[[4096, 2], [1, 32]], base=-8192, channel_multiplier=32,
                                 allow_small_or_imprecise_dtypes=True),
                lambda e: e.iota(posw, pattern=[[1, 4]], base=7680 - 8192, channel_multiplier=4,
                                 allow_small_or_imprecise_dtypes=True),
            ]
            for h in range(8):
                fns1.append(lambda e, h=h: e.memset(slopes[:, h:h + 1], 2.0 ** (-(h + 1))))
            for t in range(2):
                fns1.append(lambda e, t=t: e.memset(Ex[:, t, :], 1.0))
            for t in range(2):
                fns1.append(lambda e, t=t: e.affine_select(out=Ex[:, t, :], in_=Ex[:, t, :], pattern=[[1, 128]], compare_op=ALU.is_ge,
                                                          fill=freg(e, 0.0), base=128 * t, channel_multiplier=-2))
            for t in range(2):
                fns1.append(lambda e, t=t: e.affine_select(out=Ex[:, t, :], in_=Ex[:, t, :], pattern=[[-1, 128]], compare_op=ALU.is_ge,
                                                          fill=freg(e, 0.0), base=1 - 128 * t, channel_multiplier=2))
            chain(S.pool, fns1, 'dsetup')
            S.dma('sp', lambda e: e.dma_start(out=pt_i, in_=I['page_table'].rearrange("s (t j) -> j s t", j=32),
                                              allow_slow_non_contiguous=True), w=['pt_i'])
            S.dma('sp', lambda e: e.dma_start(out=Wl4[:, :, 0:2], in_=I['wk_cmp'].partition_broadcast(128)), w=['Wl4a'])
            S.dma('sp', lambda e: e.dma_start(out=Wl4[:, :, 2:4], in_=I['wv_cmp'].partition_broadcast(128)), w=['Wl4b'])
            for s in range(NS):
                S.dma('sp', lambda e, s=s: e.dma_start(out=vrow[0:1, s, 0, :], in_=kvsv[s:s + 1, 0, :]), r=['kvsv'],
                      w=[('vrow', s)])
                S.dma('sp', lambda e, s=s: e.dma_start(out=vrow[0:1, s, 1, :], in_=kvsv[s:s + 1, 1, :]), r=['kvsv'],
                      w=[('vrow', s)])

            fns2 = [
                lambda e: e.tensor_copy(out=pt_f, in_=pt_i.rearrange("p s t -> p (s t)")),
                lambda e: e.tensor_single_scalar(out=qcol_i, in_=qcol_i, scalar=3, op=ALU.bitwise_and),
                lambda e: e.tensor_tensor(out=ABc, in0=posc.unsqueeze(2).broadcast_to([128, 2, 8]),
                                          in1=slopes.unsqueeze(1).broadcast_to([128, 2, 8]), op=ALU.mult),
                lambda e: e.tensor_copy(out=qcol_f, in_=qcol_i),
                lambda e: e.tensor_tensor(out=ABw, in0=posw.unsqueeze(1).broadcast_to([128, 8, 4]),
                                          in1=slopes.unsqueeze(2).broadcast_to([128, 8, 4]), op=ALU.mult),
            ]
            for t in range(2):
                fns2.append(lambda e, t=t: e.tensor_tensor(out=ABs[:, t, :, :], in0=poss[:, t, :].unsqueeze(1).broadcast_to([128, 8, 32]),
                                                          in1=slopes.unsqueeze(2).broadcast_to([128, 8, 32]), op=ALU.mult))
            chain(S.dve, fns2, 'dsetup2', r=['dsetup', 'pt_i'])
            S.pe(lambda e: e.matmul(PS(1, 0, 128, 0, NS * 2), lhsT=E32, rhs=pt_f, start=True, stop=True),
                 r=['dsetup', 'dsetup2'], w=[pk(1)])
            S.dve(lambda e: e.tensor_scalar(out=idx_f, in0=PS(1, 0, 128, 0, NS * 2), scalar1=4.0, scalar2=qcol_f[:, 0:1],
                                            op0=ALU.mult, op1=ALU.add), r=[pk(1), 'dsetup2'], w=['idx_f'])
            idx_mb = A.alloc([128, NS * 2], F32)

            idx_c = A.alloc([128, 2, NS * 2], F32)
            chain(S.dve, [
                lambda e: e.tensor_single_scalar(out=idx_mb, in_=idx_f, scalar=float(HALF), op=ALU.is_ge),
                lambda e: e.tensor_scalar(out=idx_m, in0=idx_mb, scalar1=-1.0, scalar2=1.0, op0=ALU.mult, op1=ALU.add),
                lambda e: e.tensor_scalar(out=idx_c[:, 0, :], in0=idx_f, scalar1=float(HALF - 1), scalar2=None, op0=ALU.min),
                lambda e: e.tensor_scalar(out=idx_c[:, 1, :], in0=idx_f, scalar1=float(HALF), scalar2=0.0, op0=ALU.subtract,
                                          op1=ALU.max),
                lambda e: e.tensor_copy(out=idx_i, in_=idx_c[:, 0, :]),
                lambda e: e.tensor_copy(out=idx_b, in_=idx_c[:, 1, :]),
            ], 'idx_i', r=['idx_f'])
            for br in range(2):
                S.dve(lambda e, br=br: e.tensor_tensor(
                    out=prodn[:, br, :, :, :], in0=qs_f, in1=ksn[:, br, :, :].unsqueeze(2).broadcast_to([68, 2, 4, NS]),
                    op=ALU.mult), r=['qs_f', 'ksn'], w=[('prodn', br)])
            S.pe(lambda e: e.matmul(PS(7, 0, 1, 0, 64), lhsT=ones_f[0:68, 0:1],
                                    rhs=prodn.rearrange("p a g r s -> p (a g r s)"), start=True, stop=True),
                 r=[('prodn', 0), ('prodn', 1), 'ones_f'], w=[pk(7)])
            S.act(lambda e: e.activation(out=pnew.rearrange("p a g r s -> p (a g r s)"), in_=PS(7, 0, 1, 0, 64), func=AF.Exp),
                  r=[pk(7)], w=['pnew'])
            def gmm_d(e):
                ins = None
                for g in range(2):
                    for j in range(3):
                        c0_ = 12 * g + j
                        ins = e.matmul(PS(7, 0, 4, 64 + (g * 3 + j) * NS, 64 + (g * 3 + j + 1) * NS),
                                       lhsT=ident_b[0:24, c0_:c0_ + 10:3], rhs=gs_s, start=True, stop=True)
                return ins
            S.pe(gmm_d, r=['gs_s', 'ident_b'], w=[pk(7)])
            S.act(lambda e: e.copy(out=gts.rearrange("p a s -> p (a s)"), in_=PS(7, 0, 4, 64, 64 + 6 * NS)), r=[pk(7)],
                  w=['gts'])

            def gather(cname, s, t, xb):
                col = s * 2 + t
                S.dma('pool', lambda e: e.indirect_dma_start(
                    out=Xb[xb].rearrange("p l c -> p (l c)"), out_offset=None, in_=cache_v[(cname, 'a')],
                    in_offset=bass.IndirectOffsetOnAxis(ap=idx_i[:, col:col + 1], axis=0)), r=['idx_i'], w=[('Xb', xb)])
                S.dma('pool', lambda e: e.indirect_dma_start(
                    out=xT_raw32[:, 0:8192], out_offset=None, in_=cache_v[(cname, 'b')],
                    in_offset=bass.IndirectOffsetOnAxis(ap=idx_b[:, col:col + 1], axis=0)), r=['idx_i'], w=['tmp4'])
                S.dve(lambda e: e.tensor_scalar(out=Xb[xb].rearrange("p l c -> p (l c)"), in0=Xb[xb].rearrange("p l c -> p (l c)"),
                                                scalar1=idx_m[:, col:col + 1], scalar2=None, op0=ALU.mult),
                      r=[('Xb', xb), 'idx_i'], w=[('Xb', xb)])
                S.dve(lambda e: e.scalar_tensor_tensor(out=Xb[xb].rearrange("p l c -> p (l c)"), in0=xT_raw32[:, 0:8192],
                                                       scalar=idx_mb[:, col:col + 1], in1=Xb[xb].rearrange("p l c -> p (l c)"),
                                                       op0=ALU.mult, op1=ALU.add), r=[('Xb', xb), 'tmp4', 'idx_i'], w=[('Xb', xb)])

            stage_end(105)
            for s in range(NS):
                for t in range(2):
                    gather('cmp', s, t, t)
                    S.dve(lambda e, t=t: e.tensor_tensor(
                        out=Xb[t].rearrange("p l (a d) -> p l a d", a=4), in0=Xb[t].rearrange("p l (a d) -> p l a d", a=4),
                        in1=Wl4.unsqueeze(3).broadcast_to([128, 32, 4, 64]), op=ALU.mult),
                        r=[('Xb', t), 'Wl4a', 'Wl4b'], w=[('Xb', t)])
                    S.dve(lambda e, t=t: e.tensor_reduce(out=kvc[:, t, :], in_=Xb[t].rearrange("p l c -> p c l"), axis=AX.X,
                                                         op=ALU.add), r=[('Xb', t)], w=[('kvc', t)])
                stage_end(106)
                S.dve(lambda e, s=s: e.tensor_copy(
                    out=qrep, in_=qs_f[0:64, :, :, s].rearrange("p g r -> p (g r)").unsqueeze(2).broadcast_to([64, 8, 128])),
                    r=['qs_f'], w=['qrep'])

                def qbm(e):
                    ins = None
                    for h in range(8):
                        ins = e.matmul(PS(0, 0, 128, h * 64, (h + 1) * 64), lhsT=qrep[:, h, :], rhs=ident_b[0:64, 0:64],
                                       start=True, stop=True)
                    return ins
                S.pe(qbm, r=['qrep', 'ident_b'], w=[pk(0)])
                S.act(lambda e: e.copy(out=qb.rearrange("p h d -> p (h d)"), in_=PS(0)), r=[pk(0)], w=['qb'])
                for t in range(2):
                    S.dve(lambda e, t=t: e.tensor_tensor(
                        out=tmpc, in0=kvc[:, t, 0:128].rearrange("p (g d) -> p g d", g=2).unsqueeze(2).broadcast_to([128, 2, 4, 64]),
                        in1=qb.rearrange("p (g r) d -> p g r d", g=2), op=ALU.mult),
                        r=[('kvc', t), 'qb'], w=['tmpc'])
                    S.dve(lambda e, t=t: e.tensor_reduce(out=sc[:, t, :], in_=tmpc.rearrange("p g r d -> p (g r) d"),
                                                         axis=AX.X, op=ALU.add), r=['tmpc'], w=[('sc', t)])
                S.dve(lambda e: e.tensor_tensor(out=sc, in0=sc, in1=ABc, op=ALU.add), r=[('sc', 0), ('sc', 1), 'dsetup2'],
                      w=['sc'])
                S.act(lambda e: e.activation(out=sc, in_=sc, func=AF.Exp), r=['sc'], w=['sc'])
                S.dve(lambda e: e.tensor_tensor(out=Esum, in0=sc[:, 0, :], in1=sc[:, 1, :], op=ALU.add), r=['sc'], w=['Esum'])
                S.pe(lambda e: e.matmul(PS(1, 0, 128, 0, 8), lhsT=ones_f, rhs=Esum, start=True, stop=True),
                     r=['Esum', 'ones_f'], w=[pk(1)])
                S.dve(lambda e: e.reciprocal(out=Esum, in_=PS(1, 0, 128, 0, 8)), r=[pk(1)], w=['Esum'])
                S.dve(lambda e: e.tensor_tensor(out=sc, in0=sc, in1=Esum.unsqueeze(1).broadcast_to([128, 2, 8]), op=ALU.mult),
                      r=['sc', 'Esum'], w=['sc'])

                def ocm(e):
                    ins = None
                    for g in range(2):
                        for t in range(2):
                            ins = e.matmul(PS(2, 0, 4, g * 64, (g + 1) * 64), lhsT=sc[:, t, 4 * g:4 * g + 4],
                                           rhs=kvc[:, t, 128 + g * 64:128 + (g + 1) * 64], start=(t == 0), stop=(t == 1))
                    return ins
                S.pe(ocm, r=['sc', ('kvc', 0), ('kvc', 1)], w=[pk(2)])
                S.dve(lambda e: e.tensor_reduce(out=pcg, in_=sc.rearrange("p t (g r) -> p t g r", g=2), axis=AX.X, op=ALU.add),
                      r=['sc'], w=['pcg'])

                def trp(e):
                    ins = None
                    for t in range(2):
                        ins = e.transpose(out=PS(3, 0, 2, t * 128, (t + 1) * 128), in_=pcg[:, t, :], identity=ident_f)
                    return ins
                S.pe(trp, r=['pcg', 'ident_f'], w=[pk(3)])
                S.act(lambda e: e.copy(out=impn, in_=PS(3, 0, 2, 0, 256)), r=[pk(3)], w=['impn'])
                S.dve(lambda e: e.tensor_reduce(out=impd[:, 0:128], in_=impn.rearrange("p (a b) -> p a b", b=2), axis=AX.X,
                                                op=ALU.add), r=['impn'], w=['impd'])

                S.pool(lambda e: e.memset(impd[:, 0:1], FORCE), r=['impd'], w=['impd'])
                S.pool(lambda e: e.memset(impd[:, 127:129], FORCE), r=['impd'], w=['impd'])
                S.dve(lambda e: e.max(out=m8d[:, 0:8], in_=impd), r=['impd'], w=['m8da'])
                S.dve(lambda e: e.match_replace(out=impd2, in_to_replace=m8d[:, 0:8], in_values=impd, imm_value=-3.0e38),
                      r=['impd', 'm8da'], w=['impd2'])
                S.dve(lambda e: e.max(out=m8d[:, 8:16], in_=impd2), r=['impd2'], w=['m8db'])
                S.dve(lambda e: e.tensor_scalar(out=Md, in0=impd, scalar1=m8d[:, 15:16], scalar2=None, op0=ALU.is_ge),
                      r=['impd', 'm8db'], w=['Md'])
                S.pe(lambda e: e.transpose(out=PS(3, 0, 128, 256, 258), in_=Md[:, 0:128], identity=ident_f[0:2, 0:2]),
                     r=['Md', 'ident_f'], w=[pk(3)])
                S.act(lambda e: e.copy(out=MT, in_=PS(3, 0, 128, 256, 258)), r=[pk(3)], w=['MT'])

                def mskm(e):
                    ins = None
                    for t in range(2):
                        ins = e.matmul(PS(3, 0, 128, 260 + 2 * t, 262 + 2 * t), lhsT=Ex[:, t, :], rhs=MT, start=True, stop=True)
                    return ins
                S.pe(mskm, r=['MT', 'dsetup'], w=[pk(3)])
                S.act(lambda e: e.copy(out=mskd.rearrange("p t g -> p (t g)"), in_=PS(3, 0, 128, 260, 264)), r=[pk(3)],
                      w=['mskd'])
                for t in range(2):
                    gather('sel', s, t, t)
                    for g in range(2):
                        S.dve(lambda e, t=t, g=g: e.tensor_tensor(
                            out=tmp4, in0=Xb[t][:, :, g * 64:(g + 1) * 64].unsqueeze(2).broadcast_to([128, 32, 4, 64]),
                            in1=qb[:, 4 * g:4 * g + 4, :].unsqueeze(1).broadcast_to([128, 32, 4, 64]), op=ALU.mult),
                            r=[('Xb', t), 'qb'], w=['tmp4'])
                        S.dve(lambda e, t=t, g=g: e.tensor_reduce(
                            out=ss[:, t, 4 * g:4 * g + 4, :].rearrange("p r l -> p l r"), in_=tmp4, axis=AX.X, op=ALU.add),
                            r=['tmp4'], w=[('ss', t, g)])
                SSK = [('ss', t, g) for t in range(2) for g in range(2)]
                S.dve(lambda e: e.tensor_tensor(out=ss, in0=ss, in1=ABs, op=ALU.add), r=SSK + ['dsetup2'], w=['ss'])
                S.act(lambda e: e.activation(out=ss, in_=ss, func=AF.Exp), r=['ss'], w=['ss'])
                for t in range(2):
                    S.dve(lambda e, t=t: e.tensor_tensor(
                        out=ss[:, t, :, :].rearrange("p (g r) l -> p g (r l)", g=2),
                        in0=ss[:, t, :, :].rearrange("p (g r) l -> p g (r l)", g=2),
                        in1=mskd[:, t, :].unsqueeze(2).broadcast_to([128, 2, 128]), op=ALU.mult),
                        r=['ss', 'mskd'], w=['ss'])
                S.dve(lambda e: e.tensor_reduce(out=Pl, in_=ss, axis=AX.X, op=ALU.add), r=['ss'], w=['Pl'])

                def pvs(e, s=s):
                    ins = None
                    for g in range(2):
                        k = 0
                        for t in range(2):
                            for l in range(32):
                                ins = e.matmul(PS(4, 0, 4, g * 64, (g + 1) * 64), lhsT=ss[:, t, 4 * g:4 * g + 4, l],
                                               rhs=Xb[t][:, l, 128 + g * 64:128 + (g + 1) * 64], start=(k == 0), stop=False)
                                k += 1
                        ins = e.matmul(PS(4, 0, 4, g * 64, (g + 1) * 64), lhsT=pnew[0:1, 0, g, :, s],
                                       rhs=vrow[0:1, s, 0, g * 64:(g + 1) * 64], start=False, stop=True)
                        for t in range(2):
                            ins = e.matmul(PS(5, 0, 4, g, g + 1), lhsT=Pl[:, t, 4 * g:4 * g + 4], rhs=ones_f[:, 0:1],
                                           start=(t == 0), stop=False)
                        ins = e.matmul(PS(5, 0, 4, g, g + 1), lhsT=pnew[0:1, 0, g, :, s], rhs=ones_f[0:1, 0:1],
                                       start=False, stop=True)
                    return ins
                S.pe(pvs, r=['ss', 'Pl', ('Xb', 0), ('Xb', 1), 'pnew', ('vrow', s), 'ones_f'], w=[pk(4), pk(5)])
                S.dma('sp', lambda e, s=s: e.dma_start(out=Wn, in_=I['cache_win'][s * 512:(s + 1) * 512, :].rearrange(
                    "(p j) c -> p j c", j=4)), w=['Wn'])
                for g in range(2):
                    S.dve(lambda e, g=g: e.tensor_tensor(
                        out=tmp4[:, 0:4, :, :], in0=Wn[:, :, g * 64:(g + 1) * 64].unsqueeze(2).broadcast_to([128, 4, 4, 64]),
                        in1=qb[:, 4 * g:4 * g + 4, :].unsqueeze(1).broadcast_to([128, 4, 4, 64]), op=ALU.mult),
                        r=['Wn', 'qb'], w=['tmp4'])
                    S.dve(lambda e, g=g: e.tensor_reduce(out=sw[:, 4 * g:4 * g + 4, :].rearrange("p r l -> p l r"),
                                                         in_=tmp4[:, 0:4, :, :], axis=AX.X, op=ALU.add),
                          r=['tmp4'], w=[('sw', g)])
                S.dve(lambda e: e.tensor_tensor(out=sw, in0=sw, in1=ABw, op=ALU.add), r=[('sw', 0), ('sw', 1), 'dsetup2'],
                      w=['sw'])
                S.act(lambda e: e.activation(out=sw, in_=sw, func=AF.Exp), r=['sw'], w=['sw'])
                S.pool(lambda e: e.memset(sw[0:1, :, 0:1], 0.0), r=['sw'], w=['sw'])
                S.dve(lambda e: e.tensor_reduce(out=Plw, in_=sw, axis=AX.X, op=ALU.add), r=['sw'], w=['Plw'])

                def pvw(e, s=s):
                    ins = None
                    for g in range(2):
                        for l in range(4):
                            ins = e.matmul(PS(6, 0, 4, g * 64, (g + 1) * 64), lhsT=sw[:, 4 * g:4 * g + 4, l],
                                           rhs=Wn[:, l, 128 + g * 64:128 + (g + 1) * 64], start=(l == 0), stop=False)
                        ins = e.matmul(PS(6, 0, 4, g * 64, (g + 1) * 64), lhsT=pnew[0:1, 1, g, :, s],
                                       rhs=vrow[0:1, s, 1, g * 64:(g + 1) * 64], start=False, stop=True)
                        ins = e.matmul(PS(5, 0, 4, 2 + g, 3 + g), lhsT=Plw[:, 4 * g:4 * g + 4], rhs=ones_f[:, 0:1],
                                       start=True, stop=False)
                        ins = e.matmul(PS(5, 0, 4, 2 + g, 3 + g), lhsT=pnew[0:1, 1, g, :, s], rhs=ones_f[0:1, 0:1],
                                       start=False, stop=True)
                    return ins
                S.pe(pvw, r=['sw', 'Plw', 'Wn', 'pnew', ('vrow', s), 'ones_f'], w=[pk(6), pk(5)])
                S.dve(lambda e: e.reciprocal(out=rsd, in_=PS(5, 0, 4, 0, 4)), r=[pk(5)], w=['rsd'])
                for g in range(2):
                    S.dve(lambda e, g=g, s=s: e.tensor_scalar(out=accd[:, g, :], in0=PS(2, 0, 4, g * 64, (g + 1) * 64),
                                                              scalar1=gts[:, g * 3 + 0, s:s + 1], scalar2=None, op0=ALU.mult),
                          r=[pk(2), 'gts'], w=[('accd', g)])
                    for bi, (bank, col) in enumerate([(4, g), (6, 2 + g)]):
                        S.dve(lambda e, g=g, s=s, bi=bi, col=col: e.tensor_tensor(
                            out=rsd[:, col:col + 1], in0=rsd[:, col:col + 1], in1=gts[:, g * 3 + 1 + bi, s:s + 1], op=ALU.mult),
                            r=['rsd', 'gts'], w=['rsd'])
                        S.dve(lambda e, g=g, bank=bank, col=col: e.scalar_tensor_tensor(
                            out=accd[:, g, :], in0=PS(bank, 0, 4, g * 64, (g + 1) * 64), scalar=rsd[:, col:col + 1],
                            in1=accd[:, g, :], op0=ALU.mult, op1=ALU.add), r=[pk(bank), 'rsd', ('accd', g)], w=[('accd', g)])
                    S.pe(lambda e, g=g: e.transpose(out=PS(7, 0, 64, 128 + 4 * g, 132 + 4 * g), in_=accd[:, g, :],
                                                    identity=ident_f[0:4, 0:4]), r=[('accd', g), 'ident_f'], w=[pk(7)])
                    S.dve(lambda e, g=g, s=s: e.tensor_tensor(
                        out=ybT[0:64, 4 * g:4 * g + 4, T + s], in0=PS(7, 0, 64, 128 + 4 * g, 132 + 4 * g),
                        in1=ybT[0:64, 4 * g:4 * g + 4, T + s], op=ALU.mult),
                        r=[pk(7)] + [('ybT', 4 * g + r_, T) for r_ in range(4)],
                        w=[('ybT', 4 * g + r_, T) for r_ in range(4)] + ['dec_yb'])
            if os.environ.get('KDEV_DUMP'):
                yp = O['y_prompt']
                S.dma('sp', lambda e: e.dma_start(out=yp[0:128, :], in_=Xb[0][:, 0:4, :].rearrange("p l c -> p (l c)")), r=[('Xb', 0)])
                S.dma('sp', lambda e: e.dma_start(out=yp[128:256, 0:512], in_=kvc.rearrange("p t c -> p (t c)")), r=[('kvc', 0), ('kvc', 1)])
                S.dma('sp', lambda e: e.dma_start(out=yp[256:384, 0:16], in_=sc.rearrange("p t h -> p (t h)")), r=['sc'])
                S.dma('sp', lambda e: e.dma_start(out=yp[384:512, 0:512], in_=ss.rearrange("p t h l -> p (t h l)")), r=['ss'])
                S.dma('sp', lambda e: e.dma_start(out=yp[512:516, 0:4], in_=rsd), r=['rsd'])
                S.dma('sp', lambda e: e.dma_start(out=yp[516:520, 0:128], in_=accd.rearrange("p g d -> p (g d)")), r=[('accd', 0), ('accd', 1)])
                S.dma('sp', lambda e: e.dma_start(out=yp[640:768, 0:8], in_=idx_f), r=['idx_i'])
                S.dma('sp', lambda e: e.dma_start(out=yp[768:896, 0:4], in_=mskd.rearrange("p t g -> p (t g)")), r=['mskd'])
                S.dma('sp', lambda e: e.dma_start(out=yp[896:898, 0:129], in_=impd), r=['impd'])
                S.dma('sp', lambda e: e.dma_start(out=yp[900:1028, 0:512], in_=qb.rearrange("p h d -> p (h d)")), r=['qb'])
                S.dma('sp', lambda e: e.dma_start(out=yp[1028:1029, 0:64], in_=pnew.rearrange("p a g r s -> p (a g r s)")), r=['pnew'])
                S.dma('sp', lambda e: e.dma_start(out=yp[1030:1034, 0:24], in_=gts.rearrange("p a s -> p (a s)")), r=['gts'])
            S.barrier()
            A.release()
            stage_end(11)
            ALPHA = float((2 * 2) ** 0.25)
            x1T = xT
            A.mark()
            wo_a = A.alloc([128, 4, DM], BF16)
            wo_b = A.alloc([64, 8, DM], BF16)
            lnp = A.alloc([128, 2, DM], F32)
            xres = [A.alloc([128, DM], F32) for _ in range(2)]
            x1b = [A.alloc([128, DM], BF16) for _ in range(2)]
            bst = A.alloc([128, 2, 6], F32)
            mv = A.alloc([128, 4], F32)
            for j in range(4):
                S.dma('pool', lambda e, j=j: e.dma_start(out=wo_a[:, j, :], in_=I['w_out_even'][j * 128:(j + 1) * 128, :]),
                      w=[('wo_a', j)])
            for h in range(8):
                S.dma('pool', lambda e, h=h: e.dma_start(out=wo_b[:, h, :],
                                                         in_=I['w_out_even'][512 + h * 64:512 + (h + 1) * 64, :]),
                      w=[('wo_b', h)])
            S.dma('sp', lambda e: e.dma_start(out=lnp[:, 0, :], in_=I['ln_g'][0:1, :].partition_broadcast(128)), w=[('lnp', 0)])
            S.dma('sp', lambda e: e.dma_start(out=lnp[:, 1, :], in_=I['ln_b'][0:1, :].partition_broadcast(128)), w=[('lnp', 1)])
            WO = [('wo_a', j) for j in range(4)] + [('wo_b', h) for h in range(8)]

            def resid_ln(n, rows, c0, xsrc, yk, lidx, ya_, yb_, outs):
                b = n % 2
                S.dma('sp', lambda e: e.dma_start(out=xres[b][0:rows, :], in_=xsrc), w=[('xres', b)])

                def mm(e):
                    ins = None
                    for cb in range(2):
                        k = 0
                        tot = len(ya_) + len(yb_)
                        for (lt, rt) in ya_ + yb_:
                            ins = e.matmul(PS(cb, 0, rows, 0, 512), lhsT=lt[:, c0:c0 + rows], rhs=rt[:, cb * 512:(cb + 1) * 512],
                                           start=(k == 0), stop=(k == tot - 1))
                            k += 1
                    return ins
                S.pe(mm, r=yk + WO, w=[pk(0), pk(1)])
                for cb in range(2):
                    S.dve(lambda e, cb=cb: e.scalar_tensor_tensor(
                        out=xres[b][0:rows, cb * 512:(cb + 1) * 512], in0=xres[b][0:rows, cb * 512:(cb + 1) * 512],
                        scalar=ALPHA, in1=PS(cb, 0, rows, 0, 512), op0=ALU.mult, op1=ALU.add),
                        r=[('xres', b), pk(cb)], w=[('xres', b)])
                    S.dve(lambda e, cb=cb: e.bn_stats(out=bst[0:rows, cb, :], in_=xres[b][0:rows, cb * 512:(cb + 1) * 512]),
                          r=[('xres', b)], w=[('bst', cb)])
                S.dve(lambda e: e.bn_aggr(out=mv[0:rows, 0:2], in_=bst[0:rows, :, :].rearrange("p a b -> p (a b)")),
                      r=[('bst', 0), ('bst', 1)], w=['mv'])
                S.dve(lambda e: e.tensor_scalar(out=mv[0:rows, 2:3], in0=mv[0:rows, 1:2], scalar1=1e-5, scalar2=None,
                                                op0=ALU.add), r=['mv'], w=['mv2'])
                S.act(lambda e: e.activation(out=mv[0:rows, 2:3], in_=mv[0:rows, 2:3], func=AF.Ln), r=['mv2'], w=['mv2'])
                S.act(lambda e: e.activation(out=mv[0:rows, 2:3], in_=mv[0:rows, 2:3], func=AF.Exp, scale=-0.5),
                      r=['mv2'], w=['mv2'])
                S.dve(lambda e: e.tensor_scalar(out=xres[b][0:rows, :], in0=xres[b][0:rows, :], scalar1=mv[0:rows, 0:1],
                                                scalar2=mv[0:rows, 2:3], op0=ALU.subtract, op1=ALU.mult),
                      r=[('xres', b), 'mv', 'mv2'], w=[('xres', b)])
                S.dve(lambda e: e.tensor_tensor(out=xres[b][0:rows, :], in0=xres[b][0:rows, :], in1=lnp[0:rows, 2 * lidx, :],
                                                op=ALU.mult), r=[('xres', b), ('lnp', 2 * lidx)], w=[('xres', b)])
                S.pool(lambda e: e.tensor_tensor(out=xres[b][0:rows, :], in0=xres[b][0:rows, :],
                                                 in1=lnp[0:rows, 2 * lidx + 1, :], op=ALU.add),
                       r=[('xres', b), ('lnp', 2 * lidx + 1)], w=[('xres', b)])
                outs(b)

            def l0_outs(n, rows, c0):
                def f(b):
                    S.dma('sp', lambda e: e.dma_start(out=x1_scr[c0:c0 + rows, :], in_=xres[b][0:rows, :]),
                          r=[('xres', b)], w=[('x1scr', n)])
                    if n == 15:
                        S.dma('sp', lambda e: e.dma_start(out=O['shift_p'], in_=xres[b][127:128, :]), r=[('xres', b)])
                    if n == 16:
                        S.dma('sp', lambda e: e.dma_start(out=O['shift_s'], in_=xres[b][0:rows, :]), r=[('xres', b)])
                    S.act(lambda e: e.copy(out=x1b[b][0:rows, :], in_=xres[b][0:rows, :]), r=[('xres', b)], w=[('x1b', b)])
                    bank = 2 + (n % 2)

                    def tr(e):
                        ins = None
                        for c in range(8):
                            ins = e.transpose(out=PSB(bank, 0, 128, c * rows, (c + 1) * rows),
                                              in_=x1b[b][0:rows, c * 128:(c + 1) * 128], identity=ident_b[0:rows, 0:rows])
                        return ins
                    S.pe(tr, r=[('x1b', b), 'ident_b'], w=[pk(bank)])
                    S.dve(lambda e: e.tensor_copy(out=x1T[:, :, c0:c0 + rows],
                                                  in_=PSB(bank, 0, 128, 0, 8 * rows).rearrange("p (c t) -> p c t", c=8)),
                          r=[pk(bank)], w=[('x1T', n)])
                return f

            for n in range(17):
                rows = 128 if n < 16 else NS
                c0 = n * 128 if n < 16 else T
                xsrc = I['xp'][c0:c0 + 128, :] if n < 16 else I['xs']
                blk = (c0 // 512) * 512 if n < 16 else T
                yk = [('yaT', j, blk) for j in range(4)] + [('ybT', h, blk) for h in range(8)]
                if n == 16:
                    yk += ['dec_yb']
                ya_ = [(yaT[:, j, :], wo_a[:, j, :]) for j in range(4)]
                yb_ = [(ybT[0:64, h, :], wo_b[0:64, h, :]) for h in range(8)]
                resid_ln(n, rows, c0, xsrc, yk, 0, ya_, yb_, l0_outs(n, rows, c0))
            S.barrier()
            A.release()
            stage_end(10)
            S.barrier()
            A.top = L1_BASE
            A.marks = []
            A.mark()
            dT = A.alloc([128, 8, TT], BF16)
            xsh = A.alloc([128, 8, TT], BF16)
            muT = A.alloc([128, 6, 8], F32)
            vecs = A.alloc([128, 5, 8], F32)
            shb = A.alloc([NS, DM], BF16)
            stg = [A.alloc([128, 1024], F32) for _ in range(3)]
            wl1 = [A.alloc([128, 8, 128], BF16) for _ in range(3)]
            wfull = A.alloc([128, 8, 1024], BF16)
            lo1 = A.alloc([128, 8, 64], BF16)
            lo2 = A.alloc([64, 1024], BF16)
            t1T = A.alloc([64, TT], BF16)
            S.dma('sp', lambda e: e.dma_start(out=muT, in_=I['mu_c'].rearrange("i (c p) -> p i c", p=128),
                                              allow_slow_non_contiguous=True), w=['muT'])
            for i, nm in enumerate(['w0', 'a0', 'k_k', 'k_a', 'r_k']):
                S.dma('sp', lambda e, i=i, nm=nm: e.dma_start(out=vecs[:, i, :], in_=I[nm].rearrange("o (c p) -> p (o c)", p=128),
                                                              allow_slow_non_contiguous=True), w=[('vecs', i)])
            S.dma('pool', lambda e: e.dma_start(out=shb, in_=I['state_shift']), w=['shb'])

            def trsh(e):
                ins = None
                for c in range(8):
                    ins = e.transpose(out=PSB(0, 0, 128, c * NS, (c + 1) * NS), in_=shb[0:NS, c * 128:(c + 1) * 128],
                                      identity=ident_b[0:NS, 0:NS])
                return ins
            S.pe(trsh, r=['shb', 'ident_b'], w=[pk(0)])
            X1K = [('x1T', n) for n in range(17)]
            S.dve(lambda e: e.tensor_tensor(out=dT[:, :, T:TT], in0=PSB(0, 0, 128, 0, 8 * NS).rearrange("p (c t) -> p c t", c=8),
                                            in1=x1T[:, :, T:TT], op=ALU.subtract), r=[pk(0)] + X1K, w=['dT'])
            S.dve(lambda e: e.tensor_tensor(out=dT[:, :, 1:T], in0=x1T[:, :, 0:T - 1], in1=x1T[:, :, 1:T], op=ALU.subtract),
                  r=X1K, w=['dT'])
            S.dve(lambda e: e.tensor_scalar(out=dT[:, :, 0:1], in0=x1T[:, :, 0:1], scalar1=-1.0, scalar2=None, op0=ALU.mult),
                  r=X1K, w=['dT'])

            def mk_xsh(i):
                for c in range(8):
                    eng = S.dve if c % 2 == 0 else S.pool
                    if c % 2 == 0:
                        S.dve(lambda e, c=c: e.scalar_tensor_tensor(out=xsh[:, c, :], in0=dT[:, c, :], scalar=muT[:, i, c:c + 1],
                                                                    in1=x1T[:, c, :], op0=ALU.mult, op1=ALU.add),
                              r=['dT', 'muT'] + X1K, w=[('xsh', c)])
                    else:
                        S.pool(lambda e, c=c: e.tensor_scalar(out=xsh[:, c, :], in0=dT[:, c, :], scalar1=muT[:, i, c:c + 1],
                                                              scalar2=None, op0=ALU.mult), r=['dT', 'muT'], w=[('xsh', c)])
                        S.pool(lambda e, c=c: e.tensor_tensor(out=xsh[:, c, :], in0=xsh[:, c, :], in1=x1T[:, c, :], op=ALU.add),
                               r=[('xsh', c)] + X1K, w=[('xsh', c)])
            XSH = [('xsh', c) for c in range(8)]
            wrk_t = I['w_rkvz'].rearrange("(i c p) f -> i p c f", i=4, p=128)
            sgc = [0]
            wlc = [0]

            def proj_fm1(widx, dst, func, bias_i):
                for pr in range(8):
                    wi = wlc[0] % 3
                    wlc[0] += 1
                    S.dma('pool', lambda e, wi=wi, pr=pr: e.dma_start(out=wl1[wi], in_=wrk_t[widx][:, :, pr * 128:(pr + 1) * 128]),
                          w=[('wl1', wi)])
                    si = sgc[0] % 3
                    sgc[0] += 1
                    for bi_, (c0, n) in enumerate(BLKS):
                        bank = pbank[0] % 4
                        pbank[0] += 1

                        def mm(e, wi=wi, c0=c0, n=n, bank=bank):
                            ins = None
                            for c in range(8):
                                ins = e.matmul(PS(bank, 0, 128, 0, n), lhsT=wl1[wi][:, c, :], rhs=xsh[:, c, c0:c0 + n],
                                               start=(c == 0), stop=(c == 7))
                            return ins
                        S.pe(mm, r=[('wl1', wi)] + XSH, w=[pk(bank)])
                        cc0 = c0 if c0 < T else 0
                        sdst = stg[si][:, cc0 % 1024:cc0 % 1024 + n] if c0 < T else stg[si][:, 0:n]
                        S.act(lambda e, bank=bank, n=n, sdst=sdst: e.copy(out=sdst, in_=PS(bank, 0, 128, 0, n)),
                              r=[pk(bank)], w=[('stg', si)])
                        if bi_ in (1, 3, 4):
                            lo = {1: 0, 3: 1024, 4: T}[bi_]
                            wdt = 1024 if bi_ != 4 else NS
                            S.dma('sp', lambda e, si=si, pr=pr, lo=lo, wdt=wdt: e.dma_start(out=dst[:, pr, lo:lo + wdt],
                                                                                           in_=stg[si][:, 0:wdt]),
                                  r=[('stg', si)], w=[('scr', widx, pr)])
                            if bi_ != 4:
                                si = sgc[0] % 3
                                sgc[0] += 1

            def proj_tm1(widx, dst, silu):
                for c in range(8):
                    S.dma('pool', lambda e, c=c: e.dma_start(out=wfull[:, c, :], in_=wrk_t[widx][:, c, :]), w=[('wfull', c)])
                WF = [('wfull', c) for c in range(8)]
                for n in range(17):
                    rows = 128 if n < 16 else NS
                    c0 = n * 128 if n < 16 else T
                    si = sgc[0] % 3
                    sgc[0] += 1

                    def mm(e, rows=rows, c0=c0):
                        ins = None
                        for cb in range(2):
                            for c in range(8):
                                ins = e.matmul(PS(4 + cb, 0, rows, 0, 512), lhsT=xsh[:, c, c0:c0 + rows],
                                               rhs=wfull[:, c, cb * 512:(cb + 1) * 512], start=(c == 0), stop=(c == 7))
                        return ins
                    S.pe(mm, r=WF + XSH, w=[pk(4), pk(5)])
                    for cb in range(2):
                        if silu:
                            S.act(lambda e, cb=cb, rows=rows, si=si: e.activation(
                                out=stg[si][0:rows, cb * 512:(cb + 1) * 512], in_=PS(4 + cb, 0, rows, 0, 512), func=AF.Silu),
                                r=[pk(4 + cb)], w=[('stg', si)])
                        else:
                            S.act(lambda e, cb=cb, rows=rows, si=si: e.copy(out=stg[si][0:rows, cb * 512:(cb + 1) * 512],
                                                                            in_=PS(4 + cb, 0, rows, 0, 512)),
                                  r=[pk(4 + cb)], w=[('stg', si)])
                    S.dma('sp', lambda e, rows=rows, c0=c0, si=si: e.dma_start(out=dst[c0:c0 + rows, :], in_=stg[si][0:rows, :]),
                          r=[('stg', si)], w=[('scrt', widx, n)])

            def proj_lora(mi, w1n, w2n, vi, dst, use_tanh):
                S.dma('pool', lambda e: e.dma_start(out=lo1, in_=I[w1n].rearrange("(c p) f -> p c f", p=128)), w=['lo1'])
                S.dma('pool', lambda e: e.dma_start(out=lo2, in_=I[w2n]), w=['lo2'])
                for (c0, n) in BLKS:
                    bank = pbank[0] % 4
                    pbank[0] += 1

                    def mm(e, c0=c0, n=n, bank=bank):
                        ins = None
                        for c in range(8):
                            ins = e.matmul(PS(bank, 0, 64, 0, n), lhsT=lo1[:, c, :], rhs=xsh[:, c, c0:c0 + n],
                                           start=(c == 0), stop=(c == 7))
                        return ins
                    S.pe(mm, r=['lo1'] + XSH, w=[pk(bank)])
                    if use_tanh:
                        S.act(lambda e, c0=c0, n=n, bank=bank: e.activation(out=t1T[:, c0:c0 + n], in_=PS(bank, 0, 64, 0, n),
                                                                            func=AF.Tanh), r=[pk(bank)], w=[('t1T', c0)])
                    else:
                        S.act(lambda e, c0=c0, n=n, bank=bank: e.copy(out=t1T[:, c0:c0 + n], in_=PS(bank, 0, 64, 0, n)),
                              r=[pk(bank)], w=[('t1T', c0)])
                T1K = [('t1T', c0) for (c0, n) in BLKS]
                for pr in range(8):
                    si = sgc[0] % 3
                    sgc[0] += 1
                    for bi_, (c0, n) in enumerate(BLKS):
                        bank = pbank[0] % 4
                        pbank[0] += 1
                        S.pe(lambda e, pr=pr, c0=c0, n=n, bank=bank: e.matmul(
                            PS(bank, 0, 128, 0, n), lhsT=lo2[:, pr * 128:(pr + 1) * 128], rhs=t1T[:, c0:c0 + n],
                            start=True, stop=True), r=['lo2'] + T1K, w=[pk(bank)])
                        sdst = stg[si][:, c0 % 1024:c0 % 1024 + n] if c0 < T else stg[si][:, 0:n]
                        S.act(lambda e, bank=bank, n=n, sdst=sdst, pr=pr: e.activation(
                            out=sdst, in_=PS(bank, 0, 128, 0, n), func=AF.Sigmoid, bias=vecs[:, vi, pr:pr + 1]),
                            r=[pk(bank), ('vecs', vi)], w=[('stg', si)])
                        if bi_ in (1, 3, 4):
                            lo = {1: 0, 3: 1024, 4: T}[bi_]
                            wdt = 1024 if bi_ != 4 else NS
                            S.dma('sp', lambda e, si=si, pr=pr, lo=lo, wdt=wdt: e.dma_start(out=dst[:, pr, lo:lo + wdt],
                                                                                           in_=stg[si][:, 0:wdt]),
                                  r=[('stg', si)], w=[('scr', mi, pr)])
                            if bi_ != 4:
                                si = sgc[0] % 3
                                sgc[0] += 1

            mk_xsh(0)
            proj_fm1(0, r_scr, None, None)
            mk_xsh(1)
            proj_fm1(1, k_scr, None, None)
            mk_xsh(2)
            proj_tm1(2, v_scr, False)
            mk_xsh(3)
            proj_tm1(3, z_scr, True)
            mk_xsh(4)
            proj_lora(4, 'w1', 'w2', 0, sw_scr, True)
            mk_xsh(5)
            proj_lora(5, 'a1', 'a2', 1, a_scr, False)
            S.barrier()
            A.release()
            stage_end(12)
            A.top = xT_off
            A.marks = []
            A.mark()
            C = 64
            NEG_E05 = -float(np.exp(-0.5))
            wo_o = A.alloc([128, 8, DM], BF16)
            for c in range(8):
                S.dma('pool', lambda e, c=c: e.dma_start(out=wo_o[:, c, :], in_=I['w_out_odd'][c * 128:(c + 1) * 128, :]),
                      w=[('wo_o', c)])
            WOO = [('wo_o', c) for c in range(8)]
            gnp = A.alloc([64, 4, DM], F32)
            S.dma('sp', lambda e: e.dma_start(out=gnp[:, 0, :], in_=I['gn_g'].partition_broadcast(64)), w=[('gnp', 0)])
            S.dma('sp', lambda e: e.dma_start(out=gnp[:, 1, :], in_=I['gn_b'].partition_broadcast(64)), w=[('gnp', 1)])
            S.dma('sp', lambda e: e.dma_start(out=gnp[:, 2, :], in_=I['ln_g'][1:2, :].partition_broadcast(64)), w=[('gnp', 2)])
            S.dma('sp', lambda e: e.dma_start(out=gnp[:, 3, :], in_=I['ln_b'][1:2, :].partition_broadcast(64)), w=[('gnp', 3)])
            vec1 = A.alloc([64, 5, 16], F32)
            for i, nm in enumerate(['w0', 'a0', 'k_k', 'k_a', 'r_k']):
                S.dma('sp', lambda e, i=i, nm=nm: e.dma_start(out=vec1[:, i, :], in_=I[nm].rearrange("o (ph c) -> c (o ph)", c=64),
                                                              allow_slow_non_contiguous=True), w=['vec1'])
            m_lt = A.alloc([64, 64], F32)
            m_le = A.alloc([64, 64], F32)
            m_gt = A.alloc([64, 64], F32)
            blk1 = A.alloc([64, 64], F32)
            sel2 = A.alloc([64, 1], BF16)
            ones_t = A.alloc([64, 64], F32)

            def l1const(e):
                e.memset(ones_t, 1.0)
                e.memset(m_lt, 1.0)
                e.memset(m_le, 1.0)
                e.memset(m_gt, 1.0)
                e.affine_select(out=m_lt, in_=m_lt, pattern=[[1, 64]], compare_op=ALU.is_ge, fill=freg(e, 0.0), base=-1,
                                channel_multiplier=-1)
                e.affine_select(out=m_le, in_=m_le, pattern=[[1, 64]], compare_op=ALU.is_ge, fill=freg(e, 0.0), base=0,
                                channel_multiplier=-1)
                e.affine_select(out=m_gt, in_=m_gt, pattern=[[-1, 64]], compare_op=ALU.is_ge, fill=freg(e, 0.0), base=-1,
                                channel_multiplier=1)
                e.memset(blk1, 1.0)
                return e.memset(sel2, 1.0)
            S.pool(l1const, w=['l1c'])

            ST = A.alloc([64, 16, 64], F32)
            STb = A.alloc([64, 16, 64], BF16)
            fr = A.alloc([64, 16, C], F32)
            fk = A.alloc([64, 16, C], F32)
            fw = A.alloc([64, 16, C], F32)
            fa = A.alloc([64, 16, C], F32)
            cl = A.alloc([64, 16, C], F32)
            L1 = A.alloc([64, 16, C], F32)
            L2 = A.alloc([64, 16, C], F32)
            L3 = A.alloc([64, 16, C], F32)
            Lend = A.alloc([64, 16], F32)
            t_a = A.alloc([64, 16, C], F32)
            t_b = A.alloc([64, 16, C], F32)
            kap = fw
            kmd = A.alloc([64, 16, C], F32)
            kt_b = A.alloc([64, 16, C], BF16)
            bt_b = A.alloc([64, 16, C], BF16)
            ktl_b = A.alloc([64, 16, C], BF16)
            rt_b = A.alloc([64, 16, C], BF16)
            bh_f = A.alloc([64, 16, C], BF16)
            kh_f = A.alloc([64, 16, C], BF16)
            pr_b = A.alloc([64, 16, C], BF16)
            Bh = A.alloc([64, DM], BF16)
            Kh = A.alloc([64, DM], BF16)
            Vf2 = [A.alloc([64, DM], F32) for _ in range(2)]
            Vb = A.alloc([64, DM], BF16)
            Zf2 = [A.alloc([64, DM], F32) for _ in range(2)]
            gN = A.alloc([64, 16, 64], BF16)
            gNT = A.alloc([64, 16, 64], BF16)
            gAk = A.alloc([64, 16, 64], BF16)
            gBb = A.alloc([64, 16, 64], BF16)
            gBk = A.alloc([64, 16, 64], BF16)
            gX = [A.alloc([64, 16, 64], BF16) for _ in range(2)]
            gP = [A.alloc([64, 16, 64], BF16) for _ in range(2)]
            gPT = [A.alloc([64, 16, 64], BF16) for _ in range(2)]
            Rm = A.alloc([64, DM], BF16)
            Ub = A.alloc([64, DM], BF16)
            Yf = A.alloc([64, DM], F32)
            Yc = A.alloc([64, DM], F32)
            st1 = A.alloc([64, 16], F32)
            st2 = A.alloc([64, 16], F32)
            rkb2 = [A.alloc([64, 16], F32) for _ in range(2)]
            gb = A.alloc([64, DM], BF16)
            gT = A.alloc([128, 8, 64], BF16)
            xr2 = [A.alloc([64, DM], F32) for _ in range(2)]
            bst1 = A.alloc([64, 2, 6], F32)
            mv1 = A.alloc([64, 4], F32)
            wko = A.alloc([64, 16, 64], F32)
            stin2 = A.alloc([128, 8, 64], F32)

            def hp_rows(ap3, h):
                return ap3[:, h, :]

            def chunk(cols, ncol, first, xsrc_rows, yout, yrows, ck, par, post_seq=None):
                Vf, Zf, xr, rkb = Vf2[par], Zf2[par], xr2[par], rkb2[par]
                kVf, kZf, kxr, krkb = ('Vf', par), ('Zf', par), ('xr', par), ('rkb', par)
                pad = ncol < C
                if pad:
                    cols = cols - (C - 1)
                for (tile_, scr, nm) in ((fr, r_scr, 'fr'), (fk, k_scr, 'fk'), (fw, sw_scr, 'fw'), (fa, a_scr, 'fa')):
                    S.dma('sp', lambda e, tile_=tile_, scr=scr: e.dma_start(
                        out=tile_.rearrange("c (pr hp) t -> c pr hp t", hp=2),
                        in_=scr.rearrange("(hp c) pr t -> c pr hp t", hp=2)[:, :, :, cols:cols + C]), r=[('scr_all',)], w=[nm])
                    if pad:
                        S.pool(lambda e, tile_=tile_: e.memset(tile_[:, :, 0:C - 1], 0.0), r=[nm], w=[nm])
                S.dma('sp', lambda e: e.dma_start(out=Vf, in_=v_scr[cols:cols + C, :]), r=[('scr_all',)], w=[kVf])
                S.dma('sp', lambda e: e.dma_start(out=Zf, in_=z_scr[cols:cols + C, :]), r=[('scr_all',)], w=[kZf])
                S.dma('sp', lambda e: e.dma_start(out=xr, in_=x1_scr[cols:cols + C, :]), r=[('scr_all',)], w=[kxr])
                if pad:
                    S.pool(lambda e: e.memset(Vf[0:C - 1, :], 0.0), r=[kVf], w=[kVf])
                    S.pool(lambda e: e.memset(Zf[0:C - 1, :], 0.0), r=[kZf], w=[kZf])
                S.act(lambda e: e.copy(out=Vb, in_=Vf), r=[kVf], w=['Vb'])
                S.dve(lambda e: e.tensor_scalar(out=fw, in0=fw, scalar1=NEG_E05, scalar2=None, op0=ALU.mult), r=['fw'], w=['fw'])

                def scans(e):
                    ins = None
                    for pr_ in range(16):
                        ins = e.tensor_tensor_scan(out=cl[:, pr_, :], data0=ones_t, data1=fw[:, pr_, :], initial=0.0,
                                                   op0=ALU.mult, op1=ALU.add)
                    return ins
                S.dve(scans, r=['fw', 'l1c'], w=['cl'])
                S.act(lambda e: e.activation(out=L1, in_=cl, func=AF.Exp), r=['cl'], w=['L1'])
                S.act(lambda e: e.activation(out=L3, in_=cl, func=AF.Exp, scale=-1.0), r=['cl'], w=['L3'])
                S.dve(lambda e: e.tensor_tensor(out=t_a, in0=cl, in1=fw, op=ALU.subtract), r=['cl', 'fw'], w=['t_a'])
                S.act(lambda e: e.activation(out=L2, in_=t_a, func=AF.Exp), r=['t_a'], w=['L2'])
                S.dve(lambda e: e.tensor_copy(out=Lend, in_=L1[:, :, C - 1]), r=['L1'], w=['Lend'])
                S.dve(lambda e: e.tensor_tensor(out=kap, in0=fk, in1=vec1[:, 2, :].unsqueeze(2).broadcast_to([64, 16, C]),
                                                op=ALU.mult), r=['fk', 'vec1'], w=['fw'])
                S.dve(lambda e: e.tensor_tensor(out=t_b, in0=kap, in1=kap, op=ALU.mult), r=['fw'], w=['t_b'])
                def ssqm(e):
                    e.matmul(PS(0, 0, 64, 0, 512), lhsT=blk1, rhs=t_b[:, 0:8, :].rearrange("p a t -> p (a t)"), start=True, stop=True)
                    return e.matmul(PS(1, 0, 64, 0, 512), lhsT=blk1, rhs=t_b[:, 8:16, :].rearrange("p a t -> p (a t)"),
                                    start=True, stop=True)
                S.pe(ssqm, r=['t_b', 'l1c'], w=[pk(0), pk(1)])
                for hb in range(2):
                    S.dve(lambda e, hb=hb: e.tensor_scalar(out=t_b[:, hb * 8:(hb + 1) * 8, :].rearrange("p a t -> p (a t)"),
                                                           in0=PS(hb, 0, 64, 0, 512), scalar1=1e-24, scalar2=None, op0=ALU.max),
                          r=[pk(hb)], w=['t_b'])
                S.act(lambda e: e.activation(out=t_b, in_=t_b, func=AF.Ln), r=['t_b'], w=['t_b'])
                S.act(lambda e: e.activation(out=t_b, in_=t_b, func=AF.Exp, scale=-0.5), r=['t_b'], w=['t_b'])
                S.dve(lambda e: e.tensor_tensor(out=kap, in0=kap, in1=t_b, op=ALU.mult), r=['fw', 't_b'], w=['fw'])
                S.dve(lambda e: e.tensor_scalar(out=t_a, in0=fa, scalar1=-1.0, scalar2=None, op0=ALU.add), r=['fa', 'L2'], w=['t_a'])
                S.dve(lambda e: e.tensor_tensor(out=t_a, in0=t_a, in1=vec1[:, 3, :].unsqueeze(2).broadcast_to([64, 16, C]),
                                                op=ALU.mult), r=['t_a', 'vec1'], w=['t_a'])
                S.dve(lambda e: e.scalar_tensor_tensor(out=kmd, in0=t_a, scalar=1.0, in1=fk, op0=ALU.add, op1=ALU.mult),
                      r=['t_a', 'fk'], w=['kmd'])
                S.dve(lambda e: e.tensor_tensor(out=kt_b, in0=kap, in1=L2, op=ALU.mult), r=['fw', 'L2'], w=['kt_b'])
                S.dve(lambda e: e.tensor_tensor(out=t_a, in0=kap, in1=fa, op=ALU.mult), r=['fw', 'fa', 'kmd'], w=['t_a'])
                S.dve(lambda e: e.tensor_tensor(out=t_a, in0=t_a, in1=L3, op=ALU.mult), r=['t_a', 'L3'], w=['t_a'])
                S.act(lambda e: e.copy(out=bt_b, in_=t_a), r=['t_a'], w=['bt_b'])
                S.dve(lambda e: e.tensor_tensor(out=bh_f, in0=t_a, in1=Lend.unsqueeze(2).broadcast_to([64, 16, C]), op=ALU.mult),
                      r=['t_a', 'Lend'], w=['bh_f'])
                S.dve(lambda e: e.tensor_tensor(out=t_b, in0=kmd, in1=L3, op=ALU.mult), r=['kmd', 'L3', 'fw'], w=['t_b'])
                S.act(lambda e: e.copy(out=ktl_b, in_=t_b), r=['t_b'], w=['ktl_b'])
                S.dve(lambda e: e.tensor_tensor(out=kh_f, in0=t_b, in1=Lend.unsqueeze(2).broadcast_to([64, 16, C]), op=ALU.mult),
                      r=['t_b', 'Lend'], w=['kh_f'])
                S.dve(lambda e: e.tensor_tensor(out=rt_b, in0=fr, in1=L1, op=ALU.mult), r=['fr', 'L1'], w=['rt_b'])
                S.dve(lambda e: e.tensor_tensor(out=t_b, in0=fr, in1=kmd, op=ALU.mult), r=['fr', 'kmd', 'ktl_b', 'kh_f'], w=['t_b'])
                S.dve(lambda e: e.tensor_tensor(out=pr_b, in0=t_b, in1=vec1[:, 4, :].unsqueeze(2).broadcast_to([64, 16, C]),
                                                op=ALU.mult), r=['t_b', 'vec1'], w=['pr_b'])
                for (src, dst_, nm, bank) in ((bh_f, Bh, 'Bh', 2), (kh_f, Kh, 'Kh', 3)):
                    def trb(e, src=src, bank=bank):
                        ins = None
                        for h in range(16):
                            ins = e.transpose(out=PSB(bank, 0, 64, h * 64, (h + 1) * 64), in_=src[:, h, :],
                                              identity=ident_b[0:64, 0:64])
                        return ins
                    S.pe(trb, r=[nm.lower() + '_f' if False else ('bh_f' if nm == 'Bh' else 'kh_f'), 'ident_b'], w=[pk(bank)])
                    S.act(lambda e, dst_=dst_, bank=bank: e.copy(out=dst_, in_=PSB(bank, 0, 64, 0, 1024)), r=[pk(bank)], w=[nm])

                def rkm(e):
                    ins = None
                    for h in range(16):
                        ins = e.matmul(PS(1, 0, 64, 2 * h, 2 * h + 1), lhsT=pr_b[:, h, :], rhs=sel2, start=True, stop=True)
                    return ins
                S.pe(rkm, r=['pr_b', 'l1c'], w=[pk(1)])
                S.act(lambda e: e.copy(out=rkb, in_=PS(1, 0, 64, 0, 32).rearrange("p (h two) -> p h two", two=2)[:, :, 0]),
                      r=[pk(1)], w=[krkb])
                def gram(lt, rt, dst_, mask, banks, nm, deps):
                    def gm(e):
                        ins = None
                        for h in range(16):
                            ins = e.matmul(PS(banks[h // 8], 0, 64, (h % 8) * 64, (h % 8 + 1) * 64), lhsT=hp_rows(lt, h),
                                           rhs=hp_rows(rt, h), start=True, stop=True)
                        return ins
                    S.pe(gm, r=deps, w=[pk(banks[0]), pk(banks[1])])
                    for hb in range(2):
                        S.dve(lambda e, hb=hb: e.tensor_tensor(
                            out=dst_[:, hb * 8:(hb + 1) * 8, :], in0=PS(banks[hb], 0, 64, 0, 512).rearrange("p (h t) -> p h t", h=8),
                            in1=mask.unsqueeze(1).broadcast_to([64, 8, 64]), op=ALU.mult),
                            r=[pk(banks[hb]), 'l1c'], w=[(nm, hb)])
                gram(bt_b, kt_b, gN, m_lt, (4, 5), 'gN', ['bt_b', 'kt_b'])
                gram(kt_b, bt_b, gNT, m_gt, (6, 7), 'gNT', ['bt_b', 'kt_b'])
                gram(ktl_b, kt_b, gAk, m_lt, (4, 5), 'gAk', ['ktl_b', 'kt_b'])
                gram(bt_b, rt_b, gBb, m_le, (6, 7), 'gBb', ['bt_b', 'rt_b'])
                gram(ktl_b, rt_b, gBk, m_le, (4, 5), 'gBk', ['ktl_b', 'rt_b'])
                for hb in range(2):
                    S.dve(lambda e, hb=hb: e.scalar_tensor_tensor(
                        out=gX[0][:, hb * 8:(hb + 1) * 8, :], in0=gN[:, hb * 8:(hb + 1) * 8, :], scalar=-1.0,
                        in1=ident_b[0:64, 0:64].unsqueeze(1).broadcast_to([64, 8, 64]), op0=ALU.mult, op1=ALU.add),
                        r=[('gN', hb), 'ident_b'], w=[('gX', 0, hb)])
                Pc, PTc, Pk, PTk = gN, gNT, 'gN', 'gNT'
                xi = 0
                for lev in range(5):
                    Pn, PTn = gP[lev % 2], gPT[lev % 2]
                    Pnk, PTnk = ('gP', lev % 2), ('gPT', lev % 2)

                    def sq(e, Pc=Pc, PTc=PTc):
                        ins = None
                        for h in range(16):
                            ins = e.matmul(PS(h // 8, 0, 64, (h % 8) * 64, (h % 8 + 1) * 64), lhsT=PTc[:, h, :], rhs=Pc[:, h, :],
                                           start=True, stop=True)
                        for h in range(16):
                            ins = e.matmul(PS(2 + h // 8, 0, 64, (h % 8) * 64, (h % 8 + 1) * 64), lhsT=Pc[:, h, :],
                                           rhs=PTc[:, h, :], start=True, stop=True)
                        return ins
                    S.pe(sq, r=[(Pk, 0), (Pk, 1), (PTk, 0), (PTk, 1)] if isinstance(Pk, str) else
                         [Pk + (0,), Pk + (1,), PTk + (0,), PTk + (1,)], w=[pk(0), pk(1), pk(2), pk(3)])
                    for hb in range(2):
                        S.act(lambda e, hb=hb, Pn=Pn: e.copy(out=Pn[:, hb * 8:(hb + 1) * 8, :].rearrange("p h t -> p (h t)"),
                                                             in_=PS(hb, 0, 64, 0, 512)), r=[pk(hb)], w=[Pnk + (hb,)])
                        S.dve(lambda e, hb=hb, PTn=PTn: e.tensor_copy(out=PTn[:, hb * 8:(hb + 1) * 8, :].rearrange("p h t -> p (h t)"),
                                                                      in_=PS(2 + hb, 0, 64, 0, 512)), r=[pk(2 + hb)], w=[PTnk + (hb,)])
                    Xo, Xn = gX[xi % 2], gX[(xi + 1) % 2]

                    def xm(e, PTn=PTn, Xo=Xo):
                        ins = None
                        for h in range(16):
                            ins = e.matmul(PS(4 + h // 8, 0, 64, (h % 8) * 64, (h % 8 + 1) * 64), lhsT=PTn[:, h, :], rhs=Xo[:, h, :],
                                           start=True, stop=True)
                        return ins
                    S.pe(xm, r=[PTnk + (0,), PTnk + (1,), ('gX', xi % 2, 0), ('gX', xi % 2, 1)], w=[pk(4), pk(5)])
                    for hb in range(2):
                        S.dve(lambda e, hb=hb, Xo=Xo, Xn=Xn: e.tensor_tensor(
                            out=Xn[:, hb * 8:(hb + 1) * 8, :].rearrange("p h t -> p (h t)"), in0=PS(4 + hb, 0, 64, 0, 512),
                            in1=Xo[:, hb * 8:(hb + 1) * 8, :].rearrange("p h t -> p (h t)"), op=ALU.add),
                            r=[pk(4 + hb), ('gX', xi % 2, hb)], w=[('gX', (xi + 1) % 2, hb)])
                    xi += 1
                    Pc, PTc, Pk, PTk = Pn, PTn, Pnk, PTnk
                Xf = gX[xi % 2]
                XFK = [('gX', xi % 2, 0), ('gX', xi % 2, 1)]
                yield 'pre'
                if first is not None:
                    first()

                def m1(e):
                    ins = None
                    for h in range(16):
                        o_ = PS(h // 8, 0, 64, (h % 8) * 64, (h % 8 + 1) * 64)
                        e.matmul(o_, lhsT=hp_rows(kt_b, h), rhs=hp_rows(STb, h), start=True, stop=False)
                        ins = e.matmul(o_, lhsT=gAk[:, h, :], rhs=Vb[:, h * 64:(h + 1) * 64], start=False, stop=True)
                    return ins
                S.pe(m1, r=['kt_b', 'STb', ('gAk', 0), ('gAk', 1), 'Vb'], w=[pk(0), pk(1)])
                for hb in range(2):
                    S.act(lambda e, hb=hb: e.activation(out=Rm[:, hb * 512:(hb + 1) * 512], in_=PS(hb, 0, 64, 0, 512),
                                                        func=AF.Copy, scale=-1.0), r=[pk(hb)], w=[('Rm', hb)])

                def m3(e):
                    ins = None
                    for h in range(16):
                        ins = e.matmul(PS(2 + h // 8, 0, 64, (h % 8) * 64, (h % 8 + 1) * 64), lhsT=Xf[:, h, :],
                                       rhs=Rm[:, h * 64:(h + 1) * 64], start=True, stop=True)
                    return ins
                S.pe(m3, r=XFK + [('Rm', 0), ('Rm', 1)], w=[pk(2), pk(3)])
                for hb in range(2):
                    S.act(lambda e, hb=hb: e.copy(out=Ub[:, hb * 512:(hb + 1) * 512], in_=PS(2 + hb, 0, 64, 0, 512)),
                          r=[pk(2 + hb)], w=[('Ub', hb)])

                def m4(e):
                    ins = None
                    for h in range(16):
                        o_ = PS(4 + h // 8, 0, 64, (h % 8) * 64, (h % 8 + 1) * 64)
                        e.matmul(o_, lhsT=hp_rows(rt_b, h), rhs=hp_rows(STb, h), start=True, stop=False)
                        e.matmul(o_, lhsT=gBb[:, h, :], rhs=Ub[:, h * 64:(h + 1) * 64], start=False, stop=False)
                        ins = e.matmul(o_, lhsT=gBk[:, h, :], rhs=Vb[:, h * 64:(h + 1) * 64], start=False, stop=True)
                    return ins
                S.pe(m4, r=['rt_b', 'STb', ('gBb', 0), ('gBb', 1), ('gBk', 0), ('gBk', 1), ('Ub', 0), ('Ub', 1), 'Vb'],
                     w=[pk(4), pk(5)])
                for hb in range(2):
                    S.act(lambda e, hb=hb: e.copy(out=Yf[:, hb * 512:(hb + 1) * 512], in_=PS(4 + hb, 0, 64, 0, 512)),
                          r=[pk(4 + hb)], w=[('Yf', hb)])

                def m5(e):
                    ins = None
                    for h in range(16):
                        o_ = PS(6 + h // 8, 0, 64, (h % 8) * 64, (h % 8 + 1) * 64)
                        e.matmul(o_, lhsT=Bh[:, h * 64:(h + 1) * 64], rhs=Ub[:, h * 64:(h + 1) * 64], start=True, stop=False)
                        ins = e.matmul(o_, lhsT=Kh[:, h * 64:(h + 1) * 64], rhs=Vb[:, h * 64:(h + 1) * 64], start=False, stop=True)
                    return ins
                S.pe(m5, r=['Bh', 'Kh', ('Ub', 0), ('Ub', 1), 'Vb'], w=[pk(6), pk(7)])
                S.dve(lambda e: e.tensor_tensor(out=ST, in0=ST, in1=Lend.unsqueeze(2).broadcast_to([64, 16, 64]), op=ALU.mult),
                      r=['ST', 'Lend', 'STb'], w=['ST'])
                for hb in range(2):
                    S.dve(lambda e, hb=hb: e.tensor_tensor(out=ST[:, hb * 8:(hb + 1) * 8, :], in0=ST[:, hb * 8:(hb + 1) * 8, :],
                                                           in1=PS(6 + hb, 0, 64, 0, 512).rearrange("p (a v) -> p a v", a=8), op=ALU.add),
                          r=['ST', pk(6 + hb)], w=['ST'])
                S.act(lambda e: e.copy(out=STb, in_=ST), r=['ST'], w=['STb'])
                if post_seq is not None:
                    post_seq()
                yield 'seq'
                Y3 = Yf.rearrange("p (h v) -> p h v", h=16)
                Yc3 = Yc.rearrange("p (h v) -> p h v", h=16)
                YK = [('Yf', 0), ('Yf', 1)]
                S.dve(lambda e: e.tensor_reduce(out=st1, in_=Y3, axis=AX.X, op=ALU.add), r=YK, w=['st1'])
                S.dve(lambda e: e.tensor_scalar(out=st1, in0=st1, scalar1=1.0 / 64.0, scalar2=None, op0=ALU.mult), r=['st1'], w=['st1'])
                S.dve(lambda e: e.tensor_tensor(out=Yc3, in0=Y3, in1=st1.unsqueeze(2).broadcast_to([64, 16, 64]), op=ALU.subtract),
                      r=YK + ['st1'], w=['Yc'])
                S.pool(lambda e: e.tensor_tensor(out=Yf, in0=Yc, in1=Yc, op=ALU.mult), r=['Yc'] + YK, w=YK)
                S.dve(lambda e: e.tensor_reduce(out=st2, in_=Y3, axis=AX.X, op=ALU.add), r=YK, w=['st2'])
                S.dve(lambda e: e.tensor_scalar(out=st2, in0=st2, scalar1=1.0 / 64.0, scalar2=64e-5, op0=ALU.mult, op1=ALU.add),
                      r=['st2'], w=['st2'])
                S.act(lambda e: e.activation(out=st2, in_=st2, func=AF.Ln), r=['st2'], w=['st2'])
                S.act(lambda e: e.activation(out=st2, in_=st2, func=AF.Exp, scale=-0.5), r=['st2'], w=['st2'])
                S.dve(lambda e: e.tensor_tensor(out=Yc3, in0=Yc3, in1=st2.unsqueeze(2).broadcast_to([64, 16, 64]), op=ALU.mult),
                      r=['Yc', 'st2'], w=['Yc'])
                S.dve(lambda e: e.tensor_tensor(out=Yc, in0=Yc, in1=gnp[:, 0, :], op=ALU.mult), r=['Yc', ('gnp', 0)], w=['Yc'])
                S.pool(lambda e: e.tensor_tensor(out=Yc, in0=Yc, in1=gnp[:, 1, :], op=ALU.add), r=['Yc', ('gnp', 1)], w=['Yc'])
                S.dve(lambda e: e.tensor_tensor(out=Y3, in0=Vf.rearrange("p (h v) -> p h v", h=16),
                                                in1=rkb.unsqueeze(2).broadcast_to([64, 16, 64]), op=ALU.mult),
                      r=[kVf, krkb] + YK, w=YK)
                S.pool(lambda e: e.tensor_tensor(out=Yc, in0=Yc, in1=Yf, op=ALU.add), r=['Yc'] + YK, w=['Yc'])
                S.dve(lambda e: e.tensor_tensor(out=gb, in0=Yc, in1=Zf, op=ALU.mult), r=['Yc', kZf], w=['gb'])
                yield 'tailA'
                def trg(e):
                    ins = None
                    for c in range(8):
                        ins = e.transpose(out=PSB(7, 0, 128, c * 64, (c + 1) * 64), in_=gb[:, c * 128:(c + 1) * 128],
                                          identity=ident_b[0:64, 0:64])
                    return ins
                S.pe(trg, r=['gb', 'ident_b'], w=[pk(7)])
                S.act(lambda e: e.copy(out=gT.rearrange("p c t -> p (c t)"), in_=PSB(7, 0, 128, 0, 512)), r=[pk(7)], w=['gT'])

                def mo(e):
                    ins = None
                    for cb in range(2):
                        for c in range(8):
                            ins = e.matmul(PS(cb, 0, 64, 0, 512), lhsT=gT[:, c, :], rhs=wo_o[:, c, cb * 512:(cb + 1) * 512],
                                           start=(c == 0), stop=(c == 7))
                    return ins
                S.pe(mo, r=['gT'] + WOO, w=[pk(0), pk(1)])
                for cb in range(2):
                    S.dve(lambda e, cb=cb: e.scalar_tensor_tensor(
                        out=xr[:, cb * 512:(cb + 1) * 512], in0=xr[:, cb * 512:(cb + 1) * 512], scalar=ALPHA,
                        in1=PS(cb, 0, 64, 0, 512), op0=ALU.mult, op1=ALU.add), r=[kxr, pk(cb)], w=[kxr])
                    S.dve(lambda e, cb=cb: e.bn_stats(out=bst1[:, cb, :], in_=xr[:, cb * 512:(cb + 1) * 512]), r=[kxr], w=[('bst1', cb)])
                S.dve(lambda e: e.bn_aggr(out=mv1[:, 0:2], in_=bst1.rearrange("p a b -> p (a b)")), r=[('bst1', 0), ('bst1', 1)],
                      w=['mv1'])
                S.dve(lambda e: e.tensor_scalar(out=mv1[:, 2:3], in0=mv1[:, 1:2], scalar1=1e-5, scalar2=None, op0=ALU.add),
                      r=['mv1'], w=['mv1b'])
                S.act(lambda e: e.activation(out=mv1[:, 2:3], in_=mv1[:, 2:3], func=AF.Ln), r=['mv1b'], w=['mv1b'])
                S.act(lambda e: e.activation(out=mv1[:, 2:3], in_=mv1[:, 2:3], func=AF.Exp, scale=-0.5), r=['mv1b'], w=['mv1b'])
                S.dve(lambda e: e.tensor_scalar(out=xr, in0=xr, scalar1=mv1[:, 0:1], scalar2=mv1[:, 2:3], op0=ALU.subtract,
                                                op1=ALU.mult), r=[kxr, 'mv1', 'mv1b'], w=[kxr])
                S.dve(lambda e: e.tensor_tensor(out=xr, in0=xr, in1=gnp[:, 2, :], op=ALU.mult), r=[kxr, ('gnp', 2)], w=[kxr])
                S.pool(lambda e: e.tensor_tensor(out=xr, in0=xr, in1=gnp[:, 3, :], op=ALU.add), r=[kxr, ('gnp', 3)], w=[kxr])
                S.dma('sp', lambda e: e.dma_start(out=yout, in_=(xr[C - 1:C, :] if pad else xr)), r=[kxr])

            def write_state(dst):
                def trs(e):
                    ins = None
                    for h in range(16):
                        ins = e.transpose(out=PS(6 + h // 8, 0, 64, (h % 8) * 64, (h % 8 + 1) * 64), in_=ST[:, h, :],
                                          identity=ident_f[0:64, 0:64])
                    return ins
                S.pe(trs, r=['ST', 'ident_f'], w=[pk(6), pk(7)])
                for hb in range(2):
                    S.act(lambda e, hb=hb: e.copy(out=wko[:, hb * 8:(hb + 1) * 8, :].rearrange("p a b -> p (a b)"),
                                                  in_=PS(6 + hb, 0, 64, 0, 512)), r=[pk(6 + hb)], w=['wko'])
                S.dma('sp', lambda e: e.dma_start(out=dst.rearrange("(h v) k -> v h k", h=16), in_=wko), r=['wko'])

            def zero_state():
                S.pool(lambda e: e.memset(ST, 0.0), w=['ST'])
                S.pool(lambda e: e.memset(STb, 0.0), w=['STb'])

            def mk_load_state(s):
                def load_state():
                    S.dma('sp', lambda e: e.dma_start(
                        out=stin2, in_=I['state_wkv'][s * 1024:(s + 1) * 1024, :].rearrange("(pr p) k -> p pr k", pr=8)), w=['stin'])

                    def tri(e):
                        ins = None
                        for pr_ in range(8):
                            ins = e.transpose(out=PS(6 + pr_ // 4, 0, 64, (pr_ % 4) * 128, (pr_ % 4 + 1) * 128), in_=stin2[:, pr_, :],
                                              identity=ident_f)
                        return ins
                    S.pe(tri, r=['stin', 'ident_f'], w=[pk(6), pk(7)])
                    for hb in range(2):
                        S.dve(lambda e, hb=hb: e.tensor_copy(out=ST[:, hb * 8:(hb + 1) * 8, :].rearrange("p a v -> p (a v)"),
                                                             in_=PS(6 + hb, 0, 64, 0, 512)), r=[pk(6 + hb)], w=['ST'])
                    S.act(lambda e: e.copy(out=STb, in_=ST), r=['ST'], w=['STb'])
                return load_state

            gens = []
            nchk = T // C
            for ci in range(nchk):
                gens.append(chunk(ci * C, C, zero_state if ci == 0 else None, x1_scr[ci * C:(ci + 1) * C, :],
                                  O['y_prompt'][ci * C:(ci + 1) * C, :], C, ci, ci % 2,
                                  (lambda: write_state(O['wkv_p'])) if ci == nchk - 1 else None))
            for s in range(NS):
                gens.append(chunk(T + s, 1, mk_load_state(s), x1_scr[T + s:T + s + 1, :], O['y_sample'][s:s + 1, :], 1, 100 + s,
                                  (nchk + s) % 2, (lambda s=s: write_state(O['wkv_s'][s * 1024:(s + 1) * 1024, :]))))
            next(gens[0])
            next(gens[0])
            for i_ in range(1, len(gens)):
                next(gens[i_])
                next(gens[i_ - 1])
                next(gens[i_])
                for _ in gens[i_ - 1]:
                    pass
            next(gens[-1])
            for _ in gens[-1]:
                pass
            S.barrier()
            A.release()
            stage_end(13)
        except _Stop:
            pass
        S.analyze()
        S.emit(nc, sems, dsems)
        print("arena peak bytes", A.peak, "ops", len(S.ops))
    return nc


_NC_CACHE = {}


def kernel(**inputs):
    inp = {k: np.asarray(v) for k, v in inputs.items()}
    if 'nc' not in _NC_CACHE:
        _NC_CACHE['nc'] = build()
    nc = _NC_CACHE['nc']
    f = np.ascontiguousarray
    shared = {
        "cache_cmp_a": f(inp['cache_cmp_kv'][0, :NPOOL // 2].reshape(NPOOL * 64, 256)),
        "cache_cmp_b": f(inp['cache_cmp_kv'][0, NPOOL // 2:].reshape(NPOOL * 64, 256)),
        "cache_sel_a": f(inp['cache_sel_kv'][0, :NPOOL // 2].reshape(NPOOL * 64, 256)),
        "cache_sel_b": f(inp['cache_sel_kv'][0, NPOOL // 2:].reshape(NPOOL * 64, 256)),
        "w_in": f(inp['w_in_even'][0]),
        "conv_w": f(inp['conv_w'][0].reshape(31, 512)),
        "conv_b": f(inp['conv_b'].reshape(1, 512)),
        "conv_ln_g": f(inp['conv_ln_g'].reshape(1, 512)),
        "conv_ln_b": f(inp['conv_ln_b'].reshape(1, 512)),
        "wk_cmp": f(inp['wk_cmp'][0]),
        "wv_cmp": f(inp['wv_cmp'][0]),
        "w_out_even": f(inp['w_out_even'][0]),
        "mu_c": f(inp['mu_c'][0]),
        "w_rkvz": f(inp['w_rkvz'][0].reshape(4 * DM, DM)),
        "w0": f(inp['w0'].reshape(1, DM)),
        "w1": f(inp['w1'][0]),
        "w2": f(inp['w2'][0]),
        "a0": f(inp['a0'].reshape(1, DM)),
        "a1": f(inp['a1'][0]),
        "a2": f(inp['a2'][0]),
        "k_k": f(inp['k_k'].reshape(1, DM)),
        "k_a": f(inp['k_a'].reshape(1, DM)),
        "r_k": f(inp['r_k'].reshape(1, DM)),
        "gn_g": f(inp['gn_g'].reshape(1, DM)),
        "gn_b": f(inp['gn_b'].reshape(1, DM)),
        "w_out_odd": f(inp['w_out_odd'][0]),
        "ln_g": f(inp['ln_g']),
        "ln_b": f(inp['ln_b']),
    }
    in_maps = []
    for c in range(NCORES):
        s0, s1 = c * NS, (c + 1) * NS
        m = dict(shared)
        m["xp"] = f(inp['x_prompt'][c])
        m["xs"] = f(inp['x_sample'][s0:s1, 0, :])
        m["cache_win"] = f(inp['cache_win_kv'][0, s0:s1].reshape(NS * 512, 256))
        m["state_conv"] = f(inp['state_conv'][0, s0:s1].reshape(NS * 30, 512))
        m["state_wkv"] = f(inp['state_wkv'][0, s0:s1].reshape(NS * 16 * 64, 64))
        m["state_shift"] = f(inp['state_shift'][0, s0:s1])
        m["page_table"] = f(inp['page_table'][s0:s1].astype(np.int32))
        in_maps.append(m)
    res = run_bass_kernel_spmd(nc, in_maps, core_ids=list(range(NCORES)), trace=bool(os.environ.get('KDEV_TRACE')))
    if os.environ.get('KDEV_TRACE'):
        print('EXEC_TIME_NS', res.exec_time_ns)
    R = res.results

    def cat(name):
        return np.stack([np.asarray(R[c][name]) for c in range(NCORES)], axis=0)
    y_prompt = cat("y_prompt").reshape(8, T, DM)
    y_sample = cat("y_sample").reshape(32, 1, DM)
    cmp_p = cat("cmp_p").reshape(1, 8, T, 2, 2, 64)
    cmp_s = cat("cmp_s").reshape(1, 32, 1, 2, 2, 64)
    sel_p = cat("sel_p").reshape(1, 8, T, 2, 2, 64)
    sel_s = cat("sel_s").reshape(1, 32, 1, 2, 2, 64)
    win_p = cat("win_p").reshape(1, 8, 512, 2, 2, 64)
    win_s = cat("win_s").reshape(1, 32, 512, 2, 2, 64)
    conv_p = cat("conv_p").reshape(1, 8, 30, 512)
    conv_s = cat("conv_s").reshape(1, 32, 30, 512)
    wkv_p = cat("wkv_p").reshape(1, 8, 16, 64, 64)
    wkv_s = cat("wkv_s").reshape(1, 32, 16, 64, 64)
    shift_p = cat("shift_p").reshape(1, 8, DM)
    shift_s = cat("shift_s").reshape(1, 32, DM)
    return (y_prompt, y_sample, cmp_p, cmp_s, sel_p, sel_s, win_p, win_s, conv_p, conv_s,
            wkv_p, wkv_s, shift_p, shift_s)
```

```python
import numpy as np
import concourse.bass as bass
import concourse.mybir as mybir
from concourse.bass_utils import run_bass_kernel_spmd

F32 = mybir.dt.float32
BF16 = mybir.dt.bfloat16
I32 = mybir.dt.int32
AF = mybir.ActivationFunctionType
ALU = mybir.AluOpType
AX = mybir.AxisListType

NCORES = 8
T = 2048
NS = 4
TT = T + NS
DM = 1024
import os
NPOOL = int(os.environ.get('KDEV_NPOOL', '2560'))
STAGE = int(os.environ.get('KDEV_STAGE', '99'))


class _Stop(Exception):
    pass


def stage_end(k):
    if STAGE == k:
        raise _Stop()
ENGS = ('pe', 'act', 'dve', 'pool', 'sp')
SAME_ENG_DIST = 3
NDS = {'sp': 8, 'act': 4, 'pool': 6}


class Op:
    __slots__ = ('eng', 'fn', 'r', 'w', 'dma', 'deps', 'signal', 'count', 'dsem', 'dcount',
                 'dprev', 'waits', 'idx', 'eidx', 'bar', 'need')


class Sched:
    def __init__(self):
        self.ops = []
        self.nbar = 0

    def add(self, eng, fn, r=(), w=(), dma=False):
        op = Op()
        op.eng, op.fn, op.r, op.w, op.dma = eng, fn, tuple(r), tuple(w), dma
        op.signal = False
        op.bar = None
        op.count = 0
        self.ops.append(op)
        return op

    def pe(self, fn, r=(), w=()):
        return self.add('pe', fn, r, w)

    def act(self, fn, r=(), w=()):
        return self.add('act', fn, r, w)

    def dve(self, fn, r=(), w=()):
        return self.add('dve', fn, r, w)

    def pool(self, fn, r=(), w=()):
        return self.add('pool', fn, r, w)

    def dma(self, q, fn, r=(), w=()):
        return self.add(q, fn, r, w, dma=True)

    def barrier(self):
        self.nbar += 1
        for e in ENGS:
            op = self.add(e, None)
            op.bar = self.nbar

    def analyze(self):
        ops = self.ops
        last_w = {}
        rd = {}
        eng_ops = {e: [] for e in ENGS}
        last_on_eng = {}
        all_dmas = []
        cur_bar = None
        snap = None
        for i, op in enumerate(ops):
            op.idx = i
            deps = set()
            if op.bar is not None:
                if cur_bar != op.bar:
                    cur_bar = op.bar
                    snap = (dict(last_on_eng), list(all_dmas))
                    all_dmas = []
                for f, j in snap[0].items():
                    if f != op.eng and ops[j].bar is None:
                        deps.add(j)
                deps.update(snap[1])
            else:
                for k in op.r:
                    j = last_w.get(k)
                    if j is not None:
                        deps.add(j)
                for k in op.w:
                    j = last_w.get(k)
                    if j is not None:
                        deps.add(j)
                    rr = rd.get(k)
                    if rr:
                        deps.update(rr[0].values())
                        deps.update(rr[1])
                for k in op.r:
                    rr = rd.setdefault(k, ({}, []))
                    if op.dma:
                        rr[1].append(i)
                    else:
                        rr[0][op.eng] = i
                for k in op.w:
                    last_w[k] = i
                    rd[k] = ({}, [])
            deps.discard(i)
            op.deps = deps
            op.eidx = len(eng_ops[op.eng])
            eng_ops[op.eng].append(op)
            last_on_eng[op.eng] = i
            if op.dma:
                all_dmas.append(i)
        for op in ops:
            need = []
            for j in op.deps:
                d = ops[j]
                if d.dma:
                    need.append(j)
                elif d.eng == op.eng:
                    if op.eng == 'pe' and not op.dma:
                        continue
                    if op.eidx - d.eidx > SAME_ENG_DIST:
                        continue
                    need.append(j)
                else:
                    need.append(j)
            op.need = need
            for j in need:
                if not ops[j].dma:
                    ops[j].signal = True
        self.eng_ops = eng_ops

    def emit(self, nc, sems, dsems):
        ops = self.ops
        eng_ops = self.eng_ops
        for e in ENGS:
            cnt = 0
            for op in eng_ops[e]:
                if op.signal:
                    cnt += 1
                    op.count = cnt
        print('SEMCOUNTS', {e: max([op.count for op in eng_ops[e]] + [0]) for e in ENGS}, {e: len(eng_ops[e]) for e in ENGS})
        finals = {}
        for q, pool in dsems.items():
            uses = [0] * len(pool)
            k = 0
            for op in eng_ops[q]:
                if op.dma:
                    s = k % len(pool)
                    k += 1
                    op.dsem = pool[s]
                    op.dprev = 16 * uses[s]
                    uses[s] += 1
                    op.dcount = 16 * uses[s]
                    finals[id(pool[s])] = (pool[s], op.dcount)
        for e in ENGS:
            waited = {}
            for op in eng_ops[e]:
                ws = []
                for j in sorted(op.need):
                    d = ops[j]
                    if d.dma:
                        sem, val = d.dsem, d.dcount
                    else:
                        sem, val = sems[d.eng], d.count
                    if waited.get(id(sem), 0) >= val:
                        continue
                    waited[id(sem)] = val
                    ws.append((sem, val))
                if op.dma and op.dprev > 0:
                    if waited.get(id(op.dsem), 0) < op.dprev:
                        waited[id(op.dsem)] = op.dprev
                        ws.append((op.dsem, op.dprev))
                op.waits = ws

        def make(e):
            def body(eo):
                for op in eng_ops[e]:
                    for (sem, v) in op.waits:
                        eo.wait_ge(sem, v)
                    if op.fn is not None:
                        ins = op.fn(eo)
                        if op.dma:
                            ins.then_inc(op.dsem, 16)
                        elif op.signal:
                            ins.then_inc(sems[e], 1)
                if e == 'sp':
                    for (sem, v) in finals.values():
                        eo.wait_ge(sem, v)
            return body

        with nc.Block() as block:
            block.tensor(make('pe'))
            block.scalar(make('act'))
            block.vector(make('dve'))
            block.gpsimd(make('pool'))
            block.sync(make('sp'))


class Arena:
    def __init__(self, t32, nbytes):
        self.t32 = t32
        self.tbf = t32.bitcast(BF16)
        self.ti32 = t32.bitcast(I32)
        self.nbytes = nbytes
        self.top = 0
        self.marks = []
        self.peak = 0

    def alloc(self, shape, dt):
        es = 2 if dt == BF16 else 4
        n = 1
        for s in shape[1:]:
            n *= s
        nb = (n * es + 63) // 64 * 64
        off = self.top
        self.top += nb
        self.peak = max(self.peak, self.top)
        assert self.top <= self.nbytes, ("SBUF arena overflow", self.top, self.nbytes)
        base = {BF16: self.tbf, F32: self.t32, I32: self.ti32}[dt]
        v = base[0:shape[0], off // es: off // es + n]
        if len(shape) > 2:
            names = "abcdefg"[:len(shape) - 1]
            pat = "p (%s) -> p %s" % (" ".join(names), " ".join(names))
            v = v.rearrange(pat, **{names[i]: shape[1 + i] for i in range(len(shape) - 2)})
        return v

    def mark(self):
        self.marks.append(self.top)

    def release(self):
        self.top = self.marks.pop()


OUT_SPECS = [
    ("y_prompt", [T, DM]),
    ("y_sample", [NS, DM]),
    ("cmp_p", [T, 256]),
    ("cmp_s", [NS, 256]),
    ("sel_p", [T, 256]),
    ("sel_s", [NS, 256]),
    ("win_p", [512, 256]),
    ("win_s", [NS * 512, 256]),
    ("conv_p", [30, 512]),
    ("conv_s", [NS * 30, 512]),
    ("wkv_p", [16 * 64, 64]),
    ("wkv_s", [NS * 16 * 64, 64]),
    ("shift_p", [1, DM]),
    ("shift_s", [NS, DM]),
]

IN_SPECS = [
    ("xp", [T, DM], F32),
    ("xs", [NS, DM], F32),
    ("cache_cmp_k", [NPOOL * 128, 128], F32),
    ("cache_cmp_v", [NPOOL * 128, 128], F32),
    ("cache_sel_k", [NPOOL * 128, 128], F32),
    ("cache_sel_v", [NPOOL * 128, 128], F32),
    ("cache_win", [NS * 512, 256], F32),
    ("state_conv", [NS * 30, 512], F32),
    ("state_wkv", [NS * 16 * 64, 64], F32),
    ("state_shift", [NS, DM], F32),
    ("page_table", [NS, 64], I32),
    ("w_in", [DM, 3352], F32),
    ("conv_w", [31, 512], F32),
    ("conv_b", [1, 512], F32),
    ("conv_ln_g", [1, 512], F32),
    ("conv_ln_b", [1, 512], F32),
    ("wk_cmp", [32, 2], F32),
    ("wv_cmp", [32, 2], F32),
    ("w_out_even", [DM, DM], F32),
    ("mu_c", [6, DM], F32),
    ("w_rkvz", [4 * DM, DM], F32),
    ("w0", [1, DM], F32),
    ("w1", [DM, 64], F32),
    ("w2", [64, DM], F32),
    ("a0", [1, DM], F32),
    ("a1", [DM, 64], F32),
    ("a2", [64, DM], F32),
    ("k_k", [1, DM], F32),
    ("k_a", [1, DM], F32),
    ("r_k", [1, DM], F32),
    ("gn_g", [1, DM], F32),
    ("gn_b", [1, DM], F32),
    ("w_out_odd", [DM, DM], F32),
    ("ln_g", [2, DM], F32),
    ("ln_b", [2, DM], F32),
]

C_AVAL, C_AGLU, C_ZA, C_Q, C_KV, C_G3, C_ZB = 0, 512, 1024, 1536, 2048, 2816, 2840


def build(stage=99):
    nc = bass.Bass("TRN2", target_bir_lowering=False)
    I = {}
    for name, shape, dt in IN_SPECS:
        I[name] = nc.dram_tensor(name, shape, dt, kind="ExternalInput").ap()
    O = {}
    for name, shape in OUT_SPECS:
        O[name] = nc.dram_tensor(name, shape, F32, kind="ExternalOutput").ap()

    x1_scr = nc.dram_tensor("x1_scr", [TT, DM], F32, kind="Internal").ap()
    r_scr = nc.dram_tensor("r_scr", [128, 8, TT], F32, kind="Internal").ap()
    k_scr = nc.dram_tensor("k_scr", [128, 8, TT], F32, kind="Internal").ap()
    sw_scr = nc.dram_tensor("sw_scr", [128, 8, TT], F32, kind="Internal").ap()
    a_scr = nc.dram_tensor("a_scr", [128, 8, TT], F32, kind="Internal").ap()
    v_scr = nc.dram_tensor("v_scr", [TT, DM], F32, kind="Internal").ap()
    z_scr = nc.dram_tensor("z_scr", [TT, DM], F32, kind="Internal").ap()
    S = Sched()
    ARENA_BYTES = 176 * 1024
    from contextlib import ExitStack
    with ExitStack() as st:
        arena_t = st.enter_context(nc.sbuf_tensor("arena", [128, ARENA_BYTES // 4], F32))
        ps_t = st.enter_context(nc.psum_tensor("psum", [128, 4096], F32))
        sems = {e: st.enter_context(nc.semaphore("sem_" + e)) for e in ENGS}
        dsems = {q: [st.enter_context(nc.semaphore("dsem_%s%d" % (q, i))) for i in range(n)]
                 for q, n in NDS.items()}
        A = Arena(arena_t, ARENA_BYTES)
        ps_bf = ps_t.bitcast(BF16)
        try:

            def PS(b, p0=0, p1=128, c0=0, c1=512):
                return ps_t[p0:p1, b * 512 + c0: b * 512 + c1]

            def PSB(b, p0=0, p1=128, c0=0, c1=1024):
                return ps_bf[p0:p1, b * 1024 + c0: b * 1024 + c1]

            def pk(b):
                return ('ps', b)

            FR = {}

            def freg(e, v):
                if v not in FR:
                    FR[v] = e.to_reg(float(v))
                return FR[v]

            ident_f = A.alloc([128, 128], F32)
            ident_b = A.alloc([128, 128], BF16)
            ones_f = A.alloc([128, 128], F32)

            def mk_ident(e):
                e.memset(ones_f, 1.0)
                return e.affine_select(out=ident_f, in_=ones_f, pattern=[[-1, 128]], compare_op=ALU.is_equal,
                                       fill=freg(e, 0.0), base=0, channel_multiplier=1)
            S.pool(mk_ident, w=['ident_f', 'ones_f'])
            S.dve(lambda e: e.tensor_copy(out=ident_b, in_=ident_f), r=['ident_f'], w=['ident_b'])

            xT_off = A.top
            xT = A.alloc([128, 8, TT], BF16)
            xT_raw32 = arena_t[:, xT_off // 4: xT_off // 4 + 8208]
            L1_BASE = A.top
            yaT = A.alloc([128, 4, TT], BF16)
            Vsel_off = A.top
            Vsel = A.alloc([128, 16, 128], BF16)
            Vwin_off = A.top
            Vwin = A.alloc([128, 16, 128], BF16)
            Vsel_raw32 = arena_t[:, Vsel_off // 4: Vsel_off // 4 + 1024]
            Vwin_raw32 = arena_t[:, Vwin_off // 4: Vwin_off // 4 + 1024]
            kvsv = A.alloc([NS, 2, 128], F32)
            qs_f = A.alloc([68, 2, 4, NS], F32)
            ksn = A.alloc([68, 2, 2, NS], F32)
            gs_s = A.alloc([24, NS], BF16)
            w_in_t = I['w_in'].rearrange("(c p) f -> p c f", p=128)
            A.mark()
            xst = [A.alloc([128, DM], BF16) for _ in range(2)]
            xp_t = I['xp'].rearrange("(n p) d -> n p d", p=128)
            for n in range(16):
                b = n % 2
                S.dma('pool', lambda e, n=n, b=b: e.dma_start(out=xst[b], in_=xp_t[n]), w=[('xst', b)])
                bank = n % 2

                def tr(e, b=b, bank=bank):
                    ins = None
                    for c in range(8):
                        ins = e.transpose(out=PSB(bank, 0, 128, c * 128, (c + 1) * 128),
                                          in_=xst[b][:, c * 128:(c + 1) * 128], identity=ident_b)
                    return ins
                S.pe(tr, r=[('xst', b), 'ident_b'], w=[pk(bank)])
                S.dve(lambda e, n=n, bank=bank: e.tensor_copy(
                    out=xT[:, :, n * 128:(n + 1) * 128],
                    in_=PSB(bank).rearrange("p (c t) -> p c t", c=8)),
                    r=[pk(bank)], w=[('xT', n)])
            S.dma('pool', lambda e: e.dma_start(out=xst[0][0:NS, :], in_=I['xs']), w=[('xst', 0)])

            def trs(e):
                ins = None
                for c in range(8):
                    ins = e.transpose(out=PSB(0, 0, 128, c * NS, (c + 1) * NS),
                                      in_=xst[0][0:NS, c * 128:(c + 1) * 128], identity=ident_b[0:NS, 0:NS])
                return ins
            S.pe(trs, r=[('xst', 0), 'ident_b'], w=[pk(0)])
            S.dve(lambda e: e.tensor_copy(out=xT[:, :, T:TT],
                                          in_=PSB(0, 0, 128, 0, 8 * NS).rearrange("p (c t) -> p c t", c=8)),
                  r=[pk(0)], w=[('xT', 16)])
            XT_ALL = [('xT', n) for n in range(17)]
            S.barrier()
            A.release()
            stage_end(1)

            A.mark()
            wkv = A.alloc([128, 8, 768], BF16)
            kvst = [A.alloc([128, 768], F32) for _ in range(2)]
            winb = [A.alloc([128, 4, 256], F32) for _ in range(2)]
            for c in range(8):
                S.dma('pool', lambda e, c=c: e.dma_start(out=wkv[:, c, :], in_=w_in_t[:, c, C_KV:C_KV + 768]),
                      w=[('wkv', c)])
            WKV = [('wkv', c) for c in range(8)]
            for n in range(16):
                bA, bB = 4 + (n % 2) * 2, 5 + (n % 2) * 2

                def mmkv(e, n=n, bA=bA, bB=bB):
                    ins = None
                    for c in range(8):
                        ins = e.matmul(PS(bA), lhsT=xT[:, c, n * 128:(n + 1) * 128], rhs=wkv[:, c, 0:512],
                                       start=(c == 0), stop=(c == 7))
                    for c in range(8):
                        ins = e.matmul(PS(bB, 0, 128, 0, 256), lhsT=xT[:, c, n * 128:(n + 1) * 128],
                                       rhs=wkv[:, c, 512:768], start=(c == 0), stop=(c == 7))
                    return ins
                S.pe(mmkv, r=[('xT', n)] + WKV, w=[pk(bA), pk(bB)])
                sb = n % 2
                S.act(lambda e, sb=sb, bA=bA: e.copy(out=kvst[sb][:, 0:512], in_=PS(bA)),
                      r=[pk(bA)], w=[('kvst', sb, 0)])
                S.act(lambda e, sb=sb, bB=bB: e.copy(out=kvst[sb][:, 512:768], in_=PS(bB, 0, 128, 0, 256)),
                      r=[pk(bB)], w=[('kvst', sb, 1)])
                S.dma('sp', lambda e, n=n, sb=sb: e.dma_start(out=O['cmp_p'][n * 128:(n + 1) * 128, :],
                                                               in_=kvst[sb][:, 0:256]), r=[('kvst', sb, 0)])
                S.dma('sp', lambda e, n=n, sb=sb: e.dma_start(out=O['sel_p'][n * 128:(n + 1) * 128, :],
                                                               in_=kvst[sb][:, 256:512]), r=[('kvst', sb, 0)])
                if n >= 12:
                    S.dma('sp', lambda e, n=n, sb=sb: e.dma_start(
                        out=O['win_p'][(n - 12) * 128:(n - 11) * 128, :], in_=kvst[sb][:, 512:768]),
                        r=[('kvst', sb, 1)])
                S.pool(lambda e, n=n, sb=sb: e.tensor_copy(out=Vsel[:, n, :], in_=kvst[sb][:, 384:512]),
                       r=[('kvst', sb, 0)], w=[('Vsel', n)])
                S.pool(lambda e, n=n, sb=sb: e.tensor_copy(out=Vwin[:, n, :], in_=kvst[sb][:, 640:768]),
                       r=[('kvst', sb, 1)], w=[('Vwin', n)])
            stage_end(21)
            def mmkvs(e):
                ins = None
                for c in range(8):
                    ins = e.matmul(PS(4, 0, NS, 0, 512), lhsT=xT[:, c, T:TT], rhs=wkv[:, c, 0:512],
                                   start=(c == 0), stop=(c == 7))
                for c in range(8):
                    ins = e.matmul(PS(5, 0, NS, 0, 256), lhsT=xT[:, c, T:TT], rhs=wkv[:, c, 512:768],
                                   start=(c == 0), stop=(c == 7))
                return ins
            S.pe(mmkvs, r=[('xT', 16)] + WKV, w=[pk(4), pk(5)])
            S.act(lambda e: e.copy(out=kvst[0][0:NS, 0:512], in_=PS(4, 0, NS, 0, 512)), r=[pk(4)], w=[('kvst', 0, 0)])
            S.act(lambda e: e.copy(out=kvst[0][0:NS, 512:768], in_=PS(5, 0, NS, 0, 256)), r=[pk(5)], w=[('kvst', 0, 1)])
            S.dma('sp', lambda e: e.dma_start(out=O['cmp_s'], in_=kvst[0][0:NS, 0:256]), r=[('kvst', 0, 0)])
            S.dma('sp', lambda e: e.dma_start(out=O['sel_s'], in_=kvst[0][0:NS, 256:512]), r=[('kvst', 0, 0)])
            win_s_v = O['win_s'].rearrange("(s r) c -> s r c", r=512)
            S.dma('sp', lambda e: e.dma_start(out=win_s_v[:, 511, :], in_=kvst[0][0:NS, 512:768]), r=[('kvst', 0, 1)])
            S.pool(lambda e: e.tensor_copy(out=kvsv[0:NS, 0, :], in_=kvst[0][0:NS, 384:512]),
                   r=[('kvst', 0, 0)], w=['kvsv'])
            S.pool(lambda e: e.tensor_copy(out=kvsv[0:NS, 1, :], in_=kvst[0][0:NS, 640:768]),
                   r=[('kvst', 0, 1)], w=['kvsv'])
            stage_end(22)
            for s in range(NS):
                wb_ = winb[s % 2]
                src = I['cache_win'][s * 512:(s + 1) * 512, :].rearrange("(p j) c -> p j c", j=4)
                dst = O['win_s'][s * 512:(s + 1) * 512, :].rearrange("(p j) c -> p j c", j=4)
                S.dma('sp', lambda e, wb_=wb_, src=src: e.dma_start(out=wb_, in_=src), w=[('winb', s % 2)])
                S.dma('sp', lambda e, wb_=wb_, dst=dst: e.dma_start(out=dst[:, 0:3, :], in_=wb_[:, 1:4, :]),
                      r=[('winb', s % 2)])
                S.dma('sp', lambda e, wb_=wb_, dst=dst: e.dma_start(out=dst[0:127, 3, :], in_=wb_[1:128, 0, :]),
                      r=[('winb', s % 2)])
            stage_end(23)
            S.barrier()
            A.release()
            stage_end(2)
            BLKS = [(tb * 512, 512) for tb in range(4)] + [(T, NS)]

            def xT_keys(c0, n):
                if c0 >= T:
                    return [('xT', 16)]
                return [('xT', c0 // 128 + i) for i in range(max(1, n // 128))]

            pbank = [0]

            def proj_multi(units, blks=BLKS):
                for (c0, n) in blks:
                    for (wt, wkey, M, evac, aug) in units:
                        bank = pbank[0] % 4
                        pbank[0] += 1

                        def mm(e, c0=c0, n=n, bank=bank, wt=wt, M=M, aug=aug):
                            ins = None
                            for c in range(8):
                                ins = e.matmul(PS(bank, 0, M, 0, n), lhsT=wt[:, c, :], rhs=xT[:, c, c0:c0 + n],
                                               start=(c == 0), stop=(c == 7 and aug is None))
                            if aug is not None:
                                ins = e.matmul(PS(bank, 0, M, 0, n), lhsT=aug, rhs=bas[0:3, c0:c0 + n],
                                               start=False, stop=True)
                            return ins
                        S.pe(mm, r=[wkey, 'bas', 'bas0', 'coef'] + xT_keys(c0, n), w=[pk(bank)])
                        evac(bank, c0, n)

            def proj_fm(wt, wkey, M, evac, aug=None, blks=BLKS):
                proj_multi([(wt, wkey, M, evac, aug)], blks)


            A.mark()
            NWB = 4
            wbuf = [A.alloc([128, 8, 128], BF16) for _ in range(NWB)]
            wctr = [0]

            def load_w(col0, M):
                i = wctr[0] % NWB
                wctr[0] += 1
                S.dma('pool', lambda e: e.dma_start(out=wbuf[i][:, :, 0:M], in_=w_in_t[:, :, col0:col0 + M]),
                      w=[('wbuf', i)])
                return wbuf[i][:, :, 0:M], ('wbuf', i)
            u_ext = A.alloc([128, 4, 30 + T], BF16)
            sza = A.alloc([128, 4, TT], BF16)
            c32 = A.alloc([128, 4, TT], F32)
            u32t = A.alloc([128, 4, 32], F32)
            us32 = A.alloc([128, 4, NS], F32)
            cw_sb = A.alloc([31, 512], F32)
            cwT = A.alloc([128, 4, 31], F32)
            cvec = A.alloc([128, 3, 4], F32)
            A.mark()
            sig = [A.alloc([128, 512], F32) for _ in range(2)]
            sigc = [0]
            S.pool(lambda e: e.memset(u_ext[:, :, 0:30], 0.0), w=[('uext', 'h')])
            for j in range(4):
                wtg, wkg = load_w(C_AGLU + j * 128, 128)
                wta, wka = load_w(C_AVAL + j * 128, 128)
                wtz, wkz = load_w(C_ZA + j * 128, 128)

                def ev_sig(bank, c0, n, j=j):
                    sb = sigc[0] % 2
                    S.act(lambda e: e.activation(out=sig[sb][:, 0:n], in_=PS(bank, 0, 128, 0, n), func=AF.Sigmoid),
                          r=[pk(bank)], w=[('sig', sb)])

                def ev_u(bank, c0, n, j=j):
                    sb = sigc[0] % 2
                    sigc[0] += 1
                    if c0 < T:
                        S.dve(lambda e: e.tensor_tensor(out=u_ext[:, j, 30 + c0:30 + c0 + n], in0=PS(bank, 0, 128, 0, n),
                                                        in1=sig[sb][:, 0:n], op=ALU.mult),
                              r=[pk(bank), ('sig', sb)], w=[('uext', j, c0)])
                        if c0 == 1536:
                            S.dve(lambda e: e.tensor_tensor(out=u32t[:, j, :], in0=PS(bank, 0, 128, 480, 512),
                                                            in1=sig[sb][:, 480:512], op=ALU.mult),
                                  r=[pk(bank), ('sig', sb)], w=[('u32t', j)])
                    else:
                        S.dve(lambda e: e.tensor_tensor(out=us32[:, j, :], in0=PS(bank, 0, 128, 0, n),
                                                        in1=sig[sb][:, 0:n], op=ALU.mult),
                              r=[pk(bank), ('sig', sb)], w=[('us32', j)])

                def ev_za(bank, c0, n, j=j):
                    S.act(lambda e: e.activation(out=sza[:, j, c0:c0 + n], in_=PS(bank, 0, 128, 0, n), func=AF.Silu),
                          r=[pk(bank)], w=[('sza', j, c0)])
                proj_multi([(wtg, wkg, 128, ev_sig, None), (wta, wka, 128, ev_u, None), (wtz, wkz, 128, ev_za, None)])

            stage_end(3)
            S.dma('sp', lambda e: e.dma_start(out=cw_sb, in_=I['conv_w']), w=['cw_sb'])

            def trcw(e):
                ins = None
                for j in range(4):
                    ins = e.transpose(out=PS(7, 0, 128, j * 32, j * 32 + 31), in_=cw_sb[:, j * 128:(j + 1) * 128],
                                      identity=ident_f[0:31, 0:31])
                return ins
            S.pe(trcw, r=['cw_sb', 'ident_f'], w=[pk(7)])
            S.dve(lambda e: e.tensor_copy(out=cwT, in_=PS(7, 0, 128, 0, 128).rearrange("p (j t) -> p j t", j=4)[:, :, 0:31]),
                  r=[pk(7)], w=['cwT'])
            for i, nm in enumerate(['conv_b', 'conv_ln_g', 'conv_ln_b']):
                S.dma('sp', lambda e, i=i, nm=nm: e.dma_start(out=cvec[:, i, :],
                                                              in_=I[nm].rearrange("o (j p) -> p (o j)", p=128),
                                                              allow_slow_non_contiguous=True),
                      w=[('cvec', i)])

            diag = [A.alloc([128, 31, 128], BF16) for _ in range(2)]
            for j in range(4):
                db = j % 2
                for tap in range(31):
                    S.pool(lambda e, j=j, tap=tap, db=db: e.tensor_scalar(
                        out=diag[db][:, tap, :], in0=ident_b, scalar1=cwT[:, j, tap:tap + 1], scalar2=None, op0=ALU.mult),
                        r=['cwT', 'ident_b'], w=[('diag', db, tap)])
                for tb in range(4):
                    bank = 4 + (tb % 2)

                    def cm(e, j=j, tb=tb, db=db, bank=bank):
                        ins = None
                        for tap in range(31):
                            ins = e.matmul(PS(bank), lhsT=diag[db][:, tap, :],
                                           rhs=u_ext[:, j, tb * 512 + tap: tb * 512 + tap + 512],
                                           start=(tap == 0), stop=(tap == 30))
                        return ins
                    rk = [('diag', db, tap) for tap in range(31)] + [('uext', j, tb * 512)]
                    rk += [('uext', j, (tb - 1) * 512)] if tb > 0 else [('uext', 'h')]
                    S.pe(cm, r=rk, w=[pk(bank)])
                    S.act(lambda e, j=j, tb=tb, bank=bank: e.activation(
                        out=c32[:, j, tb * 512:(tb + 1) * 512], in_=PS(bank), func=AF.Identity, bias=cvec[:, 0, j:j + 1]),
                        r=[pk(bank), ('cvec', 0)], w=[('c32', j, tb * 512)])

            stage_end(4)
            exts = A.alloc([128, 4, NS, 31], F32)
            scv = A.alloc([30, NS, 512], F32)
            S.dma('sp', lambda e: e.dma_start(out=scv, in_=I['state_conv'].rearrange("(s r) c -> r s c", r=30)), w=['scv'])
            for s in range(NS):
                def trs_(e, s=s):
                    ins = None
                    for j in range(4):
                        ins = e.transpose(out=PS(6, 0, 128, j * 32, j * 32 + 30), in_=scv[:, s, j * 128:(j + 1) * 128],
                                          identity=ident_f[0:30, 0:30])
                    return ins
                S.pe(trs_, r=['scv', 'ident_f'], w=[pk(6)])
                S.dve(lambda e, s=s: e.tensor_copy(
                    out=exts[:, :, s, 0:30], in_=PS(6, 0, 128, 0, 128).rearrange("p (j t) -> p j t", j=4)[:, :, 0:30]),
                    r=[pk(6)], w=[('exts', s)])
            S.dve(lambda e: e.tensor_copy(out=exts[:, :, :, 30], in_=us32),
                  r=[('us32', j) for j in range(4)], w=[('exts', 'u')])
            prod = A.alloc([128, 4, NS, 31], F32)
            S.dve(lambda e: e.tensor_tensor(out=prod, in0=exts, in1=cwT.unsqueeze(2).broadcast_to([128, 4, NS, 31]),
                                            op=ALU.mult),
                  r=[('exts', s) for s in range(NS)] + [('exts', 'u'), 'cwT'], w=['prod'])
            cs_ = A.alloc([128, 4, NS], F32)
            S.dve(lambda e: e.tensor_reduce(out=cs_, in_=prod, axis=AX.X, op=ALU.add), r=['prod'], w=['cs'])
            S.dve(lambda e: e.tensor_tensor(out=c32[:, :, T:TT], in0=cs_,
                                            in1=cvec[:, 0, :].unsqueeze(2).broadcast_to([128, 4, NS]), op=ALU.add),
                  r=['cs', ('cvec', 0)], w=[('c32', j, T) for j in range(4)])

            cvo = A.alloc([32, 512], F32)

            def tru(e):
                ins = None
                for j in range(4):
                    ins = e.transpose(out=PS(6, 0, 32, j * 128, (j + 1) * 128), in_=u32t[:, j, :], identity=ident_f)
                return ins
            S.pe(tru, r=[('u32t', j) for j in range(4)] + ['ident_f'], w=[pk(6)])
            S.act(lambda e: e.copy(out=cvo, in_=PS(6, 0, 32, 0, 512)), r=[pk(6)], w=['cvo'])
            S.dma('sp', lambda e: e.dma_start(out=O['conv_p'], in_=cvo[2:32, :]), r=['cvo'])
            cso = A.alloc([NS, 512], F32)

            def trus(e):
                ins = None
                for j in range(4):
                    ins = e.transpose(out=PS(6, 0, NS, j * 128, (j + 1) * 128), in_=us32[:, j, :], identity=ident_f)
                return ins
            S.pe(trus, r=[('us32', j) for j in range(4)] + ['ident_f'], w=[pk(6)])
            S.act(lambda e: e.copy(out=cso, in_=PS(6, 0, NS, 0, 512)), r=[pk(6)], w=['cso'])
            conv_s_v = O['conv_s'].rearrange("(s r) c -> s r c", r=30)
            S.dma('sp', lambda e: e.dma_start(out=conv_s_v[:, 29, :], in_=cso), r=['cso'])
            for s in range(NS):
                S.dma('sp', lambda e, s=s: e.dma_start(out=conv_s_v[s, 0:29, :], in_=scv[1:30, s, :]), r=['scv'])

            stage_end(5)
            S.barrier()
            A.release()
            onesm = A.alloc([128, 128], F32)
            S.pool(lambda e: e.memset(onesm, 1.0 / 512.0), w=['onesm'])
            sq4 = A.alloc([128, 4, 512], F32)
            mean_sb = A.alloc([128, 512], F32)
            rstd_sb = A.alloc([128, 512], F32)
            tmpv = A.alloc([128, 512], F32)
            tj = [A.alloc([128, 512], F32) for _ in range(2)]
            tjc = 0
            for (c0, n) in BLKS:
                ck = [('c32', j, c0) for j in range(4)]
                S.act(lambda e, c0=c0, n=n: e.activation(out=sq4[:, :, 0:n], in_=c32[:, :, c0:c0 + n], func=AF.Square),
                      r=ck, w=['sq4'])

                def stm(e, c0=c0, n=n):
                    ins = None
                    for j in range(4):
                        ins = e.matmul(PS(0, 0, 128, 0, n), lhsT=onesm, rhs=c32[:, j, c0:c0 + n], start=(j == 0), stop=(j == 3))
                    for j in range(4):
                        ins = e.matmul(PS(1, 0, 128, 0, n), lhsT=onesm, rhs=sq4[:, j, 0:n], start=(j == 0), stop=(j == 3))
                    return ins
                S.pe(stm, r=ck + ['sq4', 'onesm'], w=[pk(0), pk(1)])
                S.act(lambda e, n=n: e.copy(out=mean_sb[:, 0:n], in_=PS(0, 0, 128, 0, n)), r=[pk(0)], w=['mean_sb'])
                S.dve(lambda e, n=n: e.tensor_tensor(out=tmpv[:, 0:n], in0=mean_sb[:, 0:n], in1=mean_sb[:, 0:n], op=ALU.mult),
                      r=['mean_sb'], w=['tmpv'])
                S.dve(lambda e, n=n: e.tensor_tensor(out=tmpv[:, 0:n], in0=PS(1, 0, 128, 0, n), in1=tmpv[:, 0:n],
                                                     op=ALU.subtract), r=[pk(1), 'tmpv'], w=['tmpv'])
                S.dve(lambda e, n=n: e.tensor_scalar(out=tmpv[:, 0:n], in0=tmpv[:, 0:n], scalar1=1e-5, scalar2=None,
                                                     op0=ALU.add), r=['tmpv'], w=['tmpv'])
                S.act(lambda e, n=n: e.activation(out=tmpv[:, 0:n], in_=tmpv[:, 0:n], func=AF.Ln), r=['tmpv'], w=['tmpv'])
                S.act(lambda e, n=n: e.activation(out=rstd_sb[:, 0:n], in_=tmpv[:, 0:n], func=AF.Exp, scale=-0.5),
                      r=['tmpv'], w=['rstd_sb'])
                for j in range(4):
                    tb_ = tj[tjc % 2]
                    tk = ('tj', tjc % 2)
                    tjc += 1
                    S.dve(lambda e, j=j, c0=c0, n=n, tb_=tb_: e.tensor_tensor(
                        out=tb_[:, 0:n], in0=c32[:, j, c0:c0 + n], in1=mean_sb[:, 0:n], op=ALU.subtract),
                        r=[('c32', j, c0), 'mean_sb'], w=[tk])
                    S.dve(lambda e, n=n, tb_=tb_: e.tensor_tensor(out=tb_[:, 0:n], in0=tb_[:, 0:n], in1=rstd_sb[:, 0:n],
                                                                  op=ALU.mult), r=[tk, 'rstd_sb'], w=[tk])
                    S.dve(lambda e, j=j, n=n, tb_=tb_: e.tensor_scalar(
                        out=tb_[:, 0:n], in0=tb_[:, 0:n], scalar1=cvec[:, 1, j:j + 1], scalar2=cvec[:, 2, j:j + 1],
                        op0=ALU.mult, op1=ALU.add), r=[tk, ('cvec', 1), ('cvec', 2)], w=[tk])
                    S.act(lambda e, n=n, tb_=tb_: e.activation(out=tb_[:, 0:n], in_=tb_[:, 0:n], func=AF.Silu),
                          r=[tk], w=[tk])
                    S.pool(lambda e, j=j, c0=c0, n=n, tb_=tb_: e.tensor_tensor(
                        out=yaT[:, j, c0:c0 + n], in0=tb_[:, 0:n], in1=sza[:, j, c0:c0 + n], op=ALU.mult),
                        r=[tk, ('sza', j, c0)], w=[('yaT', j, c0)])
            S.barrier()
            A.release()
            stage_end(6)
            FORCE = 1.0e4
            BIG = 30000.0
            ybT = A.alloc([64, 8, TT], BF16)
            A.mark()
            Qaug = [A.alloc([128, 4, TT], BF16) for g in range(2)]
            Ksel = [A.alloc([128, TT], BF16) for g in range(2)]
            Kwin = [A.alloc([68, TT], BF16) for g in range(2)]
            gsig = A.alloc([24, TT], BF16)
            KCa = [A.alloc([68, 64], BF16) for g in range(2)]
            VC = [A.alloc([64, 64], BF16) for g in range(2)]
            kcT = A.alloc([64, 2, 2, 64], F32)
            Wkv = A.alloc([64, 2, 2, 32], F32)
            oh = A.alloc([24, 24, 64], BF16)
            ones64 = A.alloc([128, 64], BF16)
            Mpad = A.alloc([128, 128], BF16)
            A.mark()
            NWB = 4
            wpad = [A.alloc([128, 8, 68], BF16) for _ in range(NWB)]
            for i in range(NWB):
                S.pool(lambda e, i=i: e.memset(wpad[i], 0.0), w=[('wpad', i)])
            wpc = [0]

            def load_wpad(col0, M=64):
                i = wpc[0] % NWB
                wpc[0] += 1
                S.dma('pool', lambda e: e.dma_start(out=wpad[i][:, :, 0:M], in_=w_in_t[:, :, col0:col0 + M]),
                      w=[('wpad', i)])
                return wpad[i], ('wpad', i)
            bas = A.alloc([3, TT], BF16)
            basc = A.alloc([3, 64], BF16)
            brow = A.alloc([1, 2, TT], BF16)
            browc = A.alloc([1, 2, 64], BF16)
            coefQ = A.alloc([3, 8, 68], BF16)
            coefK = A.alloc([3, 68], BF16)
            cq0 = A.alloc([1, 3, 8, 68], BF16)
            ck0 = A.alloc([1, 3, 68], BF16)

            def mkbas(e):
                e.iota(brow[:, 0, 0:T], pattern=[[1, 32], [0, 64]], base=0, channel_multiplier=0,
                       allow_small_or_imprecise_dtypes=True)
                e.iota(brow[:, 1, 0:T], pattern=[[0, 32], [1, 64]], base=0, channel_multiplier=0,
                       allow_small_or_imprecise_dtypes=True)
                e.memset(brow[:, 0, T:TT], 128.0)
                e.memset(brow[:, 1, T:TT], 0.0)
                e.iota(browc[:, 0, :], pattern=[[1, 32], [0, 2]], base=0, channel_multiplier=0,
                       allow_small_or_imprecise_dtypes=True)
                e.iota(browc[:, 1, :], pattern=[[0, 32], [32, 2]], base=31, channel_multiplier=0,
                       allow_small_or_imprecise_dtypes=True)
                e.memset(bas[0:1, :], 1.0)
                e.memset(basc[0:1, :], 1.0)
                e.memset(cq0, 0.0)
                e.memset(ck0, 0.0)
                for h in range(8):
                    sl = 2.0 ** (-(h + 1))
                    e.memset(cq0[:, 0, h, 64:65], 8.0 * sl * 64.0)
                    e.memset(cq0[:, 0, h, 65:66], 8.0 * sl)
                    e.memset(cq0[:, 1, h, 66:67], -8.0 * sl * 64.0)
                    e.memset(cq0[:, 2, h, 67:68], -8.0 * sl)
                e.memset(ck0[:, 0, 66:68], 1.0)
                e.memset(ck0[:, 1, 64:65], 1.0)
                e.memset(ck0[:, 2, 65:66], 1.0)
                e.memset(ones64, 1.0)
                e.memset(Mpad, 0.0)
                return e.tensor_copy(out=oh, in_=ident_b[0:24, 0:24].unsqueeze(2).broadcast_to([24, 24, 64]))
            S.pool(mkbas, r=['ident_b'], w=['brow', 'bas0', 'ones64', 'Mpad', 'oh'])
            for i in range(2):
                S.dma('sp', lambda e, i=i: e.dma_start(out=bas[1 + i:2 + i, :], in_=brow[0:1, i, :]), r=['brow'], w=['bas'])
                S.dma('sp', lambda e, i=i: e.dma_start(out=basc[1 + i:2 + i, :], in_=browc[0:1, i, :]), r=['brow'], w=['bas'])
            for i in range(3):
                S.dma('sp', lambda e, i=i: e.dma_start(out=coefQ[i:i + 1, :, :], in_=cq0[0:1, i, :, :]), r=['brow'], w=['coef'])
                S.dma('sp', lambda e, i=i: e.dma_start(out=coefK[i:i + 1, :], in_=ck0[0:1, i, :]), r=['brow'], w=['coef'])

            QK = [[('Q', g, c0) for (c0, n) in BLKS] + [('Qm', g, qt) for qt in range(17)] for g in range(2)]
            KK = [[('Ks', g, c0) for (c0, n) in BLKS] for g in range(2)]
            for g in range(2):
                S.pool(lambda e, g=g: e.memset(Qaug[g][64:128, :, :], 0.0), w=QK[g])

                def kinit(e, g=g):
                    e.memset(Ksel[g][64:128, :], 0.0)
                    e.memset(Ksel[g][96:128, 0:T], 1.0)
                    e.affine_select(out=Ksel[g][96:128, 0:T], in_=Ksel[g][96:128, 0:T], pattern=[[1, T]],
                                    compare_op=ALU.is_ge, fill=freg(e, 0.0), base=0, channel_multiplier=-64)
                    return e.affine_select(out=Ksel[g][96:128, 0:T], in_=Ksel[g][96:128, 0:T], pattern=[[-1, T]],
                                           compare_op=ALU.is_ge, fill=freg(e, 0.0), base=63, channel_multiplier=64)
                S.pool(kinit, w=KK[g])

            stage_end(7)
            wt, wk_ = load_wpad(C_G3, 24)

            def ev_g(bank, c0, n):
                S.act(lambda e: e.activation(out=gsig[0:24, c0:c0 + n], in_=PS(bank, 0, 24, 0, n), func=AF.Sigmoid),
                      r=[pk(bank)], w=[('gsig', c0)])
            proj_fm(wt[:, :, 0:24], wk_, 24, ev_g)
            for h in range(8):
                wt, wk_ = load_wpad(C_ZB + h * 64, 64)

                def ev_zb(bank, c0, n, h=h):
                    S.act(lambda e: e.activation(out=ybT[0:64, h, c0:c0 + n], in_=PS(bank, 0, 64, 0, n), func=AF.Silu),
                          r=[pk(bank)], w=[('ybT', h, c0)])
                proj_fm(wt[:, :, 0:64], wk_, 64, ev_zb)
            for h in range(8):
                g, r_ = h // 4, h % 4
                wt, wk_ = load_wpad(C_Q + h * 64)

                def ev_q(bank, c0, n, g=g, r_=r_):
                    S.dve(lambda e: e.tensor_scalar(out=Qaug[g][0:68, r_, c0:c0 + n], in0=PS(bank, 0, 68, 0, n),
                                                    scalar1=0.125, scalar2=None, op0=ALU.mult),
                          r=[pk(bank)], w=[('Q', g, c0)])
                proj_fm(wt, wk_, 68, ev_q, aug=coefQ[0:3, h, :])
            augK = coefK[0:3, :]
            for g in range(2):
                wt, wk_ = load_wpad(C_KV + 256 + g * 64)

                def ev_ks(bank, c0, n, g=g):
                    S.act(lambda e: e.copy(out=Ksel[g][0:68, c0:c0 + n], in_=PS(bank, 0, 68, 0, n)),
                          r=[pk(bank)], w=[('Ks', g, c0)])
                proj_fm(wt, wk_, 68, ev_ks, aug=augK)
                wt, wk_ = load_wpad(C_KV + 512 + g * 64)

                def ev_kw(bank, c0, n, g=g):
                    S.act(lambda e: e.copy(out=Kwin[g][0:68, c0:c0 + n], in_=PS(bank, 0, 68, 0, n)),
                          r=[pk(bank)], w=[('Kw', g, c0)])
                proj_fm(wt, wk_, 68, ev_kw, aug=augK)
            for kv_, nm in enumerate(['wk_cmp', 'wv_cmp']):
                for g in range(2):
                    S.dma('sp', lambda e, kv_=kv_, nm=nm, g=g: e.dma_start(
                        out=Wkv[:, kv_, g:g + 1, :], in_=I[nm][:, g:g + 1].rearrange("l o -> o l").partition_broadcast(64),
                        allow_slow_non_contiguous=True), w=[('Wkv', kv_, g)])
            ptmp = [A.alloc([64, 16, 32], F32) for _ in range(1)]
            pcnt = [0]
            for kv_ in range(2):
                for g in range(2):
                    wt, wk_ = load_wpad(C_KV + kv_ * 128 + g * 64, 64)

                    def ev_pool(bank, c0, n, kv_=kv_, g=g):
                        pb = 0
                        pcnt[0] += 1
                        S.dve(lambda e: e.tensor_tensor(
                            out=ptmp[pb], in0=PS(bank, 0, 64, 0, 512).rearrange("p (a l) -> p a l", l=32),
                            in1=Wkv[:, kv_, g:g + 1, :].broadcast_to([64, 16, 32]), op=ALU.mult),
                            r=[pk(bank), ('Wkv', kv_, g)], w=[('ptmp', pb)])
                        S.dve(lambda e: e.tensor_reduce(out=kcT[:, kv_, g, c0 // 32:c0 // 32 + 16], in_=ptmp[pb],
                                                        axis=AX.X, op=ALU.add),
                              r=[('ptmp', pb)], w=[('kcT', kv_, g, c0)])
                    proj_fm(wt[:, :, 0:64], wk_, 64, ev_pool, blks=BLKS[0:4])
            for g in range(2):
                kck = [('kcT', 0, g, c0) for (c0, n) in BLKS[0:4]]
                vck = [('kcT', 1, g, c0) for (c0, n) in BLKS[0:4]]
                S.dve(lambda e, g=g: e.tensor_copy(out=KCa[g][0:64, :], in_=kcT[:, 0, g, :]), r=kck, w=[('KCa', g)])

                def mmaug(e, g=g):
                    return e.matmul(PS(7, 0, 68, 0, 64), lhsT=coefK[0:3, :], rhs=basc[0:3, :], start=True, stop=True)
                S.pe(mmaug, r=['coef', 'bas', 'bas0'], w=[pk(7)])
                S.act(lambda e, g=g: e.copy(out=KCa[g][64:68, :], in_=PS(7, 64, 68, 0, 64)), r=[pk(7)], w=[('KCa', g)])
                S.pe(lambda e, g=g: e.transpose(out=PS(6, 0, 64, 0, 64), in_=kcT[:, 1, g, :], identity=ident_f[0:64, 0:64]),
                     r=vck + ['ident_f'], w=[pk(6)])
                S.act(lambda e, g=g: e.copy(out=VC[g], in_=PS(6, 0, 64, 0, 64)), r=[pk(6)], w=[('VC', g)])

            stage_end(8)
            S.barrier()
            A.release()
            Ec = A.alloc([128, 4, 64], F32)
            sums_c = A.alloc([128, 4], F32)
            imp64 = A.alloc([128, 64], F32)
            imp = A.alloc([128, 32], F32)
            imp2 = A.alloc([128, 32], F32)
            m8 = A.alloc([128, 16], F32)
            PT = [A.alloc([128, 512], BF16) for _ in range(3)]
            acc = [A.alloc([64, 512], F32) for _ in range(2)]
            rs_ = [A.alloc([64, 512], F32) for _ in range(2)]
            tt_ = [A.alloc([64, 512], F32) for _ in range(2)]
            ptc = [0]
            sbc = [0]
            rsc = [0]

            def qkeys(g, qt):
                return [('Q', g, (qt // 4) * 512), ('Qm', g, qt)]

            pend = [None]

            def attn_pair(g, qt, lhsT, lkeys, KKrows, vt, vkeys, masks, ob, sb_, first, last):
                q0 = qt * 128
                sbank = sbc[0] % 2
                sbc[0] += 1
                pt = PT[ptc[0] % 3]
                ptk = ('PT', ptc[0] % 3)
                ptc[0] += 1
                M = lhsT.shape[1]
                S.pe(lambda e: e.matmul(PS(sbank, 0, M, 0, 512), lhsT=lhsT, rhs=Qaug[g][0:KKrows, :, q0:q0 + 128],
                                        start=True, stop=True),
                     r=lkeys + qkeys(g, qt), w=[pk(sbank)])
                S.act(lambda e: e.activation(out=pt[0:M, :], in_=PS(sbank, 0, M, 0, 512), func=AF.Exp),
                      r=[pk(sbank)], w=[ptk])
                for (base, cm, pat) in masks:
                    S.pool(lambda e, base=base, cm=cm, pat=pat: e.affine_select(
                        out=pt[0:M, :], in_=pt[0:M, :], pattern=pat, compare_op=ALU.is_ge, fill=freg(e, 0.0), base=base,
                        channel_multiplier=cm), r=[ptk], w=[ptk])

                def pv(e):
                    e.matmul(PS(ob, 0, 64, 0, 512), lhsT=vt, rhs=pt[0:M, :], start=first, stop=last)
                    return e.matmul(PS(sb_, 0, 64, 0, 512), lhsT=ones64[0:M, :], rhs=pt[0:M, :], start=first, stop=last)
                prev = pend[0]
                pend[0] = lambda: S.pe(pv, r=[ptk, 'ones64'] + vkeys, w=[pk(ob), pk(sb_)])
                if prev is not None:
                    prev()

            def flush_pv():
                if pend[0] is not None:
                    pend[0]()
                    pend[0] = None

            def combine(g, qt, j, ob, sb_, ab):
                flush_pv()
                q0 = qt * 128
                rb = rsc[0] % 2
                rsc[0] += 1

                def gmm(e):
                    ins = None
                    for r_ in range(4):
                        ins = e.matmul(PS(6, 0, 64, r_ * 128, (r_ + 1) * 128), lhsT=oh[:, 3 * (4 * g + r_) + j, :],
                                       rhs=gsig[0:24, q0:q0 + 128], start=True, stop=True)
                    return ins
                S.pe(gmm, r=['oh', ('gsig', (qt // 4) * 512)], w=[pk(6)])
                S.dve(lambda e: e.tensor_scalar(out=rs_[rb], in0=PS(sb_, 0, 64, 0, 512), scalar1=1e-30, scalar2=None,
                                                op0=ALU.max), r=[pk(sb_)], w=[('rs', rb)])
                S.dve(lambda e: e.reciprocal(out=rs_[rb], in_=rs_[rb]), r=[('rs', rb)], w=[('rs', rb)])
                S.dve(lambda e: e.tensor_tensor(out=rs_[rb], in0=rs_[rb], in1=PS(6, 0, 64, 0, 512), op=ALU.mult),
                      r=[('rs', rb), pk(6)], w=[('rs', rb)])
                if j == 0:
                    S.dve(lambda e: e.tensor_tensor(out=acc[ab], in0=PS(ob, 0, 64, 0, 512), in1=rs_[rb], op=ALU.mult),
                          r=[pk(ob), ('rs', rb)], w=[('acc', ab)])
                else:
                    S.dve(lambda e: e.tensor_tensor(out=tt_[rb], in0=PS(ob, 0, 64, 0, 512), in1=rs_[rb], op=ALU.mult),
                          r=[pk(ob), ('rs', rb)], w=[('tt', rb)])
                    S.pool(lambda e: e.tensor_tensor(out=acc[ab], in0=acc[ab], in1=tt_[rb], op=ALU.add),
                           r=[('tt', rb), ('acc', ab)], w=[('acc', ab)])

            pat_q = [[0, 4], [1, 128]]
            pat_qn = [[0, 4], [-1, 128]]
            def partA(qt, g):
                q0 = qt * 128
                def cmm(e, g=g, q0=q0):
                    ins = None
                    for r_ in range(4):
                        ins = e.matmul(PS(7, 0, 128, r_ * 64, (r_ + 1) * 64), lhsT=Qaug[g][0:68, r_, q0:q0 + 128],
                                       rhs=KCa[g][0:68, :], start=True, stop=True)
                    return ins
                S.pe(cmm, r=qkeys(g, qt) + [('KCa', g)], w=[pk(7)])
                S.act(lambda e: e.activation(out=Ec, in_=PS(7, 0, 128, 0, 256).rearrange("p (r n) -> p r n", r=4),
                                             func=AF.Exp), r=[pk(7)], w=['Ec'])
                S.pool(lambda e, q0=q0: e.affine_select(out=Ec, in_=Ec, pattern=[[0, 4], [-32, 64]],
                                                        compare_op=ALU.is_ge, fill=freg(e, 0.0), base=q0 - 31,
                                                        channel_multiplier=1), r=['Ec'], w=['Ec'])
                S.dve(lambda e: e.tensor_reduce(out=sums_c, in_=Ec, axis=AX.X, op=ALU.add), r=['Ec'], w=['sums_c'])
                S.dve(lambda e: e.tensor_scalar(out=sums_c, in0=sums_c, scalar1=1e-30, scalar2=None, op0=ALU.max),
                      r=['sums_c'], w=['sums_c'])
                S.dve(lambda e: e.reciprocal(out=sums_c, in_=sums_c), r=['sums_c'], w=['sums_c'])
                S.dve(lambda e: e.tensor_tensor(out=Ec, in0=Ec, in1=sums_c.unsqueeze(2).broadcast_to([128, 4, 64]),
                                                op=ALU.mult), r=['Ec', 'sums_c'], w=['Ec'])
                S.dve(lambda e: e.tensor_reduce(out=imp64, in_=Ec.rearrange("p r n -> p n r"), axis=AX.X, op=ALU.add),
                      r=['Ec'], w=['imp64'])
                S.dve(lambda e: e.tensor_reduce(out=imp, in_=imp64.rearrange("p (a b) -> p a b", b=2), axis=AX.X,
                                                op=ALU.add), r=['imp64'], w=['imp'])
                S.pool(lambda e, q0=q0: e.affine_select(out=imp, in_=imp, pattern=[[-64, 32]], compare_op=ALU.is_ge,
                                                        fill=freg(e, FORCE), base=q0 - 128, channel_multiplier=1),
                       r=['imp'], w=['imp'])
                S.pool(lambda e: e.memset(imp[:, 0:1], FORCE), r=['imp'], w=['imp'])
                S.pool(lambda e, q0=q0: e.affine_select(out=imp, in_=imp, pattern=[[-64, 32]], compare_op=ALU.is_ge,
                                                        fill=freg(e, -1.0e30), base=q0, channel_multiplier=1),
                       r=['imp'], w=['imp'])
                S.dve(lambda e: e.max(out=m8[:, 0:8], in_=imp), r=['imp'], w=['m8a'])
                S.dve(lambda e: e.match_replace(out=imp2, in_to_replace=m8[:, 0:8], in_values=imp, imm_value=-3.0e38),
                      r=['imp', 'm8a'], w=['imp2'])
                S.dve(lambda e: e.max(out=m8[:, 8:16], in_=imp2), r=['imp2'], w=['m8b'])
                S.dve(lambda e: e.tensor_scalar(out=Mpad[:, 96:128], in0=imp, scalar1=m8[:, 15:16], scalar2=None,
                                                op0=ALU.is_ge), r=['imp', 'm8b'], w=['Mpad'])

            def partB(qt, g):
                q0 = qt * 128
                S.pe(lambda e: e.matmul(PS(6, 0, 128, 0, 128), lhsT=Mpad, rhs=ident_b, start=True, stop=True),
                     r=['Mpad', 'ident_b'], w=[pk(6)])
                S.dve(lambda e, g=g, q0=q0: e.tensor_scalar(
                    out=Qaug[g][96:128, :, q0:q0 + 128],
                    in0=PS(6, 96, 128, 0, 128).unsqueeze(1).broadcast_to([32, 4, 128]),
                    scalar1=1.0, scalar2=BIG, op0=ALU.subtract, op1=ALU.mult), r=[pk(6)], w=[('Qm', g, qt)])

            def partC(qt, g):
                q0 = qt * 128
                ab = (qt * 2 + g) % 2
                attn_pair(g, qt, KCa[g][0:68, :], [('KCa', g)], 68, VC[g], [('VC', g)],
                          [(q0 - 31, -32, pat_q)], 2, 3, True, True)
                combine(g, qt, 0, 2, 3, ab)
                for kt in range(qt + 1):
                    masks = [(0, -1, pat_q)] if kt == qt else []
                    attn_pair(g, qt, Ksel[g][:, kt * 128:(kt + 1) * 128], [('Ks', g, (kt // 4) * 512)] + KK[g][0:0], 128,
                              Vsel[:, kt, g * 64:(g + 1) * 64], [('Vsel', kt)], masks, 4, 5, kt == 0, kt == qt)
                combine(g, qt, 1, 4, 5, ab)
                k_lo = max(0, qt - 4)
                for kt in range(k_lo, qt + 1):
                    masks = []
                    if kt == qt:
                        masks.append((0, -1, pat_q))
                    if kt == qt - 4:
                        masks.append((-1, 1, pat_qn))
                    attn_pair(g, qt, Kwin[g][0:68, kt * 128:(kt + 1) * 128], [('Kw', g, (kt // 4) * 512)], 68,
                              Vwin[:, kt, g * 64:(g + 1) * 64], [('Vwin', kt)], masks, 2, 3, kt == k_lo, kt == qt)
                combine(g, qt, 2, 2, 3, ab)
                S.dve(lambda e, g=g, q0=q0, ab=ab: e.tensor_tensor(
                    out=ybT[0:64, 4 * g:4 * g + 4, q0:q0 + 128],
                    in0=acc[ab].rearrange("p (r q) -> p r q", r=4),
                    in1=ybT[0:64, 4 * g:4 * g + 4, q0:q0 + 128], op=ALU.mult),
                    r=[('acc', ab)] + [('ybT', 4 * g + r_, (qt // 4) * 512) for r_ in range(4)],
                    w=[('ybT', 4 * g + r_, (qt // 4) * 512) for r_ in range(4)])

            its = [(qt, g) for qt in range(16) for g in range(2)]
            partA(*its[0])
            partB(*its[0])
            for i_, (qt, g) in enumerate(its):
                if i_ + 1 < len(its):
                    partA(*its[i_ + 1])
                partC(qt, g)
                if i_ + 1 < len(its):
                    partB(*its[i_ + 1])
            for g in range(2):
                S.dve(lambda e, g=g: e.tensor_copy(out=qs_f[:, g, :, :], in_=Qaug[g][0:68, :, T:TT]), r=[('Q', g, T)], w=['qs_f'])
                S.dve(lambda e, g=g: e.tensor_copy(out=ksn[:, 0, g, :], in_=Ksel[g][0:68, T:TT]), r=[('Ks', g, T)], w=['ksn'])
                S.dve(lambda e, g=g: e.tensor_copy(out=ksn[:, 1, g, :], in_=Kwin[g][0:68, T:TT]), r=[('Kw', g, T)], w=['ksn'])
            S.dve(lambda e: e.tensor_copy(out=gs_s, in_=gsig[0:24, T:TT]), r=[('gsig', T)], w=['gs_s'])
            stage_end(9)
            S.barrier()
            A.release()
            A.mark()
            XK = [A.alloc([128, 32, 128], F32) for _ in range(2)]
            XV = [A.alloc([128, 32, 128], F32) for _ in range(2)]
            tmp4 = xT_raw32[:, 0:8192].rearrange("p (l r d) -> p l r d", l=32, r=4)
            pt_i = A.alloc([32, NS, 2], I32)
            pt_f = A.alloc([32, NS * 2], F32)
            E32 = A.alloc([32, 128], F32)
            qcol_i = A.alloc([128, 1], I32)
            qcol_f = A.alloc([128, 1], F32)
            idx_f = A.alloc([128, NS * 2], F32)
            idx_i = A.alloc([128, NS * 2], I32)
            Wl4 = A.alloc([128, 32, 4], F32)
            slopes = A.alloc([128, 8], F32)
            posc = A.alloc([128, 2], F32)
            ABc = A.alloc([128, 2, 8], F32)
            poss = A.alloc([128, 2, 32], F32)
            ABs = A.alloc([128, 2, 8, 32], F32)
            posw = A.alloc([128, 4], F32)
            ABw = A.alloc([128, 8, 4], F32)
            Ex = A.alloc([128, 2, 128], BF16)
            qrep = A.alloc([64, 8, 128], BF16)
            qb = A.alloc([128, 8, 64], F32)
            kvc = Vwin_raw32[:, 0:512].rearrange("p (t c) -> p t c", t=2)
            tmpc = A.alloc([128, 2, 4, 64], F32)
            sc = A.alloc([128, 2, 8], F32)
            Esum = A.alloc([128, 8], F32)
            pcg = A.alloc([128, 2, 2], F32)
            impn = A.alloc([2, 256], F32)
            impd = A.alloc([2, 129], F32)
            impd2 = A.alloc([2, 129], F32)
            m8d = A.alloc([2, 16], F32)
            Md = A.alloc([2, 129], F32)
            MT = A.alloc([128, 2], BF16)
            mskd = A.alloc([128, 2, 2], F32)
            ss = Vwin_raw32[:, 512:1024].rearrange("p (t h l) -> p t h l", t=2, h=8)
            Pl = A.alloc([128, 2, 8], F32)
            Wn = Vsel_raw32[:, 0:1024].rearrange("p (l c) -> p l c", l=4)
            sw = A.alloc([128, 8, 4], F32)
            Plw = A.alloc([128, 8], F32)
            prodn = A.alloc([68, 2, 2, 4, NS], F32)
            pnew = A.alloc([1, 2, 2, 4, NS], F32)
            vrow = A.alloc([1, NS, 2, 128], F32)
            accd = A.alloc([4, 2, 64], F32)
            gts = A.alloc([4, 6, NS], F32)
            rsd = A.alloc([4, 4], F32)

            cache_v = {(c_, hf): I['cache_%s_%s' % (c_, hf)].rearrange("(b q) c -> b (q c)", q=32)
                       for c_ in ('cmp', 'sel') for hf in ('k', 'v')}

            def chain(addfn, fns, key, r=()):
                for fn in fns:
                    addfn(fn, r=[key] + list(r), w=[key])

            fns1 = [
                lambda e: e.iota(qcol_i, pattern=[[0, 1]], base=0, channel_multiplier=1),
                lambda e: e.memset(E32, 1.0),
                lambda e: e.memset(accd, 0.0),
                lambda e: e.affine_select(out=E32, in_=E32, pattern=[[1, 128]], compare_op=ALU.is_ge, fill=freg(e, 0.0), base=0,
                                          channel_multiplier=-4),
                lambda e: e.iota(posc, pattern=[[4096, 2]], base=31 - 8192, channel_multiplier=32,
                                 allow_small_or_imprecise_dtypes=True),
                lambda e: e.affine_select(out=E32, in_=E32, pattern=[[-1, 128]], compare_op=ALU.is_ge, fill=freg(e, 0.0), base=3,
                                          channel_multiplier=4),
                lambda e: e.iota(poss, pattern=[[4096, 2], [1, 32]], base=-8192, channel_multiplier=32,
                                 allow_small_or_imprecise_dtypes=True),
                lambda e: e.iota(posw, pattern=[[1, 4]], base=7680 - 8192, channel_multiplier=4,
                                 allow_small_or_imprecise_dtypes=True),
            ]
            for h in range(8):
                fns1.append(lambda e, h=h: e.memset(slopes[:, h:h + 1], 2.0 ** (-(h + 1))))
            for t in range(2):
                fns1.append(lambda e, t=t: e.memset(Ex[:, t, :], 1.0))
            for t in range(2):
                fns1.append(lambda e, t=t: e.affine_select(out=Ex[:, t, :], in_=Ex[:, t, :], pattern=[[1, 128]], compare_op=ALU.is_ge,
                                                          fill=freg(e, 0.0), base=128 * t, channel_multiplier=-2))
            for t in range(2):
                fns1.append(lambda e, t=t: e.affine_select(out=Ex[:, t, :], in_=Ex[:, t, :], pattern=[[-1, 128]], compare_op=ALU.is_ge,
                                                          fill=freg(e, 0.0), base=1 - 128 * t, channel_multiplier=2))
            chain(S.pool, fns1, 'dsetup')
            S.dma('sp', lambda e: e.dma_start(out=pt_i, in_=I['page_table'].rearrange("s (t j) -> j s t", j=32),
                                              allow_slow_non_contiguous=True), w=['pt_i'])
            S.dma('sp', lambda e: e.dma_start(out=Wl4[:, :, 0:2], in_=I['wk_cmp'].partition_broadcast(128)), w=['Wl4a'])
            S.dma('sp', lambda e: e.dma_start(out=Wl4[:, :, 2:4], in_=I['wv_cmp'].partition_broadcast(128)), w=['Wl4b'])
            for s in range(NS):
                S.dma('sp', lambda e, s=s: e.dma_start(out=vrow[0:1, s, 0, :], in_=kvsv[s:s + 1, 0, :]), r=['kvsv'],
                      w=[('vrow', s)])
                S.dma('sp', lambda e, s=s: e.dma_start(out=vrow[0:1, s, 1, :], in_=kvsv[s:s + 1, 1, :]), r=['kvsv'],
                      w=[('vrow', s)])

            fns2 = [
                lambda e: e.tensor_copy(out=pt_f, in_=pt_i.rearrange("p s t -> p (s t)")),
                lambda e: e.tensor_single_scalar(out=qcol_i, in_=qcol_i, scalar=3, op=ALU.bitwise_and),
                lambda e: e.tensor_tensor(out=ABc, in0=posc.unsqueeze(2).broadcast_to([128, 2, 8]),
                                          in1=slopes.unsqueeze(1).broadcast_to([128, 2, 8]), op=ALU.mult),
                lambda e: e.tensor_copy(out=qcol_f, in_=qcol_i),
                lambda e: e.tensor_tensor(out=ABw, in0=posw.unsqueeze(1).broadcast_to([128, 8, 4]),
                                          in1=slopes.unsqueeze(2).broadcast_to([128, 8, 4]), op=ALU.mult),
            ]
            for t in range(2):
                fns2.append(lambda e, t=t: e.tensor_tensor(out=ABs[:, t, :, :], in0=poss[:, t, :].unsqueeze(1).broadcast_to([128, 8, 32]),
                                                          in1=slopes.unsqueeze(2).broadcast_to([128, 8, 32]), op=ALU.mult))
            chain(S.dve, fns2, 'dsetup2', r=['dsetup', 'pt_i'])
            S.pe(lambda e: e.matmul(PS(1, 0, 128, 0, NS * 2), lhsT=E32, rhs=pt_f, start=True, stop=True),
                 r=['dsetup', 'dsetup2'], w=[pk(1)])
            S.dve(lambda e: e.tensor_scalar(out=idx_f, in0=PS(1, 0, 128, 0, NS * 2), scalar1=4.0, scalar2=qcol_f[:, 0:1],
                                            op0=ALU.mult, op1=ALU.add), r=[pk(1), 'dsetup2'], w=['idx_f'])
            chain(S.dve, [lambda e: e.tensor_copy(out=idx_i, in_=idx_f)], 'idx_i', r=['idx_f'])
            for br in range(2):
                S.dve(lambda e, br=br: e.tensor_tensor(
                    out=prodn[:, br, :, :, :], in0=qs_f, in1=ksn[:, br, :, :].unsqueeze(2).broadcast_to([68, 2, 4, NS]),
                    op=ALU.mult), r=['qs_f', 'ksn'], w=[('prodn', br)])
            S.pe(lambda e: e.matmul(PS(7, 0, 1, 0, 64), lhsT=ones_f[0:68, 0:1],
                                    rhs=prodn.rearrange("p a g r s -> p (a g r s)"), start=True, stop=True),
                 r=[('prodn', 0), ('prodn', 1), 'ones_f'], w=[pk(7)])
            S.act(lambda e: e.activation(out=pnew.rearrange("p a g r s -> p (a g r s)"), in_=PS(7, 0, 1, 0, 64), func=AF.Exp),
                  r=[pk(7)], w=['pnew'])
            def gmm_d(e):
                ins = None
                for g in range(2):
                    for j in range(3):
                        c0_ = 12 * g + j
                        ins = e.matmul(PS(7, 0, 4, 64 + (g * 3 + j) * NS, 64 + (g * 3 + j + 1) * NS),
                                       lhsT=ident_b[0:24, c0_:c0_ + 10:3], rhs=gs_s, start=True, stop=True)
                return ins
            S.pe(gmm_d, r=['gs_s', 'ident_b'], w=[pk(7)])
            S.act(lambda e: e.copy(out=gts.rearrange("p a s -> p (a s)"), in_=PS(7, 0, 4, 64, 64 + 6 * NS)), r=[pk(7)],
                  w=['gts'])

            def gather(cname, s, t, xb):
                col = s * 2 + t
                for (buf, hf, kn) in ((XK, 'k', 'XK'), (XV, 'v', 'XV')):
                    S.dma('pool', lambda e, buf=buf, hf=hf: e.indirect_dma_start(
                        out=buf[xb].rearrange("p l c -> p (l c)"), out_offset=None, in_=cache_v[(cname, hf)],
                        in_offset=bass.IndirectOffsetOnAxis(ap=idx_i[:, col:col + 1], axis=0)), r=['idx_i'], w=[(kn, xb)])

            for s in range(NS):
                for t in range(2):
                    gather('cmp', s, t, t)
                    for hi_, (buf, kn) in enumerate(((XK, 'XK'), (XV, 'XV'))):
                        S.dve(lambda e, t=t, buf=buf, hi_=hi_: e.tensor_tensor(
                            out=buf[t].rearrange("p l (a d) -> p l a d", a=2), in0=buf[t].rearrange("p l (a d) -> p l a d", a=2),
                            in1=Wl4[:, :, 2 * hi_:2 * hi_ + 2].unsqueeze(3).broadcast_to([128, 32, 2, 64]), op=ALU.mult),
                            r=[(kn, t), 'Wl4a', 'Wl4b'], w=[(kn, t)])
                        S.dve(lambda e, t=t, buf=buf, hi_=hi_: e.tensor_reduce(
                            out=kvc[:, t, hi_ * 128:(hi_ + 1) * 128], in_=buf[t].rearrange("p l c -> p c l"), axis=AX.X, op=ALU.add),
                            r=[(kn, t)], w=[('kvc', t)])
                stage_end(106)
                S.dve(lambda e, s=s: e.tensor_copy(
                    out=qrep, in_=qs_f[0:64, :, :, s].rearrange("p g r -> p (g r)").unsqueeze(2).broadcast_to([64, 8, 128])),
                    r=['qs_f'], w=['qrep'])

                def qbm(e):
                    ins = None
                    for h in range(8):
                        ins = e.matmul(PS(0, 0, 128, h * 64, (h + 1) * 64), lhsT=qrep[:, h, :], rhs=ident_b[0:64, 0:64],
                                       start=True, stop=True)
                    return ins
                S.pe(qbm, r=['qrep', 'ident_b'], w=[pk(0)])
                S.act(lambda e: e.copy(out=qb.rearrange("p h d -> p (h d)"), in_=PS(0)), r=[pk(0)], w=['qb'])
                for t in range(2):
                    S.dve(lambda e, t=t: e.tensor_tensor(
                        out=tmpc, in0=kvc[:, t, 0:128].rearrange("p (g d) -> p g d", g=2).unsqueeze(2).broadcast_to([128, 2, 4, 64]),
                        in1=qb.rearrange("p (g r) d -> p g r d", g=2), op=ALU.mult),
                        r=[('kvc', t), 'qb'], w=['tmpc'])
                    S.dve(lambda e, t=t: e.tensor_reduce(out=sc[:, t, :], in_=tmpc.rearrange("p g r d -> p (g r) d"),
                                                         axis=AX.X, op=ALU.add), r=['tmpc'], w=[('sc', t)])
                S.dve(lambda e: e.tensor_tensor(out=sc, in0=sc, in1=ABc, op=ALU.add), r=[('sc', 0), ('sc', 1), 'dsetup2'],
                      w=['sc'])
                S.act(lambda e: e.activation(out=sc, in_=sc, func=AF.Exp), r=['sc'], w=['sc'])
                S.dve(lambda e: e.tensor_tensor(out=Esum, in0=sc[:, 0, :], in1=sc[:, 1, :], op=ALU.add), r=['sc'], w=['Esum'])
                S.pe(lambda e: e.matmul(PS(1, 0, 128, 0, 8), lhsT=ones_f, rhs=Esum, start=True, stop=True),
                     r=['Esum', 'ones_f'], w=[pk(1)])
                S.dve(lambda e: e.reciprocal(out=Esum, in_=PS(1, 0, 128, 0, 8)), r=[pk(1)], w=['Esum'])
                S.dve(lambda e: e.tensor_tensor(out=sc, in0=sc, in1=Esum.unsqueeze(1).broadcast_to([128, 2, 8]), op=ALU.mult),
                      r=['sc', 'Esum'], w=['sc'])

                def ocm(e):
                    ins = None
                    for g in range(2):
                        for t in range(2):
                            ins = e.matmul(PS(2, 0, 4, g * 64, (g + 1) * 64), lhsT=sc[:, t, 4 * g:4 * g + 4],
                                           rhs=kvc[:, t, 128 + g * 64:128 + (g + 1) * 64], start=(t == 0), stop=(t == 1))
                    return ins
                S.pe(ocm, r=['sc', ('kvc', 0), ('kvc', 1)], w=[pk(2)])
                S.dve(lambda e: e.tensor_reduce(out=pcg, in_=sc.rearrange("p t (g r) -> p t g r", g=2), axis=AX.X, op=ALU.add),
                      r=['sc'], w=['pcg'])

                def trp(e):
                    ins = None
                    for t in range(2):
                        ins = e.transpose(out=PS(3, 0, 2, t * 128, (t + 1) * 128), in_=pcg[:, t, :], identity=ident_f)
                    return ins
                S.pe(trp, r=['pcg', 'ident_f'], w=[pk(3)])
                S.act(lambda e: e.copy(out=impn, in_=PS(3, 0, 2, 0, 256)), r=[pk(3)], w=['impn'])
                S.dve(lambda e: e.tensor_reduce(out=impd[:, 0:128], in_=impn.rearrange("p (a b) -> p a b", b=2), axis=AX.X,
                                                op=ALU.add), r=['impn'], w=['impd'])

                S.pool(lambda e: e.memset(impd[:, 0:1], FORCE), r=['impd'], w=['impd'])
                S.pool(lambda e: e.memset(impd[:, 127:129], FORCE), r=['impd'], w=['impd'])
                S.dve(lambda e: e.max(out=m8d[:, 0:8], in_=impd), r=['impd'], w=['m8da'])
                S.dve(lambda e: e.match_replace(out=impd2, in_to_replace=m8d[:, 0:8], in_values=impd, imm_value=-3.0e38),
                      r=['impd', 'm8da'], w=['impd2'])
                S.dve(lambda e: e.max(out=m8d[:, 8:16], in_=impd2), r=['impd2'], w=['m8db'])
                S.dve(lambda e: e.tensor_scalar(out=Md, in0=impd, scalar1=m8d[:, 15:16], scalar2=None, op0=ALU.is_ge),
                      r=['impd', 'm8db'], w=['Md'])
                S.pe(lambda e: e.transpose(out=PS(3, 0, 128, 256, 258), in_=Md[:, 0:128], identity=ident_f[0:2, 0:2]),
                     r=['Md', 'ident_f'], w=[pk(3)])
                S.act(lambda e: e.copy(out=MT, in_=PS(3, 0, 128, 256, 258)), r=[pk(3)], w=['MT'])

                def mskm(e):
                    ins = None
                    for t in range(2):
                        ins = e.matmul(PS(3, 0, 128, 260 + 2 * t, 262 + 2 * t), lhsT=Ex[:, t, :], rhs=MT, start=True, stop=True)
                    return ins
                S.pe(mskm, r=['MT', 'dsetup'], w=[pk(3)])
                S.act(lambda e: e.copy(out=mskd.rearrange("p t g -> p (t g)"), in_=PS(3, 0, 128, 260, 264)), r=[pk(3)],
                      w=['mskd'])
                for t in range(2):
                    gather('sel', s, t, t)
                    for g in range(2):
                        S.dve(lambda e, t=t, g=g: e.tensor_tensor(
                            out=tmp4, in0=XK[t][:, :, g * 64:(g + 1) * 64].unsqueeze(2).broadcast_to([128, 32, 4, 64]),
                            in1=qb[:, 4 * g:4 * g + 4, :].unsqueeze(1).broadcast_to([128, 32, 4, 64]), op=ALU.mult),
                            r=[('XK', t), 'qb'], w=['tmp4'])
                        S.dve(lambda e, t=t, g=g: e.tensor_reduce(
                            out=ss[:, t, 4 * g:4 * g + 4, :].rearrange("p r l -> p l r"), in_=tmp4, axis=AX.X, op=ALU.add),
                            r=['tmp4'], w=[('ss', t, g)])
                SSK = [('ss', t, g) for t in range(2) for g in range(2)]
                S.dve(lambda e: e.tensor_tensor(out=ss, in0=ss, in1=ABs, op=ALU.add), r=SSK + ['dsetup2'], w=['ss'])
                S.act(lambda e: e.activation(out=ss, in_=ss, func=AF.Exp), r=['ss'], w=['ss'])
                for t in range(2):
                    S.dve(lambda e, t=t: e.tensor_tensor(
                        out=ss[:, t, :, :].rearrange("p (g r) l -> p g (r l)", g=2),
                        in0=ss[:, t, :, :].rearrange("p (g r) l -> p g (r l)", g=2),
                        in1=mskd[:, t, :].unsqueeze(2).broadcast_to([128, 2, 128]), op=ALU.mult),
                        r=['ss', 'mskd'], w=['ss'])
                S.dve(lambda e: e.tensor_reduce(out=Pl, in_=ss, axis=AX.X, op=ALU.add), r=['ss'], w=['Pl'])

                def pvs(e, s=s):
                    ins = None
                    for g in range(2):
                        k = 0
                        for t in range(2):
                            for l in range(32):
                                ins = e.matmul(PS(4, 0, 4, g * 64, (g + 1) * 64), lhsT=ss[:, t, 4 * g:4 * g + 4, l],
                                               rhs=XV[t][:, l, g * 64:(g + 1) * 64], start=(k == 0), stop=False)
                                k += 1
                        ins = e.matmul(PS(4, 0, 4, g * 64, (g + 1) * 64), lhsT=pnew[0:1, 0, g, :, s],
                                       rhs=vrow[0:1, s, 0, g * 64:(g + 1) * 64], start=False, stop=True)
                        for t in range(2):
                            ins = e.matmul(PS(5, 0, 4, g, g + 1), lhsT=Pl[:, t, 4 * g:4 * g + 4], rhs=ones_f[:, 0:1],
                                           start=(t == 0), stop=False)
                        ins = e.matmul(PS(5, 0, 4, g, g + 1), lhsT=pnew[0:1, 0, g, :, s], rhs=ones_f[0:1, 0:1],
                                       start=False, stop=True)
                    return ins
                S.pe(pvs, r=['ss', 'Pl', ('XV', 0), ('XV', 1), 'pnew', ('vrow', s), 'ones_f'], w=[pk(4), pk(5)])
                S.dma('sp', lambda e, s=s: e.dma_start(out=Wn, in_=I['cache_win'][s * 512:(s + 1) * 512, :].rearrange(
                    "(p j) c -> p j c", j=4)), w=['Wn'])
                for g in range(2):
                    S.dve(lambda e, g=g: e.tensor_tensor(
                        out=tmp4[:, 0:4, :, :], in0=Wn[:, :, g * 64:(g + 1) * 64].unsqueeze(2).broadcast_to([128, 4, 4, 64]),
                        in1=qb[:, 4 * g:4 * g + 4, :].unsqueeze(1).broadcast_to([128, 4, 4, 64]), op=ALU.mult),
                        r=['Wn', 'qb'], w=['tmp4'])
                    S.dve(lambda e, g=g: e.tensor_reduce(out=sw[:, 4 * g:4 * g + 4, :].rearrange("p r l -> p l r"),
                                                         in_=tmp4[:, 0:4, :, :], axis=AX.X, op=ALU.add),
                          r=['tmp4'], w=[('sw', g)])
                S.dve(lambda e: e.tensor_tensor(out=sw, in0=sw, in1=ABw, op=ALU.add), r=[('sw', 0), ('sw', 1), 'dsetup2'],
                      w=['sw'])
                S.act(lambda e: e.activation(out=sw, in_=sw, func=AF.Exp), r=['sw'], w=['sw'])
                S.pool(lambda e: e.memset(sw[0:1, :, 0:1], 0.0), r=['sw'], w=['sw'])
                S.dve(lambda e: e.tensor_reduce(out=Plw, in_=sw, axis=AX.X, op=ALU.add), r=['sw'], w=['Plw'])

                def pvw(e, s=s):
                    ins = None
                    for g in range(2):
                        for l in range(4):
                            ins = e.matmul(PS(6, 0, 4, g * 64, (g + 1) * 64), lhsT=sw[:, 4 * g:4 * g + 4, l],
                                           rhs=Wn[:, l, 128 + g * 64:128 + (g + 1) * 64], start=(l == 0), stop=False)
                        ins = e.matmul(PS(6, 0, 4, g * 64, (g + 1) * 64), lhsT=pnew[0:1, 1, g, :, s],
                                       rhs=vrow[0:1, s, 1, g * 64:(g + 1) * 64], start=False, stop=True)
                        ins = e.matmul(PS(5, 0, 4, 2 + g, 3 + g), lhsT=Plw[:, 4 * g:4 * g + 4], rhs=ones_f[:, 0:1],
                                       start=True, stop=False)
                        ins = e.matmul(PS(5, 0, 4, 2 + g, 3 + g), lhsT=pnew[0:1, 1, g, :, s], rhs=ones_f[0:1, 0:1],
                                       start=False, stop=True)
                    return ins
                S.pe(pvw, r=['sw', 'Plw', 'Wn', 'pnew', ('vrow', s), 'ones_f'], w=[pk(6), pk(5)])
                S.dve(lambda e: e.reciprocal(out=rsd, in_=PS(5, 0, 4, 0, 4)), r=[pk(5)], w=['rsd'])
                for g in range(2):
                    S.dve(lambda e, g=g, s=s: e.tensor_scalar(out=accd[:, g, :], in0=PS(2, 0, 4, g * 64, (g + 1) * 64),
                                                              scalar1=gts[:, g * 3 + 0, s:s + 1], scalar2=None, op0=ALU.mult),
                          r=[pk(2), 'gts'], w=[('accd', g)])
                    for bi, (bank, col) in enumerate([(4, g), (6, 2 + g)]):
                        S.dve(lambda e, g=g, s=s, bi=bi, col=col: e.tensor_tensor(
                            out=rsd[:, col:col + 1], in0=rsd[:, col:col + 1], in1=gts[:, g * 3 + 1 + bi, s:s + 1], op=ALU.mult),
                            r=['rsd', 'gts'], w=['rsd'])
                        S.dve(lambda e, g=g, bank=bank, col=col: e.scalar_tensor_tensor(
                            out=accd[:, g, :], in0=PS(bank, 0, 4, g * 64, (g + 1) * 64), scalar=rsd[:, col:col + 1],
                            in1=accd[:, g, :], op0=ALU.mult, op1=ALU.add), r=[pk(bank), 'rsd', ('accd', g)], w=[('accd', g)])
                    S.pe(lambda e, g=g: e.transpose(out=PS(7, 0, 64, 128 + 4 * g, 132 + 4 * g), in_=accd[:, g, :],
                                                    identity=ident_f[0:4, 0:4]), r=[('accd', g), 'ident_f'], w=[pk(7)])
                    S.dve(lambda e, g=g, s=s: e.tensor_tensor(
                        out=ybT[0:64, 4 * g:4 * g + 4, T + s], in0=PS(7, 0, 64, 128 + 4 * g, 132 + 4 * g),
                        in1=ybT[0:64, 4 * g:4 * g + 4, T + s], op=ALU.mult),
                        r=[pk(7)] + [('ybT', 4 * g + r_, T) for r_ in range(4)],
                        w=[('ybT', 4 * g + r_, T) for r_ in range(4)] + ['dec_yb'])
            if os.environ.get('KDEV_DUMP'):
                yp = O['y_prompt']
                S.dma('sp', lambda e: e.dma_start(out=yp[0:128, :], in_=Xb[0][:, 0:4, :].rearrange("p l c -> p (l c)")), r=[('Xb', 0)])
                S.dma('sp', lambda e: e.dma_start(out=yp[128:256, 0:512], in_=kvc.rearrange("p t c -> p (t c)")), r=[('kvc', 0), ('kvc', 1)])
                S.dma('sp', lambda e: e.dma_start(out=yp[256:384, 0:16], in_=sc.rearrange("p t h -> p (t h)")), r=['sc'])
                S.dma('sp', lambda e: e.dma_start(out=yp[384:512, 0:512], in_=ss.rearrange("p t h l -> p (t h l)")), r=['ss'])
                S.dma('sp', lambda e: e.dma_start(out=yp[512:516, 0:4], in_=rsd), r=['rsd'])
                S.dma('sp', lambda e: e.dma_start(out=yp[516:520, 0:128], in_=accd.rearrange("p g d -> p (g d)")), r=[('accd', 0), ('accd', 1)])
                S.dma('sp', lambda e: e.dma_start(out=yp[640:768, 0:8], in_=idx_f), r=['idx_i'])
                S.dma('sp', lambda e: e.dma_start(out=yp[768:896, 0:4], in_=mskd.rearrange("p t g -> p (t g)")), r=['mskd'])
                S.dma('sp', lambda e: e.dma_start(out=yp[896:898, 0:129], in_=impd), r=['impd'])
                S.dma('sp', lambda e: e.dma_start(out=yp[900:1028, 0:512], in_=qb.rearrange("p h d -> p (h d)")), r=['qb'])
                S.dma('sp', lambda e: e.dma_start(out=yp[1028:1029, 0:64], in_=pnew.rearrange("p a g r s -> p (a g r s)")), r=['pnew'])
                S.dma('sp', lambda e: e.dma_start(out=yp[1030:1034, 0:24], in_=gts.rearrange("p a s -> p (a s)")), r=['gts'])
            S.barrier()
            A.release()
            stage_end(11)
            ALPHA = float((2 * 2) ** 0.25)
            x1T = xT
            A.mark()
            wo_a = A.alloc([128, 4, DM], BF16)
            wo_b = A.alloc([64, 8, DM], BF16)
            lnp = A.alloc([128, 2, DM], F32)
            xres = [A.alloc([128, DM], F32) for _ in range(2)]
            x1b = [A.alloc([128, DM], BF16) for _ in range(2)]
            bst = A.alloc([128, 2, 6], F32)
            mv = A.alloc([128, 4], F32)
            for j in range(4):
                S.dma('pool', lambda e, j=j: e.dma_start(out=wo_a[:, j, :], in_=I['w_out_even'][j * 128:(j + 1) * 128, :]),
                      w=[('wo_a', j)])
            for h in range(8):
                S.dma('pool', lambda e, h=h: e.dma_start(out=wo_b[:, h, :],
                                                         in_=I['w_out_even'][512 + h * 64:512 + (h + 1) * 64, :]),
                      w=[('wo_b', h)])
            S.dma('sp', lambda e: e.dma_start(out=lnp[:, 0, :], in_=I['ln_g'][0:1, :].partition_broadcast(128)), w=[('lnp', 0)])
            S.dma('sp', lambda e: e.dma_start(out=lnp[:, 1, :], in_=I['ln_b'][0:1, :].partition_broadcast(128)), w=[('lnp', 1)])
            WO = [('wo_a', j) for j in range(4)] + [('wo_b', h) for h in range(8)]

            def resid_ln(n, rows, c0, xsrc, yk, lidx, ya_, yb_, outs):
                b = n % 2
                S.dma('sp', lambda e: e.dma_start(out=xres[b][0:rows, :], in_=xsrc), w=[('xres', b)])

                def mm(e):
                    ins = None
                    for cb in range(2):
                        k = 0
                        tot = len(ya_) + len(yb_)
                        for (lt, rt) in ya_ + yb_:
                            ins = e.matmul(PS(cb, 0, rows, 0, 512), lhsT=lt[:, c0:c0 + rows], rhs=rt[:, cb * 512:(cb + 1) * 512],
                                           start=(k == 0), stop=(k == tot - 1))
                            k += 1
                    return ins
                S.pe(mm, r=yk + WO, w=[pk(0), pk(1)])
                for cb in range(2):
                    S.dve(lambda e, cb=cb: e.scalar_tensor_tensor(
                        out=xres[b][0:rows, cb * 512:(cb + 1) * 512], in0=xres[b][0:rows, cb * 512:(cb + 1) * 512],
                        scalar=ALPHA, in1=PS(cb, 0, rows, 0, 512), op0=ALU.mult, op1=ALU.add),
                        r=[('xres', b), pk(cb)], w=[('xres', b)])
                    S.dve(lambda e, cb=cb: e.bn_stats(out=bst[0:rows, cb, :], in_=xres[b][0:rows, cb * 512:(cb + 1) * 512]),
                          r=[('xres', b)], w=[('bst', cb)])
                S.dve(lambda e: e.bn_aggr(out=mv[0:rows, 0:2], in_=bst[0:rows, :, :].rearrange("p a b -> p (a b)")),
                      r=[('bst', 0), ('bst', 1)], w=['mv'])
                S.dve(lambda e: e.tensor_scalar(out=mv[0:rows, 2:3], in0=mv[0:rows, 1:2], scalar1=1e-5, scalar2=None,
                                                op0=ALU.add), r=['mv'], w=['mv2'])
                S.act(lambda e: e.activation(out=mv[0:rows, 2:3], in_=mv[0:rows, 2:3], func=AF.Ln), r=['mv2'], w=['mv2'])
                S.act(lambda e: e.activation(out=mv[0:rows, 2:3], in_=mv[0:rows, 2:3], func=AF.Exp, scale=-0.5),
                      r=['mv2'], w=['mv2'])
                S.dve(lambda e: e.tensor_scalar(out=xres[b][0:rows, :], in0=xres[b][0:rows, :], scalar1=mv[0:rows, 0:1],
                                                scalar2=mv[0:rows, 2:3], op0=ALU.subtract, op1=ALU.mult),
                      r=[('xres', b), 'mv', 'mv2'], w=[('xres', b)])
                S.dve(lambda e: e.tensor_tensor(out=xres[b][0:rows, :], in0=xres[b][0:rows, :], in1=lnp[0:rows, 2 * lidx, :],
                                                op=ALU.mult), r=[('xres', b), ('lnp', 2 * lidx)], w=[('xres', b)])
                S.pool(lambda e: e.tensor_tensor(out=xres[b][0:rows, :], in0=xres[b][0:rows, :],
                                                 in1=lnp[0:rows, 2 * lidx + 1, :], op=ALU.add),
                       r=[('xres', b), ('lnp', 2 * lidx + 1)], w=[('xres', b)])
                outs(b)

            def l0_outs(n, rows, c0):
                def f(b):
                    S.dma('sp', lambda e: e.dma_start(out=x1_scr[c0:c0 + rows, :], in_=xres[b][0:rows, :]),
                          r=[('xres', b)], w=[('x1scr', n)])
                    if n == 15:
                        S.dma('sp', lambda e: e.dma_start(out=O['shift_p'], in_=xres[b][127:128, :]), r=[('xres', b)])
                    if n == 16:
                        S.dma('sp', lambda e: e.dma_start(out=O['shift_s'], in_=xres[b][0:rows, :]), r=[('xres', b)])
                    S.act(lambda e: e.copy(out=x1b[b][0:rows, :], in_=xres[b][0:rows, :]), r=[('xres', b)], w=[('x1b', b)])
                    bank = 2 + (n % 2)

                    def tr(e):
                        ins = None
                        for c in range(8):
                            ins = e.transpose(out=PSB(bank, 0, 128, c * rows, (c + 1) * rows),
                                              in_=x1b[b][0:rows, c * 128:(c + 1) * 128], identity=ident_b[0:rows, 0:rows])
                        return ins
                    S.pe(tr, r=[('x1b', b), 'ident_b'], w=[pk(bank)])
                    S.dve(lambda e: e.tensor_copy(out=x1T[:, :, c0:c0 + rows],
                                                  in_=PSB(bank, 0, 128, 0, 8 * rows).rearrange("p (c t) -> p c t", c=8)),
                          r=[pk(bank)], w=[('x1T', n)])
                return f

            for n in range(17):
                rows = 128 if n < 16 else NS
                c0 = n * 128 if n < 16 else T
                xsrc = I['xp'][c0:c0 + 128, :] if n < 16 else I['xs']
                blk = (c0 // 512) * 512 if n < 16 else T
                yk = [('yaT', j, blk) for j in range(4)] + [('ybT', h, blk) for h in range(8)]
                if n == 16:
                    yk += ['dec_yb']
                ya_ = [(yaT[:, j, :], wo_a[:, j, :]) for j in range(4)]
                yb_ = [(ybT[0:64, h, :], wo_b[0:64, h, :]) for h in range(8)]
                resid_ln(n, rows, c0, xsrc, yk, 0, ya_, yb_, l0_outs(n, rows, c0))
            S.barrier()
            A.release()
            stage_end(10)
            S.barrier()
            A.top = L1_BASE
            A.marks = []
            A.mark()
            dT = A.alloc([128, 8, TT], BF16)
            xsh = A.alloc([128, 8, TT], BF16)
            muT = A.alloc([128, 6, 8], F32)
            vecs = A.alloc([128, 5, 8], F32)
            shb = A.alloc([NS, DM], BF16)
            stg = [A.alloc([128, 1024], F32) for _ in range(3)]
            wl1 = [A.alloc([128, 8, 128], BF16) for _ in range(3)]
            wfull = A.alloc([128, 8, 1024], BF16)
            lo1 = A.alloc([128, 8, 64], BF16)
            lo2 = A.alloc([64, 1024], BF16)
            t1T = A.alloc([64, TT], BF16)
            S.dma('sp', lambda e: e.dma_start(out=muT, in_=I['mu_c'].rearrange("i (c p) -> p i c", p=128),
                                              allow_slow_non_contiguous=True), w=['muT'])
            for i, nm in enumerate(['w0', 'a0', 'k_k', 'k_a', 'r_k']):
                S.dma('sp', lambda e, i=i, nm=nm: e.dma_start(out=vecs[:, i, :], in_=I[nm].rearrange("o (c p) -> p (o c)", p=128),
                                                              allow_slow_non_contiguous=True), w=[('vecs', i)])
            S.dma('pool', lambda e: e.dma_start(out=shb, in_=I['state_shift']), w=['shb'])

            def trsh(e):
                ins = None
                for c in range(8):
                    ins = e.transpose(out=PSB(0, 0, 128, c * NS, (c + 1) * NS), in_=shb[0:NS, c * 128:(c + 1) * 128],
                                      identity=ident_b[0:NS, 0:NS])
                return ins
            S.pe(trsh, r=['shb', 'ident_b'], w=[pk(0)])
            X1K = [('x1T', n) for n in range(17)]
            S.dve(lambda e: e.tensor_tensor(out=dT[:, :, T:TT], in0=PSB(0, 0, 128, 0, 8 * NS).rearrange("p (c t) -> p c t", c=8),
                                            in1=x1T[:, :, T:TT], op=ALU.subtract), r=[pk(0)] + X1K, w=['dT'])
            S.dve(lambda e: e.tensor_tensor(out=dT[:, :, 1:T], in0=x1T[:, :, 0:T - 1], in1=x1T[:, :, 1:T], op=ALU.subtract),
                  r=X1K, w=['dT'])
            S.dve(lambda e: e.tensor_scalar(out=dT[:, :, 0:1], in0=x1T[:, :, 0:1], scalar1=-1.0, scalar2=None, op0=ALU.mult),
                  r=X1K, w=['dT'])

            def mk_xsh(i):
                for c in range(8):
                    eng = S.dve if c % 2 == 0 else S.pool
                    if c % 2 == 0:
                        S.dve(lambda e, c=c: e.scalar_tensor_tensor(out=xsh[:, c, :], in0=dT[:, c, :], scalar=muT[:, i, c:c + 1],
                                                                    in1=x1T[:, c, :], op0=ALU.mult, op1=ALU.add),
                              r=['dT', 'muT'] + X1K, w=[('xsh', c)])
                    else:
                        S.pool(lambda e, c=c: e.tensor_scalar(out=xsh[:, c, :], in0=dT[:, c, :], scalar1=muT[:, i, c:c + 1],
                                                              scalar2=None, op0=ALU.mult), r=['dT', 'muT'], w=[('xsh', c)])
                        S.pool(lambda e, c=c: e.tensor_tensor(out=xsh[:, c, :], in0=xsh[:, c, :], in1=x1T[:, c, :], op=ALU.add),
                               r=[('xsh', c)] + X1K, w=[('xsh', c)])
            XSH = [('xsh', c) for c in range(8)]
            wrk_t = I['w_rkvz'].rearrange("(i c p) f -> i p c f", i=4, p=128)
            sgc = [0]
            wlc = [0]

            def proj_fm1(widx, dst, func, bias_i):
                for pr in range(8):
                    wi = wlc[0] % 3
                    wlc[0] += 1
                    S.dma('pool', lambda e, wi=wi, pr=pr: e.dma_start(out=wl1[wi], in_=wrk_t[widx][:, :, pr * 128:(pr + 1) * 128]),
                          w=[('wl1', wi)])
                    si = sgc[0] % 3
                    sgc[0] += 1
                    for bi_, (c0, n) in enumerate(BLKS):
                        bank = pbank[0] % 4
                        pbank[0] += 1

                        def mm(e, wi=wi, c0=c0, n=n, bank=bank):
                            ins = None
                            for c in range(8):
                                ins = e.matmul(PS(bank, 0, 128, 0, n), lhsT=wl1[wi][:, c, :], rhs=xsh[:, c, c0:c0 + n],
                                               start=(c == 0), stop=(c == 7))
                            return ins
                        S.pe(mm, r=[('wl1', wi)] + XSH, w=[pk(bank)])
                        cc0 = c0 if c0 < T else 0
                        sdst = stg[si][:, cc0 % 1024:cc0 % 1024 + n] if c0 < T else stg[si][:, 0:n]
                        S.act(lambda e, bank=bank, n=n, sdst=sdst: e.copy(out=sdst, in_=PS(bank, 0, 128, 0, n)),
                              r=[pk(bank)], w=[('stg', si)])
                        if bi_ in (1, 3, 4):
                            lo = {1: 0, 3: 1024, 4: T}[bi_]
                            wdt = 1024 if bi_ != 4 else NS
                            S.dma('sp', lambda e, si=si, pr=pr, lo=lo, wdt=wdt: e.dma_start(out=dst[:, pr, lo:lo + wdt],
                                                                                           in_=stg[si][:, 0:wdt]),
                                  r=[('stg', si)], w=[('scr', widx, pr)])
                            if bi_ != 4:
                                si = sgc[0] % 3
                                sgc[0] += 1

            def proj_tm1(widx, dst, silu):
                for c in range(8):
                    S.dma('pool', lambda e, c=c: e.dma_start(out=wfull[:, c, :], in_=wrk_t[widx][:, c, :]), w=[('wfull', c)])
                WF = [('wfull', c) for c in range(8)]
                for n in range(17):
                    rows = 128 if n < 16 else NS
                    c0 = n * 128 if n < 16 else T
                    si = sgc[0] % 3
                    sgc[0] += 1

                    def mm(e, rows=rows, c0=c0):
                        ins = None
                        for cb in range(2):
                            for c in range(8):
                                ins = e.matmul(PS(4 + cb, 0, rows, 0, 512), lhsT=xsh[:, c, c0:c0 + rows],
                                               rhs=wfull[:, c, cb * 512:(cb + 1) * 512], start=(c == 0), stop=(c == 7))
                        return ins
                    S.pe(mm, r=WF + XSH, w=[pk(4), pk(5)])
                    for cb in range(2):
                        if silu:
                            S.act(lambda e, cb=cb, rows=rows, si=si: e.activation(
                                out=stg[si][0:rows, cb * 512:(cb + 1) * 512], in_=PS(4 + cb, 0, rows, 0, 512), func=AF.Silu),
                                r=[pk(4 + cb)], w=[('stg', si)])
                        else:
                            S.act(lambda e, cb=cb, rows=rows, si=si: e.copy(out=stg[si][0:rows, cb * 512:(cb + 1) * 512],
                                                                            in_=PS(4 + cb, 0, rows, 0, 512)),
                                  r=[pk(4 + cb)], w=[('stg', si)])
                    S.dma('sp', lambda e, rows=rows, c0=c0, si=si: e.dma_start(out=dst[c0:c0 + rows, :], in_=stg[si][0:rows, :]),
                          r=[('stg', si)], w=[('scrt', widx, n)])

            def proj_lora(mi, w1n, w2n, vi, dst, use_tanh):
                S.dma('pool', lambda e: e.dma_start(out=lo1, in_=I[w1n].rearrange("(c p) f -> p c f", p=128)), w=['lo1'])
                S.dma('pool', lambda e: e.dma_start(out=lo2, in_=I[w2n]), w=['lo2'])
                for (c0, n) in BLKS:
                    bank = pbank[0] % 4
                    pbank[0] += 1

                    def mm(e, c0=c0, n=n, bank=bank):
                        ins = None
                        for c in range(8):
                            ins = e.matmul(PS(bank, 0, 64, 0, n), lhsT=lo1[:, c, :], rhs=xsh[:, c, c0:c0 + n],
                                           start=(c == 0), stop=(c == 7))
                        return ins
                    S.pe(mm, r=['lo1'] + XSH, w=[pk(bank)])
                    if use_tanh:
                        S.act(lambda e, c0=c0, n=n, bank=bank: e.activation(out=t1T[:, c0:c0 + n], in_=PS(bank, 0, 64, 0, n),
                                                                            func=AF.Tanh), r=[pk(bank)], w=[('t1T', c0)])
                    else:
                        S.act(lambda e, c0=c0, n=n, bank=bank: e.copy(out=t1T[:, c0:c0 + n], in_=PS(bank, 0, 64, 0, n)),
                              r=[pk(bank)], w=[('t1T', c0)])
                T1K = [('t1T', c0) for (c0, n) in BLKS]
                for pr in range(8):
                    si = sgc[0] % 3
                    sgc[0] += 1
                    for bi_, (c0, n) in enumerate(BLKS):
                        bank = pbank[0] % 4
                        pbank[0] += 1
                        S.pe(lambda e, pr=pr, c0=c0, n=n, bank=bank: e.matmul(
                            PS(bank, 0, 128, 0, n), lhsT=lo2[:, pr * 128:(pr + 1) * 128], rhs=t1T[:, c0:c0 + n],
                            start=True, stop=True), r=['lo2'] + T1K, w=[pk(bank)])
                        sdst = stg[si][:, c0 % 1024:c0 % 1024 + n] if c0 < T else stg[si][:, 0:n]
                        S.act(lambda e, bank=bank, n=n, sdst=sdst, pr=pr: e.activation(
                            out=sdst, in_=PS(bank, 0, 128, 0, n), func=AF.Sigmoid, bias=vecs[:, vi, pr:pr + 1]),
                            r=[pk(bank), ('vecs', vi)], w=[('stg', si)])
                        if bi_ in (1, 3, 4):
                            lo = {1: 0, 3: 1024, 4: T}[bi_]
                            wdt = 1024 if bi_ != 4 else NS
                            S.dma('sp', lambda e, si=si, pr=pr, lo=lo, wdt=wdt: e.dma_start(out=dst[:, pr, lo:lo + wdt],
                                                                                           in_=stg[si][:, 0:wdt]),
                                  r=[('stg', si)], w=[('scr', mi, pr)])
                            if bi_ != 4:
                                si = sgc[0] % 3
                                sgc[0] += 1

            mk_xsh(0)
            proj_fm1(0, r_scr, None, None)
            mk_xsh(1)
            proj_fm1(1, k_scr, None, None)
            mk_xsh(2)
            proj_tm1(2, v_scr, False)
            mk_xsh(3)
            proj_tm1(3, z_scr, True)
            mk_xsh(4)
            proj_lora(4, 'w1', 'w2', 0, sw_scr, True)
            mk_xsh(5)
            proj_lora(5, 'a1', 'a2', 1, a_scr, False)
            S.barrier()
            A.release()
            stage_end(12)
            A.top = xT_off
            A.marks = []
            A.mark()
            C = 64
            NEG_E05 = -float(np.exp(-0.5))
            wo_o = A.alloc([128, 8, DM], BF16)
            for c in range(8):
                S.dma('pool', lambda e, c=c: e.dma_start(out=wo_o[:, c, :], in_=I['w_out_odd'][c * 128:(c + 1) * 128, :]),
                      w=[('wo_o', c)])
            WOO = [('wo_o', c) for c in range(8)]
            gnp = A.alloc([64, 4, DM], F32)
            S.dma('sp', lambda e: e.dma_start(out=gnp[:, 0, :], in_=I['gn_g'].partition_broadcast(64)), w=[('gnp', 0)])
            S.dma('sp', lambda e: e.dma_start(out=gnp[:, 1, :], in_=I['gn_b'].partition_broadcast(64)), w=[('gnp', 1)])
            S.dma('sp', lambda e: e.dma_start(out=gnp[:, 2, :], in_=I['ln_g'][1:2, :].partition_broadcast(64)), w=[('gnp', 2)])
            S.dma('sp', lambda e: e.dma_start(out=gnp[:, 3, :], in_=I['ln_b'][1:2, :].partition_broadcast(64)), w=[('gnp', 3)])
            vec1 = A.alloc([64, 5, 16], F32)
            for i, nm in enumerate(['w0', 'a0', 'k_k', 'k_a', 'r_k']):
                S.dma('sp', lambda e, i=i, nm=nm: e.dma_start(out=vec1[:, i, :], in_=I[nm].rearrange("o (ph c) -> c (o ph)", c=64),
                                                              allow_slow_non_contiguous=True), w=['vec1'])
            m_lt = A.alloc([64, 64], F32)
            m_le = A.alloc([64, 64], F32)
            m_gt = A.alloc([64, 64], F32)
            blk1 = A.alloc([64, 64], F32)
            sel2 = A.alloc([64, 1], BF16)
            ones_t = A.alloc([64, 64], F32)

            def l1const(e):
                e.memset(ones_t, 1.0)
                e.memset(m_lt, 1.0)
                e.memset(m_le, 1.0)
                e.memset(m_gt, 1.0)
                e.affine_select(out=m_lt, in_=m_lt, pattern=[[1, 64]], compare_op=ALU.is_ge, fill=freg(e, 0.0), base=-1,
                                channel_multiplier=-1)
                e.affine_select(out=m_le, in_=m_le, pattern=[[1, 64]], compare_op=ALU.is_ge, fill=freg(e, 0.0), base=0,
                                channel_multiplier=-1)
                e.affine_select(out=m_gt, in_=m_gt, pattern=[[-1, 64]], compare_op=ALU.is_ge, fill=freg(e, 0.0), base=-1,
                                channel_multiplier=1)
                e.memset(blk1, 1.0)
                return e.memset(sel2, 1.0)
            S.pool(l1const, w=['l1c'])

            ST = A.alloc([64, 16, 64], F32)
            STb = A.alloc([64, 16, 64], BF16)
            fr = A.alloc([64, 16, C], F32)
            fk = A.alloc([64, 16, C], F32)
            fw = A.alloc([64, 16, C], F32)
            fa = A.alloc([64, 16, C], F32)
            cl = A.alloc([64, 16, C], F32)
            L1 = A.alloc([64, 16, C], F32)
            L2 = A.alloc([64, 16, C], F32)
            L3 = A.alloc([64, 16, C], F32)
            Lend = A.alloc([64, 16], F32)
            t_a = A.alloc([64, 16, C], F32)
            t_b = A.alloc([64, 16, C], F32)
            kap = fw
            kmd = A.alloc([64, 16, C], F32)
            kt_b = A.alloc([64, 16, C], BF16)
            bt_b = A.alloc([64, 16, C], BF16)
            ktl_b = A.alloc([64, 16, C], BF16)
            rt_b = A.alloc([64, 16, C], BF16)
            bh_f = A.alloc([64, 16, C], BF16)
            kh_f = A.alloc([64, 16, C], BF16)
            pr_b = A.alloc([64, 16, C], BF16)
            Bh = A.alloc([64, DM], BF16)
            Kh = A.alloc([64, DM], BF16)
            Vf2 = [A.alloc([64, DM], F32) for _ in range(2)]
            Vb = A.alloc([64, DM], BF16)
            Zf2 = [A.alloc([64, DM], F32) for _ in range(2)]
            gN = A.alloc([64, 16, 64], BF16)
            gNT = A.alloc([64, 16, 64], BF16)
            gAk = A.alloc([64, 16, 64], BF16)
            gBb = A.alloc([64, 16, 64], BF16)
            gBk = A.alloc([64, 16, 64], BF16)
            gX = [A.alloc([64, 16, 64], BF16) for _ in range(2)]
            gP = [A.alloc([64, 16, 64], BF16) for _ in range(2)]
            gPT = [A.alloc([64, 16, 64], BF16) for _ in range(2)]
            Rm = A.alloc([64, DM], BF16)
            Ub = A.alloc([64, DM], BF16)
            Yf = A.alloc([64, DM], F32)
            Yc = A.alloc([64, DM], F32)
            st1 = A.alloc([64, 16], F32)
            st2 = A.alloc([64, 16], F32)
            rkb2 = [A.alloc([64, 16], F32) for _ in range(2)]
            gb = A.alloc([64, DM], BF16)
            gT = A.alloc([128, 8, 64], BF16)
            xr2 = [A.alloc([64, DM], F32) for _ in range(2)]
            bst1 = A.alloc([64, 2, 6], F32)
            mv1 = A.alloc([64, 4], F32)
            wko = A.alloc([64, 16, 64], F32)
            stin2 = A.alloc([128, 8, 64], F32)

            def hp_rows(ap3, h):
                return ap3[:, h, :]

            def chunk(cols, ncol, first, xsrc_rows, yout, yrows, ck, par, post_seq=None):
                Vf, Zf, xr, rkb = Vf2[par], Zf2[par], xr2[par], rkb2[par]
                kVf, kZf, kxr, krkb = ('Vf', par), ('Zf', par), ('xr', par), ('rkb', par)
                pad = ncol < C
                if pad:
                    cols = cols - (C - 1)
                for (tile_, scr, nm) in ((fr, r_scr, 'fr'), (fk, k_scr, 'fk'), (fw, sw_scr, 'fw'), (fa, a_scr, 'fa')):
                    S.dma('sp', lambda e, tile_=tile_, scr=scr: e.dma_start(
                        out=tile_.rearrange("c (pr hp) t -> c pr hp t", hp=2),
                        in_=scr.rearrange("(hp c) pr t -> c pr hp t", hp=2)[:, :, :, cols:cols + C]), r=[('scr_all',)], w=[nm])
                    if pad:
                        S.pool(lambda e, tile_=tile_: e.memset(tile_[:, :, 0:C - 1], 0.0), r=[nm], w=[nm])
                S.dma('sp', lambda e: e.dma_start(out=Vf, in_=v_scr[cols:cols + C, :]), r=[('scr_all',)], w=[kVf])
                S.dma('sp', lambda e: e.dma_start(out=Zf, in_=z_scr[cols:cols + C, :]), r=[('scr_all',)], w=[kZf])
                S.dma('sp', lambda e: e.dma_start(out=xr, in_=x1_scr[cols:cols + C, :]), r=[('scr_all',)], w=[kxr])
                if pad:
                    S.pool(lambda e: e.memset(Vf[0:C - 1, :], 0.0), r=[kVf], w=[kVf])
                    S.pool(lambda e: e.memset(Zf[0:C - 1, :], 0.0), r=[kZf], w=[kZf])
                S.act(lambda e: e.copy(out=Vb, in_=Vf), r=[kVf], w=['Vb'])
                S.dve(lambda e: e.tensor_scalar(out=fw, in0=fw, scalar1=NEG_E05, scalar2=None, op0=ALU.mult), r=['fw'], w=['fw'])

                def scans(e):
                    ins = None
                    for pr_ in range(16):
                        ins = e.tensor_tensor_scan(out=cl[:, pr_, :], data0=ones_t, data1=fw[:, pr_, :], initial=0.0,
                                                   op0=ALU.mult, op1=ALU.add)
                    return ins
                S.dve(scans, r=['fw', 'l1c'], w=['cl'])
                S.act(lambda e: e.activation(out=L1, in_=cl, func=AF.Exp), r=['cl'], w=['L1'])
                S.act(lambda e: e.activation(out=L3, in_=cl, func=AF.Exp, scale=-1.0), r=['cl'], w=['L3'])
                S.dve(lambda e: e.tensor_tensor(out=t_a, in0=cl, in1=fw, op=ALU.subtract), r=['cl', 'fw'], w=['t_a'])
                S.act(lambda e: e.activation(out=L2, in_=t_a, func=AF.Exp), r=['t_a'], w=['L2'])
                S.dve(lambda e: e.tensor_copy(out=Lend, in_=L1[:, :, C - 1]), r=['L1'], w=['Lend'])
                S.dve(lambda e: e.tensor_tensor(out=kap, in0=fk, in1=vec1[:, 2, :].unsqueeze(2).broadcast_to([64, 16, C]),
                                                op=ALU.mult), r=['fk', 'vec1'], w=['fw'])
                S.dve(lambda e: e.tensor_tensor(out=t_b, in0=kap, in1=kap, op=ALU.mult), r=['fw'], w=['t_b'])
                def ssqm(e):
                    e.matmul(PS(0, 0, 64, 0, 512), lhsT=blk1, rhs=t_b[:, 0:8, :].rearrange("p a t -> p (a t)"), start=True, stop=True)
                    return e.matmul(PS(1, 0, 64, 0, 512), lhsT=blk1, rhs=t_b[:, 8:16, :].rearrange("p a t -> p (a t)"),
                                    start=True, stop=True)
                S.pe(ssqm, r=['t_b', 'l1c'], w=[pk(0), pk(1)])
                for hb in range(2):
                    S.dve(lambda e, hb=hb: e.tensor_scalar(out=t_b[:, hb * 8:(hb + 1) * 8, :].rearrange("p a t -> p (a t)"),
                                                           in0=PS(hb, 0, 64, 0, 512), scalar1=1e-24, scalar2=None, op0=ALU.max),
                          r=[pk(hb)], w=['t_b'])
                S.act(lambda e: e.activation(out=t_b, in_=t_b, func=AF.Ln), r=['t_b'], w=['t_b'])
                S.act(lambda e: e.activation(out=t_b, in_=t_b, func=AF.Exp, scale=-0.5), r=['t_b'], w=['t_b'])
                S.dve(lambda e: e.tensor_tensor(out=kap, in0=kap, in1=t_b, op=ALU.mult), r=['fw', 't_b'], w=['fw'])
                S.dve(lambda e: e.tensor_scalar(out=t_a, in0=fa, scalar1=-1.0, scalar2=None, op0=ALU.add), r=['fa', 'L2'], w=['t_a'])
                S.dve(lambda e: e.tensor_tensor(out=t_a, in0=t_a, in1=vec1[:, 3, :].unsqueeze(2).broadcast_to([64, 16, C]),
                                                op=ALU.mult), r=['t_a', 'vec1'], w=['t_a'])
                S.dve(lambda e: e.scalar_tensor_tensor(out=kmd, in0=t_a, scalar=1.0, in1=fk, op0=ALU.add, op1=ALU.mult),
                      r=['t_a', 'fk'], w=['kmd'])
                S.dve(lambda e: e.tensor_tensor(out=kt_b, in0=kap, in1=L2, op=ALU.mult), r=['fw', 'L2'], w=['kt_b'])
                S.dve(lambda e: e.tensor_tensor(out=t_a, in0=kap, in1=fa, op=ALU.mult), r=['fw', 'fa', 'kmd'], w=['t_a'])
                S.dve(lambda e: e.tensor_tensor(out=t_a, in0=t_a, in1=L3, op=ALU.mult), r=['t_a', 'L3'], w=['t_a'])
                S.act(lambda e: e.copy(out=bt_b, in_=t_a), r=['t_a'], w=['bt_b'])
                S.dve(lambda e: e.tensor_tensor(out=bh_f, in0=t_a, in1=Lend.unsqueeze(2).broadcast_to([64, 16, C]), op=ALU.mult),
                      r=['t_a', 'Lend'], w=['bh_f'])
                S.dve(lambda e: e.tensor_tensor(out=t_b, in0=kmd, in1=L3, op=ALU.mult), r=['kmd', 'L3', 'fw'], w=['t_b'])
                S.act(lambda e: e.copy(out=ktl_b, in_=t_b), r=['t_b'], w=['ktl_b'])
                S.dve(lambda e: e.tensor_tensor(out=kh_f, in0=t_b, in1=Lend.unsqueeze(2).broadcast_to([64, 16, C]), op=ALU.mult),
                      r=['t_b', 'Lend'], w=['kh_f'])
                S.dve(lambda e: e.tensor_tensor(out=rt_b, in0=fr, in1=L1, op=ALU.mult), r=['fr', 'L1'], w=['rt_b'])
                S.dve(lambda e: e.tensor_tensor(out=t_b, in0=fr, in1=kmd, op=ALU.mult), r=['fr', 'kmd', 'ktl_b', 'kh_f'], w=['t_b'])
                S.dve(lambda e: e.tensor_tensor(out=pr_b, in0=t_b, in1=vec1[:, 4, :].unsqueeze(2).broadcast_to([64, 16, C]),
                                                op=ALU.mult), r=['t_b', 'vec1'], w=['pr_b'])
                for (src, dst_, nm, bank) in ((bh_f, Bh, 'Bh', 2), (kh_f, Kh, 'Kh', 3)):
                    def trb(e, src=src, bank=bank):
                        ins = None
                        for h in range(16):
                            ins = e.transpose(out=PSB(bank, 0, 64, h * 64, (h + 1) * 64), in_=src[:, h, :],
                                              identity=ident_b[0:64, 0:64])
                        return ins
                    S.pe(trb, r=[nm.lower() + '_f' if False else ('bh_f' if nm == 'Bh' else 'kh_f'), 'ident_b'], w=[pk(bank)])
                    S.act(lambda e, dst_=dst_, bank=bank: e.copy(out=dst_, in_=PSB(bank, 0, 64, 0, 1024)), r=[pk(bank)], w=[nm])

                def rkm(e):
                    ins = None
                    for h in range(16):
                        ins = e.matmul(PS(1, 0, 64, 2 * h, 2 * h + 1), lhsT=pr_b[:, h, :], rhs=sel2, start=True, stop=True)
                    return ins
                S.pe(rkm, r=['pr_b', 'l1c'], w=[pk(1)])
                S.act(lambda e: e.copy(out=rkb, in_=PS(1, 0, 64, 0, 32).rearrange("p (h two) -> p h two", two=2)[:, :, 0]),
                      r=[pk(1)], w=[krkb])
                def gram(lt, rt, dst_, mask, banks, nm, deps):
                    def gm(e):
                        ins = None
                        for h in range(16):
                            ins = e.matmul(PS(banks[h // 8], 0, 64, (h % 8) * 64, (h % 8 + 1) * 64), lhsT=hp_rows(lt, h),
                                           rhs=hp_rows(rt, h), start=True, stop=True)
                        return ins
                    S.pe(gm, r=deps, w=[pk(banks[0]), pk(banks[1])])
                    for hb in range(2):
                        S.dve(lambda e, hb=hb: e.tensor_tensor(
                            out=dst_[:, hb * 8:(hb + 1) * 8, :], in0=PS(banks[hb], 0, 64, 0, 512).rearrange("p (h t) -> p h t", h=8),
                            in1=mask.unsqueeze(1).broadcast_to([64, 8, 64]), op=ALU.mult),
                            r=[pk(banks[hb]), 'l1c'], w=[(nm, hb)])
                gram(bt_b, kt_b, gN, m_lt, (4, 5), 'gN', ['bt_b', 'kt_b'])
                gram(kt_b, bt_b, gNT, m_gt, (6, 7), 'gNT', ['bt_b', 'kt_b'])
                gram(ktl_b, kt_b, gAk, m_lt, (4, 5), 'gAk', ['ktl_b', 'kt_b'])
                gram(bt_b, rt_b, gBb, m_le, (6, 7), 'gBb', ['bt_b', 'rt_b'])
                gram(ktl_b, rt_b, gBk, m_le, (4, 5), 'gBk', ['ktl_b', 'rt_b'])
                for hb in range(2):
                    S.dve(lambda e, hb=hb: e.scalar_tensor_tensor(
                        out=gX[0][:, hb * 8:(hb + 1) * 8, :], in0=gN[:, hb * 8:(hb + 1) * 8, :], scalar=-1.0,
                        in1=ident_b[0:64, 0:64].unsqueeze(1).broadcast_to([64, 8, 64]), op0=ALU.mult, op1=ALU.add),
                        r=[('gN', hb), 'ident_b'], w=[('gX', 0, hb)])
                Pc, PTc, Pk, PTk = gN, gNT, 'gN', 'gNT'
                xi = 0
                for lev in range(5):
                    Pn, PTn = gP[lev % 2], gPT[lev % 2]
                    Pnk, PTnk = ('gP', lev % 2), ('gPT', lev % 2)

                    def sq(e, Pc=Pc, PTc=PTc):
                        ins = None
                        for h in range(16):
                            ins = e.matmul(PS(h // 8, 0, 64, (h % 8) * 64, (h % 8 + 1) * 64), lhsT=PTc[:, h, :], rhs=Pc[:, h, :],
                                           start=True, stop=True)
                        for h in range(16):
                            ins = e.matmul(PS(2 + h // 8, 0, 64, (h % 8) * 64, (h % 8 + 1) * 64), lhsT=Pc[:, h, :],
                                           rhs=PTc[:, h, :], start=True, stop=True)
                        return ins
                    S.pe(sq, r=[(Pk, 0), (Pk, 1), (PTk, 0), (PTk, 1)] if isinstance(Pk, str) else
                         [Pk + (0,), Pk + (1,), PTk + (0,), PTk + (1,)], w=[pk(0), pk(1), pk(2), pk(3)])
                    for hb in range(2):
                        S.act(lambda e, hb=hb, Pn=Pn: e.copy(out=Pn[:, hb * 8:(hb + 1) * 8, :].rearrange("p h t -> p (h t)"),
                                                             in_=PS(hb, 0, 64, 0, 512)), r=[pk(hb)], w=[Pnk + (hb,)])
                        S.dve(lambda e, hb=hb, PTn=PTn: e.tensor_copy(out=PTn[:, hb * 8:(hb + 1) * 8, :].rearrange("p h t -> p (h t)"),
                                                                      in_=PS(2 + hb, 0, 64, 0, 512)), r=[pk(2 + hb)], w=[PTnk + (hb,)])
                    Xo, Xn = gX[xi % 2], gX[(xi + 1) % 2]

                    def xm(e, PTn=PTn, Xo=Xo):
                        ins = None
                        for h in range(16):
                            ins = e.matmul(PS(4 + h // 8, 0, 64, (h % 8) * 64, (h % 8 + 1) * 64), lhsT=PTn[:, h, :], rhs=Xo[:, h, :],
                                           start=True, stop=True)
                        return ins
                    S.pe(xm, r=[PTnk + (0,), PTnk + (1,), ('gX', xi % 2, 0), ('gX', xi % 2, 1)], w=[pk(4), pk(5)])
                    for hb in range(2):
                        S.dve(lambda e, hb=hb, Xo=Xo, Xn=Xn: e.tensor_tensor(
                            out=Xn[:, hb * 8:(hb + 1) * 8, :].rearrange("p h t -> p (h t)"), in0=PS(4 + hb, 0, 64, 0, 512),
                            in1=Xo[:, hb * 8:(hb + 1) * 8, :].rearrange("p h t -> p (h t)"), op=ALU.add),
                            r=[pk(4 + hb), ('gX', xi % 2, hb)], w=[('gX', (xi + 1) % 2, hb)])
                    xi += 1
                    Pc, PTc, Pk, PTk = Pn, PTn, Pnk, PTnk
                Xf = gX[xi % 2]
                XFK = [('gX', xi % 2, 0), ('gX', xi % 2, 1)]
                yield 'pre'
                if first is not None:
                    first()

                def m1(e):
                    ins = None
                    for h in range(16):
                        o_ = PS(h // 8, 0, 64, (h % 8) * 64, (h % 8 + 1) * 64)
                        e.matmul(o_, lhsT=hp_rows(kt_b, h), rhs=hp_rows(STb, h), start=True, stop=False)
                        ins = e.matmul(o_, lhsT=gAk[:, h, :], rhs=Vb[:, h * 64:(h + 1) * 64], start=False, stop=True)
                    return ins
                S.pe(m1, r=['kt_b', 'STb', ('gAk', 0), ('gAk', 1), 'Vb'], w=[pk(0), pk(1)])
                for hb in range(2):
                    S.act(lambda e, hb=hb: e.activation(out=Rm[:, hb * 512:(hb + 1) * 512], in_=PS(hb, 0, 64, 0, 512),
                                                        func=AF.Copy, scale=-1.0), r=[pk(hb)], w=[('Rm', hb)])

                def m3(e):
                    ins = None
                    for h in range(16):
                        ins = e.matmul(PS(2 + h // 8, 0, 64, (h % 8) * 64, (h % 8 + 1) * 64), lhsT=Xf[:, h, :],
                                       rhs=Rm[:, h * 64:(h + 1) * 64], start=True, stop=True)
                    return ins
                S.pe(m3, r=XFK + [('Rm', 0), ('Rm', 1)], w=[pk(2), pk(3)])
                for hb in range(2):
                    S.act(lambda e, hb=hb: e.copy(out=Ub[:, hb * 512:(hb + 1) * 512], in_=PS(2 + hb, 0, 64, 0, 512)),
                          r=[pk(2 + hb)], w=[('Ub', hb)])

                def m4(e):
                    ins = None
                    for h in range(16):
                        o_ = PS(4 + h // 8, 0, 64, (h % 8) * 64, (h % 8 + 1) * 64)
                        e.matmul(o_, lhsT=hp_rows(rt_b, h), rhs=hp_rows(STb, h), start=True, stop=False)
                        e.matmul(o_, lhsT=gBb[:, h, :], rhs=Ub[:, h * 64:(h + 1) * 64], start=False, stop=False)
                        ins = e.matmul(o_, lhsT=gBk[:, h, :], rhs=Vb[:, h * 64:(h + 1) * 64], start=False, stop=True)
                    return ins
                S.pe(m4, r=['rt_b', 'STb', ('gBb', 0), ('gBb', 1), ('gBk', 0), ('gBk', 1), ('Ub', 0), ('Ub', 1), 'Vb'],
                     w=[pk(4), pk(5)])
                for hb in range(2):
                    S.act(lambda e, hb=hb: e.copy(out=Yf[:, hb * 512:(hb + 1) * 512], in_=PS(4 + hb, 0, 64, 0, 512)),
                          r=[pk(4 + hb)], w=[('Yf', hb)])

                def m5(e):
                    ins = None
                    for h in range(16):
                        o_ = PS(6 + h // 8, 0, 64, (h % 8) * 64, (h % 8 + 1) * 64)
                        e.matmul(o_, lhsT=Bh[:, h * 64:(h + 1) * 64], rhs=Ub[:, h * 64:(h + 1) * 64], start=True, stop=False)
                        ins = e.matmul(o_, lhsT=Kh[:, h * 64:(h + 1) * 64], rhs=Vb[:, h * 64:(h + 1) * 64], start=False, stop=True)
                    return ins
                S.pe(m5, r=['Bh', 'Kh', ('Ub', 0), ('Ub', 1), 'Vb'], w=[pk(6), pk(7)])
                S.dve(lambda e: e.tensor_tensor(out=ST, in0=ST, in1=Lend.unsqueeze(2).broadcast_to([64, 16, 64]), op=ALU.mult),
                      r=['ST', 'Lend', 'STb'], w=['ST'])
                for hb in range(2):
                    S.dve(lambda e, hb=hb: e.tensor_tensor(out=ST[:, hb * 8:(hb + 1) * 8, :], in0=ST[:, hb * 8:(hb + 1) * 8, :],
                                                           in1=PS(6 + hb, 0, 64, 0, 512).rearrange("p (a v) -> p a v", a=8), op=ALU.add),
                          r=['ST', pk(6 + hb)], w=['ST'])
                S.act(lambda e: e.copy(out=STb, in_=ST), r=['ST'], w=['STb'])
                if post_seq is not None:
                    post_seq()
                yield 'seq'
                Y3 = Yf.rearrange("p (h v) -> p h v", h=16)
                Yc3 = Yc.rearrange("p (h v) -> p h v", h=16)
                YK = [('Yf', 0), ('Yf', 1)]
                S.dve(lambda e: e.tensor_reduce(out=st1, in_=Y3, axis=AX.X, op=ALU.add), r=YK, w=['st1'])
                S.dve(lambda e: e.tensor_scalar(out=st1, in0=st1, scalar1=1.0 / 64.0, scalar2=None, op0=ALU.mult), r=['st1'], w=['st1'])
                S.dve(lambda e: e.tensor_tensor(out=Yc3, in0=Y3, in1=st1.unsqueeze(2).broadcast_to([64, 16, 64]), op=ALU.subtract),
                      r=YK + ['st1'], w=['Yc'])
                S.pool(lambda e: e.tensor_tensor(out=Yf, in0=Yc, in1=Yc, op=ALU.mult), r=['Yc'] + YK, w=YK)
                S.dve(lambda e: e.tensor_reduce(out=st2, in_=Y3, axis=AX.X, op=ALU.add), r=YK, w=['st2'])
                S.dve(lambda e: e.tensor_scalar(out=st2, in0=st2, scalar1=1.0 / 64.0, scalar2=64e-5, op0=ALU.mult, op1=ALU.add),
                      r=['st2'], w=['st2'])
                S.act(lambda e: e.activation(out=st2, in_=st2, func=AF.Ln), r=['st2'], w=['st2'])
                S.act(lambda e: e.activation(out=st2, in_=st2, func=AF.Exp, scale=-0.5), r=['st2'], w=['st2'])
                S.dve(lambda e: e.tensor_tensor(out=Yc3, in0=Yc3, in1=st2.unsqueeze(2).broadcast_to([64, 16, 64]), op=ALU.mult),
                      r=['Yc', 'st2'], w=['Yc'])
                S.dve(lambda e: e.tensor_tensor(out=Yc, in0=Yc, in1=gnp[:, 0, :], op=ALU.mult), r=['Yc', ('gnp', 0)], w=['Yc'])
                S.pool(lambda e: e.tensor_tensor(out=Yc, in0=Yc, in1=gnp[:, 1, :], op=ALU.add), r=['Yc', ('gnp', 1)], w=['Yc'])
                S.dve(lambda e: e.tensor_tensor(out=Y3, in0=Vf.rearrange("p (h v) -> p h v", h=16),
                                                in1=rkb.unsqueeze(2).broadcast_to([64, 16, 64]), op=ALU.mult),
                      r=[kVf, krkb] + YK, w=YK)
                S.pool(lambda e: e.tensor_tensor(out=Yc, in0=Yc, in1=Yf, op=ALU.add), r=['Yc'] + YK, w=['Yc'])
                S.dve(lambda e: e.tensor_tensor(out=gb, in0=Yc, in1=Zf, op=ALU.mult), r=['Yc', kZf], w=['gb'])
                yield 'tailA'
                def trg(e):
                    ins = None
                    for c in range(8):
                        ins = e.transpose(out=PSB(7, 0, 128, c * 64, (c + 1) * 64), in_=gb[:, c * 128:(c + 1) * 128],
                                          identity=ident_b[0:64, 0:64])
                    return ins
                S.pe(trg, r=['gb', 'ident_b'], w=[pk(7)])
                S.act(lambda e: e.copy(out=gT.rearrange("p c t -> p (c t)"), in_=PSB(7, 0, 128, 0, 512)), r=[pk(7)], w=['gT'])

                def mo(e):
                    ins = None
                    for cb in range(2):
                        for c in range(8):
                            ins = e.matmul(PS(cb, 0, 64, 0, 512), lhsT=gT[:, c, :], rhs=wo_o[:, c, cb * 512:(cb + 1) * 512],
                                           start=(c == 0), stop=(c == 7))
                    return ins
                S.pe(mo, r=['gT'] + WOO, w=[pk(0), pk(1)])
                for cb in range(2):
                    S.dve(lambda e, cb=cb: e.scalar_tensor_tensor(
                        out=xr[:, cb * 512:(cb + 1) * 512], in0=xr[:, cb * 512:(cb + 1) * 512], scalar=ALPHA,
                        in1=PS(cb, 0, 64, 0, 512), op0=ALU.mult, op1=ALU.add), r=[kxr, pk(cb)], w=[kxr])
                    S.dve(lambda e, cb=cb: e.bn_stats(out=bst1[:, cb, :], in_=xr[:, cb * 512:(cb + 1) * 512]), r=[kxr], w=[('bst1', cb)])
                S.dve(lambda e: e.bn_aggr(out=mv1[:, 0:2], in_=bst1.rearrange("p a b -> p (a b)")), r=[('bst1', 0), ('bst1', 1)],
                      w=['mv1'])
                S.dve(lambda e: e.tensor_scalar(out=mv1[:, 2:3], in0=mv1[:, 1:2], scalar1=1e-5, scalar2=None, op0=ALU.add),
                      r=['mv1'], w=['mv1b'])
                S.act(lambda e: e.activation(out=mv1[:, 2:3], in_=mv1[:, 2:3], func=AF.Ln), r=['mv1b'], w=['mv1b'])
                S.act(lambda e: e.activation(out=mv1[:, 2:3], in_=mv1[:, 2:3], func=AF.Exp, scale=-0.5), r=['mv1b'], w=['mv1b'])
                S.dve(lambda e: e.tensor_scalar(out=xr, in0=xr, scalar1=mv1[:, 0:1], scalar2=mv1[:, 2:3], op0=ALU.subtract,
                                                op1=ALU.mult), r=[kxr, 'mv1', 'mv1b'], w=[kxr])
                S.dve(lambda e: e.tensor_tensor(out=xr, in0=xr, in1=gnp[:, 2, :], op=ALU.mult), r=[kxr, ('gnp', 2)], w=[kxr])
                S.pool(lambda e: e.tensor_tensor(out=xr, in0=xr, in1=gnp[:, 3, :], op=ALU.add), r=[kxr, ('gnp', 3)], w=[kxr])
                S.dma('sp', lambda e: e.dma_start(out=yout, in_=(xr[C - 1:C, :] if pad else xr)), r=[kxr])

            def write_state(dst):
                def trs(e):
                    ins = None
                    for h in range(16):
                        ins = e.transpose(out=PS(6 + h // 8, 0, 64, (h % 8) * 64, (h % 8 + 1) * 64), in_=ST[:, h, :],
                                          identity=ident_f[0:64, 0:64])
                    return ins
                S.pe(trs, r=['ST', 'ident_f'], w=[pk(6), pk(7)])
                for hb in range(2):
                    S.act(lambda e, hb=hb: e.copy(out=wko[:, hb * 8:(hb + 1) * 8, :].rearrange("p a b -> p (a b)"),
                                                  in_=PS(6 + hb, 0, 64, 0, 512)), r=[pk(6 + hb)], w=['wko'])
                S.dma('sp', lambda e: e.dma_start(out=dst.rearrange("(h v) k -> v h k", h=16), in_=wko), r=['wko'])

            def zero_state():
                S.pool(lambda e: e.memset(ST, 0.0), w=['ST'])
                S.pool(lambda e: e.memset(STb, 0.0), w=['STb'])

            def mk_load_state(s):
                def load_state():
                    S.dma('sp', lambda e: e.dma_start(
                        out=stin2, in_=I['state_wkv'][s * 1024:(s + 1) * 1024, :].rearrange("(pr p) k -> p pr k", pr=8)), w=['stin'])

                    def tri(e):
                        ins = None
                        for pr_ in range(8):
                            ins = e.transpose(out=PS(6 + pr_ // 4, 0, 64, (pr_ % 4) * 128, (pr_ % 4 + 1) * 128), in_=stin2[:, pr_, :],
                                              identity=ident_f)
                        return ins
                    S.pe(tri, r=['stin', 'ident_f'], w=[pk(6), pk(7)])
                    for hb in range(2):
                        S.dve(lambda e, hb=hb: e.tensor_copy(out=ST[:, hb * 8:(hb + 1) * 8, :].rearrange("p a v -> p (a v)"),
                                                             in_=PS(6 + hb, 0, 64, 0, 512)), r=[pk(6 + hb)], w=['ST'])
                    S.act(lambda e: e.copy(out=STb, in_=ST), r=['ST'], w=['STb'])
                return load_state

            gens = []
            nchk = T // C
            for ci in range(nchk):
                gens.append(chunk(ci * C, C, zero_state if ci == 0 else None, x1_scr[ci * C:(ci + 1) * C, :],
                                  O['y_prompt'][ci * C:(ci + 1) * C, :], C, ci, ci % 2,
                                  (lambda: write_state(O['wkv_p'])) if ci == nchk - 1 else None))
            for s in range(NS):
                gens.append(chunk(T + s, 1, mk_load_state(s), x1_scr[T + s:T + s + 1, :], O['y_sample'][s:s + 1, :], 1, 100 + s,
                                  (nchk + s) % 2, (lambda s=s: write_state(O['wkv_s'][s * 1024:(s + 1) * 1024, :]))))
            next(gens[0])
            next(gens[0])
            for i_ in range(1, len(gens)):
                next(gens[i_])
                next(gens[i_ - 1])
                next(gens[i_])
                for _ in gens[i_ - 1]:
                    pass
            next(gens[-1])
            for _ in gens[-1]:
                pass
            S.barrier()
            A.release()
            stage_end(13)
        except _Stop:
            pass
        S.analyze()
        S.emit(nc, sems, dsems)
        print("arena peak bytes", A.peak, "ops", len(S.ops))
    return nc


_NC_CACHE = {}


def kernel(**inputs):
    inp = {k: np.asarray(v) for k, v in inputs.items()}
    if 'nc' not in _NC_CACHE:
        _NC_CACHE['nc'] = build()
    nc = _NC_CACHE['nc']
    f = np.ascontiguousarray
    shared = {
        "cache_cmp_k": f(inp['cache_cmp_kv'][0][:, :, 0].reshape(NPOOL * 128, 128)),
        "cache_cmp_v": f(inp['cache_cmp_kv'][0][:, :, 1].reshape(NPOOL * 128, 128)),
        "cache_sel_k": f(inp['cache_sel_kv'][0][:, :, 0].reshape(NPOOL * 128, 128)),
        "cache_sel_v": f(inp['cache_sel_kv'][0][:, :, 1].reshape(NPOOL * 128, 128)),
        "w_in": f(inp['w_in_even'][0]),
        "conv_w": f(inp['conv_w'][0].reshape(31, 512)),
        "conv_b": f(inp['conv_b'].reshape(1, 512)),
        "conv_ln_g": f(inp['conv_ln_g'].reshape(1, 512)),
        "conv_ln_b": f(inp['conv_ln_b'].reshape(1, 512)),
        "wk_cmp": f(inp['wk_cmp'][0]),
        "wv_cmp": f(inp['wv_cmp'][0]),
        "w_out_even": f(inp['w_out_even'][0]),
        "mu_c": f(inp['mu_c'][0]),
        "w_rkvz": f(inp['w_rkvz'][0].reshape(4 * DM, DM)),
        "w0": f(inp['w0'].reshape(1, DM)),
        "w1": f(inp['w1'][0]),
        "w2": f(inp['w2'][0]),
        "a0": f(inp['a0'].reshape(1, DM)),
        "a1": f(inp['a1'][0]),
        "a2": f(inp['a2'][0]),
        "k_k": f(inp['k_k'].reshape(1, DM)),
        "k_a": f(inp['k_a'].reshape(1, DM)),
        "r_k": f(inp['r_k'].reshape(1, DM)),
        "gn_g": f(inp['gn_g'].reshape(1, DM)),
        "gn_b": f(inp['gn_b'].reshape(1, DM)),
        "w_out_odd": f(inp['w_out_odd'][0]),
        "ln_g": f(inp['ln_g']),
        "ln_b": f(inp['ln_b']),
    }
    in_maps = []
    for c in range(NCORES):
        s0, s1 = c * NS, (c + 1) * NS
        m = dict(shared)
        m["xp"] = f(inp['x_prompt'][c])
        m["xs"] = f(inp['x_sample'][s0:s1, 0, :])
        m["cache_win"] = f(inp['cache_win_kv'][0, s0:s1].reshape(NS * 512, 256))
        m["state_conv"] = f(inp['state_conv'][0, s0:s1].reshape(NS * 30, 512))
        m["state_wkv"] = f(inp['state_wkv'][0, s0:s1].reshape(NS * 16 * 64, 64))
        m["state_shift"] = f(inp['state_shift'][0, s0:s1])
        m["page_table"] = f(inp['page_table'][s0:s1].astype(np.int32))
        in_maps.append(m)
    res = run_bass_kernel_spmd(nc, in_maps, core_ids=list(range(NCORES)), trace=bool(os.environ.get('KDEV_TRACE')))
    if os.environ.get('KDEV_TRACE'):
        print('EXEC_TIME_NS', res.exec_time_ns)
    R = res.results

    def cat(name):
        return np.stack([np.asarray(R[c][name]) for c in range(NCORES)], axis=0)
    y_prompt = cat("y_prompt").reshape(8, T, DM)
    y_sample = cat("y_sample").reshape(32, 1, DM)
    cmp_p = cat("cmp_p").reshape(1, 8, T, 2, 2, 64)
    cmp_s = cat("cmp_s").reshape(1, 32, 1, 2, 2, 64)
    sel_p = cat("sel_p").reshape(1, 8, T, 2, 2, 64)
    sel_s = cat("sel_s").reshape(1, 32, 1, 2, 2, 64)
    win_p = cat("win_p").reshape(1, 8, 512, 2, 2, 64)
    win_s = cat("win_s").reshape(1, 32, 512, 2, 2, 64)
    conv_p = cat("conv_p").reshape(1, 8, 30, 512)
    conv_s = cat("conv_s").reshape(1, 32, 30, 512)
    wkv_p = cat("wkv_p").reshape(1, 8, 16, 64, 64)
    wkv_s = cat("wkv_s").reshape(1, 32, 16, 64, 64)
    shift_p = cat("shift_p").reshape(1, 8, DM)
    shift_s = cat("shift_s").reshape(1, 32, DM)
    return (y_prompt, y_sample, cmp_p, cmp_s, sel_p, sel_s, win_p, win_s, conv_p, conv_s,
            wkv_p, wkv_s, shift_p, shift_s)
```

```python
import numpy as np
import concourse.bass as bass
import concourse.mybir as mybir
from concourse.bass_utils import run_bass_kernel_spmd

F32 = mybir.dt.float32
BF16 = mybir.dt.bfloat16
I32 = mybir.dt.int32
AF = mybir.ActivationFunctionType
ALU = mybir.AluOpType
AX = mybir.AxisListType

NCORES = 8
T = 2048
NS = 4
TT = T + NS
DM = 1024
import os
NPOOL = int(os.environ.get('KDEV_NPOOL', '2560'))
STAGE = int(os.environ.get('KDEV_STAGE', '99'))


class _Stop(Exception):
    pass


def stage_end(k):
    if STAGE == k:
        raise _Stop()
ENGS = ('pe', 'act', 'dve', 'pool', 'sp')
SAME_ENG_DIST = 3
NDS = {'sp': 8, 'act': 4, 'pool': 6}


class Op:
    __slots__ = ('eng', 'fn', 'r', 'w', 'dma', 'deps', 'signal', 'count', 'dsem', 'dcount',
                 'dprev', 'waits', 'idx', 'eidx', 'bar', 'need')


class Sched:
    def __init__(self):
        self.ops = []
        self.nbar = 0

    def add(self, eng, fn, r=(), w=(), dma=False):
        op = Op()
        op.eng, op.fn, op.r, op.w, op.dma = eng, fn, tuple(r), tuple(w), dma
        op.signal = False
        op.bar = None
        op.count = 0
        self.ops.append(op)
        return op

    def pe(self, fn, r=(), w=()):
        return self.add('pe', fn, r, w)

    def act(self, fn, r=(), w=()):
        return self.add('act', fn, r, w)

    def dve(self, fn, r=(), w=()):
        return self.add('dve', fn, r, w)

    def pool(self, fn, r=(), w=()):
        return self.add('pool', fn, r, w)

    def dma(self, q, fn, r=(), w=()):
        return self.add(q, fn, r, w, dma=True)

    def barrier(self):
        self.nbar += 1
        for e in ENGS:
            op = self.add(e, None)
            op.bar = self.nbar

    def analyze(self):
        ops = self.ops
        last_w = {}
        rd = {}
        eng_ops = {e: [] for e in ENGS}
        last_on_eng = {}
        all_dmas = []
        cur_bar = None
        snap = None
        for i, op in enumerate(ops):
            op.idx = i
            deps = set()
            if op.bar is not None:
                if cur_bar != op.bar:
                    cur_bar = op.bar
                    snap = (dict(last_on_eng), list(all_dmas))
                    all_dmas = []
                for f, j in snap[0].items():
                    if f != op.eng and ops[j].bar is None:
                        deps.add(j)
                deps.update(snap[1])
            else:
                for k in op.r:
                    j = last_w.get(k)
                    if j is not None:
                        deps.add(j)
                for k in op.w:
                    j = last_w.get(k)
                    if j is not None:
                        deps.add(j)
                    rr = rd.get(k)
                    if rr:
                        deps.update(rr[0].values())
                        deps.update(rr[1])
                for k in op.r:
                    rr = rd.setdefault(k, ({}, []))
                    if op.dma:
                        rr[1].append(i)
                    else:
                        rr[0][op.eng] = i
                for k in op.w:
                    last_w[k] = i
                    rd[k] = ({}, [])
            deps.discard(i)
            op.deps = deps
            op.eidx = len(eng_ops[op.eng])
            eng_ops[op.eng].append(op)
            last_on_eng[op.eng] = i
            if op.dma:
                all_dmas.append(i)
        for op in ops:
            need = []
            for j in op.deps:
                d = ops[j]
                if d.dma:
                    need.append(j)
                elif d.eng == op.eng:
                    if op.eng == 'pe' and not op.dma:
                        continue
                    if op.eidx - d.eidx > SAME_ENG_DIST:
                        continue
                    need.append(j)
                else:
                    need.append(j)
            op.need = need
            for j in need:
                if not ops[j].dma:
                    ops[j].signal = True
        self.eng_ops = eng_ops

    def emit(self, nc, sems, dsems):
        ops = self.ops
        eng_ops = self.eng_ops
        for e in ENGS:
            cnt = 0
            for op in eng_ops[e]:
                if op.signal:
                    cnt += 1
                    op.count = cnt
        print('SEMCOUNTS', {e: max([op.count for op in eng_ops[e]] + [0]) for e in ENGS}, {e: len(eng_ops[e]) for e in ENGS})
        finals = {}
        for q, pool in dsems.items():
            uses = [0] * len(pool)
            k = 0
            for op in eng_ops[q]:
                if op.dma:
                    s = k % len(pool)
                    k += 1
                    op.dsem = pool[s]
                    op.dprev = 16 * uses[s]
                    uses[s] += 1
                    op.dcount = 16 * uses[s]
                    finals[id(pool[s])] = (pool[s], op.dcount)
        for e in ENGS:
            waited = {}
            for op in eng_ops[e]:
                ws = []
                for j in sorted(op.need):
                    d = ops[j]
                    if d.dma:
                        sem, val = d.dsem, d.dcount
                    else:
                        sem, val = sems[d.eng], d.count
                    if waited.get(id(sem), 0) >= val:
                        continue
                    waited[id(sem)] = val
                    ws.append((sem, val))
                if op.dma and op.dprev > 0:
                    if waited.get(id(op.dsem), 0) < op.dprev:
                        waited[id(op.dsem)] = op.dprev
                        ws.append((op.dsem, op.dprev))
                op.waits = ws

        def make(e):
            def body(eo):
                for op in eng_ops[e]:
                    for (sem, v) in op.waits:
                        eo.wait_ge(sem, v)
                    if op.fn is not None:
                        ins = op.fn(eo)
                        if op.dma:
                            ins.then_inc(op.dsem, 16)
                        elif op.signal:
                            ins.then_inc(sems[e], 1)
                if e == 'sp':
                    for (sem, v) in finals.values():
                        eo.wait_ge(sem, v)
            return body

        with nc.Block() as block:
            block.tensor(make('pe'))
            block.scalar(make('act'))
            block.vector(make('dve'))
            block.gpsimd(make('pool'))
            block.sync(make('sp'))


class Arena:
    def __init__(self, t32, nbytes):
        self.t32 = t32
        self.tbf = t32.bitcast(BF16)
        self.ti32 = t32.bitcast(I32)
        self.nbytes = nbytes
        self.top = 0
        self.marks = []
        self.peak = 0

    def alloc(self, shape, dt):
        es = 2 if dt == BF16 else 4
        n = 1
        for s in shape[1:]:
            n *= s
        nb = (n * es + 63) // 64 * 64
        off = self.top
        self.top += nb
        self.peak = max(self.peak, self.top)
        assert self.top <= self.nbytes, ("SBUF arena overflow", self.top, self.nbytes)
        base = {BF16: self.tbf, F32: self.t32, I32: self.ti32}[dt]
        v = base[0:shape[0], off // es: off // es + n]
        if len(shape) > 2:
            names = "abcdefg"[:len(shape) - 1]
            pat = "p (%s) -> p %s" % (" ".join(names), " ".join(names))
            v = v.rearrange(pat, **{names[i]: shape[1 + i] for i in range(len(shape) - 2)})
        return v

    def mark(self):
        self.marks.append(self.top)

    def release(self):
        self.top = self.marks.pop()


OUT_SPECS = [
    ("y_prompt", [T, DM]),
    ("y_sample", [NS, DM]),
    ("cmp_p", [T, 256]),
    ("cmp_s", [NS, 256]),
    ("sel_p", [T, 256]),
    ("sel_s", [NS, 256]),
    ("win_p", [512, 256]),
    ("win_s", [NS * 512, 256]),
    ("conv_p", [30, 512]),
    ("conv_s", [NS * 30, 512]),
    ("wkv_p", [16 * 64, 64]),
    ("wkv_s", [NS * 16 * 64, 64]),
    ("shift_p", [1, DM]),
    ("shift_s", [NS, DM]),
]

IN_SPECS = [
    ("xp", [T, DM], F32),
    ("xs", [NS, DM], F32),
    ("cache_cmp_k", [NPOOL * 128, 128], F32),
    ("cache_cmp_v", [NPOOL * 128, 128], F32),
    ("cache_sel_k", [NPOOL * 128, 128], F32),
    ("cache_sel_v", [NPOOL * 128, 128], F32),
    ("cache_win", [NS * 512, 256], F32),
    ("state_conv", [NS * 30, 512], F32),
    ("state_wkv", [NS * 16 * 64, 64], F32),
    ("state_shift", [NS, DM], F32),
    ("page_table", [NS, 64], I32),
    ("w_in", [DM, 3352], F32),
    ("conv_w", [31, 512], F32),
    ("conv_b", [1, 512], F32),
    ("conv_ln_g", [1, 512], F32),
    ("conv_ln_b", [1, 512], F32),
    ("wk_cmp", [32, 2], F32),
    ("wv_cmp", [32, 2], F32),
    ("w_out_even", [DM, DM], F32),
    ("mu_c", [6, DM], F32),
    ("w_rkvz", [4 * DM, DM], F32),
    ("w0", [1, DM], F32),
    ("w1", [DM, 64], F32),
    ("w2", [64, DM], F32),
    ("a0", [1, DM], F32),
    ("a1", [DM, 64], F32),
    ("a2", [64, DM], F32),
    ("k_k", [1, DM], F32),
    ("k_a", [1, DM], F32),
    ("r_k", [1, DM], F32),
    ("gn_g", [1, DM], F32),
    ("gn_b", [1, DM], F32),
    ("w_out_odd", [DM, DM], F32),
    ("ln_g", [2, DM], F32),
    ("ln_b", [2, DM], F32),
]

C_AVAL, C_AGLU, C_ZA, C_Q, C_KV, C_G3, C_ZB = 0, 512, 1024, 1536, 2048, 2816, 2840


def build(stage=99):
    nc = bass.Bass("TRN2", target_bir_lowering=False)
    I = {}
    for name, shape, dt in IN_SPECS:
        I[name] = nc.dram_tensor(name, shape, dt, kind="ExternalInput").ap()
    O = {}
    for name, shape in OUT_SPECS:
        O[name] = nc.dram_tensor(name, shape, F32, kind="ExternalOutput").ap()

    x1_scr = nc.dram_tensor("x1_scr", [TT, DM], F32, kind="Internal").ap()
    r_scr = nc.dram_tensor("r_scr", [128, 8, TT], F32, kind="Internal").ap()
    k_scr = nc.dram_tensor("k_scr", [128, 8, TT], F32, kind="Internal").ap()
    sw_scr = nc.dram_tensor("sw_scr", [128, 8, TT], F32, kind="Internal").ap()
    a_scr = nc.dram_tensor("a_scr", [128, 8, TT], F32, kind="Internal").ap()
    v_scr = nc.dram_tensor("v_scr", [TT, DM], F32, kind="Internal").ap()
    z_scr = nc.dram_tensor("z_scr", [TT, DM], F32, kind="Internal").ap()
    S = Sched()
    ARENA_BYTES = 176 * 1024
    from contextlib import ExitStack
    with ExitStack() as st:
        arena_t = st.enter_context(nc.sbuf_tensor("arena", [128, ARENA_BYTES // 4], F32))
        ps_t = st.enter_context(nc.psum_tensor("psum", [128, 4096], F32))
        sems = {e: st.enter_context(nc.semaphore("sem_" + e)) for e in ENGS}
        dsems = {q: [st.enter_context(nc.semaphore("dsem_%s%d" % (q, i))) for i in range(n)]
                 for q, n in NDS.items()}
        A = Arena(arena_t, ARENA_BYTES)
        ps_bf = ps_t.bitcast(BF16)
        try:

            def PS(b, p0=0, p1=128, c0=0, c1=512):
                return ps_t[p0:p1, b * 512 + c0: b * 512 + c1]

            def PSB(b, p0=0, p1=128, c0=0, c1=1024):
                return ps_bf[p0:p1, b * 1024 + c0: b * 1024 + c1]

            def pk(b):
                return ('ps', b)

            FR = {}

            def freg(e, v):
                if v not in FR:
                    FR[v] = e.to_reg(float(v))
                return FR[v]

            ident_f = A.alloc([128, 128], F32)
            ident_b = A.alloc([128, 128], BF16)
            ones_f = A.alloc([128, 128], F32)

            def mk_ident(e):
                e.memset(ones_f, 1.0)
                return e.affine_select(out=ident_f, in_=ones_f, pattern=[[-1, 128]], compare_op=ALU.is_equal,
                                       fill=freg(e, 0.0), base=0, channel_multiplier=1)
            S.pool(mk_ident, w=['ident_f', 'ones_f'])
            S.dve(lambda e: e.tensor_copy(out=ident_b, in_=ident_f), r=['ident_f'], w=['ident_b'])

            xT_off = A.top
            xT = A.alloc([128, 8, TT], BF16)
            xT_raw32 = arena_t[:, xT_off // 4: xT_off // 4 + 8208]
            L1_BASE = A.top
            yaT = A.alloc([128, 4, TT], BF16)
            Vsel_off = A.top
            Vsel = A.alloc([128, 16, 128], BF16)
            Vwin_off = A.top
            Vwin = A.alloc([128, 16, 128], BF16)
            Vsel_raw32 = arena_t[:, Vsel_off // 4: Vsel_off // 4 + 1024]
            Vwin_raw32 = arena_t[:, Vwin_off // 4: Vwin_off // 4 + 1024]
            kvsv = A.alloc([NS, 2, 128], F32)
            qs_f = A.alloc([68, 2, 4, NS], F32)
            ksn = A.alloc([68, 2, 2, NS], F32)
            gs_s = A.alloc([24, NS], BF16)
            w_in_t = I['w_in'].rearrange("(c p) f -> p c f", p=128)
            A.mark()
            xst = [A.alloc([128, DM], BF16) for _ in range(2)]
            xp_t = I['xp'].rearrange("(n p) d -> n p d", p=128)
            for n in range(16):
                b = n % 2
                S.dma('pool', lambda e, n=n, b=b: e.dma_start(out=xst[b], in_=xp_t[n]), w=[('xst', b)])
                bank = n % 2

                def tr(e, b=b, bank=bank):
                    ins = None
                    for c in range(8):
                        ins = e.transpose(out=PSB(bank, 0, 128, c * 128, (c + 1) * 128),
                                          in_=xst[b][:, c * 128:(c + 1) * 128], identity=ident_b)
                    return ins
                S.pe(tr, r=[('xst', b), 'ident_b'], w=[pk(bank)])
                S.dve(lambda e, n=n, bank=bank: e.tensor_copy(
                    out=xT[:, :, n * 128:(n + 1) * 128],
                    in_=PSB(bank).rearrange("p (c t) -> p c t", c=8)),
                    r=[pk(bank)], w=[('xT', n)])
            S.dma('pool', lambda e: e.dma_start(out=xst[0][0:NS, :], in_=I['xs']), w=[('xst', 0)])

            def trs(e):
                ins = None
                for c in range(8):
                    ins = e.transpose(out=PSB(0, 0, 128, c * NS, (c + 1) * NS),
                                      in_=xst[0][0:NS, c * 128:(c + 1) * 128], identity=ident_b[0:NS, 0:NS])
                return ins
            S.pe(trs, r=[('xst', 0), 'ident_b'], w=[pk(0)])
            S.dve(lambda e: e.tensor_copy(out=xT[:, :, T:TT],
                                          in_=PSB(0, 0, 128, 0, 8 * NS).rearrange("p (c t) -> p c t", c=8)),
                  r=[pk(0)], w=[('xT', 16)])
            XT_ALL = [('xT', n) for n in range(17)]
            S.barrier()
            A.release()
            stage_end(1)

            A.mark()
            wkv = A.alloc([128, 8, 768], BF16)
            kvst = [A.alloc([128, 768], F32) for _ in range(2)]
            winb = [A.alloc([128, 4, 256], F32) for _ in range(2)]
            for c in range(8):
                S.dma('pool', lambda e, c=c: e.dma_start(out=wkv[:, c, :], in_=w_in_t[:, c, C_KV:C_KV + 768]),
                      w=[('wkv', c)])
            WKV = [('wkv', c) for c in range(8)]
            for n in range(16):
                bA, bB = 4 + (n % 2) * 2, 5 + (n % 2) * 2

                def mmkv(e, n=n, bA=bA, bB=bB):
                    ins = None
                    for c in range(8):
                        ins = e.matmul(PS(bA), lhsT=xT[:, c, n * 128:(n + 1) * 128], rhs=wkv[:, c, 0:512],
                                       start=(c == 0), stop=(c == 7))
                    for c in range(8):
                        ins = e.matmul(PS(bB, 0, 128, 0, 256), lhsT=xT[:, c, n * 128:(n + 1) * 128],
                                       rhs=wkv[:, c, 512:768], start=(c == 0), stop=(c == 7))
                    return ins
                S.pe(mmkv, r=[('xT', n)] + WKV, w=[pk(bA), pk(bB)])
                sb = n % 2
                S.act(lambda e, sb=sb, bA=bA: e.copy(out=kvst[sb][:, 0:512], in_=PS(bA)),
                      r=[pk(bA)], w=[('kvst', sb, 0)])
                S.act(lambda e, sb=sb, bB=bB: e.copy(out=kvst[sb][:, 512:768], in_=PS(bB, 0, 128, 0, 256)),
                      r=[pk(bB)], w=[('kvst', sb, 1)])
                S.dma('sp', lambda e, n=n, sb=sb: e.dma_start(out=O['cmp_p'][n * 128:(n + 1) * 128, :],
                                                               in_=kvst[sb][:, 0:256]), r=[('kvst', sb, 0)])
                S.dma('sp', lambda e, n=n, sb=sb: e.dma_start(out=O['sel_p'][n * 128:(n + 1) * 128, :],
                                                               in_=kvst[sb][:, 256:512]), r=[('kvst', sb, 0)])
                if n >= 12:
                    S.dma('sp', lambda e, n=n, sb=sb: e.dma_start(
                        out=O['win_p'][(n - 12) * 128:(n - 11) * 128, :], in_=kvst[sb][:, 512:768]),
                        r=[('kvst', sb, 1)])
                S.pool(lambda e, n=n, sb=sb: e.tensor_copy(out=Vsel[:, n, :], in_=kvst[sb][:, 384:512]),
                       r=[('kvst', sb, 0)], w=[('Vsel', n)])
                S.pool(lambda e, n=n, sb=sb: e.tensor_copy(out=Vwin[:, n, :], in_=kvst[sb][:, 640:768]),
                       r=[('kvst', sb, 1)], w=[('Vwin', n)])
            stage_end(21)
            def mmkvs(e):
                ins = None
                for c in range(8):
                    ins = e.matmul(PS(4, 0, NS, 0, 512), lhsT=xT[:, c, T:TT], rhs=wkv[:, c, 0:512],
                                   start=(c == 0), stop=(c == 7))
                for c in range(8):
                    ins = e.matmul(PS(5, 0, NS, 0, 256), lhsT=xT[:, c, T:TT], rhs=wkv[:, c, 512:768],
                                   start=(c == 0), stop=(c == 7))
                return ins
            S.pe(mmkvs, r=[('xT', 16)] + WKV, w=[pk(4), pk(5)])
            S.act(lambda e: e.copy(out=kvst[0][0:NS, 0:512], in_=PS(4, 0, NS, 0, 512)), r=[pk(4)], w=[('kvst', 0, 0)])
            S.act(lambda e: e.copy(out=kvst[0][0:NS, 512:768], in_=PS(5, 0, NS, 0, 256)), r=[pk(5)], w=[('kvst', 0, 1)])
            S.dma('sp', lambda e: e.dma_start(out=O['cmp_s'], in_=kvst[0][0:NS, 0:256]), r=[('kvst', 0, 0)])
            S.dma('sp', lambda e: e.dma_start(out=O['sel_s'], in_=kvst[0][0:NS, 256:512]), r=[('kvst', 0, 0)])
            win_s_v = O['win_s'].rearrange("(s r) c -> s r c", r=512)
            S.dma('sp', lambda e: e.dma_start(out=win_s_v[:, 511, :], in_=kvst[0][0:NS, 512:768]), r=[('kvst', 0, 1)])
            S.pool(lambda e: e.tensor_copy(out=kvsv[0:NS, 0, :], in_=kvst[0][0:NS, 384:512]),
                   r=[('kvst', 0, 0)], w=['kvsv'])
            S.pool(lambda e: e.tensor_copy(out=kvsv[0:NS, 1, :], in_=kvst[0][0:NS, 640:768]),
                   r=[('kvst', 0, 1)], w=['kvsv'])
            stage_end(22)
            for s in range(NS):
                wb_ = winb[s % 2]
                src = I['cache_win'][s * 512:(s + 1) * 512, :].rearrange("(p j) c -> p j c", j=4)
                dst = O['win_s'][s * 512:(s + 1) * 512, :].rearrange("(p j) c -> p j c", j=4)
                S.dma('sp', lambda e, wb_=wb_, src=src: e.dma_start(out=wb_, in_=src), w=[('winb', s % 2)])
                S.dma('sp', lambda e, wb_=wb_, dst=dst: e.dma_start(out=dst[:, 0:3, :], in_=wb_[:, 1:4, :]),
                      r=[('winb', s % 2)])
                S.dma('sp', lambda e, wb_=wb_, dst=dst: e.dma_start(out=dst[0:127, 3, :], in_=wb_[1:128, 0, :]),
                      r=[('winb', s % 2)])
            stage_end(23)
            S.barrier()
            A.release()
            stage_end(2)
            BLKS = [(tb * 512, 512) for tb in range(4)] + [(T, NS)]

            def xT_keys(c0, n):
                if c0 >= T:
                    return [('xT', 16)]
                return [('xT', c0 // 128 + i) for i in range(max(1, n // 128))]

            pbank = [0]

            def proj_multi(units, blks=BLKS):
                for (c0, n) in blks:
                    for (wt, wkey, M, evac, aug) in units:
                        bank = pbank[0] % 4
                        pbank[0] += 1

                        def mm(e, c0=c0, n=n, bank=bank, wt=wt, M=M, aug=aug):
                            ins = None
                            for c in range(8):
                                ins = e.matmul(PS(bank, 0, M, 0, n), lhsT=wt[:, c, :], rhs=xT[:, c, c0:c0 + n],
                                               start=(c == 0), stop=(c == 7 and aug is None))
                            if aug is not None:
                                ins = e.matmul(PS(bank, 0, M, 0, n), lhsT=aug, rhs=bas[0:3, c0:c0 + n],
                                               start=False, stop=True)
                            return ins
                        S.pe(mm, r=[wkey, 'bas', 'bas0', 'coef'] + xT_keys(c0, n), w=[pk(bank)])
                        evac(bank, c0, n)

            def proj_fm(wt, wkey, M, evac, aug=None, blks=BLKS):
                proj_multi([(wt, wkey, M, evac, aug)], blks)


            A.mark()
            NWB = 4
            wbuf = [A.alloc([128, 8, 128], BF16) for _ in range(NWB)]
            wctr = [0]

            def load_w(col0, M):
                i = wctr[0] % NWB
                wctr[0] += 1
                S.dma('pool', lambda e: e.dma_start(out=wbuf[i][:, :, 0:M], in_=w_in_t[:, :, col0:col0 + M]),
                      w=[('wbuf', i)])
                return wbuf[i][:, :, 0:M], ('wbuf', i)
            u_ext = A.alloc([128, 4, 30 + T], BF16)
            sza = A.alloc([128, 4, TT], BF16)
            c32 = A.alloc([128, 4, TT], F32)
            u32t = A.alloc([128, 4, 32], F32)
            us32 = A.alloc([128, 4, NS], F32)
            cw_sb = A.alloc([31, 512], F32)
            cwT = A.alloc([128, 4, 31], F32)
            cvec = A.alloc([128, 3, 4], F32)
            A.mark()
            sig = [A.alloc([128, 512], F32) for _ in range(2)]
            sigc = [0]
            S.pool(lambda e: e.memset(u_ext[:, :, 0:30], 0.0), w=[('uext', 'h')])
            for j in range(4):
                wtg, wkg = load_w(C_AGLU + j * 128, 128)
                wta, wka = load_w(C_AVAL + j * 128, 128)
                wtz, wkz = load_w(C_ZA + j * 128, 128)

                def ev_sig(bank, c0, n, j=j):
                    sb = sigc[0] % 2
                    S.act(lambda e: e.activation(out=sig[sb][:, 0:n], in_=PS(bank, 0, 128, 0, n), func=AF.Sigmoid),
                          r=[pk(bank)], w=[('sig', sb)])

                def ev_u(bank, c0, n, j=j):
                    sb = sigc[0] % 2
                    sigc[0] += 1
                    if c0 < T:
                        S.dve(lambda e: e.tensor_tensor(out=u_ext[:, j, 30 + c0:30 + c0 + n], in0=PS(bank, 0, 128, 0, n),
                                                        in1=sig[sb][:, 0:n], op=ALU.mult),
                              r=[pk(bank), ('sig', sb)], w=[('uext', j, c0)])
                        if c0 == 1536:
                            S.dve(lambda e: e.tensor_tensor(out=u32t[:, j, :], in0=PS(bank, 0, 128, 480, 512),
                                                            in1=sig[sb][:, 480:512], op=ALU.mult),
                                  r=[pk(bank), ('sig', sb)], w=[('u32t', j)])
                    else:
                        S.dve(lambda e: e.tensor_tensor(out=us32[:, j, :], in0=PS(bank, 0, 128, 0, n),
                                                        in1=sig[sb][:, 0:n], op=ALU.mult),
                              r=[pk(bank), ('sig', sb)], w=[('us32', j)])

                def ev_za(bank, c0, n, j=j):
                    S.act(lambda e: e.activation(out=sza[:, j, c0:c0 + n], in_=PS(bank, 0, 128, 0, n), func=AF.Silu),
                          r=[pk(bank)], w=[('sza', j, c0)])
                proj_multi([(wtg, wkg, 128, ev_sig, None), (wta, wka, 128, ev_u, None), (wtz, wkz, 128, ev_za, None)])

            stage_end(3)
            S.dma('sp', lambda e: e.dma_start(out=cw_sb, in_=I['conv_w']), w=['cw_sb'])

            def trcw(e):
                ins = None
                for j in range(4):
                    ins = e.transpose(out=PS(7, 0, 128, j * 32, j * 32 + 31), in_=cw_sb[:, j * 128:(j + 1) * 128],
                                      identity=ident_f[0:31, 0:31])
                return ins
            S.pe(trcw, r=['cw_sb', 'ident_f'], w=[pk(7)])
            S.dve(lambda e: e.tensor_copy(out=cwT, in_=PS(7, 0, 128, 0, 128).rearrange("p (j t) -> p j t", j=4)[:, :, 0:31]),
                  r=[pk(7)], w=['cwT'])
            for i, nm in enumerate(['conv_b', 'conv_ln_g', 'conv_ln_b']):
                S.dma('sp', lambda e, i=i, nm=nm: e.dma_start(out=cvec[:, i, :],
                                                              in_=I[nm].rearrange("o (j p) -> p (o j)", p=128),
                                                              allow_slow_non_contiguous=True),
                      w=[('cvec', i)])

            diag = [A.alloc([128, 31, 128], BF16) for _ in range(2)]
            for j in range(4):
                db = j % 2
                for tap in range(31):
                    S.pool(lambda e, j=j, tap=tap, db=db: e.tensor_scalar(
                        out=diag[db][:, tap, :], in0=ident_b, scalar1=cwT[:, j, tap:tap + 1], scalar2=None, op0=ALU.mult),
                        r=['cwT', 'ident_b'], w=[('diag', db, tap)])
                for tb in range(4):
                    bank = 4 + (tb % 2)

                    def cm(e, j=j, tb=tb, db=db, bank=bank):
                        ins = None
                        for tap in range(31):
                            ins = e.matmul(PS(bank), lhsT=diag[db][:, tap, :],
                                           rhs=u_ext[:, j, tb * 512 + tap: tb * 512 + tap + 512],
                                           start=(tap == 0), stop=(tap == 30))
                        return ins
                    rk = [('diag', db, tap) for tap in range(31)] + [('uext', j, tb * 512)]
                    rk += [('uext', j, (tb - 1) * 512)] if tb > 0 else [('uext', 'h')]
                    S.pe(cm, r=rk, w=[pk(bank)])
                    S.act(lambda e, j=j, tb=tb, bank=bank: e.activation(
                        out=c32[:, j, tb * 512:(tb + 1) * 512], in_=PS(bank), func=AF.Identity, bias=cvec[:, 0, j:j + 1]),
                        r=[pk(bank), ('cvec', 0)], w=[('c32', j, tb * 512)])

            stage_end(4)
            exts = A.alloc([128, 4, NS, 31], F32)
            scv = A.alloc([30, NS, 512], F32)
            S.dma('sp', lambda e: e.dma_start(out=scv, in_=I['state_conv'].rearrange("(s r) c -> r s c", r=30)), w=['scv'])
            for s in range(NS):
                def trs_(e, s=s):
                    ins = None
                    for j in range(4):
                        ins = e.transpose(out=PS(6, 0, 128, j * 32, j * 32 + 30), in_=scv[:, s, j * 128:(j + 1) * 128],
                                          identity=ident_f[0:30, 0:30])
                    return ins
                S.pe(trs_, r=['scv', 'ident_f'], w=[pk(6)])
                S.dve(lambda e, s=s: e.tensor_copy(
                    out=exts[:, :, s, 0:30], in_=PS(6, 0, 128, 0, 128).rearrange("p (j t) -> p j t", j=4)[:, :, 0:30]),
                    r=[pk(6)], w=[('exts', s)])
            S.dve(lambda e: e.tensor_copy(out=exts[:, :, :, 30], in_=us32),
                  r=[('us32', j) for j in range(4)], w=[('exts', 'u')])
            prod = A.alloc([128, 4, NS, 31], F32)
            S.dve(lambda e: e.tensor_tensor(out=prod, in0=exts, in1=cwT.unsqueeze(2).broadcast_to([128, 4, NS, 31]),
                                            op=ALU.mult),
                  r=[('exts', s) for s in range(NS)] + [('exts', 'u'), 'cwT'], w=['prod'])
            cs_ = A.alloc([128, 4, NS], F32)
            S.dve(lambda e: e.tensor_reduce(out=cs_, in_=prod, axis=AX.X, op=ALU.add), r=['prod'], w=['cs'])
            S.dve(lambda e: e.tensor_tensor(out=c32[:, :, T:TT], in0=cs_,
                                            in1=cvec[:, 0, :].unsqueeze(2).broadcast_to([128, 4, NS]), op=ALU.add),
                  r=['cs', ('cvec', 0)], w=[('c32', j, T) for j in range(4)])

            cvo = A.alloc([32, 512], F32)

            def tru(e):
                ins = None
                for j in range(4):
                    ins = e.transpose(out=PS(6, 0, 32, j * 128, (j + 1) * 128), in_=u32t[:, j, :], identity=ident_f)
                return ins
            S.pe(tru, r=[('u32t', j) for j in range(4)] + ['ident_f'], w=[pk(6)])
            S.act(lambda e: e.copy(out=cvo, in_=PS(6, 0, 32, 0, 512)), r=[pk(6)], w=['cvo'])
            S.dma('sp', lambda e: e.dma_start(out=O['conv_p'], in_=cvo[2:32, :]), r=['cvo'])
            cso = A.alloc([NS, 512], F32)

            def trus(e):
                ins = None
                for j in range(4):
                    ins = e.transpose(out=PS(6, 0, NS, j * 128, (j + 1) * 128), in_=us32[:, j, :], identity=ident_f)
                return ins
            S.pe(trus, r=[('us32', j) for j in range(4)] + ['ident_f'], w=[pk(6)])
            S.act(lambda e: e.copy(out=cso, in_=PS(6, 0, NS, 0, 512)), r=[pk(6)], w=['cso'])
            conv_s_v = O['conv_s'].rearrange("(s r) c -> s r c", r=30)
            S.dma('sp', lambda e: e.dma_start(out=conv_s_v[:, 29, :], in_=cso), r=['cso'])
            for s in range(NS):
                S.dma('sp', lambda e, s=s: e.dma_start(out=conv_s_v[s, 0:29, :], in_=scv[1:30, s, :]), r=['scv'])

            stage_end(5)
            S.barrier()
            A.release()
            onesm = A.alloc([128, 128], F32)
            S.pool(lambda e: e.memset(onesm, 1.0 / 512.0), w=['onesm'])
            sq4 = A.alloc([128, 4, 512], F32)
            mean_sb = A.alloc([128, 512], F32)
            rstd_sb = A.alloc([128, 512], F32)
            tmpv = A.alloc([128, 512], F32)
            tj = [A.alloc([128, 512], F32) for _ in range(2)]
            tjc = 0
            for (c0, n) in BLKS:
                ck = [('c32', j, c0) for j in range(4)]
                S.act(lambda e, c0=c0, n=n: e.activation(out=sq4[:, :, 0:n], in_=c32[:, :, c0:c0 + n], func=AF.Square),
                      r=ck, w=['sq4'])

                def stm(e, c0=c0, n=n):
                    ins = None
                    for j in range(4):
                        ins = e.matmul(PS(0, 0, 128, 0, n), lhsT=onesm, rhs=c32[:, j, c0:c0 + n], start=(j == 0), stop=(j == 3))
                    for j in range(4):
                        ins = e.matmul(PS(1, 0, 128, 0, n), lhsT=onesm, rhs=sq4[:, j, 0:n], start=(j == 0), stop=(j == 3))
                    return ins
                S.pe(stm, r=ck + ['sq4', 'onesm'], w=[pk(0), pk(1)])
                S.act(lambda e, n=n: e.copy(out=mean_sb[:, 0:n], in_=PS(0, 0, 128, 0, n)), r=[pk(0)], w=['mean_sb'])
                S.dve(lambda e, n=n: e.tensor_tensor(out=tmpv[:, 0:n], in0=mean_sb[:, 0:n], in1=mean_sb[:, 0:n], op=ALU.mult),
                      r=['mean_sb'], w=['tmpv'])
                S.dve(lambda e, n=n: e.tensor_tensor(out=tmpv[:, 0:n], in0=PS(1, 0, 128, 0, n), in1=tmpv[:, 0:n],
                                                     op=ALU.subtract), r=[pk(1), 'tmpv'], w=['tmpv'])
                S.dve(lambda e, n=n: e.tensor_scalar(out=tmpv[:, 0:n], in0=tmpv[:, 0:n], scalar1=1e-5, scalar2=None,
                                                     op0=ALU.add), r=['tmpv'], w=['tmpv'])
                S.act(lambda e, n=n: e.activation(out=tmpv[:, 0:n], in_=tmpv[:, 0:n], func=AF.Ln), r=['tmpv'], w=['tmpv'])
                S.act(lambda e, n=n: e.activation(out=rstd_sb[:, 0:n], in_=tmpv[:, 0:n], func=AF.Exp, scale=-0.5),
                      r=['tmpv'], w=['rstd_sb'])
                for j in range(4):
                    tb_ = tj[tjc % 2]
                    tk = ('tj', tjc % 2)
                    tjc += 1
                    S.dve(lambda e, j=j, c0=c0, n=n, tb_=tb_: e.tensor_tensor(
                        out=tb_[:, 0:n], in0=c32[:, j, c0:c0 + n], in1=mean_sb[:, 0:n], op=ALU.subtract),
                        r=[('c32', j, c0), 'mean_sb'], w=[tk])
                    S.dve(lambda e, n=n, tb_=tb_: e.tensor_tensor(out=tb_[:, 0:n], in0=tb_[:, 0:n], in1=rstd_sb[:, 0:n],
                                                                  op=ALU.mult), r=[tk, 'rstd_sb'], w=[tk])
                    S.dve(lambda e, j=j, n=n, tb_=tb_: e.tensor_scalar(
                        out=tb_[:, 0:n], in0=tb_[:, 0:n], scalar1=cvec[:, 1, j:j + 1], scalar2=cvec[:, 2, j:j + 1],
                        op0=ALU.mult, op1=ALU.add), r=[tk, ('cvec', 1), ('cvec', 2)], w=[tk])
                    S.act(lambda e, n=n, tb_=tb_: e.activation(out=tb_[:, 0:n], in_=tb_[:, 0:n], func=AF.Silu),
                          r=[tk], w=[tk])
                    S.pool(lambda e, j=j, c0=c0, n=n, tb_=tb_: e.tensor_tensor(
                        out=yaT[:, j, c0:c0 + n], in0=tb_[:, 0:n], in1=sza[:, j, c0:c0 + n], op=ALU.mult),
                        r=[tk, ('sza', j, c0)], w=[('yaT', j, c0)])
            S.barrier()
            A.release()
            stage_end(6)
            FORCE = 1.0e4
            BIG = 30000.0
            ybT = A.alloc([64, 8, TT], BF16)
            A.mark()
            Qaug = [A.alloc([128, 4, TT], BF16) for g in range(2)]
            Ksel = [A.alloc([128, TT], BF16) for g in range(2)]
            Kwin = [A.alloc([68, TT], BF16) for g in range(2)]
            gsig = A.alloc([24, TT], BF16)
            KCa = [A.alloc([68, 64], BF16) for g in range(2)]
            VC = [A.alloc([64, 64], BF16) for g in range(2)]
            kcT = A.alloc([64, 2, 2, 64], F32)
            Wkv = A.alloc([64, 2, 2, 32], F32)
            oh = A.alloc([24, 24, 64], BF16)
            ones64 = A.alloc([128, 64], BF16)
            Mpad = A.alloc([128, 128], BF16)
            A.mark()
            NWB = 4
            wpad = [A.alloc([128, 8, 68], BF16) for _ in range(NWB)]
            for i in range(NWB):
                S.pool(lambda e, i=i: e.memset(wpad[i], 0.0), w=[('wpad', i)])
            wpc = [0]

            def load_wpad(col0, M=64):
                i = wpc[0] % NWB
                wpc[0] += 1
                S.dma('pool', lambda e: e.dma_start(out=wpad[i][:, :, 0:M], in_=w_in_t[:, :, col0:col0 + M]),
                      w=[('wpad', i)])
                return wpad[i], ('wpad', i)
            bas = A.alloc([3, TT], BF16)
            basc = A.alloc([3, 64], BF16)
            brow = A.alloc([1, 2, TT], BF16)
            browc = A.alloc([1, 2, 64], BF16)
            coefQ = A.alloc([3, 8, 68], BF16)
            coefK = A.alloc([3, 68], BF16)
            cq0 = A.alloc([1, 3, 8, 68], BF16)
            ck0 = A.alloc([1, 3, 68], BF16)

            def mkbas(e):
                e.iota(brow[:, 0, 0:T], pattern=[[1, 32], [0, 64]], base=0, channel_multiplier=0,
                       allow_small_or_imprecise_dtypes=True)
                e.iota(brow[:, 1, 0:T], pattern=[[0, 32], [1, 64]], base=0, channel_multiplier=0,
                       allow_small_or_imprecise_dtypes=True)
                e.memset(brow[:, 0, T:TT], 128.0)
                e.memset(brow[:, 1, T:TT], 0.0)
                e.iota(browc[:, 0, :], pattern=[[1, 32], [0, 2]], base=0, channel_multiplier=0,
                       allow_small_or_imprecise_dtypes=True)
                e.iota(browc[:, 1, :], pattern=[[0, 32], [32, 2]], base=31, channel_multiplier=0,
                       allow_small_or_imprecise_dtypes=True)
                e.memset(bas[0:1, :], 1.0)
                e.memset(basc[0:1, :], 1.0)
                e.memset(cq0, 0.0)
                e.memset(ck0, 0.0)
                for h in range(8):
                    sl = 2.0 ** (-(h + 1))
                    e.memset(cq0[:, 0, h, 64:65], 8.0 * sl * 64.0)
                    e.memset(cq0[:, 0, h, 65:66], 8.0 * sl)
                    e.memset(cq0[:, 1, h, 66:67], -8.0 * sl * 64.0)
                    e.memset(cq0[:, 2, h, 67:68], -8.0 * sl)
                e.memset(ck0[:, 0, 66:68], 1.0)
                e.memset(ck0[:, 1, 64:65], 1.0)
                e.memset(ck0[:, 2, 65:66], 1.0)
                e.memset(ones64, 1.0)
                e.memset(Mpad, 0.0)
                return e.tensor_copy(out=oh, in_=ident_b[0:24, 0:24].unsqueeze(2).broadcast_to([24, 24, 64]))
            S.pool(mkbas, r=['ident_b'], w=['brow', 'bas0', 'ones64', 'Mpad', 'oh'])
            for i in range(2):
                S.dma('sp', lambda e, i=i: e.dma_start(out=bas[1 + i:2 + i, :], in_=brow[0:1, i, :]), r=['brow'], w=['bas'])
                S.dma('sp', lambda e, i=i: e.dma_start(out=basc[1 + i:2 + i, :], in_=browc[0:1, i, :]), r=['brow'], w=['bas'])
            for i in range(3):
                S.dma('sp', lambda e, i=i: e.dma_start(out=coefQ[i:i + 1, :, :], in_=cq0[0:1, i, :, :]), r=['brow'], w=['coef'])
                S.dma('sp', lambda e, i=i: e.dma_start(out=coefK[i:i + 1, :], in_=ck0[0:1, i, :]), r=['brow'], w=['coef'])

            QK = [[('Q', g, c0) for (c0, n) in BLKS] + [('Qm', g, qt) for qt in range(17)] for g in range(2)]
            KK = [[('Ks', g, c0) for (c0, n) in BLKS] for g in range(2)]
            for g in range(2):
                S.pool(lambda e, g=g: e.memset(Qaug[g][64:128, :, :], 0.0), w=QK[g])

                def kinit(e, g=g):
                    e.memset(Ksel[g][64:128, :], 0.0)
                    e.memset(Ksel[g][96:128, 0:T], 1.0)
                    e.affine_select(out=Ksel[g][96:128, 0:T], in_=Ksel[g][96:128, 0:T], pattern=[[1, T]],
                                    compare_op=ALU.is_ge, fill=freg(e, 0.0), base=0, channel_multiplier=-64)
                    return e.affine_select(out=Ksel[g][96:128, 0:T], in_=Ksel[g][96:128, 0:T], pattern=[[-1, T]],
                                           compare_op=ALU.is_ge, fill=freg(e, 0.0), base=63, channel_multiplier=64)
                S.pool(kinit, w=KK[g])

            stage_end(7)
            wt, wk_ = load_wpad(C_G3, 24)

            def ev_g(bank, c0, n):
                S.act(lambda e: e.activation(out=gsig[0:24, c0:c0 + n], in_=PS(bank, 0, 24, 0, n), func=AF.Sigmoid),
                      r=[pk(bank)], w=[('gsig', c0)])
            proj_fm(wt[:, :, 0:24], wk_, 24, ev_g)
            for h in range(8):
                wt, wk_ = load_wpad(C_ZB + h * 64, 64)

                def ev_zb(bank, c0, n, h=h):
                    S.act(lambda e: e.activation(out=ybT[0:64, h, c0:c0 + n], in_=PS(bank, 0, 64, 0, n), func=AF.Silu),
                          r=[pk(bank)], w=[('ybT', h, c0)])
                proj_fm(wt[:, :, 0:64], wk_, 64, ev_zb)
            for h in range(8):
                g, r_ = h // 4, h % 4
                wt, wk_ = load_wpad(C_Q + h * 64)

                def ev_q(bank, c0, n, g=g, r_=r_):
                    S.dve(lambda e: e.tensor_scalar(out=Qaug[g][0:68, r_, c0:c0 + n], in0=PS(bank, 0, 68, 0, n),
                                                    scalar1=0.125, scalar2=None, op0=ALU.mult),
                          r=[pk(bank)], w=[('Q', g, c0)])
                proj_fm(wt, wk_, 68, ev_q, aug=coefQ[0:3, h, :])
            augK = coefK[0:3, :]
            for g in range(2):
                wt, wk_ = load_wpad(C_KV + 256 + g * 64)

                def ev_ks(bank, c0, n, g=g):
                    S.act(lambda e: e.copy(out=Ksel[g][0:68, c0:c0 + n], in_=PS(bank, 0, 68, 0, n)),
                          r=[pk(bank)], w=[('Ks', g, c0)])
                proj_fm(wt, wk_, 68, ev_ks, aug=augK)
                wt, wk_ = load_wpad(C_KV + 512 + g * 64)

                def ev_kw(bank, c0, n, g=g):
                    S.act(lambda e: e.copy(out=Kwin[g][0:68, c0:c0 + n], in_=PS(bank, 0, 68, 0, n)),
                          r=[pk(bank)], w=[('Kw', g, c0)])
                proj_fm(wt, wk_, 68, ev_kw, aug=augK)
            for kv_, nm in enumerate(['wk_cmp', 'wv_cmp']):
                for g in range(2):
                    S.dma('sp', lambda e, kv_=kv_, nm=nm, g=g: e.dma_start(
                        out=Wkv[:, kv_, g:g + 1, :], in_=I[nm][:, g:g + 1].rearrange("l o -> o l").partition_broadcast(64),
                        allow_slow_non_contiguous=True), w=[('Wkv', kv_, g)])
            ptmp = [A.alloc([64, 16, 32], F32) for _ in range(1)]
            pcnt = [0]
            for kv_ in range(2):
                for g in range(2):
                    wt, wk_ = load_wpad(C_KV + kv_ * 128 + g * 64, 64)

                    def ev_pool(bank, c0, n, kv_=kv_, g=g):
                        pb = 0
                        pcnt[0] += 1
                        S.dve(lambda e: e.tensor_tensor(
                            out=ptmp[pb], in0=PS(bank, 0, 64, 0, 512).rearrange("p (a l) -> p a l", l=32),
                            in1=Wkv[:, kv_, g:g + 1, :].broadcast_to([64, 16, 32]), op=ALU.mult),
                            r=[pk(bank), ('Wkv', kv_, g)], w=[('ptmp', pb)])
                        S.dve(lambda e: e.tensor_reduce(out=kcT[:, kv_, g, c0 // 32:c0 // 32 + 16], in_=ptmp[pb],
                                                        axis=AX.X, op=ALU.add),
                              r=[('ptmp', pb)], w=[('kcT', kv_, g, c0)])
                    proj_fm(wt[:, :, 0:64], wk_, 64, ev_pool, blks=BLKS[0:4])
            for g in range(2):
                kck = [('kcT', 0, g, c0) for (c0, n) in BLKS[0:4]]
                vck = [('kcT', 1, g, c0) for (c0, n) in BLKS[0:4]]
                S.dve(lambda e, g=g: e.tensor_copy(out=KCa[g][0:64, :], in_=kcT[:, 0, g, :]), r=kck, w=[('KCa', g)])

                def mmaug(e, g=g):
                    return e.matmul(PS(7, 0, 68, 0, 64), lhsT=coefK[0:3, :], rhs=basc[0:3, :], start=True, stop=True)
                S.pe(mmaug, r=['coef', 'bas', 'bas0'], w=[pk(7)])
                S.act(lambda e, g=g: e.copy(out=KCa[g][64:68, :], in_=PS(7, 64, 68, 0, 64)), r=[pk(7)], w=[('KCa', g)])
                S.pe(lambda e, g=g: e.transpose(out=PS(6, 0, 64, 0, 64), in_=kcT[:, 1, g, :], identity=ident_f[0:64, 0:64]),
                     r=vck + ['ident_f'], w=[pk(6)])
                S.act(lambda e, g=g: e.copy(out=VC[g], in_=PS(6, 0, 64, 0, 64)), r=[pk(6)], w=[('VC', g)])

            stage_end(8)
            S.barrier()
            A.release()
            Ec = A.alloc([128, 4, 64], F32)
            sums_c = A.alloc([128, 4], F32)
            imp64 = A.alloc([128, 64], F32)
            imp = A.alloc([128, 32], F32)
            imp2 = A.alloc([128, 32], F32)
            m8 = A.alloc([128, 16], F32)
            PT = [A.alloc([128, 512], BF16) for _ in range(3)]
            acc = [A.alloc([64, 512], F32) for _ in range(2)]
            rs_ = [A.alloc([64, 512], F32) for _ in range(2)]
            tt_ = [A.alloc([64, 512], F32) for _ in range(2)]
            ptc = [0]
            sbc = [0]
            rsc = [0]

            def qkeys(g, qt):
                return [('Q', g, (qt // 4) * 512), ('Qm', g, qt)]

            pend = [None]

            def attn_pair(g, qt, lhsT, lkeys, KKrows, vt, vkeys, masks, ob, sb_, first, last):
                q0 = qt * 128
                sbank = sbc[0] % 2
                sbc[0] += 1
                pt = PT[ptc[0] % 3]
                ptk = ('PT', ptc[0] % 3)
                ptc[0] += 1
                M = lhsT.shape[1]
                S.pe(lambda e: e.matmul(PS(sbank, 0, M, 0, 512), lhsT=lhsT, rhs=Qaug[g][0:KKrows, :, q0:q0 + 128],
                                        start=True, stop=True),
                     r=lkeys + qkeys(g, qt), w=[pk(sbank)])
                S.act(lambda e: e.activation(out=pt[0:M, :], in_=PS(sbank, 0, M, 0, 512), func=AF.Exp),
                      r=[pk(sbank)], w=[ptk])
                for (base, cm, pat) in masks:
                    S.pool(lambda e, base=base, cm=cm, pat=pat: e.affine_select(
                        out=pt[0:M, :], in_=pt[0:M, :], pattern=pat, compare_op=ALU.is_ge, fill=freg(e, 0.0), base=base,
                        channel_multiplier=cm), r=[ptk], w=[ptk])

                def pv(e):
                    e.matmul(PS(ob, 0, 64, 0, 512), lhsT=vt, rhs=pt[0:M, :], start=first, stop=last)
                    return e.matmul(PS(sb_, 0, 64, 0, 512), lhsT=ones64[0:M, :], rhs=pt[0:M, :], start=first, stop=last)
                prev = pend[0]
                pend[0] = lambda: S.pe(pv, r=[ptk, 'ones64'] + vkeys, w=[pk(ob), pk(sb_)])
                if prev is not None:
                    prev()

            def flush_pv():
                if pend[0] is not None:
                    pend[0]()
                    pend[0] = None

            def combine(g, qt, j, ob, sb_, ab):
                flush_pv()
                q0 = qt * 128
                rb = rsc[0] % 2
                rsc[0] += 1

                def gmm(e):
                    ins = None
                    for r_ in range(4):
                        ins = e.matmul(PS(6, 0, 64, r_ * 128, (r_ + 1) * 128), lhsT=oh[:, 3 * (4 * g + r_) + j, :],
                                       rhs=gsig[0:24, q0:q0 + 128], start=True, stop=True)
                    return ins
                S.pe(gmm, r=['oh', ('gsig', (qt // 4) * 512)], w=[pk(6)])
                S.dve(lambda e: e.tensor_scalar(out=rs_[rb], in0=PS(sb_, 0, 64, 0, 512), scalar1=1e-30, scalar2=None,
                                                op0=ALU.max), r=[pk(sb_)], w=[('rs', rb)])
                S.dve(lambda e: e.reciprocal(out=rs_[rb], in_=rs_[rb]), r=[('rs', rb)], w=[('rs', rb)])
                S.dve(lambda e: e.tensor_tensor(out=rs_[rb], in0=rs_[rb], in1=PS(6, 0, 64, 0, 512), op=ALU.mult),
                      r=[('rs', rb), pk(6)], w=[('rs', rb)])
                if j == 0:
                    S.dve(lambda e: e.tensor_tensor(out=acc[ab], in0=PS(ob, 0, 64, 0, 512), in1=rs_[rb], op=ALU.mult),
                          r=[pk(ob), ('rs', rb)], w=[('acc', ab)])
                else:
                    S.dve(lambda e: e.tensor_tensor(out=tt_[rb], in0=PS(ob, 0, 64, 0, 512), in1=rs_[rb], op=ALU.mult),
                          r=[pk(ob), ('rs', rb)], w=[('tt', rb)])
                    S.pool(lambda e: e.tensor_tensor(out=acc[ab], in0=acc[ab], in1=tt_[rb], op=ALU.add),
                           r=[('tt', rb), ('acc', ab)], w=[('acc', ab)])

            pat_q = [[0, 4], [1, 128]]
            pat_qn = [[0, 4], [-1, 128]]
            def partA(qt, g):
                q0 = qt * 128
                def cmm(e, g=g, q0=q0):
                    ins = None
                    for r_ in range(4):
                        ins = e.matmul(PS(7, 0, 128, r_ * 64, (r_ + 1) * 64), lhsT=Qaug[g][0:68, r_, q0:q0 + 128],
                                       rhs=KCa[g][0:68, :], start=True, stop=True)
                    return ins
                S.pe(cmm, r=qkeys(g, qt) + [('KCa', g)], w=[pk(7)])
                S.act(lambda e: e.activation(out=Ec, in_=PS(7, 0, 128, 0, 256).rearrange("p (r n) -> p r n", r=4),
                                             func=AF.Exp), r=[pk(7)], w=['Ec'])
                S.pool(lambda e, q0=q0: e.affine_select(out=Ec, in_=Ec, pattern=[[0, 4], [-32, 64]],
                                                        compare_op=ALU.is_ge, fill=freg(e, 0.0), base=q0 - 31,
                                                        channel_multiplier=1), r=['Ec'], w=['Ec'])
                S.dve(lambda e: e.tensor_reduce(out=sums_c, in_=Ec, axis=AX.X, op=ALU.add), r=['Ec'], w=['sums_c'])
                S.dve(lambda e: e.tensor_scalar(out=sums_c, in0=sums_c, scalar1=1e-30, scalar2=None, op0=ALU.max),
                      r=['sums_c'], w=['sums_c'])
                S.dve(lambda e: e.reciprocal(out=sums_c, in_=sums_c), r=['sums_c'], w=['sums_c'])
                S.dve(lambda e: e.tensor_tensor(out=Ec, in0=Ec, in1=sums_c.unsqueeze(2).broadcast_to([128, 4, 64]),
                                                op=ALU.mult), r=['Ec', 'sums_c'], w=['Ec'])
                S.dve(lambda e: e.tensor_reduce(out=imp64, in_=Ec.rearrange("p r n -> p n r"), axis=AX.X, op=ALU.add),
                      r=['Ec'], w=['imp64'])
                S.dve(lambda e: e.tensor_reduce(out=imp, in_=imp64.rearrange("p (a b) -> p a b", b=2), axis=AX.X,
                                                op=ALU.add), r=['imp64'], w=['imp'])
                S.pool(lambda e, q0=q0: e.affine_select(out=imp, in_=imp, pattern=[[-64, 32]], compare_op=ALU.is_ge,
                                                        fill=freg(e, FORCE), base=q0 - 128, channel_multiplier=1),
                       r=['imp'], w=['imp'])
                S.pool(lambda e: e.memset(imp[:, 0:1], FORCE), r=['imp'], w=['imp'])
                S.pool(lambda e, q0=q0: e.affine_select(out=imp, in_=imp, pattern=[[-64, 32]], compare_op=ALU.is_ge,
                                                        fill=freg(e, -1.0e30), base=q0, channel_multiplier=1),
                       r=['imp'], w=['imp'])
                S.dve(lambda e: e.max(out=m8[:, 0:8], in_=imp), r=['imp'], w=['m8a'])
                S.dve(lambda e: e.match_replace(out=imp2, in_to_replace=m8[:, 0:8], in_values=imp, imm_value=-3.0e38),
                      r=['imp', 'm8a'], w=['imp2'])
                S.dve(lambda e: e.max(out=m8[:, 8:16], in_=imp2), r=['imp2'], w=['m8b'])
                S.dve(lambda e: e.tensor_scalar(out=Mpad[:, 96:128], in0=imp, scalar1=m8[:, 15:16], scalar2=None,
                                                op0=ALU.is_ge), r=['imp', 'm8b'], w=['Mpad'])

            def partB(qt, g):
                q0 = qt * 128
                S.pe(lambda e: e.matmul(PS(6, 0, 128, 0, 128), lhsT=Mpad, rhs=ident_b, start=True, stop=True),
                     r=['Mpad', 'ident_b'], w=[pk(6)])
                S.dve(lambda e, g=g, q0=q0: e.tensor_scalar(
                    out=Qaug[g][96:128, :, q0:q0 + 128],
                    in0=PS(6, 96, 128, 0, 128).unsqueeze(1).broadcast_to([32, 4, 128]),
                    scalar1=1.0, scalar2=BIG, op0=ALU.subtract, op1=ALU.mult), r=[pk(6)], w=[('Qm', g, qt)])

            def partC(qt, g):
                q0 = qt * 128
                ab = (qt * 2 + g) % 2
                attn_pair(g, qt, KCa[g][0:68, :], [('KCa', g)], 68, VC[g], [('VC', g)],
                          [(q0 - 31, -32, pat_q)], 2, 3, True, True)
                combine(g, qt, 0, 2, 3, ab)
                for kt in range(qt + 1):
                    masks = [(0, -1, pat_q)] if kt == qt else []
                    attn_pair(g, qt, Ksel[g][:, kt * 128:(kt + 1) * 128], [('Ks', g, (kt // 4) * 512)] + KK[g][0:0], 128,
                              Vsel[:, kt, g * 64:(g + 1) * 64], [('Vsel', kt)], masks, 4, 5, kt == 0, kt == qt)
                combine(g, qt, 1, 4, 5, ab)
                k_lo = max(0, qt - 4)
                for kt in range(k_lo, qt + 1):
                    masks = []
                    if kt == qt:
                        masks.append((0, -1, pat_q))
                    if kt == qt - 4:
                        masks.append((-1, 1, pat_qn))
                    attn_pair(g, qt, Kwin[g][0:68, kt * 128:(kt + 1) * 128], [('Kw', g, (kt // 4) * 512)], 68,
                              Vwin[:, kt, g * 64:(g + 1) * 64], [('Vwin', kt)], masks, 2, 3, kt == k_lo, kt == qt)
                combine(g, qt, 2, 2, 3, ab)
                S.dve(lambda e, g=g, q0=q0, ab=ab: e.tensor_tensor(
                    out=ybT[0:64, 4 * g:4 * g + 4, q0:q0 + 128],
                    in0=acc[ab].rearrange("p (r q) -> p r q", r=4),
                    in1=ybT[0:64, 4 * g:4 * g + 4, q0:q0 + 128], op=ALU.mult),
                    r=[('acc', ab)] + [('ybT', 4 * g + r_, (qt // 4) * 512) for r_ in range(4)],
                    w=[('ybT', 4 * g + r_, (qt // 4) * 512) for r_ in range(4)])

            its = [(qt, g) for qt in range(16) for g in range(2)]
            partA(*its[0])
            partB(*its[0])
            for i_, (qt, g) in enumerate(its):
                if i_ + 1 < len(its):
                    partA(*its[i_ + 1])
                partC(qt, g)
                if i_ + 1 < len(its):
                    partB(*its[i_ + 1])
            for g in range(2):
                S.dve(lambda e, g=g: e.tensor_copy(out=qs_f[:, g, :, :], in_=Qaug[g][0:68, :, T:TT]), r=[('Q', g, T)], w=['qs_f'])
                S.dve(lambda e, g=g: e.tensor_copy(out=ksn[:, 0, g, :], in_=Ksel[g][0:68, T:TT]), r=[('Ks', g, T)], w=['ksn'])
                S.dve(lambda e, g=g: e.tensor_copy(out=ksn[:, 1, g, :], in_=Kwin[g][0:68, T:TT]), r=[('Kw', g, T)], w=['ksn'])
            S.dve(lambda e: e.tensor_copy(out=gs_s, in_=gsig[0:24, T:TT]), r=[('gsig', T)], w=['gs_s'])
            stage_end(9)
            S.barrier()
            A.release()
            A.mark()
            XK = [A.alloc([128, 32, 128], F32) for _ in range(2)]
            XV = [A.alloc([128, 32, 128], F32) for _ in range(2)]
            tmp4 = xT_raw32[:, 0:8192].rearrange("p (l r d) -> p l r d", l=32, r=4)
            pt_i = A.alloc([32, NS, 2], I32)
            pt_f = A.alloc([32, NS * 2], F32)
            E32 = A.alloc([32, 128], F32)
            qcol_i = A.alloc([128, 1], I32)
            qcol_f = A.alloc([128, 1], F32)
            idx_f = A.alloc([128, NS * 2], F32)
            idx_i = A.alloc([128, NS * 2], I32)
            Wl4 = A.alloc([128, 32, 4], F32)
            slopes = A.alloc([128, 8], F32)
            posc = A.alloc([128, 2], F32)
            ABc = A.alloc([128, 2, 8], F32)
            poss = A.alloc([128, 2, 32], F32)
            ABs = A.alloc([128, 2, 8, 32], F32)
            posw = A.alloc([128, 4], F32)
            ABw = A.alloc([128, 8, 4], F32)
            Ex = A.alloc([128, 2, 128], BF16)
            qrep = A.alloc([64, 8, 128], BF16)
            qb = A.alloc([128, 8, 64], F32)
            kvc = Vwin_raw32[:, 0:512].rearrange("p (t c) -> p t c", t=2)
            tmpc = A.alloc([128, 2, 4, 64], F32)
            sc = A.alloc([128, 2, 8], F32)
            Esum = A.alloc([128, 8], F32)
            pcg = A.alloc([128, 2, 2], F32)
            impn = A.alloc([2, 256], F32)
            impd = A.alloc([2, 129], F32)
            impd2 = A.alloc([2, 129], F32)
            m8d = A.alloc([2, 16], F32)
            Md = A.alloc([2, 129], F32)
            MT = A.alloc([128, 2], BF16)
            mskd = A.alloc([128, 2, 2], F32)
            ss = Vwin_raw32[:, 512:1024].rearrange("p (t h l) -> p t h l", t=2, h=8)
            Pl = A.alloc([128, 2, 8], F32)
            Wn = Vsel_raw32[:, 0:1024].rearrange("p (l c) -> p l c", l=4)
            sw = A.alloc([128, 8, 4], F32)
            Plw = A.alloc([128, 8], F32)
            prodn = A.alloc([68, 2, 2, 4, NS], F32)
            pnew = A.alloc([1, 2, 2, 4, NS], F32)
            vrow = A.alloc([1, NS, 2, 128], F32)
            accd = A.alloc([4, 2, 64], F32)
            gts = A.alloc([4, 6, NS], F32)
            rsd = A.alloc([4, 4], F32)

            cache_v = {(c_, hf): I['cache_%s_%s' % (c_, hf)].rearrange("(b q) c -> b (q c)", q=32)
                       for c_ in ('cmp', 'sel') for hf in ('k', 'v')}

            def chain(addfn, fns, key, r=()):
                for fn in fns:
                    addfn(fn, r=[key] + list(r), w=[key])

            fns1 = [
                lambda e: e.iota(qcol_i, pattern=[[0, 1]], base=0, channel_multiplier=1),
                lambda e: e.memset(E32, 1.0),
                lambda e: e.memset(accd, 0.0),
                lambda e: e.affine_select(out=E32, in_=E32, pattern=[[1, 128]], compare_op=ALU.is_ge, fill=freg(e, 0.0), base=0,
                                          channel_multiplier=-4),
                lambda e: e.iota(posc, pattern=[[4096, 2]], base=31 - 8192, channel_multiplier=32,
                                 allow_small_or_imprecise_dtypes=True),
                lambda e: e.affine_select(out=E32, in_=E32, pattern=[[-1, 128]], compare_op=ALU.is_ge, fill=freg(e, 0.0), base=3,
                                          channel_multiplier=4),
                lambda e: e.iota(poss, pattern=[[4096, 2], [1, 32]], base=-8192, channel_multiplier=32,
                                 allow_small_or_imprecise_dtypes=True),
                lambda e: e.iota(posw, pattern=[[1, 4]], base=7680 - 8192, channel_multiplier=4,
                                 allow_small_or_imprecise_dtypes=True),
            ]
            for h in range(8):
                fns1.append(lambda e, h=h: e.memset(slopes[:, h:h + 1], 2.0 ** (-(h + 1))))
            for t in range(2):
                fns1.append(lambda e, t=t: e.memset(Ex[:, t, :], 1.0))
            for t in range(2):
                fns1.append(lambda e, t=t: e.affine_select(out=Ex[:, t, :], in_=Ex[:, t, :], pattern=[[1, 128]], compare_op=ALU.is_ge,
                                                          fill=freg(e, 0.0), base=128 * t, channel_multiplier=-2))
            for t in range(2):
                fns1.append(lambda e, t=t: e.affine_select(out=Ex[:, t, :], in_=Ex[:, t, :], pattern=[[-1, 128]], compare_op=ALU.is_ge,
                                                          fill=freg(e, 0.0), base=1 - 128 * t, channel_multiplier=2))
            chain(S.pool, fns1, 'dsetup')
            S.dma('sp', lambda e: e.dma_start(out=pt_i, in_=I['page_table'].rearrange("s (t j) -> j s t", j=32),
                                              allow_slow_non_contiguous=True), w=['pt_i'])
            S.dma('sp', lambda e: e.dma_start(out=Wl4[:, :, 0:2], in_=I['wk_cmp'].partition_broadcast(128)), w=['Wl4a'])
            S.dma('sp', lambda e: e.dma_start(out=Wl4[:, :, 2:4], in_=I['wv_cmp'].partition_broadcast(128)), w=['Wl4b'])
            for s in range(NS):
                S.dma('sp', lambda e, s=s: e.dma_start(out=vrow[0:1, s, 0, :], in_=kvsv[s:s + 1, 0, :]), r=['kvsv'],
                      w=[('vrow', s)])
                S.dma('sp', lambda e, s=s: e.dma_start(out=vrow[0:1, s, 1, :], in_=kvsv[s:s + 1, 1, :]), r=['kvsv'],
                      w=[('vrow', s)])

            fns2 = [
                lambda e: e.tensor_copy(out=pt_f, in_=pt_i.rearrange("p s t -> p (s t)")),
                lambda e: e.tensor_single_scalar(out=qcol_i, in_=qcol_i, scalar=3, op=ALU.bitwise_and),
                lambda e: e.tensor_tensor(out=ABc, in0=posc.unsqueeze(2).broadcast_to([128, 2, 8]),
                                          in1=slopes.unsqueeze(1).broadcast_to([128, 2, 8]), op=ALU.mult),
                lambda e: e.tensor_copy(out=qcol_f, in_=qcol_i),
                lambda e: e.tensor_tensor(out=ABw, in0=posw.unsqueeze(1).broadcast_to([128, 8, 4]),
                                          in1=slopes.unsqueeze(2).broadcast_to([128, 8, 4]), op=ALU.mult),
            ]
            for t in range(2):
                fns2.append(lambda e, t=t: e.tensor_tensor(out=ABs[:, t, :, :], in0=poss[:, t, :].unsqueeze(1).broadcast_to([128, 8, 32]),
                                                          in1=slopes.unsqueeze(2).broadcast_to([128, 8, 32]), op=ALU.mult))
            chain(S.dve, fns2, 'dsetup2', r=['dsetup', 'pt_i'])
            S.pe(lambda e: e.matmul(PS(1, 0, 128, 0, NS * 2), lhsT=E32, rhs=pt_f, start=True, stop=True),
                 r=['dsetup', 'dsetup2'], w=[pk(1)])
            S.dve(lambda e: e.tensor_scalar(out=idx_f, in0=PS(1, 0, 128, 0, NS * 2), scalar1=4.0, scalar2=qcol_f[:, 0:1],
                                            op0=ALU.mult, op1=ALU.add), r=[pk(1), 'dsetup2'], w=['idx_f'])
            chain(S.dve, [lambda e: e.tensor_copy(out=idx_i, in_=idx_f)], 'idx_i', r=['idx_f'])
            for br in range(2):
                S.dve(lambda e, br=br: e.tensor_tensor(
                    out=prodn[:, br, :, :, :], in0=qs_f, in1=ksn[:, br, :, :].unsqueeze(2).broadcast_to([68, 2, 4, NS]),
                    op=ALU.mult), r=['qs_f', 'ksn'], w=[('prodn', br)])
            S.pe(lambda e: e.matmul(PS(7, 0, 1, 0, 64), lhsT=ones_f[0:68, 0:1],
                                    rhs=prodn.rearrange("p a g r s -> p (a g r s)"), start=True, stop=True),
                 r=[('prodn', 0), ('prodn', 1), 'ones_f'], w=[pk(7)])
            S.act(lambda e: e.activation(out=pnew.rearrange("p a g r s -> p (a g r s)"), in_=PS(7, 0, 1, 0, 64), func=AF.Exp),
                  r=[pk(7)], w=['pnew'])
            def gmm_d(e):
                ins = None
                for g in range(2):
                    for j in range(3):
                        c0_ = 12 * g + j
                        ins = e.matmul(PS(7, 0, 4, 64 + (g * 3 + j) * NS, 64 + (g * 3 + j + 1) * NS),
                                       lhsT=ident_b[0:24, c0_:c0_ + 10:3], rhs=gs_s, start=True, stop=True)
                return ins
            S.pe(gmm_d, r=['gs_s', 'ident_b'], w=[pk(7)])
            S.act(lambda e: e.copy(out=gts.rearrange("p a s -> p (a s)"), in_=PS(7, 0, 4, 64, 64 + 6 * NS)), r=[pk(7)],
                  w=['gts'])

            def gather(cname, s, t, xb):
                col = s * 2 + t
                for (buf, hf, kn) in ((XK, 'k', 'XK'), (XV, 'v', 'XV')):
                    S.dma('pool', lambda e, buf=buf, hf=hf: e.indirect_dma_start(
                        out=buf[xb].rearrange("p l c -> p (l c)"), out_offset=None, in_=cache_v[(cname, hf)],
                        in_offset=bass.IndirectOffsetOnAxis(ap=idx_i[:, col:col + 1], axis=0)), r=['idx_i'], w=[(kn, xb)])

            for s in range(NS):
                for t in range(2):
                    gather('cmp', s, t, t)
                    for hi_, (buf, kn) in enumerate(((XK, 'XK'), (XV, 'XV'))):
                        S.dve(lambda e, t=t, buf=buf, hi_=hi_: e.tensor_tensor(
                            out=buf[t].rearrange("p l (a d) -> p l a d", a=2), in0=buf[t].rearrange("p l (a d) -> p l a d", a=2),
                            in1=Wl4[:, :, 2 * hi_:2 * hi_ + 2].unsqueeze(3).broadcast_to([128, 32, 2, 64]), op=ALU.mult),
                            r=[(kn, t), 'Wl4a', 'Wl4b'], w=[(kn, t)])
                        S.dve(lambda e, t=t, buf=buf, hi_=hi_: e.tensor_reduce(
                            out=kvc[:, t, hi_ * 128:(hi_ + 1) * 128], in_=buf[t].rearrange("p l c -> p c l"), axis=AX.X, op=ALU.add),
                            r=[(kn, t)], w=[('kvc', t)])
                stage_end(106)
                S.dve(lambda e, s=s: e.tensor_copy(
                    out=qrep, in_=qs_f[0:64, :, :, s].rearrange("p g r -> p (g r)").unsqueeze(2).broadcast_to([64, 8, 128])),
                    r=['qs_f'], w=['qrep'])

                def qbm(e):
                    ins = None
                    for h in range(8):
                        ins = e.matmul(PS(0, 0, 128, h * 64, (h + 1) * 64), lhsT=qrep[:, h, :], rhs=ident_b[0:64, 0:64],
                                       start=True, stop=True)
                    return ins
                S.pe(qbm, r=['qrep', 'ident_b'], w=[pk(0)])
                S.act(lambda e: e.copy(out=qb.rearrange("p h d -> p (h d)"), in_=PS(0)), r=[pk(0)], w=['qb'])
                for t in range(2):
                    S.dve(lambda e, t=t: e.tensor_tensor(
                        out=tmpc, in0=kvc[:, t, 0:128].rearrange("p (g d) -> p g d", g=2).unsqueeze(2).broadcast_to([128, 2, 4, 64]),
                        in1=qb.rearrange("p (g r) d -> p g r d", g=2), op=ALU.mult),
                        r=[('kvc', t), 'qb'], w=['tmpc'])
                    S.dve(lambda e, t=t: e.tensor_reduce(out=sc[:, t, :], in_=tmpc.rearrange("p g r d -> p (g r) d"),
                                                         axis=AX.X, op=ALU.add), r=['tmpc'], w=[('sc', t)])
                S.dve(lambda e: e.tensor_tensor(out=sc, in0=sc, in1=ABc, op=ALU.add), r=[('sc', 0), ('sc', 1), 'dsetup2'],
                      w=['sc'])
                S.act(lambda e: e.activation(out=sc, in_=sc, func=AF.Exp), r=['sc'], w=['sc'])
                S.dve(lambda e: e.tensor_tensor(out=Esum, in0=sc[:, 0, :], in1=sc[:, 1, :], op=ALU.add), r=['sc'], w=['Esum'])
                S.pe(lambda e: e.matmul(PS(1, 0, 128, 0, 8), lhsT=ones_f, rhs=Esum, start=True, stop=True),
                     r=['Esum', 'ones_f'], w=[pk(1)])
                S.dve(lambda e: e.reciprocal(out=Esum, in_=PS(1, 0, 128, 0, 8)), r=[pk(1)], w=['Esum'])
                S.dve(lambda e: e.tensor_tensor(out=sc, in0=sc, in1=Esum.unsqueeze(1).broadcast_to([128, 2, 8]), op=ALU.mult),
                      r=['sc', 'Esum'], w=['sc'])

                def ocm(e):
                    ins = None
                    for g in range(2):
                        for t in range(2):
                            ins = e.matmul(PS(2, 0, 4, g * 64, (g + 1) * 64), lhsT=sc[:, t, 4 * g:4 * g + 4],
                                           rhs=kvc[:, t, 128 + g * 64:128 + (g + 1) * 64], start=(t == 0), stop=(t == 1))
                    return ins
                S.pe(ocm, r=['sc', ('kvc', 0), ('kvc', 1)], w=[pk(2)])
                S.dve(lambda e: e.tensor_reduce(out=pcg, in_=sc.rearrange("p t (g r) -> p t g r", g=2), axis=AX.X, op=ALU.add),
                      r=['sc'], w=['pcg'])

                def trp(e):
                    ins = None
                    for t in range(2):
                        ins = e.transpose(out=PS(3, 0, 2, t * 128, (t + 1) * 128), in_=pcg[:, t, :], identity=ident_f)
                    return ins
                S.pe(trp, r=['pcg', 'ident_f'], w=[pk(3)])
                S.act(lambda e: e.copy(out=impn, in_=PS(3, 0, 2, 0, 256)), r=[pk(3)], w=['impn'])
                S.dve(lambda e: e.tensor_reduce(out=impd[:, 0:128], in_=impn.rearrange("p (a b) -> p a b", b=2), axis=AX.X,
                                                op=ALU.add), r=['impn'], w=['impd'])

                S.pool(lambda e: e.memset(impd[:, 0:1], FORCE), r=['impd'], w=['impd'])
                S.pool(lambda e: e.memset(impd[:, 127:129], FORCE), r=['impd'], w=['impd'])
                S.dve(lambda e: e.max(out=m8d[:, 0:8], in_=impd), r=['impd'], w=['m8da'])
                S.dve(lambda e: e.match_replace(out=impd2, in_to_replace=m8d[:, 0:8], in_values=impd, imm_value=-3.0e38),
                      r=['impd', 'm8da'], w=['impd2'])
                S.dve(lambda e: e.max(out=m8d[:, 8:16], in_=impd2), r=['impd2'], w=['m8db'])
                S.dve(lambda e: e.tensor_scalar(out=Md, in0=impd, scalar1=m8d[:, 15:16], scalar2=None, op0=ALU.is_ge),
                      r=['impd', 'm8db'], w=['Md'])
                S.pe(lambda e: e.transpose(out=PS(3, 0, 128, 256, 258), in_=Md[:, 0:128], identity=ident_f[0:2, 0:2]),
                     r=['Md', 'ident_f'], w=[pk(3)])
                S.act(lambda e: e.copy(out=MT, in_=PS(3, 0, 128, 256, 258)), r=[pk(3)], w=['MT'])

                def mskm(e):
                    ins = None
                    for t in range(2):
                        ins = e.matmul(PS(3, 0, 128, 260 + 2 * t, 262 + 2 * t), lhsT=Ex[:, t, :], rhs=MT, start=True, stop=True)
                    return ins
                S.pe(mskm, r=['MT', 'dsetup'], w=[pk(3)])
                S.act(lambda e: e.copy(out=mskd.rearrange("p t g -> p (t g)"), in_=PS(3, 0, 128, 260, 264)), r=[pk(3)],
                      w=['mskd'])
                for t in range(2):
                    gather('sel', s, t, t)
                    for g in range(2):
                        S.dve(lambda e, t=t, g=g: e.tensor_tensor(
                            out=tmp4, in0=XK[t][:, :, g * 64:(g + 1) * 64].unsqueeze(2).broadcast_to([128, 32, 4, 64]),
                            in1=qb[:, 4 * g:4 * g + 4, :].unsqueeze(1).broadcast_to([128, 32, 4, 64]), op=ALU.mult),
                            r=[('XK', t), 'qb'], w=['tmp4'])
                        S.dve(lambda e, t=t, g=g: e.tensor_reduce(
                            out=ss[:, t, 4 * g:4 * g + 4, :].rearrange("p r l -> p l r"), in_=tmp4, axis=AX.X, op=ALU.add),
                            r=['tmp4'], w=[('ss', t, g)])
                SSK = [('ss', t, g) for t in range(2) for g in range(2)]
                S.dve(lambda e: e.tensor_tensor(out=ss, in0=ss, in1=ABs, op=ALU.add), r=SSK + ['dsetup2'], w=['ss'])
                S.act(lambda e: e.activation(out=ss, in_=ss, func=AF.Exp), r=['ss'], w=['ss'])
                for t in range(2):
                    S.dve(lambda e, t=t: e.tensor_tensor(
                        out=ss[:, t, :, :].rearrange("p (g r) l -> p g (r l)", g=2),
                        in0=ss[:, t, :, :].rearrange("p (g r) l -> p g (r l)", g=2),
                        in1=mskd[:, t, :].unsqueeze(2).broadcast_to([128, 2, 128]), op=ALU.mult),
                        r=['ss', 'mskd'], w=['ss'])
                S.dve(lambda e: e.tensor_reduce(out=Pl, in_=ss, axis=AX.X, op=ALU.add), r=['ss'], w=['Pl'])

                def pvs(e, s=s):
                    ins = None
                    for g in range(2):
                        k = 0
                        for t in range(2):
                            for l in range(32):
                                ins = e.matmul(PS(4, 0, 4, g * 64, (g + 1) * 64), lhsT=ss[:, t, 4 * g:4 * g + 4, l],
                                               rhs=XV[t][:, l, g * 64:(g + 1) * 64], start=(k == 0), stop=False)
                                k += 1
                        ins = e.matmul(PS(4, 0, 4, g * 64, (g + 1) * 64), lhsT=pnew[0:1, 0, g, :, s],
                                       rhs=vrow[0:1, s, 0, g * 64:(g + 1) * 64], start=False, stop=True)
                        for t in range(2):
                            ins = e.matmul(PS(5, 0, 4, g, g + 1), lhsT=Pl[:, t, 4 * g:4 * g + 4], rhs=ones_f[:, 0:1],
                                           start=(t == 0), stop=False)
                        ins = e.matmul(PS(5, 0, 4, g, g + 1), lhsT=pnew[0:1, 0, g, :, s], rhs=ones_f[0:1, 0:1],
                                       start=False, stop=True)
                    return ins
                S.pe(pvs, r=['ss', 'Pl', ('XV', 0), ('XV', 1), 'pnew', ('vrow', s), 'ones_f'], w=[pk(4), pk(5)])
                S.dma('sp', lambda e, s=s: e.dma_start(out=Wn, in_=I['cache_win'][s * 512:(s + 1) * 512, :].rearrange(
                    "(p j) c -> p j c", j=4)), w=['Wn'])
                for g in range(2):
                    S.dve(lambda e, g=g: e.tensor_tensor(
                        out=tmp4[:, 0:4, :, :], in0=Wn[:, :, g * 64:(g + 1) * 64].unsqueeze(2).broadcast_to([128, 4, 4, 64]),
                        in1=qb[:, 4 * g:4 * g + 4, :].unsqueeze(1).broadcast_to([128, 4, 4, 64]), op=ALU.mult),
                        r=['Wn', 'qb'], w=['tmp4'])
                    S.dve(lambda e, g=g: e.tensor_reduce(out=sw[:, 4 * g:4 * g + 4, :].rearrange("p r l -> p l r"),
                                                         in_=tmp4[:, 0:4, :, :], axis=AX.X, op=ALU.add),
                          r=['tmp4'], w=[('sw', g)])
                S.dve(lambda e: e.tensor_tensor(out=sw, in0=sw, in1=ABw, op=ALU.add), r=[('sw', 0), ('sw', 1), 'dsetup2'],
                      w=['sw'])
                S.act(lambda e: e.activation(out=sw, in_=sw, func=AF.Exp), r=['sw'], w=['sw'])
                S.pool(lambda e: e.memset(sw[0:1, :, 0:1], 0.0), r=['sw'], w=['sw'])
                S.dve(lambda e: e.tensor_reduce(out=Plw, in_=sw, axis=AX.X, op=ALU.add), r=['sw'], w=['Plw'])

                def pvw(e, s=s):
                    ins = None
                    for g in range(2):
                        for l in range(4):
                            ins = e.matmul(PS(6, 0, 4, g * 64, (g + 1) * 64), lhsT=sw[:, 4 * g:4 * g + 4, l],
                                           rhs=Wn[:, l, 128 + g * 64:128 + (g + 1) * 64], start=(l == 0), stop=False)
                        ins = e.matmul(PS(6, 0, 4, g * 64, (g + 1) * 64), lhsT=pnew[0:1, 1, g, :, s],
                                       rhs=vrow[0:1, s, 1, g * 64:(g + 1) * 64], start=False, stop=True)
                        ins = e.matmul(PS(5, 0, 4, 2 + g, 3 + g), lhsT=Plw[:, 4 * g:4 * g + 4], rhs=ones_f[:, 0:1],
                                       start=True, stop=False)
                        ins = e.matmul(PS(5, 0, 4, 2 + g, 3 + g), lhsT=pnew[0:1, 1, g, :, s], rhs=ones_f[0:1, 0:1],
                                       start=False, stop=True)
                    return ins
                S.pe(pvw, r=['sw', 'Plw', 'Wn', 'pnew', ('vrow', s), 'ones_f'], w=[pk(6), pk(5)])
                S.dve(lambda e: e.reciprocal(out=rsd, in_=PS(5, 0, 4, 0, 4)), r=[pk(5)], w=['rsd'])
                for g in range(2):
                    S.dve(lambda e, g=g, s=s: e.tensor_scalar(out=accd[:, g, :], in0=PS(2, 0, 4, g * 64, (g + 1) * 64),
                                                              scalar1=gts[:, g * 3 + 0, s:s + 1], scalar2=None, op0=ALU.mult),
                          r=[pk(2), 'gts'], w=[('accd', g)])
                    for bi, (bank, col) in enumerate([(4, g), (6, 2 + g)]):
                        S.dve(lambda e, g=g, s=s, bi=bi, col=col: e.tensor_tensor(
                            out=rsd[:, col:col + 1], in0=rsd[:, col:col + 1], in1=gts[:, g * 3 + 1 + bi, s:s + 1], op=ALU.mult),
                            r=['rsd', 'gts'], w=['rsd'])
                        S.dve(lambda e, g=g, bank=bank, col=col: e.scalar_tensor_tensor(
                            out=accd[:, g, :], in0=PS(bank, 0, 4, g * 64, (g + 1) * 64), scalar=rsd[:, col:col + 1],
                            in1=accd[:, g, :], op0=ALU.mult, op1=ALU.add), r=[pk(bank), 'rsd', ('accd', g)], w=[('accd', g)])
                    S.pe(lambda e, g=g: e.transpose(out=PS(7, 0, 64, 128 + 4 * g, 132 + 4 * g), in_=accd[:, g, :],
                                                    identity=ident_f[0:4, 0:4]), r=[('accd', g), 'ident_f'], w=[pk(7)])
                    S.dve(lambda e, g=g, s=s: e.tensor_tensor(
                        out=ybT[0:64, 4 * g:4 * g + 4, T + s], in0=PS(7, 0, 64, 128 + 4 * g, 132 + 4 * g),
                        in1=ybT[0:64, 4 * g:4 * g + 4, T + s], op=ALU.mult),
                        r=[pk(7)] + [('ybT', 4 * g + r_, T) for r_ in range(4)],
                        w=[('ybT', 4 * g + r_, T) for r_ in range(4)] + ['dec_yb'])
            if os.environ.get('KDEV_DUMP'):
                yp = O['y_prompt']
                S.dma('sp', lambda e: e.dma_start(out=yp[0:128, :], in_=Xb[0][:, 0:4, :].rearrange("p l c -> p (l c)")), r=[('Xb', 0)])
                S.dma('sp', lambda e: e.dma_start(out=yp[128:256, 0:512], in_=kvc.rearrange("p t c -> p (t c)")), r=[('kvc', 0), ('kvc', 1)])
                S.dma('sp', lambda e: e.dma_start(out=yp[256:384, 0:16], in_=sc.rearrange("p t h -> p (t h)")), r=['sc'])
                S.dma('sp', lambda e: e.dma_start(out=yp[384:512, 0:512], in_=ss.rearrange("p t h l -> p (t h l)")), r=['ss'])
                S.dma('sp', lambda e: e.dma_start(out=yp[512:516, 0:4], in_=rsd), r=['rsd'])
                S.dma('sp', lambda e: e.dma_start(out=yp[516:520, 0:128], in_=accd.rearrange("p g d -> p (g d)")), r=[('accd', 0), ('accd', 1)])
                S.dma('sp', lambda e: e.dma_start(out=yp[640:768, 0:8], in_=idx_f), r=['idx_i'])
                S.dma('sp', lambda e: e.dma_start(out=yp[768:896, 0:4], in_=mskd.rearrange("p t g -> p (t g)")), r=['mskd'])
                S.dma('sp', lambda e: e.dma_start(out=yp[896:898, 0:129], in_=impd), r=['impd'])
                S.dma('sp', lambda e: e.dma_start(out=yp[900:1028, 0:512], in_=qb.rearrange("p h d -> p (h d)")), r=['qb'])
                S.dma('sp', lambda e: e.dma_start(out=yp[1028:1029, 0:64], in_=pnew.rearrange("p a g r s -> p (a g r s)")), r=['pnew'])
                S.dma('sp', lambda e: e.dma_start(out=yp[1030:1034, 0:24], in_=gts.rearrange("p a s -> p (a s)")), r=['gts'])
            S.barrier()
            A.release()
            stage_end(11)
            ALPHA = float((2 * 2) ** 0.25)
            x1T = xT
            A.mark()
            wo_a = A.alloc([128, 4, DM], BF16)
            wo_b = A.alloc([64, 8, DM], BF16)
            lnp = A.alloc([128, 2, DM], F32)
            xres = [A.alloc([128, DM], F32) for _ in range(2)]
            x1b = [A.alloc([128, DM], BF16) for _ in range(2)]
            bst = A.alloc([128, 2, 6], F32)
            mv = A.alloc([128, 4], F32)
            for j in range(4):
                S.dma('pool', lambda e, j=j: e.dma_start(out=wo_a[:, j, :], in_=I['w_out_even'][j * 128:(j + 1) * 128, :]),
                      w=[('wo_a', j)])
            for h in range(8):
                S.dma('pool', lambda e, h=h: e.dma_start(out=wo_b[:, h, :],
                                                         in_=I['w_out_even'][512 + h * 64:512 + (h + 1) * 64, :]),
                      w=[('wo_b', h)])
            S.dma('sp', lambda e: e.dma_start(out=lnp[:, 0, :], in_=I['ln_g'][0:1, :].partition_broadcast(128)), w=[('lnp', 0)])
            S.dma('sp', lambda e: e.dma_start(out=lnp[:, 1, :], in_=I['ln_b'][0:1, :].partition_broadcast(128)), w=[('lnp', 1)])
            WO = [('wo_a', j) for j in range(4)] + [('wo_b', h) for h in range(8)]

            def resid_ln(n, rows, c0, xsrc, yk, lidx, ya_, yb_, outs):
                b = n % 2
                S.dma('sp', lambda e: e.dma_start(out=xres[b][0:rows, :], in_=xsrc), w=[('xres', b)])

                def mm(e):
                    ins = None
                    for cb in range(2):
                        k = 0
                        tot = len(ya_) + len(yb_)
                        for (lt, rt) in ya_ + yb_:
                            ins = e.matmul(PS(cb, 0, rows, 0, 512), lhsT=lt[:, c0:c0 + rows], rhs=rt[:, cb * 512:(cb + 1) * 512],
                                           start=(k == 0), stop=(k == tot - 1))
                            k += 1
                    return ins
                S.pe(mm, r=yk + WO, w=[pk(0), pk(1)])
                for cb in range(2):
                    S.dve(lambda e, cb=cb: e.scalar_tensor_tensor(
                        out=xres[b][0:rows, cb * 512:(cb + 1) * 512], in0=xres[b][0:rows, cb * 512:(cb + 1) * 512],
                        scalar=ALPHA, in1=PS(cb, 0, rows, 0, 512), op0=ALU.mult, op1=ALU.add),
                        r=[('xres', b), pk(cb)], w=[('xres', b)])
                    S.dve(lambda e, cb=cb: e.bn_stats(out=bst[0:rows, cb, :], in_=xres[b][0:rows, cb * 512:(cb + 1) * 512]),
                          r=[('xres', b)], w=[('bst', cb)])
                S.dve(lambda e: e.bn_aggr(out=mv[0:rows, 0:2], in_=bst[0:rows, :, :].rearrange("p a b -> p (a b)")),
                      r=[('bst', 0), ('bst', 1)], w=['mv'])
                S.dve(lambda e: e.tensor_scalar(out=mv[0:rows, 2:3], in0=mv[0:rows, 1:2], scalar1=1e-5, scalar2=None,
                                                op0=ALU.add), r=['mv'], w=['mv2'])
                S.act(lambda e: e.activation(out=mv[0:rows, 2:3], in_=mv[0:rows, 2:3], func=AF.Ln), r=['mv2'], w=['mv2'])
                S.act(lambda e: e.activation(out=mv[0:rows, 2:3], in_=mv[0:rows, 2:3], func=AF.Exp, scale=-0.5),
                      r=['mv2'], w=['mv2'])
                S.dve(lambda e: e.tensor_scalar(out=xres[b][0:rows, :], in0=xres[b][0:rows, :], scalar1=mv[0:rows, 0:1],
                                                scalar2=mv[0:rows, 2:3], op0=ALU.subtract, op1=ALU.mult),
                      r=[('xres', b), 'mv', 'mv2'], w=[('xres', b)])
                S.dve(lambda e: e.tensor_tensor(out=xres[b][0:rows, :], in0=xres[b][0:rows, :], in1=lnp[0:rows, 2 * lidx, :],
                                                op=ALU.mult), r=[('xres', b), ('lnp', 2 * lidx)], w=[('xres', b)])
                S.pool(lambda e: e.tensor_tensor(out=xres[b][0:rows, :], in0=xres[b][0:rows, :],
                                                 in1=lnp[0:rows, 2 * lidx + 1, :], op=ALU.add),
                       r=[('xres', b), ('lnp', 2 * lidx + 1)], w=[('xres', b)])
                outs(b)

            def l0_outs(n, rows, c0):
                def f(b):
                    S.dma('sp', lambda e: e.dma_start(out=x1_scr[c0:c0 + rows, :], in_=xres[b][0:rows, :]),
                          r=[('xres', b)], w=[('x1scr', n)])
                    if n == 15:
                        S.dma('sp', lambda e: e.dma_start(out=O['shift_p'], in_=xres[b][127:128, :]), r=[('xres', b)])
                    if n == 16:
                        S.dma('sp', lambda e: e.dma_start(out=O['shift_s'], in_=xres[b][0:rows, :]), r=[('xres', b)])
                    S.act(lambda e: e.copy(out=x1b[b][0:rows, :], in_=xres[b][0:rows, :]), r=[('xres', b)], w=[('x1b', b)])
                    bank = 2 + (n % 2)

                    def tr(e):
                        ins = None
                        for c in range(8):
                            ins = e.transpose(out=PSB(bank, 0, 128, c * rows, (c + 1) * rows),
                                              in_=x1b[b][0:rows, c * 128:(c + 1) * 128], identity=ident_b[0:rows, 0:rows])
                        return ins
                    S.pe(tr, r=[('x1b', b), 'ident_b'], w=[pk(bank)])
                    S.dve(lambda e: e.tensor_copy(out=x1T[:, :, c0:c0 + rows],
                                                  in_=PSB(bank, 0, 128, 0, 8 * rows).rearrange("p (c t) -> p c t", c=8)),
                          r=[pk(bank)], w=[('x1T', n)])
                return f

            for n in range(17):
                rows = 128 if n < 16 else NS
                c0 = n * 128 if n < 16 else T
                xsrc = I['xp'][c0:c0 + 128, :] if n < 16 else I['xs']
                blk = (c0 // 512) * 512 if n < 16 else T
                yk = [('yaT', j, blk) for j in range(4)] + [('ybT', h, blk) for h in range(8)]
                if n == 16:
                    yk += ['dec_yb']
                ya_ = [(yaT[:, j, :], wo_a[:, j, :]) for j in range(4)]
                yb_ = [(ybT[0:64, h, :], wo_b[0:64, h, :]) for h in range(8)]
                resid_ln(n, rows, c0, xsrc, yk, 0, ya_, yb_, l0_outs(n, rows, c0))
            S.barrier()
            A.release()
            stage_end(10)
            S.barrier()
            A.top = L1_BASE
            A.marks = []
            A.mark()
            dT = A.alloc([128, 8, TT], BF16)
            xsh = A.alloc([128, 8, TT], BF16)
            muT = A.alloc([128, 6, 8], F32)
            vecs = A.alloc([128, 5, 8], F32)
            shb = A.alloc([NS, DM], BF16)
            stg = [A.alloc([128, 1024], F32) for _ in range(3)]
            wl1 = [A.alloc([128, 8, 128], BF16) for _ in range(3)]
            wfull = A.alloc([128, 8, 1024], BF16)
            lo1 = A.alloc([128, 8, 64], BF16)
            lo2 = A.alloc([64, 1024], BF16)
            t1T = A.alloc([64, TT], BF16)
            S.dma('sp', lambda e: e.dma_start(out=muT, in_=I['mu_c'].rearrange("i (c p) -> p i c", p=128),
                                              allow_slow_non_contiguous=True), w=['muT'])
            for i, nm in enumerate(['w0', 'a0', 'k_k', 'k_a', 'r_k']):
                S.dma('sp', lambda e, i=i, nm=nm: e.dma_start(out=vecs[:, i, :], in_=I[nm].rearrange("o (c p) -> p (o c)", p=128),
                                                              allow_slow_non_contiguous=True), w=[('vecs', i)])
            S.dma('pool', lambda e: e.dma_start(out=shb, in_=I['state_shift']), w=['shb'])

            def trsh(e):
                ins = None
                for c in range(8):
                    ins = e.transpose(out=PSB(0, 0, 128, c * NS, (c + 1) * NS), in_=shb[0:NS, c * 128:(c + 1) * 128],
                                      identity=ident_b[0:NS, 0:NS])
                return ins
            S.pe(trsh, r=['shb', 'ident_b'], w=[pk(0)])
            X1K = [('x1T', n) for n in range(17)]
            S.dve(lambda e: e.tensor_tensor(out=dT[:, :, T:TT], in0=PSB(0, 0, 128, 0, 8 * NS).rearrange("p (c t) -> p c t", c=8),
                                            in1=x1T[:, :, T:TT], op=ALU.subtract), r=[pk(0)] + X1K, w=['dT'])
            S.dve(lambda e: e.tensor_tensor(out=dT[:, :, 1:T], in0=x1T[:, :, 0:T - 1], in1=x1T[:, :, 1:T], op=ALU.subtract),
                  r=X1K, w=['dT'])
            S.dve(lambda e: e.tensor_scalar(out=dT[:, :, 0:1], in0=x1T[:, :, 0:1], scalar1=-1.0, scalar2=None, op0=ALU.mult),
                  r=X1K, w=['dT'])

            def mk_xsh(i):
                for c in range(8):
                    eng = S.dve if c % 2 == 0 else S.pool
                    if c % 4 != 3:
                        S.dve(lambda e, c=c: e.scalar_tensor_tensor(out=xsh[:, c, :], in0=dT[:, c, :], scalar=muT[:, i, c:c + 1],
                                                                    in1=x1T[:, c, :], op0=ALU.mult, op1=ALU.add),
                              r=['dT', 'muT'] + X1K, w=[('xsh', c)])
                    else:
                        S.pool(lambda e, c=c: e.tensor_scalar(out=xsh[:, c, :], in0=dT[:, c, :], scalar1=muT[:, i, c:c + 1],
                                                              scalar2=None, op0=ALU.mult), r=['dT', 'muT'], w=[('xsh', c)])
                        S.pool(lambda e, c=c: e.tensor_tensor(out=xsh[:, c, :], in0=xsh[:, c, :], in1=x1T[:, c, :], op=ALU.add),
                               r=[('xsh', c)] + X1K, w=[('xsh', c)])
            XSH = [('xsh', c) for c in range(8)]
            wrk_t = I['w_rkvz'].rearrange("(i c p) f -> i p c f", i=4, p=128)
            sgc = [0]
            wlc = [0]

            def proj_fm1(widx, dst, func, bias_i):
                for pr in range(8):
                    wi = wlc[0] % 3
                    wlc[0] += 1
                    S.dma('pool', lambda e, wi=wi, pr=pr: e.dma_start(out=wl1[wi], in_=wrk_t[widx][:, :, pr * 128:(pr + 1) * 128]),
                          w=[('wl1', wi)])
                    si = sgc[0] % 3
                    sgc[0] += 1
                    for bi_, (c0, n) in enumerate(BLKS):
                        bank = pbank[0] % 4
                        pbank[0] += 1

                        def mm(e, wi=wi, c0=c0, n=n, bank=bank):
                            ins = None
                            for c in range(8):
                                ins = e.matmul(PS(bank, 0, 128, 0, n), lhsT=wl1[wi][:, c, :], rhs=xsh[:, c, c0:c0 + n],
                                               start=(c == 0), stop=(c == 7))
                            return ins
                        S.pe(mm, r=[('wl1', wi)] + XSH, w=[pk(bank)])
                        cc0 = c0 if c0 < T else 0
                        sdst = stg[si][:, cc0 % 1024:cc0 % 1024 + n] if c0 < T else stg[si][:, 0:n]
                        S.act(lambda e, bank=bank, n=n, sdst=sdst: e.copy(out=sdst, in_=PS(bank, 0, 128, 0, n)),
                              r=[pk(bank)], w=[('stg', si)])
                        if bi_ in (1, 3, 4):
                            lo = {1: 0, 3: 1024, 4: T}[bi_]
                            wdt = 1024 if bi_ != 4 else NS
                            S.dma('sp', lambda e, si=si, pr=pr, lo=lo, wdt=wdt: e.dma_start(out=dst[:, pr, lo:lo + wdt],
                                                                                           in_=stg[si][:, 0:wdt]),
                                  r=[('stg', si)], w=[('scr', widx, pr)])
                            if bi_ != 4:
                                si = sgc[0] % 3
                                sgc[0] += 1

            def proj_tm1(widx, dst, silu):
                for c in range(8):
                    S.dma('pool', lambda e, c=c: e.dma_start(out=wfull[:, c, :], in_=wrk_t[widx][:, c, :]), w=[('wfull', c)])
                WF = [('wfull', c) for c in range(8)]
                for n in range(17):
                    rows = 128 if n < 16 else NS
                    c0 = n * 128 if n < 16 else T
                    si = sgc[0] % 3
                    sgc[0] += 1

                    def mm(e, rows=rows, c0=c0):
                        ins = None
                        for cb in range(2):
                            for c in range(8):
                                ins = e.matmul(PS(4 + cb, 0, rows, 0, 512), lhsT=xsh[:, c, c0:c0 + rows],
                                               rhs=wfull[:, c, cb * 512:(cb + 1) * 512], start=(c == 0), stop=(c == 7))
                        return ins
                    S.pe(mm, r=WF + XSH, w=[pk(4), pk(5)])
                    for cb in range(2):
                        if silu:
                            S.act(lambda e, cb=cb, rows=rows, si=si: e.activation(
                                out=stg[si][0:rows, cb * 512:(cb + 1) * 512], in_=PS(4 + cb, 0, rows, 0, 512), func=AF.Silu),
                                r=[pk(4 + cb)], w=[('stg', si)])
                        else:
                            S.act(lambda e, cb=cb, rows=rows, si=si: e.copy(out=stg[si][0:rows, cb * 512:(cb + 1) * 512],
                                                                            in_=PS(4 + cb, 0, rows, 0, 512)),
                                  r=[pk(4 + cb)], w=[('stg', si)])
                    S.dma('sp', lambda e, rows=rows, c0=c0, si=si: e.dma_start(out=dst[c0:c0 + rows, :], in_=stg[si][0:rows, :]),
                          r=[('stg', si)], w=[('scrt', widx, n)])

            def proj_lora(mi, w1n, w2n, vi, dst, use_tanh):
                S.dma('pool', lambda e: e.dma_start(out=lo1, in_=I[w1n].rearrange("(c p) f -> p c f", p=128)), w=['lo1'])
                S.dma('pool', lambda e: e.dma_start(out=lo2, in_=I[w2n]), w=['lo2'])
                for (c0, n) in BLKS:
                    bank = pbank[0] % 4
                    pbank[0] += 1

                    def mm(e, c0=c0, n=n, bank=bank):
                        ins = None
                        for c in range(8):
                            ins = e.matmul(PS(bank, 0, 64, 0, n), lhsT=lo1[:, c, :], rhs=xsh[:, c, c0:c0 + n],
                                           start=(c == 0), stop=(c == 7))
                        return ins
                    S.pe(mm, r=['lo1'] + XSH, w=[pk(bank)])
                    if use_tanh:
                        S.act(lambda e, c0=c0, n=n, bank=bank: e.activation(out=t1T[:, c0:c0 + n], in_=PS(bank, 0, 64, 0, n),
                                                                            func=AF.Tanh), r=[pk(bank)], w=[('t1T', c0)])
                    else:
                        S.act(lambda e, c0=c0, n=n, bank=bank: e.copy(out=t1T[:, c0:c0 + n], in_=PS(bank, 0, 64, 0, n)),
                              r=[pk(bank)], w=[('t1T', c0)])
                T1K = [('t1T', c0) for (c0, n) in BLKS]
                for pr in range(8):
                    si = sgc[0] % 3
                    sgc[0] += 1
                    for bi_, (c0, n) in enumerate(BLKS):
                        bank = pbank[0] % 4
                        pbank[0] += 1
                        S.pe(lambda e, pr=pr, c0=c0, n=n, bank=bank: e.matmul(
                            PS(bank, 0, 128, 0, n), lhsT=lo2[:, pr * 128:(pr + 1) * 128], rhs=t1T[:, c0:c0 + n],
                            start=True, stop=True), r=['lo2'] + T1K, w=[pk(bank)])
                        sdst = stg[si][:, c0 % 1024:c0 % 1024 + n] if c0 < T else stg[si][:, 0:n]
                        S.act(lambda e, bank=bank, n=n, sdst=sdst, pr=pr: e.activation(
                            out=sdst, in_=PS(bank, 0, 128, 0, n), func=AF.Sigmoid, bias=vecs[:, vi, pr:pr + 1]),
                            r=[pk(bank), ('vecs', vi)], w=[('stg', si)])
                        if bi_ in (1, 3, 4):
                            lo = {1: 0, 3: 1024, 4: T}[bi_]
                            wdt = 1024 if bi_ != 4 else NS
                            S.dma('sp', lambda e, si=si, pr=pr, lo=lo, wdt=wdt: e.dma_start(out=dst[:, pr, lo:lo + wdt],
                                                                                           in_=stg[si][:, 0:wdt]),
                                  r=[('stg', si)], w=[('scr', mi, pr)])
                            if bi_ != 4:
                                si = sgc[0] % 3
                                sgc[0] += 1

            mk_xsh(0)
            proj_fm1(0, r_scr, None, None)
            mk_xsh(1)
            proj_fm1(1, k_scr, None, None)
            mk_xsh(2)
            proj_tm1(2, v_scr, False)
            mk_xsh(3)
            proj_tm1(3, z_scr, True)
            mk_xsh(4)
            proj_lora(4, 'w1', 'w2', 0, sw_scr, True)
            mk_xsh(5)
            proj_lora(5, 'a1', 'a2', 1, a_scr, False)
            S.barrier()
            A.release()
            stage_end(12)
            A.top = xT_off
            A.marks = []
            A.mark()
            C = 64
            NEG_E05 = -float(np.exp(-0.5))
            wo_o = A.alloc([128, 8, DM], BF16)
            for c in range(8):
                S.dma('pool', lambda e, c=c: e.dma_start(out=wo_o[:, c, :], in_=I['w_out_odd'][c * 128:(c + 1) * 128, :]),
                      w=[('wo_o', c)])
            WOO = [('wo_o', c) for c in range(8)]
            gnp = A.alloc([64, 4, DM], F32)
            S.dma('sp', lambda e: e.dma_start(out=gnp[:, 0, :], in_=I['gn_g'].partition_broadcast(64)), w=[('gnp', 0)])
            S.dma('sp', lambda e: e.dma_start(out=gnp[:, 1, :], in_=I['gn_b'].partition_broadcast(64)), w=[('gnp', 1)])
            S.dma('sp', lambda e: e.dma_start(out=gnp[:, 2, :], in_=I['ln_g'][1:2, :].partition_broadcast(64)), w=[('gnp', 2)])
            S.dma('sp', lambda e: e.dma_start(out=gnp[:, 3, :], in_=I['ln_b'][1:2, :].partition_broadcast(64)), w=[('gnp', 3)])
            vec1 = A.alloc([64, 5, 16], F32)
            for i, nm in enumerate(['w0', 'a0', 'k_k', 'k_a', 'r_k']):
                S.dma('sp', lambda e, i=i, nm=nm: e.dma_start(out=vec1[:, i, :], in_=I[nm].rearrange("o (ph c) -> c (o ph)", c=64),
                                                              allow_slow_non_contiguous=True), w=['vec1'])
            m_lt = A.alloc([64, 64], F32)
            m_le = A.alloc([64, 64], F32)
            m_gt = A.alloc([64, 64], F32)
            blk1 = A.alloc([64, 64], F32)
            sel2 = A.alloc([64, 1], BF16)
            ones_t = A.alloc([64, 64], F32)

            def l1const(e):
                e.memset(ones_t, 1.0)
                e.memset(m_lt, 1.0)
                e.memset(m_le, 1.0)
                e.memset(m_gt, 1.0)
                e.affine_select(out=m_lt, in_=m_lt, pattern=[[1, 64]], compare_op=ALU.is_ge, fill=freg(e, 0.0), base=-1,
                                channel_multiplier=-1)
                e.affine_select(out=m_le, in_=m_le, pattern=[[1, 64]], compare_op=ALU.is_ge, fill=freg(e, 0.0), base=0,
                                channel_multiplier=-1)
                e.affine_select(out=m_gt, in_=m_gt, pattern=[[-1, 64]], compare_op=ALU.is_ge, fill=freg(e, 0.0), base=-1,
                                channel_multiplier=1)
                e.memset(blk1, 1.0)
                return e.memset(sel2, 1.0)
            S.pool(l1const, w=['l1c'])

            ST = A.alloc([64, 16, 64], F32)
            STb = A.alloc([64, 16, 64], BF16)
            fr = A.alloc([64, 16, C], F32)
            fk = A.alloc([64, 16, C], F32)
            fw = A.alloc([64, 16, C], F32)
            fa = A.alloc([64, 16, C], F32)
            cl = A.alloc([64, 16, C], F32)
            L1 = A.alloc([64, 16, C], F32)
            L2 = A.alloc([64, 16, C], F32)
            L3 = A.alloc([64, 16, C], F32)
            Lend = A.alloc([64, 16], F32)
            t_a = A.alloc([64, 16, C], F32)
            t_b = A.alloc([64, 16, C], F32)
            kap = fw
            kmd = A.alloc([64, 16, C], F32)
            kt_b = A.alloc([64, 16, C], BF16)
            bt_b = A.alloc([64, 16, C], BF16)
            ktl_b = A.alloc([64, 16, C], BF16)
            rt_b = A.alloc([64, 16, C], BF16)
            bh_f = A.alloc([64, 16, C], BF16)
            kh_f = A.alloc([64, 16, C], BF16)
            pr_b = A.alloc([64, 16, C], BF16)
            Bh = A.alloc([64, DM], BF16)
            Kh = A.alloc([64, DM], BF16)
            Vf2 = [A.alloc([64, DM], F32) for _ in range(2)]
            Vb = A.alloc([64, DM], BF16)
            Zf2 = [A.alloc([64, DM], F32) for _ in range(2)]
            gN = A.alloc([64, 16, 64], BF16)
            gNT = A.alloc([64, 16, 64], BF16)
            gAk = A.alloc([64, 16, 64], BF16)
            gBb = A.alloc([64, 16, 64], BF16)
            gBk = A.alloc([64, 16, 64], BF16)
            gX = [A.alloc([64, 16, 64], BF16) for _ in range(2)]
            gP = [A.alloc([64, 16, 64], BF16) for _ in range(2)]
            gPT = [A.alloc([64, 16, 64], BF16) for _ in range(2)]
            Rm = A.alloc([64, DM], BF16)
            Ub = A.alloc([64, DM], BF16)
            Yf = A.alloc([64, DM], F32)
            Yc = A.alloc([64, DM], F32)
            st1 = A.alloc([64, 16], F32)
            st2 = A.alloc([64, 16], F32)
            rkb2 = [A.alloc([64, 16], F32) for _ in range(2)]
            gb = A.alloc([64, DM], BF16)
            gT = A.alloc([128, 8, 64], BF16)
            xr2 = [A.alloc([64, DM], F32) for _ in range(2)]
            bst1 = A.alloc([64, 2, 6], F32)
            mv1 = A.alloc([64, 4], F32)
            wko = A.alloc([64, 16, 64], F32)
            stin2 = A.alloc([128, 8, 64], F32)

            print('L1B_TOP', A.top)

            def hp_rows(ap3, h):
                return ap3[:, h, :]

            def chunk(cols, ncol, first, xsrc_rows, yout, yrows, ck, par, post_seq=None):
                Vf, Zf, xr, rkb = Vf2[par], Zf2[par], xr2[par], rkb2[par]
                kVf, kZf, kxr, krkb = ('Vf', par), ('Zf', par), ('xr', par), ('rkb', par)
                pad = ncol < C
                if pad:
                    cols = cols - (C - 1)
                for (tile_, scr, nm) in ((fr, r_scr, 'fr'), (fk, k_scr, 'fk'), (fw, sw_scr, 'fw'), (fa, a_scr, 'fa')):
                    S.dma('sp', lambda e, tile_=tile_, scr=scr: e.dma_start(
                        out=tile_.rearrange("c (pr hp) t -> c pr hp t", hp=2),
                        in_=scr.rearrange("(hp c) pr t -> c pr hp t", hp=2)[:, :, :, cols:cols + C]), r=[('scr_all',)], w=[nm])
                    if pad:
                        S.pool(lambda e, tile_=tile_: e.memset(tile_[:, :, 0:C - 1], 0.0), r=[nm], w=[nm])
                S.dma('sp', lambda e: e.dma_start(out=Vf, in_=v_scr[cols:cols + C, :]), r=[('scr_all',)], w=[kVf])
                S.dma('sp', lambda e: e.dma_start(out=Zf, in_=z_scr[cols:cols + C, :]), r=[('scr_all',)], w=[kZf])
                S.dma('sp', lambda e: e.dma_start(out=xr, in_=x1_scr[cols:cols + C, :]), r=[('scr_all',)], w=[kxr])
                if pad:
                    S.pool(lambda e: e.memset(Vf[0:C - 1, :], 0.0), r=[kVf], w=[kVf])
                    S.pool(lambda e: e.memset(Zf[0:C - 1, :], 0.0), r=[kZf], w=[kZf])
                S.act(lambda e: e.copy(out=Vb, in_=Vf), r=[kVf], w=['Vb'])
                S.dve(lambda e: e.tensor_scalar(out=fw, in0=fw, scalar1=NEG_E05, scalar2=None, op0=ALU.mult), r=['fw'], w=['fw'])

                def scans(e):
                    ins = None
                    for pr_ in range(16):
                        ins = e.tensor_tensor_scan(out=cl[:, pr_, :], data0=ones_t, data1=fw[:, pr_, :], initial=0.0,
                                                   op0=ALU.mult, op1=ALU.add)
                    return ins
                S.dve(scans, r=['fw', 'l1c'], w=['cl'])
                S.act(lambda e: e.activation(out=L1, in_=cl, func=AF.Exp), r=['cl'], w=['L1'])
                S.act(lambda e: e.activation(out=L3, in_=cl, func=AF.Exp, scale=-1.0), r=['cl'], w=['L3'])
                S.dve(lambda e: e.tensor_tensor(out=t_a, in0=cl, in1=fw, op=ALU.subtract), r=['cl', 'fw'], w=['t_a'])
                S.act(lambda e: e.activation(out=L2, in_=t_a, func=AF.Exp), r=['t_a'], w=['L2'])
                S.dve(lambda e: e.tensor_copy(out=Lend, in_=L1[:, :, C - 1]), r=['L1'], w=['Lend'])
                S.dve(lambda e: e.tensor_tensor(out=kap, in0=fk, in1=vec1[:, 2, :].unsqueeze(2).broadcast_to([64, 16, C]),
                                                op=ALU.mult), r=['fk', 'vec1'], w=['fw'])
                S.dve(lambda e: e.tensor_tensor(out=t_b, in0=kap, in1=kap, op=ALU.mult), r=['fw'], w=['t_b'])
                def ssqm(e):
                    e.matmul(PS(0, 0, 64, 0, 512), lhsT=blk1, rhs=t_b[:, 0:8, :].rearrange("p a t -> p (a t)"), start=True, stop=True)
                    return e.matmul(PS(1, 0, 64, 0, 512), lhsT=blk1, rhs=t_b[:, 8:16, :].rearrange("p a t -> p (a t)"),
                                    start=True, stop=True)
                S.pe(ssqm, r=['t_b', 'l1c'], w=[pk(0), pk(1)])
                for hb in range(2):
                    S.dve(lambda e, hb=hb: e.tensor_scalar(out=t_b[:, hb * 8:(hb + 1) * 8, :].rearrange("p a t -> p (a t)"),
                                                           in0=PS(hb, 0, 64, 0, 512), scalar1=1e-24, scalar2=None, op0=ALU.max),
                          r=[pk(hb)], w=['t_b'])
                S.act(lambda e: e.activation(out=t_b, in_=t_b, func=AF.Ln), r=['t_b'], w=['t_b'])
                S.act(lambda e: e.activation(out=t_b, in_=t_b, func=AF.Exp, scale=-0.5), r=['t_b'], w=['t_b'])
                S.dve(lambda e: e.tensor_tensor(out=kap, in0=kap, in1=t_b, op=ALU.mult), r=['fw', 't_b'], w=['fw'])
                S.pool(lambda e: e.tensor_scalar(out=cl, in0=fa, scalar1=-1.0, scalar2=None, op0=ALU.add), r=['fa', 'cl', 'L2'], w=['cl'])
                S.pool(lambda e: e.tensor_tensor(out=cl, in0=cl, in1=vec1[:, 3, :].unsqueeze(2).broadcast_to([64, 16, C]),
                                                 op=ALU.mult), r=['cl', 'vec1'], w=['cl'])
                S.dve(lambda e: e.scalar_tensor_tensor(out=kmd, in0=cl, scalar=1.0, in1=fk, op0=ALU.add, op1=ALU.mult),
                      r=['cl', 'fk'], w=['kmd'])
                S.dve(lambda e: e.tensor_tensor(out=kt_b, in0=kap, in1=L2, op=ALU.mult), r=['fw', 'L2'], w=['kt_b'])
                S.dve(lambda e: e.tensor_tensor(out=t_a, in0=kap, in1=fa, op=ALU.mult), r=['fw', 'fa', 'kmd'], w=['t_a'])
                S.dve(lambda e: e.tensor_tensor(out=t_a, in0=t_a, in1=L3, op=ALU.mult), r=['t_a', 'L3'], w=['t_a'])
                S.act(lambda e: e.copy(out=bt_b, in_=t_a), r=['t_a'], w=['bt_b'])
                S.dve(lambda e: e.tensor_tensor(out=bh_f, in0=t_a, in1=Lend.unsqueeze(2).broadcast_to([64, 16, C]), op=ALU.mult),
                      r=['t_a', 'Lend'], w=['bh_f'])
                S.dve(lambda e: e.tensor_tensor(out=t_b, in0=kmd, in1=L3, op=ALU.mult), r=['kmd', 'L3', 'fw'], w=['t_b'])
                S.act(lambda e: e.copy(out=ktl_b, in_=t_b), r=['t_b'], w=['ktl_b'])
                S.dve(lambda e: e.tensor_tensor(out=kh_f, in0=t_b, in1=Lend.unsqueeze(2).broadcast_to([64, 16, C]), op=ALU.mult),
                      r=['t_b', 'Lend'], w=['kh_f'])
                S.pool(lambda e: e.tensor_tensor(out=rt_b, in0=fr, in1=L1, op=ALU.mult), r=['fr', 'L1'], w=['rt_b'])
                S.pool(lambda e: e.tensor_tensor(out=L2, in0=fr, in1=kmd, op=ALU.mult), r=['fr', 'kmd', 'L2', 'kt_b'], w=['L2'])
                S.pool(lambda e: e.tensor_tensor(out=pr_b, in0=L2, in1=vec1[:, 4, :].unsqueeze(2).broadcast_to([64, 16, C]),
                                                 op=ALU.mult), r=['L2', 'vec1'], w=['pr_b'])
                for (src, dst_, nm, bank) in ((bh_f, Bh, 'Bh', 2), (kh_f, Kh, 'Kh', 3)):
                    def trb(e, src=src, bank=bank):
                        ins = None
                        for h in range(16):
                            ins = e.transpose(out=PSB(bank, 0, 64, h * 64, (h + 1) * 64), in_=src[:, h, :],
                                              identity=ident_b[0:64, 0:64])
                        return ins
                    S.pe(trb, r=[nm.lower() + '_f' if False else ('bh_f' if nm == 'Bh' else 'kh_f'), 'ident_b'], w=[pk(bank)])
                    S.act(lambda e, dst_=dst_, bank=bank: e.copy(out=dst_, in_=PSB(bank, 0, 64, 0, 1024)), r=[pk(bank)], w=[nm])

                def rkm(e):
                    ins = None
                    for h in range(16):
                        ins = e.matmul(PS(1, 0, 64, 2 * h, 2 * h + 1), lhsT=pr_b[:, h, :], rhs=sel2, start=True, stop=True)
                    return ins
                S.pe(rkm, r=['pr_b', 'l1c'], w=[pk(1)])
                S.act(lambda e: e.copy(out=rkb, in_=PS(1, 0, 64, 0, 32).rearrange("p (h two) -> p h two", two=2)[:, :, 0]),
                      r=[pk(1)], w=[krkb])
                def gram(lt, rt, dst_, mask, banks, nm, deps):
                    def gm(e):
                        ins = None
                        for h in range(16):
                            ins = e.matmul(PS(banks[h // 8], 0, 64, (h % 8) * 64, (h % 8 + 1) * 64), lhsT=hp_rows(lt, h),
                                           rhs=hp_rows(rt, h), start=True, stop=True)
                        return ins
                    S.pe(gm, r=deps, w=[pk(banks[0]), pk(banks[1])])
                    for hb in range(2):
                        S.dve(lambda e, hb=hb: e.tensor_tensor(
                            out=dst_[:, hb * 8:(hb + 1) * 8, :], in0=PS(banks[hb], 0, 64, 0, 512).rearrange("p (h t) -> p h t", h=8),
                            in1=mask.unsqueeze(1).broadcast_to([64, 8, 64]), op=ALU.mult),
                            r=[pk(banks[hb]), 'l1c'], w=[(nm, hb)])
                gram(bt_b, kt_b, gN, m_lt, (4, 5), 'gN', ['bt_b', 'kt_b'])
                gram(kt_b, bt_b, gNT, m_gt, (6, 7), 'gNT', ['bt_b', 'kt_b'])
                gram(ktl_b, kt_b, gAk, m_lt, (4, 5), 'gAk', ['ktl_b', 'kt_b'])
                gram(bt_b, rt_b, gBb, m_le, (6, 7), 'gBb', ['bt_b', 'rt_b'])
                gram(ktl_b, rt_b, gBk, m_le, (4, 5), 'gBk', ['ktl_b', 'rt_b'])
                for hb in range(2):
                    S.dve(lambda e, hb=hb: e.scalar_tensor_tensor(
                        out=gX[0][:, hb * 8:(hb + 1) * 8, :], in0=gN[:, hb * 8:(hb + 1) * 8, :], scalar=-1.0,
                        in1=ident_b[0:64, 0:64].unsqueeze(1).broadcast_to([64, 8, 64]), op0=ALU.mult, op1=ALU.add),
                        r=[('gN', hb), 'ident_b'], w=[('gX', 0, hb)])
                Pc, PTc, Pk, PTk = gN, gNT, 'gN', 'gNT'
                xi = 0
                for lev in range(0 if pad else 5):
                    Pn, PTn = gP[lev % 2], gPT[lev % 2]
                    Pnk, PTnk = ('gP', lev % 2), ('gPT', lev % 2)

                    def sq(e, Pc=Pc, PTc=PTc):
                        ins = None
                        for h in range(16):
                            ins = e.matmul(PS(h // 8, 0, 64, (h % 8) * 64, (h % 8 + 1) * 64), lhsT=PTc[:, h, :], rhs=Pc[:, h, :],
                                           start=True, stop=True)
                        for h in range(16):
                            ins = e.matmul(PS(2 + h // 8, 0, 64, (h % 8) * 64, (h % 8 + 1) * 64), lhsT=Pc[:, h, :],
                                           rhs=PTc[:, h, :], start=True, stop=True)
                        return ins
                    S.pe(sq, r=[(Pk, 0), (Pk, 1), (PTk, 0), (PTk, 1)] if isinstance(Pk, str) else
                         [Pk + (0,), Pk + (1,), PTk + (0,), PTk + (1,)], w=[pk(0), pk(1), pk(2), pk(3)])
                    for hb in range(2):
                        S.act(lambda e, hb=hb, Pn=Pn: e.copy(out=Pn[:, hb * 8:(hb + 1) * 8, :].rearrange("p h t -> p (h t)"),
                                                             in_=PS(hb, 0, 64, 0, 512)), r=[pk(hb)], w=[Pnk + (hb,)])
                        S.dve(lambda e, hb=hb, PTn=PTn: e.tensor_copy(out=PTn[:, hb * 8:(hb + 1) * 8, :].rearrange("p h t -> p (h t)"),
                                                                      in_=PS(2 + hb, 0, 64, 0, 512)), r=[pk(2 + hb)], w=[PTnk + (hb,)])
                    Xo, Xn = gX[xi % 2], gX[(xi + 1) % 2]

                    def xm(e, PTn=PTn, Xo=Xo):
                        ins = None
                        for h in range(16):
                            ins = e.matmul(PS(4 + h // 8, 0, 64, (h % 8) * 64, (h % 8 + 1) * 64), lhsT=PTn[:, h, :], rhs=Xo[:, h, :],
                                           start=True, stop=True)
                        return ins
                    S.pe(xm, r=[PTnk + (0,), PTnk + (1,), ('gX', xi % 2, 0), ('gX', xi % 2, 1)], w=[pk(4), pk(5)])
                    for hb in range(2):
                        S.dve(lambda e, hb=hb, Xo=Xo, Xn=Xn: e.tensor_tensor(
                            out=Xn[:, hb * 8:(hb + 1) * 8, :].rearrange("p h t -> p (h t)"), in0=PS(4 + hb, 0, 64, 0, 512),
                            in1=Xo[:, hb * 8:(hb + 1) * 8, :].rearrange("p h t -> p (h t)"), op=ALU.add),
                            r=[pk(4 + hb), ('gX', xi % 2, hb)], w=[('gX', (xi + 1) % 2, hb)])
                    xi += 1
                    Pc, PTc, Pk, PTk = Pn, PTn, Pnk, PTnk
                Xf = gX[xi % 2]
                XFK = [('gX', xi % 2, 0), ('gX', xi % 2, 1)]
                yield 'pre'
                if first is not None:
                    first()

                def m1(e):
                    ins = None
                    for h in range(16):
                        o_ = PS(h // 8, 0, 64, (h % 8) * 64, (h % 8 + 1) * 64)
                        e.matmul(o_, lhsT=hp_rows(kt_b, h), rhs=hp_rows(STb, h), start=True, stop=False)
                        ins = e.matmul(o_, lhsT=gAk[:, h, :], rhs=Vb[:, h * 64:(h + 1) * 64], start=False, stop=True)
                    return ins
                S.pe(m1, r=['kt_b', 'STb', ('gAk', 0), ('gAk', 1), 'Vb'], w=[pk(0), pk(1)])
                for hb in range(2):
                    S.act(lambda e, hb=hb: e.activation(out=Rm[:, hb * 512:(hb + 1) * 512], in_=PS(hb, 0, 64, 0, 512),
                                                        func=AF.Copy, scale=-1.0), r=[pk(hb)], w=[('Rm', hb)])

                def m3(e):
                    ins = None
                    for h in range(16):
                        ins = e.matmul(PS(2 + h // 8, 0, 64, (h % 8) * 64, (h % 8 + 1) * 64), lhsT=Xf[:, h, :],
                                       rhs=Rm[:, h * 64:(h + 1) * 64], start=True, stop=True)
                    return ins
                S.pe(m3, r=XFK + [('Rm', 0), ('Rm', 1)], w=[pk(2), pk(3)])
                for hb in range(2):
                    S.act(lambda e, hb=hb: e.copy(out=Ub[:, hb * 512:(hb + 1) * 512], in_=PS(2 + hb, 0, 64, 0, 512)),
                          r=[pk(2 + hb)], w=[('Ub', hb)])

                def m4(e):
                    ins = None
                    for h in range(16):
                        o_ = PS(4 + h // 8, 0, 64, (h % 8) * 64, (h % 8 + 1) * 64)
                        e.matmul(o_, lhsT=hp_rows(rt_b, h), rhs=hp_rows(STb, h), start=True, stop=False)
                        e.matmul(o_, lhsT=gBb[:, h, :], rhs=Ub[:, h * 64:(h + 1) * 64], start=False, stop=False)
                        ins = e.matmul(o_, lhsT=gBk[:, h, :], rhs=Vb[:, h * 64:(h + 1) * 64], start=False, stop=True)
                    return ins
                S.pe(m4, r=['rt_b', 'STb', ('gBb', 0), ('gBb', 1), ('gBk', 0), ('gBk', 1), ('Ub', 0), ('Ub', 1), 'Vb'],
                     w=[pk(4), pk(5)])
                for hb in range(2):
                    S.act(lambda e, hb=hb: e.copy(out=Yf[:, hb * 512:(hb + 1) * 512], in_=PS(4 + hb, 0, 64, 0, 512)),
                          r=[pk(4 + hb)], w=[('Yf', hb)])

                def m5(e):
                    ins = None
                    for h in range(16):
                        o_ = PS(6 + h // 8, 0, 64, (h % 8) * 64, (h % 8 + 1) * 64)
                        e.matmul(o_, lhsT=Bh[:, h * 64:(h + 1) * 64], rhs=Ub[:, h * 64:(h + 1) * 64], start=True, stop=False)
                        ins = e.matmul(o_, lhsT=Kh[:, h * 64:(h + 1) * 64], rhs=Vb[:, h * 64:(h + 1) * 64], start=False, stop=True)
                    return ins
                S.pe(m5, r=['Bh', 'Kh', ('Ub', 0), ('Ub', 1), 'Vb'], w=[pk(6), pk(7)])
                S.dve(lambda e: e.tensor_tensor(out=ST, in0=ST, in1=Lend.unsqueeze(2).broadcast_to([64, 16, 64]), op=ALU.mult),
                      r=['ST', 'Lend', 'STb'], w=['ST'])
                for hb in range(2):
                    S.dve(lambda e, hb=hb: e.tensor_tensor(out=ST[:, hb * 8:(hb + 1) * 8, :], in0=ST[:, hb * 8:(hb + 1) * 8, :],
                                                           in1=PS(6 + hb, 0, 64, 0, 512).rearrange("p (a v) -> p a v", a=8), op=ALU.add),
                          r=['ST', pk(6 + hb)], w=['ST'])
                S.act(lambda e: e.copy(out=STb, in_=ST), r=['ST'], w=['STb'])
                if post_seq is not None:
                    post_seq()
                yield 'seq'
                Y3 = Yf.rearrange("p (h v) -> p h v", h=16)
                Yc3 = Yc.rearrange("p (h v) -> p h v", h=16)
                YK = [('Yf', 0), ('Yf', 1)]
                S.dve(lambda e: e.tensor_reduce(out=st1, in_=Y3, axis=AX.X, op=ALU.add), r=YK, w=['st1'])
                S.dve(lambda e: e.tensor_scalar(out=st1, in0=st1, scalar1=1.0 / 64.0, scalar2=None, op0=ALU.mult), r=['st1'], w=['st1'])
                S.dve(lambda e: e.tensor_tensor(out=Yc3, in0=Y3, in1=st1.unsqueeze(2).broadcast_to([64, 16, 64]), op=ALU.subtract),
                      r=YK + ['st1'], w=['Yc'])
                S.pool(lambda e: e.tensor_tensor(out=Yf, in0=Yc, in1=Yc, op=ALU.mult), r=['Yc'] + YK, w=YK)
                S.dve(lambda e: e.tensor_reduce(out=st2, in_=Y3, axis=AX.X, op=ALU.add), r=YK, w=['st2'])
                S.dve(lambda e: e.tensor_scalar(out=st2, in0=st2, scalar1=1.0 / 64.0, scalar2=64e-5, op0=ALU.mult, op1=ALU.add),
                      r=['st2'], w=['st2'])
                S.act(lambda e: e.activation(out=st2, in_=st2, func=AF.Ln), r=['st2'], w=['st2'])
                S.act(lambda e: e.activation(out=st2, in_=st2, func=AF.Exp, scale=-0.5), r=['st2'], w=['st2'])
                S.dve(lambda e: e.tensor_tensor(out=Yc3, in0=Yc3, in1=st2.unsqueeze(2).broadcast_to([64, 16, 64]), op=ALU.mult),
                      r=['Yc', 'st2'], w=['Yc'])
                S.dve(lambda e: e.tensor_tensor(out=Yc, in0=Yc, in1=gnp[:, 0, :], op=ALU.mult), r=['Yc', ('gnp', 0)], w=['Yc'])
                S.pool(lambda e: e.tensor_tensor(out=Yc, in0=Yc, in1=gnp[:, 1, :], op=ALU.add), r=['Yc', ('gnp', 1)], w=['Yc'])
                S.dve(lambda e: e.tensor_tensor(out=Y3, in0=Vf.rearrange("p (h v) -> p h v", h=16),
                                                in1=rkb.unsqueeze(2).broadcast_to([64, 16, 64]), op=ALU.mult),
                      r=[kVf, krkb] + YK, w=YK)
                S.pool(lambda e: e.tensor_tensor(out=Yc, in0=Yc, in1=Yf, op=ALU.add), r=['Yc'] + YK, w=['Yc'])
                S.dve(lambda e: e.tensor_tensor(out=gb, in0=Yc, in1=Zf, op=ALU.mult), r=['Yc', kZf], w=['gb'])
                yield 'tailA'
                def trg(e):
                    ins = None
                    for c in range(8):
                        ins = e.transpose(out=PSB(7, 0, 128, c * 64, (c + 1) * 64), in_=gb[:, c * 128:(c + 1) * 128],
                                          identity=ident_b[0:64, 0:64])
                    return ins
                S.pe(trg, r=['gb', 'ident_b'], w=[pk(7)])
                S.act(lambda e: e.copy(out=gT.rearrange("p c t -> p (c t)"), in_=PSB(7, 0, 128, 0, 512)), r=[pk(7)], w=['gT'])

                def mo(e):
                    ins = None
                    for cb in range(2):
                        for c in range(8):
                            ins = e.matmul(PS(cb, 0, 64, 0, 512), lhsT=gT[:, c, :], rhs=wo_o[:, c, cb * 512:(cb + 1) * 512],
                                           start=(c == 0), stop=(c == 7))
                    return ins
                S.pe(mo, r=['gT'] + WOO, w=[pk(0), pk(1)])
                for cb in range(2):
                    S.dve(lambda e, cb=cb: e.scalar_tensor_tensor(
                        out=xr[:, cb * 512:(cb + 1) * 512], in0=xr[:, cb * 512:(cb + 1) * 512], scalar=ALPHA,
                        in1=PS(cb, 0, 64, 0, 512), op0=ALU.mult, op1=ALU.add), r=[kxr, pk(cb)], w=[kxr])
                    S.dve(lambda e, cb=cb: e.bn_stats(out=bst1[:, cb, :], in_=xr[:, cb * 512:(cb + 1) * 512]), r=[kxr], w=[('bst1', cb)])
                S.dve(lambda e: e.bn_aggr(out=mv1[:, 0:2], in_=bst1.rearrange("p a b -> p (a b)")), r=[('bst1', 0), ('bst1', 1)],
                      w=['mv1'])
                S.dve(lambda e: e.tensor_scalar(out=mv1[:, 2:3], in0=mv1[:, 1:2], scalar1=1e-5, scalar2=None, op0=ALU.add),
                      r=['mv1'], w=['mv1b'])
                S.act(lambda e: e.activation(out=mv1[:, 2:3], in_=mv1[:, 2:3], func=AF.Ln), r=['mv1b'], w=['mv1b'])
                S.act(lambda e: e.activation(out=mv1[:, 2:3], in_=mv1[:, 2:3], func=AF.Exp, scale=-0.5), r=['mv1b'], w=['mv1b'])
                S.dve(lambda e: e.tensor_scalar(out=xr, in0=xr, scalar1=mv1[:, 0:1], scalar2=mv1[:, 2:3], op0=ALU.subtract,
                                                op1=ALU.mult), r=[kxr, 'mv1', 'mv1b'], w=[kxr])
                S.dve(lambda e: e.tensor_tensor(out=xr, in0=xr, in1=gnp[:, 2, :], op=ALU.mult), r=[kxr, ('gnp', 2)], w=[kxr])
                S.pool(lambda e: e.tensor_tensor(out=xr, in0=xr, in1=gnp[:, 3, :], op=ALU.add), r=[kxr, ('gnp', 3)], w=[kxr])
                S.dma('sp', lambda e: e.dma_start(out=yout, in_=(xr[C - 1:C, :] if pad else xr)), r=[kxr])

            def write_state(dst):
                def trs(e):
                    ins = None
                    for h in range(16):
                        ins = e.transpose(out=PS(6 + h // 8, 0, 64, (h % 8) * 64, (h % 8 + 1) * 64), in_=ST[:, h, :],
                                          identity=ident_f[0:64, 0:64])
                    return ins
                S.pe(trs, r=['ST', 'ident_f'], w=[pk(6), pk(7)])
                for hb in range(2):
                    S.act(lambda e, hb=hb: e.copy(out=wko[:, hb * 8:(hb + 1) * 8, :].rearrange("p a b -> p (a b)"),
                                                  in_=PS(6 + hb, 0, 64, 0, 512)), r=[pk(6 + hb)], w=['wko'])
                S.dma('sp', lambda e: e.dma_start(out=dst.rearrange("(h v) k -> v h k", h=16), in_=wko), r=['wko'])

            def zero_state():
                S.pool(lambda e: e.memset(ST, 0.0), w=['ST'])
                S.pool(lambda e: e.memset(STb, 0.0), w=['STb'])

            def mk_load_state(s):
                def load_state():
                    S.dma('sp', lambda e: e.dma_start(
                        out=stin2, in_=I['state_wkv'][s * 1024:(s + 1) * 1024, :].rearrange("(pr p) k -> p pr k", pr=8)), w=['stin'])

                    def tri(e):
                        ins = None
                        for pr_ in range(8):
                            ins = e.transpose(out=PS(6 + pr_ // 4, 0, 64, (pr_ % 4) * 128, (pr_ % 4 + 1) * 128), in_=stin2[:, pr_, :],
                                              identity=ident_f)
                        return ins
                    S.pe(tri, r=['stin', 'ident_f'], w=[pk(6), pk(7)])
                    for hb in range(2):
                        S.dve(lambda e, hb=hb: e.tensor_copy(out=ST[:, hb * 8:(hb + 1) * 8, :].rearrange("p a v -> p (a v)"),
                                                             in_=PS(6 + hb, 0, 64, 0, 512)), r=[pk(6 + hb)], w=['ST'])
                    S.act(lambda e: e.copy(out=STb, in_=ST), r=['ST'], w=['STb'])
                return load_state

            gens = []
            nchk = T // C
            for ci in range(nchk):
                gens.append(chunk(ci * C, C, zero_state if ci == 0 else None, x1_scr[ci * C:(ci + 1) * C, :],
                                  O['y_prompt'][ci * C:(ci + 1) * C, :], C, ci, ci % 2,
                                  (lambda: write_state(O['wkv_p'])) if ci == nchk - 1 else None))
            for s in range(NS):
                gens.append(chunk(T + s, 1, mk_load_state(s), x1_scr[T + s:T + s + 1, :], O['y_sample'][s:s + 1, :], 1, 100 + s,
                                  (nchk + s) % 2, (lambda s=s: write_state(O['wkv_s'][s * 1024:(s + 1) * 1024, :]))))
            next(gens[0])
            next(gens[0])
            for i_ in range(1, len(gens)):
                next(gens[i_])
                next(gens[i_ - 1])
                next(gens[i_])
                for _ in gens[i_ - 1]:
                    pass
            next(gens[-1])
            for _ in gens[-1]:
                pass
            S.barrier()
            A.release()
            stage_end(13)
        except _Stop:
            pass
        S.analyze()
        S.emit(nc, sems, dsems)
        print("arena peak bytes", A.peak, "ops", len(S.ops))
    return nc


_NC_CACHE = {}


def kernel(**inputs):
    inp = {k: np.asarray(v) for k, v in inputs.items()}
    if 'nc' not in _NC_CACHE:
        _NC_CACHE['nc'] = build()
    nc = _NC_CACHE['nc']
    f = np.ascontiguousarray
    shared = {
        "cache_cmp_k": f(inp['cache_cmp_kv'][0][:, :, 0].reshape(NPOOL * 128, 128)),
        "cache_cmp_v": f(inp['cache_cmp_kv'][0][:, :, 1].reshape(NPOOL * 128, 128)),
        "cache_sel_k": f(inp['cache_sel_kv'][0][:, :, 0].reshape(NPOOL * 128, 128)),
        "cache_sel_v": f(inp['cache_sel_kv'][0][:, :, 1].reshape(NPOOL * 128, 128)),
        "w_in": f(inp['w_in_even'][0]),
        "conv_w": f(inp['conv_w'][0].reshape(31, 512)),
        "conv_b": f(inp['conv_b'].reshape(1, 512)),
        "conv_ln_g": f(inp['conv_ln_g'].reshape(1, 512)),
        "conv_ln_b": f(inp['conv_ln_b'].reshape(1, 512)),
        "wk_cmp": f(inp['wk_cmp'][0]),
        "wv_cmp": f(inp['wv_cmp'][0]),
        "w_out_even": f(inp['w_out_even'][0]),
        "mu_c": f(inp['mu_c'][0]),
        "w_rkvz": f(inp['w_rkvz'][0].reshape(4 * DM, DM)),
        "w0": f(inp['w0'].reshape(1, DM)),
        "w1": f(inp['w1'][0]),
        "w2": f(inp['w2'][0]),
        "a0": f(inp['a0'].reshape(1, DM)),
        "a1": f(inp['a1'][0]),
        "a2": f(inp['a2'][0]),
        "k_k": f(inp['k_k'].reshape(1, DM)),
        "k_a": f(inp['k_a'].reshape(1, DM)),
        "r_k": f(inp['r_k'].reshape(1, DM)),
        "gn_g": f(inp['gn_g'].reshape(1, DM)),
        "gn_b": f(inp['gn_b'].reshape(1, DM)),
        "w_out_odd": f(inp['w_out_odd'][0]),
        "ln_g": f(inp['ln_g']),
        "ln_b": f(inp['ln_b']),
    }
    in_maps = []
    for c in range(NCORES):
        s0, s1 = c * NS, (c + 1) * NS
        m = dict(shared)
        m["xp"] = f(inp['x_prompt'][c])
        m["xs"] = f(inp['x_sample'][s0:s1, 0, :])
        m["cache_win"] = f(inp['cache_win_kv'][0, s0:s1].reshape(NS * 512, 256))
        m["state_conv"] = f(inp['state_conv'][0, s0:s1].reshape(NS * 30, 512))
        m["state_wkv"] = f(inp['state_wkv'][0, s0:s1].reshape(NS * 16 * 64, 64))
        m["state_shift"] = f(inp['state_shift'][0, s0:s1])
        m["page_table"] = f(inp['page_table'][s0:s1].astype(np.int32))
        in_maps.append(m)
    res = run_bass_kernel_spmd(nc, in_maps, core_ids=list(range(NCORES)), trace=bool(os.environ.get('KDEV_TRACE')))
    if os.environ.get('KDEV_TRACE'):
        print('EXEC_TIME_NS', res.exec_time_ns)
    R = res.results

    def cat(name):
        return np.stack([np.asarray(R[c][name]) for c in range(NCORES)], axis=0)
    y_prompt = cat("y_prompt").reshape(8, T, DM)
    y_sample = cat("y_sample").reshape(32, 1, DM)
    cmp_p = cat("cmp_p").reshape(1, 8, T, 2, 2, 64)
    cmp_s = cat("cmp_s").reshape(1, 32, 1, 2, 2, 64)
    sel_p = cat("sel_p").reshape(1, 8, T, 2, 2, 64)
    sel_s = cat("sel_s").reshape(1, 32, 1, 2, 2, 64)
    win_p = cat("win_p").reshape(1, 8, 512, 2, 2, 64)
    win_s = cat("win_s").reshape(1, 32, 512, 2, 2, 64)
    conv_p = cat("conv_p").reshape(1, 8, 30, 512)
    conv_s = cat("conv_s").reshape(1, 32, 30, 512)
    wkv_p = cat("wkv_p").reshape(1, 8, 16, 64, 64)
    wkv_s = cat("wkv_s").reshape(1, 32, 16, 64, 64)
    shift_p = cat("shift_p").reshape(1, 8, DM)
    shift_s = cat("shift_s").reshape(1, 32, DM)
    return (y_prompt, y_sample, cmp_p, cmp_s, sel_p, sel_s, win_p, win_s, conv_p, conv_s,
            wkv_p, wkv_s, shift_p, shift_s)
```

```python
import numpy as np
import concourse.bass as bass
import concourse.mybir as mybir
from concourse.bass_utils import run_bass_kernel_spmd

F32 = mybir.dt.float32
BF16 = mybir.dt.bfloat16
I32 = mybir.dt.int32
AF = mybir.ActivationFunctionType
ALU = mybir.AluOpType
AX = mybir.AxisListType

NCORES = 8
T = 2048
NS = 4
TT = T + NS
DM = 1024
import os
NPOOL = int(os.environ.get('KDEV_NPOOL', '2560'))
STAGE = int(os.environ.get('KDEV_STAGE', '99'))


class _Stop(Exception):
    pass


def stage_end(k):
    if STAGE == k:
        raise _Stop()
ENGS = ('pe', 'act', 'dve', 'pool', 'sp')
SAME_ENG_DIST = 3
NDS = {'sp': 8, 'act': 4, 'pool': 6}


class Op:
    __slots__ = ('eng', 'fn', 'r', 'w', 'dma', 'deps', 'signal', 'count', 'dsem', 'dcount',
                 'dprev', 'waits', 'idx', 'eidx', 'bar', 'need')


class Sched:
    def __init__(self):
        self.ops = []
        self.nbar = 0

    def add(self, eng, fn, r=(), w=(), dma=False):
        op = Op()
        op.eng, op.fn, op.r, op.w, op.dma = eng, fn, tuple(r), tuple(w), dma
        op.signal = False
        op.bar = None
        op.count = 0
        self.ops.append(op)
        return op

    def pe(self, fn, r=(), w=()):
        return self.add('pe', fn, r, w)

    def act(self, fn, r=(), w=()):
        return self.add('act', fn, r, w)

    def dve(self, fn, r=(), w=()):
        return self.add('dve', fn, r, w)

    def pool(self, fn, r=(), w=()):
        return self.add('pool', fn, r, w)

    def dma(self, q, fn, r=(), w=()):
        return self.add(q, fn, r, w, dma=True)

    def barrier(self):
        self.nbar += 1
        for e in ENGS:
            op = self.add(e, None)
            op.bar = self.nbar

    def analyze(self):
        ops = self.ops
        last_w = {}
        rd = {}
        eng_ops = {e: [] for e in ENGS}
        last_on_eng = {}
        all_dmas = []
        cur_bar = None
        snap = None
        for i, op in enumerate(ops):
            op.idx = i
            deps = set()
            if op.bar is not None:
                if cur_bar != op.bar:
                    cur_bar = op.bar
                    snap = (dict(last_on_eng), list(all_dmas))
                    all_dmas = []
                for f, j in snap[0].items():
                    if f != op.eng and ops[j].bar is None:
                        deps.add(j)
                deps.update(snap[1])
            else:
                for k in op.r:
                    j = last_w.get(k)
                    if j is not None:
                        deps.add(j)
                for k in op.w:
                    j = last_w.get(k)
                    if j is not None:
                        deps.add(j)
                    rr = rd.get(k)
                    if rr:
                        deps.update(rr[0].values())
                        deps.update(rr[1])
                for k in op.r:
                    rr = rd.setdefault(k, ({}, []))
                    if op.dma:
                        rr[1].append(i)
                    else:
                        rr[0][op.eng] = i
                for k in op.w:
                    last_w[k] = i
                    rd[k] = ({}, [])
            deps.discard(i)
            op.deps = deps
            op.eidx = len(eng_ops[op.eng])
            eng_ops[op.eng].append(op)
            last_on_eng[op.eng] = i
            if op.dma:
                all_dmas.append(i)
        for op in ops:
            need = []
            for j in op.deps:
                d = ops[j]
                if d.dma:
                    need.append(j)
                elif d.eng == op.eng:
                    if op.eng == 'pe' and not op.dma:
                        continue
                    if op.eidx - d.eidx > SAME_ENG_DIST:
                        continue
                    need.append(j)
                else:
                    need.append(j)
            op.need = need
            for j in need:
                if not ops[j].dma:
                    ops[j].signal = True
        self.eng_ops = eng_ops

    def emit(self, nc, sems, dsems):
        ops = self.ops
        eng_ops = self.eng_ops
        for e in ENGS:
            cnt = 0
            for op in eng_ops[e]:
                if op.signal:
                    cnt += 1
                    op.count = cnt
        print('SEMCOUNTS', {e: max([op.count for op in eng_ops[e]] + [0]) for e in ENGS}, {e: len(eng_ops[e]) for e in ENGS})
        finals = {}
        for q, pool in dsems.items():
            uses = [0] * len(pool)
            k = 0
            for op in eng_ops[q]:
                if op.dma:
                    s = k % len(pool)
                    k += 1
                    op.dsem = pool[s]
                    op.dprev = 16 * uses[s]
                    uses[s] += 1
                    op.dcount = 16 * uses[s]
                    finals[id(pool[s])] = (pool[s], op.dcount)
        for e in ENGS:
            waited = {}
            for op in eng_ops[e]:
                ws = []
                for j in sorted(op.need):
                    d = ops[j]
                    if d.dma:
                        sem, val = d.dsem, d.dcount
                    else:
                        sem, val = sems[d.eng], d.count
                    if waited.get(id(sem), 0) >= val:
                        continue
                    waited[id(sem)] = val
                    ws.append((sem, val))
                if op.dma and op.dprev > 0:
                    if waited.get(id(op.dsem), 0) < op.dprev:
                        waited[id(op.dsem)] = op.dprev
                        ws.append((op.dsem, op.dprev))
                op.waits = ws

        def make(e):
            def body(eo):
                for op in eng_ops[e]:
                    for (sem, v) in op.waits:
                        eo.wait_ge(sem, v)
                    if op.fn is not None:
                        ins = op.fn(eo)
                        if op.dma:
                            ins.then_inc(op.dsem, 16)
                        elif op.signal:
                            ins.then_inc(sems[e], 1)
                if e == 'sp':
                    for (sem, v) in finals.values():
                        eo.wait_ge(sem, v)
            return body

        with nc.Block() as block:
            block.tensor(make('pe'))
            block.scalar(make('act'))
            block.vector(make('dve'))
            block.gpsimd(make('pool'))
            block.sync(make('sp'))


class Arena:
    def __init__(self, t32, nbytes):
        self.t32 = t32
        self.tbf = t32.bitcast(BF16)
        self.ti32 = t32.bitcast(I32)
        self.nbytes = nbytes
        self.top = 0
        self.marks = []
        self.peak = 0

    def alloc(self, shape, dt):
        es = 2 if dt == BF16 else 4
        n = 1
        for s in shape[1:]:
            n *= s
        nb = (n * es + 63) // 64 * 64
        off = self.top
        self.top += nb
        self.peak = max(self.peak, self.top)
        assert self.top <= self.nbytes, ("SBUF arena overflow", self.top, self.nbytes)
        base = {BF16: self.tbf, F32: self.t32, I32: self.ti32}[dt]
        v = base[0:shape[0], off // es: off // es + n]
        if len(shape) > 2:
            names = "abcdefg"[:len(shape) - 1]
            pat = "p (%s) -> p %s" % (" ".join(names), " ".join(names))
            v = v.rearrange(pat, **{names[i]: shape[1 + i] for i in range(len(shape) - 2)})
        return v

    def mark(self):
        self.marks.append(self.top)

    def release(self):
        self.top = self.marks.pop()


OUT_SPECS = [
    ("y_prompt", [T, DM]),
    ("y_sample", [NS, DM]),
    ("cmp_p", [T, 256]),
    ("cmp_s", [NS, 256]),
    ("sel_p", [T, 256]),
    ("sel_s", [NS, 256]),
    ("win_p", [512, 256]),
    ("win_s", [NS * 512, 256]),
    ("conv_p", [30, 512]),
    ("conv_s", [NS * 30, 512]),
    ("wkv_p", [16 * 64, 64]),
    ("wkv_s", [NS * 16 * 64, 64]),
    ("shift_p", [1, DM]),
    ("shift_s", [NS, DM]),
]

IN_SPECS = [
    ("xp", [T, DM], F32),
    ("xs", [NS, DM], F32),
    ("cache_cmp_k", [NPOOL * 128, 128], F32),
    ("cache_cmp_v", [NPOOL * 128, 128], F32),
    ("cache_sel_k", [NPOOL * 128, 128], F32),
    ("cache_sel_v", [NPOOL * 128, 128], F32),
    ("cache_win", [NS * 512, 256], F32),
    ("state_conv", [NS * 30, 512], F32),
    ("state_wkv", [NS * 16 * 64, 64], F32),
    ("state_shift", [NS, DM], F32),
    ("page_table", [NS, 64], I32),
    ("w_in", [DM, 3352], F32),
    ("conv_w", [31, 512], F32),
    ("conv_b", [1, 512], F32),
    ("conv_ln_g", [1, 512], F32),
    ("conv_ln_b", [1, 512], F32),
    ("wk_cmp", [32, 2], F32),
    ("wv_cmp", [32, 2], F32),
    ("w_out_even", [DM, DM], F32),
    ("mu_c", [6, DM], F32),
    ("w_rkvz", [4 * DM, DM], F32),
    ("w0", [1, DM], F32),
    ("w1", [DM, 64], F32),
    ("w2", [64, DM], F32),
    ("a0", [1, DM], F32),
    ("a1", [DM, 64], F32),
    ("a2", [64, DM], F32),
    ("k_k", [1, DM], F32),
    ("k_a", [1, DM], F32),
    ("r_k", [1, DM], F32),
    ("gn_g", [1, DM], F32),
    ("gn_b", [1, DM], F32),
    ("w_out_odd", [DM, DM], F32),
    ("ln_g", [2, DM], F32),
    ("ln_b", [2, DM], F32),
]

C_AVAL, C_AGLU, C_ZA, C_Q, C_KV, C_G3, C_ZB = 0, 512, 1024, 1536, 2048, 2816, 2840


def build(stage=99):
    nc = bass.Bass("TRN2", target_bir_lowering=False)
    I = {}
    for name, shape, dt in IN_SPECS:
        I[name] = nc.dram_tensor(name, shape, dt, kind="ExternalInput").ap()
    O = {}
    for name, shape in OUT_SPECS:
        O[name] = nc.dram_tensor(name, shape, F32, kind="ExternalOutput").ap()

    x1_scr = nc.dram_tensor("x1_scr", [TT, DM], F32, kind="Internal").ap()
    r_scr = nc.dram_tensor("r_scr", [128, 8, TT], F32, kind="Internal").ap()
    k_scr = nc.dram_tensor("k_scr", [128, 8, TT], F32, kind="Internal").ap()
    sw_scr = nc.dram_tensor("sw_scr", [128, 8, TT], F32, kind="Internal").ap()
    a_scr = nc.dram_tensor("a_scr", [128, 8, TT], F32, kind="Internal").ap()
    v_scr = nc.dram_tensor("v_scr", [TT, DM], F32, kind="Internal").ap()
    z_scr = nc.dram_tensor("z_scr", [TT, DM], F32, kind="Internal").ap()
    S = Sched()
    ARENA_BYTES = 176 * 1024
    from contextlib import ExitStack
    with ExitStack() as st:
        arena_t = st.enter_context(nc.sbuf_tensor("arena", [128, ARENA_BYTES // 4], F32))
        ps_t = st.enter_context(nc.psum_tensor("psum", [128, 4096], F32))
        sems = {e: st.enter_context(nc.semaphore("sem_" + e)) for e in ENGS}
        dsems = {q: [st.enter_context(nc.semaphore("dsem_%s%d" % (q, i))) for i in range(n)]
                 for q, n in NDS.items()}
        A = Arena(arena_t, ARENA_BYTES)
        ps_bf = ps_t.bitcast(BF16)
        try:

            def PS(b, p0=0, p1=128, c0=0, c1=512):
                return ps_t[p0:p1, b * 512 + c0: b * 512 + c1]

            def PSB(b, p0=0, p1=128, c0=0, c1=1024):
                return ps_bf[p0:p1, b * 1024 + c0: b * 1024 + c1]

            def pk(b):
                return ('ps', b)

            FR = {}

            def freg(e, v):
                if v not in FR:
                    FR[v] = e.to_reg(float(v))
                return FR[v]

            ident_f = A.alloc([128, 128], F32)
            ident_b = A.alloc([128, 128], BF16)
            ones_f = A.alloc([128, 128], F32)

            def mk_ident(e):
                e.memset(ones_f, 1.0)
                return e.affine_select(out=ident_f, in_=ones_f, pattern=[[-1, 128]], compare_op=ALU.is_equal,
                                       fill=freg(e, 0.0), base=0, channel_multiplier=1)
            S.pool(mk_ident, w=['ident_f', 'ones_f'])
            S.dve(lambda e: e.tensor_copy(out=ident_b, in_=ident_f), r=['ident_f'], w=['ident_b'])

            xT_off = A.top
            xT = A.alloc([128, 8, TT], BF16)
            xT_raw32 = arena_t[:, xT_off // 4: xT_off // 4 + 8208]
            L1_BASE = A.top
            yaT = A.alloc([128, 4, TT], BF16)
            Vsel_off = A.top
            Vsel = A.alloc([128, 16, 128], BF16)
            Vwin_off = A.top
            Vwin = A.alloc([128, 16, 128], BF16)
            Vsel_raw32 = arena_t[:, Vsel_off // 4: Vsel_off // 4 + 1024]
            Vwin_raw32 = arena_t[:, Vwin_off // 4: Vwin_off // 4 + 1024]
            kvsv = A.alloc([NS, 2, 128], F32)
            qs_f = A.alloc([68, 2, 4, NS], F32)
            ksn = A.alloc([68, 2, 2, NS], F32)
            gs_s = A.alloc([24, NS], BF16)
            w_in_t = I['w_in'].rearrange("(c p) f -> p c f", p=128)
            A.mark()
            xst = [A.alloc([128, DM], BF16) for _ in range(2)]
            xp_t = I['xp'].rearrange("(n p) d -> n p d", p=128)
            for n in range(16):
                b = n % 2
                S.dma('pool', lambda e, n=n, b=b: e.dma_start(out=xst[b], in_=xp_t[n]), w=[('xst', b)])
                bank = n % 2

                def tr(e, b=b, bank=bank):
                    ins = None
                    for c in range(8):
                        ins = e.transpose(out=PSB(bank, 0, 128, c * 128, (c + 1) * 128),
                                          in_=xst[b][:, c * 128:(c + 1) * 128], identity=ident_b)
                    return ins
                S.pe(tr, r=[('xst', b), 'ident_b'], w=[pk(bank)])
                S.dve(lambda e, n=n, bank=bank: e.tensor_copy(
                    out=xT[:, :, n * 128:(n + 1) * 128],
                    in_=PSB(bank).rearrange("p (c t) -> p c t", c=8)),
                    r=[pk(bank)], w=[('xT', n)])
            S.dma('pool', lambda e: e.dma_start(out=xst[0][0:NS, :], in_=I['xs']), w=[('xst', 0)])

            def trs(e):
                ins = None
                for c in range(8):
                    ins = e.transpose(out=PSB(0, 0, 128, c * NS, (c + 1) * NS),
                                      in_=xst[0][0:NS, c * 128:(c + 1) * 128], identity=ident_b[0:NS, 0:NS])
                return ins
            S.pe(trs, r=[('xst', 0), 'ident_b'], w=[pk(0)])
            S.dve(lambda e: e.tensor_copy(out=xT[:, :, T:TT],
                                          in_=PSB(0, 0, 128, 0, 8 * NS).rearrange("p (c t) -> p c t", c=8)),
                  r=[pk(0)], w=[('xT', 16)])
            XT_ALL = [('xT', n) for n in range(17)]
            S.barrier()
            A.release()
            stage_end(1)

            A.mark()
            wkv = A.alloc([128, 8, 768], BF16)
            kvst = [A.alloc([128, 768], F32) for _ in range(2)]
            winb = [A.alloc([128, 4, 256], F32) for _ in range(2)]
            for c in range(8):
                S.dma('pool', lambda e, c=c: e.dma_start(out=wkv[:, c, :], in_=w_in_t[:, c, C_KV:C_KV + 768]),
                      w=[('wkv', c)])
            WKV = [('wkv', c) for c in range(8)]
            for n in range(16):
                bA, bB = 4 + (n % 2) * 2, 5 + (n % 2) * 2

                def mmkv(e, n=n, bA=bA, bB=bB):
                    ins = None
                    for c in range(8):
                        ins = e.matmul(PS(bA), lhsT=xT[:, c, n * 128:(n + 1) * 128], rhs=wkv[:, c, 0:512],
                                       start=(c == 0), stop=(c == 7))
                    for c in range(8):
                        ins = e.matmul(PS(bB, 0, 128, 0, 256), lhsT=xT[:, c, n * 128:(n + 1) * 128],
                                       rhs=wkv[:, c, 512:768], start=(c == 0), stop=(c == 7))
                    return ins
                S.pe(mmkv, r=[('xT', n)] + WKV, w=[pk(bA), pk(bB)])
                sb = n % 2
                S.act(lambda e, sb=sb, bA=bA: e.copy(out=kvst[sb][:, 0:512], in_=PS(bA)),
                      r=[pk(bA)], w=[('kvst', sb, 0)])
                S.act(lambda e, sb=sb, bB=bB: e.copy(out=kvst[sb][:, 512:768], in_=PS(bB, 0, 128, 0, 256)),
                      r=[pk(bB)], w=[('kvst', sb, 1)])
                S.dma('sp', lambda e, n=n, sb=sb: e.dma_start(out=O['cmp_p'][n * 128:(n + 1) * 128, :],
                                                               in_=kvst[sb][:, 0:256]), r=[('kvst', sb, 0)])
                S.dma('sp', lambda e, n=n, sb=sb: e.dma_start(out=O['sel_p'][n * 128:(n + 1) * 128, :],
                                                               in_=kvst[sb][:, 256:512]), r=[('kvst', sb, 0)])
                if n >= 12:
                    S.dma('sp', lambda e, n=n, sb=sb: e.dma_start(
                        out=O['win_p'][(n - 12) * 128:(n - 11) * 128, :], in_=kvst[sb][:, 512:768]),
                        r=[('kvst', sb, 1)])
                S.pool(lambda e, n=n, sb=sb: e.tensor_copy(out=Vsel[:, n, :], in_=kvst[sb][:, 384:512]),
                       r=[('kvst', sb, 0)], w=[('Vsel', n)])
                S.pool(lambda e, n=n, sb=sb: e.tensor_copy(out=Vwin[:, n, :], in_=kvst[sb][:, 640:768]),
                       r=[('kvst', sb, 1)], w=[('Vwin', n)])
            stage_end(21)
            def mmkvs(e):
                ins = None
                for c in range(8):
                    ins = e.matmul(PS(4, 0, NS, 0, 512), lhsT=xT[:, c, T:TT], rhs=wkv[:, c, 0:512],
                                   start=(c == 0), stop=(c == 7))
                for c in range(8):
                    ins = e.matmul(PS(5, 0, NS, 0, 256), lhsT=xT[:, c, T:TT], rhs=wkv[:, c, 512:768],
                                   start=(c == 0), stop=(c == 7))
                return ins
            S.pe(mmkvs, r=[('xT', 16)] + WKV, w=[pk(4), pk(5)])
            S.act(lambda e: e.copy(out=kvst[0][0:NS, 0:512], in_=PS(4, 0, NS, 0, 512)), r=[pk(4)], w=[('kvst', 0, 0)])
            S.act(lambda e: e.copy(out=kvst[0][0:NS, 512:768], in_=PS(5, 0, NS, 0, 256)), r=[pk(5)], w=[('kvst', 0, 1)])
            S.dma('sp', lambda e: e.dma_start(out=O['cmp_s'], in_=kvst[0][0:NS, 0:256]), r=[('kvst', 0, 0)])
            S.dma('sp', lambda e: e.dma_start(out=O['sel_s'], in_=kvst[0][0:NS, 256:512]), r=[('kvst', 0, 0)])
            win_s_v = O['win_s'].rearrange("(s r) c -> s r c", r=512)
            S.dma('sp', lambda e: e.dma_start(out=win_s_v[:, 511, :], in_=kvst[0][0:NS, 512:768]), r=[('kvst', 0, 1)])
            S.pool(lambda e: e.tensor_copy(out=kvsv[0:NS, 0, :], in_=kvst[0][0:NS, 384:512]),
                   r=[('kvst', 0, 0)], w=['kvsv'])
            S.pool(lambda e: e.tensor_copy(out=kvsv[0:NS, 1, :], in_=kvst[0][0:NS, 640:768]),
                   r=[('kvst', 0, 1)], w=['kvsv'])
            stage_end(22)
            for s in range(NS):
                wb_ = winb[s % 2]
                src = I['cache_win'][s * 512:(s + 1) * 512, :].rearrange("(p j) c -> p j c", j=4)
                dst = O['win_s'][s * 512:(s + 1) * 512, :].rearrange("(p j) c -> p j c", j=4)
                S.dma('sp', lambda e, wb_=wb_, src=src: e.dma_start(out=wb_, in_=src), w=[('winb', s % 2)])
                S.dma('sp', lambda e, wb_=wb_, dst=dst: e.dma_start(out=dst[:, 0:3, :], in_=wb_[:, 1:4, :]),
                      r=[('winb', s % 2)])
                S.dma('sp', lambda e, wb_=wb_, dst=dst: e.dma_start(out=dst[0:127, 3, :], in_=wb_[1:128, 0, :]),
                      r=[('winb', s % 2)])
            stage_end(23)
            S.barrier()
            A.release()
            stage_end(2)
            BLKS = [(tb * 512, 512) for tb in range(4)] + [(T, NS)]

            def xT_keys(c0, n):
                if c0 >= T:
                    return [('xT', 16)]
                return [('xT', c0 // 128 + i) for i in range(max(1, n // 128))]

            pbank = [0]

            def proj_multi(units, blks=BLKS):
                for (c0, n) in blks:
                    for (wt, wkey, M, evac, aug) in units:
                        bank = pbank[0] % 4
                        pbank[0] += 1

                        def mm(e, c0=c0, n=n, bank=bank, wt=wt, M=M, aug=aug):
                            ins = None
                            for c in range(8):
                                ins = e.matmul(PS(bank, 0, M, 0, n), lhsT=wt[:, c, :], rhs=xT[:, c, c0:c0 + n],
                                               start=(c == 0), stop=(c == 7 and aug is None))
                            if aug is not None:
                                ins = e.matmul(PS(bank, 0, M, 0, n), lhsT=aug, rhs=bas[0:3, c0:c0 + n],
                                               start=False, stop=True)
                            return ins
                        S.pe(mm, r=[wkey, 'bas', 'bas0', 'coef'] + xT_keys(c0, n), w=[pk(bank)])
                        evac(bank, c0, n)

            def proj_fm(wt, wkey, M, evac, aug=None, blks=BLKS):
                proj_multi([(wt, wkey, M, evac, aug)], blks)


            A.mark()
            NWB = 4
            wbuf = [A.alloc([128, 8, 128], BF16) for _ in range(NWB)]
            wctr = [0]

            def load_w(col0, M):
                i = wctr[0] % NWB
                wctr[0] += 1
                S.dma('pool', lambda e: e.dma_start(out=wbuf[i][:, :, 0:M], in_=w_in_t[:, :, col0:col0 + M]),
                      w=[('wbuf', i)])
                return wbuf[i][:, :, 0:M], ('wbuf', i)
            u_ext = A.alloc([128, 4, 30 + T], BF16)
            sza = A.alloc([128, 4, TT], BF16)
            c32 = A.alloc([128, 4, TT], F32)
            u32t = A.alloc([128, 4, 32], F32)
            us32 = A.alloc([128, 4, NS], F32)
            cw_sb = A.alloc([31, 512], F32)
            cwT = A.alloc([128, 4, 31], F32)
            cvec = A.alloc([128, 3, 4], F32)
            A.mark()
            sig = [A.alloc([128, 512], F32) for _ in range(2)]
            sigc = [0]
            S.pool(lambda e: e.memset(u_ext[:, :, 0:30], 0.0), w=[('uext', 'h')])
            for j in range(4):
                wtg, wkg = load_w(C_AGLU + j * 128, 128)
                wta, wka = load_w(C_AVAL + j * 128, 128)
                wtz, wkz = load_w(C_ZA + j * 128, 128)

                def ev_sig(bank, c0, n, j=j):
                    sb = sigc[0] % 2
                    S.act(lambda e: e.activation(out=sig[sb][:, 0:n], in_=PS(bank, 0, 128, 0, n), func=AF.Sigmoid),
                          r=[pk(bank)], w=[('sig', sb)])

                def ev_u(bank, c0, n, j=j):
                    sb = sigc[0] % 2
                    sigc[0] += 1
                    if c0 < T:
                        S.dve(lambda e: e.tensor_tensor(out=u_ext[:, j, 30 + c0:30 + c0 + n], in0=PS(bank, 0, 128, 0, n),
                                                        in1=sig[sb][:, 0:n], op=ALU.mult),
                              r=[pk(bank), ('sig', sb)], w=[('uext', j, c0)])
                        if c0 == 1536:
                            S.dve(lambda e: e.tensor_tensor(out=u32t[:, j, :], in0=PS(bank, 0, 128, 480, 512),
                                                            in1=sig[sb][:, 480:512], op=ALU.mult),
                                  r=[pk(bank), ('sig', sb)], w=[('u32t', j)])
                    else:
                        S.dve(lambda e: e.tensor_tensor(out=us32[:, j, :], in0=PS(bank, 0, 128, 0, n),
                                                        in1=sig[sb][:, 0:n], op=ALU.mult),
                              r=[pk(bank), ('sig', sb)], w=[('us32', j)])

                def ev_za(bank, c0, n, j=j):
                    S.act(lambda e: e.activation(out=sza[:, j, c0:c0 + n], in_=PS(bank, 0, 128, 0, n), func=AF.Silu),
                          r=[pk(bank)], w=[('sza', j, c0)])
                proj_multi([(wtg, wkg, 128, ev_sig, None), (wta, wka, 128, ev_u, None), (wtz, wkz, 128, ev_za, None)])

            stage_end(3)
            S.dma('sp', lambda e: e.dma_start(out=cw_sb, in_=I['conv_w']), w=['cw_sb'])

            def trcw(e):
                ins = None
                for j in range(4):
                    ins = e.transpose(out=PS(7, 0, 128, j * 32, j * 32 + 31), in_=cw_sb[:, j * 128:(j + 1) * 128],
                                      identity=ident_f[0:31, 0:31])
                return ins
            S.pe(trcw, r=['cw_sb', 'ident_f'], w=[pk(7)])
            S.dve(lambda e: e.tensor_copy(out=cwT, in_=PS(7, 0, 128, 0, 128).rearrange("p (j t) -> p j t", j=4)[:, :, 0:31]),
                  r=[pk(7)], w=['cwT'])
            for i, nm in enumerate(['conv_b', 'conv_ln_g', 'conv_ln_b']):
                S.dma('sp', lambda e, i=i, nm=nm: e.dma_start(out=cvec[:, i, :],
                                                              in_=I[nm].rearrange("o (j p) -> p (o j)", p=128),
                                                              allow_slow_non_contiguous=True),
                      w=[('cvec', i)])

            diag = [A.alloc([128, 31, 128], BF16) for _ in range(2)]
            for j in range(4):
                db = j % 2
                for tap in range(31):
                    S.pool(lambda e, j=j, tap=tap, db=db: e.tensor_scalar(
                        out=diag[db][:, tap, :], in0=ident_b, scalar1=cwT[:, j, tap:tap + 1], scalar2=None, op0=ALU.mult),
                        r=['cwT', 'ident_b'], w=[('diag', db, tap)])
                for tb in range(4):
                    bank = 4 + (tb % 2)

                    def cm(e, j=j, tb=tb, db=db, bank=bank):
                        ins = None
                        for tap in range(31):
                            ins = e.matmul(PS(bank), lhsT=diag[db][:, tap, :],
                                           rhs=u_ext[:, j, tb * 512 + tap: tb * 512 + tap + 512],
                                           start=(tap == 0), stop=(tap == 30))
                        return ins
                    rk = [('diag', db, tap) for tap in range(31)] + [('uext', j, tb * 512)]
                    rk += [('uext', j, (tb - 1) * 512)] if tb > 0 else [('uext', 'h')]
                    S.pe(cm, r=rk, w=[pk(bank)])
                    S.act(lambda e, j=j, tb=tb, bank=bank: e.activation(
                        out=c32[:, j, tb * 512:(tb + 1) * 512], in_=PS(bank), func=AF.Identity, bias=cvec[:, 0, j:j + 1]),
                        r=[pk(bank), ('cvec', 0)], w=[('c32', j, tb * 512)])

            stage_end(4)
            exts = A.alloc([128, 4, NS, 31], F32)
            scv = A.alloc([30, NS, 512], F32)
            S.dma('sp', lambda e: e.dma_start(out=scv, in_=I['state_conv'].rearrange("(s r) c -> r s c", r=30)), w=['scv'])
            for s in range(NS):
                def trs_(e, s=s):
                    ins = None
                    for j in range(4):
                        ins = e.transpose(out=PS(6, 0, 128, j * 32, j * 32 + 30), in_=scv[:, s, j * 128:(j + 1) * 128],
                                          identity=ident_f[0:30, 0:30])
                    return ins
                S.pe(trs_, r=['scv', 'ident_f'], w=[pk(6)])
                S.dve(lambda e, s=s: e.tensor_copy(
                    out=exts[:, :, s, 0:30], in_=PS(6, 0, 128, 0, 128).rearrange("p (j t) -> p j t", j=4)[:, :, 0:30]),
                    r=[pk(6)], w=[('exts', s)])
            S.dve(lambda e: e.tensor_copy(out=exts[:, :, :, 30], in_=us32),
                  r=[('us32', j) for j in range(4)], w=[('exts', 'u')])
            prod = A.alloc([128, 4, NS, 31], F32)
            S.dve(lambda e: e.tensor_tensor(out=prod, in0=exts, in1=cwT.unsqueeze(2).broadcast_to([128, 4, NS, 31]),
                                            op=ALU.mult),
                  r=[('exts', s) for s in range(NS)] + [('exts', 'u'), 'cwT'], w=['prod'])
            cs_ = A.alloc([128, 4, NS], F32)
            S.dve(lambda e: e.tensor_reduce(out=cs_, in_=prod, axis=AX.X, op=ALU.add), r=['prod'], w=['cs'])
            S.dve(lambda e: e.tensor_tensor(out=c32[:, :, T:TT], in0=cs_,
                                            in1=cvec[:, 0, :].unsqueeze(2).broadcast_to([128, 4, NS]), op=ALU.add),
                  r=['cs', ('cvec', 0)], w=[('c32', j, T) for j in range(4)])

            cvo = A.alloc([32, 512], F32)

            def tru(e):
                ins = None
                for j in range(4):
                    ins = e.transpose(out=PS(6, 0, 32, j * 128, (j + 1) * 128), in_=u32t[:, j, :], identity=ident_f)
                return ins
            S.pe(tru, r=[('u32t', j) for j in range(4)] + ['ident_f'], w=[pk(6)])
            S.act(lambda e: e.copy(out=cvo, in_=PS(6, 0, 32, 0, 512)), r=[pk(6)], w=['cvo'])
            S.dma('sp', lambda e: e.dma_start(out=O['conv_p'], in_=cvo[2:32, :]), r=['cvo'])
            cso = A.alloc([NS, 512], F32)

            def trus(e):
                ins = None
                for j in range(4):
                    ins = e.transpose(out=PS(6, 0, NS, j * 128, (j + 1) * 128), in_=us32[:, j, :], identity=ident_f)
                return ins
            S.pe(trus, r=[('us32', j) for j in range(4)] + ['ident_f'], w=[pk(6)])
            S.act(lambda e: e.copy(out=cso, in_=PS(6, 0, NS, 0, 512)), r=[pk(6)], w=['cso'])
            conv_s_v = O['conv_s'].rearrange("(s r) c -> s r c", r=30)
            S.dma('sp', lambda e: e.dma_start(out=conv_s_v[:, 29, :], in_=cso), r=['cso'])
            for s in range(NS):
                S.dma('sp', lambda e, s=s: e.dma_start(out=conv_s_v[s, 0:29, :], in_=scv[1:30, s, :]), r=['scv'])

            stage_end(5)
            S.barrier()
            A.release()
            onesm = A.alloc([128, 128], F32)
            S.pool(lambda e: e.memset(onesm, 1.0 / 512.0), w=['onesm'])
            sq4 = A.alloc([128, 4, 512], F32)
            mean_sb = A.alloc([128, 512], F32)
            rstd_sb = A.alloc([128, 512], F32)
            tmpv = A.alloc([128, 512], F32)
            tj = [A.alloc([128, 512], F32) for _ in range(2)]
            tjc = 0
            for (c0, n) in BLKS:
                ck = [('c32', j, c0) for j in range(4)]
                S.act(lambda e, c0=c0, n=n: e.activation(out=sq4[:, :, 0:n], in_=c32[:, :, c0:c0 + n], func=AF.Square),
                      r=ck, w=['sq4'])

                def stm(e, c0=c0, n=n):
                    ins = None
                    for j in range(4):
                        ins = e.matmul(PS(0, 0, 128, 0, n), lhsT=onesm, rhs=c32[:, j, c0:c0 + n], start=(j == 0), stop=(j == 3))
                    for j in range(4):
                        ins = e.matmul(PS(1, 0, 128, 0, n), lhsT=onesm, rhs=sq4[:, j, 0:n], start=(j == 0), stop=(j == 3))
                    return ins
                S.pe(stm, r=ck + ['sq4', 'onesm'], w=[pk(0), pk(1)])
                S.act(lambda e, n=n: e.copy(out=mean_sb[:, 0:n], in_=PS(0, 0, 128, 0, n)), r=[pk(0)], w=['mean_sb'])
                S.dve(lambda e, n=n: e.tensor_tensor(out=tmpv[:, 0:n], in0=mean_sb[:, 0:n], in1=mean_sb[:, 0:n], op=ALU.mult),
                      r=['mean_sb'], w=['tmpv'])
                S.dve(lambda e, n=n: e.tensor_tensor(out=tmpv[:, 0:n], in0=PS(1, 0, 128, 0, n), in1=tmpv[:, 0:n],
                                                     op=ALU.subtract), r=[pk(1), 'tmpv'], w=['tmpv'])
                S.dve(lambda e, n=n: e.tensor_scalar(out=tmpv[:, 0:n], in0=tmpv[:, 0:n], scalar1=1e-5, scalar2=None,
                                                     op0=ALU.add), r=['tmpv'], w=['tmpv'])
                S.act(lambda e, n=n: e.activation(out=tmpv[:, 0:n], in_=tmpv[:, 0:n], func=AF.Ln), r=['tmpv'], w=['tmpv'])
                S.act(lambda e, n=n: e.activation(out=rstd_sb[:, 0:n], in_=tmpv[:, 0:n], func=AF.Exp, scale=-0.5),
                      r=['tmpv'], w=['rstd_sb'])
                for j in range(4):
                    tb_ = tj[tjc % 2]
                    tk = ('tj', tjc % 2)
                    tjc += 1
                    S.dve(lambda e, j=j, c0=c0, n=n, tb_=tb_: e.tensor_tensor(
                        out=tb_[:, 0:n], in0=c32[:, j, c0:c0 + n], in1=mean_sb[:, 0:n], op=ALU.subtract),
                        r=[('c32', j, c0), 'mean_sb'], w=[tk])
                    S.dve(lambda e, n=n, tb_=tb_: e.tensor_tensor(out=tb_[:, 0:n], in0=tb_[:, 0:n], in1=rstd_sb[:, 0:n],
                                                                  op=ALU.mult), r=[tk, 'rstd_sb'], w=[tk])
                    S.dve(lambda e, j=j, n=n, tb_=tb_: e.tensor_scalar(
                        out=tb_[:, 0:n], in0=tb_[:, 0:n], scalar1=cvec[:, 1, j:j + 1], scalar2=cvec[:, 2, j:j + 1],
                        op0=ALU.mult, op1=ALU.add), r=[tk, ('cvec', 1), ('cvec', 2)], w=[tk])
                    S.act(lambda e, n=n, tb_=tb_: e.activation(out=tb_[:, 0:n], in_=tb_[:, 0:n], func=AF.Silu),
                          r=[tk], w=[tk])
                    S.pool(lambda e, j=j, c0=c0, n=n, tb_=tb_: e.tensor_tensor(
                        out=yaT[:, j, c0:c0 + n], in0=tb_[:, 0:n], in1=sza[:, j, c0:c0 + n], op=ALU.mult),
                        r=[tk, ('sza', j, c0)], w=[('yaT', j, c0)])
            S.barrier()
            A.release()
            stage_end(6)
            FORCE = 1.0e4
            BIG = 30000.0
            ybT = A.alloc([64, 8, TT], BF16)
            A.mark()
            Qaug = [A.alloc([128, 4, TT], BF16) for g in range(2)]
            Ksel = [A.alloc([128, TT], BF16) for g in range(2)]
            Kwin = [A.alloc([68, TT], BF16) for g in range(2)]
            gsig = A.alloc([24, TT], BF16)
            KCa = [A.alloc([68, 64], BF16) for g in range(2)]
            VC = [A.alloc([64, 64], BF16) for g in range(2)]
            kcT = A.alloc([64, 2, 2, 64], F32)
            Wkv = A.alloc([64, 2, 2, 32], F32)
            oh = A.alloc([24, 24, 64], BF16)
            ones64 = A.alloc([128, 64], BF16)
            Mpad = A.alloc([128, 128], BF16)
            A.mark()
            NWB = 4
            wpad = [A.alloc([128, 8, 68], BF16) for _ in range(NWB)]
            for i in range(NWB):
                S.pool(lambda e, i=i: e.memset(wpad[i], 0.0), w=[('wpad', i)])
            wpc = [0]

            def load_wpad(col0, M=64):
                i = wpc[0] % NWB
                wpc[0] += 1
                S.dma('pool', lambda e: e.dma_start(out=wpad[i][:, :, 0:M], in_=w_in_t[:, :, col0:col0 + M]),
                      w=[('wpad', i)])
                return wpad[i], ('wpad', i)
            bas = A.alloc([3, TT], BF16)
            basc = A.alloc([3, 64], BF16)
            brow = A.alloc([1, 2, TT], BF16)
            browc = A.alloc([1, 2, 64], BF16)
            coefQ = A.alloc([3, 8, 68], BF16)
            coefK = A.alloc([3, 68], BF16)
            cq0 = A.alloc([1, 3, 8, 68], BF16)
            ck0 = A.alloc([1, 3, 68], BF16)

            def mkbas(e):
                e.iota(brow[:, 0, 0:T], pattern=[[1, 32], [0, 64]], base=0, channel_multiplier=0,
                       allow_small_or_imprecise_dtypes=True)
                e.iota(brow[:, 1, 0:T], pattern=[[0, 32], [1, 64]], base=0, channel_multiplier=0,
                       allow_small_or_imprecise_dtypes=True)
                e.memset(brow[:, 0, T:TT], 128.0)
                e.memset(brow[:, 1, T:TT], 0.0)
                e.iota(browc[:, 0, :], pattern=[[1, 32], [0, 2]], base=0, channel_multiplier=0,
                       allow_small_or_imprecise_dtypes=True)
                e.iota(browc[:, 1, :], pattern=[[0, 32], [32, 2]], base=31, channel_multiplier=0,
                       allow_small_or_imprecise_dtypes=True)
                e.memset(bas[0:1, :], 1.0)
                e.memset(basc[0:1, :], 1.0)
                e.memset(cq0, 0.0)
                e.memset(ck0, 0.0)
                for h in range(8):
                    sl = 2.0 ** (-(h + 1))
                    e.memset(cq0[:, 0, h, 64:65], 8.0 * sl * 64.0)
                    e.memset(cq0[:, 0, h, 65:66], 8.0 * sl)
                    e.memset(cq0[:, 1, h, 66:67], -8.0 * sl * 64.0)
                    e.memset(cq0[:, 2, h, 67:68], -8.0 * sl)
                e.memset(ck0[:, 0, 66:68], 1.0)
                e.memset(ck0[:, 1, 64:65], 1.0)
                e.memset(ck0[:, 2, 65:66], 1.0)
                e.memset(ones64, 1.0)
                e.memset(Mpad, 0.0)
                return e.tensor_copy(out=oh, in_=ident_b[0:24, 0:24].unsqueeze(2).broadcast_to([24, 24, 64]))
            S.pool(mkbas, r=['ident_b'], w=['brow', 'bas0', 'ones64', 'Mpad', 'oh'])
            for i in range(2):
                S.dma('sp', lambda e, i=i: e.dma_start(out=bas[1 + i:2 + i, :], in_=brow[0:1, i, :]), r=['brow'], w=['bas'])
                S.dma('sp', lambda e, i=i: e.dma_start(out=basc[1 + i:2 + i, :], in_=browc[0:1, i, :]), r=['brow'], w=['bas'])
            for i in range(3):
                S.dma('sp', lambda e, i=i: e.dma_start(out=coefQ[i:i + 1, :, :], in_=cq0[0:1, i, :, :]), r=['brow'], w=['coef'])
                S.dma('sp', lambda e, i=i: e.dma_start(out=coefK[i:i + 1, :], in_=ck0[0:1, i, :]), r=['brow'], w=['coef'])

            QK = [[('Q', g, c0) for (c0, n) in BLKS] + [('Qm', g, qt) for qt in range(17)] for g in range(2)]
            KK = [[('Ks', g, c0) for (c0, n) in BLKS] for g in range(2)]
            for g in range(2):
                S.pool(lambda e, g=g: e.memset(Qaug[g][64:128, :, :], 0.0), w=QK[g])

                def kinit(e, g=g):
                    e.memset(Ksel[g][64:128, :], 0.0)
                    e.memset(Ksel[g][96:128, 0:T], 1.0)
                    e.affine_select(out=Ksel[g][96:128, 0:T], in_=Ksel[g][96:128, 0:T], pattern=[[1, T]],
                                    compare_op=ALU.is_ge, fill=freg(e, 0.0), base=0, channel_multiplier=-64)
                    return e.affine_select(out=Ksel[g][96:128, 0:T], in_=Ksel[g][96:128, 0:T], pattern=[[-1, T]],
                                           compare_op=ALU.is_ge, fill=freg(e, 0.0), base=63, channel_multiplier=64)
                S.pool(kinit, w=KK[g])

            stage_end(7)
            wt, wk_ = load_wpad(C_G3, 24)

            def ev_g(bank, c0, n):
                S.act(lambda e: e.activation(out=gsig[0:24, c0:c0 + n], in_=PS(bank, 0, 24, 0, n), func=AF.Sigmoid),
                      r=[pk(bank)], w=[('gsig', c0)])
            proj_fm(wt[:, :, 0:24], wk_, 24, ev_g)
            for h in range(8):
                wt, wk_ = load_wpad(C_ZB + h * 64, 64)

                def ev_zb(bank, c0, n, h=h):
                    S.act(lambda e: e.activation(out=ybT[0:64, h, c0:c0 + n], in_=PS(bank, 0, 64, 0, n), func=AF.Silu),
                          r=[pk(bank)], w=[('ybT', h, c0)])
                proj_fm(wt[:, :, 0:64], wk_, 64, ev_zb)
            for h in range(8):
                g, r_ = h // 4, h % 4
                wt, wk_ = load_wpad(C_Q + h * 64)

                def ev_q(bank, c0, n, g=g, r_=r_):
                    S.dve(lambda e: e.tensor_scalar(out=Qaug[g][0:68, r_, c0:c0 + n], in0=PS(bank, 0, 68, 0, n),
                                                    scalar1=0.125, scalar2=None, op0=ALU.mult),
                          r=[pk(bank)], w=[('Q', g, c0)])
                proj_fm(wt, wk_, 68, ev_q, aug=coefQ[0:3, h, :])
            augK = coefK[0:3, :]
            for g in range(2):
                wt, wk_ = load_wpad(C_KV + 256 + g * 64)

                def ev_ks(bank, c0, n, g=g):
                    S.act(lambda e: e.copy(out=Ksel[g][0:68, c0:c0 + n], in_=PS(bank, 0, 68, 0, n)),
                          r=[pk(bank)], w=[('Ks', g, c0)])
                proj_fm(wt, wk_, 68, ev_ks, aug=augK)
                wt, wk_ = load_wpad(C_KV + 512 + g * 64)

                def ev_kw(bank, c0, n, g=g):
                    S.act(lambda e: e.copy(out=Kwin[g][0:68, c0:c0 + n], in_=PS(bank, 0, 68, 0, n)),
                          r=[pk(bank)], w=[('Kw', g, c0)])
                proj_fm(wt, wk_, 68, ev_kw, aug=augK)
            for kv_, nm in enumerate(['wk_cmp', 'wv_cmp']):
                for g in range(2):
                    S.dma('sp', lambda e, kv_=kv_, nm=nm, g=g: e.dma_start(
                        out=Wkv[:, kv_, g:g + 1, :], in_=I[nm][:, g:g + 1].rearrange("l o -> o l").partition_broadcast(64),
                        allow_slow_non_contiguous=True), w=[('Wkv', kv_, g)])
            ptmp = [A.alloc([64, 16, 32], F32) for _ in range(1)]
            pcnt = [0]
            for kv_ in range(2):
                for g in range(2):
                    wt, wk_ = load_wpad(C_KV + kv_ * 128 + g * 64, 64)

                    def ev_pool(bank, c0, n, kv_=kv_, g=g):
                        pb = 0
                        pcnt[0] += 1
                        S.dve(lambda e: e.tensor_tensor(
                            out=ptmp[pb], in0=PS(bank, 0, 64, 0, 512).rearrange("p (a l) -> p a l", l=32),
                            in1=Wkv[:, kv_, g:g + 1, :].broadcast_to([64, 16, 32]), op=ALU.mult),
                            r=[pk(bank), ('Wkv', kv_, g)], w=[('ptmp', pb)])
                        S.dve(lambda e: e.tensor_reduce(out=kcT[:, kv_, g, c0 // 32:c0 // 32 + 16], in_=ptmp[pb],
                                                        axis=AX.X, op=ALU.add),
                              r=[('ptmp', pb)], w=[('kcT', kv_, g, c0)])
                    proj_fm(wt[:, :, 0:64], wk_, 64, ev_pool, blks=BLKS[0:4])
            for g in range(2):
                kck = [('kcT', 0, g, c0) for (c0, n) in BLKS[0:4]]
                vck = [('kcT', 1, g, c0) for (c0, n) in BLKS[0:4]]
                S.dve(lambda e, g=g: e.tensor_copy(out=KCa[g][0:64, :], in_=kcT[:, 0, g, :]), r=kck, w=[('KCa', g)])

                def mmaug(e, g=g):
                    return e.matmul(PS(7, 0, 68, 0, 64), lhsT=coefK[0:3, :], rhs=basc[0:3, :], start=True, stop=True)
                S.pe(mmaug, r=['coef', 'bas', 'bas0'], w=[pk(7)])
                S.act(lambda e, g=g: e.copy(out=KCa[g][64:68, :], in_=PS(7, 64, 68, 0, 64)), r=[pk(7)], w=[('KCa', g)])
                S.pe(lambda e, g=g: e.transpose(out=PS(6, 0, 64, 0, 64), in_=kcT[:, 1, g, :], identity=ident_f[0:64, 0:64]),
                     r=vck + ['ident_f'], w=[pk(6)])
                S.act(lambda e, g=g: e.copy(out=VC[g], in_=PS(6, 0, 64, 0, 64)), r=[pk(6)], w=[('VC', g)])

            stage_end(8)
            S.barrier()
            A.release()
            Ec = A.alloc([128, 4, 64], F32)
            sums_c = A.alloc([128, 4], F32)
            imp64 = A.alloc([128, 64], F32)
            imp = A.alloc([128, 32], F32)
            imp2 = A.alloc([128, 32], F32)
            m8 = A.alloc([128, 16], F32)
            PT = [A.alloc([128, 512], BF16) for _ in range(3)]
            acc = [A.alloc([64, 512], F32) for _ in range(2)]
            rs_ = [A.alloc([64, 512], F32) for _ in range(2)]
            tt_ = [A.alloc([64, 512], F32) for _ in range(2)]
            ptc = [0]
            sbc = [0]
            rsc = [0]

            def qkeys(g, qt):
                return [('Q', g, (qt // 4) * 512), ('Qm', g, qt)]

            pend = [None]

            def attn_pair(g, qt, lhsT, lkeys, KKrows, vt, vkeys, masks, ob, sb_, first, last):
                q0 = qt * 128
                sbank = sbc[0] % 2
                sbc[0] += 1
                pt = PT[ptc[0] % 3]
                ptk = ('PT', ptc[0] % 3)
                ptc[0] += 1
                M = lhsT.shape[1]
                S.pe(lambda e: e.matmul(PS(sbank, 0, M, 0, 512), lhsT=lhsT, rhs=Qaug[g][0:KKrows, :, q0:q0 + 128],
                                        start=True, stop=True),
                     r=lkeys + qkeys(g, qt), w=[pk(sbank)])
                S.act(lambda e: e.activation(out=pt[0:M, :], in_=PS(sbank, 0, M, 0, 512), func=AF.Exp),
                      r=[pk(sbank)], w=[ptk])
                for (base, cm, pat) in masks:
                    S.pool(lambda e, base=base, cm=cm, pat=pat: e.affine_select(
                        out=pt[0:M, :], in_=pt[0:M, :], pattern=pat, compare_op=ALU.is_ge, fill=freg(e, 0.0), base=base,
                        channel_multiplier=cm), r=[ptk], w=[ptk])

                def pv(e):
                    e.matmul(PS(ob, 0, 64, 0, 512), lhsT=vt, rhs=pt[0:M, :], start=first, stop=last)
                    return e.matmul(PS(sb_, 0, 64, 0, 512), lhsT=ones64[0:M, :], rhs=pt[0:M, :], start=first, stop=last)
                prev = pend[0]
                pend[0] = lambda: S.pe(pv, r=[ptk, 'ones64'] + vkeys, w=[pk(ob), pk(sb_)])
                if prev is not None:
                    prev()

            def flush_pv():
                if pend[0] is not None:
                    pend[0]()
                    pend[0] = None

            def combine(g, qt, j, ob, sb_, ab):
                flush_pv()
                q0 = qt * 128
                rb = rsc[0] % 2
                rsc[0] += 1

                def gmm(e):
                    ins = None
                    for r_ in range(4):
                        ins = e.matmul(PS(6, 0, 64, r_ * 128, (r_ + 1) * 128), lhsT=oh[:, 3 * (4 * g + r_) + j, :],
                                       rhs=gsig[0:24, q0:q0 + 128], start=True, stop=True)
                    return ins
                S.pe(gmm, r=['oh', ('gsig', (qt // 4) * 512)], w=[pk(6)])
                S.dve(lambda e: e.tensor_scalar(out=rs_[rb], in0=PS(sb_, 0, 64, 0, 512), scalar1=1e-30, scalar2=None,
                                                op0=ALU.max), r=[pk(sb_)], w=[('rs', rb)])
                S.dve(lambda e: e.reciprocal(out=rs_[rb], in_=rs_[rb]), r=[('rs', rb)], w=[('rs', rb)])
                S.dve(lambda e: e.tensor_tensor(out=rs_[rb], in0=rs_[rb], in1=PS(6, 0, 64, 0, 512), op=ALU.mult),
                      r=[('rs', rb), pk(6)], w=[('rs', rb)])
                if j == 0:
                    S.dve(lambda e: e.tensor_tensor(out=acc[ab], in0=PS(ob, 0, 64, 0, 512), in1=rs_[rb], op=ALU.mult),
                          r=[pk(ob), ('rs', rb)], w=[('acc', ab)])
                else:
                    S.dve(lambda e: e.tensor_tensor(out=tt_[rb], in0=PS(ob, 0, 64, 0, 512), in1=rs_[rb], op=ALU.mult),
                          r=[pk(ob), ('rs', rb)], w=[('tt', rb)])
                    S.pool(lambda e: e.tensor_tensor(out=acc[ab], in0=acc[ab], in1=tt_[rb], op=ALU.add),
                           r=[('tt', rb), ('acc', ab)], w=[('acc', ab)])

            pat_q = [[0, 4], [1, 128]]
            pat_qn = [[0, 4], [-1, 128]]
            def partA(qt, g):
                q0 = qt * 128
                def cmm(e, g=g, q0=q0):
                    ins = None
                    for r_ in range(4):
                        ins = e.matmul(PS(7, 0, 128, r_ * 64, (r_ + 1) * 64), lhsT=Qaug[g][0:68, r_, q0:q0 + 128],
                                       rhs=KCa[g][0:68, :], start=True, stop=True)
                    return ins
                S.pe(cmm, r=qkeys(g, qt) + [('KCa', g)], w=[pk(7)])
                S.act(lambda e: e.activation(out=Ec, in_=PS(7, 0, 128, 0, 256).rearrange("p (r n) -> p r n", r=4),
                                             func=AF.Exp), r=[pk(7)], w=['Ec'])
                S.pool(lambda e, q0=q0: e.affine_select(out=Ec, in_=Ec, pattern=[[0, 4], [-32, 64]],
                                                        compare_op=ALU.is_ge, fill=freg(e, 0.0), base=q0 - 31,
                                                        channel_multiplier=1), r=['Ec'], w=['Ec'])
                S.dve(lambda e: e.tensor_reduce(out=sums_c, in_=Ec, axis=AX.X, op=ALU.add), r=['Ec'], w=['sums_c'])
                S.dve(lambda e: e.tensor_scalar(out=sums_c, in0=sums_c, scalar1=1e-30, scalar2=None, op0=ALU.max),
                      r=['sums_c'], w=['sums_c'])
                S.dve(lambda e: e.reciprocal(out=sums_c, in_=sums_c), r=['sums_c'], w=['sums_c'])
                S.dve(lambda e: e.tensor_tensor(out=Ec, in0=Ec, in1=sums_c.unsqueeze(2).broadcast_to([128, 4, 64]),
                                                op=ALU.mult), r=['Ec', 'sums_c'], w=['Ec'])
                S.dve(lambda e: e.tensor_reduce(out=imp64, in_=Ec.rearrange("p r n -> p n r"), axis=AX.X, op=ALU.add),
                      r=['Ec'], w=['imp64'])
                S.dve(lambda e: e.tensor_reduce(out=imp, in_=imp64.rearrange("p (a b) -> p a b", b=2), axis=AX.X,
                                                op=ALU.add), r=['imp64'], w=['imp'])
                S.pool(lambda e, q0=q0: e.affine_select(out=imp, in_=imp, pattern=[[-64, 32]], compare_op=ALU.is_ge,
                                                        fill=freg(e, FORCE), base=q0 - 128, channel_multiplier=1),
                       r=['imp'], w=['imp'])
                S.pool(lambda e: e.memset(imp[:, 0:1], FORCE), r=['imp'], w=['imp'])
                S.pool(lambda e, q0=q0: e.affine_select(out=imp, in_=imp, pattern=[[-64, 32]], compare_op=ALU.is_ge,
                                                        fill=freg(e, -1.0e30), base=q0, channel_multiplier=1),
                       r=['imp'], w=['imp'])
                S.dve(lambda e: e.max(out=m8[:, 0:8], in_=imp), r=['imp'], w=['m8a'])
                S.dve(lambda e: e.match_replace(out=imp2, in_to_replace=m8[:, 0:8], in_values=imp, imm_value=-3.0e38),
                      r=['imp', 'm8a'], w=['imp2'])
                S.dve(lambda e: e.max(out=m8[:, 8:16], in_=imp2), r=['imp2'], w=['m8b'])
                S.dve(lambda e: e.tensor_scalar(out=Mpad[:, 96:128], in0=imp, scalar1=m8[:, 15:16], scalar2=None,
                                                op0=ALU.is_ge), r=['imp', 'm8b'], w=['Mpad'])

            def partB(qt, g):
                q0 = qt * 128
                S.pe(lambda e: e.matmul(PS(6, 0, 128, 0, 128), lhsT=Mpad, rhs=ident_b, start=True, stop=True),
                     r=['Mpad', 'ident_b'], w=[pk(6)])
                S.dve(lambda e, g=g, q0=q0: e.tensor_scalar(
                    out=Qaug[g][96:128, :, q0:q0 + 128],
                    in0=PS(6, 96, 128, 0, 128).unsqueeze(1).broadcast_to([32, 4, 128]),
                    scalar1=1.0, scalar2=BIG, op0=ALU.subtract, op1=ALU.mult), r=[pk(6)], w=[('Qm', g, qt)])

            def partC(qt, g):
                q0 = qt * 128
                ab = (qt * 2 + g) % 2
                attn_pair(g, qt, KCa[g][0:68, :], [('KCa', g)], 68, VC[g], [('VC', g)],
                          [(q0 - 31, -32, pat_q)], 2, 3, True, True)
                combine(g, qt, 0, 2, 3, ab)
                for kt in range(qt + 1):
                    masks = [(0, -1, pat_q)] if kt == qt else []
                    attn_pair(g, qt, Ksel[g][:, kt * 128:(kt + 1) * 128], [('Ks', g, (kt // 4) * 512)] + KK[g][0:0], 128,
                              Vsel[:, kt, g * 64:(g + 1) * 64], [('Vsel', kt)], masks, 4, 5, kt == 0, kt == qt)
                combine(g, qt, 1, 4, 5, ab)
                k_lo = max(0, qt - 4)
                for kt in range(k_lo, qt + 1):
                    masks = []
                    if kt == qt:
                        masks.append((0, -1, pat_q))
                    if kt == qt - 4:
                        masks.append((-1, 1, pat_qn))
                    attn_pair(g, qt, Kwin[g][0:68, kt * 128:(kt + 1) * 128], [('Kw', g, (kt // 4) * 512)], 68,
                              Vwin[:, kt, g * 64:(g + 1) * 64], [('Vwin', kt)], masks, 2, 3, kt == k_lo, kt == qt)
                combine(g, qt, 2, 2, 3, ab)
                S.dve(lambda e, g=g, q0=q0, ab=ab: e.tensor_tensor(
                    out=ybT[0:64, 4 * g:4 * g + 4, q0:q0 + 128],
                    in0=acc[ab].rearrange("p (r q) -> p r q", r=4),
                    in1=ybT[0:64, 4 * g:4 * g + 4, q0:q0 + 128], op=ALU.mult),
                    r=[('acc', ab)] + [('ybT', 4 * g + r_, (qt // 4) * 512) for r_ in range(4)],
                    w=[('ybT', 4 * g + r_, (qt // 4) * 512) for r_ in range(4)])

            its = [(qt, g) for qt in range(16) for g in range(2)]
            partA(*its[0])
            partB(*its[0])
            for i_, (qt, g) in enumerate(its):
                if i_ + 1 < len(its):
                    partA(*its[i_ + 1])
                partC(qt, g)
                if i_ + 1 < len(its):
                    partB(*its[i_ + 1])
            for g in range(2):
                S.dve(lambda e, g=g: e.tensor_copy(out=qs_f[:, g, :, :], in_=Qaug[g][0:68, :, T:TT]), r=[('Q', g, T)], w=['qs_f'])
                S.dve(lambda e, g=g: e.tensor_copy(out=ksn[:, 0, g, :], in_=Ksel[g][0:68, T:TT]), r=[('Ks', g, T)], w=['ksn'])
                S.dve(lambda e, g=g: e.tensor_copy(out=ksn[:, 1, g, :], in_=Kwin[g][0:68, T:TT]), r=[('Kw', g, T)], w=['ksn'])
            S.dve(lambda e: e.tensor_copy(out=gs_s, in_=gsig[0:24, T:TT]), r=[('gsig', T)], w=['gs_s'])
            stage_end(9)
            S.barrier()
            A.release()
            A.mark()
            XK = [A.alloc([128, 32, 128], F32) for _ in range(2)]
            XV = [A.alloc([128, 32, 128], F32) for _ in range(2)]
            tmp4 = xT_raw32[:, 0:8192].rearrange("p (l r d) -> p l r d", l=32, r=4)
            pt_i = A.alloc([32, NS, 2], I32)
            pt_f = A.alloc([32, NS * 2], F32)
            E32 = A.alloc([32, 128], F32)
            qcol_i = A.alloc([128, 1], I32)
            qcol_f = A.alloc([128, 1], F32)
            idx_f = A.alloc([128, NS * 2], F32)
            idx_i = A.alloc([128, NS * 2], I32)
            Wl4 = A.alloc([128, 32, 4], F32)
            slopes = A.alloc([128, 8], F32)
            posc = A.alloc([128, 2], F32)
            ABc = A.alloc([128, 2, 8], F32)
            poss = A.alloc([128, 2, 32], F32)
            ABs = A.alloc([128, 2, 8, 32], F32)
            posw = A.alloc([128, 4], F32)
            ABw = A.alloc([128, 8, 4], F32)
            Ex = A.alloc([128, 2, 128], BF16)
            qrep = A.alloc([64, 8, 128], BF16)
            qb = A.alloc([128, 8, 64], F32)
            kvc = Vwin_raw32[:, 0:512].rearrange("p (t c) -> p t c", t=2)
            tmpc = A.alloc([128, 2, 4, 64], F32)
            sc = A.alloc([128, 2, 8], F32)
            Esum = A.alloc([128, 8], F32)
            pcg = A.alloc([128, 2, 2], F32)
            impn = A.alloc([2, 256], F32)
            impd = A.alloc([2, 129], F32)
            impd2 = A.alloc([2, 129], F32)
            m8d = A.alloc([2, 16], F32)
            Md = A.alloc([2, 129], F32)
            MT = A.alloc([128, 2], BF16)
            mskd = A.alloc([128, 2, 2], F32)
            ss = Vwin_raw32[:, 512:1024].rearrange("p (t h l) -> p t h l", t=2, h=8)
            Pl = A.alloc([128, 2, 8], F32)
            Wn = Vsel_raw32[:, 0:1024].rearrange("p (l c) -> p l c", l=4)
            sw = A.alloc([128, 8, 4], F32)
            Plw = A.alloc([128, 8], F32)
            prodn = A.alloc([68, 2, 2, 4, NS], F32)
            pnew = A.alloc([1, 2, 2, 4, NS], F32)
            vrow = A.alloc([1, NS, 2, 128], F32)
            accd = A.alloc([4, 2, 64], F32)
            gts = A.alloc([4, 6, NS], F32)
            rsd = A.alloc([4, 4], F32)

            cache_v = {(c_, hf): I['cache_%s_%s' % (c_, hf)].rearrange("(b q) c -> b (q c)", q=32)
                       for c_ in ('cmp', 'sel') for hf in ('k', 'v')}

            def chain(addfn, fns, key, r=()):
                for fn in fns:
                    addfn(fn, r=[key] + list(r), w=[key])

            fns1 = [
                lambda e: e.iota(qcol_i, pattern=[[0, 1]], base=0, channel_multiplier=1),
                lambda e: e.memset(E32, 1.0),
                lambda e: e.memset(accd, 0.0),
                lambda e: e.affine_select(out=E32, in_=E32, pattern=[[1, 128]], compare_op=ALU.is_ge, fill=freg(e, 0.0), base=0,
                                          channel_multiplier=-4),
                lambda e: e.iota(posc, pattern=[[4096, 2]], base=31 - 8192, channel_multiplier=32,
                                 allow_small_or_imprecise_dtypes=True),
                lambda e: e.affine_select(out=E32, in_=E32, pattern=[[-1, 128]], compare_op=ALU.is_ge, fill=freg(e, 0.0), base=3,
                                          channel_multiplier=4),
                lambda e: e.iota(poss, pattern=[[4096, 2], [1, 32]], base=-8192, channel_multiplier=32,
                                 allow_small_or_imprecise_dtypes=True),
                lambda e: e.iota(posw, pattern=[[1, 4]], base=7680 - 8192, channel_multiplier=4,
                                 allow_small_or_imprecise_dtypes=True),
            ]
            for h in range(8):
                fns1.append(lambda e, h=h: e.memset(slopes[:, h:h + 1], 2.0 ** (-(h + 1))))
            for t in range(2):
                fns1.append(lambda e, t=t: e.memset(Ex[:, t, :], 1.0))
            for t in range(2):
                fns1.append(lambda e, t=t: e.affine_select(out=Ex[:, t, :], in_=Ex[:, t, :], pattern=[[1, 128]], compare_op=ALU.is_ge,
                                                          fill=freg(e, 0.0), base=128 * t, channel_multiplier=-2))
            for t in range(2):
                fns1.append(lambda e, t=t: e.affine_select(out=Ex[:, t, :], in_=Ex[:, t, :], pattern=[[-1, 128]], compare_op=ALU.is_ge,
                                                          fill=freg(e, 0.0), base=1 - 128 * t, channel_multiplier=2))
            chain(S.pool, fns1, 'dsetup')
            S.dma('sp', lambda e: e.dma_start(out=pt_i, in_=I['page_table'].rearrange("s (t j) -> j s t", j=32),
                                              allow_slow_non_contiguous=True), w=['pt_i'])
            S.dma('sp', lambda e: e.dma_start(out=Wl4[:, :, 0:2], in_=I['wk_cmp'].partition_broadcast(128)), w=['Wl4a'])
            S.dma('sp', lambda e: e.dma_start(out=Wl4[:, :, 2:4], in_=I['wv_cmp'].partition_broadcast(128)), w=['Wl4b'])
            for s in range(NS):
                S.dma('sp', lambda e, s=s: e.dma_start(out=vrow[0:1, s, 0, :], in_=kvsv[s:s + 1, 0, :]), r=['kvsv'],
                      w=[('vrow', s)])
                S.dma('sp', lambda e, s=s: e.dma_start(out=vrow[0:1, s, 1, :], in_=kvsv[s:s + 1, 1, :]), r=['kvsv'],
                      w=[('vrow', s)])

            fns2 = [
                lambda e: e.tensor_copy(out=pt_f, in_=pt_i.rearrange("p s t -> p (s t)")),
                lambda e: e.tensor_single_scalar(out=qcol_i, in_=qcol_i, scalar=3, op=ALU.bitwise_and),
                lambda e: e.tensor_tensor(out=ABc, in0=posc.unsqueeze(2).broadcast_to([128, 2, 8]),
                                          in1=slopes.unsqueeze(1).broadcast_to([128, 2, 8]), op=ALU.mult),
                lambda e: e.tensor_copy(out=qcol_f, in_=qcol_i),
                lambda e: e.tensor_tensor(out=ABw, in0=posw.unsqueeze(1).broadcast_to([128, 8, 4]),
                                          in1=slopes.unsqueeze(2).broadcast_to([128, 8, 4]), op=ALU.mult),
            ]
            for t in range(2):
                fns2.append(lambda e, t=t: e.tensor_tensor(out=ABs[:, t, :, :], in0=poss[:, t, :].unsqueeze(1).broadcast_to([128, 8, 32]),
                                                          in1=slopes.unsqueeze(2).broadcast_to([128, 8, 32]), op=ALU.mult))
            chain(S.dve, fns2, 'dsetup2', r=['dsetup', 'pt_i'])
            S.pe(lambda e: e.matmul(PS(1, 0, 128, 0, NS * 2), lhsT=E32, rhs=pt_f, start=True, stop=True),
                 r=['dsetup', 'dsetup2'], w=[pk(1)])
            S.dve(lambda e: e.tensor_scalar(out=idx_f, in0=PS(1, 0, 128, 0, NS * 2), scalar1=4.0, scalar2=qcol_f[:, 0:1],
                                            op0=ALU.mult, op1=ALU.add), r=[pk(1), 'dsetup2'], w=['idx_f'])
            chain(S.dve, [lambda e: e.tensor_copy(out=idx_i, in_=idx_f)], 'idx_i', r=['idx_f'])
            for br in range(2):
                S.dve(lambda e, br=br: e.tensor_tensor(
                    out=prodn[:, br, :, :, :], in0=qs_f, in1=ksn[:, br, :, :].unsqueeze(2).broadcast_to([68, 2, 4, NS]),
                    op=ALU.mult), r=['qs_f', 'ksn'], w=[('prodn', br)])
            S.pe(lambda e: e.matmul(PS(7, 0, 1, 0, 64), lhsT=ones_f[0:68, 0:1],
                                    rhs=prodn.rearrange("p a g r s -> p (a g r s)"), start=True, stop=True),
                 r=[('prodn', 0), ('prodn', 1), 'ones_f'], w=[pk(7)])
            S.act(lambda e: e.activation(out=pnew.rearrange("p a g r s -> p (a g r s)"), in_=PS(7, 0, 1, 0, 64), func=AF.Exp),
                  r=[pk(7)], w=['pnew'])
            def gmm_d(e):
                ins = None
                for g in range(2):
                    for j in range(3):
                        c0_ = 12 * g + j
                        ins = e.matmul(PS(7, 0, 4, 64 + (g * 3 + j) * NS, 64 + (g * 3 + j + 1) * NS),
                                       lhsT=ident_b[0:24, c0_:c0_ + 10:3], rhs=gs_s, start=True, stop=True)
                return ins
            S.pe(gmm_d, r=['gs_s', 'ident_b'], w=[pk(7)])
            S.act(lambda e: e.copy(out=gts.rearrange("p a s -> p (a s)"), in_=PS(7, 0, 4, 64, 64 + 6 * NS)), r=[pk(7)],
                  w=['gts'])

            def gather(cname, s, t, xb):
                col = s * 2 + t
                for (buf, hf, kn) in ((XK, 'k', 'XK'), (XV, 'v', 'XV')):
                    S.dma('pool', lambda e, buf=buf, hf=hf: e.indirect_dma_start(
                        out=buf[xb].rearrange("p l c -> p (l c)"), out_offset=None, in_=cache_v[(cname, hf)],
                        in_offset=bass.IndirectOffsetOnAxis(ap=idx_i[:, col:col + 1], axis=0)), r=['idx_i'], w=[(kn, xb)])

            for s in range(NS):
                for t in range(2):
                    gather('cmp', s, t, t)
                    for hi_, (buf, kn) in enumerate(((XK, 'XK'), (XV, 'XV'))):
                        S.dve(lambda e, t=t, buf=buf, hi_=hi_: e.tensor_tensor(
                            out=buf[t].rearrange("p l (a d) -> p l a d", a=2), in0=buf[t].rearrange("p l (a d) -> p l a d", a=2),
                            in1=Wl4[:, :, 2 * hi_:2 * hi_ + 2].unsqueeze(3).broadcast_to([128, 32, 2, 64]), op=ALU.mult),
                            r=[(kn, t), 'Wl4a', 'Wl4b'], w=[(kn, t)])
                        S.dve(lambda e, t=t, buf=buf, hi_=hi_: e.tensor_reduce(
                            out=kvc[:, t, hi_ * 128:(hi_ + 1) * 128], in_=buf[t].rearrange("p l c -> p c l"), axis=AX.X, op=ALU.add),
                            r=[(kn, t)], w=[('kvc', t)])
                stage_end(106)
                S.dve(lambda e, s=s: e.tensor_copy(
                    out=qrep, in_=qs_f[0:64, :, :, s].rearrange("p g r -> p (g r)").unsqueeze(2).broadcast_to([64, 8, 128])),
                    r=['qs_f'], w=['qrep'])

                def qbm(e):
                    ins = None
                    for h in range(8):
                        ins = e.matmul(PS(0, 0, 128, h * 64, (h + 1) * 64), lhsT=qrep[:, h, :], rhs=ident_b[0:64, 0:64],
                                       start=True, stop=True)
                    return ins
                S.pe(qbm, r=['qrep', 'ident_b'], w=[pk(0)])
                S.act(lambda e: e.copy(out=qb.rearrange("p h d -> p (h d)"), in_=PS(0)), r=[pk(0)], w=['qb'])
                for t in range(2):
                    S.dve(lambda e, t=t: e.tensor_tensor(
                        out=tmpc, in0=kvc[:, t, 0:128].rearrange("p (g d) -> p g d", g=2).unsqueeze(2).broadcast_to([128, 2, 4, 64]),
                        in1=qb.rearrange("p (g r) d -> p g r d", g=2), op=ALU.mult),
                        r=[('kvc', t), 'qb'], w=['tmpc'])
                    S.dve(lambda e, t=t: e.tensor_reduce(out=sc[:, t, :], in_=tmpc.rearrange("p g r d -> p (g r) d"),
                                                         axis=AX.X, op=ALU.add), r=['tmpc'], w=[('sc', t)])
                S.dve(lambda e: e.tensor_tensor(out=sc, in0=sc, in1=ABc, op=ALU.add), r=[('sc', 0), ('sc', 1), 'dsetup2'],
                      w=['sc'])
                S.act(lambda e: e.activation(out=sc, in_=sc, func=AF.Exp), r=['sc'], w=['sc'])
                S.dve(lambda e: e.tensor_tensor(out=Esum, in0=sc[:, 0, :], in1=sc[:, 1, :], op=ALU.add), r=['sc'], w=['Esum'])
                S.pe(lambda e: e.matmul(PS(1, 0, 128, 0, 8), lhsT=ones_f, rhs=Esum, start=True, stop=True),
                     r=['Esum', 'ones_f'], w=[pk(1)])
                S.dve(lambda e: e.reciprocal(out=Esum, in_=PS(1, 0, 128, 0, 8)), r=[pk(1)], w=['Esum'])
                S.dve(lambda e: e.tensor_tensor(out=sc, in0=sc, in1=Esum.unsqueeze(1).broadcast_to([128, 2, 8]), op=ALU.mult),
                      r=['sc', 'Esum'], w=['sc'])

                def ocm(e):
                    ins = None
                    for g in range(2):
                        for t in range(2):
                            ins = e.matmul(PS(2, 0, 4, g * 64, (g + 1) * 64), lhsT=sc[:, t, 4 * g:4 * g + 4],
                                           rhs=kvc[:, t, 128 + g * 64:128 + (g + 1) * 64], start=(t == 0), stop=(t == 1))
                    return ins
                S.pe(ocm, r=['sc', ('kvc', 0), ('kvc', 1)], w=[pk(2)])
                S.dve(lambda e: e.tensor_reduce(out=pcg, in_=sc.rearrange("p t (g r) -> p t g r", g=2), axis=AX.X, op=ALU.add),
                      r=['sc'], w=['pcg'])

                def trp(e):
                    ins = None
                    for t in range(2):
                        ins = e.transpose(out=PS(3, 0, 2, t * 128, (t + 1) * 128), in_=pcg[:, t, :], identity=ident_f)
                    return ins
                S.pe(trp, r=['pcg', 'ident_f'], w=[pk(3)])
                S.act(lambda e: e.copy(out=impn, in_=PS(3, 0, 2, 0, 256)), r=[pk(3)], w=['impn'])
                S.dve(lambda e: e.tensor_reduce(out=impd[:, 0:128], in_=impn.rearrange("p (a b) -> p a b", b=2), axis=AX.X,
                                                op=ALU.add), r=['impn'], w=['impd'])

                S.pool(lambda e: e.memset(impd[:, 0:1], FORCE), r=['impd'], w=['impd'])
                S.pool(lambda e: e.memset(impd[:, 127:129], FORCE), r=['impd'], w=['impd'])
                S.dve(lambda e: e.max(out=m8d[:, 0:8], in_=impd), r=['impd'], w=['m8da'])
                S.dve(lambda e: e.match_replace(out=impd2, in_to_replace=m8d[:, 0:8], in_values=impd, imm_value=-3.0e38),
                      r=['impd', 'm8da'], w=['impd2'])
                S.dve(lambda e: e.max(out=m8d[:, 8:16], in_=impd2), r=['impd2'], w=['m8db'])
                S.dve(lambda e: e.tensor_scalar(out=Md, in0=impd, scalar1=m8d[:, 15:16], scalar2=None, op0=ALU.is_ge),
                      r=['impd', 'm8db'], w=['Md'])
                S.pe(lambda e: e.transpose(out=PS(3, 0, 128, 256, 258), in_=Md[:, 0:128], identity=ident_f[0:2, 0:2]),
                     r=['Md', 'ident_f'], w=[pk(3)])
                S.act(lambda e: e.copy(out=MT, in_=PS(3, 0, 128, 256, 258)), r=[pk(3)], w=['MT'])

                def mskm(e):
                    ins = None
                    for t in range(2):
                        ins = e.matmul(PS(3, 0, 128, 260 + 2 * t, 262 + 2 * t), lhsT=Ex[:, t, :], rhs=MT, start=True, stop=True)
                    return ins
                S.pe(mskm, r=['MT', 'dsetup'], w=[pk(3)])
                S.act(lambda e: e.copy(out=mskd.rearrange("p t g -> p (t g)"), in_=PS(3, 0, 128, 260, 264)), r=[pk(3)],
                      w=['mskd'])
                for t in range(2):
                    gather('sel', s, t, t)
                    for g in range(2):
                        S.dve(lambda e, t=t, g=g: e.tensor_tensor(
                            out=tmp4, in0=XK[t][:, :, g * 64:(g + 1) * 64].unsqueeze(2).broadcast_to([128, 32, 4, 64]),
                            in1=qb[:, 4 * g:4 * g + 4, :].unsqueeze(1).broadcast_to([128, 32, 4, 64]), op=ALU.mult),
                            r=[('XK', t), 'qb'], w=['tmp4'])
                        S.dve(lambda e, t=t, g=g: e.tensor_reduce(
                            out=ss[:, t, 4 * g:4 * g + 4, :].rearrange("p r l -> p l r"), in_=tmp4, axis=AX.X, op=ALU.add),
                            r=['tmp4'], w=[('ss', t, g)])
                SSK = [('ss', t, g) for t in range(2) for g in range(2)]
                S.dve(lambda e: e.tensor_tensor(out=ss, in0=ss, in1=ABs, op=ALU.add), r=SSK + ['dsetup2'], w=['ss'])
                S.act(lambda e: e.activation(out=ss, in_=ss, func=AF.Exp), r=['ss'], w=['ss'])
                for t in range(2):
                    S.dve(lambda e, t=t: e.tensor_tensor(
                        out=ss[:, t, :, :].rearrange("p (g r) l -> p g (r l)", g=2),
                        in0=ss[:, t, :, :].rearrange("p (g r) l -> p g (r l)", g=2),
                        in1=mskd[:, t, :].unsqueeze(2).broadcast_to([128, 2, 128]), op=ALU.mult),
                        r=['ss', 'mskd'], w=['ss'])
                S.dve(lambda e: e.tensor_reduce(out=Pl, in_=ss, axis=AX.X, op=ALU.add), r=['ss'], w=['Pl'])

                def pvs(e, s=s):
                    ins = None
                    for g in range(2):
                        k = 0
                        for t in range(2):
                            for l in range(32):
                                ins = e.matmul(PS(4, 0, 4, g * 64, (g + 1) * 64), lhsT=ss[:, t, 4 * g:4 * g + 4, l],
                                               rhs=XV[t][:, l, g * 64:(g + 1) * 64], start=(k == 0), stop=False)
                                k += 1
                        ins = e.matmul(PS(4, 0, 4, g * 64, (g + 1) * 64), lhsT=pnew[0:1, 0, g, :, s],
                                       rhs=vrow[0:1, s, 0, g * 64:(g + 1) * 64], start=False, stop=True)
                        for t in range(2):
                            ins = e.matmul(PS(5, 0, 4, g, g + 1), lhsT=Pl[:, t, 4 * g:4 * g + 4], rhs=ones_f[:, 0:1],
                                           start=(t == 0), stop=False)
                        ins = e.matmul(PS(5, 0, 4, g, g + 1), lhsT=pnew[0:1, 0, g, :, s], rhs=ones_f[0:1, 0:1],
                                       start=False, stop=True)
                    return ins
                S.pe(pvs, r=['ss', 'Pl', ('XV', 0), ('XV', 1), 'pnew', ('vrow', s), 'ones_f'], w=[pk(4), pk(5)])
                S.dma('sp', lambda e, s=s: e.dma_start(out=Wn, in_=I['cache_win'][s * 512:(s + 1) * 512, :].rearrange(
                    "(p j) c -> p j c", j=4)), w=['Wn'])
                for g in range(2):
                    S.dve(lambda e, g=g: e.tensor_tensor(
                        out=tmp4[:, 0:4, :, :], in0=Wn[:, :, g * 64:(g + 1) * 64].unsqueeze(2).broadcast_to([128, 4, 4, 64]),
                        in1=qb[:, 4 * g:4 * g + 4, :].unsqueeze(1).broadcast_to([128, 4, 4, 64]), op=ALU.mult),
                        r=['Wn', 'qb'], w=['tmp4'])
                    S.dve(lambda e, g=g: e.tensor_reduce(out=sw[:, 4 * g:4 * g + 4, :].rearrange("p r l -> p l r"),
                                                         in_=tmp4[:, 0:4, :, :], axis=AX.X, op=ALU.add),
                          r=['tmp4'], w=[('sw', g)])
                S.dve(lambda e: e.tensor_tensor(out=sw, in0=sw, in1=ABw, op=ALU.add), r=[('sw', 0), ('sw', 1), 'dsetup2'],
                      w=['sw'])
                S.act(lambda e: e.activation(out=sw, in_=sw, func=AF.Exp), r=['sw'], w=['sw'])
                S.pool(lambda e: e.memset(sw[0:1, :, 0:1], 0.0), r=['sw'], w=['sw'])
                S.dve(lambda e: e.tensor_reduce(out=Plw, in_=sw, axis=AX.X, op=ALU.add), r=['sw'], w=['Plw'])

                def pvw(e, s=s):
                    ins = None
                    for g in range(2):
                        for l in range(4):
                            ins = e.matmul(PS(6, 0, 4, g * 64, (g + 1) * 64), lhsT=sw[:, 4 * g:4 * g + 4, l],
                                           rhs=Wn[:, l, 128 + g * 64:128 + (g + 1) * 64], start=(l == 0), stop=False)
                        ins = e.matmul(PS(6, 0, 4, g * 64, (g + 1) * 64), lhsT=pnew[0:1, 1, g, :, s],
                                       rhs=vrow[0:1, s, 1, g * 64:(g + 1) * 64], start=False, stop=True)
                        ins = e.matmul(PS(5, 0, 4, 2 + g, 3 + g), lhsT=Plw[:, 4 * g:4 * g + 4], rhs=ones_f[:, 0:1],
                                       start=True, stop=False)
                        ins = e.matmul(PS(5, 0, 4, 2 + g, 3 + g), lhsT=pnew[0:1, 1, g, :, s], rhs=ones_f[0:1, 0:1],
                                       start=False, stop=True)
                    return ins
                S.pe(pvw, r=['sw', 'Plw', 'Wn', 'pnew', ('vrow', s), 'ones_f'], w=[pk(6), pk(5)])
                S.dve(lambda e: e.reciprocal(out=rsd, in_=PS(5, 0, 4, 0, 4)), r=[pk(5)], w=['rsd'])
                for g in range(2):
                    S.dve(lambda e, g=g, s=s: e.tensor_scalar(out=accd[:, g, :], in0=PS(2, 0, 4, g * 64, (g + 1) * 64),
                                                              scalar1=gts[:, g * 3 + 0, s:s + 1], scalar2=None, op0=ALU.mult),
                          r=[pk(2), 'gts'], w=[('accd', g)])
                    for bi, (bank, col) in enumerate([(4, g), (6, 2 + g)]):
                        S.dve(lambda e, g=g, s=s, bi=bi, col=col: e.tensor_tensor(
                            out=rsd[:, col:col + 1], in0=rsd[:, col:col + 1], in1=gts[:, g * 3 + 1 + bi, s:s + 1], op=ALU.mult),
                            r=['rsd', 'gts'], w=['rsd'])
                        S.dve(lambda e, g=g, bank=bank, col=col: e.scalar_tensor_tensor(
                            out=accd[:, g, :], in0=PS(bank, 0, 4, g * 64, (g + 1) * 64), scalar=rsd[:, col:col + 1],
                            in1=accd[:, g, :], op0=ALU.mult, op1=ALU.add), r=[pk(bank), 'rsd', ('accd', g)], w=[('accd', g)])
                    S.pe(lambda e, g=g: e.transpose(out=PS(7, 0, 64, 128 + 4 * g, 132 + 4 * g), in_=accd[:, g, :],
                                                    identity=ident_f[0:4, 0:4]), r=[('accd', g), 'ident_f'], w=[pk(7)])
                    S.dve(lambda e, g=g, s=s: e.tensor_tensor(
                        out=ybT[0:64, 4 * g:4 * g + 4, T + s], in0=PS(7, 0, 64, 128 + 4 * g, 132 + 4 * g),
                        in1=ybT[0:64, 4 * g:4 * g + 4, T + s], op=ALU.mult),
                        r=[pk(7)] + [('ybT', 4 * g + r_, T) for r_ in range(4)],
                        w=[('ybT', 4 * g + r_, T) for r_ in range(4)] + ['dec_yb'])
            if os.environ.get('KDEV_DUMP'):
                yp = O['y_prompt']
                S.dma('sp', lambda e: e.dma_start(out=yp[0:128, :], in_=Xb[0][:, 0:4, :].rearrange("p l c -> p (l c)")), r=[('Xb', 0)])
                S.dma('sp', lambda e: e.dma_start(out=yp[128:256, 0:512], in_=kvc.rearrange("p t c -> p (t c)")), r=[('kvc', 0), ('kvc', 1)])
                S.dma('sp', lambda e: e.dma_start(out=yp[256:384, 0:16], in_=sc.rearrange("p t h -> p (t h)")), r=['sc'])
                S.dma('sp', lambda e: e.dma_start(out=yp[384:512, 0:512], in_=ss.rearrange("p t h l -> p (t h l)")), r=['ss'])
                S.dma('sp', lambda e: e.dma_start(out=yp[512:516, 0:4], in_=rsd), r=['rsd'])
                S.dma('sp', lambda e: e.dma_start(out=yp[516:520, 0:128], in_=accd.rearrange("p g d -> p (g d)")), r=[('accd', 0), ('accd', 1)])
                S.dma('sp', lambda e: e.dma_start(out=yp[640:768, 0:8], in_=idx_f), r=['idx_i'])
                S.dma('sp', lambda e: e.dma_start(out=yp[768:896, 0:4], in_=mskd.rearrange("p t g -> p (t g)")), r=['mskd'])
                S.dma('sp', lambda e: e.dma_start(out=yp[896:898, 0:129], in_=impd), r=['impd'])
                S.dma('sp', lambda e: e.dma_start(out=yp[900:1028, 0:512], in_=qb.rearrange("p h d -> p (h d)")), r=['qb'])
                S.dma('sp', lambda e: e.dma_start(out=yp[1028:1029, 0:64], in_=pnew.rearrange("p a g r s -> p (a g r s)")), r=['pnew'])
                S.dma('sp', lambda e: e.dma_start(out=yp[1030:1034, 0:24], in_=gts.rearrange("p a s -> p (a s)")), r=['gts'])
            S.barrier()
            A.release()
            stage_end(11)
            ALPHA = float((2 * 2) ** 0.25)
            x1T = xT
            A.mark()
            wo_a = A.alloc([128, 4, DM], BF16)
            wo_b = A.alloc([64, 8, DM], BF16)
            lnp = A.alloc([128, 2, DM], F32)
            xres = [A.alloc([128, DM], F32) for _ in range(2)]
            x1b = [A.alloc([128, DM], BF16) for _ in range(2)]
            bst = A.alloc([128, 2, 6], F32)
            mv = A.alloc([128, 4], F32)
            for j in range(4):
                S.dma('pool', lambda e, j=j: e.dma_start(out=wo_a[:, j, :], in_=I['w_out_even'][j * 128:(j + 1) * 128, :]),
                      w=[('wo_a', j)])
            for h in range(8):
                S.dma('pool', lambda e, h=h: e.dma_start(out=wo_b[:, h, :],
                                                         in_=I['w_out_even'][512 + h * 64:512 + (h + 1) * 64, :]),
                      w=[('wo_b', h)])
            S.dma('sp', lambda e: e.dma_start(out=lnp[:, 0, :], in_=I['ln_g'][0:1, :].partition_broadcast(128)), w=[('lnp', 0)])
            S.dma('sp', lambda e: e.dma_start(out=lnp[:, 1, :], in_=I['ln_b'][0:1, :].partition_broadcast(128)), w=[('lnp', 1)])
            WO = [('wo_a', j) for j in range(4)] + [('wo_b', h) for h in range(8)]

            def resid_ln(n, rows, c0, xsrc, yk, lidx, ya_, yb_, outs):
                b = n % 2
                S.dma('sp', lambda e: e.dma_start(out=xres[b][0:rows, :], in_=xsrc), w=[('xres', b)])

                def mm(e):
                    ins = None
                    for cb in range(2):
                        k = 0
                        tot = len(ya_) + len(yb_)
                        for (lt, rt) in ya_ + yb_:
                            ins = e.matmul(PS(cb, 0, rows, 0, 512), lhsT=lt[:, c0:c0 + rows], rhs=rt[:, cb * 512:(cb + 1) * 512],
                                           start=(k == 0), stop=(k == tot - 1))
                            k += 1
                    return ins
                S.pe(mm, r=yk + WO, w=[pk(0), pk(1)])
                for cb in range(2):
                    S.dve(lambda e, cb=cb: e.scalar_tensor_tensor(
                        out=xres[b][0:rows, cb * 512:(cb + 1) * 512], in0=xres[b][0:rows, cb * 512:(cb + 1) * 512],
                        scalar=ALPHA, in1=PS(cb, 0, rows, 0, 512), op0=ALU.mult, op1=ALU.add),
                        r=[('xres', b), pk(cb)], w=[('xres', b)])
                    S.dve(lambda e, cb=cb: e.bn_stats(out=bst[0:rows, cb, :], in_=xres[b][0:rows, cb * 512:(cb + 1) * 512]),
                          r=[('xres', b)], w=[('bst', cb)])
                S.dve(lambda e: e.bn_aggr(out=mv[0:rows, 0:2], in_=bst[0:rows, :, :].rearrange("p a b -> p (a b)")),
                      r=[('bst', 0), ('bst', 1)], w=['mv'])
                S.dve(lambda e: e.tensor_scalar(out=mv[0:rows, 2:3], in0=mv[0:rows, 1:2], scalar1=1e-5, scalar2=None,
                                                op0=ALU.add), r=['mv'], w=['mv2'])
                S.act(lambda e: e.activation(out=mv[0:rows, 2:3], in_=mv[0:rows, 2:3], func=AF.Ln), r=['mv2'], w=['mv2'])
                S.act(lambda e: e.activation(out=mv[0:rows, 2:3], in_=mv[0:rows, 2:3], func=AF.Exp, scale=-0.5),
                      r=['mv2'], w=['mv2'])
                S.dve(lambda e: e.tensor_scalar(out=xres[b][0:rows, :], in0=xres[b][0:rows, :], scalar1=mv[0:rows, 0:1],
                                                scalar2=mv[0:rows, 2:3], op0=ALU.subtract, op1=ALU.mult),
                      r=[('xres', b), 'mv', 'mv2'], w=[('xres', b)])
                S.dve(lambda e: e.tensor_tensor(out=xres[b][0:rows, :], in0=xres[b][0:rows, :], in1=lnp[0:rows, 2 * lidx, :],
                                                op=ALU.mult), r=[('xres', b), ('lnp', 2 * lidx)], w=[('xres', b)])
                S.pool(lambda e: e.tensor_tensor(out=xres[b][0:rows, :], in0=xres[b][0:rows, :],
                                                 in1=lnp[0:rows, 2 * lidx + 1, :], op=ALU.add),
                       r=[('xres', b), ('lnp', 2 * lidx + 1)], w=[('xres', b)])
                outs(b)

            def l0_outs(n, rows, c0):
                def f(b):
                    S.dma('sp', lambda e: e.dma_start(out=x1_scr[c0:c0 + rows, :], in_=xres[b][0:rows, :]),
                          r=[('xres', b)], w=[('x1scr', n)])
                    if n == 15:
                        S.dma('sp', lambda e: e.dma_start(out=O['shift_p'], in_=xres[b][127:128, :]), r=[('xres', b)])
                    if n == 16:
                        S.dma('sp', lambda e: e.dma_start(out=O['shift_s'], in_=xres[b][0:rows, :]), r=[('xres', b)])
                    S.act(lambda e: e.copy(out=x1b[b][0:rows, :], in_=xres[b][0:rows, :]), r=[('xres', b)], w=[('x1b', b)])
                    bank = 2 + (n % 2)

                    def tr(e):
                        ins = None
                        for c in range(8):
                            ins = e.transpose(out=PSB(bank, 0, 128, c * rows, (c + 1) * rows),
                                              in_=x1b[b][0:rows, c * 128:(c + 1) * 128], identity=ident_b[0:rows, 0:rows])
                        return ins
                    S.pe(tr, r=[('x1b', b), 'ident_b'], w=[pk(bank)])
                    S.dve(lambda e: e.tensor_copy(out=x1T[:, :, c0:c0 + rows],
                                                  in_=PSB(bank, 0, 128, 0, 8 * rows).rearrange("p (c t) -> p c t", c=8)),
                          r=[pk(bank)], w=[('x1T', n)])
                return f

            for n in range(17):
                rows = 128 if n < 16 else NS
                c0 = n * 128 if n < 16 else T
                xsrc = I['xp'][c0:c0 + 128, :] if n < 16 else I['xs']
                blk = (c0 // 512) * 512 if n < 16 else T
                yk = [('yaT', j, blk) for j in range(4)] + [('ybT', h, blk) for h in range(8)]
                if n == 16:
                    yk += ['dec_yb']
                ya_ = [(yaT[:, j, :], wo_a[:, j, :]) for j in range(4)]
                yb_ = [(ybT[0:64, h, :], wo_b[0:64, h, :]) for h in range(8)]
                resid_ln(n, rows, c0, xsrc, yk, 0, ya_, yb_, l0_outs(n, rows, c0))
            S.barrier()
            A.release()
            stage_end(10)
            S.barrier()
            A.top = L1_BASE
            A.marks = []
            A.mark()
            dT = A.alloc([128, 8, TT], BF16)
            xsh = A.alloc([128, 8, TT], BF16)
            muT = A.alloc([128, 6, 8], F32)
            vecs = A.alloc([128, 5, 8], F32)
            shb = A.alloc([NS, DM], BF16)
            stg = [A.alloc([128, 1024], F32) for _ in range(3)]
            wl1 = [A.alloc([128, 8, 128], BF16) for _ in range(3)]
            wfull = A.alloc([128, 8, 1024], BF16)
            lo1 = A.alloc([128, 8, 64], BF16)
            lo2 = A.alloc([64, 1024], BF16)
            t1T = A.alloc([64, TT], BF16)
            S.dma('sp', lambda e: e.dma_start(out=muT, in_=I['mu_c'].rearrange("i (c p) -> p i c", p=128),
                                              allow_slow_non_contiguous=True), w=['muT'])
            for i, nm in enumerate(['w0', 'a0', 'k_k', 'k_a', 'r_k']):
                S.dma('sp', lambda e, i=i, nm=nm: e.dma_start(out=vecs[:, i, :], in_=I[nm].rearrange("o (c p) -> p (o c)", p=128),
                                                              allow_slow_non_contiguous=True), w=[('vecs', i)])
            S.dma('pool', lambda e: e.dma_start(out=shb, in_=I['state_shift']), w=['shb'])

            def trsh(e):
                ins = None
                for c in range(8):
                    ins = e.transpose(out=PSB(0, 0, 128, c * NS, (c + 1) * NS), in_=shb[0:NS, c * 128:(c + 1) * 128],
                                      identity=ident_b[0:NS, 0:NS])
                return ins
            S.pe(trsh, r=['shb', 'ident_b'], w=[pk(0)])
            X1K = [('x1T', n) for n in range(17)]
            S.dve(lambda e: e.tensor_tensor(out=dT[:, :, T:TT], in0=PSB(0, 0, 128, 0, 8 * NS).rearrange("p (c t) -> p c t", c=8),
                                            in1=x1T[:, :, T:TT], op=ALU.subtract), r=[pk(0)] + X1K, w=['dT'])
            S.dve(lambda e: e.tensor_tensor(out=dT[:, :, 1:T], in0=x1T[:, :, 0:T - 1], in1=x1T[:, :, 1:T], op=ALU.subtract),
                  r=X1K, w=['dT'])
            S.dve(lambda e: e.tensor_scalar(out=dT[:, :, 0:1], in0=x1T[:, :, 0:1], scalar1=-1.0, scalar2=None, op0=ALU.mult),
                  r=X1K, w=['dT'])

            def mk_xsh(i):
                for c in range(8):
                    eng = S.dve if c % 2 == 0 else S.pool
                    if True:
                        S.dve(lambda e, c=c: e.scalar_tensor_tensor(out=xsh[:, c, :], in0=dT[:, c, :], scalar=muT[:, i, c:c + 1],
                                                                    in1=x1T[:, c, :], op0=ALU.mult, op1=ALU.add),
                              r=['dT', 'muT'] + X1K, w=[('xsh', c)])
                    else:
                        S.pool(lambda e, c=c: e.tensor_scalar(out=xsh[:, c, :], in0=dT[:, c, :], scalar1=muT[:, i, c:c + 1],
                                                              scalar2=None, op0=ALU.mult), r=['dT', 'muT'], w=[('xsh', c)])
                        S.pool(lambda e, c=c: e.tensor_tensor(out=xsh[:, c, :], in0=xsh[:, c, :], in1=x1T[:, c, :], op=ALU.add),
                               r=[('xsh', c)] + X1K, w=[('xsh', c)])
            XSH = [('xsh', c) for c in range(8)]
            wrk_t = I['w_rkvz'].rearrange("(i c p) f -> i p c f", i=4, p=128)
            sgc = [0]
            wlc = [0]

            def proj_fm1(widx, dst, func, bias_i):
                for pr in range(8):
                    wi = wlc[0] % 3
                    wlc[0] += 1
                    S.dma('pool', lambda e, wi=wi, pr=pr: e.dma_start(out=wl1[wi], in_=wrk_t[widx][:, :, pr * 128:(pr + 1) * 128]),
                          w=[('wl1', wi)])
                    si = sgc[0] % 3
                    sgc[0] += 1
                    for bi_, (c0, n) in enumerate(BLKS):
                        bank = pbank[0] % 4
                        pbank[0] += 1

                        def mm(e, wi=wi, c0=c0, n=n, bank=bank):
                            ins = None
                            for c in range(8):
                                ins = e.matmul(PS(bank, 0, 128, 0, n), lhsT=wl1[wi][:, c, :], rhs=xsh[:, c, c0:c0 + n],
                                               start=(c == 0), stop=(c == 7))
                            return ins
                        S.pe(mm, r=[('wl1', wi)] + XSH, w=[pk(bank)])
                        cc0 = c0 if c0 < T else 0
                        sdst = stg[si][:, cc0 % 1024:cc0 % 1024 + n] if c0 < T else stg[si][:, 0:n]
                        S.act(lambda e, bank=bank, n=n, sdst=sdst: e.copy(out=sdst, in_=PS(bank, 0, 128, 0, n)),
                              r=[pk(bank)], w=[('stg', si)])
                        if bi_ in (1, 3, 4):
                            lo = {1: 0, 3: 1024, 4: T}[bi_]
                            wdt = 1024 if bi_ != 4 else NS
                            S.dma('sp', lambda e, si=si, pr=pr, lo=lo, wdt=wdt: e.dma_start(out=dst[:, pr, lo:lo + wdt],
                                                                                           in_=stg[si][:, 0:wdt]),
                                  r=[('stg', si)], w=[('scr', widx, pr)])
                            if bi_ != 4:
                                si = sgc[0] % 3
                                sgc[0] += 1

            def proj_tm1(widx, dst, silu):
                for c in range(8):
                    S.dma('pool', lambda e, c=c: e.dma_start(out=wfull[:, c, :], in_=wrk_t[widx][:, c, :]), w=[('wfull', c)])
                WF = [('wfull', c) for c in range(8)]
                for n in range(17):
                    rows = 128 if n < 16 else NS
                    c0 = n * 128 if n < 16 else T
                    si = sgc[0] % 3
                    sgc[0] += 1

                    def mm(e, rows=rows, c0=c0):
                        ins = None
                        for cb in range(2):
                            for c in range(8):
                                ins = e.matmul(PS(4 + cb, 0, rows, 0, 512), lhsT=xsh[:, c, c0:c0 + rows],
                                               rhs=wfull[:, c, cb * 512:(cb + 1) * 512], start=(c == 0), stop=(c == 7))
                        return ins
                    S.pe(mm, r=WF + XSH, w=[pk(4), pk(5)])
                    for cb in range(2):
                        if silu:
                            S.act(lambda e, cb=cb, rows=rows, si=si: e.activation(
                                out=stg[si][0:rows, cb * 512:(cb + 1) * 512], in_=PS(4 + cb, 0, rows, 0, 512), func=AF.Silu),
                                r=[pk(4 + cb)], w=[('stg', si)])
                        else:
                            S.act(lambda e, cb=cb, rows=rows, si=si: e.copy(out=stg[si][0:rows, cb * 512:(cb + 1) * 512],
                                                                            in_=PS(4 + cb, 0, rows, 0, 512)),
                                  r=[pk(4 + cb)], w=[('stg', si)])
                    S.dma('sp', lambda e, rows=rows, c0=c0, si=si: e.dma_start(out=dst[c0:c0 + rows, :], in_=stg[si][0:rows, :]),
                          r=[('stg', si)], w=[('scrt', widx, n)])

            def proj_lora(mi, w1n, w2n, vi, dst, use_tanh):
                S.dma('pool', lambda e: e.dma_start(out=lo1, in_=I[w1n].rearrange("(c p) f -> p c f", p=128)), w=['lo1'])
                S.dma('pool', lambda e: e.dma_start(out=lo2, in_=I[w2n]), w=['lo2'])
                for (c0, n) in BLKS:
                    bank = pbank[0] % 4
                    pbank[0] += 1

                    def mm(e, c0=c0, n=n, bank=bank):
                        ins = None
                        for c in range(8):
                            ins = e.matmul(PS(bank, 0, 64, 0, n), lhsT=lo1[:, c, :], rhs=xsh[:, c, c0:c0 + n],
                                           start=(c == 0), stop=(c == 7))
                        return ins
                    S.pe(mm, r=['lo1'] + XSH, w=[pk(bank)])
                    if use_tanh:
                        S.act(lambda e, c0=c0, n=n, bank=bank: e.activation(out=t1T[:, c0:c0 + n], in_=PS(bank, 0, 64, 0, n),
                                                                            func=AF.Tanh), r=[pk(bank)], w=[('t1T', c0)])
                    else:
                        S.act(lambda e, c0=c0, n=n, bank=bank: e.copy(out=t1T[:, c0:c0 + n], in_=PS(bank, 0, 64, 0, n)),
                              r=[pk(bank)], w=[('t1T', c0)])
                T1K = [('t1T', c0) for (c0, n) in BLKS]
                for pr in range(8):
                    si = sgc[0] % 3
                    sgc[0] += 1
                    for bi_, (c0, n) in enumerate(BLKS):
                        bank = pbank[0] % 4
                        pbank[0] += 1
                        S.pe(lambda e, pr=pr, c0=c0, n=n, bank=bank: e.matmul(
                            PS(bank, 0, 128, 0, n), lhsT=lo2[:, pr * 128:(pr + 1) * 128], rhs=t1T[:, c0:c0 + n],
                            start=True, stop=True), r=['lo2'] + T1K, w=[pk(bank)])
                        sdst = stg[si][:, c0 % 1024:c0 % 1024 + n] if c0 < T else stg[si][:, 0:n]
                        S.act(lambda e, bank=bank, n=n, sdst=sdst, pr=pr: e.activation(
                            out=sdst, in_=PS(bank, 0, 128, 0, n), func=AF.Sigmoid, bias=vecs[:, vi, pr:pr + 1]),
                            r=[pk(bank), ('vecs', vi)], w=[('stg', si)])
                        if bi_ in (1, 3, 4):
                            lo = {1: 0, 3: 1024, 4: T}[bi_]
                            wdt = 1024 if bi_ != 4 else NS
                            S.dma('sp', lambda e, si=si, pr=pr, lo=lo, wdt=wdt: e.dma_start(out=dst[:, pr, lo:lo + wdt],
                                                                                           in_=stg[si][:, 0:wdt]),
                                  r=[('stg', si)], w=[('scr', mi, pr)])
                            if bi_ != 4:
                                si = sgc[0] % 3
                                sgc[0] += 1

            mk_xsh(0)
            proj_fm1(0, r_scr, None, None)
            mk_xsh(1)
            proj_fm1(1, k_scr, None, None)
            mk_xsh(2)
            proj_tm1(2, v_scr, False)
            mk_xsh(3)
            proj_tm1(3, z_scr, True)
            mk_xsh(4)
            proj_lora(4, 'w1', 'w2', 0, sw_scr, True)
            mk_xsh(5)
            proj_lora(5, 'a1', 'a2', 1, a_scr, False)
            S.barrier()
            A.release()
            stage_end(12)
            A.top = xT_off
            A.marks = []
            A.mark()
            C = 64
            NEG_E05 = -float(np.exp(-0.5))
            wo_o = A.alloc([128, 8, DM], BF16)
            for c in range(8):
                S.dma('pool', lambda e, c=c: e.dma_start(out=wo_o[:, c, :], in_=I['w_out_odd'][c * 128:(c + 1) * 128, :]),
                      w=[('wo_o', c)])
            WOO = [('wo_o', c) for c in range(8)]
            gnp = A.alloc([64, 4, DM], F32)
            S.dma('sp', lambda e: e.dma_start(out=gnp[:, 0, :], in_=I['gn_g'].partition_broadcast(64)), w=[('gnp', 0)])
            S.dma('sp', lambda e: e.dma_start(out=gnp[:, 1, :], in_=I['gn_b'].partition_broadcast(64)), w=[('gnp', 1)])
            S.dma('sp', lambda e: e.dma_start(out=gnp[:, 2, :], in_=I['ln_g'][1:2, :].partition_broadcast(64)), w=[('gnp', 2)])
            S.dma('sp', lambda e: e.dma_start(out=gnp[:, 3, :], in_=I['ln_b'][1:2, :].partition_broadcast(64)), w=[('gnp', 3)])
            vec1 = A.alloc([64, 5, 16], F32)
            for i, nm in enumerate(['w0', 'a0', 'k_k', 'k_a', 'r_k']):
                S.dma('sp', lambda e, i=i, nm=nm: e.dma_start(out=vec1[:, i, :], in_=I[nm].rearrange("o (ph c) -> c (o ph)", c=64),
                                                              allow_slow_non_contiguous=True), w=['vec1'])
            m_lt = A.alloc([64, 64], F32)
            m_le = A.alloc([64, 64], F32)
            m_gt = A.alloc([64, 64], F32)
            blk1 = A.alloc([64, 64], F32)
            sel2 = A.alloc([64, 1], BF16)
            ones_t = A.alloc([64, 64], F32)

            def l1const(e):
                e.memset(ones_t, 1.0)
                e.memset(m_lt, 1.0)
                e.memset(m_le, 1.0)
                e.memset(m_gt, 1.0)
                e.affine_select(out=m_lt, in_=m_lt, pattern=[[1, 64]], compare_op=ALU.is_ge, fill=freg(e, 0.0), base=-1,
                                channel_multiplier=-1)
                e.affine_select(out=m_le, in_=m_le, pattern=[[1, 64]], compare_op=ALU.is_ge, fill=freg(e, 0.0), base=0,
                                channel_multiplier=-1)
                e.affine_select(out=m_gt, in_=m_gt, pattern=[[-1, 64]], compare_op=ALU.is_ge, fill=freg(e, 0.0), base=-1,
                                channel_multiplier=1)
                e.memset(blk1, 1.0)
                return e.memset(sel2, 1.0)
            S.pool(l1const, w=['l1c'])

            ST = A.alloc([64, 16, 64], F32)
            STb = A.alloc([64, 16, 64], BF16)
            fr = A.alloc([64, 16, C], F32)
            fk = A.alloc([64, 16, C], F32)
            fw = A.alloc([64, 16, C], F32)
            fa = A.alloc([64, 16, C], F32)
            cl = A.alloc([64, 16, C], F32)
            L1 = A.alloc([64, 16, C], F32)
            L2 = A.alloc([64, 16, C], F32)
            L3 = A.alloc([64, 16, C], F32)
            Lend = A.alloc([64, 16], F32)
            t_a = A.alloc([64, 16, C], F32)
            t_b = A.alloc([64, 16, C], F32)
            kap = fw
            kmd = A.alloc([64, 16, C], F32)
            kt_b = A.alloc([64, 16, C], BF16)
            bt_b = A.alloc([64, 16, C], BF16)
            ktl_b = A.alloc([64, 16, C], BF16)
            rt_b = A.alloc([64, 16, C], BF16)
            bh_f = A.alloc([64, 16, C], BF16)
            kh_f = A.alloc([64, 16, C], BF16)
            pr_b = A.alloc([64, 16, C], BF16)
            Bh = A.alloc([64, DM], BF16)
            Kh = A.alloc([64, DM], BF16)
            Vf2 = [A.alloc([64, DM], F32) for _ in range(2)]
            Vb = A.alloc([64, DM], BF16)
            Zf2 = [A.alloc([64, DM], F32) for _ in range(2)]
            gN = A.alloc([64, 16, 64], BF16)
            gNT = A.alloc([64, 16, 64], BF16)
            gAk = A.alloc([64, 16, 64], BF16)
            gBb = A.alloc([64, 16, 64], BF16)
            gBk = A.alloc([64, 16, 64], BF16)
            gX = [A.alloc([64, 16, 64], BF16) for _ in range(2)]
            gP = [A.alloc([64, 16, 64], BF16) for _ in range(2)]
            gPT = [A.alloc([64, 16, 64], BF16) for _ in range(2)]
            Rm = A.alloc([64, DM], BF16)
            Ub = A.alloc([64, DM], BF16)
            Yf = A.alloc([64, DM], F32)
            Yc = A.alloc([64, DM], F32)
            st1 = A.alloc([64, 16], F32)
            st2 = A.alloc([64, 16], F32)
            rkb2 = [A.alloc([64, 16], F32) for _ in range(2)]
            gb = A.alloc([64, DM], BF16)
            gT = A.alloc([128, 8, 64], BF16)
            xr2 = [A.alloc([64, DM], F32) for _ in range(2)]
            bst1 = A.alloc([64, 2, 6], F32)
            mv1 = A.alloc([64, 4], F32)
            wko = A.alloc([64, 16, 64], F32)
            stin2 = A.alloc([128, 8, 64], F32)

            print('L1B_TOP', A.top)

            def hp_rows(ap3, h):
                return ap3[:, h, :]

            def chunk(cols, ncol, first, xsrc_rows, yout, yrows, ck, par, post_seq=None):
                Vf, Zf, xr, rkb = Vf2[par], Zf2[par], xr2[par], rkb2[par]
                kVf, kZf, kxr, krkb = ('Vf', par), ('Zf', par), ('xr', par), ('rkb', par)
                pad = ncol < C
                if pad:
                    cols = cols - (C - 1)
                for (tile_, scr, nm) in ((fr, r_scr, 'fr'), (fk, k_scr, 'fk'), (fw, sw_scr, 'fw'), (fa, a_scr, 'fa')):
                    S.dma('sp', lambda e, tile_=tile_, scr=scr: e.dma_start(
                        out=tile_.rearrange("c (pr hp) t -> c pr hp t", hp=2),
                        in_=scr.rearrange("(hp c) pr t -> c pr hp t", hp=2)[:, :, :, cols:cols + C]), r=[('scr_all',)], w=[nm])
                    if pad:
                        S.pool(lambda e, tile_=tile_: e.memset(tile_[:, :, 0:C - 1], 0.0), r=[nm], w=[nm])
                S.dma('sp', lambda e: e.dma_start(out=Vf, in_=v_scr[cols:cols + C, :]), r=[('scr_all',)], w=[kVf])
                S.dma('sp', lambda e: e.dma_start(out=Zf, in_=z_scr[cols:cols + C, :]), r=[('scr_all',)], w=[kZf])
                S.dma('sp', lambda e: e.dma_start(out=xr, in_=x1_scr[cols:cols + C, :]), r=[('scr_all',)], w=[kxr])
                if pad:
                    S.pool(lambda e: e.memset(Vf[0:C - 1, :], 0.0), r=[kVf], w=[kVf])
                    S.pool(lambda e: e.memset(Zf[0:C - 1, :], 0.0), r=[kZf], w=[kZf])
                S.act(lambda e: e.copy(out=Vb, in_=Vf), r=[kVf], w=['Vb'])
                S.dve(lambda e: e.tensor_scalar(out=fw, in0=fw, scalar1=NEG_E05, scalar2=None, op0=ALU.mult), r=['fw'], w=['fw'])

                def scans(e):
                    ins = None
                    for pr_ in range(16):
                        ins = e.tensor_tensor_scan(out=cl[:, pr_, :], data0=ones_t, data1=fw[:, pr_, :], initial=0.0,
                                                   op0=ALU.mult, op1=ALU.add)
                    return ins
                S.dve(scans, r=['fw', 'l1c'], w=['cl'])
                S.act(lambda e: e.activation(out=L1, in_=cl, func=AF.Exp), r=['cl'], w=['L1'])
                S.act(lambda e: e.activation(out=L3, in_=cl, func=AF.Exp, scale=-1.0), r=['cl'], w=['L3'])
                S.dve(lambda e: e.tensor_tensor(out=t_a, in0=cl, in1=fw, op=ALU.subtract), r=['cl', 'fw'], w=['t_a'])
                S.act(lambda e: e.activation(out=L2, in_=t_a, func=AF.Exp), r=['t_a'], w=['L2'])
                S.dve(lambda e: e.tensor_copy(out=Lend, in_=L1[:, :, C - 1]), r=['L1'], w=['Lend'])
                S.dve(lambda e: e.tensor_tensor(out=kap, in0=fk, in1=vec1[:, 2, :].unsqueeze(2).broadcast_to([64, 16, C]),
                                                op=ALU.mult), r=['fk', 'vec1'], w=['fw'])
                S.dve(lambda e: e.tensor_tensor(out=t_b, in0=kap, in1=kap, op=ALU.mult), r=['fw'], w=['t_b'])
                def ssqm(e):
                    e.matmul(PS(0, 0, 64, 0, 512), lhsT=blk1, rhs=t_b[:, 0:8, :].rearrange("p a t -> p (a t)"), start=True, stop=True)
                    return e.matmul(PS(1, 0, 64, 0, 512), lhsT=blk1, rhs=t_b[:, 8:16, :].rearrange("p a t -> p (a t)"),
                                    start=True, stop=True)
                S.pe(ssqm, r=['t_b', 'l1c'], w=[pk(0), pk(1)])
                for hb in range(2):
                    S.dve(lambda e, hb=hb: e.tensor_scalar(out=t_b[:, hb * 8:(hb + 1) * 8, :].rearrange("p a t -> p (a t)"),
                                                           in0=PS(hb, 0, 64, 0, 512), scalar1=1e-24, scalar2=None, op0=ALU.max),
                          r=[pk(hb)], w=['t_b'])
                S.act(lambda e: e.activation(out=t_b, in_=t_b, func=AF.Ln), r=['t_b'], w=['t_b'])
                S.act(lambda e: e.activation(out=t_b, in_=t_b, func=AF.Exp, scale=-0.5), r=['t_b'], w=['t_b'])
                S.dve(lambda e: e.tensor_tensor(out=kap, in0=kap, in1=t_b, op=ALU.mult), r=['fw', 't_b'], w=['fw'])
                S.pool(lambda e: e.tensor_scalar(out=cl, in0=fa, scalar1=-1.0, scalar2=None, op0=ALU.add), r=['fa', 'cl', 'L2'], w=['cl'])
                S.pool(lambda e: e.tensor_tensor(out=cl, in0=cl, in1=vec1[:, 3, :].unsqueeze(2).broadcast_to([64, 16, C]),
                                                 op=ALU.mult), r=['cl', 'vec1'], w=['cl'])
                S.dve(lambda e: e.scalar_tensor_tensor(out=kmd, in0=cl, scalar=1.0, in1=fk, op0=ALU.add, op1=ALU.mult),
                      r=['cl', 'fk'], w=['kmd'])
                S.dve(lambda e: e.tensor_tensor(out=kt_b, in0=kap, in1=L2, op=ALU.mult), r=['fw', 'L2'], w=['kt_b'])
                S.dve(lambda e: e.tensor_tensor(out=t_a, in0=kap, in1=fa, op=ALU.mult), r=['fw', 'fa', 'kmd'], w=['t_a'])
                S.dve(lambda e: e.tensor_tensor(out=t_a, in0=t_a, in1=L3, op=ALU.mult), r=['t_a', 'L3'], w=['t_a'])
                S.act(lambda e: e.copy(out=bt_b, in_=t_a), r=['t_a'], w=['bt_b'])
                S.dve(lambda e: e.tensor_tensor(out=bh_f, in0=t_a, in1=Lend.unsqueeze(2).broadcast_to([64, 16, C]), op=ALU.mult),
                      r=['t_a', 'Lend'], w=['bh_f'])
                S.dve(lambda e: e.tensor_tensor(out=t_b, in0=kmd, in1=L3, op=ALU.mult), r=['kmd', 'L3', 'fw'], w=['t_b'])
                S.act(lambda e: e.copy(out=ktl_b, in_=t_b), r=['t_b'], w=['ktl_b'])
                S.dve(lambda e: e.tensor_tensor(out=kh_f, in0=t_b, in1=Lend.unsqueeze(2).broadcast_to([64, 16, C]), op=ALU.mult),
                      r=['t_b', 'Lend'], w=['kh_f'])
                S.pool(lambda e: e.tensor_tensor(out=rt_b, in0=fr, in1=L1, op=ALU.mult), r=['fr', 'L1'], w=['rt_b'])
                S.pool(lambda e: e.tensor_tensor(out=L2, in0=fr, in1=kmd, op=ALU.mult), r=['fr', 'kmd', 'L2', 'kt_b'], w=['L2'])
                S.pool(lambda e: e.tensor_tensor(out=pr_b, in0=L2, in1=vec1[:, 4, :].unsqueeze(2).broadcast_to([64, 16, C]),
                                                 op=ALU.mult), r=['L2', 'vec1'], w=['pr_b'])
                for (src, dst_, nm, bank) in ((bh_f, Bh, 'Bh', 2), (kh_f, Kh, 'Kh', 3)):
                    def trb(e, src=src, bank=bank):
                        ins = None
                        for h in range(16):
                            ins = e.transpose(out=PSB(bank, 0, 64, h * 64, (h + 1) * 64), in_=src[:, h, :],
                                              identity=ident_b[0:64, 0:64])
                        return ins
                    S.pe(trb, r=[nm.lower() + '_f' if False else ('bh_f' if nm == 'Bh' else 'kh_f'), 'ident_b'], w=[pk(bank)])
                    S.act(lambda e, dst_=dst_, bank=bank: e.copy(out=dst_, in_=PSB(bank, 0, 64, 0, 1024)), r=[pk(bank)], w=[nm])

                def rkm(e):
                    ins = None
                    for h in range(16):
                        ins = e.matmul(PS(1, 0, 64, 2 * h, 2 * h + 1), lhsT=pr_b[:, h, :], rhs=sel2, start=True, stop=True)
                    return ins
                S.pe(rkm, r=['pr_b', 'l1c'], w=[pk(1)])
                S.act(lambda e: e.copy(out=rkb, in_=PS(1, 0, 64, 0, 32).rearrange("p (h two) -> p h two", two=2)[:, :, 0]),
                      r=[pk(1)], w=[krkb])
                def gram(lt, rt, dst_, mask, banks, nm, deps):
                    def gm(e):
                        ins = None
                        for h in range(16):
                            ins = e.matmul(PS(banks[h // 8], 0, 64, (h % 8) * 64, (h % 8 + 1) * 64), lhsT=hp_rows(lt, h),
                                           rhs=hp_rows(rt, h), start=True, stop=True)
                        return ins
                    S.pe(gm, r=deps, w=[pk(banks[0]), pk(banks[1])])
                    for hb in range(2):
                        S.dve(lambda e, hb=hb: e.tensor_tensor(
                            out=dst_[:, hb * 8:(hb + 1) * 8, :], in0=PS(banks[hb], 0, 64, 0, 512).rearrange("p (h t) -> p h t", h=8),
                            in1=mask.unsqueeze(1).broadcast_to([64, 8, 64]), op=ALU.mult),
                            r=[pk(banks[hb]), 'l1c'], w=[(nm, hb)])
                gram(bt_b, kt_b, gN, m_lt, (4, 5), 'gN', ['bt_b', 'kt_b'])
                gram(kt_b, bt_b, gNT, m_gt, (6, 7), 'gNT', ['bt_b', 'kt_b'])
                gram(ktl_b, kt_b, gAk, m_lt, (4, 5), 'gAk', ['ktl_b', 'kt_b'])
                gram(bt_b, rt_b, gBb, m_le, (6, 7), 'gBb', ['bt_b', 'rt_b'])
                gram(ktl_b, rt_b, gBk, m_le, (4, 5), 'gBk', ['ktl_b', 'rt_b'])
                for hb in range(2):
                    S.dve(lambda e, hb=hb: e.scalar_tensor_tensor(
                        out=gX[0][:, hb * 8:(hb + 1) * 8, :], in0=gN[:, hb * 8:(hb + 1) * 8, :], scalar=-1.0,
                        in1=ident_b[0:64, 0:64].unsqueeze(1).broadcast_to([64, 8, 64]), op0=ALU.mult, op1=ALU.add),
                        r=[('gN', hb), 'ident_b'], w=[('gX', 0, hb)])
                Pc, PTc, Pk, PTk = gN, gNT, 'gN', 'gNT'
                xi = 0
                for lev in range(0 if pad else 5):
                    Pn, PTn = gP[lev % 2], gPT[lev % 2]
                    Pnk, PTnk = ('gP', lev % 2), ('gPT', lev % 2)

                    def sq(e, Pc=Pc, PTc=PTc):
                        ins = None
                        for h in range(16):
                            ins = e.matmul(PS(h // 8, 0, 64, (h % 8) * 64, (h % 8 + 1) * 64), lhsT=PTc[:, h, :], rhs=Pc[:, h, :],
                                           start=True, stop=True)
                        for h in range(16):
                            ins = e.matmul(PS(2 + h // 8, 0, 64, (h % 8) * 64, (h % 8 + 1) * 64), lhsT=Pc[:, h, :],
                                           rhs=PTc[:, h, :], start=True, stop=True)
                        return ins
                    S.pe(sq, r=[(Pk, 0), (Pk, 1), (PTk, 0), (PTk, 1)] if isinstance(Pk, str) else
                         [Pk + (0,), Pk + (1,), PTk + (0,), PTk + (1,)], w=[pk(0), pk(1), pk(2), pk(3)])
                    for hb in range(2):
                        S.act(lambda e, hb=hb, Pn=Pn: e.copy(out=Pn[:, hb * 8:(hb + 1) * 8, :].rearrange("p h t -> p (h t)"),
                                                             in_=PS(hb, 0, 64, 0, 512)), r=[pk(hb)], w=[Pnk + (hb,)])
                        S.dve(lambda e, hb=hb, PTn=PTn: e.tensor_copy(out=PTn[:, hb * 8:(hb + 1) * 8, :].rearrange("p h t -> p (h t)"),
                                                                      in_=PS(2 + hb, 0, 64, 0, 512)), r=[pk(2 + hb)], w=[PTnk + (hb,)])
                    Xo, Xn = gX[xi % 2], gX[(xi + 1) % 2]

                    def xm(e, PTn=PTn, Xo=Xo):
                        ins = None
                        for h in range(16):
                            ins = e.matmul(PS(4 + h // 8, 0, 64, (h % 8) * 64, (h % 8 + 1) * 64), lhsT=PTn[:, h, :], rhs=Xo[:, h, :],
                                           start=True, stop=True)
                        return ins
                    S.pe(xm, r=[PTnk + (0,), PTnk + (1,), ('gX', xi % 2, 0), ('gX', xi % 2, 1)], w=[pk(4), pk(5)])
                    for hb in range(2):
                        S.dve(lambda e, hb=hb, Xo=Xo, Xn=Xn: e.tensor_tensor(
                            out=Xn[:, hb * 8:(hb + 1) * 8, :].rearrange("p h t -> p (h t)"), in0=PS(4 + hb, 0, 64, 0, 512),
                            in1=Xo[:, hb * 8:(hb + 1) * 8, :].rearrange("p h t -> p (h t)"), op=ALU.add),
                            r=[pk(4 + hb), ('gX', xi % 2, hb)], w=[('gX', (xi + 1) % 2, hb)])
                    xi += 1
                    Pc, PTc, Pk, PTk = Pn, PTn, Pnk, PTnk
                Xf = gX[xi % 2]
                XFK = [('gX', xi % 2, 0), ('gX', xi % 2, 1)]
                yield 'pre'
                if first is not None:
                    first()

                def m1(e):
                    ins = None
                    for h in range(16):
                        o_ = PS(h // 8, 0, 64, (h % 8) * 64, (h % 8 + 1) * 64)
                        e.matmul(o_, lhsT=hp_rows(kt_b, h), rhs=hp_rows(STb, h), start=True, stop=False)
                        ins = e.matmul(o_, lhsT=gAk[:, h, :], rhs=Vb[:, h * 64:(h + 1) * 64], start=False, stop=True)
                    return ins
                S.pe(m1, r=['kt_b', 'STb', ('gAk', 0), ('gAk', 1), 'Vb'], w=[pk(0), pk(1)])
                for hb in range(2):
                    S.act(lambda e, hb=hb: e.activation(out=Rm[:, hb * 512:(hb + 1) * 512], in_=PS(hb, 0, 64, 0, 512),
                                                        func=AF.Copy, scale=-1.0), r=[pk(hb)], w=[('Rm', hb)])

                def m3(e):
                    ins = None
                    for h in range(16):
                        ins = e.matmul(PS(2 + h // 8, 0, 64, (h % 8) * 64, (h % 8 + 1) * 64), lhsT=Xf[:, h, :],
                                       rhs=Rm[:, h * 64:(h + 1) * 64], start=True, stop=True)
                    return ins
                S.pe(m3, r=XFK + [('Rm', 0), ('Rm', 1)], w=[pk(2), pk(3)])
                for hb in range(2):
                    S.act(lambda e, hb=hb: e.copy(out=Ub[:, hb * 512:(hb + 1) * 512], in_=PS(2 + hb, 0, 64, 0, 512)),
                          r=[pk(2 + hb)], w=[('Ub', hb)])

                def m4(e):
                    ins = None
                    for h in range(16):
                        o_ = PS(4 + h // 8, 0, 64, (h % 8) * 64, (h % 8 + 1) * 64)
                        e.matmul(o_, lhsT=hp_rows(rt_b, h), rhs=hp_rows(STb, h), start=True, stop=False)
                        e.matmul(o_, lhsT=gBb[:, h, :], rhs=Ub[:, h * 64:(h + 1) * 64], start=False, stop=False)
                        ins = e.matmul(o_, lhsT=gBk[:, h, :], rhs=Vb[:, h * 64:(h + 1) * 64], start=False, stop=True)
                    return ins
                S.pe(m4, r=['rt_b', 'STb', ('gBb', 0), ('gBb', 1), ('gBk', 0), ('gBk', 1), ('Ub', 0), ('Ub', 1), 'Vb'],
                     w=[pk(4), pk(5)])
                for hb in range(2):
                    S.act(lambda e, hb=hb: e.copy(out=Yf[:, hb * 512:(hb + 1) * 512], in_=PS(4 + hb, 0, 64, 0, 512)),
                          r=[pk(4 + hb)], w=[('Yf', hb)])

                def m5(e):
                    ins = None
                    for h in range(16):
                        o_ = PS(6 + h // 8, 0, 64, (h % 8) * 64, (h % 8 + 1) * 64)
                        e.matmul(o_, lhsT=Bh[:, h * 64:(h + 1) * 64], rhs=Ub[:, h * 64:(h + 1) * 64], start=True, stop=False)
                        ins = e.matmul(o_, lhsT=Kh[:, h * 64:(h + 1) * 64], rhs=Vb[:, h * 64:(h + 1) * 64], start=False, stop=True)
                    return ins
                S.pe(m5, r=['Bh', 'Kh', ('Ub', 0), ('Ub', 1), 'Vb'], w=[pk(6), pk(7)])
                S.dve(lambda e: e.tensor_tensor(out=ST, in0=ST, in1=Lend.unsqueeze(2).broadcast_to([64, 16, 64]), op=ALU.mult),
                      r=['ST', 'Lend', 'STb'], w=['ST'])
                for hb in range(2):
                    S.dve(lambda e, hb=hb: e.tensor_tensor(out=ST[:, hb * 8:(hb + 1) * 8, :], in0=ST[:, hb * 8:(hb + 1) * 8, :],
                                                           in1=PS(6 + hb, 0, 64, 0, 512).rearrange("p (a v) -> p a v", a=8), op=ALU.add),
                          r=['ST', pk(6 + hb)], w=['ST'])
                S.act(lambda e: e.copy(out=STb, in_=ST), r=['ST'], w=['STb'])
                if post_seq is not None:
                    post_seq()
                yield 'seq'
                Y3 = Yf.rearrange("p (h v) -> p h v", h=16)
                Yc3 = Yc.rearrange("p (h v) -> p h v", h=16)
                YK = [('Yf', 0), ('Yf', 1)]
                S.dve(lambda e: e.tensor_reduce(out=st1, in_=Y3, axis=AX.X, op=ALU.add), r=YK, w=['st1'])
                S.dve(lambda e: e.tensor_scalar(out=st1, in0=st1, scalar1=1.0 / 64.0, scalar2=None, op0=ALU.mult), r=['st1'], w=['st1'])
                S.dve(lambda e: e.tensor_tensor(out=Yc3, in0=Y3, in1=st1.unsqueeze(2).broadcast_to([64, 16, 64]), op=ALU.subtract),
                      r=YK + ['st1'], w=['Yc'])
                S.pool(lambda e: e.tensor_tensor(out=Yf, in0=Yc, in1=Yc, op=ALU.mult), r=['Yc'] + YK, w=YK)
                S.dve(lambda e: e.tensor_reduce(out=st2, in_=Y3, axis=AX.X, op=ALU.add), r=YK, w=['st2'])
                S.dve(lambda e: e.tensor_scalar(out=st2, in0=st2, scalar1=1.0 / 64.0, scalar2=64e-5, op0=ALU.mult, op1=ALU.add),
                      r=['st2'], w=['st2'])
                S.act(lambda e: e.activation(out=st2, in_=st2, func=AF.Ln), r=['st2'], w=['st2'])
                S.act(lambda e: e.activation(out=st2, in_=st2, func=AF.Exp, scale=-0.5), r=['st2'], w=['st2'])
                S.dve(lambda e: e.tensor_tensor(out=Yc3, in0=Yc3, in1=st2.unsqueeze(2).broadcast_to([64, 16, 64]), op=ALU.mult),
                      r=['Yc', 'st2'], w=['Yc'])
                S.dve(lambda e: e.tensor_tensor(out=Yc, in0=Yc, in1=gnp[:, 0, :], op=ALU.mult), r=['Yc', ('gnp', 0)], w=['Yc'])
                S.pool(lambda e: e.tensor_tensor(out=Yc, in0=Yc, in1=gnp[:, 1, :], op=ALU.add), r=['Yc', ('gnp', 1)], w=['Yc'])
                S.dve(lambda e: e.tensor_tensor(out=Y3, in0=Vf.rearrange("p (h v) -> p h v", h=16),
                                                in1=rkb.unsqueeze(2).broadcast_to([64, 16, 64]), op=ALU.mult),
                      r=[kVf, krkb] + YK, w=YK)
                S.pool(lambda e: e.tensor_tensor(out=Yc, in0=Yc, in1=Yf, op=ALU.add), r=['Yc'] + YK, w=['Yc'])
                S.dve(lambda e: e.tensor_tensor(out=gb, in0=Yc, in1=Zf, op=ALU.mult), r=['Yc', kZf], w=['gb'])
                yield 'tailA'
                def trg(e):
                    ins = None
                    for c in range(8):
                        ins = e.transpose(out=PSB(7, 0, 128, c * 64, (c + 1) * 64), in_=gb[:, c * 128:(c + 1) * 128],
                                          identity=ident_b[0:64, 0:64])
                    return ins
                S.pe(trg, r=['gb', 'ident_b'], w=[pk(7)])
                S.act(lambda e: e.copy(out=gT.rearrange("p c t -> p (c t)"), in_=PSB(7, 0, 128, 0, 512)), r=[pk(7)], w=['gT'])

                def mo(e):
                    ins = None
                    for cb in range(2):
                        for c in range(8):
                            ins = e.matmul(PS(cb, 0, 64, 0, 512), lhsT=gT[:, c, :], rhs=wo_o[:, c, cb * 512:(cb + 1) * 512],
                                           start=(c == 0), stop=(c == 7))
                    return ins
                S.pe(mo, r=['gT'] + WOO, w=[pk(0), pk(1)])
                for cb in range(2):
                    S.dve(lambda e, cb=cb: e.scalar_tensor_tensor(
                        out=xr[:, cb * 512:(cb + 1) * 512], in0=xr[:, cb * 512:(cb + 1) * 512], scalar=ALPHA,
                        in1=PS(cb, 0, 64, 0, 512), op0=ALU.mult, op1=ALU.add), r=[kxr, pk(cb)], w=[kxr])
                    S.dve(lambda e, cb=cb: e.bn_stats(out=bst1[:, cb, :], in_=xr[:, cb * 512:(cb + 1) * 512]), r=[kxr], w=[('bst1', cb)])
                S.dve(lambda e: e.bn_aggr(out=mv1[:, 0:2], in_=bst1.rearrange("p a b -> p (a b)")), r=[('bst1', 0), ('bst1', 1)],
                      w=['mv1'])
                S.dve(lambda e: e.tensor_scalar(out=mv1[:, 2:3], in0=mv1[:, 1:2], scalar1=1e-5, scalar2=None, op0=ALU.add),
                      r=['mv1'], w=['mv1b'])
                S.act(lambda e: e.activation(out=mv1[:, 2:3], in_=mv1[:, 2:3], func=AF.Ln), r=['mv1b'], w=['mv1b'])
                S.act(lambda e: e.activation(out=mv1[:, 2:3], in_=mv1[:, 2:3], func=AF.Exp, scale=-0.5), r=['mv1b'], w=['mv1b'])
                S.dve(lambda e: e.tensor_scalar(out=xr, in0=xr, scalar1=mv1[:, 0:1], scalar2=mv1[:, 2:3], op0=ALU.subtract,
                                                op1=ALU.mult), r=[kxr, 'mv1', 'mv1b'], w=[kxr])
                S.dve(lambda e: e.tensor_tensor(out=xr, in0=xr, in1=gnp[:, 2, :], op=ALU.mult), r=[kxr, ('gnp', 2)], w=[kxr])
                S.pool(lambda e: e.tensor_tensor(out=xr, in0=xr, in1=gnp[:, 3, :], op=ALU.add), r=[kxr, ('gnp', 3)], w=[kxr])
                S.dma('sp', lambda e: e.dma_start(out=yout, in_=(xr[C - 1:C, :] if pad else xr)), r=[kxr])

            def write_state(dst):
                def trs(e):
                    ins = None
                    for h in range(16):
                        ins = e.transpose(out=PS(6 + h // 8, 0, 64, (h % 8) * 64, (h % 8 + 1) * 64), in_=ST[:, h, :],
                                          identity=ident_f[0:64, 0:64])
                    return ins
                S.pe(trs, r=['ST', 'ident_f'], w=[pk(6), pk(7)])
                for hb in range(2):
                    S.act(lambda e, hb=hb: e.copy(out=wko[:, hb * 8:(hb + 1) * 8, :].rearrange("p a b -> p (a b)"),
                                                  in_=PS(6 + hb, 0, 64, 0, 512)), r=[pk(6 + hb)], w=['wko'])
                S.dma('sp', lambda e: e.dma_start(out=dst.rearrange("(h v) k -> v h k", h=16), in_=wko), r=['wko'])

            def zero_state():
                S.pool(lambda e: e.memset(ST, 0.0), w=['ST'])
                S.pool(lambda e: e.memset(STb, 0.0), w=['STb'])

            def mk_load_state(s):
                def load_state():
                    S.dma('sp', lambda e: e.dma_start(
                        out=stin2, in_=I['state_wkv'][s * 1024:(s + 1) * 1024, :].rearrange("(pr p) k -> p pr k", pr=8)), w=['stin'])

                    def tri(e):
                        ins = None
                        for pr_ in range(8):
                            ins = e.transpose(out=PS(6 + pr_ // 4, 0, 64, (pr_ % 4) * 128, (pr_ % 4 + 1) * 128), in_=stin2[:, pr_, :],
                                              identity=ident_f)
                        return ins
                    S.pe(tri, r=['stin', 'ident_f'], w=[pk(6), pk(7)])
                    for hb in range(2):
                        S.dve(lambda e, hb=hb: e.tensor_copy(out=ST[:, hb * 8:(hb + 1) * 8, :].rearrange("p a v -> p (a v)"),
                                                             in_=PS(6 + hb, 0, 64, 0, 512)), r=[pk(6 + hb)], w=['ST'])
                    S.act(lambda e: e.copy(out=STb, in_=ST), r=['ST'], w=['STb'])
                return load_state

            gens = []
            nchk = T // C
            for ci in range(nchk):
                gens.append(chunk(ci * C, C, zero_state if ci == 0 else None, x1_scr[ci * C:(ci + 1) * C, :],
                                  O['y_prompt'][ci * C:(ci + 1) * C, :], C, ci, ci % 2,
                                  (lambda: write_state(O['wkv_p'])) if ci == nchk - 1 else None))
            for s in range(NS):
                gens.append(chunk(T + s, 1, mk_load_state(s), x1_scr[T + s:T + s + 1, :], O['y_sample'][s:s + 1, :], 1, 100 + s,
                                  (nchk + s) % 2, (lambda s=s: write_state(O['wkv_s'][s * 1024:(s + 1) * 1024, :]))))
            next(gens[0])
            next(gens[0])
            for i_ in range(1, len(gens)):
                next(gens[i_])
                next(gens[i_ - 1])
                next(gens[i_])
                for _ in gens[i_ - 1]:
                    pass
            next(gens[-1])
            for _ in gens[-1]:
                pass
            S.barrier()
            A.release()
            stage_end(13)
        except _Stop:
            pass
        S.analyze()
        S.emit(nc, sems, dsems)
        print("arena peak bytes", A.peak, "ops", len(S.ops))
    return nc


_NC_CACHE = {}


def kernel(**inputs):
    inp = {k: np.asarray(v) for k, v in inputs.items()}
    if 'nc' not in _NC_CACHE:
        _NC_CACHE['nc'] = build()
    nc = _NC_CACHE['nc']
    f = np.ascontiguousarray
    shared = {
        "cache_cmp_k": f(inp['cache_cmp_kv'][0][:, :, 0].reshape(NPOOL * 128, 128)),
        "cache_cmp_v": f(inp['cache_cmp_kv'][0][:, :, 1].reshape(NPOOL * 128, 128)),
        "cache_sel_k": f(inp['cache_sel_kv'][0][:, :, 0].reshape(NPOOL * 128, 128)),
        "cache_sel_v": f(inp['cache_sel_kv'][0][:, :, 1].reshape(NPOOL * 128, 128)),
        "w_in": f(inp['w_in_even'][0]),
        "conv_w": f(inp['conv_w'][0].reshape(31, 512)),
        "conv_b": f(inp['conv_b'].reshape(1, 512)),
        "conv_ln_g": f(inp['conv_ln_g'].reshape(1, 512)),
        "conv_ln_b": f(inp['conv_ln_b'].reshape(1, 512)),
        "wk_cmp": f(inp['wk_cmp'][0]),
        "wv_cmp": f(inp['wv_cmp'][0]),
        "w_out_even": f(inp['w_out_even'][0]),
        "mu_c": f(inp['mu_c'][0]),
        "w_rkvz": f(inp['w_rkvz'][0].reshape(4 * DM, DM)),
        "w0": f(inp['w0'].reshape(1, DM)),
        "w1": f(inp['w1'][0]),
        "w2": f(inp['w2'][0]),
        "a0": f(inp['a0'].reshape(1, DM)),
        "a1": f(inp['a1'][0]),
        "a2": f(inp['a2'][0]),
        "k_k": f(inp['k_k'].reshape(1, DM)),
        "k_a": f(inp['k_a'].reshape(1, DM)),
        "r_k": f(inp['r_k'].reshape(1, DM)),
        "gn_g": f(inp['gn_g'].reshape(1, DM)),
        "gn_b": f(inp['gn_b'].reshape(1, DM)),
        "w_out_odd": f(inp['w_out_odd'][0]),
        "ln_g": f(inp['ln_g']),
        "ln_b": f(inp['ln_b']),
    }
    in_maps = []
    for c in range(NCORES):
        s0, s1 = c * NS, (c + 1) * NS
        m = dict(shared)
        m["xp"] = f(inp['x_prompt'][c])
        m["xs"] = f(inp['x_sample'][s0:s1, 0, :])
        m["cache_win"] = f(inp['cache_win_kv'][0, s0:s1].reshape(NS * 512, 256))
        m["state_conv"] = f(inp['state_conv'][0, s0:s1].reshape(NS * 30, 512))
        m["state_wkv"] = f(inp['state_wkv'][0, s0:s1].reshape(NS * 16 * 64, 64))
        m["state_shift"] = f(inp['state_shift'][0, s0:s1])
        m["page_table"] = f(inp['page_table'][s0:s1].astype(np.int32))
        in_maps.append(m)
    res = run_bass_kernel_spmd(nc, in_maps, core_ids=list(range(NCORES)), trace=bool(os.environ.get('KDEV_TRACE')))
    if os.environ.get('KDEV_TRACE'):
        print('EXEC_TIME_NS', res.exec_time_ns)
    R = res.results

    def cat(name):
        return np.stack([np.asarray(R[c][name]) for c in range(NCORES)], axis=0)
    y_prompt = cat("y_prompt").reshape(8, T, DM)
    y_sample = cat("y_sample").reshape(32, 1, DM)
    cmp_p = cat("cmp_p").reshape(1, 8, T, 2, 2, 64)
    cmp_s = cat("cmp_s").reshape(1, 32, 1, 2, 2, 64)
    sel_p = cat("sel_p").reshape(1, 8, T, 2, 2, 64)
    sel_s = cat("sel_s").reshape(1, 32, 1, 2, 2, 64)
    win_p = cat("win_p").reshape(1, 8, 512, 2, 2, 64)
    win_s = cat("win_s").reshape(1, 32, 512, 2, 2, 64)
    conv_p = cat("conv_p").reshape(1, 8, 30, 512)
    conv_s = cat("conv_s").reshape(1, 32, 30, 512)
    wkv_p = cat("wkv_p").reshape(1, 8, 16, 64, 64)
    wkv_s = cat("wkv_s").reshape(1, 32, 16, 64, 64)
    shift_p = cat("shift_p").reshape(1, 8, DM)
    shift_s = cat("shift_s").reshape(1, 32, DM)
    return (y_prompt, y_sample, cmp_p, cmp_s, sel_p, sel_s, win_p, win_s, conv_p, conv_s,
            wkv_p, wkv_s, shift_p, shift_s)
```

```python
import numpy as np
import concourse.bass as bass
import concourse.mybir as mybir
from concourse.bass_utils import run_bass_kernel_spmd

F32 = mybir.dt.float32
BF16 = mybir.dt.bfloat16
I32 = mybir.dt.int32
AF = mybir.ActivationFunctionType
ALU = mybir.AluOpType
AX = mybir.AxisListType

NCORES = 8
T = 2048
NS = 4
TT = T + NS
DM = 1024
import os
NPOOL = int(os.environ.get('KDEV_NPOOL', '2560'))
STAGE = int(os.environ.get('KDEV_STAGE', '99'))


class _Stop(Exception):
    pass


def stage_end(k):
    if STAGE == k:
        raise _Stop()
ENGS = ('pe', 'act', 'dve', 'pool', 'sp')
SAME_ENG_DIST = 3
NDS = {'sp': 8, 'act': 4, 'pool': 6}


class Op:
    __slots__ = ('eng', 'fn', 'r', 'w', 'dma', 'deps', 'signal', 'count', 'dsem', 'dcount',
                 'dprev', 'waits', 'idx', 'eidx', 'bar', 'need')


class Sched:
    def __init__(self):
        self.ops = []
        self.nbar = 0

    def add(self, eng, fn, r=(), w=(), dma=False):
        op = Op()
        op.eng, op.fn, op.r, op.w, op.dma = eng, fn, tuple(r), tuple(w), dma
        op.signal = False
        op.bar = None
        op.count = 0
        self.ops.append(op)
        return op

    def pe(self, fn, r=(), w=()):
        return self.add('pe', fn, r, w)

    def act(self, fn, r=(), w=()):
        return self.add('act', fn, r, w)

    def dve(self, fn, r=(), w=()):
        return self.add('dve', fn, r, w)

    def pool(self, fn, r=(), w=()):
        return self.add('pool', fn, r, w)

    def dma(self, q, fn, r=(), w=()):
        return self.add(q, fn, r, w, dma=True)

    def barrier(self):
        self.nbar += 1
        for e in ENGS:
            op = self.add(e, None)
            op.bar = self.nbar

    def analyze(self):
        ops = self.ops
        last_w = {}
        rd = {}
        eng_ops = {e: [] for e in ENGS}
        last_on_eng = {}
        all_dmas = []
        cur_bar = None
        snap = None
        for i, op in enumerate(ops):
            op.idx = i
            deps = set()
            if op.bar is not None:
                if cur_bar != op.bar:
                    cur_bar = op.bar
                    snap = (dict(last_on_eng), list(all_dmas))
                    all_dmas = []
                for f, j in snap[0].items():
                    if f != op.eng and ops[j].bar is None:
                        deps.add(j)
                deps.update(snap[1])
            else:
                for k in op.r:
                    j = last_w.get(k)
                    if j is not None:
                        deps.add(j)
                for k in op.w:
                    j = last_w.get(k)
                    if j is not None:
                        deps.add(j)
                    rr = rd.get(k)
                    if rr:
                        deps.update(rr[0].values())
                        deps.update(rr[1])
                for k in op.r:
                    rr = rd.setdefault(k, ({}, []))
                    if op.dma:
                        rr[1].append(i)
                    else:
                        rr[0][op.eng] = i
                for k in op.w:
                    last_w[k] = i
                    rd[k] = ({}, [])
            deps.discard(i)
            op.deps = deps
            op.eidx = len(eng_ops[op.eng])
            eng_ops[op.eng].append(op)
            last_on_eng[op.eng] = i
            if op.dma:
                all_dmas.append(i)
        for op in ops:
            need = []
            for j in op.deps:
                d = ops[j]
                if d.dma:
                    need.append(j)
                elif d.eng == op.eng:
                    if op.eng == 'pe' and not op.dma:
                        continue
                    if op.eidx - d.eidx > SAME_ENG_DIST:
                        continue
                    need.append(j)
                else:
                    need.append(j)
            op.need = need
            for j in need:
                if not ops[j].dma:
                    ops[j].signal = True
        self.eng_ops = eng_ops

    def emit(self, nc, sems, dsems):
        ops = self.ops
        eng_ops = self.eng_ops
        for e in ENGS:
            cnt = 0
            for op in eng_ops[e]:
                if op.signal:
                    cnt += 1
                    op.count = cnt
        print('SEMCOUNTS', {e: max([op.count for op in eng_ops[e]] + [0]) for e in ENGS}, {e: len(eng_ops[e]) for e in ENGS})
        finals = {}
        for q, pool in dsems.items():
            uses = [0] * len(pool)
            k = 0
            for op in eng_ops[q]:
                if op.dma:
                    s = k % len(pool)
                    k += 1
                    op.dsem = pool[s]
                    op.dprev = 16 * uses[s]
                    uses[s] += 1
                    op.dcount = 16 * uses[s]
                    finals[id(pool[s])] = (pool[s], op.dcount)
        for e in ENGS:
            waited = {}
            for op in eng_ops[e]:
                ws = []
                for j in sorted(op.need):
                    d = ops[j]
                    if d.dma:
                        sem, val = d.dsem, d.dcount
                    else:
                        sem, val = sems[d.eng], d.count
                    if waited.get(id(sem), 0) >= val:
                        continue
                    waited[id(sem)] = val
                    ws.append((sem, val))
                if op.dma and op.dprev > 0:
                    if waited.get(id(op.dsem), 0) < op.dprev:
                        waited[id(op.dsem)] = op.dprev
                        ws.append((op.dsem, op.dprev))
                op.waits = ws

        def make(e):
            def body(eo):
                for op in eng_ops[e]:
                    for (sem, v) in op.waits:
                        eo.wait_ge(sem, v)
                    if op.fn is not None:
                        ins = op.fn(eo)
                        if op.dma:
                            ins.then_inc(op.dsem, 16)
                        elif op.signal:
                            ins.then_inc(sems[e], 1)
                if e == 'sp':
                    for (sem, v) in finals.values():
                        eo.wait_ge(sem, v)
            return body

        with nc.Block() as block:
            block.tensor(make('pe'))
            block.scalar(make('act'))
            block.vector(make('dve'))
            block.gpsimd(make('pool'))
            block.sync(make('sp'))


class Arena:
    def __init__(self, t32, nbytes):
        self.t32 = t32
        self.tbf = t32.bitcast(BF16)
        self.ti32 = t32.bitcast(I32)
        self.nbytes = nbytes
        self.top = 0
        self.marks = []
        self.peak = 0

    def alloc(self, shape, dt):
        es = 2 if dt == BF16 else 4
        n = 1
        for s in shape[1:]:
            n *= s
        nb = (n * es + 63) // 64 * 64
        off = self.top
        self.top += nb
        self.peak = max(self.peak, self.top)
        assert self.top <= self.nbytes, ("SBUF arena overflow", self.top, self.nbytes)
        base = {BF16: self.tbf, F32: self.t32, I32: self.ti32}[dt]
        v = base[0:shape[0], off // es: off // es + n]
        if len(shape) > 2:
            names = "abcdefg"[:len(shape) - 1]
            pat = "p (%s) -> p %s" % (" ".join(names), " ".join(names))
            v = v.rearrange(pat, **{names[i]: shape[1 + i] for i in range(len(shape) - 2)})
        return v

    def mark(self):
        self.marks.append(self.top)

    def release(self):
        self.top = self.marks.pop()


OUT_SPECS = [
    ("y_prompt", [T, DM]),
    ("y_sample", [NS, DM]),
    ("cmp_p", [T, 256]),
    ("cmp_s", [NS, 256]),
    ("sel_p", [T, 256]),
    ("sel_s", [NS, 256]),
    ("win_p", [512, 256]),
    ("win_s", [NS * 512, 256]),
    ("conv_p", [30, 512]),
    ("conv_s", [NS * 30, 512]),
    ("wkv_p", [16 * 64, 64]),
    ("wkv_s", [NS * 16 * 64, 64]),
    ("shift_p", [1, DM]),
    ("shift_s", [NS, DM]),
]

IN_SPECS = [
    ("xp", [T, DM], F32),
    ("xs", [NS, DM], F32),
    ("cache_cmp_k", [NPOOL * 128, 128], F32),
    ("cache_cmp_v", [NPOOL * 128, 128], F32),
    ("cache_sel_k", [NPOOL * 128, 128], F32),
    ("cache_sel_v", [NPOOL * 128, 128], F32),
    ("cache_win", [NS * 512, 256], F32),
    ("state_conv", [NS * 30, 512], F32),
    ("state_wkv", [NS * 16 * 64, 64], F32),
    ("state_shift", [NS, DM], F32),
    ("page_table", [NS, 64], I32),
    ("w_in", [DM, 3352], F32),
    ("conv_w", [31, 512], F32),
    ("conv_b", [1, 512], F32),
    ("conv_ln_g", [1, 512], F32),
    ("conv_ln_b", [1, 512], F32),
    ("wk_cmp", [32, 2], F32),
    ("wv_cmp", [32, 2], F32),
    ("w_out_even", [DM, DM], F32),
    ("mu_c", [6, DM], F32),
    ("w_rkvz", [4 * DM, DM], F32),
    ("w0", [1, DM], F32),
    ("w1", [DM, 64], F32),
    ("w2", [64, DM], F32),
    ("a0", [1, DM], F32),
    ("a1", [DM, 64], F32),
    ("a2", [64, DM], F32),
    ("k_k", [1, DM], F32),
    ("k_a", [1, DM], F32),
    ("r_k", [1, DM], F32),
    ("gn_g", [1, DM], F32),
    ("gn_b", [1, DM], F32),
    ("w_out_odd", [DM, DM], F32),
    ("ln_g", [2, DM], F32),
    ("ln_b", [2, DM], F32),
]

C_AVAL, C_AGLU, C_ZA, C_Q, C_KV, C_G3, C_ZB = 0, 512, 1024, 1536, 2048, 2816, 2840


def build(stage=99):
    nc = bass.Bass("TRN2", target_bir_lowering=False)
    I = {}
    for name, shape, dt in IN_SPECS:
        I[name] = nc.dram_tensor(name, shape, dt, kind="ExternalInput").ap()
    O = {}
    for name, shape in OUT_SPECS:
        O[name] = nc.dram_tensor(name, shape, F32, kind="ExternalOutput").ap()

    x1_scr = nc.dram_tensor("x1_scr", [TT, DM], F32, kind="Internal").ap()
    r_scr = nc.dram_tensor("r_scr", [128, 8, TT], F32, kind="Internal").ap()
    k_scr = nc.dram_tensor("k_scr", [128, 8, TT], F32, kind="Internal").ap()
    sw_scr = nc.dram_tensor("sw_scr", [128, 8, TT], F32, kind="Internal").ap()
    a_scr = nc.dram_tensor("a_scr", [128, 8, TT], F32, kind="Internal").ap()
    v_scr = nc.dram_tensor("v_scr", [TT, DM], F32, kind="Internal").ap()
    z_scr = nc.dram_tensor("z_scr", [TT, DM], F32, kind="Internal").ap()
    S = Sched()
    ARENA_BYTES = 176 * 1024
    from contextlib import ExitStack
    with ExitStack() as st:
        arena_t = st.enter_context(nc.sbuf_tensor("arena", [128, ARENA_BYTES // 4], F32))
        ps_t = st.enter_context(nc.psum_tensor("psum", [128, 4096], F32))
        sems = {e: st.enter_context(nc.semaphore("sem_" + e)) for e in ENGS}
        dsems = {q: [st.enter_context(nc.semaphore("dsem_%s%d" % (q, i))) for i in range(n)]
                 for q, n in NDS.items()}
        A = Arena(arena_t, ARENA_BYTES)
        ps_bf = ps_t.bitcast(BF16)
        try:

            def PS(b, p0=0, p1=128, c0=0, c1=512):
                return ps_t[p0:p1, b * 512 + c0: b * 512 + c1]

            def PSB(b, p0=0, p1=128, c0=0, c1=1024):
                return ps_bf[p0:p1, b * 1024 + c0: b * 1024 + c1]

            def pk(b):
                return ('ps', b)

            FR = {}

            def freg(e, v):
                if v not in FR:
                    FR[v] = e.to_reg(float(v))
                return FR[v]

            ident_f = A.alloc([128, 128], F32)
            ident_b = A.alloc([128, 128], BF16)
            ones_f = A.alloc([128, 128], F32)

            def mk_ident(e):
                e.memset(ones_f, 1.0)
                return e.affine_select(out=ident_f, in_=ones_f, pattern=[[-1, 128]], compare_op=ALU.is_equal,
                                       fill=freg(e, 0.0), base=0, channel_multiplier=1)
            S.pool(mk_ident, w=['ident_f', 'ones_f'])
            S.dve(lambda e: e.tensor_copy(out=ident_b, in_=ident_f), r=['ident_f'], w=['ident_b'])

            xT_off = A.top
            xT = A.alloc([128, 8, TT], BF16)
            xT_raw32 = arena_t[:, xT_off // 4: xT_off // 4 + 8208]
            L1_BASE = A.top
            yaT = A.alloc([128, 4, TT], BF16)
            Vsel_off = A.top
            Vsel = A.alloc([128, 16, 128], BF16)
            Vwin_off = A.top
            Vwin = A.alloc([128, 16, 128], BF16)
            Vsel_raw32 = arena_t[:, Vsel_off // 4: Vsel_off // 4 + 1024]
            Vwin_raw32 = arena_t[:, Vwin_off // 4: Vwin_off // 4 + 1024]
            kvsv = A.alloc([NS, 2, 128], F32)
            qs_f = A.alloc([68, 2, 4, NS], F32)
            ksn = A.alloc([68, 2, 2, NS], F32)
            gs_s = A.alloc([24, NS], BF16)
            w_in_t = I['w_in'].rearrange("(c p) f -> p c f", p=128)
            A.mark()
            xst = [A.alloc([128, DM], BF16) for _ in range(2)]
            xp_t = I['xp'].rearrange("(n p) d -> n p d", p=128)
            for n in range(16):
                b = n % 2
                S.dma('pool', lambda e, n=n, b=b: e.dma_start(out=xst[b], in_=xp_t[n]), w=[('xst', b)])
                bank = n % 2

                def tr(e, b=b, bank=bank):
                    ins = None
                    for c in range(8):
                        ins = e.transpose(out=PSB(bank, 0, 128, c * 128, (c + 1) * 128),
                                          in_=xst[b][:, c * 128:(c + 1) * 128], identity=ident_b)
                    return ins
                S.pe(tr, r=[('xst', b), 'ident_b'], w=[pk(bank)])
                S.dve(lambda e, n=n, bank=bank: e.tensor_copy(
                    out=xT[:, :, n * 128:(n + 1) * 128],
                    in_=PSB(bank).rearrange("p (c t) -> p c t", c=8)),
                    r=[pk(bank)], w=[('xT', n)])
            S.dma('pool', lambda e: e.dma_start(out=xst[0][0:NS, :], in_=I['xs']), w=[('xst', 0)])

            def trs(e):
                ins = None
                for c in range(8):
                    ins = e.transpose(out=PSB(0, 0, 128, c * NS, (c + 1) * NS),
                                      in_=xst[0][0:NS, c * 128:(c + 1) * 128], identity=ident_b[0:NS, 0:NS])
                return ins
            S.pe(trs, r=[('xst', 0), 'ident_b'], w=[pk(0)])
            S.dve(lambda e: e.tensor_copy(out=xT[:, :, T:TT],
                                          in_=PSB(0, 0, 128, 0, 8 * NS).rearrange("p (c t) -> p c t", c=8)),
                  r=[pk(0)], w=[('xT', 16)])
            XT_ALL = [('xT', n) for n in range(17)]
            S.barrier()
            A.release()
            stage_end(1)

            A.mark()
            wkv = A.alloc([128, 8, 768], BF16)
            kvst = [A.alloc([128, 768], F32) for _ in range(2)]
            winb = [A.alloc([128, 4, 256], F32) for _ in range(2)]
            for c in range(8):
                S.dma('pool', lambda e, c=c: e.dma_start(out=wkv[:, c, :], in_=w_in_t[:, c, C_KV:C_KV + 768]),
                      w=[('wkv', c)])
            WKV = [('wkv', c) for c in range(8)]
            for n in range(16):
                bA, bB = 4 + (n % 2) * 2, 5 + (n % 2) * 2

                def mmkv(e, n=n, bA=bA, bB=bB):
                    ins = None
                    for c in range(8):
                        ins = e.matmul(PS(bA), lhsT=xT[:, c, n * 128:(n + 1) * 128], rhs=wkv[:, c, 0:512],
                                       start=(c == 0), stop=(c == 7))
                    for c in range(8):
                        ins = e.matmul(PS(bB, 0, 128, 0, 256), lhsT=xT[:, c, n * 128:(n + 1) * 128],
                                       rhs=wkv[:, c, 512:768], start=(c == 0), stop=(c == 7))
                    return ins
                S.pe(mmkv, r=[('xT', n)] + WKV, w=[pk(bA), pk(bB)])
                sb = n % 2
                S.act(lambda e, sb=sb, bA=bA: e.copy(out=kvst[sb][:, 0:512], in_=PS(bA)),
                      r=[pk(bA)], w=[('kvst', sb, 0)])
                S.act(lambda e, sb=sb, bB=bB: e.copy(out=kvst[sb][:, 512:768], in_=PS(bB, 0, 128, 0, 256)),
                      r=[pk(bB)], w=[('kvst', sb, 1)])
                S.dma('sp', lambda e, n=n, sb=sb: e.dma_start(out=O['cmp_p'][n * 128:(n + 1) * 128, :],
                                                               in_=kvst[sb][:, 0:256]), r=[('kvst', sb, 0)])
                S.dma('sp', lambda e, n=n, sb=sb: e.dma_start(out=O['sel_p'][n * 128:(n + 1) * 128, :],
                                                               in_=kvst[sb][:, 256:512]), r=[('kvst', sb, 0)])
                if n >= 12:
                    S.dma('sp', lambda e, n=n, sb=sb: e.dma_start(
                        out=O['win_p'][(n - 12) * 128:(n - 11) * 128, :], in_=kvst[sb][:, 512:768]),
                        r=[('kvst', sb, 1)])
                S.pool(lambda e, n=n, sb=sb: e.tensor_copy(out=Vsel[:, n, :], in_=kvst[sb][:, 384:512]),
                       r=[('kvst', sb, 0)], w=[('Vsel', n)])
                S.pool(lambda e, n=n, sb=sb: e.tensor_copy(out=Vwin[:, n, :], in_=kvst[sb][:, 640:768]),
                       r=[('kvst', sb, 1)], w=[('Vwin', n)])
            stage_end(21)
            def mmkvs(e):
                ins = None
                for c in range(8):
                    ins = e.matmul(PS(4, 0, NS, 0, 512), lhsT=xT[:, c, T:TT], rhs=wkv[:, c, 0:512],
                                   start=(c == 0), stop=(c == 7))
                for c in range(8):
                    ins = e.matmul(PS(5, 0, NS, 0, 256), lhsT=xT[:, c, T:TT], rhs=wkv[:, c, 512:768],
                                   start=(c == 0), stop=(c == 7))
                return ins
            S.pe(mmkvs, r=[('xT', 16)] + WKV, w=[pk(4), pk(5)])
            S.act(lambda e: e.copy(out=kvst[0][0:NS, 0:512], in_=PS(4, 0, NS, 0, 512)), r=[pk(4)], w=[('kvst', 0, 0)])
            S.act(lambda e: e.copy(out=kvst[0][0:NS, 512:768], in_=PS(5, 0, NS, 0, 256)), r=[pk(5)], w=[('kvst', 0, 1)])
            S.dma('sp', lambda e: e.dma_start(out=O['cmp_s'], in_=kvst[0][0:NS, 0:256]), r=[('kvst', 0, 0)])
            S.dma('sp', lambda e: e.dma_start(out=O['sel_s'], in_=kvst[0][0:NS, 256:512]), r=[('kvst', 0, 0)])
            win_s_v = O['win_s'].rearrange("(s r) c -> s r c", r=512)
            S.dma('sp', lambda e: e.dma_start(out=win_s_v[:, 511, :], in_=kvst[0][0:NS, 512:768]), r=[('kvst', 0, 1)])
            S.pool(lambda e: e.tensor_copy(out=kvsv[0:NS, 0, :], in_=kvst[0][0:NS, 384:512]),
                   r=[('kvst', 0, 0)], w=['kvsv'])
            S.pool(lambda e: e.tensor_copy(out=kvsv[0:NS, 1, :], in_=kvst[0][0:NS, 640:768]),
                   r=[('kvst', 0, 1)], w=['kvsv'])
            stage_end(22)
            for s in range(NS):
                wb_ = winb[s % 2]
                src = I['cache_win'][s * 512:(s + 1) * 512, :].rearrange("(p j) c -> p j c", j=4)
                dst = O['win_s'][s * 512:(s + 1) * 512, :].rearrange("(p j) c -> p j c", j=4)
                S.dma('sp', lambda e, wb_=wb_, src=src: e.dma_start(out=wb_, in_=src), w=[('winb', s % 2)])
                S.dma('sp', lambda e, wb_=wb_, dst=dst: e.dma_start(out=dst[:, 0:3, :], in_=wb_[:, 1:4, :]),
                      r=[('winb', s % 2)])
                S.dma('sp', lambda e, wb_=wb_, dst=dst: e.dma_start(out=dst[0:127, 3, :], in_=wb_[1:128, 0, :]),
                      r=[('winb', s % 2)])
            stage_end(23)
            S.barrier()
            A.release()
            stage_end(2)
            BLKS = [(tb * 512, 512) for tb in range(4)] + [(T, NS)]

            def xT_keys(c0, n):
                if c0 >= T:
                    return [('xT', 16)]
                return [('xT', c0 // 128 + i) for i in range(max(1, n // 128))]

            pbank = [0]

            def proj_multi(units, blks=BLKS):
                for (c0, n) in blks:
                    for (wt, wkey, M, evac, aug) in units:
                        bank = pbank[0] % 4
                        pbank[0] += 1

                        def mm(e, c0=c0, n=n, bank=bank, wt=wt, M=M, aug=aug):
                            ins = None
                            for c in range(8):
                                ins = e.matmul(PS(bank, 0, M, 0, n), lhsT=wt[:, c, :], rhs=xT[:, c, c0:c0 + n],
                                               start=(c == 0), stop=(c == 7 and aug is None))
                            if aug is not None:
                                ins = e.matmul(PS(bank, 0, M, 0, n), lhsT=aug, rhs=bas[0:3, c0:c0 + n],
                                               start=False, stop=True)
                            return ins
                        S.pe(mm, r=[wkey, 'bas', 'bas0', 'coef'] + xT_keys(c0, n), w=[pk(bank)])
                        evac(bank, c0, n)

            def proj_fm(wt, wkey, M, evac, aug=None, blks=BLKS):
                proj_multi([(wt, wkey, M, evac, aug)], blks)


            A.mark()
            NWB = 4
            wbuf = [A.alloc([128, 8, 128], BF16) for _ in range(NWB)]
            wctr = [0]

            def load_w(col0, M):
                i = wctr[0] % NWB
                wctr[0] += 1
                S.dma('pool', lambda e: e.dma_start(out=wbuf[i][:, :, 0:M], in_=w_in_t[:, :, col0:col0 + M]),
                      w=[('wbuf', i)])
                return wbuf[i][:, :, 0:M], ('wbuf', i)
            u_ext = A.alloc([128, 4, 30 + T], BF16)
            sza = A.alloc([128, 4, TT], BF16)
            c32 = A.alloc([128, 4, TT], F32)
            u32t = A.alloc([128, 4, 32], F32)
            us32 = A.alloc([128, 4, NS], F32)
            cw_sb = A.alloc([31, 512], F32)
            cwT = A.alloc([128, 4, 31], F32)
            cvec = A.alloc([128, 3, 4], F32)
            A.mark()
            sig = [A.alloc([128, 512], F32) for _ in range(2)]
            sigc = [0]
            S.pool(lambda e: e.memset(u_ext[:, :, 0:30], 0.0), w=[('uext', 'h')])
            for j in range(4):
                wtg, wkg = load_w(C_AGLU + j * 128, 128)
                wta, wka = load_w(C_AVAL + j * 128, 128)
                wtz, wkz = load_w(C_ZA + j * 128, 128)

                def ev_sig(bank, c0, n, j=j):
                    sb = sigc[0] % 2
                    S.act(lambda e: e.activation(out=sig[sb][:, 0:n], in_=PS(bank, 0, 128, 0, n), func=AF.Sigmoid),
                          r=[pk(bank)], w=[('sig', sb)])

                def ev_u(bank, c0, n, j=j):
                    sb = sigc[0] % 2
                    sigc[0] += 1
                    if c0 < T:
                        S.dve(lambda e: e.tensor_tensor(out=u_ext[:, j, 30 + c0:30 + c0 + n], in0=PS(bank, 0, 128, 0, n),
                                                        in1=sig[sb][:, 0:n], op=ALU.mult),
                              r=[pk(bank), ('sig', sb)], w=[('uext', j, c0)])
                        if c0 == 1536:
                            S.dve(lambda e: e.tensor_tensor(out=u32t[:, j, :], in0=PS(bank, 0, 128, 480, 512),
                                                            in1=sig[sb][:, 480:512], op=ALU.mult),
                                  r=[pk(bank), ('sig', sb)], w=[('u32t', j)])
                    else:
                        S.dve(lambda e: e.tensor_tensor(out=us32[:, j, :], in0=PS(bank, 0, 128, 0, n),
                                                        in1=sig[sb][:, 0:n], op=ALU.mult),
                              r=[pk(bank), ('sig', sb)], w=[('us32', j)])

                def ev_za(bank, c0, n, j=j):
                    S.act(lambda e: e.activation(out=sza[:, j, c0:c0 + n], in_=PS(bank, 0, 128, 0, n), func=AF.Silu),
                          r=[pk(bank)], w=[('sza', j, c0)])
                proj_multi([(wtg, wkg, 128, ev_sig, None), (wta, wka, 128, ev_u, None), (wtz, wkz, 128, ev_za, None)])

            stage_end(3)
            S.dma('sp', lambda e: e.dma_start(out=cw_sb, in_=I['conv_w']), w=['cw_sb'])

            def trcw(e):
                ins = None
                for j in range(4):
                    ins = e.transpose(out=PS(7, 0, 128, j * 32, j * 32 + 31), in_=cw_sb[:, j * 128:(j + 1) * 128],
                                      identity=ident_f[0:31, 0:31])
                return ins
            S.pe(trcw, r=['cw_sb', 'ident_f'], w=[pk(7)])
            S.dve(lambda e: e.tensor_copy(out=cwT, in_=PS(7, 0, 128, 0, 128).rearrange("p (j t) -> p j t", j=4)[:, :, 0:31]),
                  r=[pk(7)], w=['cwT'])
            for i, nm in enumerate(['conv_b', 'conv_ln_g', 'conv_ln_b']):
                S.dma('sp', lambda e, i=i, nm=nm: e.dma_start(out=cvec[:, i, :],
                                                              in_=I[nm].rearrange("o (j p) -> p (o j)", p=128),
                                                              allow_slow_non_contiguous=True),
                      w=[('cvec', i)])

            diag = [A.alloc([128, 31, 128], BF16) for _ in range(2)]
            for j in range(4):
                db = j % 2
                for tap in range(31):
                    S.pool(lambda e, j=j, tap=tap, db=db: e.tensor_scalar(
                        out=diag[db][:, tap, :], in0=ident_b, scalar1=cwT[:, j, tap:tap + 1], scalar2=None, op0=ALU.mult),
                        r=['cwT', 'ident_b'], w=[('diag', db, tap)])
                for tb in range(4):
                    bank = 4 + (tb % 2)

                    def cm(e, j=j, tb=tb, db=db, bank=bank):
                        ins = None
                        for tap in range(31):
                            ins = e.matmul(PS(bank), lhsT=diag[db][:, tap, :],
                                           rhs=u_ext[:, j, tb * 512 + tap: tb * 512 + tap + 512],
                                           start=(tap == 0), stop=(tap == 30))
                        return ins
                    rk = [('diag', db, tap) for tap in range(31)] + [('uext', j, tb * 512)]
                    rk += [('uext', j, (tb - 1) * 512)] if tb > 0 else [('uext', 'h')]
                    S.pe(cm, r=rk, w=[pk(bank)])
                    S.act(lambda e, j=j, tb=tb, bank=bank: e.activation(
                        out=c32[:, j, tb * 512:(tb + 1) * 512], in_=PS(bank), func=AF.Identity, bias=cvec[:, 0, j:j + 1]),
                        r=[pk(bank), ('cvec', 0)], w=[('c32', j, tb * 512)])

            stage_end(4)
            exts = A.alloc([128, 4, NS, 31], F32)
            scv = A.alloc([30, NS, 512], F32)
            S.dma('sp', lambda e: e.dma_start(out=scv, in_=I['state_conv'].rearrange("(s r) c -> r s c", r=30)), w=['scv'])
            for s in range(NS):
                def trs_(e, s=s):
                    ins = None
                    for j in range(4):
                        ins = e.transpose(out=PS(6, 0, 128, j * 32, j * 32 + 30), in_=scv[:, s, j * 128:(j + 1) * 128],
                                          identity=ident_f[0:30, 0:30])
                    return ins
                S.pe(trs_, r=['scv', 'ident_f'], w=[pk(6)])
                S.dve(lambda e, s=s: e.tensor_copy(
                    out=exts[:, :, s, 0:30], in_=PS(6, 0, 128, 0, 128).rearrange("p (j t) -> p j t", j=4)[:, :, 0:30]),
                    r=[pk(6)], w=[('exts', s)])
            S.dve(lambda e: e.tensor_copy(out=exts[:, :, :, 30], in_=us32),
                  r=[('us32', j) for j in range(4)], w=[('exts', 'u')])
            prod = A.alloc([128, 4, NS, 31], F32)
            S.dve(lambda e: e.tensor_tensor(out=prod, in0=exts, in1=cwT.unsqueeze(2).broadcast_to([128, 4, NS, 31]),
                                            op=ALU.mult),
                  r=[('exts', s) for s in range(NS)] + [('exts', 'u'), 'cwT'], w=['prod'])
            cs_ = A.alloc([128, 4, NS], F32)
            S.dve(lambda e: e.tensor_reduce(out=cs_, in_=prod, axis=AX.X, op=ALU.add), r=['prod'], w=['cs'])
            S.dve(lambda e: e.tensor_tensor(out=c32[:, :, T:TT], in0=cs_,
                                            in1=cvec[:, 0, :].unsqueeze(2).broadcast_to([128, 4, NS]), op=ALU.add),
                  r=['cs', ('cvec', 0)], w=[('c32', j, T) for j in range(4)])

            cvo = A.alloc([32, 512], F32)

            def tru(e):
                ins = None
                for j in range(4):
                    ins = e.transpose(out=PS(6, 0, 32, j * 128, (j + 1) * 128), in_=u32t[:, j, :], identity=ident_f)
                return ins
            S.pe(tru, r=[('u32t', j) for j in range(4)] + ['ident_f'], w=[pk(6)])
            S.act(lambda e: e.copy(out=cvo, in_=PS(6, 0, 32, 0, 512)), r=[pk(6)], w=['cvo'])
            S.dma('sp', lambda e: e.dma_start(out=O['conv_p'], in_=cvo[2:32, :]), r=['cvo'])
            cso = A.alloc([NS, 512], F32)

            def trus(e):
                ins = None
                for j in range(4):
                    ins = e.transpose(out=PS(6, 0, NS, j * 128, (j + 1) * 128), in_=us32[:, j, :], identity=ident_f)
                return ins
            S.pe(trus, r=[('us32', j) for j in range(4)] + ['ident_f'], w=[pk(6)])
            S.act(lambda e: e.copy(out=cso, in_=PS(6, 0, NS, 0, 512)), r=[pk(6)], w=['cso'])
            conv_s_v = O['conv_s'].rearrange("(s r) c -> s r c", r=30)
            S.dma('sp', lambda e: e.dma_start(out=conv_s_v[:, 29, :], in_=cso), r=['cso'])
            for s in range(NS):
                S.dma('sp', lambda e, s=s: e.dma_start(out=conv_s_v[s, 0:29, :], in_=scv[1:30, s, :]), r=['scv'])

            stage_end(5)
            S.barrier()
            A.release()
            onesm = A.alloc([128, 128], F32)
            S.pool(lambda e: e.memset(onesm, 1.0 / 512.0), w=['onesm'])
            sq4 = A.alloc([128, 4, 512], F32)
            mean_sb = A.alloc([128, 512], F32)
            rstd_sb = A.alloc([128, 512], F32)
            tmpv = A.alloc([128, 512], F32)
            tj = [A.alloc([128, 512], F32) for _ in range(2)]
            tjc = 0
            for (c0, n) in BLKS:
                ck = [('c32', j, c0) for j in range(4)]
                S.act(lambda e, c0=c0, n=n: e.activation(out=sq4[:, :, 0:n], in_=c32[:, :, c0:c0 + n], func=AF.Square),
                      r=ck, w=['sq4'])

                def stm(e, c0=c0, n=n):
                    ins = None
                    for j in range(4):
                        ins = e.matmul(PS(0, 0, 128, 0, n), lhsT=onesm, rhs=c32[:, j, c0:c0 + n], start=(j == 0), stop=(j == 3))
                    for j in range(4):
                        ins = e.matmul(PS(1, 0, 128, 0, n), lhsT=onesm, rhs=sq4[:, j, 0:n], start=(j == 0), stop=(j == 3))
                    return ins
                S.pe(stm, r=ck + ['sq4', 'onesm'], w=[pk(0), pk(1)])
                S.act(lambda e, n=n: e.copy(out=mean_sb[:, 0:n], in_=PS(0, 0, 128, 0, n)), r=[pk(0)], w=['mean_sb'])
                S.dve(lambda e, n=n: e.tensor_tensor(out=tmpv[:, 0:n], in0=mean_sb[:, 0:n], in1=mean_sb[:, 0:n], op=ALU.mult),
                      r=['mean_sb'], w=['tmpv'])
                S.dve(lambda e, n=n: e.tensor_tensor(out=tmpv[:, 0:n], in0=PS(1, 0, 128, 0, n), in1=tmpv[:, 0:n],
                                                     op=ALU.subtract), r=[pk(1), 'tmpv'], w=['tmpv'])
                S.dve(lambda e, n=n: e.tensor_scalar(out=tmpv[:, 0:n], in0=tmpv[:, 0:n], scalar1=1e-5, scalar2=None,
                                                     op0=ALU.add), r=['tmpv'], w=['tmpv'])
                S.act(lambda e, n=n: e.activation(out=tmpv[:, 0:n], in_=tmpv[:, 0:n], func=AF.Ln), r=['tmpv'], w=['tmpv'])
                S.act(lambda e, n=n: e.activation(out=rstd_sb[:, 0:n], in_=tmpv[:, 0:n], func=AF.Exp, scale=-0.5),
                      r=['tmpv'], w=['rstd_sb'])
                for j in range(4):
                    tb_ = tj[tjc % 2]
                    tk = ('tj', tjc % 2)
                    tjc += 1
                    S.dve(lambda e, j=j, c0=c0, n=n, tb_=tb_: e.tensor_tensor(
                        out=tb_[:, 0:n], in0=c32[:, j, c0:c0 + n], in1=mean_sb[:, 0:n], op=ALU.subtract),
                        r=[('c32', j, c0), 'mean_sb'], w=[tk])
                    S.dve(lambda e, n=n, tb_=tb_: e.tensor_tensor(out=tb_[:, 0:n], in0=tb_[:, 0:n], in1=rstd_sb[:, 0:n],
                                                                  op=ALU.mult), r=[tk, 'rstd_sb'], w=[tk])
                    S.dve(lambda e, j=j, n=n, tb_=tb_: e.tensor_scalar(
                        out=tb_[:, 0:n], in0=tb_[:, 0:n], scalar1=cvec[:, 1, j:j + 1], scalar2=cvec[:, 2, j:j + 1],
                        op0=ALU.mult, op1=ALU.add), r=[tk, ('cvec', 1), ('cvec', 2)], w=[tk])
                    S.act(lambda e, n=n, tb_=tb_: e.activation(out=tb_[:, 0:n], in_=tb_[:, 0:n], func=AF.Silu),
                          r=[tk], w=[tk])
                    S.pool(lambda e, j=j, c0=c0, n=n, tb_=tb_: e.tensor_tensor(
                        out=yaT[:, j, c0:c0 + n], in0=tb_[:, 0:n], in1=sza[:, j, c0:c0 + n], op=ALU.mult),
                        r=[tk, ('sza', j, c0)], w=[('yaT', j, c0)])
            S.barrier()
            A.release()
            stage_end(6)
            FORCE = 1.0e4
            BIG = 30000.0
            ybT = A.alloc([64, 8, TT], BF16)
            A.mark()
            Qaug = [A.alloc([128, 4, TT], BF16) for g in range(2)]
            Ksel = [A.alloc([128, TT], BF16) for g in range(2)]
            Kwin = [A.alloc([68, TT], BF16) for g in range(2)]
            gsig = A.alloc([24, TT], BF16)
            KCa = [A.alloc([68, 64], BF16) for g in range(2)]
            VC = [A.alloc([64, 64], BF16) for g in range(2)]
            kcT = A.alloc([64, 2, 2, 64], F32)
            Wkv = A.alloc([64, 2, 2, 32], F32)
            oh = A.alloc([24, 24, 64], BF16)
            ones64 = A.alloc([128, 64], BF16)
            Mpad = A.alloc([128, 128], BF16)
            A.mark()
            NWB = 4
            wpad = [A.alloc([128, 8, 68], BF16) for _ in range(NWB)]
            for i in range(NWB):
                S.pool(lambda e, i=i: e.memset(wpad[i], 0.0), w=[('wpad', i)])
            wpc = [0]

            def load_wpad(col0, M=64):
                i = wpc[0] % NWB
                wpc[0] += 1
                S.dma('pool', lambda e: e.dma_start(out=wpad[i][:, :, 0:M], in_=w_in_t[:, :, col0:col0 + M]),
                      w=[('wpad', i)])
                return wpad[i], ('wpad', i)
            bas = A.alloc([3, TT], BF16)
            basc = A.alloc([3, 64], BF16)
            brow = A.alloc([1, 2, TT], BF16)
            browc = A.alloc([1, 2, 64], BF16)
            coefQ = A.alloc([3, 8, 68], BF16)
            coefK = A.alloc([3, 68], BF16)
            cq0 = A.alloc([1, 3, 8, 68], BF16)
            ck0 = A.alloc([1, 3, 68], BF16)

            def mkbas(e):
                e.iota(brow[:, 0, 0:T], pattern=[[1, 32], [0, 64]], base=0, channel_multiplier=0,
                       allow_small_or_imprecise_dtypes=True)
                e.iota(brow[:, 1, 0:T], pattern=[[0, 32], [1, 64]], base=0, channel_multiplier=0,
                       allow_small_or_imprecise_dtypes=True)
                e.memset(brow[:, 0, T:TT], 128.0)
                e.memset(brow[:, 1, T:TT], 0.0)
                e.iota(browc[:, 0, :], pattern=[[1, 32], [0, 2]], base=0, channel_multiplier=0,
                       allow_small_or_imprecise_dtypes=True)
                e.iota(browc[:, 1, :], pattern=[[0, 32], [32, 2]], base=31, channel_multiplier=0,
                       allow_small_or_imprecise_dtypes=True)
                e.memset(bas[0:1, :], 1.0)
                e.memset(basc[0:1, :], 1.0)
                e.memset(cq0, 0.0)
                e.memset(ck0, 0.0)
                for h in range(8):
                    sl = 2.0 ** (-(h + 1))
                    e.memset(cq0[:, 0, h, 64:65], 8.0 * sl * 64.0)
                    e.memset(cq0[:, 0, h, 65:66], 8.0 * sl)
                    e.memset(cq0[:, 1, h, 66:67], -8.0 * sl * 64.0)
                    e.memset(cq0[:, 2, h, 67:68], -8.0 * sl)
                e.memset(ck0[:, 0, 66:68], 1.0)
                e.memset(ck0[:, 1, 64:65], 1.0)
                e.memset(ck0[:, 2, 65:66], 1.0)
                e.memset(ones64, 1.0)
                e.memset(Mpad, 0.0)
                return e.tensor_copy(out=oh, in_=ident_b[0:24, 0:24].unsqueeze(2).broadcast_to([24, 24, 64]))
            S.pool(mkbas, r=['ident_b'], w=['brow', 'bas0', 'ones64', 'Mpad', 'oh'])
            for i in range(2):
                S.dma('sp', lambda e, i=i: e.dma_start(out=bas[1 + i:2 + i, :], in_=brow[0:1, i, :]), r=['brow'], w=['bas'])
                S.dma('sp', lambda e, i=i: e.dma_start(out=basc[1 + i:2 + i, :], in_=browc[0:1, i, :]), r=['brow'], w=['bas'])
            for i in range(3):
                S.dma('sp', lambda e, i=i: e.dma_start(out=coefQ[i:i + 1, :, :], in_=cq0[0:1, i, :, :]), r=['brow'], w=['coef'])
                S.dma('sp', lambda e, i=i: e.dma_start(out=coefK[i:i + 1, :], in_=ck0[0:1, i, :]), r=['brow'], w=['coef'])

            QK = [[('Q', g, c0) for (c0, n) in BLKS] + [('Qm', g, qt) for qt in range(17)] for g in range(2)]
            KK = [[('Ks', g, c0) for (c0, n) in BLKS] for g in range(2)]
            for g in range(2):
                S.pool(lambda e, g=g: e.memset(Qaug[g][64:128, :, :], 0.0), w=QK[g])

                def kinit(e, g=g):
                    e.memset(Ksel[g][64:128, :], 0.0)
                    e.memset(Ksel[g][96:128, 0:T], 1.0)
                    e.affine_select(out=Ksel[g][96:128, 0:T], in_=Ksel[g][96:128, 0:T], pattern=[[1, T]],
                                    compare_op=ALU.is_ge, fill=freg(e, 0.0), base=0, channel_multiplier=-64)
                    return e.affine_select(out=Ksel[g][96:128, 0:T], in_=Ksel[g][96:128, 0:T], pattern=[[-1, T]],
                                           compare_op=ALU.is_ge, fill=freg(e, 0.0), base=63, channel_multiplier=64)
                S.pool(kinit, w=KK[g])

            stage_end(7)
            wt, wk_ = load_wpad(C_G3, 24)

            def ev_g(bank, c0, n):
                S.act(lambda e: e.activation(out=gsig[0:24, c0:c0 + n], in_=PS(bank, 0, 24, 0, n), func=AF.Sigmoid),
                      r=[pk(bank)], w=[('gsig', c0)])
            proj_fm(wt[:, :, 0:24], wk_, 24, ev_g)
            for h in range(8):
                wt, wk_ = load_wpad(C_ZB + h * 64, 64)

                def ev_zb(bank, c0, n, h=h):
                    S.act(lambda e: e.activation(out=ybT[0:64, h, c0:c0 + n], in_=PS(bank, 0, 64, 0, n), func=AF.Silu),
                          r=[pk(bank)], w=[('ybT', h, c0)])
                proj_fm(wt[:, :, 0:64], wk_, 64, ev_zb)
            for h in range(8):
                g, r_ = h // 4, h % 4
                wt, wk_ = load_wpad(C_Q + h * 64)

                def ev_q(bank, c0, n, g=g, r_=r_):
                    S.dve(lambda e: e.tensor_scalar(out=Qaug[g][0:68, r_, c0:c0 + n], in0=PS(bank, 0, 68, 0, n),
                                                    scalar1=0.125, scalar2=None, op0=ALU.mult),
                          r=[pk(bank)], w=[('Q', g, c0)])
                proj_fm(wt, wk_, 68, ev_q, aug=coefQ[0:3, h, :])
            augK = coefK[0:3, :]
            for g in range(2):
                wt, wk_ = load_wpad(C_KV + 256 + g * 64)

                def ev_ks(bank, c0, n, g=g):
                    S.act(lambda e: e.copy(out=Ksel[g][0:68, c0:c0 + n], in_=PS(bank, 0, 68, 0, n)),
                          r=[pk(bank)], w=[('Ks', g, c0)])
                proj_fm(wt, wk_, 68, ev_ks, aug=augK)
                wt, wk_ = load_wpad(C_KV + 512 + g * 64)

                def ev_kw(bank, c0, n, g=g):
                    S.act(lambda e: e.copy(out=Kwin[g][0:68, c0:c0 + n], in_=PS(bank, 0, 68, 0, n)),
                          r=[pk(bank)], w=[('Kw', g, c0)])
                proj_fm(wt, wk_, 68, ev_kw, aug=augK)
            for kv_, nm in enumerate(['wk_cmp', 'wv_cmp']):
                for g in range(2):
                    S.dma('sp', lambda e, kv_=kv_, nm=nm, g=g: e.dma_start(
                        out=Wkv[:, kv_, g:g + 1, :], in_=I[nm][:, g:g + 1].rearrange("l o -> o l").partition_broadcast(64),
                        allow_slow_non_contiguous=True), w=[('Wkv', kv_, g)])
            ptmp = [A.alloc([64, 16, 32], F32) for _ in range(1)]
            pcnt = [0]
            for kv_ in range(2):
                for g in range(2):
                    wt, wk_ = load_wpad(C_KV + kv_ * 128 + g * 64, 64)

                    def ev_pool(bank, c0, n, kv_=kv_, g=g):
                        pb = 0
                        pcnt[0] += 1
                        S.dve(lambda e: e.tensor_tensor(
                            out=ptmp[pb], in0=PS(bank, 0, 64, 0, 512).rearrange("p (a l) -> p a l", l=32),
                            in1=Wkv[:, kv_, g:g + 1, :].broadcast_to([64, 16, 32]), op=ALU.mult),
                            r=[pk(bank), ('Wkv', kv_, g)], w=[('ptmp', pb)])
                        S.dve(lambda e: e.tensor_reduce(out=kcT[:, kv_, g, c0 // 32:c0 // 32 + 16], in_=ptmp[pb],
                                                        axis=AX.X, op=ALU.add),
                              r=[('ptmp', pb)], w=[('kcT', kv_, g, c0)])
                    proj_fm(wt[:, :, 0:64], wk_, 64, ev_pool, blks=BLKS[0:4])
            for g in range(2):
                kck = [('kcT', 0, g, c0) for (c0, n) in BLKS[0:4]]
                vck = [('kcT', 1, g, c0) for (c0, n) in BLKS[0:4]]
                S.dve(lambda e, g=g: e.tensor_copy(out=KCa[g][0:64, :], in_=kcT[:, 0, g, :]), r=kck, w=[('KCa', g)])

                def mmaug(e, g=g):
                    return e.matmul(PS(7, 0, 68, 0, 64), lhsT=coefK[0:3, :], rhs=basc[0:3, :], start=True, stop=True)
                S.pe(mmaug, r=['coef', 'bas', 'bas0'], w=[pk(7)])
                S.act(lambda e, g=g: e.copy(out=KCa[g][64:68, :], in_=PS(7, 64, 68, 0, 64)), r=[pk(7)], w=[('KCa', g)])
                S.pe(lambda e, g=g: e.transpose(out=PS(6, 0, 64, 0, 64), in_=kcT[:, 1, g, :], identity=ident_f[0:64, 0:64]),
                     r=vck + ['ident_f'], w=[pk(6)])
                S.act(lambda e, g=g: e.copy(out=VC[g], in_=PS(6, 0, 64, 0, 64)), r=[pk(6)], w=[('VC', g)])

            stage_end(8)
            S.barrier()
            A.release()
            Ec = A.alloc([128, 4, 64], F32)
            sums_c = A.alloc([128, 4], F32)
            imp64 = A.alloc([128, 64], F32)
            imp = A.alloc([128, 32], F32)
            imp2 = A.alloc([128, 32], F32)
            m8 = A.alloc([128, 16], F32)
            PT = [A.alloc([128, 512], BF16) for _ in range(3)]
            acc = [A.alloc([64, 512], F32) for _ in range(2)]
            rs_ = [A.alloc([64, 512], F32) for _ in range(2)]
            tt_ = [A.alloc([64, 512], F32) for _ in range(2)]
            ptc = [0]
            sbc = [0]
            rsc = [0]

            def qkeys(g, qt):
                return [('Q', g, (qt // 4) * 512), ('Qm', g, qt)]

            pend = [None]

            def attn_pair(g, qt, lhsT, lkeys, KKrows, vt, vkeys, masks, ob, sb_, first, last):
                q0 = qt * 128
                sbank = sbc[0] % 2
                sbc[0] += 1
                pt = PT[ptc[0] % 3]
                ptk = ('PT', ptc[0] % 3)
                ptc[0] += 1
                M = lhsT.shape[1]
                S.pe(lambda e: e.matmul(PS(sbank, 0, M, 0, 512), lhsT=lhsT, rhs=Qaug[g][0:KKrows, :, q0:q0 + 128],
                                        start=True, stop=True),
                     r=lkeys + qkeys(g, qt), w=[pk(sbank)])
                S.act(lambda e: e.activation(out=pt[0:M, :], in_=PS(sbank, 0, M, 0, 512), func=AF.Exp),
                      r=[pk(sbank)], w=[ptk])
                for (base, cm, pat) in masks:
                    S.pool(lambda e, base=base, cm=cm, pat=pat: e.affine_select(
                        out=pt[0:M, :], in_=pt[0:M, :], pattern=pat, compare_op=ALU.is_ge, fill=freg(e, 0.0), base=base,
                        channel_multiplier=cm), r=[ptk], w=[ptk])

                def pv(e):
                    e.matmul(PS(ob, 0, 64, 0, 512), lhsT=vt, rhs=pt[0:M, :], start=first, stop=last)
                    return e.matmul(PS(sb_, 0, 64, 0, 512), lhsT=ones64[0:M, :], rhs=pt[0:M, :], start=first, stop=last)
                prev = pend[0]
                pend[0] = lambda: S.pe(pv, r=[ptk, 'ones64'] + vkeys, w=[pk(ob), pk(sb_)])
                if prev is not None:
                    prev()

            def flush_pv():
                if pend[0] is not None:
                    pend[0]()
                    pend[0] = None

            def combine(g, qt, j, ob, sb_, ab):
                flush_pv()
                q0 = qt * 128
                rb = rsc[0] % 2
                rsc[0] += 1

                def gmm(e):
                    ins = None
                    for r_ in range(4):
                        ins = e.matmul(PS(6, 0, 64, r_ * 128, (r_ + 1) * 128), lhsT=oh[:, 3 * (4 * g + r_) + j, :],
                                       rhs=gsig[0:24, q0:q0 + 128], start=True, stop=True)
                    return ins
                S.pe(gmm, r=['oh', ('gsig', (qt // 4) * 512)], w=[pk(6)])
                S.dve(lambda e: e.tensor_scalar(out=rs_[rb], in0=PS(sb_, 0, 64, 0, 512), scalar1=1e-30, scalar2=None,
                                                op0=ALU.max), r=[pk(sb_)], w=[('rs', rb)])
                S.dve(lambda e: e.reciprocal(out=rs_[rb], in_=rs_[rb]), r=[('rs', rb)], w=[('rs', rb)])
                S.dve(lambda e: e.tensor_tensor(out=rs_[rb], in0=rs_[rb], in1=PS(6, 0, 64, 0, 512), op=ALU.mult),
                      r=[('rs', rb), pk(6)], w=[('rs', rb)])
                if j == 0:
                    S.dve(lambda e: e.tensor_tensor(out=acc[ab], in0=PS(ob, 0, 64, 0, 512), in1=rs_[rb], op=ALU.mult),
                          r=[pk(ob), ('rs', rb)], w=[('acc', ab)])
                else:
                    S.dve(lambda e: e.tensor_tensor(out=tt_[rb], in0=PS(ob, 0, 64, 0, 512), in1=rs_[rb], op=ALU.mult),
                          r=[pk(ob), ('rs', rb)], w=[('tt', rb)])
                    S.pool(lambda e: e.tensor_tensor(out=acc[ab], in0=acc[ab], in1=tt_[rb], op=ALU.add),
                           r=[('tt', rb), ('acc', ab)], w=[('acc', ab)])

            pat_q = [[0, 4], [1, 128]]
            pat_qn = [[0, 4], [-1, 128]]
            def partA(qt, g):
                q0 = qt * 128
                def cmm(e, g=g, q0=q0):
                    ins = None
                    for r_ in range(4):
                        ins = e.matmul(PS(7, 0, 128, r_ * 64, (r_ + 1) * 64), lhsT=Qaug[g][0:68, r_, q0:q0 + 128],
                                       rhs=KCa[g][0:68, :], start=True, stop=True)
                    return ins
                S.pe(cmm, r=qkeys(g, qt) + [('KCa', g)], w=[pk(7)])
                S.act(lambda e: e.activation(out=Ec, in_=PS(7, 0, 128, 0, 256).rearrange("p (r n) -> p r n", r=4),
                                             func=AF.Exp), r=[pk(7)], w=['Ec'])
                S.pool(lambda e, q0=q0: e.affine_select(out=Ec, in_=Ec, pattern=[[0, 4], [-32, 64]],
                                                        compare_op=ALU.is_ge, fill=freg(e, 0.0), base=q0 - 31,
                                                        channel_multiplier=1), r=['Ec'], w=['Ec'])
                S.dve(lambda e: e.tensor_reduce(out=sums_c, in_=Ec, axis=AX.X, op=ALU.add), r=['Ec'], w=['sums_c'])
                S.dve(lambda e: e.tensor_scalar(out=sums_c, in0=sums_c, scalar1=1e-30, scalar2=None, op0=ALU.max),
                      r=['sums_c'], w=['sums_c'])
                S.dve(lambda e: e.reciprocal(out=sums_c, in_=sums_c), r=['sums_c'], w=['sums_c'])
                S.dve(lambda e: e.tensor_tensor(out=Ec, in0=Ec, in1=sums_c.unsqueeze(2).broadcast_to([128, 4, 64]),
                                                op=ALU.mult), r=['Ec', 'sums_c'], w=['Ec'])
                S.dve(lambda e: e.tensor_reduce(out=imp64, in_=Ec.rearrange("p r n -> p n r"), axis=AX.X, op=ALU.add),
                      r=['Ec'], w=['imp64'])
                S.dve(lambda e: e.tensor_reduce(out=imp, in_=imp64.rearrange("p (a b) -> p a b", b=2), axis=AX.X,
                                                op=ALU.add), r=['imp64'], w=['imp'])
                S.pool(lambda e, q0=q0: e.affine_select(out=imp, in_=imp, pattern=[[-64, 32]], compare_op=ALU.is_ge,
                                                        fill=freg(e, FORCE), base=q0 - 128, channel_multiplier=1),
                       r=['imp'], w=['imp'])
                S.pool(lambda e: e.memset(imp[:, 0:1], FORCE), r=['imp'], w=['imp'])
                S.pool(lambda e, q0=q0: e.affine_select(out=imp, in_=imp, pattern=[[-64, 32]], compare_op=ALU.is_ge,
                                                        fill=freg(e, -1.0e30), base=q0, channel_multiplier=1),
                       r=['imp'], w=['imp'])
                S.dve(lambda e: e.max(out=m8[:, 0:8], in_=imp), r=['imp'], w=['m8a'])
                S.dve(lambda e: e.match_replace(out=imp2, in_to_replace=m8[:, 0:8], in_values=imp, imm_value=-3.0e38),
                      r=['imp', 'm8a'], w=['imp2'])
                S.dve(lambda e: e.max(out=m8[:, 8:16], in_=imp2), r=['imp2'], w=['m8b'])
                S.dve(lambda e: e.tensor_scalar(out=Mpad[:, 96:128], in0=imp, scalar1=m8[:, 15:16], scalar2=None,
                                                op0=ALU.is_ge), r=['imp', 'm8b'], w=['Mpad'])

            def partB(qt, g):
                q0 = qt * 128
                S.pe(lambda e: e.matmul(PS(6, 0, 128, 0, 128), lhsT=Mpad, rhs=ident_b, start=True, stop=True),
                     r=['Mpad', 'ident_b'], w=[pk(6)])
                S.dve(lambda e, g=g, q0=q0: e.tensor_scalar(
                    out=Qaug[g][96:128, :, q0:q0 + 128],
                    in0=PS(6, 96, 128, 0, 128).unsqueeze(1).broadcast_to([32, 4, 128]),
                    scalar1=1.0, scalar2=BIG, op0=ALU.subtract, op1=ALU.mult), r=[pk(6)], w=[('Qm', g, qt)])

            def partC(qt, g):
                q0 = qt * 128
                ab = (qt * 2 + g) % 2
                attn_pair(g, qt, KCa[g][0:68, :], [('KCa', g)], 68, VC[g], [('VC', g)],
                          [(q0 - 31, -32, pat_q)], 2, 3, True, True)
                combine(g, qt, 0, 2, 3, ab)
                for kt in range(qt + 1):
                    masks = [(0, -1, pat_q)] if kt == qt else []
                    attn_pair(g, qt, Ksel[g][:, kt * 128:(kt + 1) * 128], [('Ks', g, (kt // 4) * 512)] + KK[g][0:0], 128,
                              Vsel[:, kt, g * 64:(g + 1) * 64], [('Vsel', kt)], masks, 4, 5, kt == 0, kt == qt)
                combine(g, qt, 1, 4, 5, ab)
                k_lo = max(0, qt - 4)
                for kt in range(k_lo, qt + 1):
                    masks = []
                    if kt == qt:
                        masks.append((0, -1, pat_q))
                    if kt == qt - 4:
                        masks.append((-1, 1, pat_qn))
                    attn_pair(g, qt, Kwin[g][0:68, kt * 128:(kt + 1) * 128], [('Kw', g, (kt // 4) * 512)], 68,
                              Vwin[:, kt, g * 64:(g + 1) * 64], [('Vwin', kt)], masks, 2, 3, kt == k_lo, kt == qt)
                combine(g, qt, 2, 2, 3, ab)
                S.dve(lambda e, g=g, q0=q0, ab=ab: e.tensor_tensor(
                    out=ybT[0:64, 4 * g:4 * g + 4, q0:q0 + 128],
                    in0=acc[ab].rearrange("p (r q) -> p r q", r=4),
                    in1=ybT[0:64, 4 * g:4 * g + 4, q0:q0 + 128], op=ALU.mult),
                    r=[('acc', ab)] + [('ybT', 4 * g + r_, (qt // 4) * 512) for r_ in range(4)],
                    w=[('ybT', 4 * g + r_, (qt // 4) * 512) for r_ in range(4)])

            its = [(qt, g) for qt in range(16) for g in range(2)]
            partA(*its[0])
            partB(*its[0])
            for i_, (qt, g) in enumerate(its):
                if i_ + 1 < len(its):
                    partA(*its[i_ + 1])
                partC(qt, g)
                if i_ + 1 < len(its):
                    partB(*its[i_ + 1])
            for g in range(2):
                S.dve(lambda e, g=g: e.tensor_copy(out=qs_f[:, g, :, :], in_=Qaug[g][0:68, :, T:TT]), r=[('Q', g, T)], w=['qs_f'])
                S.dve(lambda e, g=g: e.tensor_copy(out=ksn[:, 0, g, :], in_=Ksel[g][0:68, T:TT]), r=[('Ks', g, T)], w=['ksn'])
                S.dve(lambda e, g=g: e.tensor_copy(out=ksn[:, 1, g, :], in_=Kwin[g][0:68, T:TT]), r=[('Kw', g, T)], w=['ksn'])
            S.dve(lambda e: e.tensor_copy(out=gs_s, in_=gsig[0:24, T:TT]), r=[('gsig', T)], w=['gs_s'])
            stage_end(9)
            S.barrier()
            A.release()
            A.mark()
            XK = [A.alloc([128, 32, 128], F32) for _ in range(2)]
            XV = [A.alloc([128, 32, 128], F32) for _ in range(2)]
            tmp4 = xT_raw32[:, 0:8192].rearrange("p (l r d) -> p l r d", l=32, r=4)
            pt_i = A.alloc([32, NS, 2], I32)
            pt_f = A.alloc([32, NS * 2], F32)
            E32 = A.alloc([32, 128], F32)
            qcol_i = A.alloc([128, 1], I32)
            qcol_f = A.alloc([128, 1], F32)
            idx_f = A.alloc([128, NS * 2], F32)
            idx_i = A.alloc([128, NS * 2], I32)
            Wl4 = A.alloc([128, 32, 4], F32)
            slopes = A.alloc([128, 8], F32)
            posc = A.alloc([128, 2], F32)
            ABc = A.alloc([128, 2, 8], F32)
            poss = A.alloc([128, 2, 32], F32)
            ABs = A.alloc([128, 2, 8, 32], F32)
            posw = A.alloc([128, 4], F32)
            ABw = A.alloc([128, 8, 4], F32)
            Ex = A.alloc([128, 2, 128], BF16)
            qrep = A.alloc([64, 8, 128], BF16)
            qb = A.alloc([128, 8, 64], F32)
            kvc = Vwin_raw32[:, 0:512].rearrange("p (t c) -> p t c", t=2)
            tmpc = A.alloc([128, 2, 4, 64], F32)
            sc = A.alloc([128, 2, 8], F32)
            Esum = A.alloc([128, 8], F32)
            pcg = A.alloc([128, 2, 2], F32)
            impn = A.alloc([2, 256], F32)
            impd = A.alloc([2, 129], F32)
            impd2 = A.alloc([2, 129], F32)
            m8d = A.alloc([2, 16], F32)
            Md = A.alloc([2, 129], F32)
            MT = A.alloc([128, 2], BF16)
            mskd = A.alloc([128, 2, 2], F32)
            ss = Vwin_raw32[:, 512:1024].rearrange("p (t h l) -> p t h l", t=2, h=8)
            Pl = A.alloc([128, 2, 8], F32)
            Wn = Vsel_raw32[:, 0:1024].rearrange("p (l c) -> p l c", l=4)
            sw = A.alloc([128, 8, 4], F32)
            Plw = A.alloc([128, 8], F32)
            prodn = A.alloc([68, 2, 2, 4, NS], F32)
            pnew = A.alloc([1, 2, 2, 4, NS], F32)
            vrow = A.alloc([1, NS, 2, 128], F32)
            accd = A.alloc([4, 2, 64], F32)
            gts = A.alloc([4, 6, NS], F32)
            rsd = A.alloc([4, 4], F32)

            cache_v = {(c_, hf): I['cache_%s_%s' % (c_, hf)].rearrange("(b q) c -> b (q c)", q=32)
                       for c_ in ('cmp', 'sel') for hf in ('k', 'v')}

            def chain(addfn, fns, key, r=()):
                for fn in fns:
                    addfn(fn, r=[key] + list(r), w=[key])

            fns1 = [
                lambda e: e.iota(qcol_i, pattern=[[0, 1]], base=0, channel_multiplier=1),
                lambda e: e.memset(E32, 1.0),
                lambda e: e.memset(accd, 0.0),
                lambda e: e.affine_select(out=E32, in_=E32, pattern=[[1, 128]], compare_op=ALU.is_ge, fill=freg(e, 0.0), base=0,
                                          channel_multiplier=-4),
                lambda e: e.iota(posc, pattern=[[4096, 2]], base=31 - 8192, channel_multiplier=32,
                                 allow_small_or_imprecise_dtypes=True),
                lambda e: e.affine_select(out=E32, in_=E32, pattern=[[-1, 128]], compare_op=ALU.is_ge, fill=freg(e, 0.0), base=3,
                                          channel_multiplier=4),
                lambda e: e.iota(poss, pattern=[[4096, 2], [1, 32]], base=-8192, channel_multiplier=32,
                                 allow_small_or_imprecise_dtypes=True),
                lambda e: e.iota(posw, pattern=[[1, 4]], base=7680 - 8192, channel_multiplier=4,
                                 allow_small_or_imprecise_dtypes=True),
            ]
            for h in range(8):
                fns1.append(lambda e, h=h: e.memset(slopes[:, h:h + 1], 2.0 ** (-(h + 1))))
            for t in range(2):
                fns1.append(lambda e, t=t: e.memset(Ex[:, t, :], 1.0))
            for t in range(2):
                fns1.append(lambda e, t=t: e.affine_select(out=Ex[:, t, :], in_=Ex[:, t, :], pattern=[[1, 128]], compare_op=ALU.is_ge,
                                                          fill=freg(e, 0.0), base=128 * t, channel_multiplier=-2))
            for t in range(2):
                fns1.append(lambda e, t=t: e.affine_select(out=Ex[:, t, :], in_=Ex[:, t, :], pattern=[[-1, 128]], compare_op=ALU.is_ge,
                                                          fill=freg(e, 0.0), base=1 - 128 * t, channel_multiplier=2))
            chain(S.pool, fns1, 'dsetup')
            S.dma('sp', lambda e: e.dma_start(out=pt_i, in_=I['page_table'].rearrange("s (t j) -> j s t", j=32),
                                              allow_slow_non_contiguous=True), w=['pt_i'])
            S.dma('sp', lambda e: e.dma_start(out=Wl4[:, :, 0:2], in_=I['wk_cmp'].partition_broadcast(128)), w=['Wl4a'])
            S.dma('sp', lambda e: e.dma_start(out=Wl4[:, :, 2:4], in_=I['wv_cmp'].partition_broadcast(128)), w=['Wl4b'])
            for s in range(NS):
                S.dma('sp', lambda e, s=s: e.dma_start(out=vrow[0:1, s, 0, :], in_=kvsv[s:s + 1, 0, :]), r=['kvsv'],
                      w=[('vrow', s)])
                S.dma('sp', lambda e, s=s: e.dma_start(out=vrow[0:1, s, 1, :], in_=kvsv[s:s + 1, 1, :]), r=['kvsv'],
                      w=[('vrow', s)])

            fns2 = [
                lambda e: e.tensor_copy(out=pt_f, in_=pt_i.rearrange("p s t -> p (s t)")),
                lambda e: e.tensor_single_scalar(out=qcol_i, in_=qcol_i, scalar=3, op=ALU.bitwise_and),
                lambda e: e.tensor_tensor(out=ABc, in0=posc.unsqueeze(2).broadcast_to([128, 2, 8]),
                                          in1=slopes.unsqueeze(1).broadcast_to([128, 2, 8]), op=ALU.mult),
                lambda e: e.tensor_copy(out=qcol_f, in_=qcol_i),
                lambda e: e.tensor_tensor(out=ABw, in0=posw.unsqueeze(1).broadcast_to([128, 8, 4]),
                                          in1=slopes.unsqueeze(2).broadcast_to([128, 8, 4]), op=ALU.mult),
            ]
            for t in range(2):
                fns2.append(lambda e, t=t: e.tensor_tensor(out=ABs[:, t, :, :], in0=poss[:, t, :].unsqueeze(1).broadcast_to([128, 8, 32]),
                                                          in1=slopes.unsqueeze(2).broadcast_to([128, 8, 32]), op=ALU.mult))
            chain(S.dve, fns2, 'dsetup2', r=['dsetup', 'pt_i'])
            S.pe(lambda e: e.matmul(PS(1, 0, 128, 0, NS * 2), lhsT=E32, rhs=pt_f, start=True, stop=True),
                 r=['dsetup', 'dsetup2'], w=[pk(1)])
            S.dve(lambda e: e.tensor_scalar(out=idx_f, in0=PS(1, 0, 128, 0, NS * 2), scalar1=4.0, scalar2=qcol_f[:, 0:1],
                                            op0=ALU.mult, op1=ALU.add), r=[pk(1), 'dsetup2'], w=['idx_f'])
            chain(S.dve, [lambda e: e.tensor_copy(out=idx_i, in_=idx_f)], 'idx_i', r=['idx_f'])
            for br in range(2):
                S.dve(lambda e, br=br: e.tensor_tensor(
                    out=prodn[:, br, :, :, :], in0=qs_f, in1=ksn[:, br, :, :].unsqueeze(2).broadcast_to([68, 2, 4, NS]),
                    op=ALU.mult), r=['qs_f', 'ksn'], w=[('prodn', br)])
            S.pe(lambda e: e.matmul(PS(7, 0, 1, 0, 64), lhsT=ones_f[0:68, 0:1],
                                    rhs=prodn.rearrange("p a g r s -> p (a g r s)"), start=True, stop=True),
                 r=[('prodn', 0), ('prodn', 1), 'ones_f'], w=[pk(7)])
            S.act(lambda e: e.activation(out=pnew.rearrange("p a g r s -> p (a g r s)"), in_=PS(7, 0, 1, 0, 64), func=AF.Exp),
                  r=[pk(7)], w=['pnew'])
            def gmm_d(e):
                ins = None
                for g in range(2):
                    for j in range(3):
                        c0_ = 12 * g + j
                        ins = e.matmul(PS(7, 0, 4, 64 + (g * 3 + j) * NS, 64 + (g * 3 + j + 1) * NS),
                                       lhsT=ident_b[0:24, c0_:c0_ + 10:3], rhs=gs_s, start=True, stop=True)
                return ins
            S.pe(gmm_d, r=['gs_s', 'ident_b'], w=[pk(7)])
            S.act(lambda e: e.copy(out=gts.rearrange("p a s -> p (a s)"), in_=PS(7, 0, 4, 64, 64 + 6 * NS)), r=[pk(7)],
                  w=['gts'])

            def gather(cname, s, t, xb):
                col = s * 2 + t
                for (buf, hf, kn) in ((XK, 'k', 'XK'), (XV, 'v', 'XV')):
                    S.dma('pool', lambda e, buf=buf, hf=hf: e.indirect_dma_start(
                        out=buf[xb].rearrange("p l c -> p (l c)"), out_offset=None, in_=cache_v[(cname, hf)],
                        in_offset=bass.IndirectOffsetOnAxis(ap=idx_i[:, col:col + 1], axis=0)), r=['idx_i'], w=[(kn, xb)])

            for s in range(NS):
                for t in range(2):
                    gather('cmp', s, t, t)
                    for hi_, (buf, kn) in enumerate(((XK, 'XK'), (XV, 'XV'))):
                        S.dve(lambda e, t=t, buf=buf, hi_=hi_: e.tensor_tensor(
                            out=buf[t].rearrange("p l (a d) -> p l a d", a=2), in0=buf[t].rearrange("p l (a d) -> p l a d", a=2),
                            in1=Wl4[:, :, 2 * hi_:2 * hi_ + 2].unsqueeze(3).broadcast_to([128, 32, 2, 64]), op=ALU.mult),
                            r=[(kn, t), 'Wl4a', 'Wl4b'], w=[(kn, t)])
                        S.dve(lambda e, t=t, buf=buf, hi_=hi_: e.tensor_reduce(
                            out=kvc[:, t, hi_ * 128:(hi_ + 1) * 128], in_=buf[t].rearrange("p l c -> p c l"), axis=AX.X, op=ALU.add),
                            r=[(kn, t)], w=[('kvc', t)])
                stage_end(106)
                S.dve(lambda e, s=s: e.tensor_copy(
                    out=qrep, in_=qs_f[0:64, :, :, s].rearrange("p g r -> p (g r)").unsqueeze(2).broadcast_to([64, 8, 128])),
                    r=['qs_f'], w=['qrep'])

                def qbm(e):
                    ins = None
                    for h in range(8):
                        ins = e.matmul(PS(0, 0, 128, h * 64, (h + 1) * 64), lhsT=qrep[:, h, :], rhs=ident_b[0:64, 0:64],
                                       start=True, stop=True)
                    return ins
                S.pe(qbm, r=['qrep', 'ident_b'], w=[pk(0)])
                S.act(lambda e: e.copy(out=qb.rearrange("p h d -> p (h d)"), in_=PS(0)), r=[pk(0)], w=['qb'])
                for t in range(2):
                    S.dve(lambda e, t=t: e.tensor_tensor(
                        out=tmpc, in0=kvc[:, t, 0:128].rearrange("p (g d) -> p g d", g=2).unsqueeze(2).broadcast_to([128, 2, 4, 64]),
                        in1=qb.rearrange("p (g r) d -> p g r d", g=2), op=ALU.mult),
                        r=[('kvc', t), 'qb'], w=['tmpc'])
                    S.dve(lambda e, t=t: e.tensor_reduce(out=sc[:, t, :], in_=tmpc.rearrange("p g r d -> p (g r) d"),
                                                         axis=AX.X, op=ALU.add), r=['tmpc'], w=[('sc', t)])
                S.dve(lambda e: e.tensor_tensor(out=sc, in0=sc, in1=ABc, op=ALU.add), r=[('sc', 0), ('sc', 1), 'dsetup2'],
                      w=['sc'])
                S.act(lambda e: e.activation(out=sc, in_=sc, func=AF.Exp), r=['sc'], w=['sc'])
                S.dve(lambda e: e.tensor_tensor(out=Esum, in0=sc[:, 0, :], in1=sc[:, 1, :], op=ALU.add), r=['sc'], w=['Esum'])
                S.pe(lambda e: e.matmul(PS(1, 0, 128, 0, 8), lhsT=ones_f, rhs=Esum, start=True, stop=True),
                     r=['Esum', 'ones_f'], w=[pk(1)])
                S.dve(lambda e: e.reciprocal(out=Esum, in_=PS(1, 0, 128, 0, 8)), r=[pk(1)], w=['Esum'])
                S.dve(lambda e: e.tensor_tensor(out=sc, in0=sc, in1=Esum.unsqueeze(1).broadcast_to([128, 2, 8]), op=ALU.mult),
                      r=['sc', 'Esum'], w=['sc'])

                def ocm(e):
                    ins = None
                    for g in range(2):
                        for t in range(2):
                            ins = e.matmul(PS(2, 0, 4, g * 64, (g + 1) * 64), lhsT=sc[:, t, 4 * g:4 * g + 4],
                                           rhs=kvc[:, t, 128 + g * 64:128 + (g + 1) * 64], start=(t == 0), stop=(t == 1))
                    return ins
                S.pe(ocm, r=['sc', ('kvc', 0), ('kvc', 1)], w=[pk(2)])
                S.dve(lambda e: e.tensor_reduce(out=pcg, in_=sc.rearrange("p t (g r) -> p t g r", g=2), axis=AX.X, op=ALU.add),
                      r=['sc'], w=['pcg'])

                def trp(e):
                    ins = None
                    for t in range(2):
                        ins = e.transpose(out=PS(3, 0, 2, t * 128, (t + 1) * 128), in_=pcg[:, t, :], identity=ident_f)
                    return ins
                S.pe(trp, r=['pcg', 'ident_f'], w=[pk(3)])
                S.act(lambda e: e.copy(out=impn, in_=PS(3, 0, 2, 0, 256)), r=[pk(3)], w=['impn'])
                S.dve(lambda e: e.tensor_reduce(out=impd[:, 0:128], in_=impn.rearrange("p (a b) -> p a b", b=2), axis=AX.X,
                                                op=ALU.add), r=['impn'], w=['impd'])

                S.pool(lambda e: e.memset(impd[:, 0:1], FORCE), r=['impd'], w=['impd'])
                S.pool(lambda e: e.memset(impd[:, 127:129], FORCE), r=['impd'], w=['impd'])
                S.dve(lambda e: e.max(out=m8d[:, 0:8], in_=impd), r=['impd'], w=['m8da'])
                S.dve(lambda e: e.match_replace(out=impd2, in_to_replace=m8d[:, 0:8], in_values=impd, imm_value=-3.0e38),
                      r=['impd', 'm8da'], w=['impd2'])
                S.dve(lambda e: e.max(out=m8d[:, 8:16], in_=impd2), r=['impd2'], w=['m8db'])
                S.dve(lambda e: e.tensor_scalar(out=Md, in0=impd, scalar1=m8d[:, 15:16], scalar2=None, op0=ALU.is_ge),
                      r=['impd', 'm8db'], w=['Md'])
                S.pe(lambda e: e.transpose(out=PS(3, 0, 128, 256, 258), in_=Md[:, 0:128], identity=ident_f[0:2, 0:2]),
                     r=['Md', 'ident_f'], w=[pk(3)])
                S.act(lambda e: e.copy(out=MT, in_=PS(3, 0, 128, 256, 258)), r=[pk(3)], w=['MT'])

                def mskm(e):
                    ins = None
                    for t in range(2):
                        ins = e.matmul(PS(3, 0, 128, 260 + 2 * t, 262 + 2 * t), lhsT=Ex[:, t, :], rhs=MT, start=True, stop=True)
                    return ins
                S.pe(mskm, r=['MT', 'dsetup'], w=[pk(3)])
                S.act(lambda e: e.copy(out=mskd.rearrange("p t g -> p (t g)"), in_=PS(3, 0, 128, 260, 264)), r=[pk(3)],
                      w=['mskd'])
                for t in range(2):
                    gather('sel', s, t, t)
                    for g in range(2):
                        S.dve(lambda e, t=t, g=g: e.tensor_tensor(
                            out=tmp4, in0=XK[t][:, :, g * 64:(g + 1) * 64].unsqueeze(2).broadcast_to([128, 32, 4, 64]),
                            in1=qb[:, 4 * g:4 * g + 4, :].unsqueeze(1).broadcast_to([128, 32, 4, 64]), op=ALU.mult),
                            r=[('XK', t), 'qb'], w=['tmp4'])
                        S.dve(lambda e, t=t, g=g: e.tensor_reduce(
                            out=ss[:, t, 4 * g:4 * g + 4, :].rearrange("p r l -> p l r"), in_=tmp4, axis=AX.X, op=ALU.add),
                            r=['tmp4'], w=[('ss', t, g)])
                SSK = [('ss', t, g) for t in range(2) for g in range(2)]
                S.dve(lambda e: e.tensor_tensor(out=ss, in0=ss, in1=ABs, op=ALU.add), r=SSK + ['dsetup2'], w=['ss'])
                S.act(lambda e: e.activation(out=ss, in_=ss, func=AF.Exp), r=['ss'], w=['ss'])
                for t in range(2):
                    S.dve(lambda e, t=t: e.tensor_tensor(
                        out=ss[:, t, :, :].rearrange("p (g r) l -> p g (r l)", g=2),
                        in0=ss[:, t, :, :].rearrange("p (g r) l -> p g (r l)", g=2),
                        in1=mskd[:, t, :].unsqueeze(2).broadcast_to([128, 2, 128]), op=ALU.mult),
                        r=['ss', 'mskd'], w=['ss'])
                S.dve(lambda e: e.tensor_reduce(out=Pl, in_=ss, axis=AX.X, op=ALU.add), r=['ss'], w=['Pl'])

                def pvs(e, s=s):
                    ins = None
                    for g in range(2):
                        k = 0
                        for t in range(2):
                            for l in range(32):
                                ins = e.matmul(PS(4, 0, 4, g * 64, (g + 1) * 64), lhsT=ss[:, t, 4 * g:4 * g + 4, l],
                                               rhs=XV[t][:, l, g * 64:(g + 1) * 64], start=(k == 0), stop=False)
                                k += 1
                        ins = e.matmul(PS(4, 0, 4, g * 64, (g + 1) * 64), lhsT=pnew[0:1, 0, g, :, s],
                                       rhs=vrow[0:1, s, 0, g * 64:(g + 1) * 64], start=False, stop=True)
                        for t in range(2):
                            ins = e.matmul(PS(5, 0, 4, g, g + 1), lhsT=Pl[:, t, 4 * g:4 * g + 4], rhs=ones_f[:, 0:1],
                                           start=(t == 0), stop=False)
                        ins = e.matmul(PS(5, 0, 4, g, g + 1), lhsT=pnew[0:1, 0, g, :, s], rhs=ones_f[0:1, 0:1],
                                       start=False, stop=True)
                    return ins
                S.pe(pvs, r=['ss', 'Pl', ('XV', 0), ('XV', 1), 'pnew', ('vrow', s), 'ones_f'], w=[pk(4), pk(5)])
                S.dma('sp', lambda e, s=s: e.dma_start(out=Wn, in_=I['cache_win'][s * 512:(s + 1) * 512, :].rearrange(
                    "(p j) c -> p j c", j=4)), w=['Wn'])
                for g in range(2):
                    S.dve(lambda e, g=g: e.tensor_tensor(
                        out=tmp4[:, 0:4, :, :], in0=Wn[:, :, g * 64:(g + 1) * 64].unsqueeze(2).broadcast_to([128, 4, 4, 64]),
                        in1=qb[:, 4 * g:4 * g + 4, :].unsqueeze(1).broadcast_to([128, 4, 4, 64]), op=ALU.mult),
                        r=['Wn', 'qb'], w=['tmp4'])
                    S.dve(lambda e, g=g: e.tensor_reduce(out=sw[:, 4 * g:4 * g + 4, :].rearrange("p r l -> p l r"),
                                                         in_=tmp4[:, 0:4, :, :], axis=AX.X, op=ALU.add),
                          r=['tmp4'], w=[('sw', g)])
                S.dve(lambda e: e.tensor_tensor(out=sw, in0=sw, in1=ABw, op=ALU.add), r=[('sw', 0), ('sw', 1), 'dsetup2'],
                      w=['sw'])
                S.act(lambda e: e.activation(out=sw, in_=sw, func=AF.Exp), r=['sw'], w=['sw'])
                S.pool(lambda e: e.memset(sw[0:1, :, 0:1], 0.0), r=['sw'], w=['sw'])
                S.dve(lambda e: e.tensor_reduce(out=Plw, in_=sw, axis=AX.X, op=ALU.add), r=['sw'], w=['Plw'])

                def pvw(e, s=s):
                    ins = None
                    for g in range(2):
                        for l in range(4):
                            ins = e.matmul(PS(6, 0, 4, g * 64, (g + 1) * 64), lhsT=sw[:, 4 * g:4 * g + 4, l],
                                           rhs=Wn[:, l, 128 + g * 64:128 + (g + 1) * 64], start=(l == 0), stop=False)
                        ins = e.matmul(PS(6, 0, 4, g * 64, (g + 1) * 64), lhsT=pnew[0:1, 1, g, :, s],
                                       rhs=vrow[0:1, s, 1, g * 64:(g + 1) * 64], start=False, stop=True)
                        ins = e.matmul(PS(5, 0, 4, 2 + g, 3 + g), lhsT=Plw[:, 4 * g:4 * g + 4], rhs=ones_f[:, 0:1],
                                       start=True, stop=False)
                        ins = e.matmul(PS(5, 0, 4, 2 + g, 3 + g), lhsT=pnew[0:1, 1, g, :, s], rhs=ones_f[0:1, 0:1],
                                       start=False, stop=True)
                    return ins
                S.pe(pvw, r=['sw', 'Plw', 'Wn', 'pnew', ('vrow', s), 'ones_f'], w=[pk(6), pk(5)])
                S.dve(lambda e: e.reciprocal(out=rsd, in_=PS(5, 0, 4, 0, 4)), r=[pk(5)], w=['rsd'])
                for g in range(2):
                    S.dve(lambda e, g=g, s=s: e.tensor_scalar(out=accd[:, g, :], in0=PS(2, 0, 4, g * 64, (g + 1) * 64),
                                                              scalar1=gts[:, g * 3 + 0, s:s + 1], scalar2=None, op0=ALU.mult),
                          r=[pk(2), 'gts'], w=[('accd', g)])
                    for bi, (bank, col) in enumerate([(4, g), (6, 2 + g)]):
                        S.dve(lambda e, g=g, s=s, bi=bi, col=col: e.tensor_tensor(
                            out=rsd[:, col:col + 1], in0=rsd[:, col:col + 1], in1=gts[:, g * 3 + 1 + bi, s:s + 1], op=ALU.mult),
                            r=['rsd', 'gts'], w=['rsd'])
                        S.dve(lambda e, g=g, bank=bank, col=col: e.scalar_tensor_tensor(
                            out=accd[:, g, :], in0=PS(bank, 0, 4, g * 64, (g + 1) * 64), scalar=rsd[:, col:col + 1],
                            in1=accd[:, g, :], op0=ALU.mult, op1=ALU.add), r=[pk(bank), 'rsd', ('accd', g)], w=[('accd', g)])
                    S.pe(lambda e, g=g: e.transpose(out=PS(7, 0, 64, 128 + 4 * g, 132 + 4 * g), in_=accd[:, g, :],
                                                    identity=ident_f[0:4, 0:4]), r=[('accd', g), 'ident_f'], w=[pk(7)])
                    S.dve(lambda e, g=g, s=s: e.tensor_tensor(
                        out=ybT[0:64, 4 * g:4 * g + 4, T + s], in0=PS(7, 0, 64, 128 + 4 * g, 132 + 4 * g),
                        in1=ybT[0:64, 4 * g:4 * g + 4, T + s], op=ALU.mult),
                        r=[pk(7)] + [('ybT', 4 * g + r_, T) for r_ in range(4)],
                        w=[('ybT', 4 * g + r_, T) for r_ in range(4)] + ['dec_yb'])
            if os.environ.get('KDEV_DUMP'):
                yp = O['y_prompt']
                S.dma('sp', lambda e: e.dma_start(out=yp[0:128, :], in_=Xb[0][:, 0:4, :].rearrange("p l c -> p (l c)")), r=[('Xb', 0)])
                S.dma('sp', lambda e: e.dma_start(out=yp[128:256, 0:512], in_=kvc.rearrange("p t c -> p (t c)")), r=[('kvc', 0), ('kvc', 1)])
                S.dma('sp', lambda e: e.dma_start(out=yp[256:384, 0:16], in_=sc.rearrange("p t h -> p (t h)")), r=['sc'])
                S.dma('sp', lambda e: e.dma_start(out=yp[384:512, 0:512], in_=ss.rearrange("p t h l -> p (t h l)")), r=['ss'])
                S.dma('sp', lambda e: e.dma_start(out=yp[512:516, 0:4], in_=rsd), r=['rsd'])
                S.dma('sp', lambda e: e.dma_start(out=yp[516:520, 0:128], in_=accd.rearrange("p g d -> p (g d)")), r=[('accd', 0), ('accd', 1)])
                S.dma('sp', lambda e: e.dma_start(out=yp[640:768, 0:8], in_=idx_f), r=['idx_i'])
                S.dma('sp', lambda e: e.dma_start(out=yp[768:896, 0:4], in_=mskd.rearrange("p t g -> p (t g)")), r=['mskd'])
                S.dma('sp', lambda e: e.dma_start(out=yp[896:898, 0:129], in_=impd), r=['impd'])
                S.dma('sp', lambda e: e.dma_start(out=yp[900:1028, 0:512], in_=qb.rearrange("p h d -> p (h d)")), r=['qb'])
                S.dma('sp', lambda e: e.dma_start(out=yp[1028:1029, 0:64], in_=pnew.rearrange("p a g r s -> p (a g r s)")), r=['pnew'])
                S.dma('sp', lambda e: e.dma_start(out=yp[1030:1034, 0:24], in_=gts.rearrange("p a s -> p (a s)")), r=['gts'])
            S.barrier()
            A.release()
            stage_end(11)
            ALPHA = float((2 * 2) ** 0.25)
            x1T = xT
            A.mark()
            wo_a = A.alloc([128, 4, DM], BF16)
            wo_b = A.alloc([64, 8, DM], BF16)
            lnp = A.alloc([128, 2, DM], F32)
            xres = [A.alloc([128, DM], F32) for _ in range(2)]
            x1b = [A.alloc([128, DM], BF16) for _ in range(2)]
            bst = A.alloc([128, 2, 6], F32)
            mv = A.alloc([128, 4], F32)
            for j in range(4):
                S.dma('pool', lambda e, j=j: e.dma_start(out=wo_a[:, j, :], in_=I['w_out_even'][j * 128:(j + 1) * 128, :]),
                      w=[('wo_a', j)])
            for h in range(8):
                S.dma('pool', lambda e, h=h: e.dma_start(out=wo_b[:, h, :],
                                                         in_=I['w_out_even'][512 + h * 64:512 + (h + 1) * 64, :]),
                      w=[('wo_b', h)])
            S.dma('sp', lambda e: e.dma_start(out=lnp[:, 0, :], in_=I['ln_g'][0:1, :].partition_broadcast(128)), w=[('lnp', 0)])
            S.dma('sp', lambda e: e.dma_start(out=lnp[:, 1, :], in_=I['ln_b'][0:1, :].partition_broadcast(128)), w=[('lnp', 1)])
            WO = [('wo_a', j) for j in range(4)] + [('wo_b', h) for h in range(8)]

            def resid_ln(n, rows, c0, xsrc, yk, lidx, ya_, yb_, outs):
                b = n % 2
                S.dma('sp', lambda e: e.dma_start(out=xres[b][0:rows, :], in_=xsrc), w=[('xres', b)])

                def mm(e):
                    ins = None
                    for cb in range(2):
                        k = 0
                        tot = len(ya_) + len(yb_)
                        for (lt, rt) in ya_ + yb_:
                            ins = e.matmul(PS(cb, 0, rows, 0, 512), lhsT=lt[:, c0:c0 + rows], rhs=rt[:, cb * 512:(cb + 1) * 512],
                                           start=(k == 0), stop=(k == tot - 1))
                            k += 1
                    return ins
                S.pe(mm, r=yk + WO, w=[pk(0), pk(1)])
                for cb in range(2):
                    S.dve(lambda e, cb=cb: e.scalar_tensor_tensor(
                        out=xres[b][0:rows, cb * 512:(cb + 1) * 512], in0=xres[b][0:rows, cb * 512:(cb + 1) * 512],
                        scalar=ALPHA, in1=PS(cb, 0, rows, 0, 512), op0=ALU.mult, op1=ALU.add),
                        r=[('xres', b), pk(cb)], w=[('xres', b)])
                    S.dve(lambda e, cb=cb: e.bn_stats(out=bst[0:rows, cb, :], in_=xres[b][0:rows, cb * 512:(cb + 1) * 512]),
                          r=[('xres', b)], w=[('bst', cb)])
                S.dve(lambda e: e.bn_aggr(out=mv[0:rows, 0:2], in_=bst[0:rows, :, :].rearrange("p a b -> p (a b)")),
                      r=[('bst', 0), ('bst', 1)], w=['mv'])
                S.dve(lambda e: e.tensor_scalar(out=mv[0:rows, 2:3], in0=mv[0:rows, 1:2], scalar1=1e-5, scalar2=None,
                                                op0=ALU.add), r=['mv'], w=['mv2'])
                S.act(lambda e: e.activation(out=mv[0:rows, 2:3], in_=mv[0:rows, 2:3], func=AF.Ln), r=['mv2'], w=['mv2'])
                S.act(lambda e: e.activation(out=mv[0:rows, 2:3], in_=mv[0:rows, 2:3], func=AF.Exp, scale=-0.5),
                      r=['mv2'], w=['mv2'])
                S.dve(lambda e: e.tensor_scalar(out=xres[b][0:rows, :], in0=xres[b][0:rows, :], scalar1=mv[0:rows, 0:1],
                                                scalar2=mv[0:rows, 2:3], op0=ALU.subtract, op1=ALU.mult),
                      r=[('xres', b), 'mv', 'mv2'], w=[('xres', b)])
                S.dve(lambda e: e.tensor_tensor(out=xres[b][0:rows, :], in0=xres[b][0:rows, :], in1=lnp[0:rows, 2 * lidx, :],
                                                op=ALU.mult), r=[('xres', b), ('lnp', 2 * lidx)], w=[('xres', b)])
                S.pool(lambda e: e.tensor_tensor(out=xres[b][0:rows, :], in0=xres[b][0:rows, :],
                                                 in1=lnp[0:rows, 2 * lidx + 1, :], op=ALU.add),
                       r=[('xres', b), ('lnp', 2 * lidx + 1)], w=[('xres', b)])
                outs(b)

            def l0_outs(n, rows, c0):
                def f(b):
                    S.dma('sp', lambda e: e.dma_start(out=x1_scr[c0:c0 + rows, :], in_=xres[b][0:rows, :]),
                          r=[('xres', b)], w=[('x1scr', n)])
                    if n == 15:
                        S.dma('sp', lambda e: e.dma_start(out=O['shift_p'], in_=xres[b][127:128, :]), r=[('xres', b)])
                    if n == 16:
                        S.dma('sp', lambda e: e.dma_start(out=O['shift_s'], in_=xres[b][0:rows, :]), r=[('xres', b)])
                    S.act(lambda e: e.copy(out=x1b[b][0:rows, :], in_=xres[b][0:rows, :]), r=[('xres', b)], w=[('x1b', b)])
                    bank = 2 + (n % 2)

                    def tr(e):
                        ins = None
                        for c in range(8):
                            ins = e.transpose(out=PSB(bank, 0, 128, c * rows, (c + 1) * rows),
                                              in_=x1b[b][0:rows, c * 128:(c + 1) * 128], identity=ident_b[0:rows, 0:rows])
                        return ins
                    S.pe(tr, r=[('x1b', b), 'ident_b'], w=[pk(bank)])
                    S.dve(lambda e: e.tensor_copy(out=x1T[:, :, c0:c0 + rows],
                                                  in_=PSB(bank, 0, 128, 0, 8 * rows).rearrange("p (c t) -> p c t", c=8)),
                          r=[pk(bank)], w=[('x1T', n)])
                return f

            for n in range(17):
                rows = 128 if n < 16 else NS
                c0 = n * 128 if n < 16 else T
                xsrc = I['xp'][c0:c0 + 128, :] if n < 16 else I['xs']
                blk = (c0 // 512) * 512 if n < 16 else T
                yk = [('yaT', j, blk) for j in range(4)] + [('ybT', h, blk) for h in range(8)]
                if n == 16:
                    yk += ['dec_yb']
                ya_ = [(yaT[:, j, :], wo_a[:, j, :]) for j in range(4)]
                yb_ = [(ybT[0:64, h, :], wo_b[0:64, h, :]) for h in range(8)]
                resid_ln(n, rows, c0, xsrc, yk, 0, ya_, yb_, l0_outs(n, rows, c0))
            S.barrier()
            A.release()
            stage_end(10)
            S.barrier()
            A.top = L1_BASE
            A.marks = []
            A.mark()
            dT = A.alloc([128, 8, TT], BF16)
            xsh = A.alloc([128, 8, TT], BF16)
            muT = A.alloc([128, 6, 8], F32)
            vecs = A.alloc([128, 5, 8], F32)
            shb = A.alloc([NS, DM], BF16)
            stg = [A.alloc([128, 1024], F32) for _ in range(3)]
            wl1 = [A.alloc([128, 8, 128], BF16) for _ in range(3)]
            wfull = A.alloc([128, 8, 1024], BF16)
            lo1 = A.alloc([128, 8, 64], BF16)
            lo2 = A.alloc([64, 1024], BF16)
            t1T = A.alloc([64, TT], BF16)
            S.dma('sp', lambda e: e.dma_start(out=muT, in_=I['mu_c'].rearrange("i (c p) -> p i c", p=128),
                                              allow_slow_non_contiguous=True), w=['muT'])
            for i, nm in enumerate(['w0', 'a0', 'k_k', 'k_a', 'r_k']):
                S.dma('sp', lambda e, i=i, nm=nm: e.dma_start(out=vecs[:, i, :], in_=I[nm].rearrange("o (c p) -> p (o c)", p=128),
                                                              allow_slow_non_contiguous=True), w=[('vecs', i)])
            S.dma('pool', lambda e: e.dma_start(out=shb, in_=I['state_shift']), w=['shb'])

            def trsh(e):
                ins = None
                for c in range(8):
                    ins = e.transpose(out=PSB(0, 0, 128, c * NS, (c + 1) * NS), in_=shb[0:NS, c * 128:(c + 1) * 128],
                                      identity=ident_b[0:NS, 0:NS])
                return ins
            S.pe(trsh, r=['shb', 'ident_b'], w=[pk(0)])
            X1K = [('x1T', n) for n in range(17)]
            S.dve(lambda e: e.tensor_tensor(out=dT[:, :, T:TT], in0=PSB(0, 0, 128, 0, 8 * NS).rearrange("p (c t) -> p c t", c=8),
                                            in1=x1T[:, :, T:TT], op=ALU.subtract), r=[pk(0)] + X1K, w=['dT'])
            S.dve(lambda e: e.tensor_tensor(out=dT[:, :, 1:T], in0=x1T[:, :, 0:T - 1], in1=x1T[:, :, 1:T], op=ALU.subtract),
                  r=X1K, w=['dT'])
            S.dve(lambda e: e.tensor_scalar(out=dT[:, :, 0:1], in0=x1T[:, :, 0:1], scalar1=-1.0, scalar2=None, op0=ALU.mult),
                  r=X1K, w=['dT'])

            def mk_xsh(i):
                for c in range(8):
                    eng = S.dve if c % 2 == 0 else S.pool
                    if True:
                        S.dve(lambda e, c=c: e.scalar_tensor_tensor(out=xsh[:, c, :], in0=dT[:, c, :], scalar=muT[:, i, c:c + 1],
                                                                    in1=x1T[:, c, :], op0=ALU.mult, op1=ALU.add),
                              r=['dT', 'muT'] + X1K, w=[('xsh', c)])
                    else:
                        S.pool(lambda e, c=c: e.tensor_scalar(out=xsh[:, c, :], in0=dT[:, c, :], scalar1=muT[:, i, c:c + 1],
                                                              scalar2=None, op0=ALU.mult), r=['dT', 'muT'], w=[('xsh', c)])
                        S.pool(lambda e, c=c: e.tensor_tensor(out=xsh[:, c, :], in0=xsh[:, c, :], in1=x1T[:, c, :], op=ALU.add),
                               r=[('xsh', c)] + X1K, w=[('xsh', c)])
            XSH = [('xsh', c) for c in range(8)]
            wrk_t = I['w_rkvz'].rearrange("(i c p) f -> i p c f", i=4, p=128)
            sgc = [0]
            wlc = [0]

            def proj_fm1(widx, dst, func, bias_i):
                for pr in range(8):
                    wi = wlc[0] % 3
                    wlc[0] += 1
                    S.dma('pool', lambda e, wi=wi, pr=pr: e.dma_start(out=wl1[wi], in_=wrk_t[widx][:, :, pr * 128:(pr + 1) * 128]),
                          w=[('wl1', wi)])
                    si = sgc[0] % 3
                    sgc[0] += 1
                    for bi_, (c0, n) in enumerate(BLKS):
                        bank = pbank[0] % 4
                        pbank[0] += 1

                        def mm(e, wi=wi, c0=c0, n=n, bank=bank):
                            ins = None
                            for c in range(8):
                                ins = e.matmul(PS(bank, 0, 128, 0, n), lhsT=wl1[wi][:, c, :], rhs=xsh[:, c, c0:c0 + n],
                                               start=(c == 0), stop=(c == 7))
                            return ins
                        S.pe(mm, r=[('wl1', wi)] + XSH, w=[pk(bank)])
                        cc0 = c0 if c0 < T else 0
                        sdst = stg[si][:, cc0 % 1024:cc0 % 1024 + n] if c0 < T else stg[si][:, 0:n]
                        S.act(lambda e, bank=bank, n=n, sdst=sdst: e.copy(out=sdst, in_=PS(bank, 0, 128, 0, n)),
                              r=[pk(bank)], w=[('stg', si)])
                        if bi_ in (1, 3, 4):
                            lo = {1: 0, 3: 1024, 4: T}[bi_]
                            wdt = 1024 if bi_ != 4 else NS
                            S.dma('sp', lambda e, si=si, pr=pr, lo=lo, wdt=wdt: e.dma_start(out=dst[:, pr, lo:lo + wdt],
                                                                                           in_=stg[si][:, 0:wdt]),
                                  r=[('stg', si)], w=[('scr', widx, pr)])
                            if bi_ != 4:
                                si = sgc[0] % 3
                                sgc[0] += 1

            def proj_tm1(widx, dst, silu):
                for c in range(8):
                    S.dma('pool', lambda e, c=c: e.dma_start(out=wfull[:, c, :], in_=wrk_t[widx][:, c, :]), w=[('wfull', c)])
                WF = [('wfull', c) for c in range(8)]
                for n in range(17):
                    rows = 128 if n < 16 else NS
                    c0 = n * 128 if n < 16 else T
                    si = sgc[0] % 3
                    sgc[0] += 1

                    def mm(e, rows=rows, c0=c0):
                        ins = None
                        for cb in range(2):
                            for c in range(8):
                                ins = e.matmul(PS(4 + cb, 0, rows, 0, 512), lhsT=xsh[:, c, c0:c0 + rows],
                                               rhs=wfull[:, c, cb * 512:(cb + 1) * 512], start=(c == 0), stop=(c == 7))
                        return ins
                    S.pe(mm, r=WF + XSH, w=[pk(4), pk(5)])
                    for cb in range(2):
                        if silu:
                            S.act(lambda e, cb=cb, rows=rows, si=si: e.activation(
                                out=stg[si][0:rows, cb * 512:(cb + 1) * 512], in_=PS(4 + cb, 0, rows, 0, 512), func=AF.Silu),
                                r=[pk(4 + cb)], w=[('stg', si)])
                        else:
                            S.act(lambda e, cb=cb, rows=rows, si=si: e.copy(out=stg[si][0:rows, cb * 512:(cb + 1) * 512],
                                                                            in_=PS(4 + cb, 0, rows, 0, 512)),
                                  r=[pk(4 + cb)], w=[('stg', si)])
                    S.dma('sp', lambda e, rows=rows, c0=c0, si=si: e.dma_start(out=dst[c0:c0 + rows, :], in_=stg[si][0:rows, :]),
                          r=[('stg', si)], w=[('scrt', widx, n)])

            def proj_lora(mi, w1n, w2n, vi, dst, use_tanh):
                S.dma('pool', lambda e: e.dma_start(out=lo1, in_=I[w1n].rearrange("(c p) f -> p c f", p=128)), w=['lo1'])
                S.dma('pool', lambda e: e.dma_start(out=lo2, in_=I[w2n]), w=['lo2'])
                for (c0, n) in BLKS:
                    bank = pbank[0] % 4
                    pbank[0] += 1

                    def mm(e, c0=c0, n=n, bank=bank):
                        ins = None
                        for c in range(8):
                            ins = e.matmul(PS(bank, 0, 64, 0, n), lhsT=lo1[:, c, :], rhs=xsh[:, c, c0:c0 + n],
                                           start=(c == 0), stop=(c == 7))
                        return ins
                    S.pe(mm, r=['lo1'] + XSH, w=[pk(bank)])
                    if use_tanh:
                        S.act(lambda e, c0=c0, n=n, bank=bank: e.activation(out=t1T[:, c0:c0 + n], in_=PS(bank, 0, 64, 0, n),
                                                                            func=AF.Tanh), r=[pk(bank)], w=[('t1T', c0)])
                    else:
                        S.act(lambda e, c0=c0, n=n, bank=bank: e.copy(out=t1T[:, c0:c0 + n], in_=PS(bank, 0, 64, 0, n)),
                              r=[pk(bank)], w=[('t1T', c0)])
                T1K = [('t1T', c0) for (c0, n) in BLKS]
                for pr in range(8):
                    si = sgc[0] % 3
                    sgc[0] += 1
                    for bi_, (c0, n) in enumerate(BLKS):
                        bank = pbank[0] % 4
                        pbank[0] += 1
                        S.pe(lambda e, pr=pr, c0=c0, n=n, bank=bank: e.matmul(
                            PS(bank, 0, 128, 0, n), lhsT=lo2[:, pr * 128:(pr + 1) * 128], rhs=t1T[:, c0:c0 + n],
                            start=True, stop=True), r=['lo2'] + T1K, w=[pk(bank)])
                        sdst = stg[si][:, c0 % 1024:c0 % 1024 + n] if c0 < T else stg[si][:, 0:n]
                        S.act(lambda e, bank=bank, n=n, sdst=sdst, pr=pr: e.activation(
                            out=sdst, in_=PS(bank, 0, 128, 0, n), func=AF.Sigmoid, bias=vecs[:, vi, pr:pr + 1]),
                            r=[pk(bank), ('vecs', vi)], w=[('stg', si)])
                        if bi_ in (1, 3, 4):
                            lo = {1: 0, 3: 1024, 4: T}[bi_]
                            wdt = 1024 if bi_ != 4 else NS
                            S.dma('sp', lambda e, si=si, pr=pr, lo=lo, wdt=wdt: e.dma_start(out=dst[:, pr, lo:lo + wdt],
                                                                                           in_=stg[si][:, 0:wdt]),
                                  r=[('stg', si)], w=[('scr', mi, pr)])
                            if bi_ != 4:
                                si = sgc[0] % 3
                                sgc[0] += 1

            mk_xsh(0)
            proj_fm1(0, r_scr, None, None)
            mk_xsh(1)
            proj_fm1(1, k_scr, None, None)
            mk_xsh(2)
            proj_tm1(2, v_scr, False)
            mk_xsh(3)
            proj_tm1(3, z_scr, True)
            mk_xsh(4)
            proj_lora(4, 'w1', 'w2', 0, sw_scr, True)
            mk_xsh(5)
            proj_lora(5, 'a1', 'a2', 1, a_scr, False)
            S.barrier()
            A.release()
            stage_end(12)
            A.top = xT_off
            A.marks = []
            A.mark()
            C = 64
            NEG_E05 = -float(np.exp(-0.5))
            wo_o = A.alloc([128, 8, DM], BF16)
            for c in range(8):
                S.dma('pool', lambda e, c=c: e.dma_start(out=wo_o[:, c, :], in_=I['w_out_odd'][c * 128:(c + 1) * 128, :]),
                      w=[('wo_o', c)])
            WOO = [('wo_o', c) for c in range(8)]
            gnp = A.alloc([64, 4, DM], F32)
            S.dma('sp', lambda e: e.dma_start(out=gnp[:, 0, :], in_=I['gn_g'].partition_broadcast(64)), w=[('gnp', 0)])
            S.dma('sp', lambda e: e.dma_start(out=gnp[:, 1, :], in_=I['gn_b'].partition_broadcast(64)), w=[('gnp', 1)])
            S.dma('sp', lambda e: e.dma_start(out=gnp[:, 2, :], in_=I['ln_g'][1:2, :].partition_broadcast(64)), w=[('gnp', 2)])
            S.dma('sp', lambda e: e.dma_start(out=gnp[:, 3, :], in_=I['ln_b'][1:2, :].partition_broadcast(64)), w=[('gnp', 3)])
            vec1 = A.alloc([64, 5, 16], F32)
            for i, nm in enumerate(['w0', 'a0', 'k_k', 'k_a', 'r_k']):
                S.dma('sp', lambda e, i=i, nm=nm: e.dma_start(out=vec1[:, i, :], in_=I[nm].rearrange("o (ph c) -> c (o ph)", c=64),
                                                              allow_slow_non_contiguous=True), w=['vec1'])
            m_lt = A.alloc([64, 64], F32)
            m_le = A.alloc([64, 64], F32)
            m_gt = A.alloc([64, 64], F32)
            blk1 = A.alloc([64, 64], F32)
            sel2 = A.alloc([64, 1], BF16)
            ones_t = A.alloc([64, 64], F32)

            def l1const(e):
                e.memset(ones_t, 1.0)
                e.memset(m_lt, 1.0)
                e.memset(m_le, 1.0)
                e.memset(m_gt, 1.0)
                e.affine_select(out=m_lt, in_=m_lt, pattern=[[1, 64]], compare_op=ALU.is_ge, fill=freg(e, 0.0), base=-1,
                                channel_multiplier=-1)
                e.affine_select(out=m_le, in_=m_le, pattern=[[1, 64]], compare_op=ALU.is_ge, fill=freg(e, 0.0), base=0,
                                channel_multiplier=-1)
                e.affine_select(out=m_gt, in_=m_gt, pattern=[[-1, 64]], compare_op=ALU.is_ge, fill=freg(e, 0.0), base=-1,
                                channel_multiplier=1)
                e.memset(blk1, 1.0)
                return e.memset(sel2, 1.0)
            S.pool(l1const, w=['l1c'])

            ST = A.alloc([64, 16, 64], F32)
            STb = A.alloc([64, 16, 64], BF16)
            fr = A.alloc([64, 16, C], F32)
            fk = A.alloc([64, 16, C], F32)
            fw = A.alloc([64, 16, C], F32)
            fa = A.alloc([64, 16, C], F32)
            cl = A.alloc([64, 16, C], F32)
            L1 = A.alloc([64, 16, C], F32)
            L2 = A.alloc([64, 16, C], F32)
            L3 = A.alloc([64, 16, C], F32)
            Lend = A.alloc([64, 16], F32)
            t_a = A.alloc([64, 16, C], F32)
            t_b = A.alloc([64, 16, C], F32)
            kap = fw
            kmd = A.alloc([64, 16, C], F32)
            kt_b = A.alloc([64, 16, C], BF16)
            bt_b = A.alloc([64, 16, C], BF16)
            ktl_b = A.alloc([64, 16, C], BF16)
            rt_b = A.alloc([64, 16, C], BF16)
            bh_f = A.alloc([64, 16, C], BF16)
            kh_f = A.alloc([64, 16, C], BF16)
            pr_b = A.alloc([64, 16, C], BF16)
            Bh = A.alloc([64, DM], BF16)
            Kh = A.alloc([64, DM], BF16)
            Vf2 = [A.alloc([64, DM], F32) for _ in range(2)]
            Vb = A.alloc([64, DM], BF16)
            Zf2 = [A.alloc([64, DM], F32) for _ in range(2)]
            gN = A.alloc([64, 16, 64], BF16)
            gNT = A.alloc([64, 16, 64], BF16)
            gAk = A.alloc([64, 16, 64], BF16)
            gBb = A.alloc([64, 16, 64], BF16)
            gBk = A.alloc([64, 16, 64], BF16)
            gX = [A.alloc([64, 16, 64], BF16) for _ in range(2)]
            gP = [A.alloc([64, 16, 64], BF16) for _ in range(2)]
            gPT = [A.alloc([64, 16, 64], BF16) for _ in range(2)]
            Rm = A.alloc([64, DM], BF16)
            Ub = A.alloc([64, DM], BF16)
            Yf = A.alloc([64, DM], F32)
            Yc = A.alloc([64, DM], F32)
            st1 = A.alloc([64, 16], F32)
            st2 = A.alloc([64, 16], F32)
            rkb2 = [A.alloc([64, 16], F32) for _ in range(2)]
            gb = A.alloc([64, DM], BF16)
            gT = A.alloc([128, 8, 64], BF16)
            xr2 = [A.alloc([64, DM], F32) for _ in range(2)]
            bst1 = A.alloc([64, 2, 6], F32)
            mv1 = A.alloc([64, 4], F32)
            wko = A.alloc([64, 16, 64], F32)
            stin2 = A.alloc([128, 8, 64], F32)

            print('L1B_TOP', A.top)

            def hp_rows(ap3, h):
                return ap3[:, h, :]

            def chunk(cols, ncol, first, xsrc_rows, yout, yrows, ck, par, post_seq=None):
                Vf, Zf, xr, rkb = Vf2[par], Zf2[par], xr2[par], rkb2[par]
                kVf, kZf, kxr, krkb = ('Vf', par), ('Zf', par), ('xr', par), ('rkb', par)
                pad = ncol < C
                if pad:
                    cols = cols - (C - 1)
                for (tile_, scr, nm) in ((fr, r_scr, 'fr'), (fk, k_scr, 'fk'), (fw, sw_scr, 'fw'), (fa, a_scr, 'fa')):
                    S.dma('sp', lambda e, tile_=tile_, scr=scr: e.dma_start(
                        out=tile_.rearrange("c (pr hp) t -> c pr hp t", hp=2),
                        in_=scr.rearrange("(hp c) pr t -> c pr hp t", hp=2)[:, :, :, cols:cols + C]), r=[('scr_all',)], w=[nm])
                    if pad:
                        S.pool(lambda e, tile_=tile_: e.memset(tile_[:, :, 0:C - 1], 0.0), r=[nm], w=[nm])
                S.dma('sp', lambda e: e.dma_start(out=Vf, in_=v_scr[cols:cols + C, :]), r=[('scr_all',)], w=[kVf])
                S.dma('sp', lambda e: e.dma_start(out=Zf, in_=z_scr[cols:cols + C, :]), r=[('scr_all',)], w=[kZf])
                S.dma('sp', lambda e: e.dma_start(out=xr, in_=x1_scr[cols:cols + C, :]), r=[('scr_all',)], w=[kxr])
                if pad:
                    S.pool(lambda e: e.memset(Vf[0:C - 1, :], 0.0), r=[kVf], w=[kVf])
                    S.pool(lambda e: e.memset(Zf[0:C - 1, :], 0.0), r=[kZf], w=[kZf])
                S.act(lambda e: e.copy(out=Vb, in_=Vf), r=[kVf], w=['Vb'])
                S.dve(lambda e: e.tensor_scalar(out=fw, in0=fw, scalar1=NEG_E05, scalar2=None, op0=ALU.mult), r=['fw'], w=['fw'])

                def scans(e):
                    ins = None
                    for pr_ in range(16):
                        ins = e.tensor_tensor_scan(out=cl[:, pr_, :], data0=ones_t, data1=fw[:, pr_, :], initial=0.0,
                                                   op0=ALU.mult, op1=ALU.add)
                    return ins
                S.dve(scans, r=['fw', 'l1c'], w=['cl'])
                S.act(lambda e: e.activation(out=L1, in_=cl, func=AF.Exp), r=['cl'], w=['L1'])
                S.act(lambda e: e.activation(out=L3, in_=cl, func=AF.Exp, scale=-1.0), r=['cl'], w=['L3'])
                S.dve(lambda e: e.tensor_tensor(out=t_a, in0=cl, in1=fw, op=ALU.subtract), r=['cl', 'fw'], w=['t_a'])
                S.act(lambda e: e.activation(out=L2, in_=t_a, func=AF.Exp), r=['t_a'], w=['L2'])
                S.dve(lambda e: e.tensor_copy(out=Lend, in_=L1[:, :, C - 1]), r=['L1'], w=['Lend'])
                S.dve(lambda e: e.tensor_tensor(out=kap, in0=fk, in1=vec1[:, 2, :].unsqueeze(2).broadcast_to([64, 16, C]),
                                                op=ALU.mult), r=['fk', 'vec1'], w=['fw'])
                S.dve(lambda e: e.tensor_tensor(out=t_b, in0=kap, in1=kap, op=ALU.mult), r=['fw'], w=['t_b'])
                def ssqm(e):
                    e.matmul(PS(0, 0, 64, 0, 512), lhsT=blk1, rhs=t_b[:, 0:8, :].rearrange("p a t -> p (a t)"), start=True, stop=True)
                    return e.matmul(PS(1, 0, 64, 0, 512), lhsT=blk1, rhs=t_b[:, 8:16, :].rearrange("p a t -> p (a t)"),
                                    start=True, stop=True)
                S.pe(ssqm, r=['t_b', 'l1c'], w=[pk(0), pk(1)])
                for hb in range(2):
                    S.dve(lambda e, hb=hb: e.tensor_scalar(out=t_b[:, hb * 8:(hb + 1) * 8, :].rearrange("p a t -> p (a t)"),
                                                           in0=PS(hb, 0, 64, 0, 512), scalar1=1e-24, scalar2=None, op0=ALU.max),
                          r=[pk(hb)], w=['t_b'])
                S.act(lambda e: e.activation(out=t_b, in_=t_b, func=AF.Ln), r=['t_b'], w=['t_b'])
                S.act(lambda e: e.activation(out=t_b, in_=t_b, func=AF.Exp, scale=-0.5), r=['t_b'], w=['t_b'])
                S.dve(lambda e: e.tensor_tensor(out=kap, in0=kap, in1=t_b, op=ALU.mult), r=['fw', 't_b'], w=['fw'])
                S.pool(lambda e: e.tensor_scalar(out=cl, in0=fa, scalar1=-1.0, scalar2=None, op0=ALU.add), r=['fa', 'cl', 'L2'], w=['cl'])
                S.pool(lambda e: e.tensor_tensor(out=cl, in0=cl, in1=vec1[:, 3, :].unsqueeze(2).broadcast_to([64, 16, C]),
                                                 op=ALU.mult), r=['cl', 'vec1'], w=['cl'])
                S.dve(lambda e: e.scalar_tensor_tensor(out=kmd, in0=cl, scalar=1.0, in1=fk, op0=ALU.add, op1=ALU.mult),
                      r=['cl', 'fk'], w=['kmd'])
                S.dve(lambda e: e.tensor_tensor(out=kt_b, in0=kap, in1=L2, op=ALU.mult), r=['fw', 'L2'], w=['kt_b'])
                S.dve(lambda e: e.tensor_tensor(out=t_a, in0=kap, in1=fa, op=ALU.mult), r=['fw', 'fa', 'kmd'], w=['t_a'])
                S.dve(lambda e: e.tensor_tensor(out=t_a, in0=t_a, in1=L3, op=ALU.mult), r=['t_a', 'L3'], w=['t_a'])
                S.act(lambda e: e.copy(out=bt_b, in_=t_a), r=['t_a'], w=['bt_b'])
                S.dve(lambda e: e.tensor_tensor(out=bh_f, in0=t_a, in1=Lend.unsqueeze(2).broadcast_to([64, 16, C]), op=ALU.mult),
                      r=['t_a', 'Lend'], w=['bh_f'])
                S.dve(lambda e: e.tensor_tensor(out=t_b, in0=kmd, in1=L3, op=ALU.mult), r=['kmd', 'L3', 'fw'], w=['t_b'])
                S.act(lambda e: e.copy(out=ktl_b, in_=t_b), r=['t_b'], w=['ktl_b'])
                S.dve(lambda e: e.tensor_tensor(out=kh_f, in0=t_b, in1=Lend.unsqueeze(2).broadcast_to([64, 16, C]), op=ALU.mult),
                      r=['t_b', 'Lend'], w=['kh_f'])
                S.pool(lambda e: e.tensor_tensor(out=rt_b, in0=fr, in1=L1, op=ALU.mult), r=['fr', 'L1'], w=['rt_b'])
                S.pool(lambda e: e.tensor_tensor(out=L2, in0=fr, in1=kmd, op=ALU.mult), r=['fr', 'kmd', 'L2', 'kt_b'], w=['L2'])
                S.pool(lambda e: e.tensor_tensor(out=pr_b, in0=L2, in1=vec1[:, 4, :].unsqueeze(2).broadcast_to([64, 16, C]),
                                                 op=ALU.mult), r=['L2', 'vec1'], w=['pr_b'])
                for (src, dst_, nm, bank) in ((bh_f, Bh, 'Bh', 2), (kh_f, Kh, 'Kh', 3)):
                    def trb(e, src=src, bank=bank):
                        ins = None
                        for h in range(16):
                            ins = e.transpose(out=PSB(bank, 0, 64, h * 64, (h + 1) * 64), in_=src[:, h, :],
                                              identity=ident_b[0:64, 0:64])
                        return ins
                    S.pe(trb, r=[nm.lower() + '_f' if False else ('bh_f' if nm == 'Bh' else 'kh_f'), 'ident_b'], w=[pk(bank)])
                    S.act(lambda e, dst_=dst_, bank=bank: e.copy(out=dst_, in_=PSB(bank, 0, 64, 0, 1024)), r=[pk(bank)], w=[nm])

                def rkm(e):
                    ins = None
                    for h in range(16):
                        ins = e.matmul(PS(1, 0, 64, 2 * h, 2 * h + 1), lhsT=pr_b[:, h, :], rhs=sel2, start=True, stop=True)
                    return ins
                S.pe(rkm, r=['pr_b', 'l1c'], w=[pk(1)])
                S.act(lambda e: e.copy(out=rkb, in_=PS(1, 0, 64, 0, 32).rearrange("p (h two) -> p h two", two=2)[:, :, 0]),
                      r=[pk(1)], w=[krkb])
                def gram(lt, rt, dst_, mask, banks, nm, deps):
                    def gm(e):
                        ins = None
                        for h in range(16):
                            ins = e.matmul(PS(banks[h // 8], 0, 64, (h % 8) * 64, (h % 8 + 1) * 64), lhsT=hp_rows(lt, h),
                                           rhs=hp_rows(rt, h), start=True, stop=True)
                        return ins
                    S.pe(gm, r=deps, w=[pk(banks[0]), pk(banks[1])])
                    for hb in range(2):
                        S.dve(lambda e, hb=hb: e.tensor_tensor(
                            out=dst_[:, hb * 8:(hb + 1) * 8, :], in0=PS(banks[hb], 0, 64, 0, 512).rearrange("p (h t) -> p h t", h=8),
                            in1=mask.unsqueeze(1).broadcast_to([64, 8, 64]), op=ALU.mult),
                            r=[pk(banks[hb]), 'l1c'], w=[(nm, hb)])
                gram(bt_b, kt_b, gN, m_lt, (4, 5), 'gN', ['bt_b', 'kt_b'])
                gram(kt_b, bt_b, gNT, m_gt, (6, 7), 'gNT', ['bt_b', 'kt_b'])
                gram(ktl_b, kt_b, gAk, m_lt, (4, 5), 'gAk', ['ktl_b', 'kt_b'])
                gram(bt_b, rt_b, gBb, m_le, (6, 7), 'gBb', ['bt_b', 'rt_b'])
                gram(ktl_b, rt_b, gBk, m_le, (4, 5), 'gBk', ['ktl_b', 'rt_b'])
                for hb in range(2):
                    S.dve(lambda e, hb=hb: e.scalar_tensor_tensor(
                        out=gX[0][:, hb * 8:(hb + 1) * 8, :], in0=gN[:, hb * 8:(hb + 1) * 8, :], scalar=-1.0,
                        in1=ident_b[0:64, 0:64].unsqueeze(1).broadcast_to([64, 8, 64]), op0=ALU.mult, op1=ALU.add),
                        r=[('gN', hb), 'ident_b'], w=[('gX', 0, hb)])
                Pc, PTc, Pk, PTk = gN, gNT, 'gN', 'gNT'
                xi = 0
                for lev in range(0 if pad else 5):
                    Pn, PTn = gP[lev % 2], gPT[lev % 2]
                    Pnk, PTnk = ('gP', lev % 2), ('gPT', lev % 2)

                    def sq(e, Pc=Pc, PTc=PTc):
                        ins = None
                        for h in range(16):
                            ins = e.matmul(PS(h // 8, 0, 64, (h % 8) * 64, (h % 8 + 1) * 64), lhsT=PTc[:, h, :], rhs=Pc[:, h, :],
                                           start=True, stop=True)
                        for h in range(16):
                            ins = e.matmul(PS(2 + h // 8, 0, 64, (h % 8) * 64, (h % 8 + 1) * 64), lhsT=Pc[:, h, :],
                                           rhs=PTc[:, h, :], start=True, stop=True)
                        return ins
                    S.pe(sq, r=[(Pk, 0), (Pk, 1), (PTk, 0), (PTk, 1)] if isinstance(Pk, str) else
                         [Pk + (0,), Pk + (1,), PTk + (0,), PTk + (1,)], w=[pk(0), pk(1), pk(2), pk(3)])
                    for hb in range(2):
                        S.act(lambda e, hb=hb, Pn=Pn: e.copy(out=Pn[:, hb * 8:(hb + 1) * 8, :].rearrange("p h t -> p (h t)"),
                                                             in_=PS(hb, 0, 64, 0, 512)), r=[pk(hb)], w=[Pnk + (hb,)])
                        S.dve(lambda e, hb=hb, PTn=PTn: e.tensor_copy(out=PTn[:, hb * 8:(hb + 1) * 8, :].rearrange("p h t -> p (h t)"),
                                                                      in_=PS(2 + hb, 0, 64, 0, 512)), r=[pk(2 + hb)], w=[PTnk + (hb,)])
                    Xo, Xn = gX[xi % 2], gX[(xi + 1) % 2]

                    def xm(e, PTn=PTn, Xo=Xo):
                        ins = None
                        for h in range(16):
                            ins = e.matmul(PS(4 + h // 8, 0, 64, (h % 8) * 64, (h % 8 + 1) * 64), lhsT=PTn[:, h, :], rhs=Xo[:, h, :],
                                           start=True, stop=True)
                        return ins
                    S.pe(xm, r=[PTnk + (0,), PTnk + (1,), ('gX', xi % 2, 0), ('gX', xi % 2, 1)], w=[pk(4), pk(5)])
                    for hb in range(2):
                        S.dve(lambda e, hb=hb, Xo=Xo, Xn=Xn: e.tensor_tensor(
                            out=Xn[:, hb * 8:(hb + 1) * 8, :].rearrange("p h t -> p (h t)"), in0=PS(4 + hb, 0, 64, 0, 512),
                            in1=Xo[:, hb * 8:(hb + 1) * 8, :].rearrange("p h t -> p (h t)"), op=ALU.add),
                            r=[pk(4 + hb), ('gX', xi % 2, hb)], w=[('gX', (xi + 1) % 2, hb)])
                    xi += 1
                    Pc, PTc, Pk, PTk = Pn, PTn, Pnk, PTnk
                Xf = gX[xi % 2]
                XFK = [('gX', xi % 2, 0), ('gX', xi % 2, 1)]
                yield 'pre'
                if first is not None:
                    first()

                def m1(e):
                    ins = None
                    for h in range(16):
                        o_ = PS(h // 8, 0, 64, (h % 8) * 64, (h % 8 + 1) * 64)
                        e.matmul(o_, lhsT=hp_rows(kt_b, h), rhs=hp_rows(STb, h), start=True, stop=False)
                        ins = e.matmul(o_, lhsT=gAk[:, h, :], rhs=Vb[:, h * 64:(h + 1) * 64], start=False, stop=True)
                    return ins
                S.pe(m1, r=['kt_b', 'STb', ('gAk', 0), ('gAk', 1), 'Vb'], w=[pk(0), pk(1)])
                for hb in range(2):
                    S.act(lambda e, hb=hb: e.activation(out=Rm[:, hb * 512:(hb + 1) * 512], in_=PS(hb, 0, 64, 0, 512),
                                                        func=AF.Copy, scale=-1.0), r=[pk(hb)], w=[('Rm', hb)])

                def m3(e):
                    ins = None
                    for h in range(16):
                        ins = e.matmul(PS(2 + h // 8, 0, 64, (h % 8) * 64, (h % 8 + 1) * 64), lhsT=Xf[:, h, :],
                                       rhs=Rm[:, h * 64:(h + 1) * 64], start=True, stop=True)
                    return ins
                S.pe(m3, r=XFK + [('Rm', 0), ('Rm', 1)], w=[pk(2), pk(3)])
                for hb in range(2):
                    S.act(lambda e, hb=hb: e.copy(out=Ub[:, hb * 512:(hb + 1) * 512], in_=PS(2 + hb, 0, 64, 0, 512)),
                          r=[pk(2 + hb)], w=[('Ub', hb)])

                def m4(e):
                    ins = None
                    for h in range(16):
                        o_ = PS(4 + h // 8, 0, 64, (h % 8) * 64, (h % 8 + 1) * 64)
                        e.matmul(o_, lhsT=hp_rows(rt_b, h), rhs=hp_rows(STb, h), start=True, stop=False)
                        e.matmul(o_, lhsT=gBb[:, h, :], rhs=Ub[:, h * 64:(h + 1) * 64], start=False, stop=False)
                        ins = e.matmul(o_, lhsT=gBk[:, h, :], rhs=Vb[:, h * 64:(h + 1) * 64], start=False, stop=True)
                    return ins
                S.pe(m4, r=['rt_b', 'STb', ('gBb', 0), ('gBb', 1), ('gBk', 0), ('gBk', 1), ('Ub', 0), ('Ub', 1), 'Vb'],
                     w=[pk(4), pk(5)])
                for hb in range(2):
                    S.act(lambda e, hb=hb: e.copy(out=Yf[:, hb * 512:(hb + 1) * 512], in_=PS(4 + hb, 0, 64, 0, 512)),
                          r=[pk(4 + hb)], w=[('Yf', hb)])

                def m5(e):
                    ins = None
                    for h in range(16):
                        o_ = PS(6 + h // 8, 0, 64, (h % 8) * 64, (h % 8 + 1) * 64)
                        e.matmul(o_, lhsT=Bh[:, h * 64:(h + 1) * 64], rhs=Ub[:, h * 64:(h + 1) * 64], start=True, stop=False)
                        ins = e.matmul(o_, lhsT=Kh[:, h * 64:(h + 1) * 64], rhs=Vb[:, h * 64:(h + 1) * 64], start=False, stop=True)
                    return ins
                S.pe(m5, r=['Bh', 'Kh', ('Ub', 0), ('Ub', 1), 'Vb'], w=[pk(6), pk(7)])
                S.dve(lambda e: e.tensor_tensor(out=ST, in0=ST, in1=Lend.unsqueeze(2).broadcast_to([64, 16, 64]), op=ALU.mult),
                      r=['ST', 'Lend', 'STb'], w=['ST'])
                for hb in range(2):
                    S.dve(lambda e, hb=hb: e.tensor_tensor(out=ST[:, hb * 8:(hb + 1) * 8, :], in0=ST[:, hb * 8:(hb + 1) * 8, :],
                                                           in1=PS(6 + hb, 0, 64, 0, 512).rearrange("p (a v) -> p a v", a=8), op=ALU.add),
                          r=['ST', pk(6 + hb)], w=['ST'])
                S.act(lambda e: e.copy(out=STb, in_=ST), r=['ST'], w=['STb'])
                if post_seq is not None:
                    post_seq()
                yield 'seq'
                Y3 = Yf.rearrange("p (h v) -> p h v", h=16)
                Yc3 = Yc.rearrange("p (h v) -> p h v", h=16)
                YK = [('Yf', 0), ('Yf', 1)]
                S.dve(lambda e: e.tensor_reduce(out=st1, in_=Y3, axis=AX.X, op=ALU.add), r=YK, w=['st1'])
                S.dve(lambda e: e.tensor_scalar(out=st1, in0=st1, scalar1=1.0 / 64.0, scalar2=None, op0=ALU.mult), r=['st1'], w=['st1'])
                S.dve(lambda e: e.tensor_tensor(out=Yc3, in0=Y3, in1=st1.unsqueeze(2).broadcast_to([64, 16, 64]), op=ALU.subtract),
                      r=YK + ['st1'], w=['Yc'])
                S.pool(lambda e: e.tensor_tensor(out=Yf, in0=Yc, in1=Yc, op=ALU.mult), r=['Yc'] + YK, w=YK)
                S.dve(lambda e: e.tensor_reduce(out=st2, in_=Y3, axis=AX.X, op=ALU.add), r=YK, w=['st2'])
                S.dve(lambda e: e.tensor_scalar(out=st2, in0=st2, scalar1=1.0 / 64.0, scalar2=64e-5, op0=ALU.mult, op1=ALU.add),
                      r=['st2'], w=['st2'])
                S.act(lambda e: e.activation(out=st2, in_=st2, func=AF.Ln), r=['st2'], w=['st2'])
                S.act(lambda e: e.activation(out=st2, in_=st2, func=AF.Exp, scale=-0.5), r=['st2'], w=['st2'])
                S.dve(lambda e: e.tensor_tensor(out=Yc3, in0=Yc3, in1=st2.unsqueeze(2).broadcast_to([64, 16, 64]), op=ALU.mult),
                      r=['Yc', 'st2'], w=['Yc'])
                S.dve(lambda e: e.tensor_tensor(out=Yc, in0=Yc, in1=gnp[:, 0, :], op=ALU.mult), r=['Yc', ('gnp', 0)], w=['Yc'])
                S.pool(lambda e: e.tensor_tensor(out=Yc, in0=Yc, in1=gnp[:, 1, :], op=ALU.add), r=['Yc', ('gnp', 1)], w=['Yc'])
                S.dve(lambda e: e.tensor_tensor(out=Y3, in0=Vf.rearrange("p (h v) -> p h v", h=16),
                                                in1=rkb.unsqueeze(2).broadcast_to([64, 16, 64]), op=ALU.mult),
                      r=[kVf, krkb] + YK, w=YK)
                S.pool(lambda e: e.tensor_tensor(out=Yc, in0=Yc, in1=Yf, op=ALU.add), r=['Yc'] + YK, w=['Yc'])
                S.dve(lambda e: e.tensor_tensor(out=gb, in0=Yc, in1=Zf, op=ALU.mult), r=['Yc', kZf], w=['gb'])
                yield 'tailA'
                def trg(e):
                    ins = None
                    for c in range(8):
                        ins = e.transpose(out=PSB(7, 0, 128, c * 64, (c + 1) * 64), in_=gb[:, c * 128:(c + 1) * 128],
                                          identity=ident_b[0:64, 0:64])
                    return ins
                S.pe(trg, r=['gb', 'ident_b'], w=[pk(7)])
                S.act(lambda e: e.copy(out=gT.rearrange("p c t -> p (c t)"), in_=PSB(7, 0, 128, 0, 512)), r=[pk(7)], w=['gT'])

                def mo(e):
                    ins = None
                    for cb in range(2):
                        for c in range(8):
                            ins = e.matmul(PS(cb, 0, 64, 0, 512), lhsT=gT[:, c, :], rhs=wo_o[:, c, cb * 512:(cb + 1) * 512],
                                           start=(c == 0), stop=(c == 7))
                    return ins
                S.pe(mo, r=['gT'] + WOO, w=[pk(0), pk(1)])
                for cb in range(2):
                    S.dve(lambda e, cb=cb: e.scalar_tensor_tensor(
                        out=xr[:, cb * 512:(cb + 1) * 512], in0=xr[:, cb * 512:(cb + 1) * 512], scalar=ALPHA,
                        in1=PS(cb, 0, 64, 0, 512), op0=ALU.mult, op1=ALU.add), r=[kxr, pk(cb)], w=[kxr])
                    S.dve(lambda e, cb=cb: e.bn_stats(out=bst1[:, cb, :], in_=xr[:, cb * 512:(cb + 1) * 512]), r=[kxr], w=[('bst1', cb)])
                S.dve(lambda e: e.bn_aggr(out=mv1[:, 0:2], in_=bst1.rearrange("p a b -> p (a b)")), r=[('bst1', 0), ('bst1', 1)],
                      w=['mv1'])
                S.dve(lambda e: e.tensor_scalar(out=mv1[:, 2:3], in0=mv1[:, 1:2], scalar1=1e-5, scalar2=None, op0=ALU.add),
                      r=['mv1'], w=['mv1b'])
                S.act(lambda e: e.activation(out=mv1[:, 2:3], in_=mv1[:, 2:3], func=AF.Ln), r=['mv1b'], w=['mv1b'])
                S.act(lambda e: e.activation(out=mv1[:, 2:3], in_=mv1[:, 2:3], func=AF.Exp, scale=-0.5), r=['mv1b'], w=['mv1b'])
                S.dve(lambda e: e.tensor_scalar(out=xr, in0=xr, scalar1=mv1[:, 0:1], scalar2=mv1[:, 2:3], op0=ALU.subtract,
                                                op1=ALU.mult), r=[kxr, 'mv1', 'mv1b'], w=[kxr])
                S.dve(lambda e: e.tensor_tensor(out=xr, in0=xr, in1=gnp[:, 2, :], op=ALU.mult), r=[kxr, ('gnp', 2)], w=[kxr])
                S.pool(lambda e: e.tensor_tensor(out=xr, in0=xr, in1=gnp[:, 3, :], op=ALU.add), r=[kxr, ('gnp', 3)], w=[kxr])
                S.dma('pool', lambda e: e.dma_start(out=yout, in_=(xr[C - 1:C, :] if pad else xr)), r=[kxr])

            def write_state(dst):
                def trs(e):
                    ins = None
                    for h in range(16):
                        ins = e.transpose(out=PS(6 + h // 8, 0, 64, (h % 8) * 64, (h % 8 + 1) * 64), in_=ST[:, h, :],
                                          identity=ident_f[0:64, 0:64])
                    return ins
                S.pe(trs, r=['ST', 'ident_f'], w=[pk(6), pk(7)])
                for hb in range(2):
                    S.act(lambda e, hb=hb: e.copy(out=wko[:, hb * 8:(hb + 1) * 8, :].rearrange("p a b -> p (a b)"),
                                                  in_=PS(6 + hb, 0, 64, 0, 512)), r=[pk(6 + hb)], w=['wko'])
                S.dma('sp', lambda e: e.dma_start(out=dst.rearrange("(h v) k -> v h k", h=16), in_=wko), r=['wko'])

            def zero_state():
                S.pool(lambda e: e.memset(ST, 0.0), w=['ST'])
                S.pool(lambda e: e.memset(STb, 0.0), w=['STb'])

            def mk_load_state(s):
                def load_state():
                    S.dma('sp', lambda e: e.dma_start(
                        out=stin2, in_=I['state_wkv'][s * 1024:(s + 1) * 1024, :].rearrange("(pr p) k -> p pr k", pr=8)), w=['stin'])

                    def tri(e):
                        ins = None
                        for pr_ in range(8):
                            ins = e.transpose(out=PS(6 + pr_ // 4, 0, 64, (pr_ % 4) * 128, (pr_ % 4 + 1) * 128), in_=stin2[:, pr_, :],
                                              identity=ident_f)
                        return ins
                    S.pe(tri, r=['stin', 'ident_f'], w=[pk(6), pk(7)])
                    for hb in range(2):
                        S.dve(lambda e, hb=hb: e.tensor_copy(out=ST[:, hb * 8:(hb + 1) * 8, :].rearrange("p a v -> p (a v)"),
                                                             in_=PS(6 + hb, 0, 64, 0, 512)), r=[pk(6 + hb)], w=['ST'])
                    S.act(lambda e: e.copy(out=STb, in_=ST), r=['ST'], w=['STb'])
                return load_state

            gens = []
            nchk = T // C
            for ci in range(nchk):
                gens.append(chunk(ci * C, C, zero_state if ci == 0 else None, x1_scr[ci * C:(ci + 1) * C, :],
                                  O['y_prompt'][ci * C:(ci + 1) * C, :], C, ci, ci % 2,
                                  (lambda: write_state(O['wkv_p'])) if ci == nchk - 1 else None))
            for s in range(NS):
                gens.append(chunk(T + s, 1, mk_load_state(s), x1_scr[T + s:T + s + 1, :], O['y_sample'][s:s + 1, :], 1, 100 + s,
                                  (nchk + s) % 2, (lambda s=s: write_state(O['wkv_s'][s * 1024:(s + 1) * 1024, :]))))
            next(gens[0])
            next(gens[0])
            for i_ in range(1, len(gens)):
                next(gens[i_])
                next(gens[i_ - 1])
                next(gens[i_])
                for _ in gens[i_ - 1]:
                    pass
            next(gens[-1])
            for _ in gens[-1]:
                pass
            S.barrier()
            A.release()
            stage_end(13)
        except _Stop:
            pass
        S.analyze()
        S.emit(nc, sems, dsems)
        print("arena peak bytes", A.peak, "ops", len(S.ops))
    return nc


_NC_CACHE = {}


def kernel(**inputs):
    inp = {k: np.asarray(v) for k, v in inputs.items()}
    if 'nc' not in _NC_CACHE:
        _NC_CACHE['nc'] = build()
    nc = _NC_CACHE['nc']
    f = np.ascontiguousarray
    shared = {
        "cache_cmp_k": f(inp['cache_cmp_kv'][0][:, :, 0].reshape(NPOOL * 128, 128)),
        "cache_cmp_v": f(inp['cache_cmp_kv'][0][:, :, 1].reshape(NPOOL * 128, 128)),
        "cache_sel_k": f(inp['cache_sel_kv'][0][:, :, 0].reshape(NPOOL * 128, 128)),
        "cache_sel_v": f(inp['cache_sel_kv'][0][:, :, 1].reshape(NPOOL * 128, 128)),
        "w_in": f(inp['w_in_even'][0]),
        "conv_w": f(inp['conv_w'][0].reshape(31, 512)),
        "conv_b": f(inp['conv_b'].reshape(1, 512)),
        "conv_ln_g": f(inp['conv_ln_g'].reshape(1, 512)),
        "conv_ln_b": f(inp['conv_ln_b'].reshape(1, 512)),
        "wk_cmp": f(inp['wk_cmp'][0]),
        "wv_cmp": f(inp['wv_cmp'][0]),
        "w_out_even": f(inp['w_out_even'][0]),
        "mu_c": f(inp['mu_c'][0]),
        "w_rkvz": f(inp['w_rkvz'][0].reshape(4 * DM, DM)),
        "w0": f(inp['w0'].reshape(1, DM)),
        "w1": f(inp['w1'][0]),
        "w2": f(inp['w2'][0]),
        "a0": f(inp['a0'].reshape(1, DM)),
        "a1": f(inp['a1'][0]),
        "a2": f(inp['a2'][0]),
        "k_k": f(inp['k_k'].reshape(1, DM)),
        "k_a": f(inp['k_a'].reshape(1, DM)),
        "r_k": f(inp['r_k'].reshape(1, DM)),
        "gn_g": f(inp['gn_g'].reshape(1, DM)),
        "gn_b": f(inp['gn_b'].reshape(1, DM)),
        "w_out_odd": f(inp['w_out_odd'][0]),
        "ln_g": f(inp['ln_g']),
        "ln_b": f(inp['ln_b']),
    }
    in_maps = []
    for c in range(NCORES):
        s0, s1 = c * NS, (c + 1) * NS
        m = dict(shared)
        m["xp"] = f(inp['x_prompt'][c])
        m["xs"] = f(inp['x_sample'][s0:s1, 0, :])
        m["cache_win"] = f(inp['cache_win_kv'][0, s0:s1].reshape(NS * 512, 256))
        m["state_conv"] = f(inp['state_conv'][0, s0:s1].reshape(NS * 30, 512))
        m["state_wkv"] = f(inp['state_wkv'][0, s0:s1].reshape(NS * 16 * 64, 64))
        m["state_shift"] = f(inp['state_shift'][0, s0:s1])
        m["page_table"] = f(inp['page_table'][s0:s1].astype(np.int32))
        in_maps.append(m)
    res = run_bass_kernel_spmd(nc, in_maps, core_ids=list(range(NCORES)), trace=bool(os.environ.get('KDEV_TRACE')))
    if os.environ.get('KDEV_TRACE'):
        print('EXEC_TIME_NS', res.exec_time_ns)
    R = res.results

    def cat(name):
        return np.stack([np.asarray(R[c][name]) for c in range(NCORES)], axis=0)
    y_prompt = cat("y_prompt").reshape(8, T, DM)
    y_sample = cat("y_sample").reshape(32, 1, DM)
    cmp_p = cat("cmp_p").reshape(1, 8, T, 2, 2, 64)
    cmp_s = cat("cmp_s").reshape(1, 32, 1, 2, 2, 64)
    sel_p = cat("sel_p").reshape(1, 8, T, 2, 2, 64)
    sel_s = cat("sel_s").reshape(1, 32, 1, 2, 2, 64)
    win_p = cat("win_p").reshape(1, 8, 512, 2, 2, 64)
    win_s = cat("win_s").reshape(1, 32, 512, 2, 2, 64)
    conv_p = cat("conv_p").reshape(1, 8, 30, 512)
    conv_s = cat("conv_s").reshape(1, 32, 30, 512)
    wkv_p = cat("wkv_p").reshape(1, 8, 16, 64, 64)
    wkv_s = cat("wkv_s").reshape(1, 32, 16, 64, 64)
    shift_p = cat("shift_p").reshape(1, 8, DM)
    shift_s = cat("shift_s").reshape(1, 32, DM)
    return (y_prompt, y_sample, cmp_p, cmp_s, sel_p, sel_s, win_p, win_s, conv_p, conv_s,
            wkv_p, wkv_s, shift_p, shift_s)
```
